# Optimizing a Trainium2 kernel written in Bass

```python
import math
import jax, jax.numpy as jnp
from jax import lax
import numpy as np

D_MODEL = 1024
BATCH = 2
SEQ = 16384
DEPTH = 4

N_EVEN = (DEPTH + 1) // 2
N_ODD = DEPTH // 2
EPS = 1e-6
ROPE_THETA = 10000.0
BLOCK = 128

A_HEADS = 8
A_KV_HEADS = 2
A_GROUP = A_HEADS // A_KV_HEADS
A_HEAD_DIM = D_MODEL // 16
A_WINDOW = 128
A_WIDTH = A_HEADS * A_HEAD_DIM
A_KV_WIDTH = A_KV_HEADS * A_HEAD_DIM
B_WIDTH = D_MODEL // 2
POOL_WINDOWS = (2, 4, 8, 16)
B_GROUPS = len(POOL_WINDOWS)
B_GROUP_DIM = B_WIDTH // B_GROUPS
EVEN_IN = A_WIDTH + 2 * A_KV_WIDTH + B_WIDTH
C_HEADS = 8
C_NOPE = 64
C_ROPE = 32
C_VDIM = 64
C_Q_RANK = D_MODEL // 4
C_KV_RANK = D_MODEL // 8
C_WIDTH = C_HEADS * C_VDIM
D_WIDTH = D_MODEL // 2
D_BLOCKS = 8
D_BLOCK_DIM = D_WIDTH // D_BLOCKS
CONV_WIDTH = 4
LRU_C = 8.0
ODD_IN = C_Q_RANK + C_KV_RANK + C_ROPE + 2 * D_WIDTH
D_FF = 4 * D_MODEL

kernel_name = "hybrid_bidir_swa_pool_mla_rglru"

F32 = jnp.float32


def rmsnorm(x, g):
    xf = x.astype(F32)
    y = xf * lax.rsqrt(jnp.mean(xf * xf, axis=-1, keepdims=True) + EPS)
    return (y * g.astype(F32)).astype(x.dtype)


def rope(x):
    S, d = x.shape[1], x.shape[-1]
    half = d // 2
    inv = ROPE_THETA ** (-jnp.arange(half, dtype=F32) / half)
    ang = jnp.arange(S, dtype=F32)[:, None] * inv[None, :]
    shape = (1, S) + (1,) * (x.ndim - 3) + (half,)
    cos = jnp.cos(ang).reshape(shape)
    sin = jnp.sin(ang).reshape(shape)
    xf = x.astype(F32)
    x1, x2 = xf[..., :half], xf[..., half:]
    return jnp.concatenate([x1 * cos - x2 * sin, x2 * cos + x1 * sin], axis=-1).astype(x.dtype)


def windowed_gqa(q, k, v, sink):
    Bsz, S = q.shape[0], q.shape[1]
    nb = S // BLOCK
    qb = q.reshape(Bsz, nb, BLOCK, A_KV_HEADS, A_GROUP, A_HEAD_DIM)

    def band(t):
        tp = jnp.pad(t, ((0, 0), (BLOCK, BLOCK), (0, 0), (0, 0)))
        tp = tp.reshape(Bsz, nb + 2, BLOCK, A_KV_HEADS, A_HEAD_DIM)
        return jnp.concatenate([tp[:, :-2], tp[:, 1:-1], tp[:, 2:]], axis=2)

    kb, vb = band(k), band(v)
    s = jnp.einsum('bnqhgd,bnjhd->bnhgqj', qb, kb).astype(F32) * (A_HEAD_DIM ** -0.5)
    blk = jnp.arange(nb)[:, None, None]
    qpos = blk * BLOCK + jnp.arange(BLOCK)[None, :, None]
    kpos = (blk - 1) * BLOCK + jnp.arange(3 * BLOCK)[None, None, :]
    valid = (jnp.abs(kpos - qpos) <= A_WINDOW) & (kpos >= 0) & (kpos < S)
    s = jnp.where(valid[None, :, None, None], s, -jnp.inf)
    sk = sink.astype(F32).reshape(1, 1, A_KV_HEADS, A_GROUP, 1, 1)
    m = jnp.maximum(jnp.max(s, axis=-1, keepdims=True), sk)
    p = jnp.exp(s - m)
    p = p / (jnp.sum(p, axis=-1, keepdims=True) + jnp.exp(sk - m))
    out = jnp.einsum('bnhgqj,bnjhd->bnqhgd', p.astype(v.dtype), vb)
    return out.reshape(Bsz, S, A_WIDTH)


def multiscale_pool(u, w_pool, pool_scale):
    Bsz, S = u.shape[0], u.shape[1]
    uf = u.astype(F32).reshape(Bsz, S, B_GROUPS, B_GROUP_DIM)
    cs = jnp.pad(jnp.cumsum(uf, axis=1), ((0, 0), (1, 0), (0, 0), (0, 0)))
    t = jnp.arange(S)
    outs = []
    for g, w in enumerate(POOL_WINDOWS):
        half = w // 2
        lo = jnp.clip(t - half, 0, S)
        hi = jnp.clip(t + half, 0, S)
        win_sum = cs[:, hi, g] - cs[:, lo, g]
        cnt = (hi - lo).astype(F32)[None, :, None]
        outs.append(win_sum / cnt - uf[:, :, g])
    d = jnp.stack(outs, axis=2)
    y = jnp.einsum('bsgi,gij->bsgj', d, w_pool.astype(F32)).reshape(Bsz, S, B_WIDTH)
    return (y * pool_scale.astype(F32)).astype(u.dtype)


def dense_mla(qn, qr, kn, kr, v):
    Bsz, S = qn.shape[0], qn.shape[1]
    nb = S // BLOCK
    scale = (C_NOPE + C_ROPE) ** -0.5

    def blocks(t):
        return jnp.moveaxis(t.reshape((Bsz, nb, BLOCK) + t.shape[2:]), 1, 0)

    def one(args):
        qn_b, qr_b = args
        s = (jnp.einsum('bqhd,bkhd->bhqk', qn_b, kn) + jnp.einsum('bqhr,bkr->bhqk', qr_b, kr)).astype(F32) * scale
        p = jax.nn.softmax(s, axis=-1).astype(v.dtype)
        return jnp.einsum('bhqk,bkhd->bqhd', p, v)

    out = lax.map(one, (blocks(qn), blocks(qr)))
    return jnp.moveaxis(out, 0, 1).reshape(Bsz, S, C_WIDTH)


def _lin_combine(left, right):
    a1, b1 = left
    a2, b2 = right
    return a1 * a2, a2 * b1 + b2


def rglru_block(xr, xg, conv_w, conv_b, wa, ba, wx, bx, lam):
    Bsz, S = xr.shape[0], xr.shape[1]
    left = CONV_WIDTH // 2
    xp = jnp.pad(xr, ((0, 0), (left, CONV_WIDTH - 1 - left), (0, 0)))
    xc = conv_b + conv_w[0] * xp[:, 0:S]
    for j in range(1, CONV_WIDTH):
        xc = xc + conv_w[j] * xp[:, j:j + S]
    xcf = xc.astype(F32)
    xblk = xcf.reshape(Bsz, S, D_BLOCKS, D_BLOCK_DIM)
    hs = []
    for dirn in range(2):
        r = jax.nn.sigmoid(jnp.einsum('bsni,nij->bsnj', xblk, wa[dirn].astype(F32)).reshape(Bsz, S, D_WIDTH) + ba[dirn].astype(F32))
        i = jax.nn.sigmoid(jnp.einsum('bsni,nij->bsnj', xblk, wx[dirn].astype(F32)).reshape(Bsz, S, D_WIDTH) + bx[dirn].astype(F32))
        log_a = -LRU_C * r * jax.nn.softplus(-lam[dirn].astype(F32))
        a = jnp.exp(log_a)
        b = jnp.sqrt(-jnp.expm1(2.0 * log_a)) * (i * xcf)
        _, h = lax.associative_scan(_lin_combine, (a, b), axis=1, reverse=(dirn == 1))
        hs.append(h)
    y = (hs[0] + hs[1]) * jax.nn.gelu(xg.astype(F32))
    return y.astype(xr.dtype)


def even_mixer(h, w_in, sink, w_pool, pool_scale, w_out):
    Bsz, S = h.shape[0], h.shape[1]
    z = h @ w_in
    q, k, v, u = jnp.split(z, [A_WIDTH, A_WIDTH + A_KV_WIDTH, A_WIDTH + 2 * A_KV_WIDTH], axis=-1)
    q = rope(q.reshape(Bsz, S, A_HEADS, A_HEAD_DIM))
    k = rope(k.reshape(Bsz, S, A_KV_HEADS, A_HEAD_DIM))
    v = v.reshape(Bsz, S, A_KV_HEADS, A_HEAD_DIM)
    ya = windowed_gqa(q, k, v, sink)
    yb = multiscale_pool(u, w_pool, pool_scale)
    return jnp.concatenate([ya, yb], axis=-1) @ w_out


def odd_mixer(h, w_in, g_cq, w_uq, g_ckv, w_ukv, conv_w, conv_b, wa, ba, wx, bx, lam, w_out):
    Bsz, S = h.shape[0], h.shape[1]
    z = h @ w_in
    i1 = C_Q_RANK
    i2 = i1 + C_KV_RANK
    i3 = i2 + C_ROPE
    i4 = i3 + D_WIDTH
    cq, ckv, kr, xr, xg = jnp.split(z, [i1, i2, i3, i4], axis=-1)
    q = (rmsnorm(cq, g_cq) @ w_uq).reshape(Bsz, S, C_HEADS, C_NOPE + C_ROPE)
    qn, qr = q[..., :C_NOPE], rope(q[..., C_NOPE:])
    kv = (rmsnorm(ckv, g_ckv) @ w_ukv).reshape(Bsz, S, C_HEADS, C_NOPE + C_VDIM)
    kn, v = kv[..., :C_NOPE], kv[..., C_NOPE:]
    yc = dense_mla(qn, qr, kn, rope(kr), v)
    yd = rglru_block(xr, xg, conv_w, conv_b, wa, ba, wx, bx, lam)
    return jnp.concatenate([yc, yd], axis=-1) @ w_out


def sq_relu_mlp(h, w1, w2):
    u = jax.nn.relu(h @ w1)
    return (u * u) @ w2


def setup_inputs(seed: int = 0) -> dict:
    key = jax.random.key(seed)
    ks = iter(jax.random.split(key, 40))

    def nrm(shape, fan_in):
        return jax.random.normal(next(ks), shape, F32) * (fan_in ** -0.5)

    def gain(shape):
        return 1.0 + 0.02 * jax.random.normal(next(ks), shape, F32)

    def bias(shape):
        return 0.01 * jax.random.normal(next(ks), shape, F32)

    x = jax.random.normal(next(ks), (BATCH, SEQ, D_MODEL), F32)
    u = jax.random.uniform(next(ks), (N_ODD, 2, D_WIDTH), F32, 0.9, 0.999)
    a0 = u ** (1.0 / LRU_C)
    lam = jnp.log(a0) - jnp.log1p(-a0)
    return {
        "x": x,
        "e_norm_mix": gain((N_EVEN, D_MODEL)),
        "e_w_in": nrm((N_EVEN, D_MODEL, EVEN_IN), D_MODEL),
        "e_sink": 0.5 * jax.random.normal(next(ks), (N_EVEN, A_HEADS), F32),
        "e_w_pool": nrm((N_EVEN, B_GROUPS, B_GROUP_DIM, B_GROUP_DIM), B_GROUP_DIM),
        "e_pool_scale": gain((N_EVEN, B_WIDTH)),
        "e_w_out": nrm((N_EVEN, D_MODEL, D_MODEL), D_MODEL),
        "o_norm_mix": gain((N_ODD, D_MODEL)),
        "o_w_in": nrm((N_ODD, D_MODEL, ODD_IN), D_MODEL),
        "o_g_cq": gain((N_ODD, C_Q_RANK)),
        "o_w_uq": nrm((N_ODD, C_Q_RANK, C_HEADS * (C_NOPE + C_ROPE)), C_Q_RANK),
        "o_g_ckv": gain((N_ODD, C_KV_RANK)),
        "o_w_ukv": nrm((N_ODD, C_KV_RANK, C_HEADS * (C_NOPE + C_VDIM)), C_KV_RANK),
        "o_conv_w": nrm((N_ODD, CONV_WIDTH, D_WIDTH), CONV_WIDTH),
        "o_conv_b": bias((N_ODD, D_WIDTH)),
        "o_lru_wa": nrm((N_ODD, 2, D_BLOCKS, D_BLOCK_DIM, D_BLOCK_DIM), D_BLOCK_DIM),
        "o_lru_ba": bias((N_ODD, 2, D_WIDTH)),
        "o_lru_wx": nrm((N_ODD, 2, D_BLOCKS, D_BLOCK_DIM, D_BLOCK_DIM), D_BLOCK_DIM),
        "o_lru_bx": bias((N_ODD, 2, D_WIDTH)),
        "o_lru_lambda": lam,
        "o_w_out": nrm((N_ODD, D_MODEL, D_MODEL), D_MODEL),
        "norm_mlp": gain((DEPTH, D_MODEL)),
        "w_mlp1": nrm((DEPTH, D_MODEL, D_FF), D_MODEL),
        "w_mlp2": nrm((DEPTH, D_FF, D_MODEL), D_FF),
        "final_norm": gain((D_MODEL,)),
    }


def reference(x, e_norm_mix, e_w_in, e_sink, e_w_pool, e_pool_scale, e_w_out,
              o_norm_mix, o_w_in, o_g_cq, o_w_uq, o_g_ckv, o_w_ukv, o_conv_w, o_conv_b,
              o_lru_wa, o_lru_ba, o_lru_wx, o_lru_bx, o_lru_lambda, o_w_out,
              norm_mlp, w_mlp1, w_mlp2, final_norm):
    for layer in range(DEPTH):
        if layer % 2 == 0:
            e = layer // 2
            h = rmsnorm(x, e_norm_mix[e])
            x = x + even_mixer(h, e_w_in[e], e_sink[e], e_w_pool[e], e_pool_scale[e], e_w_out[e])
        else:
            o = layer // 2
            h = rmsnorm(x, o_norm_mix[o])
            x = x + odd_mixer(h, o_w_in[o], o_g_cq[o], o_w_uq[o], o_g_ckv[o], o_w_ukv[o],
                              o_conv_w[o], o_conv_b[o], o_lru_wa[o], o_lru_ba[o], o_lru_wx[o],
                              o_lru_bx[o], o_lru_lambda[o], o_w_out[o])
        x = x + sq_relu_mlp(rmsnorm(x, norm_mlp[layer]), w_mlp1[layer], w_mlp2[layer])
    return rmsnorm(x, final_norm)
```

```python
from contextlib import ExitStack
import numpy as np
import concourse.bass as bass
import concourse.mybir as mybir
from concourse.bass_utils import run_bass_kernel_spmd

F32 = mybir.dt.float32
BF16 = mybir.dt.bfloat16
ALU = mybir.AluOpType
AF = mybir.ActivationFunctionType

NCORES = 8
D = 1024
KC = 8
TOK = 4096
SEQ = 16384
EPS = 1e-6
DFF = 4096
EPOCH = 30000


class Buf:
    __slots__ = ("name", "writers", "readers", "sem_in", "sem_out", "n_in", "n_out", "excl")

    def __init__(self, name, excl=False):
        self.name = name
        self.excl = excl
        self.writers = {}
        self.readers = {}
        self.sem_in = None
        self.sem_out = None
        self.n_in = 0
        self.n_out = 0


class Sched:
    ENGS = ("pe", "act", "dve", "pool", "sp")

    def __init__(self, nc, stack):
        self.nc = nc
        self.stack = stack
        self.h = {"pe": nc.tensor, "act": nc.scalar, "dve": nc.vector, "pool": nc.gpsimd, "sp": nc.sync}
        self.ops = {e: [] for e in self.ENGS}
        self.cnt = {e: 0 for e in self.ENGS}
        self.sem = {e: None for e in self.ENGS}
        self.seen = {e: {} for e in self.ENGS}
        self.last = {e: None for e in self.ENGS}
        self.dma_toks = {}
        self.nsem = 0
        self.ninstr = 0
        self.sem_pool = []
        self.live = []
        self.ddbuf = Buf("dram2dram")

    def new_sem(self, name):
        self.nsem += 1
        return self.stack.enter_context(self.nc.semaphore(f"{name}_{self.nsem}"))

    def _eng_tok(self, e):
        if self.sem[e] is None or self.cnt[e] >= EPOCH:
            self.sem[e] = self.new_sem("e" + e)
            self.cnt[e] = 0
        self.cnt[e] += 1
        tok = (self.sem[e], self.cnt[e])
        self.last[e] = tok
        return tok

    def _waits(self, e, toks):
        need = {}
        seen = self.seen[e]
        for sem, val in toks:
            k = id(sem)
            if seen.get(k, 0) >= val:
                continue
            if k not in need or need[k][1] < val:
                need[k] = (sem, val)
        out = []
        for k, (sem, val) in need.items():
            seen[k] = val
            out.append((sem, val))
        return out

    def _deps(self, e, reads, writes):
        toks = []
        for b in reads:
            toks.extend(b.writers.values())
            if b.excl:
                toks.extend(b.readers.values())
        for b in writes:
            toks.extend(b.writers.values())
            toks.extend(b.readers.values())
        if e == "pe":
            own = id(self.sem["pe"]) if self.sem["pe"] is not None else None
            toks = [t for t in toks if id(t[0]) != own]
        return self._waits(e, toks)

    def op(self, e, fn, reads=(), writes=()):
        waits = self._deps(e, reads, writes)
        tok = self._eng_tok(e)
        for b in reads:
            b.readers[id(tok[0])] = tok
        for b in writes:
            b.readers = {}
            b.writers = {id(tok[0]): tok}
        self.ninstr += 1

        h = self.h[e]
        for sem, val in waits:
            h.wait_ge(sem, val)
        fn(h).then_inc(tok[0], 1)

    def dma(self, q, out_ap, in_ap, reads=(), writes=(), **kw):
        waits = self._deps(q, reads, writes)
        assert len(writes) + len(reads) >= 1 and len(writes) <= 1 and len(reads) <= 1
        if writes:
            b = writes[0]
            if b.sem_in is None:
                b.sem_in, b.n_in = self._take_sem("di")
                self.live.append((b, "in"))
            b.n_in += 16
            tok = (b.sem_in, b.n_in)
            b.readers = {}
            b.writers = {id(tok[0]): tok}
            for rb in reads:
                rb.readers[id(tok[0])] = tok
        else:
            b = reads[0]
            if b.sem_out is None:
                b.sem_out, b.n_out = self._take_sem("do")
                self.live.append((b, "out"))
            b.n_out += 16
            tok = (b.sem_out, b.n_out)
            b.readers[id(tok[0])] = tok
        self.dma_toks[id(tok[0])] = tok
        self.ninstr += 1
        h = self.h[q]
        for sem, val in waits:
            h.wait_ge(sem, val)
        h.dma_start(out=out_ap, in_=in_ap, **kw).then_inc(tok[0], 16)

    def _take_sem(self, name):
        if self.sem_pool:
            return self.sem_pool.pop()
        return self.new_sem(name), 0

    def release_dma_sems(self):
        for b, kind in self.live:
            if kind == "in":
                self.sem_pool.append((b.sem_in, b.n_in)); b.sem_in = None
                b.writers = {}
            else:
                self.sem_pool.append((b.sem_out, b.n_out)); b.sem_out = None
                b.readers = {}
        self.live = []

    def dma_dd(self, q, out_ap, in_ap, **kw):
        self.dma(q, out_ap, in_ap, writes=[self.ddbuf], **kw)

    def dma_dd_async(self, q, out_ap, in_ap, **kw):
        self.dma(q, out_ap, in_ap, writes=[Buf("dd_async")], **kw)

    def barrier(self):
        toks = [t for t in self.last.values() if t is not None] + list(self.dma_toks.values())
        for e in self.ENGS:
            waits = self._waits(e, toks)
            for sem, val in waits:
                self.h[e].wait_ge(sem, val)

    def finalize(self):
        self.barrier()


class Ctx:
    def __init__(self):
        self.nc = bass.Bass("TRN2", target_bir_lowering=False)
        self.stack = ExitStack()
        self.S = Sched(self.nc, self.stack)
        self.n = 0
        self.cur = self.stack
        self.scopes = []
        self.bind = {}
        self.hook = None

    def dram_in(self, name, shape, dt=F32):
        if name in self.bind:
            return self.bind[name]
        return self.nc.dram_tensor(name, list(shape), dt, kind="ExternalInput").ap()

    def dram_out(self, name, shape, dt=F32):
        if name in self.bind:
            return self.bind[name]
        return self.nc.dram_tensor(name, list(shape), dt, kind="ExternalOutput").ap()

    def ext_in(self, name, shape, dt=F32):
        return self.nc.dram_tensor(name, list(shape), dt, kind="ExternalInput").ap()

    def ext_out(self, name, shape, dt=F32):
        return self.nc.dram_tensor(name, list(shape), dt, kind="ExternalOutput").ap()

    def sb(self, name, shape, dt):
        self.n += 1
        return self.cur.enter_context(self.nc.sbuf_tensor(f"{name}_{self.n}", list(shape), dt))

    def ps(self, name, shape, dt=F32):
        self.n += 1
        return self.cur.enter_context(self.nc.psum_tensor(f"{name}_{self.n}", list(shape), dt))

    def dram_tmp(self, name, shape, dt=F32):
        return self.nc.dram_tensor(name, list(shape), dt, kind="Internal").ap()

    def run_hook(self):
        if self.hook is not None:
            f, self.hook = self.hook, None
            f()

    def push(self):
        st = ExitStack()
        self.scopes.append(st)
        self.cur = st

    def pop(self):
        self.S.barrier()
        if len(self.scopes) == 1:
            self.S.release_dma_sems()
        self.scopes.pop().close()
        self.cur = self.scopes[-1] if self.scopes else self.stack

    def finish(self):
        self.S.finalize()
        self.stack.close()
        return self.nc


class Ring:
    def __init__(self, items):
        self.items = items
        self.i = 0

    def next(self):
        it = self.items[self.i % len(self.items)]
        self.i += 1
        return it


def mk_ring(cx, kind, name, n, shape, dt):
    items = []
    for i in range(n):
        t = cx.sb(f"{name}{i}", shape, dt) if kind == "sb" else cx.ps(f"{name}{i}", shape, dt)
        items.append((t, Buf(f"{name}{i}", excl=(kind == "ps"))))
    return Ring(items)


def emit_rmsnorm(cx, x_t, x_b, nchunk, TB, g_t, g_b, ones_t, ones_b, sq_ring, st_ring, rstd_ring,
                 out_t, out_b, nfeat, evac_engs=("dve",)):
    S = cx.S
    st_t, st_b = st_ring.next()
    for c in range(nchunk):
        sq_t, sq_b = sq_ring.next()
        S.op("act", lambda h, c=c, sq_t=sq_t: h.activation(out=sq_t[:, 0:TB], in_=x_t[:, c, 0:TB], func=AF.Square),
             reads=[x_b], writes=[sq_b])
        S.op("pe", lambda h, c=c, sq_t=sq_t: h.matmul(st_t[:, 0:TB], lhsT=ones_t[:, :], rhs=sq_t[:, 0:TB],
                                                        start=(c == 0), stop=(c == nchunk - 1)),
             reads=[sq_b, ones_b], writes=[st_b])
    r_t, r_b = rstd_ring.next()
    S.op("act", lambda h: h.activation(out=r_t[:, 0:TB], in_=st_t[:, 0:TB], func=AF.Sqrt, bias=float(nfeat * EPS)),
         reads=[st_b], writes=[r_b])
    S.op("dve", lambda h: h.reciprocal(out=r_t[:, 0:TB], in_=r_t[:, 0:TB]), reads=[r_b], writes=[r_b])
    for c in range(nchunk):
        e = evac_engs[c % len(evac_engs)]
        S.op(e, lambda h, c=c: h.scalar_tensor_tensor(out=out_t[:, c, 0:TB], in0=x_t[:, c, 0:TB],
                                                       scalar=g_t[:, c:c + 1], in1=r_t[:, 0:TB],
                                                       op0=ALU.mult, op1=ALU.mult),
             reads=[x_b, r_b, g_b], writes=[out_b])


def build_mlp(final_norm, ntok=TOK, dbg=False, cx=None, wbf16=False):
    TB = 256
    NB = ntok // TB
    FC = DFF // 128
    own = cx is None
    cx = Ctx() if own else cx
    cx.push()
    S = cx.S
    xT = cx.dram_in("xT", [D, ntok])
    w1 = cx.dram_in("w1", [D, DFF], BF16 if wbf16 else F32)
    w2 = cx.dram_in("w2", [DFF, D], BF16 if wbf16 else F32)
    gin = cx.dram_in("g", [128, KC])
    oT = cx.dram_out("oT", [D, ntok])
    if final_norm:
        gfin = cx.dram_in("gf", [128, KC])
    if dbg:
        dh = cx.dram_out("dh", [128, KC, TB], BF16)
        da = cx.dram_out("da", [128, DFF // 128, TB], BF16)

    w1b = cx.sb("w1b", [128, KC, DFF], BF16)
    w2b = cx.sb("w2b", [128, FC, D], BF16)
    w1_bufs = [Buf(f"w1_{k}") for k in range(KC)]
    w2_bufs = [Buf(f"w2_{k}") for k in range(8)]
    g_t = cx.sb("g", [128, KC], F32); g_b = Buf("g")
    ones_t = cx.sb("ones", [128, 128], BF16); ones_b = Buf("ones")
    x_ring = mk_ring(cx, "sb", "x", 2, [128, KC, TB], F32)
    h_ring = mk_ring(cx, "sb", "h", 2, [128, KC, TB], BF16)
    a_ring = mk_ring(cx, "sb", "a", 1, [128, FC, TB], BF16)
    r_ring = mk_ring(cx, "sb", "r", 3, [128, TB], BF16)
    sq_ring = mk_ring(cx, "sb", "sq", 3, [128, TB], BF16)
    rstd_ring = mk_ring(cx, "sb", "rstd", 2, [128, TB], F32)
    o_ring = mk_ring(cx, "sb", "o", 2, [128, KC, TB], F32)
    st_ring = mk_ring(cx, "ps", "st", 1, [128, 512], F32)
    p1_ring = mk_ring(cx, "ps", "p1", 3, [128, 512], F32)
    p2_ring = mk_ring(cx, "ps", "p2", 3, [128, 512], F32)
    if final_norm:
        gf_t = cx.sb("gf", [128, KC], F32); gf_b = Buf("gf")
        f_ring = mk_ring(cx, "sb", "f", 2, [128, KC, TB], F32)

    S.dma("sp", g_t[:, :], gin[:, :], writes=[g_b])
    S.op("dve", lambda h: h.tensor_scalar_mul(out=g_t[:, :], in0=g_t[:, :], scalar1=float(np.sqrt(D))),
         reads=[g_b], writes=[g_b])
    if final_norm:
        S.dma("sp", gf_t[:, :], gfin[:, :], writes=[gf_b])
        S.op("dve", lambda h: h.tensor_scalar_mul(out=gf_t[:, :], in0=gf_t[:, :], scalar1=float(np.sqrt(D))),
             reads=[gf_b], writes=[gf_b])
    S.op("pool", lambda h: h.memset(ones_t[:, :], 1.0), writes=[ones_b])
    w1v = w1.rearrange("(k p) n -> p k n", p=128)
    w2v = w2.rearrange("(f p) n -> p f n", p=128)
    wq = "sp" if wbf16 else "pool"
    for k in range(KC):
        S.dma(wq, w1b[:, k, :], w1v[:, k, :], writes=[w1_bufs[k]])
    for j in range(8):
        S.dma(wq, w2b[:, j * 4:(j + 1) * 4, :], w2v[:, j * 4:(j + 1) * 4, :], writes=[w2_bufs[j]])
    xv = xT.rearrange("(c p) t -> p c t", p=128)
    ov = oT.rearrange("(c p) t -> p c t", p=128)

    for b in range(NB):
        t0 = b * TB
        x_t, x_b = x_ring.next()
        S.dma("sp", x_t[:, :, :], xv[:, :, t0:t0 + TB], writes=[x_b])
        h_t, h_b = h_ring.next()
        emit_rmsnorm(cx, x_t, x_b, KC, TB, g_t, g_b, ones_t, ones_b, sq_ring, st_ring, rstd_ring, h_t, h_b, D)
        a_t, a_b = a_ring.next()
        for f in range(FC):
            p_t, p_b = p1_ring.next()
            for k in range(KC):
                S.op("pe", lambda h, f=f, k=k, p_t=p_t: h.matmul(p_t[:, 0:TB], lhsT=w1b[:, k, f * 128:(f + 1) * 128],
                                                                  rhs=h_t[:, k, 0:TB], start=(k == 0), stop=(k == KC - 1)),
                     reads=[h_b, w1_bufs[k]], writes=[p_b])
            r_t, r_b = r_ring.next()
            S.op("act", lambda h, p_t=p_t, r_t=r_t: h.activation(out=r_t[:, 0:TB], in_=p_t[:, 0:TB], func=AF.Relu),
                 reads=[p_b], writes=[r_b])
            S.op("pool", lambda h, f=f, r_t=r_t: h.tensor_tensor(out=a_t[:, f, 0:TB], in0=r_t[:, 0:TB], in1=r_t[:, 0:TB],
                                                                  op=ALU.mult),
                 reads=[r_b], writes=[a_b])
        if dbg and b == 0:
            S.dma("sp", dh[:, :, :], h_t[:, :, :], reads=[h_b])
            S.dma("sp", da[:, :, :], a_t[:, :, :], reads=[a_b])
        o_t, o_b = o_ring.next()
        for c in range(KC):
            p_t, p_b = p2_ring.next()
            for f in range(FC):
                S.op("pe", lambda h, f=f, c=c, p_t=p_t: h.matmul(p_t[:, 0:TB], lhsT=w2b[:, f, c * 128:(c + 1) * 128],
                                                                  rhs=a_t[:, f, 0:TB], start=(f == 0), stop=(f == FC - 1)),
                     reads=[a_b, w2_bufs[f // 4]], writes=[p_b])
            S.op("dve", lambda h, c=c, p_t=p_t: h.tensor_tensor(out=o_t[:, c, 0:TB], in0=p_t[:, 0:TB], in1=x_t[:, c, 0:TB],
                                                                 op=ALU.add),
                 reads=[p_b, x_b], writes=[o_b])
        if final_norm:
            f_t, f_b = f_ring.next()
            emit_rmsnorm(cx, o_t, o_b, KC, TB, gf_t, gf_b, ones_t, ones_b, sq_ring, st_ring, rstd_ring, f_t, f_b, D)
            S.dma("sp", ov[:, :, t0:t0 + TB], f_t[:, :, :], reads=[f_b])
        else:
            S.dma("sp", ov[:, :, t0:t0 + TB], o_t[:, :, :], reads=[o_b])
    cx.pop()
    return cx.finish() if own else None


def run_spmd(nc, in_maps):
    res = run_bass_kernel_spmd(nc, in_maps, core_ids=list(range(len(in_maps))))
    return res.results


def vec128(v, k):
    return np.ascontiguousarray(np.asarray(v, np.float32).reshape(k, 128).T)


def load_cast(cx, q, dst_ap, src_ap, buf):
    cx.S.dma(q, dst_ap, src_ap, writes=[buf])


def build_ea(ntok=TOK, parts='quv', qlvl=4, cx=None):
    TB = 512
    NB = ntok // TB
    own = cx is None
    cx = Ctx() if own else cx
    cx.push()
    S = cx.S
    xT = cx.dram_in("xT", [D, ntok])
    w_in = cx.dram_in("w_in", [D, 1280])
    gin = cx.dram_in("g", [128, KC])
    cosd = cx.dram_in("cos", [128, ntok])
    sind = cx.dram_in("sin", [128, ntok])
    rotd = cx.dram_in("rot", [128, 128])
    QsT = cx.dram_out("QsT", [128, 4, ntok], BF16)
    KT = cx.dram_out("KT", [128, ntok], BF16)
    Vaug = cx.dram_out("Vaug", [ntok, 130], BF16)
    UT = cx.dram_out("UT", [128, 4, ntok], BF16)

    wb = cx.sb("wb", [128, KC, 1280], BF16)
    w_bufs = [Buf(f"w{k}") for k in range(KC)]
    g_t = cx.sb("g", [128, KC], F32); g_b = Buf("g")
    ones_t = cx.sb("ones", [128, 128], BF16); ones_b = Buf("ones")
    rot_t = cx.sb("rot", [128, 128], BF16); rot_b = Buf("rot")
    x_ring = mk_ring(cx, "sb", "x", 2, [128, KC, TB], F32)
    h_ring = mk_ring(cx, "sb", "h", 2, [128, KC, TB], BF16)
    sq_ring = mk_ring(cx, "sb", "sq", 3, [128, TB], BF16)
    rstd_ring = mk_ring(cx, "sb", "rstd", 2, [128, TB], F32)
    cos_ring = mk_ring(cx, "sb", "cos", 2, [128, TB], F32)
    sin_ring = mk_ring(cx, "sb", "sin", 2, [128, TB], F32)
    qb_ring = mk_ring(cx, "sb", "qb", 2, [128, TB], BF16)
    t1_ring = mk_ring(cx, "sb", "t1", 2, [128, TB], F32)
    t2_ring = mk_ring(cx, "sb", "t2", 2, [128, TB], F32)
    qo_ring = mk_ring(cx, "sb", "qo", 2, [128, 5, TB], BF16)
    uo_ring = mk_ring(cx, "sb", "uo", 2, [128, 4, TB], BF16)
    vo_ring = mk_ring(cx, "sb", "vo", 2, [128, 4, 130], BF16)
    st_ring = mk_ring(cx, "ps", "st", 1, [128, 512], F32)
    pq_ring = mk_ring(cx, "ps", "pq", 3, [128, 512], F32)
    pr_ring = mk_ring(cx, "ps", "pr", 2, [128, 512], F32)
    pv_ring = mk_ring(cx, "ps", "pv", 2, [128, 512], F32)

    S.dma("sp", g_t[:, :], gin[:, :], writes=[g_b])
    S.op("dve", lambda h: h.tensor_scalar_mul(out=g_t[:, :], in0=g_t[:, :], scalar1=float(np.sqrt(D))),
         reads=[g_b], writes=[g_b])
    S.op("pool", lambda h: h.memset(ones_t[:, :], 1.0), writes=[ones_b])
    S.dma("pool", rot_t[:, :], rotd[:, :], writes=[rot_b])
    for (vt, vb) in vo_ring.items:
        S.op("pool", lambda h, vt=vt: h.memset(vt[:, :, :], 1.0), writes=[vb])
    for k in range(KC):
        for j in range(2):
            src = w_in[k * 128:(k + 1) * 128, j * 256:(j + 1) * 256].rearrange("p (c d) -> p c d", c=4, d=64)
            dst = wb[:, k, 0:512].rearrange("p (c j d) -> p c j d", c=4, j=2, d=64)[:, :, j, :]
            S.dma("pool", dst, src, writes=[w_bufs[k]])
        S.dma("pool", wb[:, k, 512:1280], w_in[k * 128:(k + 1) * 128, 512:1280], writes=[w_bufs[k]])
    xv = xT.rearrange("(c p) t -> p c t", p=128)

    for b in range(NB):
        t0 = b * TB
        x_t, x_b = x_ring.next()
        S.dma("sp", x_t[:, :, :], xv[:, :, t0:t0 + TB], writes=[x_b])
        cos_t, cos_b = cos_ring.next()
        sin_t, sin_b = sin_ring.next()
        S.dma("sp", cos_t[:, :], cosd[:, t0:t0 + TB], writes=[cos_b])
        S.dma("sp", sin_t[:, :], sind[:, t0:t0 + TB], writes=[sin_b])
        h_t, h_b = h_ring.next()
        emit_rmsnorm(cx, x_t, x_b, KC, TB, g_t, g_b, ones_t, ones_b, sq_ring, st_ring, rstd_ring, h_t, h_b, D)
        qo_t, qo_b = qo_ring.next()
        for c in (range(5) if 'q' in parts else []):
            pq_t, pq_b = pq_ring.next()
            for k in range(KC):
                S.op("pe", lambda h, c=c, k=k, pq_t=pq_t, h_t=h_t: h.matmul(
                    pq_t[:, 0:TB], lhsT=wb[:, k, c * 128:(c + 1) * 128], rhs=h_t[:, k, :],
                    start=(k == 0), stop=(k == KC - 1)), reads=[h_b, w_bufs[k]], writes=[pq_b])
            qb_t, qb_b = qb_ring.next()
            S.op("act", lambda h, pq_t=pq_t, qb_t=qb_t: h.activation(out=qb_t[:, :], in_=pq_t[:, 0:TB], func=AF.Copy),
                 reads=[pq_b], writes=[qb_b])
            if qlvl == 1:
                S.op("act", lambda h, c=c, pq_t=pq_t, qo_t=qo_t: h.activation(out=qo_t[:, c, :], in_=pq_t[:, 0:TB], func=AF.Copy),
                     reads=[pq_b], writes=[qo_b])
                continue
            pr_t, pr_b = pr_ring.next()
            S.op("pe", lambda h, pr_t=pr_t, qb_t=qb_t: h.matmul(pr_t[:, 0:TB], lhsT=rot_t[:, :], rhs=qb_t[:, :],
                                                               start=True, stop=True),
                 reads=[qb_b, rot_b], writes=[pr_b])
            t1_t, t1_b = t1_ring.next()
            t2_t, t2_b = t2_ring.next()
            if qlvl == 2:
                S.op("act", lambda h, c=c, pr_t=pr_t, qo_t=qo_t: h.activation(out=qo_t[:, c, :], in_=pr_t[:, 0:TB], func=AF.Copy),
                     reads=[pr_b], writes=[qo_b])
                continue
            S.op("dve", lambda h, t1_t=t1_t, pq_t=pq_t, cos_t=cos_t: h.tensor_tensor(
                out=t1_t[:, :], in0=pq_t[:, 0:TB], in1=cos_t[:, :], op=ALU.mult), reads=[pq_b, cos_b], writes=[t1_b])
            if qlvl == 3:
                S.op("act", lambda h, c=c, t1_t=t1_t, qo_t=qo_t: h.activation(out=qo_t[:, c, :], in_=t1_t[:, :], func=AF.Copy),
                     reads=[t1_b], writes=[qo_b])
                continue
            S.op("dve", lambda h, t2_t=t2_t, pr_t=pr_t, sin_t=sin_t: h.tensor_tensor(
                out=t2_t[:, :], in0=pr_t[:, 0:TB], in1=sin_t[:, :], op=ALU.mult), reads=[pr_b, sin_b], writes=[t2_b])
            S.op("dve", lambda h, c=c, qo_t=qo_t, t1_t=t1_t, t2_t=t2_t: h.tensor_tensor(
                out=qo_t[:, c, :], in0=t1_t[:, :], in1=t2_t[:, :], op=ALU.add), reads=[t1_b, t2_b], writes=[qo_b])
        if 'q' in parts:
            S.dma("sp", QsT[:, :, t0:t0 + TB], qo_t[:, 0:4, :], reads=[qo_b])
            S.dma("sp", KT[:, t0:t0 + TB], qo_t[:, 4, :], reads=[qo_b])
        uo_t, uo_b = uo_ring.next()
        for gi in (range(4) if 'u' in parts else []):
            pq_t, pq_b = pq_ring.next()
            for k in range(KC):
                S.op("pe", lambda h, gi=gi, k=k, pq_t=pq_t, h_t=h_t: h.matmul(
                    pq_t[:, 0:TB], lhsT=wb[:, k, 768 + gi * 128:768 + (gi + 1) * 128], rhs=h_t[:, k, :],
                    start=(k == 0), stop=(k == KC - 1)), reads=[h_b, w_bufs[k]], writes=[pq_b])
            S.op("act", lambda h, gi=gi, pq_t=pq_t, uo_t=uo_t: h.activation(out=uo_t[:, gi, :], in_=pq_t[:, 0:TB], func=AF.Copy),
                 reads=[pq_b], writes=[uo_b])
        if 'u' in parts:
            S.dma("sp", UT[:, :, t0:t0 + TB], uo_t[:, :, :], reads=[uo_b])
        if 'v' not in parts:
            continue
        vo_t, vo_b = vo_ring.next()
        pv_t, pv_b = pv_ring.next()
        for ti in range(TB // 128):
            for k in range(KC):
                S.op("pe", lambda h, ti=ti, k=k, pv_t=pv_t, h_t=h_t: h.matmul(
                    pv_t[:, ti * 128:(ti + 1) * 128], lhsT=h_t[:, k, ti * 128:(ti + 1) * 128], rhs=wb[:, k, 640:768],
                    start=(k == 0), stop=(k == KC - 1)), reads=[h_b, w_bufs[k]], writes=[pv_b])
        for ti in range(TB // 128):
            for j in range(2):
                S.op("act", lambda h, ti=ti, j=j, pv_t=pv_t, vo_t=vo_t: h.activation(
                    out=vo_t[:, ti, j * 65:j * 65 + 64], in_=pv_t[:, ti * 128 + j * 64:ti * 128 + (j + 1) * 64], func=AF.Copy),
                    reads=[pv_b], writes=[vo_b])
        S.dma("sp", Vaug[t0:t0 + TB, :].rearrange("(i p) n -> p i n", p=128), vo_t[:, :, :], reads=[vo_b])
    cx.pop()
    return cx.finish() if own else None


def rope_tables(pos, half, nrows):
    inv = (np.float32(10000.0) ** (-np.arange(half, dtype=np.float32) / np.float32(half))).astype(np.float32)
    ang = pos.astype(np.float32)[None, :] * inv[np.arange(nrows) % half][:, None]
    return np.cos(ang).astype(np.float32), np.sin(ang).astype(np.float32)


def rot_matrix(dh, nrows=128):
    R = np.zeros((nrows, nrows), np.float32)
    half = dh // 2
    for m in range(nrows):
        d = m % dh
        base = m - d
        if d < half:
            R[base + d + half, m] = -1.0
        else:
            R[base + d - half, m] = 1.0
    return R


def build_eb(ntok=TOK, cx=None):
    TB = 512
    NB = ntok // TB
    NT = ntok // 128
    own = cx is None
    cx = Ctx() if own else cx
    cx.push()
    S = cx.S
    QsT = cx.dram_in("QsT", [128, 4, ntok], BF16)
    KTh = cx.dram_in("KTh", [128, ntok + 256], BF16)
    Vh = cx.dram_in("Vh", [ntok + 256, 130], BF16)
    UTh = cx.dram_in("UTh", [128, 4, ntok + 16], BF16)
    xT = cx.dram_in("xT", [D, ntok])
    w_pool = cx.dram_in("w_pool", [4, 128, 128])
    pscale = cx.dram_in("pscale", [128, 4])
    w_out = cx.dram_in("w_out", [D, D])
    sinkrow = cx.dram_in("sinkrow", [1, 2, 512])
    masksd = cx.dram_in("masks", [4, 128, 512])
    invcd = cx.dram_in("invc", [128, 2, 4, 16])
    oT = cx.dram_out("oT", [D, ntok])

    woA = cx.sb("woA", [128, 4, D], BF16); woA_b = Buf("woA")
    woB = cx.sb("woB", [128, 4, D], BF16); woB_b = Buf("woB")
    wp = cx.sb("wp", [128, 4, 128], BF16); wp_b = Buf("wp")
    ps_t = cx.sb("ps", [128, 4], F32); ps_b = Buf("ps")
    mk_t = cx.sb("mk", [128, 4, 512], BF16); mk_b = Buf("mk")
    invc_t = cx.sb("invc", [128, 2, 4, 16], F32); invc_b = Buf("invc")
    sk_t = cx.sb("sk", [1, 2, 512], F32); sk_b = Buf("sk")
    esk_t = cx.sb("esk", [1, 2, 512], BF16); esk_b = Buf("esk")
    sel_t = cx.sb("sel", [1, 128], BF16); sel_b = Buf("sel")
    ones32 = cx.sb("ones32", [128, 64], F32); ones32_b = Buf("ones32")
    qs_ring = mk_ring(cx, "sb", "qs", 2, [128, 4, TB], BF16)
    kt_ring = mk_ring(cx, "sb", "kt", 2, [128, 6 * 128], BF16)
    v_ring = mk_ring(cx, "sb", "v", 2, [128, 6, 130], BF16)
    u_ring = mk_ring(cx, "sb", "u", 2, [128, 4, TB + 16], BF16)
    x_ring = mk_ring(cx, "sb", "x", 2, [128, KC, TB], F32)
    p_ring = mk_ring(cx, "sb", "p", 4, [128, 512], BF16)
    osb_ring = mk_ring(cx, "sb", "osb", 3, [64, 512], F32)
    rc_ring = mk_ring(cx, "sb", "rc", 3, [128, 512], F32)
    ya_ring = mk_ring(cx, "sb", "ya", 2, [64, 8, TB], BF16)
    yp_ring = mk_ring(cx, "sb", "yp", 2, [128, 4, TB], BF16)
    yb_ring = mk_ring(cx, "sb", "yb", 2, [128, 4, TB], BF16)
    d_ring = mk_ring(cx, "sb", "d", 2, [128, 4, TB], BF16)
    tmp_rings = [mk_ring(cx, "sb", f"tp{g}", 2, [128, TB + 16], F32) for g in range(4)]
    e16_ring = mk_ring(cx, "sb", "e16", 2, [128, 16], F32)
    s_ring = mk_ring(cx, "ps", "s", 3, [128, 512], F32)
    o_ring = mk_ring(cx, "ps", "o", 2, [128, 512], F32)
    bc_ring = mk_ring(cx, "ps", "bc", 1, [128, 512], F32)
    y_ring = mk_ring(cx, "ps", "y", 2, [128, 512], F32)

    S.dma("pool", woA[:, :, :], w_out[0:512, :].rearrange("(i p) n -> p i n", p=128), writes=[woA_b])
    S.dma("pool", woB[:, :, :], w_out[512:1024, :].rearrange("(g p) n -> p g n", p=128), writes=[woB_b])
    S.dma("pool", wp[:, :, :], w_pool.rearrange("g i j -> i g j"), writes=[wp_b])
    S.dma("pool", mk_t[:, :, :], masksd.rearrange("m p n -> p m n"), writes=[mk_b])
    S.dma("sp", ps_t[:, :], pscale[:, :], writes=[ps_b])
    S.dma("sp", invc_t[:, :, :, :], invcd[:, :, :, :], writes=[invc_b])
    S.dma("sp", sk_t[:, :, :], sinkrow[:, :, :], writes=[sk_b])
    S.op("act", lambda h: h.activation(out=esk_t[:, :, :], in_=sk_t[:, :, :], func=AF.Exp), reads=[sk_b], writes=[esk_b])
    S.op("pool", lambda h: h.memset(sel_t[:, :], 0.0), writes=[sel_b])
    S.op("pool", lambda h: h.memset(sel_t[:, 64:65], 1.0), writes=[sel_b])
    S.op("pool", lambda h: h.memset(ones32[:, :], 1.0), writes=[ones32_b])
    xv = xT.rearrange("(c p) t -> p c t", p=128)
    ov = oT.rearrange("(c p) t -> p c t", p=128)
    cx.run_hook()

    for b in range(NB):
        t0 = b * TB
        qs_t, qs_b = qs_ring.next()
        kt_t, kt_b = kt_ring.next()
        v_t, v_b = v_ring.next()
        u_t, u_b = u_ring.next()
        x_t, x_b = x_ring.next()
        S.dma("sp", qs_t[:, :, :], QsT[:, :, t0:t0 + TB], writes=[qs_b])
        S.dma("sp", kt_t[:, :], KTh[:, t0:t0 + 768], writes=[kt_b])
        S.dma("sp", v_t[:, :, :], Vh[t0:t0 + 768, :].rearrange("(i p) n -> p i n", p=128), writes=[v_b])
        S.dma("sp", u_t[:, :, :], UTh[:, :, t0:t0 + TB + 16], writes=[u_b])
        S.dma("sp", x_t[:, :, :], xv[:, :, t0:t0 + TB], writes=[x_b])
        ya_t, ya_b = ya_ring.next()
        tiles = [(nl, j, mi, dm) for nl in range(4) for mi, dm in enumerate((-1, 0, 1)) for j in range(2)]
        LA = 2
        st = {}
        unit_o = {}
        deferred = []

        def emit_S(t):
            nl, j, mi, dm = tiles[t]
            i = nl + dm + 1
            s_t, s_b = s_ring.next()
            S.op("pe", lambda h: h.matmul(s_t[:, :], lhsT=kt_t[j * 64:(j + 1) * 64, i * 128:(i + 1) * 128],
                                          rhs=qs_t[j * 64:(j + 1) * 64, :, nl * 128:(nl + 1) * 128], start=True, stop=True),
                 reads=[kt_b, qs_b], writes=[s_b])
            st[t] = (s_t, s_b)

        def flush_deferred():
            while deferred:
                (o_t, o_b, osb_t, osb_b, rc_t, rc_b, nl, j) = deferred.pop(0)
                bc_t, bc_b = bc_ring.next()
                S.op("pe", lambda h: h.matmul(bc_t[0:64, :], lhsT=ones32[64:65, 0:64], rhs=rc_t[64:65, :], start=True, stop=True),
                     reads=[rc_b, ones32_b], writes=[bc_b])
                S.op("dve", lambda h: h.tensor_tensor(
                    out=ya_t[0:64, j * 4:(j + 1) * 4, nl * 128:(nl + 1) * 128],
                    in0=osb_t[:, :].rearrange("p (c q) -> p c q", c=4),
                    in1=bc_t[0:64, :].rearrange("p (c q) -> p c q", c=4), op=ALU.mult),
                    reads=[osb_b, bc_b], writes=[ya_b])

        for t in range(min(LA, len(tiles))):
            emit_S(t)
        for t in range(len(tiles)):
            nl, j, mi, dm = tiles[t]
            n = 4 * b + nl
            i = nl + dm + 1
            if mi == 0:
                unit_o[(nl, j)] = o_ring.next()
            o_t, o_b = unit_o[(nl, j)]
            s_t, s_b = st.pop(t)
            p_t, p_b = p_ring.next()
            S.op("act", lambda h: h.activation(out=p_t[:, :], in_=s_t[:, :], func=AF.Exp, scale=0.125), reads=[s_b], writes=[p_b])
            if dm != 0:
                if dm == -1:
                    mi_ = 2 if n == 0 else 0
                else:
                    mi_ = 3 if n == NT - 1 else 1
                S.op("pool", lambda h: h.tensor_tensor(out=p_t[:, :], in0=p_t[:, :], in1=mk_t[:, mi_, :], op=ALU.mult),
                     reads=[p_b, mk_b], writes=[p_b])
            if t + LA < len(tiles):
                emit_S(t + LA)
            S.op("pe", lambda h: h.matmul(o_t[0:65, :], lhsT=v_t[:, i, j * 65:(j + 1) * 65], rhs=p_t[:, :], start=(mi == 0), stop=False),
                 reads=[v_b, p_b], writes=[o_b])
            if mi == 0 and j == 1:
                flush_deferred()
            if mi == 2:
                S.op("pe", lambda h: h.matmul(o_t[0:65, :], lhsT=sel_t[0:1, 0:65], rhs=esk_t[0:1, j, :], start=False, stop=True),
                     reads=[sel_b, esk_b], writes=[o_b])
                osb_t, osb_b = osb_ring.next()
                rc_t, rc_b = rc_ring.next()
                S.op("act", lambda h: h.activation(out=osb_t[:, :], in_=o_t[0:64, :], func=AF.Copy), reads=[o_b], writes=[osb_b])
                S.op("dve", lambda h: h.reciprocal(out=rc_t[64:65, :], in_=o_t[64:65, :]), reads=[o_b], writes=[rc_b])
                deferred.append((o_t, o_b, osb_t, osb_b, rc_t, rc_b, nl, j))
        flush_deferred()
        yp_t, yp_b = yp_ring.next()
        S.dma("sp", yp_t[0:64, :, :], ya_t[0:64, 0:8:2, :], reads=[ya_b], writes=[yp_b])
        S.dma("sp", yp_t[64:128, :, :], ya_t[0:64, 1:8:2, :], reads=[ya_b], writes=[yp_b])
        d_t, d_b = d_ring.next()
        L = TB + 16
        for g in range(4):
            w = 2 << g
            steps = g + 1
            src_t, src_b, ln = None, None, L
            for s_i in range(steps):
                sh = 1 << s_i
                tp_t, tp_b = tmp_rings[g].next()
                nl_ = ln - sh
                if s_i == 0:
                    S.op("pool", lambda h, tp_t=tp_t, u_t=u_t, g=g, nl_=nl_, sh=sh: h.tensor_tensor(
                        out=tp_t[:, 0:nl_], in0=u_t[:, g, 0:nl_], in1=u_t[:, g, sh:sh + nl_], op=ALU.add),
                        reads=[u_b], writes=[tp_b])
                else:
                    S.op("pool", lambda h, tp_t=tp_t, src_t=src_t, nl_=nl_, sh=sh: h.tensor_tensor(
                        out=tp_t[:, 0:nl_], in0=src_t[:, 0:nl_], in1=src_t[:, sh:sh + nl_], op=ALU.add),
                        reads=[src_b], writes=[tp_b])
                src_t, src_b, ln = tp_t, tp_b, nl_
            off = 8 - w // 2
            S.op("dve", lambda h, d_t=d_t, src_t=src_t, u_t=u_t, g=g, off=off, w=w: h.scalar_tensor_tensor(
                out=d_t[:, g, :], in0=src_t[:, off:off + TB], scalar=1.0 / w, in1=u_t[:, g, 8:8 + TB],
                op0=ALU.mult, op1=ALU.subtract), reads=[src_b, u_b], writes=[d_b])
            for (is_edge, which, c0) in ((b == 0, 0, 0), (b == NB - 1, 1, TB - 16)):
                if not is_edge:
                    continue
                e_t, e_b = e16_ring.next()
                S.op("dve", lambda h, e_t=e_t, src_t=src_t, g=g, off=off, c0=c0, which=which: h.tensor_tensor(
                    out=e_t[:, :], in0=src_t[:, off + c0:off + c0 + 16], in1=invc_t[:, which, g, :], op=ALU.mult),
                    reads=[src_b, invc_b], writes=[e_b])
                S.op("dve", lambda h, e_t=e_t, d_t=d_t, u_t=u_t, g=g, c0=c0: h.tensor_tensor(
                    out=d_t[:, g, c0:c0 + 16], in0=e_t[:, :], in1=u_t[:, g, 8 + c0:8 + c0 + 16], op=ALU.subtract),
                    reads=[e_b, u_b, d_b], writes=[d_b])
        yb_t, yb_b = yb_ring.next()
        for g in range(4):
            y_t, y_b = y_ring.next()
            S.op("pe", lambda h, y_t=y_t, d_t=d_t, g=g: h.matmul(y_t[:, :], lhsT=wp[:, g, :], rhs=d_t[:, g, :], start=True, stop=True),
                 reads=[wp_b, d_b], writes=[y_b])
            S.op("dve", lambda h, y_t=y_t, yb_t=yb_t, g=g: h.tensor_scalar_mul(out=yb_t[:, g, :], in0=y_t[:, :], scalar1=ps_t[:, g:g + 1]),
                 reads=[y_b, ps_b], writes=[yb_b])
        for o in range(KC):
            y_t, y_b = y_ring.next()
            for hh in range(4):
                S.op("pe", lambda h, y_t=y_t, yp_t=yp_t, hh=hh, o=o: h.matmul(
                    y_t[:, :], lhsT=woA[:, hh, o * 128:(o + 1) * 128], rhs=yp_t[:, hh, :], start=(hh == 0), stop=False),
                    reads=[woA_b, yp_b], writes=[y_b])
            for g in range(4):
                S.op("pe", lambda h, y_t=y_t, yb_t=yb_t, g=g, o=o: h.matmul(
                    y_t[:, :], lhsT=woB[:, g, o * 128:(o + 1) * 128], rhs=yb_t[:, g, :], start=False, stop=(g == 3)),
                    reads=[woB_b, yb_b], writes=[y_b])
            S.op("dve", lambda h, y_t=y_t, x_t=x_t, o=o: h.tensor_tensor(out=x_t[:, o, :], in0=y_t[:, :], in1=x_t[:, o, :], op=ALU.add),
                 reads=[y_b, x_b], writes=[x_b])
        S.dma("sp", ov[:, :, t0:t0 + TB], x_t[:, :, :], reads=[x_b])
    cx.pop()
    return cx.finish() if own else None


def eb_masks(has_left, has_right):
    ki = np.arange(128)[:, None]
    qi = np.arange(128)[None, :]
    mL = np.tile((ki >= qi).astype(np.float32), (1, 4))
    mR = np.tile((ki <= qi).astype(np.float32), (1, 4))
    return np.stack([mL, mR, mL * float(has_left), mR * float(has_right)]).astype(np.float32)


def eb_invc(is_first, is_last):
    out = np.zeros((128, 2, 4, 16), np.float32)
    for g in range(4):
        w = 2 << g
        half = w // 2
        for i in range(16):
            c0 = min(i + half, w) if is_first else w
            r = 16 - i
            c1 = min(half + r, w) if is_last else w
            out[:, 0, g, i] = 1.0 / c0
            out[:, 1, g, i] = 1.0 / c1
    return out


def emit_rope(cx, src_t, src_b, nrow, TB, rot_t, rot_b, cos_t, cos_b, sin_t, sin_b, qb_ring, pr_ring, t1_ring, t2_ring,
              out_ap, out_b):
    S = cx.S
    qb_t, qb_b = qb_ring.next()
    S.op("act", lambda h: h.activation(out=qb_t[0:nrow, :], in_=src_t[0:nrow, 0:TB], func=AF.Copy), reads=[src_b], writes=[qb_b])
    pr_t, pr_b = pr_ring.next()
    S.op("pe", lambda h: h.matmul(pr_t[0:nrow, 0:TB], lhsT=rot_t[0:nrow, 0:nrow], rhs=qb_t[0:nrow, :], start=True, stop=True),
         reads=[qb_b, rot_b], writes=[pr_b])
    t1_t, t1_b = t1_ring.next()
    t2_t, t2_b = t2_ring.next()
    S.op("dve", lambda h: h.tensor_tensor(out=t1_t[0:nrow, :], in0=src_t[0:nrow, 0:TB], in1=cos_t[0:nrow, :], op=ALU.mult),
         reads=[src_b, cos_b], writes=[t1_b])
    S.op("dve", lambda h: h.tensor_tensor(out=t2_t[0:nrow, :], in0=pr_t[0:nrow, 0:TB], in1=sin_t[0:nrow, :], op=ALU.mult),
         reads=[pr_b, sin_b], writes=[t2_b])
    S.op("dve", lambda h: h.tensor_tensor(out=out_ap, in0=t1_t[0:nrow, :], in1=t2_t[0:nrow, :], op=ALU.add),
         reads=[t1_b, t2_b], writes=[out_b])


def build_oa(ntok=TOK, cx=None, mid_hook=None):
    TB = 512
    NB = ntok // TB
    NT = ntok // 128
    own = cx is None
    cx = Ctx() if own else cx
    cx.push()
    S = cx.S
    xT = cx.dram_in("xT", [D, ntok])
    xhalo = cx.dram_in("xhalo", [D, 4])
    w_in = cx.dram_in("w_in", [D, 1440])
    gin = cx.dram_in("g", [128, KC])
    gcq = cx.dram_in("g_cq", [128, 2])
    gckv = cx.dram_in("g_ckv", [128, 1])
    w_uq = cx.dram_in("w_uq", [256, 768])
    w_ukv = cx.dram_in("w_ukv", [128, 1024])
    cwd = cx.dram_in("cw", [128, 4, 4])
    cbd = cx.dram_in("cb", [128, 4])
    wad = cx.dram_in("wa", [2, 8, 64, 64])
    wxd = cx.dram_in("wx", [2, 8, 64, 64])
    bad = cx.dram_in("ba", [128, 2, 4])
    bxd = cx.dram_in("bx", [128, 2, 4])
    lamd = cx.dram_in("lam", [128, 2, 4])
    cosd = cx.dram_in("cos", [128, ntok])
    sind = cx.dram_in("sin", [128, ntok])
    rotd = cx.dram_in("rot", [128, 128])
    QN = cx.dram_out("QN", [512, ntok], BF16)
    QR = cx.dram_out("QR", [256, ntok], BF16)
    KNR = cx.dram_out("KNR", [544, ntok], BF16)
    V5 = cx.dram_out("V5", [1024, NT * 65], BF16)
    GX = cx.dram_out("GX", [512, ntok], BF16)
    AB = cx.dram_out("AB", [2, 2, 512, ntok])
    BLK = cx.dram_out("BLK", [128, NB, 2, 2, 4])
    CAB = cx.dram_out("CAB", [128, 2, 2, 4])
    XR = cx.dram_out("XR", [512, ntok])

    g_t = cx.sb("g", [128, KC], F32); g_b = Buf("g")
    gcq_t = cx.sb("gcq", [128, 2], F32); gcq_b = Buf("gcq")
    gckv_t = cx.sb("gckv", [128, 1], F32); gckv_b = Buf("gckv")
    ones_t = cx.sb("ones", [128, 128], BF16); ones_b = Buf("ones")
    xrh_t = cx.sb("xrh", [128, 4, 4], F32); xrh_b = Buf("xrh")
    cp_t = cx.sb("cp", [128, 2, 4], F32); cp_b = Buf("cp")
    blk_t = cx.sb("blk", [128, NB, 2, 2, 4], F32); blk_b = Buf("blk")
    S.dma("sp", g_t[:, :], gin[:, :], writes=[g_b])
    S.op("dve", lambda h: h.tensor_scalar_mul(out=g_t[:, :], in0=g_t[:, :], scalar1=float(np.sqrt(D))), reads=[g_b], writes=[g_b])
    S.dma("sp", gcq_t[:, :], gcq[:, :], writes=[gcq_b])
    S.op("dve", lambda h: h.tensor_scalar_mul(out=gcq_t[:, :], in0=gcq_t[:, :], scalar1=16.0), reads=[gcq_b], writes=[gcq_b])
    S.dma("sp", gckv_t[:, :], gckv[:, :], writes=[gckv_b])
    S.op("dve", lambda h: h.tensor_scalar_mul(out=gckv_t[:, :], in0=gckv_t[:, :], scalar1=float(np.sqrt(128.0))),
         reads=[gckv_b], writes=[gckv_b])
    S.op("pool", lambda h: h.memset(ones_t[:, :], 1.0), writes=[ones_b])
    S.dma("sp", cp_t[:, :, :], lamd[:, :, :], writes=[cp_b])
    S.op("act", lambda h: h.activation(out=cp_t[:, :, :], in_=cp_t[:, :, :], func=AF.Exp, scale=-1.0), reads=[cp_b], writes=[cp_b])
    S.op("act", lambda h: h.activation(out=cp_t[:, :, :], in_=cp_t[:, :, :], func=AF.Ln, bias=1.0), reads=[cp_b], writes=[cp_b])
    S.op("dve", lambda h: h.tensor_scalar_mul(out=cp_t[:, :, :], in0=cp_t[:, :, :], scalar1=-8.0), reads=[cp_b], writes=[cp_b])

    xv = xT.rearrange("(c p) t -> p c t", p=128)
    cx.push()
    wb = cx.sb("wb", [128, KC, 1440], BF16)
    w_bufs = [Buf(f"w{k}") for k in range(KC)]
    wuqn = cx.sb("wuqn", [128, 2, 512], BF16); wuqr = cx.sb("wuqr", [128, 2, 256], BF16); wuq_b = Buf("wuq")
    wk = cx.sb("wk", [128, 512], BF16); wv = cx.sb("wv", [128, 512], BF16); wkv_b = Buf("wkv")
    rot_t = cx.sb("rot", [128, 128], BF16); rot_b = Buf("rot")
    x_ring = mk_ring(cx, "sb", "x", 2, [128, KC, TB], F32)
    h_ring = mk_ring(cx, "sb", "h", 2, [128, KC, TB], BF16)
    sq_ring = mk_ring(cx, "sb", "sq", 3, [128, TB], BF16)
    rstd_ring = mk_ring(cx, "sb", "rstd", 2, [128, TB], F32)
    cos_ring = mk_ring(cx, "sb", "cos", 2, [128, TB], F32)
    sin_ring = mk_ring(cx, "sb", "sin", 2, [128, TB], F32)
    qb_ring = mk_ring(cx, "sb", "qb", 2, [128, TB], BF16)
    t1_ring = mk_ring(cx, "sb", "t1", 2, [128, TB], F32)
    t2_ring = mk_ring(cx, "sb", "t2", 2, [128, TB], F32)
    cq_ring = mk_ring(cx, "sb", "cq", 1, [128, 2, TB], F32)
    ckv_ring = mk_ring(cx, "sb", "ckv", 1, [128, 1, TB], F32)
    cqn_ring = mk_ring(cx, "sb", "cqn", 2, [128, 2, TB], BF16)
    ckvn_ring = mk_ring(cx, "sb", "ckvn", 2, [128, 1, TB], BF16)
    xr_ring = mk_ring(cx, "sb", "xr", 2, [128, 4, TB], F32)
    gx_ring = mk_ring(cx, "sb", "gx", 2, [128, 4, TB], BF16)
    qn_ring = mk_ring(cx, "sb", "qn", 2, [128, 4, TB], BF16)
    qr_ring = mk_ring(cx, "sb", "qr", 2, [128, 2, TB], BF16)
    kn_ring = mk_ring(cx, "sb", "kn", 2, [128, 4, TB], BF16)
    kr_ring = mk_ring(cx, "sb", "kr", 2, [32, TB], BF16)
    vo_ring = mk_ring(cx, "sb", "vo", 2, [128, 4, 520], BF16)
    hx_t = cx.sb("hx", [128, KC, 4], F32); hx_b = Buf("hx")
    hh_t = cx.sb("hh", [128, KC, 4], BF16); hh_b = Buf("hh")
    st_ring = mk_ring(cx, "ps", "st", 1, [128, 512], F32)
    pq_ring = mk_ring(cx, "ps", "pq", 4, [128, 512], F32)
    pr_ring = mk_ring(cx, "ps", "pr", 1, [128, 512], F32)
    pv_ring = mk_ring(cx, "ps", "pv", 2, [128, 512], F32)

    for k in range(KC):
        S.dma("pool", wb[:, k, :], w_in[k * 128:(k + 1) * 128, :], writes=[w_bufs[k]])
    for k in range(2):
        src = w_uq[k * 128:(k + 1) * 128, :].rearrange("p (h e) -> p h e", e=96)
        S.dma("pool", wuqn[:, k, :].rearrange("p (h d) -> p h d", d=64), src[:, :, 0:64], writes=[wuq_b])
        S.dma("pool", wuqr[:, k, :].rearrange("p (h d) -> p h d", d=32), src[:, :, 64:96], writes=[wuq_b])
    srckv = w_ukv.rearrange("p (h e) -> p h e", e=128)
    S.dma("pool", wk[:, :].rearrange("p (h d) -> p h d", d=64), srckv[:, :, 0:64], writes=[wkv_b])
    S.dma("pool", wv[:, :].rearrange("p (h d) -> p h d", d=64), srckv[:, :, 64:128], writes=[wkv_b])
    S.dma("pool", rot_t[:, :], rotd[:, :], writes=[rot_b])
    for (vt, vb) in vo_ring.items:
        S.op("pool", lambda h, vt=vt: h.memset(vt[:, :, :], 1.0), writes=[vb])

    def proj_tile(h_t, h_b, c0, ncols, TBx):
        pq_t, pq_b = pq_ring.next()
        for k in range(KC):
            S.op("pe", lambda h, k=k: h.matmul(pq_t[0:ncols, 0:TBx], lhsT=wb[:, k, c0:c0 + ncols], rhs=h_t[:, k, 0:TBx],
                                               start=(k == 0), stop=(k == KC - 1)), reads=[h_b, w_bufs[k]], writes=[pq_b])
        return pq_t, pq_b

    S.dma("sp", hx_t[:, :, :], xhalo.rearrange("(c p) t -> p c t", p=128), writes=[hx_b])
    emit_rmsnorm(cx, hx_t, hx_b, KC, 4, g_t, g_b, ones_t, ones_b, sq_ring, st_ring, rstd_ring, hh_t, hh_b, D)
    for c in range(4):
        pq_t, pq_b = proj_tile(hh_t, hh_b, 416 + c * 128, 128, 4)
        S.op("act", lambda h, c=c, pq_t=pq_t: h.activation(out=xrh_t[:, c, :], in_=pq_t[:, 0:4], func=AF.Copy),
             reads=[pq_b], writes=[xrh_b])

    for b in range(NB):
        t0 = b * TB
        x_t, x_b = x_ring.next()
        S.dma("sp", x_t[:, :, :], xv[:, :, t0:t0 + TB], writes=[x_b])
        cos_t, cos_b = cos_ring.next()
        sin_t, sin_b = sin_ring.next()
        S.dma("sp", cos_t[:, :], cosd[:, t0:t0 + TB], writes=[cos_b])
        S.dma("sp", sin_t[:, :], sind[:, t0:t0 + TB], writes=[sin_b])
        h_t, h_b = h_ring.next()
        emit_rmsnorm(cx, x_t, x_b, KC, TB, g_t, g_b, ones_t, ones_b, sq_ring, st_ring, rstd_ring, h_t, h_b, D)
        cq_t, cq_b = cq_ring.next()
        for c in range(2):
            pq_t, pq_b = proj_tile(h_t, h_b, c * 128, 128, TB)
            S.op("act", lambda h, c=c, pq_t=pq_t, cq_t=cq_t: h.activation(out=cq_t[:, c, :], in_=pq_t[:, 0:TB], func=AF.Copy),
                 reads=[pq_b], writes=[cq_b])
        ckv_t, ckv_b = ckv_ring.next()
        pq_t, pq_b = proj_tile(h_t, h_b, 256, 128, TB)
        S.op("act", lambda h, pq_t=pq_t, ckv_t=ckv_t: h.activation(out=ckv_t[:, 0, :], in_=pq_t[:, 0:TB], func=AF.Copy),
             reads=[pq_b], writes=[ckv_b])
        pq_t, pq_b = proj_tile(h_t, h_b, 384, 32, TB)
        kr_t, kr_b = kr_ring.next()
        emit_rope(cx, pq_t, pq_b, 32, TB, rot_t, rot_b, cos_t, cos_b, sin_t, sin_b, qb_ring, pr_ring, t1_ring, t2_ring,
                  kr_t[0:32, :], kr_b)
        S.dma("sp", KNR[512:544, t0:t0 + TB], kr_t[:, :], reads=[kr_b])
        xr_t, xr_b = xr_ring.next()
        gx_t, gx_b = gx_ring.next()
        for c in range(4):
            pq_t, pq_b = proj_tile(h_t, h_b, 416 + c * 128, 128, TB)
            S.op("act", lambda h, c=c, pq_t=pq_t, xr_t=xr_t: h.activation(out=xr_t[:, c, :], in_=pq_t[:, 0:TB], func=AF.Copy),
                 reads=[pq_b], writes=[xr_b])
        for c in range(4):
            pq_t, pq_b = proj_tile(h_t, h_b, 928 + c * 128, 128, TB)
            S.op("act", lambda h, c=c, pq_t=pq_t, gx_t=gx_t: h.activation(out=gx_t[:, c, :], in_=pq_t[:, 0:TB], func=AF.Gelu_apprx_tanh),
                 reads=[pq_b], writes=[gx_b])
        S.dma("sp", XR.rearrange("(c p) t -> p c t", p=128)[:, :, t0:t0 + TB], xr_t[:, :, :], reads=[xr_b])
        S.dma("sp", GX.rearrange("(c p) t -> p c t", p=128)[:, :, t0:t0 + TB], gx_t[:, :, :], reads=[gx_b])
        cqn_t, cqn_b = cqn_ring.next()
        emit_rmsnorm(cx, cq_t, cq_b, 2, TB, gcq_t, gcq_b, ones_t, ones_b, sq_ring, st_ring, rstd_ring, cqn_t, cqn_b, 256)
        ckvn_t, ckvn_b = ckvn_ring.next()
        emit_rmsnorm(cx, ckv_t, ckv_b, 1, TB, gckv_t, gckv_b, ones_t, ones_b, sq_ring, st_ring, rstd_ring, ckvn_t, ckvn_b, 128)
        qn_t, qn_b = qn_ring.next()
        for i in range(4):
            pq_t, pq_b = pq_ring.next()
            for k in range(2):
                S.op("pe", lambda h, i=i, k=k, pq_t=pq_t, cqn_t=cqn_t: h.matmul(
                    pq_t[:, 0:TB], lhsT=wuqn[:, k, i * 128:(i + 1) * 128], rhs=cqn_t[:, k, :], start=(k == 0), stop=(k == 1)),
                    reads=[cqn_b, wuq_b], writes=[pq_b])
            S.op("act", lambda h, i=i, pq_t=pq_t, qn_t=qn_t: h.activation(out=qn_t[:, i, :], in_=pq_t[:, 0:TB], func=AF.Copy),
                 reads=[pq_b], writes=[qn_b])
        S.dma("sp", QN.rearrange("(c p) t -> p c t", p=128)[:, :, t0:t0 + TB], qn_t[:, :, :], reads=[qn_b])
        qr_t, qr_b = qr_ring.next()
        for i in range(2):
            pq_t, pq_b = pq_ring.next()
            for k in range(2):
                S.op("pe", lambda h, i=i, k=k, pq_t=pq_t, cqn_t=cqn_t: h.matmul(
                    pq_t[:, 0:TB], lhsT=wuqr[:, k, i * 128:(i + 1) * 128], rhs=cqn_t[:, k, :], start=(k == 0), stop=(k == 1)),
                    reads=[cqn_b, wuq_b], writes=[pq_b])
            emit_rope(cx, pq_t, pq_b, 128, TB, rot_t, rot_b, cos_t, cos_b, sin_t, sin_b, qb_ring, pr_ring, t1_ring, t2_ring,
                      qr_t[:, i, :], qr_b)
        S.dma("sp", QR.rearrange("(c p) t -> p c t", p=128)[:, :, t0:t0 + TB], qr_t[:, :, :], reads=[qr_b])
        kn_t, kn_b = kn_ring.next()
        for i in range(4):
            pq_t, pq_b = pq_ring.next()
            S.op("pe", lambda h, i=i, pq_t=pq_t, ckvn_t=ckvn_t: h.matmul(
                pq_t[:, 0:TB], lhsT=wk[:, i * 128:(i + 1) * 128], rhs=ckvn_t[:, 0, :], start=True, stop=True),
                reads=[ckvn_b, wkv_b], writes=[pq_b])
            S.op("act", lambda h, i=i, pq_t=pq_t, kn_t=kn_t: h.activation(out=kn_t[:, i, :], in_=pq_t[:, 0:TB], func=AF.Copy),
                 reads=[pq_b], writes=[kn_b])
        S.dma("sp", KNR[0:512, :].rearrange("(c p) t -> p c t", p=128)[:, :, t0:t0 + TB], kn_t[:, :, :], reads=[kn_b])
        vo_t, vo_b = vo_ring.next()
        for ti in range(TB // 128):
            pv_t, pv_b = pv_ring.next()
            S.op("pe", lambda h, ti=ti, pv_t=pv_t, ckvn_t=ckvn_t: h.matmul(
                pv_t[:, :], lhsT=ckvn_t[:, 0, ti * 128:(ti + 1) * 128], rhs=wv[:, :], start=True, stop=True),
                reads=[ckvn_b, wkv_b], writes=[pv_b])
            S.op("act", lambda h, ti=ti, pv_t=pv_t, vo_t=vo_t: h.activation(
                out=vo_t[:, ti, :].rearrange("p (h e) -> p h e", e=65)[:, :, 0:64],
                in_=pv_t[:, :].rearrange("p (h d) -> p h d", d=64), func=AF.Copy), reads=[pv_b], writes=[vo_b])
        for hd in range(8):
            S.dma("sp", V5[hd * 128:(hd + 1) * 128, :].rearrange("p (i e) -> p i e", e=65)[:, b * 4:(b + 1) * 4, :],
                  vo_t[:, :, hd * 65:(hd + 1) * 65], reads=[vo_b])
    cx.pop()
    if mid_hook is not None:
        mid_hook()

    cx.push()
    wabd = cx.sb("wabd", [128, 2, 4, 128], BF16); wxbd = cx.sb("wxbd", [128, 2, 4, 128], BF16); bd_b = Buf("bd")
    cw_t = cx.sb("cw", [128, 4, 4], F32); cb_t = cx.sb("cb", [128, 4], F32); cw_b = Buf("cw")
    ba_t = cx.sb("ba", [128, 2, 4], F32); bx_t = cx.sb("bx", [128, 2, 4], F32); bb_b = Buf("bb")
    xe_ring = mk_ring(cx, "sb", "xe", 2, [128, 4, TB + 4], F32)
    xc_ring = mk_ring(cx, "sb", "xc", 2, [128, 4, TB], F32)
    xcb_ring = mk_ring(cx, "sb", "xcb", 2, [128, 4, TB], BF16)
    r_ring = mk_ring(cx, "sb", "r", 1, [128, 8, TB], F32)
    i_ring = mk_ring(cx, "sb", "i", 1, [128, 8, TB], F32)
    a_ring = mk_ring(cx, "sb", "a", 2, [128, 8, TB], F32)
    b_ring = mk_ring(cx, "sb", "b", 2, [128, 8, TB], F32)
    hl_ring = mk_ring(cx, "sb", "hl", 2, [128, TB], F32)
    sr_ring = mk_ring(cx, "sb", "sr", 2, [128, 8], F32)
    pg_ring = mk_ring(cx, "ps", "pg", 6, [128, 512], F32)
    S.op("pool", lambda h: h.memset(wabd[:, :, :, :], 0.0), writes=[bd_b])
    S.op("pool", lambda h: h.memset(wxbd[:, :, :, :], 0.0), writes=[bd_b])
    for d in range(2):
        for c in range(4):
            for hf in range(2):
                S.dma("pool", wabd[hf * 64:(hf + 1) * 64, d, c, hf * 64:(hf + 1) * 64], wad[d, 2 * c + hf, :, :], writes=[bd_b])
                S.dma("pool", wxbd[hf * 64:(hf + 1) * 64, d, c, hf * 64:(hf + 1) * 64], wxd[d, 2 * c + hf, :, :], writes=[bd_b])
    S.dma("sp", cw_t[:, :, :], cwd[:, :, :], writes=[cw_b])
    S.dma("sp", cb_t[:, :], cbd[:, :], writes=[cw_b])
    S.dma("sp", ba_t[:, :, :], bad[:, :, :], writes=[bb_b])
    S.dma("sp", bx_t[:, :, :], bxd[:, :, :], writes=[bb_b])
    XRv = XR.rearrange("(c p) t -> p c t", p=128)
    ABv = AB.rearrange("d s (c p) t -> d s p c t", p=128)

    for b in range(NB):
        t0 = b * TB
        xe_t, xe_b = xe_ring.next()
        lo = 0 if b > 0 else 2
        hi = TB + 3 if b < NB - 1 else TB + 2
        S.dma("sp", xe_t[:, :, lo:hi], XRv[:, :, t0 - 2 + lo:t0 - 2 + hi], writes=[xe_b])
        if b == 0:
            S.op("pool", lambda h, xe_t=xe_t: h.tensor_copy(out=xe_t[:, :, 0:2], in_=xrh_t[:, :, 0:2]), reads=[xrh_b, xe_b], writes=[xe_b])
        if b == NB - 1:
            S.op("pool", lambda h, xe_t=xe_t: h.tensor_copy(out=xe_t[:, :, TB + 2:TB + 3], in_=xrh_t[:, :, 2:3]),
                 reads=[xrh_b, xe_b], writes=[xe_b])
        xc_t, xc_b = xc_ring.next()
        xcb_t, xcb_b = xcb_ring.next()
        for c in range(4):
            S.op("dve", lambda h, c=c, xc_t=xc_t, xe_t=xe_t: h.tensor_scalar(
                out=xc_t[:, c, :], in0=xe_t[:, c, 0:TB], scalar1=cw_t[:, c, 0:1], scalar2=cb_t[:, c:c + 1],
                op0=ALU.mult, op1=ALU.add), reads=[xe_b, cw_b], writes=[xc_b])
            for j in range(1, 4):
                S.op("dve", lambda h, c=c, j=j, xc_t=xc_t, xe_t=xe_t: h.scalar_tensor_tensor(
                    out=xc_t[:, c, :], in0=xe_t[:, c, j:j + TB], scalar=cw_t[:, c, j:j + 1], in1=xc_t[:, c, :],
                    op0=ALU.mult, op1=ALU.add), reads=[xe_b, cw_b, xc_b], writes=[xc_b])
        S.op("pool", lambda h, xc_t=xc_t, xcb_t=xcb_t: h.tensor_copy(out=xcb_t[:, :, :], in_=xc_t[:, :, :]), reads=[xc_b], writes=[xcb_b])
        r_t, r_b = r_ring.next()
        i_t, i_b = i_ring.next()
        a_t, a_b = a_ring.next()
        b_t, b_b = b_ring.next()
        sr_t, sr_b = sr_ring.next()
        S.op("pool", lambda h, sr_t=sr_t: h.memset(sr_t[:, :], 0.0), writes=[sr_b])
        for d in range(2):
            for c in range(4):
                q = d * 4 + c
                pg_t, pg_b = pg_ring.next()
                S.op("pe", lambda h, d=d, c=c, pg_t=pg_t, xcb_t=xcb_t: h.matmul(pg_t[:, :], lhsT=wabd[:, d, c, :], rhs=xcb_t[:, c, :],
                                                                         start=True, stop=True), reads=[bd_b, xcb_b], writes=[pg_b])
                S.op("act", lambda h, d=d, c=c, q=q, pg_t=pg_t, r_t=r_t, sr_t=sr_t: h.activation(
                    out=r_t[:, q, :], in_=pg_t[:, :], func=AF.Sigmoid, bias=ba_t[:, d, c:c + 1], accum_out=sr_t[:, q:q + 1]),
                    reads=[pg_b, bb_b], writes=[r_b, sr_b])
                pg_t, pg_b = pg_ring.next()
                S.op("pe", lambda h, d=d, c=c, pg_t=pg_t, xcb_t=xcb_t: h.matmul(pg_t[:, :], lhsT=wxbd[:, d, c, :], rhs=xcb_t[:, c, :],
                                                                         start=True, stop=True), reads=[bd_b, xcb_b], writes=[pg_b])
                S.op("act", lambda h, d=d, c=c, q=q, pg_t=pg_t, i_t=i_t: h.activation(
                    out=i_t[:, q, :], in_=pg_t[:, :], func=AF.Sigmoid, bias=bx_t[:, d, c:c + 1]),
                    reads=[pg_b, bb_b], writes=[i_b])
        for d in range(2):
            for c in range(4):
                q = d * 4 + c
                S.op("act", lambda h, d=d, c=c, q=q, a_t=a_t, r_t=r_t: h.activation(
                    out=a_t[:, q, :], in_=r_t[:, q, :], func=AF.Exp, scale=cp_t[:, d, c:c + 1]), reads=[r_b, cp_b], writes=[a_b])
                S.op("act", lambda h, d=d, c=c, q=q, sr_t=sr_t, b=b: h.activation(
                    out=blk_t[:, b, d, 0, c:c + 1], in_=sr_t[:, q:q + 1], func=AF.Exp, scale=cp_t[:, d, c:c + 1]),
                    reads=[sr_b, cp_b, blk_b], writes=[blk_b])
        S.op("pool", lambda h, a_t=a_t, r_t=r_t: h.tensor_tensor(out=r_t[:, :, :], in0=a_t[:, :, :], in1=a_t[:, :, :], op=ALU.mult),
             reads=[a_b, r_b], writes=[r_b])
        S.op("act", lambda h, r_t=r_t: h.activation(out=r_t[:, :, :], in_=r_t[:, :, :], func=AF.Sqrt, scale=-1.0, bias=1.0),
             reads=[r_b], writes=[r_b])
        for d in range(2):
            S.op("pool", lambda h, d=d, i_t=i_t, xc_t=xc_t: h.tensor_tensor(out=i_t[:, d * 4:(d + 1) * 4, :], in0=i_t[:, d * 4:(d + 1) * 4, :],
                                                                        in1=xc_t[:, :, :], op=ALU.mult), reads=[i_b, xc_b], writes=[i_b])
        S.op("dve", lambda h, b_t=b_t, r_t=r_t, i_t=i_t: h.tensor_tensor(out=b_t[:, :, :], in0=r_t[:, :, :], in1=i_t[:, :, :], op=ALU.mult),
             reads=[r_b, i_b], writes=[b_b])
        for d in range(2):
            for c in range(4):
                q = d * 4 + c
                hl_t, hl_b = hl_ring.next()
                if d == 0:
                    S.op("dve", lambda h, q=q, hl_t=hl_t, a_t=a_t, b_t=b_t: h.tensor_tensor_scan(
                        out=hl_t[:, :], data0=a_t[:, q, :], data1=b_t[:, q, :], initial=0.0, op0=ALU.mult, op1=ALU.add),
                        reads=[a_b, b_b], writes=[hl_b])
                    col = TB - 1
                else:
                    S.op("dve", lambda h, q=q, hl_t=hl_t, a_t=a_t, b_t=b_t: h.tensor_tensor_scan(
                        out=hl_t[:, ::-1], data0=a_t[:, q, ::-1], data1=b_t[:, q, ::-1], initial=0.0, op0=ALU.mult, op1=ALU.add),
                        reads=[a_b, b_b], writes=[hl_b])
                    col = 0
                S.op("pool", lambda h, d=d, c=c, hl_t=hl_t, col=col, b=b: h.tensor_copy(
                    out=blk_t[:, b, d, 1, c:c + 1], in_=hl_t[:, col:col + 1]), reads=[hl_b, blk_b], writes=[blk_b])
        for d in range(2):
            S.dma("sp", ABv[d, 0, :, :, t0:t0 + TB], a_t[:, d * 4:(d + 1) * 4, :], reads=[a_b])
            S.dma("sp", ABv[d, 1, :, :, t0:t0 + TB], b_t[:, d * 4:(d + 1) * 4, :], reads=[b_b])
    cab_t = cx.sb("cab", [128, 2, 2, 4], F32); cab_b = Buf("cab")
    for d in range(2):
        S.op("pool", lambda h, d=d: h.memset(cab_t[:, d, 0, :], 1.0), writes=[cab_b])
        S.op("pool", lambda h, d=d: h.memset(cab_t[:, d, 1, :], 0.0), writes=[cab_b])
        order = range(NB) if d == 0 else range(NB - 1, -1, -1)
        for b in order:
            S.op("dve", lambda h, d=d, b=b: h.tensor_tensor(out=cab_t[:, d, 1, :], in0=cab_t[:, d, 1, :], in1=blk_t[:, b, d, 0, :],
                                                            op=ALU.mult), reads=[cab_b, blk_b], writes=[cab_b])
            S.op("dve", lambda h, d=d, b=b: h.tensor_tensor(out=cab_t[:, d, 1, :], in0=cab_t[:, d, 1, :], in1=blk_t[:, b, d, 1, :],
                                                            op=ALU.add), reads=[cab_b, blk_b], writes=[cab_b])
            S.op("dve", lambda h, d=d, b=b: h.tensor_tensor(out=cab_t[:, d, 0, :], in0=cab_t[:, d, 0, :], in1=blk_t[:, b, d, 0, :],
                                                            op=ALU.mult), reads=[cab_b, blk_b], writes=[cab_b])
    S.dma("sp", BLK[:, :, :, :, :], blk_t[:, :, :, :, :], reads=[blk_b])
    S.dma("sp", CAB[:, :, :, :], cab_t[:, :, :, :], reads=[cab_b])
    cx.pop()
    cx.pop()
    return cx.finish() if own else None


def chunk_vec(v, nch):
    return np.ascontiguousarray(np.asarray(v, np.float32).reshape(nch, 128).T)


def oa_inputs(xT, xhalo, P, pos):
    cos, sin = rope_tables(pos, 16, 128)
    return {
        "xT": np.ascontiguousarray(xT), "xhalo": np.ascontiguousarray(xhalo), "w_in": P["w_in"], "g": vec128(P["g"], 8),
        "g_cq": vec128(P["g_cq"], 2), "g_ckv": vec128(P["g_ckv"], 1), "w_uq": P["w_uq"], "w_ukv": P["w_ukv"],
        "cw": np.ascontiguousarray(P["conv_w"].reshape(4, 4, 128).transpose(2, 1, 0)),
        "cb": chunk_vec(P["conv_b"], 4), "wa": P["wa"], "wx": P["wx"],
        "ba": np.ascontiguousarray(P["ba"].reshape(2, 4, 128).transpose(2, 0, 1)),
        "bx": np.ascontiguousarray(P["bx"].reshape(2, 4, 128).transpose(2, 0, 1)),
        "lam": np.ascontiguousarray(P["lam"].reshape(2, 4, 128).transpose(2, 0, 1)),
        "cos": cos, "sin": sin, "rot": rot_matrix(32),
    }


def build_ob1(ntok=TOK, nrank=4, cx=None):
    seq = ntok * nrank
    QG = ntok // 512
    NKT = seq // 128
    NT = ntok // 128
    own = cx is None
    cx = Ctx() if own else cx
    cx.push()
    S = cx.S
    QN = cx.dram_in("QN", [512, ntok], BF16)
    QR = cx.dram_in("QR", [256, ntok], BF16)
    KNg = cx.dram_in("KNg", [8 * nrank * 64, ntok], BF16)
    KRg = cx.dram_in("KRg", [nrank * 32, ntok], BF16)
    Vg = cx.dram_in("Vg", [8 * nrank * 128, NT * 65], BF16)
    YC = cx.dram_out("YC", [512, ntok], BF16)

    q_ring = mk_ring(cx, "sb", "q", 2, [128, ntok], BF16)
    k_ring = mk_ring(cx, "sb", "k", 2, [128, seq], BF16)
    v_ring = mk_ring(cx, "sb", "v", 2, [128, NKT, 65], BF16)
    p_ring = mk_ring(cx, "sb", "p", 4, [128, 512], BF16)
    osb_ring = mk_ring(cx, "sb", "osb", 2, [64, 512], F32)
    rc_ring = mk_ring(cx, "sb", "rc", 2, [128, 512], F32)
    yc_ring = mk_ring(cx, "sb", "yc", 2, [64, 512], BF16)
    ones32 = cx.sb("ones32", [128, 64], F32); ones32_b = Buf("ones32")
    s_ring = mk_ring(cx, "ps", "s", 4, [128, 512], F32)
    o_ring = mk_ring(cx, "ps", "o", 2, [128, 512], F32)
    bc_ring = mk_ring(cx, "ps", "bc", 1, [128, 512], F32)
    S.op("pool", lambda h: h.memset(ones32[:, :], 1.0), writes=[ones32_b])
    scale = float(96 ** -0.5)
    LA = 2

    def load_head(hd):
        q_t, q_b = q_ring.next()
        k_t, k_b = k_ring.next()
        v_t, v_b = v_ring.next()
        S.dma("sp", q_t[0:64, :], QN[hd * 64:(hd + 1) * 64, :], writes=[q_b])
        S.dma("sp", q_t[64:96, :], QR[hd * 32:(hd + 1) * 32, :], writes=[q_b])
        for r in range(nrank):
            S.dma("sp", k_t[0:64, r * ntok:(r + 1) * ntok], KNg[(hd * nrank + r) * 64:(hd * nrank + r + 1) * 64, :], writes=[k_b])
            S.dma("sp", k_t[64:96, r * ntok:(r + 1) * ntok], KRg[r * 32:(r + 1) * 32, :], writes=[k_b])
            S.dma("sp", v_t[:, r * NT:(r + 1) * NT, :],
                  Vg[(hd * nrank + r) * 128:(hd * nrank + r + 1) * 128, :].rearrange("p (i e) -> p i e", e=65), writes=[v_b])
        return (q_t, q_b, k_t, k_b, v_t, v_b)

    nxt = load_head(0)
    cx.run_hook()
    for hd in range(8):
        q_t, q_b, k_t, k_b, v_t, v_b = nxt
        if hd + 1 < 8:
            nxt = load_head(hd + 1)
        for qg in range(QG):
            o_t, o_b = o_ring.next()
            stiles = {}

            def emit_s(kt):
                s_t, s_b = s_ring.next()
                S.op("pe", lambda h: h.matmul(s_t[:, :], lhsT=k_t[0:96, kt * 128:(kt + 1) * 128],
                                              rhs=q_t[0:96, qg * 512:(qg + 1) * 512], start=True, stop=True),
                     reads=[k_b, q_b], writes=[s_b])
                stiles[kt] = (s_t, s_b)

            for kt in range(min(LA, NKT)):
                emit_s(kt)
            for kt in range(NKT):
                s_t, s_b = stiles.pop(kt)
                p_t, p_b = p_ring.next()
                S.op("act", lambda h, s_t=s_t, p_t=p_t: h.activation(out=p_t[:, :], in_=s_t[:, :], func=AF.Exp, scale=scale),
                     reads=[s_b], writes=[p_b])
                if kt + LA < NKT:
                    emit_s(kt + LA)
                S.op("pe", lambda h, kt=kt, p_t=p_t: h.matmul(o_t[0:65, :], lhsT=v_t[:, kt, 0:65], rhs=p_t[:, :],
                                                              start=(kt == 0), stop=(kt == NKT - 1)),
                     reads=[v_b, p_b], writes=[o_b])
            osb_t, osb_b = osb_ring.next()
            rc_t, rc_b = rc_ring.next()
            S.op("act", lambda h: h.activation(out=osb_t[:, :], in_=o_t[0:64, :], func=AF.Copy), reads=[o_b], writes=[osb_b])
            S.op("dve", lambda h: h.reciprocal(out=rc_t[64:65, :], in_=o_t[64:65, :]), reads=[o_b], writes=[rc_b])
            bc_t, bc_b = bc_ring.next()
            S.op("pe", lambda h: h.matmul(bc_t[0:64, :], lhsT=ones32[64:65, 0:64], rhs=rc_t[64:65, :], start=True, stop=True),
                 reads=[rc_b, ones32_b], writes=[bc_b])
            yc_t, yc_b = yc_ring.next()
            S.op("dve", lambda h: h.tensor_tensor(out=yc_t[:, :], in0=osb_t[:, :], in1=bc_t[0:64, :], op=ALU.mult),
                 reads=[osb_b, bc_b], writes=[yc_b])
            S.dma("sp", YC[hd * 64:(hd + 1) * 64, qg * 512:(qg + 1) * 512], yc_t[:, :], reads=[yc_b])
    cx.pop()
    return cx.finish() if own else None


def build_ob2(ntok=TOK, ngrp=4, cx=None):
    TB = 512
    NB = ntok // TB
    own = cx is None
    cx = Ctx() if own else cx
    cx.push()
    S = cx.S
    AB = cx.dram_in("AB", [2, 2, 512, ntok])
    GX = cx.dram_in("GX", [512, ntok], BF16)
    YC = cx.dram_in("YC", [512, ntok], BF16)
    xT = cx.dram_in("xT", [D, ntok])
    w_out = cx.dram_in("w_out", [D, D])
    BLK = cx.dram_in("BLK", [128, NB, 2, 2, 4])
    CABg = cx.dram_in("CABg", [128, ngrp, 16])
    mfd = cx.dram_in("mf", [128, ngrp])
    mbd = cx.dram_in("mb", [128, ngrp])
    oT = cx.dram_out("oT", [D, ntok])

    woA = cx.sb("woA", [128, 4, D], BF16); woA_b = Buf("woA")
    woB = cx.sb("woB", [128, 4, D], BF16); woB_b = Buf("woB")
    blk_t = cx.sb("blk", [128, NB, 2, 2, 4], F32); blk_b = Buf("blk")
    cab_t = cx.sb("cab", [128, ngrp, 16], F32); cab_b = Buf("cab")
    m_t = cx.sb("m", [128, 2, ngrp], F32); m_b = Buf("m")
    hin_t = cx.sb("hin", [128, 2, 4], F32); hin_b = Buf("hin")
    tmp_t = cx.sb("tmp", [128, 4], F32); tmp_b = Buf("tmp")
    init_t = cx.sb("init", [128, NB, 2, 4], F32); init_b = Buf("init")
    ab_ring = mk_ring(cx, "sb", "ab", 2, [128, 2, 2, 4, TB], F32)
    hs_ring = mk_ring(cx, "sb", "hs", 2, [128, 2, 4, TB], F32)
    gx_ring = mk_ring(cx, "sb", "gx", 2, [128, 4, TB], BF16)
    yc_ring = mk_ring(cx, "sb", "yc", 2, [128, 4, TB], BF16)
    yd_ring = mk_ring(cx, "sb", "yd", 2, [128, 4, TB], BF16)
    x_ring = mk_ring(cx, "sb", "x", 2, [128, KC, TB], F32)
    y_ring = mk_ring(cx, "ps", "y", 3, [128, 512], F32)

    S.dma("pool", woA[:, :, :], w_out[0:512, :].rearrange("(i p) n -> p i n", p=128), writes=[woA_b])
    S.dma("pool", woB[:, :, :], w_out[512:1024, :].rearrange("(g p) n -> p g n", p=128), writes=[woB_b])
    S.dma("sp", blk_t[:, :, :, :, :], BLK[:, :, :, :, :], writes=[blk_b])
    S.dma("sp", cab_t[:, :, :], CABg[:, :, :], writes=[cab_b])
    S.dma("sp", m_t[:, 0, :], mfd[:, :], writes=[m_b])
    S.dma("sp", m_t[:, 1, :], mbd[:, :], writes=[m_b])
    S.op("pool", lambda h: h.memset(hin_t[:, :, :], 0.0), writes=[hin_b])
    for d in range(2):
        order = range(ngrp) if d == 0 else range(ngrp - 1, -1, -1)
        for i in order:
            S.op("dve", lambda h, d=d, i=i: h.tensor_tensor(out=tmp_t[:, :], in0=hin_t[:, d, :], in1=cab_t[:, i, d * 8:d * 8 + 4], op=ALU.mult),
                 reads=[hin_b, cab_b, tmp_b], writes=[tmp_b])
            S.op("dve", lambda h, d=d, i=i: h.tensor_tensor(out=tmp_t[:, :], in0=tmp_t[:, :], in1=cab_t[:, i, d * 8 + 4:d * 8 + 8], op=ALU.add),
                 reads=[tmp_b, cab_b], writes=[tmp_b])
            S.op("dve", lambda h, d=d, i=i: h.tensor_tensor(out=tmp_t[:, :], in0=tmp_t[:, :], in1=hin_t[:, d, :], op=ALU.subtract),
                 reads=[tmp_b, hin_b], writes=[tmp_b])
            S.op("dve", lambda h, d=d, i=i: h.scalar_tensor_tensor(out=hin_t[:, d, :], in0=tmp_t[:, :], scalar=m_t[:, d, i:i + 1],
                                                                   in1=hin_t[:, d, :], op0=ALU.mult, op1=ALU.add),
                 reads=[tmp_b, m_b, hin_b], writes=[hin_b])
    for d in range(2):
        order = list(range(NB)) if d == 0 else list(range(NB - 1, -1, -1))
        S.op("dve", lambda h, d=d, b0=order[0]: h.tensor_copy(out=init_t[:, b0, d, :], in_=hin_t[:, d, :]),
             reads=[hin_b, init_b], writes=[init_b])
        for bi in range(NB - 1):
            b, bn = order[bi], order[bi + 1]
            S.op("dve", lambda h, d=d, b=b, bn=bn: h.tensor_tensor(out=init_t[:, bn, d, :], in0=init_t[:, b, d, :],
                                                                   in1=blk_t[:, b, d, 0, :], op=ALU.mult),
                 reads=[init_b, blk_b], writes=[init_b])
            S.op("dve", lambda h, d=d, b=b, bn=bn: h.tensor_tensor(out=init_t[:, bn, d, :], in0=init_t[:, bn, d, :],
                                                                   in1=blk_t[:, b, d, 1, :], op=ALU.add),
                 reads=[init_b, blk_b], writes=[init_b])
    xv = xT.rearrange("(c p) t -> p c t", p=128)
    ov = oT.rearrange("(c p) t -> p c t", p=128)
    ABv = AB.rearrange("d s (c p) t -> d s p c t", p=128)
    for b in range(NB):
        t0 = b * TB
        ab_t, ab_b = ab_ring.next()
        for d in range(2):
            for s_ in range(2):
                S.dma("sp", ab_t[:, d, s_, :, :], ABv[d, s_, :, :, t0:t0 + TB], writes=[ab_b])
        gx_t, gx_b = gx_ring.next()
        yc_t, yc_b = yc_ring.next()
        x_t, x_b = x_ring.next()
        S.dma("sp", gx_t[:, :, :], GX.rearrange("(c p) t -> p c t", p=128)[:, :, t0:t0 + TB], writes=[gx_b])
        S.dma("sp", yc_t[:, :, :], YC.rearrange("(i p) t -> p i t", p=128)[:, :, t0:t0 + TB], writes=[yc_b])
        S.dma("sp", x_t[:, :, :], xv[:, :, t0:t0 + TB], writes=[x_b])
        hs_t, hs_b = hs_ring.next()
        for d in range(2):
            for c in range(4):
                if d == 0:
                    S.op("dve", lambda h, d=d, c=c, b=b: h.tensor_tensor_scan(
                        out=hs_t[:, d, c, :], data0=ab_t[:, d, 0, c, :], data1=ab_t[:, d, 1, c, :],
                        initial=init_t[:, b, d, c:c + 1], op0=ALU.mult, op1=ALU.add), reads=[ab_b, init_b, hs_b], writes=[hs_b])
                else:
                    S.op("dve", lambda h, d=d, c=c, b=b: h.tensor_tensor_scan(
                        out=hs_t[:, d, c, ::-1], data0=ab_t[:, d, 0, c, ::-1], data1=ab_t[:, d, 1, c, ::-1],
                        initial=init_t[:, b, d, c:c + 1], op0=ALU.mult, op1=ALU.add), reads=[ab_b, init_b, hs_b], writes=[hs_b])
        S.op("pool", lambda h: h.tensor_tensor(out=hs_t[:, 0, :, :], in0=hs_t[:, 0, :, :], in1=hs_t[:, 1, :, :], op=ALU.add),
             reads=[hs_b], writes=[hs_b])
        yd_t, yd_b = yd_ring.next()
        S.op("pool", lambda h: h.tensor_tensor(out=yd_t[:, :, :], in0=hs_t[:, 0, :, :], in1=gx_t[:, :, :], op=ALU.mult),
             reads=[hs_b, gx_b], writes=[yd_b])
        for o in range(KC):
            y_t, y_b = y_ring.next()
            for hh in range(4):
                S.op("pe", lambda h, hh=hh, o=o: h.matmul(y_t[:, :], lhsT=woA[:, hh, o * 128:(o + 1) * 128], rhs=yc_t[:, hh, :],
                                                         start=(hh == 0), stop=False), reads=[woA_b, yc_b], writes=[y_b])
            for g in range(4):
                S.op("pe", lambda h, g=g, o=o: h.matmul(y_t[:, :], lhsT=woB[:, g, o * 128:(o + 1) * 128], rhs=yd_t[:, g, :],
                                                       start=False, stop=(g == 3)), reads=[woB_b, yd_b], writes=[y_b])
            S.op("dve", lambda h, o=o: h.tensor_tensor(out=x_t[:, o, :], in0=y_t[:, :], in1=x_t[:, o, :], op=ALU.add),
                 reads=[y_b, x_b], writes=[x_b])
        S.dma("sp", ov[:, :, t0:t0 + TB], x_t[:, :, :], reads=[x_b])
    cx.pop()
    return cx.finish() if own else None


def allgather(cx, in_ap, out_ap, groups):
    S = cx.S
    S.barrier()
    sem = S.new_sem("cc")
    cx.nc.gpsimd.collective_compute("AllGather", ALU.bypass, replica_groups=groups, ins=[in_ap], outs=[out_ap]).then_inc(sem, 1)
    for e in S.ENGS:
        S.h[e].wait_ge(sem, 1)


def allgather_many(cx, pairs, groups):
    S = cx.S
    S.barrier()
    sem = S.new_sem("ccm")
    for (in_ap, out_ap) in pairs:
        cx.nc.gpsimd.collective_compute("AllGather", ALU.bypass, replica_groups=groups, ins=[in_ap], outs=[out_ap]).then_inc(sem, 1)
    for e in S.ENGS:
        S.h[e].wait_ge(sem, len(pairs))


def emit_select(cx, src_t, src_b, nrank, m_t, m_b, side, acc_t, acc_b):
    S = cx.S
    S.op("dve", lambda h: h.tensor_scalar_mul(out=acc_t[:, :], in0=src_t[:, 0, :], scalar1=m_t[:, side, 0:1]),
         reads=[src_b, m_b], writes=[acc_b])
    for i in range(1, nrank):
        S.op("dve", lambda h, i=i: h.scalar_tensor_tensor(out=acc_t[:, :], in0=src_t[:, i, :], scalar=m_t[:, side, i:i + 1],
                                                          in1=acc_t[:, :], op0=ALU.mult, op1=ALU.add),
             reads=[src_b, m_b, acc_b], writes=[acc_b])


def emit_even_exchange(cx, KTh, Vh, UTh, pack, packg, mlr, groups, nrank, ntok):
    S = cx.S
    cx.push()
    S.dma_dd("sp", pack[:, 0:128], KTh[:, 128:256])
    S.dma_dd("sp", pack[:, 128:256], KTh[:, ntok:ntok + 128])
    S.dma_dd("sp", pack[:, 256:288].rearrange("p (g t) -> p g t", g=4), UTh[:, :, 8:16])
    S.dma_dd("sp", pack[:, 288:320].rearrange("p (g t) -> p g t", g=4), UTh[:, :, ntok:ntok + 8])
    S.dma_dd("sp", pack[:, 320:450], Vh[128:256, :])
    S.dma_dd("sp", pack[:, 450:580], Vh[ntok:ntok + 128, :])
    allgather(cx, pack[:, :], packg[:, :], groups)
    pg_t = cx.sb("pg", [128, nrank, 580], BF16); pg_b = Buf("pg")
    m_t = cx.sb("mlr", [128, 2, nrank], F32); m_b = Buf("mlr")
    accL = cx.sb("accL", [128, 580], BF16); accL_b = Buf("accL")
    accR = cx.sb("accR", [128, 580], BF16); accR_b = Buf("accR")
    S.dma("sp", pg_t[:, :, :], packg.rearrange("(r p) n -> p r n", p=128), writes=[pg_b])
    S.dma("sp", m_t[:, :, :], mlr[:, :, :], writes=[m_b])
    emit_select(cx, pg_t, pg_b, nrank, m_t, m_b, 0, accL, accL_b)
    emit_select(cx, pg_t, pg_b, nrank, m_t, m_b, 1, accR, accR_b)
    S.dma("sp", KTh[:, 0:128], accL[:, 128:256], reads=[accL_b])
    S.dma("sp", UTh[:, :, 0:8], accL[:, 288:320].rearrange("p (g t) -> p g t", g=4), reads=[accL_b])
    S.dma("sp", Vh[0:128, :], accL[:, 450:580], reads=[accL_b])
    S.dma("sp", KTh[:, 128 + ntok:256 + ntok], accR[:, 0:128], reads=[accR_b])
    S.dma("sp", UTh[:, :, 8 + ntok:16 + ntok], accR[:, 256:288].rearrange("p (g t) -> p g t", g=4), reads=[accR_b])
    S.dma("sp", Vh[128 + ntok:256 + ntok, :], accR[:, 320:450], reads=[accR_b])
    cx.pop()


def emit_xhalo_exchange(cx, xprev, xhp, xhpg, xhalo, mlr, groups, nrank, ntok):
    S = cx.S
    cx.push()
    S.dma_dd("sp", xhp[:, 0:2], xprev[:, 0:2])
    S.dma_dd("sp", xhp[:, 2:4], xprev[:, ntok - 2:ntok])
    allgather(cx, xhp[:, :], xhpg[:, :], groups)
    xg_t = cx.sb("xg", [128, nrank, 32], F32); xg_b = Buf("xg")
    m_t = cx.sb("mlr", [128, 2, nrank], F32); m_b = Buf("mlr")
    accL = cx.sb("accL", [128, 32], F32); accL_b = Buf("accL")
    accR = cx.sb("accR", [128, 32], F32); accR_b = Buf("accR")
    for r in range(nrank):
        S.dma("sp", xg_t[:, r, :].rearrange("p (c t) -> p c t", t=4),
              xhpg[r * D:(r + 1) * D, :].rearrange("(c p) t -> p c t", p=128), writes=[xg_b])
    S.dma("sp", m_t[:, :, :], mlr[:, :, :], writes=[m_b])
    emit_select(cx, xg_t, xg_b, nrank, m_t, m_b, 0, accL, accL_b)
    emit_select(cx, xg_t, xg_b, nrank, m_t, m_b, 1, accR, accR_b)
    xhv = xhalo.rearrange("(c p) t -> p c t", p=128)
    S.dma("sp", xhv[:, :, 0:2], accL[:, :].rearrange("p (c t) -> p c t", t=4)[:, :, 2:4], reads=[accL_b])
    S.dma("sp", xhv[:, :, 2:4], accR[:, :].rearrange("p (c t) -> p c t", t=4)[:, :, 0:2], reads=[accR_b])
    cx.pop()


SMALL_SPECS = None


def build_fused(B=2, nrank=4, ntok=TOK, depth=4):
    NE, NO = (depth + 1) // 2, depth // 2
    NT = ntok // 128
    NB = ntok // 512
    groups = [[b * nrank + r for r in range(nrank)] for b in range(B)]
    cx = Ctx()
    nc = cx.nc
    I = cx.ext_in
    x0 = I("xT", [D, ntok])
    Wd = {
        "e_w_in": I("e_w_in", [NE, D, 1280]), "e_w_pool": I("e_w_pool", [NE, 4, 128, 128]), "e_w_out": I("e_w_out", [NE, D, D]),
        "o_w_in": I("o_w_in", [NO, D, 1440]), "o_w_uq": I("o_w_uq", [NO, 256, 768]), "o_w_ukv": I("o_w_ukv", [NO, 128, 1024]),
        "o_lru_wa": I("o_lru_wa", [NO, 2, 8, 64, 64]), "o_lru_wx": I("o_lru_wx", [NO, 2, 8, 64, 64]), "o_w_out": I("o_w_out", [NO, D, D]),
        "w_mlp1": I("w_mlp1", [depth, D, DFF]), "w_mlp2": I("w_mlp2", [depth, DFF, D]),
        "g_mix": I("g_mix", [depth, 128, KC]), "g_mlp": I("g_mlp", [depth, 128, KC]), "g_fin": I("g_fin", [128, KC]),
        "pscale": I("pscale", [NE, 128, 4]), "sinkrow": I("sinkrow", [NE, 1, 2, 512]),
        "g_cq": I("g_cq", [NO, 128, 2]), "g_ckv": I("g_ckv", [NO, 128, 1]), "cw": I("cw", [NO, 128, 4, 4]), "cb": I("cb", [NO, 128, 4]),
        "ba": I("ba", [NO, 128, 2, 4]), "bx": I("bx", [NO, 128, 2, 4]), "lam": I("lam", [NO, 128, 2, 4]),
        "cos32": I("cos32", [128, ntok]), "sin32": I("sin32", [128, ntok]), "cos16": I("cos16", [128, ntok]), "sin16": I("sin16", [128, ntok]),
        "rot64": I("rot64", [128, 128]), "rot32": I("rot32", [128, 128]), "masks": I("masks", [4, 128, 512]),
        "invc": I("invc", [128, 2, 4, 16]), "mfb": I("mfb", [2, 128, nrank]), "mlr": I("mlr", [128, 2, nrank]),
    }
    outT = cx.ext_out("oT", [D, ntok])

    def tmp(name, shape, dt=F32):
        return nc.dram_tensor(name, list(shape), dt, kind="Internal").ap()

    def make_precast(layer, w1b_d, w2b_d):
        def f():
            for k in range(8):
                cx.S.dma_dd_async("pool", w1b_d[k * 128:(k + 1) * 128, :], Wd["w_mlp1"][layer][k * 128:(k + 1) * 128, :])
            for k in range(8):
                cx.S.dma_dd_async("pool", w2b_d[k * 512:(k + 1) * 512, :], Wd["w_mlp2"][layer][k * 512:(k + 1) * 512, :])
        return f

    xcur = x0
    for layer in range(depth):
        L = f"L{layer}"
        xmix = tmp(L + "_xmix", [D, ntok])
        w1b_d = tmp(L + "_w1b", [D, DFF], BF16)
        w2b_d = tmp(L + "_w2b", [DFF, D], BF16)
        if layer % 2 == 0:
            e = layer // 2
            QsT = tmp(L + "_QsT", [128, 4, ntok], BF16)
            KTh = tmp(L + "_KTh", [128, ntok + 256], BF16)
            Vh = tmp(L + "_Vh", [ntok + 256, 130], BF16)
            UTh = tmp(L + "_UTh", [128, 4, ntok + 16], BF16)
            pack = tmp(L + "_pack", [128, 580], BF16)
            packg = tmp(L + "_packg", [nrank * 128, 580], BF16)
            cx.bind = {"xT": xcur, "w_in": Wd["e_w_in"][e], "g": Wd["g_mix"][layer], "cos": Wd["cos32"], "sin": Wd["sin32"],
                       "rot": Wd["rot64"], "QsT": QsT, "KT": KTh[:, 128:128 + ntok], "Vaug": Vh[128:128 + ntok, :],
                       "UT": UTh[:, :, 8:8 + ntok]}
            build_ea(ntok, cx=cx)
            emit_even_exchange(cx, KTh, Vh, UTh, pack, packg, Wd["mlr"], groups, nrank, ntok)
            cx.bind = {"QsT": QsT, "KTh": KTh, "Vh": Vh, "UTh": UTh, "xT": xcur, "w_pool": Wd["e_w_pool"][e],
                       "pscale": Wd["pscale"][e], "w_out": Wd["e_w_out"][e], "sinkrow": Wd["sinkrow"][e], "masks": Wd["masks"],
                       "invc": Wd["invc"], "oT": xmix}
            cx.hook = make_precast(layer, w1b_d, w2b_d)
            build_eb(ntok, cx=cx)
        else:
            o = layer // 2
            xhp = tmp(L + "_xhp", [D, 4]); xhpg = tmp(L + "_xhpg", [nrank * D, 4]); xhalo = tmp(L + "_xhalo", [D, 4])
            QN = tmp(L + "_QN", [512, ntok], BF16); QR = tmp(L + "_QR", [256, ntok], BF16)
            KNR = tmp(L + "_KNR", [544, ntok], BF16); V5 = tmp(L + "_V5", [1024, NT * 65], BF16)
            GX = tmp(L + "_GX", [512, ntok], BF16); AB = tmp(L + "_AB", [2, 2, 512, ntok])
            BLK = tmp(L + "_BLK", [128, NB, 2, 2, 4]); CAB = tmp(L + "_CAB", [128, 16]); XR = tmp(L + "_XR", [512, ntok])
            KNg = tmp(L + "_KNg", [8 * nrank * 64, ntok], BF16); KRg = tmp(L + "_KRg", [nrank * 32, ntok], BF16)
            Vg = tmp(L + "_Vg", [8 * nrank * 128, NT * 65], BF16)
            CABg = tmp(L + "_CABg", [nrank * 128, 16]); YC = tmp(L + "_YC", [512, ntok], BF16)
            emit_xhalo_exchange(cx, xcur, xhp, xhpg, xhalo, Wd["mlr"], groups, nrank, ntok)
            cx.bind = {"xT": xcur, "xhalo": xhalo, "w_in": Wd["o_w_in"][o], "g": Wd["g_mix"][layer], "g_cq": Wd["g_cq"][o],
                       "g_ckv": Wd["g_ckv"][o], "w_uq": Wd["o_w_uq"][o], "w_ukv": Wd["o_w_ukv"][o], "cw": Wd["cw"][o], "cb": Wd["cb"][o],
                       "wa": Wd["o_lru_wa"][o], "wx": Wd["o_lru_wx"][o], "ba": Wd["ba"][o], "bx": Wd["bx"][o], "lam": Wd["lam"][o],
                       "cos": Wd["cos16"], "sin": Wd["sin16"], "rot": Wd["rot32"], "QN": QN, "QR": QR, "KNR": KNR, "V5": V5, "GX": GX,
                       "AB": AB, "BLK": BLK, "CAB": CAB.rearrange("p (d s c) -> p d s c", d=2, s=2), "XR": XR}
            ccsem = cx.S.new_sem("ccg")
            ncc = [0]

            def gather_kv():
                pairs = []
                for hd in range(8):
                    pairs.append((KNR[hd * 64:(hd + 1) * 64, :], KNg[hd * nrank * 64:(hd + 1) * nrank * 64, :]))
                    pairs.append((V5[hd * 128:(hd + 1) * 128, :], Vg[hd * nrank * 128:(hd + 1) * nrank * 128, :]))
                pairs.append((KNR[512:544, :], KRg[:, :]))
                for (i_ap, o_ap) in pairs:
                    nc.gpsimd.collective_compute("AllGather", ALU.bypass, replica_groups=groups, ins=[i_ap], outs=[o_ap]).then_inc(ccsem, 1)
                    ncc[0] += 1

            build_oa(ntok, cx=cx, mid_hook=gather_kv)
            nc.gpsimd.collective_compute("AllGather", ALU.bypass, replica_groups=groups, ins=[CAB[:, :]], outs=[CABg[:, :]]).then_inc(ccsem, 1)
            ncc[0] += 1
            for e_ in cx.S.ENGS:
                cx.S.h[e_].wait_ge(ccsem, ncc[0])
            cx.bind = {"QN": QN, "QR": QR, "KNg": KNg, "KRg": KRg, "Vg": Vg, "YC": YC}
            cx.hook = make_precast(layer, w1b_d, w2b_d)
            build_ob1(ntok, nrank, cx=cx)
            cx.bind = {"AB": AB, "GX": GX, "YC": YC, "xT": xcur, "w_out": Wd["o_w_out"][o], "BLK": BLK,
                       "CABg": CABg.rearrange("(r p) n -> p r n", p=128), "mf": Wd["mfb"][0], "mb": Wd["mfb"][1], "oT": xmix}
            build_ob2(ntok, nrank, cx=cx)
        last = layer == depth - 1
        xnext = outT if last else tmp(L + "_xmlp", [D, ntok])
        cx.bind = {"xT": xmix, "w1": w1b_d, "w2": w2b_d, "g": Wd["g_mlp"][layer], "gf": Wd["g_fin"], "oT": xnext}
        build_mlp(last, ntok, cx=cx, wbf16=True)
        xcur = xnext
    cx.bind = {}
    return cx.finish()


_FUSED = {}


def run_model(x, W, nrank=4, ntok=TOK):
    B, Sq, _ = x.shape
    ncore = B * nrank
    assert Sq == nrank * ntok
    depth = W["norm_mlp"].shape[0]
    NE, NO = (depth + 1) // 2, depth // 2
    key = (B, nrank, ntok, depth)
    if key not in _FUSED:
        _FUSED[key] = build_fused(B, nrank, ntok, depth)
    nc = _FUSED[key]
    f32 = lambda a: np.ascontiguousarray(np.asarray(a, np.float32))
    g_mix = np.stack([vec128(W["e_norm_mix"][l // 2] if l % 2 == 0 else W["o_norm_mix"][l // 2], 8) for l in range(depth)])
    shared = {
        "e_w_in": f32(W["e_w_in"]), "e_w_pool": f32(W["e_w_pool"]), "e_w_out": f32(W["e_w_out"]),
        "o_w_in": f32(W["o_w_in"]), "o_w_uq": f32(W["o_w_uq"]), "o_w_ukv": f32(W["o_w_ukv"]),
        "o_lru_wa": f32(W["o_lru_wa"]), "o_lru_wx": f32(W["o_lru_wx"]), "o_w_out": f32(W["o_w_out"]),
        "w_mlp1": f32(W["w_mlp1"]), "w_mlp2": f32(W["w_mlp2"]),
        "g_mix": g_mix, "g_mlp": np.stack([vec128(W["norm_mlp"][l], 8) for l in range(depth)]), "g_fin": vec128(W["final_norm"], 8),
        "pscale": np.stack([vec128(W["e_pool_scale"][e], 4) for e in range(NE)]),
        "sinkrow": np.stack([np.repeat(f32(W["e_sink"][e]).reshape(2, 4), 128, axis=1).reshape(1, 2, 512) for e in range(NE)]),
        "g_cq": np.stack([vec128(W["o_g_cq"][o], 2) for o in range(NO)]),
        "g_ckv": np.stack([vec128(W["o_g_ckv"][o], 1) for o in range(NO)]),
        "cw": np.stack([f32(f32(W["o_conv_w"][o]).reshape(4, 4, 128).transpose(2, 1, 0)) for o in range(NO)]),
        "cb": np.stack([chunk_vec(W["o_conv_b"][o], 4) for o in range(NO)]),
        "ba": np.stack([f32(f32(W["o_lru_ba"][o]).reshape(2, 4, 128).transpose(2, 0, 1)) for o in range(NO)]),
        "bx": np.stack([f32(f32(W["o_lru_bx"][o]).reshape(2, 4, 128).transpose(2, 0, 1)) for o in range(NO)]),
        "lam": np.stack([f32(f32(W["o_lru_lambda"][o]).reshape(2, 4, 128).transpose(2, 0, 1)) for o in range(NO)]),
        "rot64": rot_matrix(64), "rot32": rot_matrix(32),
    }
    in_maps = []
    for c in range(ncore):
        bi, r = c // nrank, c % nrank
        pos = r * ntok + np.arange(ntok)
        cos32, sin32 = rope_tables(pos, 32, 128)
        cos16, sin16 = rope_tables(pos, 16, 128)
        mfb = np.zeros((2, 128, nrank), np.float32); mfb[0, :, :r] = 1.0; mfb[1, :, r + 1:] = 1.0
        mlr = np.zeros((128, 2, nrank), np.float32)
        if r > 0:
            mlr[:, 0, r - 1] = 1.0
        if r < nrank - 1:
            mlr[:, 1, r + 1] = 1.0
        im = dict(shared)
        im.update({"xT": np.ascontiguousarray(x[bi, r * ntok:(r + 1) * ntok, :].T), "cos32": cos32, "sin32": sin32, "cos16": cos16,
                   "sin16": sin16, "masks": eb_masks(r > 0, r < nrank - 1), "invc": eb_invc(r == 0, r == nrank - 1),
                   "mfb": mfb, "mlr": mlr})
        in_maps.append(im)
    res = run_spmd(nc, in_maps)
    out = np.empty((B, Sq, D), np.float32)
    for c in range(ncore):
        bi, r = c // nrank, c % nrank
        out[bi, r * ntok:(r + 1) * ntok, :] = res[c]["oT"].T
    return out


def kernel(**inputs):
    W = {k: np.asarray(v) for k, v in inputs.items()}
    x = np.asarray(W.pop("x"), np.float32)
    return run_model(x, W)
```

```python
from contextlib import ExitStack
import numpy as np
import concourse.bass as bass
import concourse.mybir as mybir
from concourse.bass_utils import run_bass_kernel_spmd

F32 = mybir.dt.float32
BF16 = mybir.dt.bfloat16
ALU = mybir.AluOpType
AF = mybir.ActivationFunctionType

NCORES = 8
D = 1024
KC = 8
TOK = 4096
SEQ = 16384
EPS = 1e-6
DFF = 4096
EPOCH = 30000


class Buf:
    __slots__ = ("name", "writers", "readers", "sem_in", "sem_out", "n_in", "n_out", "excl")

    def __init__(self, name, excl=False):
        self.name = name
        self.excl = excl
        self.writers = {}
        self.readers = {}
        self.sem_in = None
        self.sem_out = None
        self.n_in = 0
        self.n_out = 0


class Sched:
    ENGS = ("pe", "act", "dve", "pool", "sp")

    def __init__(self, nc, stack):
        self.nc = nc
        self.stack = stack
        self.h = {"pe": nc.tensor, "act": nc.scalar, "dve": nc.vector, "pool": nc.gpsimd, "sp": nc.sync}
        self.ops = {e: [] for e in self.ENGS}
        self.cnt = {e: 0 for e in self.ENGS}
        self.sem = {e: None for e in self.ENGS}
        self.seen = {e: {} for e in self.ENGS}
        self.last = {e: None for e in self.ENGS}
        self.dma_toks = {}
        self.nsem = 0
        self.ninstr = 0
        self.sem_pool = []
        self.live = []
        self.ddbuf = Buf("dram2dram")

    def new_sem(self, name):
        self.nsem += 1
        return self.stack.enter_context(self.nc.semaphore(f"{name}_{self.nsem}"))

    def _eng_tok(self, e):
        if self.sem[e] is None or self.cnt[e] >= EPOCH:
            self.sem[e] = self.new_sem("e" + e)
            self.cnt[e] = 0
        self.cnt[e] += 1
        tok = (self.sem[e], self.cnt[e])
        self.last[e] = tok
        return tok

    def _waits(self, e, toks):
        need = {}
        seen = self.seen[e]
        for sem, val in toks:
            k = id(sem)
            if seen.get(k, 0) >= val:
                continue
            if k not in need or need[k][1] < val:
                need[k] = (sem, val)
        out = []
        for k, (sem, val) in need.items():
            seen[k] = val
            out.append((sem, val))
        return out

    def _deps(self, e, reads, writes):
        toks = []
        for b in reads:
            toks.extend(b.writers.values())
            if b.excl:
                toks.extend(b.readers.values())
        for b in writes:
            toks.extend(b.writers.values())
            toks.extend(b.readers.values())
        if e == "pe":
            own = id(self.sem["pe"]) if self.sem["pe"] is not None else None
            toks = [t for t in toks if id(t[0]) != own]
        return self._waits(e, toks)

    def op(self, e, fn, reads=(), writes=()):
        waits = self._deps(e, reads, writes)
        tok = self._eng_tok(e)
        for b in reads:
            b.readers[id(tok[0])] = tok
        for b in writes:
            b.readers = {}
            b.writers = {id(tok[0]): tok}
        self.ninstr += 1

        h = self.h[e]
        for sem, val in waits:
            h.wait_ge(sem, val)
        fn(h).then_inc(tok[0], 1)

    def dma(self, q, out_ap, in_ap, reads=(), writes=(), **kw):
        waits = self._deps(q, reads, writes)
        assert len(writes) + len(reads) >= 1 and len(writes) <= 1 and len(reads) <= 1
        if writes:
            b = writes[0]
            if b.sem_in is None:
                b.sem_in, b.n_in = self._take_sem("di")
                self.live.append((b, "in"))
            b.n_in += 16
            tok = (b.sem_in, b.n_in)
            b.readers = {}
            b.writers = {id(tok[0]): tok}
            for rb in reads:
                rb.readers[id(tok[0])] = tok
        else:
            b = reads[0]
            if b.sem_out is None:
                b.sem_out, b.n_out = self._take_sem("do")
                self.live.append((b, "out"))
            b.n_out += 16
            tok = (b.sem_out, b.n_out)
            b.readers[id(tok[0])] = tok
        self.dma_toks[id(tok[0])] = tok
        self.ninstr += 1
        h = self.h[q]
        for sem, val in waits:
            h.wait_ge(sem, val)
        h.dma_start(out=out_ap, in_=in_ap, **kw).then_inc(tok[0], 16)

    def _take_sem(self, name):
        if self.sem_pool:
            return self.sem_pool.pop()
        return self.new_sem(name), 0

    def release_dma_sems(self):
        for b, kind in self.live:
            if kind == "in":
                self.sem_pool.append((b.sem_in, b.n_in)); b.sem_in = None
                b.writers = {}
            else:
                self.sem_pool.append((b.sem_out, b.n_out)); b.sem_out = None
                b.readers = {}
        self.live = []

    def dma_dd(self, q, out_ap, in_ap, **kw):
        self.dma(q, out_ap, in_ap, writes=[self.ddbuf], **kw)

    def dma_dd_async(self, q, out_ap, in_ap, **kw):
        self.dma(q, out_ap, in_ap, writes=[Buf("dd_async")], **kw)

    def barrier(self):
        toks = [t for t in self.last.values() if t is not None] + list(self.dma_toks.values())
        for e in self.ENGS:
            waits = self._waits(e, toks)
            for sem, val in waits:
                self.h[e].wait_ge(sem, val)

    def finalize(self):
        self.barrier()


class Ctx:
    def __init__(self):
        self.nc = bass.Bass("TRN2", target_bir_lowering=False)
        self.stack = ExitStack()
        self.S = Sched(self.nc, self.stack)
        self.n = 0
        self.cur = self.stack
        self.scopes = []
        self.bind = {}
        self.hook = None

    def dram_in(self, name, shape, dt=F32):
        if name in self.bind:
            return self.bind[name]
        return self.nc.dram_tensor(name, list(shape), dt, kind="ExternalInput").ap()

    def dram_out(self, name, shape, dt=F32):
        if name in self.bind:
            return self.bind[name]
        return self.nc.dram_tensor(name, list(shape), dt, kind="ExternalOutput").ap()

    def ext_in(self, name, shape, dt=F32):
        return self.nc.dram_tensor(name, list(shape), dt, kind="ExternalInput").ap()

    def ext_out(self, name, shape, dt=F32):
        return self.nc.dram_tensor(name, list(shape), dt, kind="ExternalOutput").ap()

    def sb(self, name, shape, dt):
        self.n += 1
        return self.cur.enter_context(self.nc.sbuf_tensor(f"{name}_{self.n}", list(shape), dt))

    def ps(self, name, shape, dt=F32):
        self.n += 1
        return self.cur.enter_context(self.nc.psum_tensor(f"{name}_{self.n}", list(shape), dt))

    def dram_tmp(self, name, shape, dt=F32):
        return self.nc.dram_tensor(name, list(shape), dt, kind="Internal").ap()

    def run_hook(self):
        if self.hook is not None:
            f, self.hook = self.hook, None
            f()

    def push(self):
        st = ExitStack()
        self.scopes.append(st)
        self.cur = st

    def pop(self):
        self.S.barrier()
        if len(self.scopes) == 1:
            self.S.release_dma_sems()
        self.scopes.pop().close()
        self.cur = self.scopes[-1] if self.scopes else self.stack

    def finish(self):
        self.S.finalize()
        self.stack.close()
        return self.nc


class Ring:
    def __init__(self, items):
        self.items = items
        self.i = 0

    def next(self):
        it = self.items[self.i % len(self.items)]
        self.i += 1
        return it


def run_interleaved(gens, width=2, stagger=2):
    it = iter(gens)
    active = []
    steps = 0
    while True:
        while len(active) < width and (not active or steps >= stagger):
            try:
                active.append(next(it))
            except StopIteration:
                break
        if not active:
            break
        steps += 1
        for g in list(active):
            try:
                next(g)
            except StopIteration:
                active.remove(g)


def mk_ring(cx, kind, name, n, shape, dt):
    items = []
    for i in range(n):
        t = cx.sb(f"{name}{i}", shape, dt) if kind == "sb" else cx.ps(f"{name}{i}", shape, dt)
        items.append((t, Buf(f"{name}{i}", excl=(kind == "ps"))))
    return Ring(items)


def emit_rmsnorm(cx, x_t, x_b, nchunk, TB, g_t, g_b, ones_t, ones_b, sq_ring, st_ring, rstd_ring,
                 out_t, out_b, nfeat, evac_engs=("dve",)):
    S = cx.S
    st_t, st_b = st_ring.next()
    for c in range(nchunk):
        sq_t, sq_b = sq_ring.next()
        S.op("act", lambda h, c=c, sq_t=sq_t: h.activation(out=sq_t[:, 0:TB], in_=x_t[:, c, 0:TB], func=AF.Square),
             reads=[x_b], writes=[sq_b])
        S.op("pe", lambda h, c=c, sq_t=sq_t: h.matmul(st_t[:, 0:TB], lhsT=ones_t[:, :], rhs=sq_t[:, 0:TB],
                                                        start=(c == 0), stop=(c == nchunk - 1)),
             reads=[sq_b, ones_b], writes=[st_b])
    r_t, r_b = rstd_ring.next()
    S.op("act", lambda h: h.activation(out=r_t[:, 0:TB], in_=st_t[:, 0:TB], func=AF.Sqrt, bias=float(nfeat * EPS)),
         reads=[st_b], writes=[r_b])
    S.op("dve", lambda h: h.reciprocal(out=r_t[:, 0:TB], in_=r_t[:, 0:TB]), reads=[r_b], writes=[r_b])
    for c in range(nchunk):
        e = evac_engs[c % len(evac_engs)]
        S.op(e, lambda h, c=c: h.scalar_tensor_tensor(out=out_t[:, c, 0:TB], in0=x_t[:, c, 0:TB],
                                                       scalar=g_t[:, c:c + 1], in1=r_t[:, 0:TB],
                                                       op0=ALU.mult, op1=ALU.mult),
             reads=[x_b, r_b, g_b], writes=[out_b])


def build_mlp(final_norm, ntok=TOK, dbg=False, cx=None, wbf16=False):
    TB = 256
    NB = ntok // TB
    FC = DFF // 128
    own = cx is None
    cx = Ctx() if own else cx
    cx.push()
    S = cx.S
    xT = cx.dram_in("xT", [D, ntok])
    w1 = cx.dram_in("w1", [D, DFF], BF16 if wbf16 else F32)
    w2 = cx.dram_in("w2", [DFF, D], BF16 if wbf16 else F32)
    gin = cx.dram_in("g", [128, KC])
    oT = cx.dram_out("oT", [D, ntok])
    if final_norm:
        gfin = cx.dram_in("gf", [128, KC])
    if dbg:
        dh = cx.dram_out("dh", [128, KC, TB], BF16)
        da = cx.dram_out("da", [128, DFF // 128, TB], BF16)

    w1b = cx.sb("w1b", [128, KC, DFF], BF16)
    w2b = cx.sb("w2b", [128, FC, D], BF16)
    w1_bufs = [Buf(f"w1_{k}") for k in range(KC)]
    w2_bufs = [Buf(f"w2_{k}") for k in range(8)]
    g_t = cx.sb("g", [128, KC], F32); g_b = Buf("g")
    ones_t = cx.sb("ones", [128, 128], BF16); ones_b = Buf("ones")
    x_ring = mk_ring(cx, "sb", "x", 2, [128, KC, TB], F32)
    h_ring = mk_ring(cx, "sb", "h", 2, [128, KC, TB], BF16)
    a_ring = mk_ring(cx, "sb", "a", 1, [128, FC, TB], BF16)
    r_ring = mk_ring(cx, "sb", "r", 3, [128, TB], BF16)
    sq_ring = mk_ring(cx, "sb", "sq", 3, [128, TB], BF16)
    rstd_ring = mk_ring(cx, "sb", "rstd", 2, [128, TB], F32)
    o_ring = mk_ring(cx, "sb", "o", 2, [128, KC, TB], F32)
    st_ring = mk_ring(cx, "ps", "st", 1, [128, 512], F32)
    p1_ring = mk_ring(cx, "ps", "p1", 3, [128, 512], F32)
    p2_ring = mk_ring(cx, "ps", "p2", 3, [128, 512], F32)
    if final_norm:
        gf_t = cx.sb("gf", [128, KC], F32); gf_b = Buf("gf")
        f_ring = mk_ring(cx, "sb", "f", 2, [128, KC, TB], F32)

    S.dma("sp", g_t[:, :], gin[:, :], writes=[g_b])
    S.op("dve", lambda h: h.tensor_scalar_mul(out=g_t[:, :], in0=g_t[:, :], scalar1=float(np.sqrt(D))),
         reads=[g_b], writes=[g_b])
    if final_norm:
        S.dma("sp", gf_t[:, :], gfin[:, :], writes=[gf_b])
        S.op("dve", lambda h: h.tensor_scalar_mul(out=gf_t[:, :], in0=gf_t[:, :], scalar1=float(np.sqrt(D))),
             reads=[gf_b], writes=[gf_b])
    S.op("pool", lambda h: h.memset(ones_t[:, :], 1.0), writes=[ones_b])
    w1v = w1.rearrange("(k p) n -> p k n", p=128)
    w2v = w2.rearrange("(f p) n -> p f n", p=128)
    wq = "sp" if wbf16 else "pool"
    for k in range(KC):
        S.dma(wq, w1b[:, k, :], w1v[:, k, :], writes=[w1_bufs[k]])
    for j in range(8):
        S.dma(wq, w2b[:, j * 4:(j + 1) * 4, :], w2v[:, j * 4:(j + 1) * 4, :], writes=[w2_bufs[j]])
    xv = xT.rearrange("(c p) t -> p c t", p=128)
    ov = oT.rearrange("(c p) t -> p c t", p=128)

    def prep(b):
        x_t, x_b = x_ring.next()
        S.dma("sp", x_t[:, :, :], xv[:, :, b * TB:(b + 1) * TB], writes=[x_b])
        h_t, h_b = h_ring.next()
        return (x_t, x_b, h_t, h_b)

    def norm(st_):
        x_t, x_b, h_t, h_b = st_
        emit_rmsnorm(cx, x_t, x_b, KC, TB, g_t, g_b, ones_t, ones_b, sq_ring, st_ring, rstd_ring, h_t, h_b, D)

    cur = prep(0)
    norm(cur)
    for b in range(NB):
        t0 = b * TB
        x_t, x_b, h_t, h_b = cur
        nxt = prep(b + 1) if b + 1 < NB else None
        a_t, a_b = a_ring.next()
        for f in range(FC):
            if f == FC // 2 and nxt is not None:
                norm(nxt)
            p_t, p_b = p1_ring.next()
            for k in range(KC):
                S.op("pe", lambda h, f=f, k=k, p_t=p_t: h.matmul(p_t[:, 0:TB], lhsT=w1b[:, k, f * 128:(f + 1) * 128],
                                                                  rhs=h_t[:, k, 0:TB], start=(k == 0), stop=(k == KC - 1)),
                     reads=[h_b, w1_bufs[k]], writes=[p_b])
            r_t, r_b = r_ring.next()
            S.op("act", lambda h, p_t=p_t, r_t=r_t: h.activation(out=r_t[:, 0:TB], in_=p_t[:, 0:TB], func=AF.Relu),
                 reads=[p_b], writes=[r_b])
            S.op("pool", lambda h, f=f, r_t=r_t: h.tensor_tensor(out=a_t[:, f, 0:TB], in0=r_t[:, 0:TB], in1=r_t[:, 0:TB],
                                                                  op=ALU.mult),
                 reads=[r_b], writes=[a_b])
        if dbg and b == 0:
            S.dma("sp", dh[:, :, :], h_t[:, :, :], reads=[h_b])
            S.dma("sp", da[:, :, :], a_t[:, :, :], reads=[a_b])
        o_t, o_b = o_ring.next()
        for c in range(KC):
            p_t, p_b = p2_ring.next()
            for f in range(FC):
                S.op("pe", lambda h, f=f, c=c, p_t=p_t: h.matmul(p_t[:, 0:TB], lhsT=w2b[:, f, c * 128:(c + 1) * 128],
                                                                  rhs=a_t[:, f, 0:TB], start=(f == 0), stop=(f == FC - 1)),
                     reads=[a_b, w2_bufs[f // 4]], writes=[p_b])
            S.op("dve", lambda h, c=c, p_t=p_t: h.tensor_tensor(out=o_t[:, c, 0:TB], in0=p_t[:, 0:TB], in1=x_t[:, c, 0:TB],
                                                                 op=ALU.add),
                 reads=[p_b, x_b], writes=[o_b])
        if final_norm:
            f_t, f_b = f_ring.next()
            emit_rmsnorm(cx, o_t, o_b, KC, TB, gf_t, gf_b, ones_t, ones_b, sq_ring, st_ring, rstd_ring, f_t, f_b, D)
            S.dma("pool", ov[:, :, t0:t0 + TB], f_t[:, :, :], reads=[f_b])
        else:
            S.dma("pool", ov[:, :, t0:t0 + TB], o_t[:, :, :], reads=[o_b])
        cur = nxt
    cx.pop()
    return cx.finish() if own else None


def run_spmd(nc, in_maps):
    res = run_bass_kernel_spmd(nc, in_maps, core_ids=list(range(len(in_maps))))
    return res.results


def vec128(v, k):
    return np.ascontiguousarray(np.asarray(v, np.float32).reshape(k, 128).T)


def load_cast(cx, q, dst_ap, src_ap, buf):
    cx.S.dma(q, dst_ap, src_ap, writes=[buf])


def build_ea(ntok=TOK, parts='quv', qlvl=4, cx=None):
    TB = 512
    NB = ntok // TB
    own = cx is None
    cx = Ctx() if own else cx
    cx.push()
    S = cx.S
    xT = cx.dram_in("xT", [D, ntok])
    w_in = cx.dram_in("w_in", [D, 1280])
    gin = cx.dram_in("g", [128, KC])
    cosd = cx.dram_in("cos", [128, ntok])
    sind = cx.dram_in("sin", [128, ntok])
    rotd = cx.dram_in("rot", [128, 128])
    QsT = cx.dram_out("QsT", [128, 4, ntok], BF16)
    KT = cx.dram_out("KT", [128, ntok], BF16)
    Vaug = cx.dram_out("Vaug", [ntok, 130], BF16)
    UT = cx.dram_out("UT", [128, 4, ntok], BF16)

    wb = cx.sb("wb", [128, KC, 1280], BF16)
    w_bufs = [Buf(f"w{k}") for k in range(KC)]
    g_t = cx.sb("g", [128, KC], F32); g_b = Buf("g")
    ones_t = cx.sb("ones", [128, 128], BF16); ones_b = Buf("ones")
    rot_t = cx.sb("rot", [128, 128], BF16); rot_b = Buf("rot")
    x_ring = mk_ring(cx, "sb", "x", 2, [128, KC, TB], F32)
    h_ring = mk_ring(cx, "sb", "h", 2, [128, KC, TB], BF16)
    sq_ring = mk_ring(cx, "sb", "sq", 3, [128, TB], BF16)
    rstd_ring = mk_ring(cx, "sb", "rstd", 2, [128, TB], F32)
    cos_ring = mk_ring(cx, "sb", "cos", 2, [128, TB], F32)
    sin_ring = mk_ring(cx, "sb", "sin", 2, [128, TB], F32)
    qb_ring = mk_ring(cx, "sb", "qb", 2, [128, TB], BF16)
    t1_ring = mk_ring(cx, "sb", "t1", 2, [128, TB], F32)
    t2_ring = mk_ring(cx, "sb", "t2", 2, [128, TB], F32)
    qo_ring = mk_ring(cx, "sb", "qo", 2, [128, 5, TB], BF16)
    uo_ring = mk_ring(cx, "sb", "uo", 2, [128, 4, TB], BF16)
    vo_ring = mk_ring(cx, "sb", "vo", 2, [128, 4, 130], BF16)
    st_ring = mk_ring(cx, "ps", "st", 1, [128, 512], F32)
    pq_ring = mk_ring(cx, "ps", "pq", 3, [128, 512], F32)
    pr_ring = mk_ring(cx, "ps", "pr", 2, [128, 512], F32)
    pv_ring = mk_ring(cx, "ps", "pv", 2, [128, 512], F32)

    S.dma("sp", g_t[:, :], gin[:, :], writes=[g_b])
    S.op("dve", lambda h: h.tensor_scalar_mul(out=g_t[:, :], in0=g_t[:, :], scalar1=float(np.sqrt(D))),
         reads=[g_b], writes=[g_b])
    S.op("pool", lambda h: h.memset(ones_t[:, :], 1.0), writes=[ones_b])
    S.dma("pool", rot_t[:, :], rotd[:, :], writes=[rot_b])
    for (vt, vb) in vo_ring.items:
        S.op("pool", lambda h, vt=vt: h.memset(vt[:, :, :], 1.0), writes=[vb])
    for k in range(KC):
        for j in range(2):
            src = w_in[k * 128:(k + 1) * 128, j * 256:(j + 1) * 256].rearrange("p (c d) -> p c d", c=4, d=64)
            dst = wb[:, k, 0:512].rearrange("p (c j d) -> p c j d", c=4, j=2, d=64)[:, :, j, :]
            S.dma("pool", dst, src, writes=[w_bufs[k]])
        S.dma("pool", wb[:, k, 512:1280], w_in[k * 128:(k + 1) * 128, 512:1280], writes=[w_bufs[k]])
    xv = xT.rearrange("(c p) t -> p c t", p=128)

    def blk(b):
        t0 = b * TB
        x_t, x_b = x_ring.next()
        S.dma("sp", x_t[:, :, :], xv[:, :, t0:t0 + TB], writes=[x_b])
        cos_t, cos_b = cos_ring.next()
        sin_t, sin_b = sin_ring.next()
        S.dma("sp", cos_t[:, :], cosd[:, t0:t0 + TB], writes=[cos_b])
        S.dma("sp", sin_t[:, :], sind[:, t0:t0 + TB], writes=[sin_b])
        h_t, h_b = h_ring.next()
        emit_rmsnorm(cx, x_t, x_b, KC, TB, g_t, g_b, ones_t, ones_b, sq_ring, st_ring, rstd_ring, h_t, h_b, D)
        yield
        qo_t, qo_b = qo_ring.next()
        for c in (range(5) if 'q' in parts else []):
            pq_t, pq_b = pq_ring.next()
            for k in range(KC):
                S.op("pe", lambda h, c=c, k=k, pq_t=pq_t, h_t=h_t: h.matmul(
                    pq_t[:, 0:TB], lhsT=wb[:, k, c * 128:(c + 1) * 128], rhs=h_t[:, k, :],
                    start=(k == 0), stop=(k == KC - 1)), reads=[h_b, w_bufs[k]], writes=[pq_b])
            qb_t, qb_b = qb_ring.next()
            S.op("act", lambda h, pq_t=pq_t, qb_t=qb_t: h.activation(out=qb_t[:, :], in_=pq_t[:, 0:TB], func=AF.Copy),
                 reads=[pq_b], writes=[qb_b])
            if qlvl == 1:
                S.op("act", lambda h, c=c, pq_t=pq_t, qo_t=qo_t: h.activation(out=qo_t[:, c, :], in_=pq_t[:, 0:TB], func=AF.Copy),
                     reads=[pq_b], writes=[qo_b])
                continue
            pr_t, pr_b = pr_ring.next()
            S.op("pe", lambda h, pr_t=pr_t, qb_t=qb_t: h.matmul(pr_t[:, 0:TB], lhsT=rot_t[:, :], rhs=qb_t[:, :],
                                                               start=True, stop=True),
                 reads=[qb_b, rot_b], writes=[pr_b])
            t1_t, t1_b = t1_ring.next()
            t2_t, t2_b = t2_ring.next()
            if qlvl == 2:
                S.op("act", lambda h, c=c, pr_t=pr_t, qo_t=qo_t: h.activation(out=qo_t[:, c, :], in_=pr_t[:, 0:TB], func=AF.Copy),
                     reads=[pr_b], writes=[qo_b])
                continue
            S.op("dve", lambda h, t1_t=t1_t, pq_t=pq_t, cos_t=cos_t: h.tensor_tensor(
                out=t1_t[:, :], in0=pq_t[:, 0:TB], in1=cos_t[:, :], op=ALU.mult), reads=[pq_b, cos_b], writes=[t1_b])
            if qlvl == 3:
                S.op("act", lambda h, c=c, t1_t=t1_t, qo_t=qo_t: h.activation(out=qo_t[:, c, :], in_=t1_t[:, :], func=AF.Copy),
                     reads=[t1_b], writes=[qo_b])
                continue
            S.op("dve", lambda h, t2_t=t2_t, pr_t=pr_t, sin_t=sin_t: h.tensor_tensor(
                out=t2_t[:, :], in0=pr_t[:, 0:TB], in1=sin_t[:, :], op=ALU.mult), reads=[pr_b, sin_b], writes=[t2_b])
            S.op("dve", lambda h, c=c, qo_t=qo_t, t1_t=t1_t, t2_t=t2_t: h.tensor_tensor(
                out=qo_t[:, c, :], in0=t1_t[:, :], in1=t2_t[:, :], op=ALU.add), reads=[t1_b, t2_b], writes=[qo_b])
        if 'q' in parts:
            S.dma("pool", QsT[:, :, t0:t0 + TB], qo_t[:, 0:4, :], reads=[qo_b])
            S.dma("pool", KT[:, t0:t0 + TB], qo_t[:, 4, :], reads=[qo_b])
        yield
        uo_t, uo_b = uo_ring.next()
        for gi in (range(4) if 'u' in parts else []):
            pq_t, pq_b = pq_ring.next()
            for k in range(KC):
                S.op("pe", lambda h, gi=gi, k=k, pq_t=pq_t, h_t=h_t: h.matmul(
                    pq_t[:, 0:TB], lhsT=wb[:, k, 768 + gi * 128:768 + (gi + 1) * 128], rhs=h_t[:, k, :],
                    start=(k == 0), stop=(k == KC - 1)), reads=[h_b, w_bufs[k]], writes=[pq_b])
            S.op("act", lambda h, gi=gi, pq_t=pq_t, uo_t=uo_t: h.activation(out=uo_t[:, gi, :], in_=pq_t[:, 0:TB], func=AF.Copy),
                 reads=[pq_b], writes=[uo_b])
        if 'u' in parts:
            S.dma("pool", UT[:, :, t0:t0 + TB], uo_t[:, :, :], reads=[uo_b])
        if 'v' not in parts:
            return
        yield
        vo_t, vo_b = vo_ring.next()
        pv_t, pv_b = pv_ring.next()
        for ti in range(TB // 128):
            for k in range(KC):
                S.op("pe", lambda h, ti=ti, k=k, pv_t=pv_t, h_t=h_t: h.matmul(
                    pv_t[:, ti * 128:(ti + 1) * 128], lhsT=h_t[:, k, ti * 128:(ti + 1) * 128], rhs=wb[:, k, 640:768],
                    start=(k == 0), stop=(k == KC - 1)), reads=[h_b, w_bufs[k]], writes=[pv_b])
        for ti in range(TB // 128):
            for j in range(2):
                S.op("act", lambda h, ti=ti, j=j, pv_t=pv_t, vo_t=vo_t: h.activation(
                    out=vo_t[:, ti, j * 65:j * 65 + 64], in_=pv_t[:, ti * 128 + j * 64:ti * 128 + (j + 1) * 64], func=AF.Copy),
                    reads=[pv_b], writes=[vo_b])
        S.dma("pool", Vaug[t0:t0 + TB, :].rearrange("(i p) n -> p i n", p=128), vo_t[:, :, :], reads=[vo_b])
        yield

    run_interleaved((blk(b) for b in range(NB)), 2, 2)
    cx.pop()
    return cx.finish() if own else None


def rope_tables(pos, half, nrows):
    inv = (np.float32(10000.0) ** (-np.arange(half, dtype=np.float32) / np.float32(half))).astype(np.float32)
    ang = pos.astype(np.float32)[None, :] * inv[np.arange(nrows) % half][:, None]
    return np.cos(ang).astype(np.float32), np.sin(ang).astype(np.float32)


def rot_matrix(dh, nrows=128):
    R = np.zeros((nrows, nrows), np.float32)
    half = dh // 2
    for m in range(nrows):
        d = m % dh
        base = m - d
        if d < half:
            R[base + d + half, m] = -1.0
        else:
            R[base + d - half, m] = 1.0
    return R


def build_eb(ntok=TOK, cx=None):
    TB = 512
    NB = ntok // TB
    NT = ntok // 128
    own = cx is None
    cx = Ctx() if own else cx
    cx.push()
    S = cx.S
    QsT = cx.dram_in("QsT", [128, 4, ntok], BF16)
    KTh = cx.dram_in("KTh", [128, ntok + 256], BF16)
    Vh = cx.dram_in("Vh", [ntok + 256, 130], BF16)
    UTh = cx.dram_in("UTh", [128, 4, ntok + 16], BF16)
    xT = cx.dram_in("xT", [D, ntok])
    w_pool = cx.dram_in("w_pool", [4, 128, 128])
    pscale = cx.dram_in("pscale", [128, 4])
    w_out = cx.dram_in("w_out", [D, D])
    sinkrow = cx.dram_in("sinkrow", [1, 2, 512])
    masksd = cx.dram_in("masks", [4, 128, 512])
    invcd = cx.dram_in("invc", [128, 2, 4, 16])
    oT = cx.dram_out("oT", [D, ntok])

    woA = cx.sb("woA", [128, 4, D], BF16); woA_b = Buf("woA")
    woB = cx.sb("woB", [128, 4, D], BF16); woB_b = Buf("woB")
    wp = cx.sb("wp", [128, 4, 128], BF16); wp_b = Buf("wp")
    ps_t = cx.sb("ps", [128, 4], F32); ps_b = Buf("ps")
    mk_t = cx.sb("mk", [128, 4, 512], BF16); mk_b = Buf("mk")
    invc_t = cx.sb("invc", [128, 2, 4, 16], F32); invc_b = Buf("invc")
    sk_t = cx.sb("sk", [1, 2, 512], F32); sk_b = Buf("sk")
    esk_t = cx.sb("esk", [1, 2, 512], BF16); esk_b = Buf("esk")
    sel_t = cx.sb("sel", [1, 128], BF16); sel_b = Buf("sel")
    ones32 = cx.sb("ones32", [128, 64], F32); ones32_b = Buf("ones32")
    qs_ring = mk_ring(cx, "sb", "qs", 2, [128, 4, TB], BF16)
    kt_ring = mk_ring(cx, "sb", "kt", 2, [128, 6 * 128], BF16)
    v_ring = mk_ring(cx, "sb", "v", 2, [128, 6, 130], BF16)
    u_ring = mk_ring(cx, "sb", "u", 2, [128, 4, TB + 16], BF16)
    x_ring = mk_ring(cx, "sb", "x", 2, [128, KC, TB], F32)
    p_ring = mk_ring(cx, "sb", "p", 4, [128, 512], BF16)
    osb_ring = mk_ring(cx, "sb", "osb", 3, [64, 512], F32)
    rc_ring = mk_ring(cx, "sb", "rc", 3, [128, 512], F32)
    ya_ring = mk_ring(cx, "sb", "ya", 2, [64, 8, TB], BF16)
    yp_ring = mk_ring(cx, "sb", "yp", 2, [128, 4, TB], BF16)
    yb_ring = mk_ring(cx, "sb", "yb", 2, [128, 4, TB], BF16)
    d_ring = mk_ring(cx, "sb", "d", 2, [128, 4, TB], BF16)
    tmp_rings = [mk_ring(cx, "sb", f"tp{g}", 2, [128, TB + 16], F32) for g in range(4)]
    e16_ring = mk_ring(cx, "sb", "e16", 2, [128, 16], F32)
    s_ring = mk_ring(cx, "ps", "s", 3, [128, 512], F32)
    o_ring = mk_ring(cx, "ps", "o", 2, [128, 512], F32)
    bc_ring = mk_ring(cx, "ps", "bc", 1, [128, 512], F32)
    y_ring = mk_ring(cx, "ps", "y", 2, [128, 512], F32)

    S.dma("pool", woA[:, :, :], w_out[0:512, :].rearrange("(i p) n -> p i n", p=128), writes=[woA_b])
    S.dma("pool", woB[:, :, :], w_out[512:1024, :].rearrange("(g p) n -> p g n", p=128), writes=[woB_b])
    S.dma("pool", wp[:, :, :], w_pool.rearrange("g i j -> i g j"), writes=[wp_b])
    S.dma("pool", mk_t[:, :, :], masksd.rearrange("m p n -> p m n"), writes=[mk_b])
    S.dma("sp", ps_t[:, :], pscale[:, :], writes=[ps_b])
    S.dma("sp", invc_t[:, :, :, :], invcd[:, :, :, :], writes=[invc_b])
    S.dma("sp", sk_t[:, :, :], sinkrow[:, :, :], writes=[sk_b])
    S.op("act", lambda h: h.activation(out=esk_t[:, :, :], in_=sk_t[:, :, :], func=AF.Exp), reads=[sk_b], writes=[esk_b])
    S.op("pool", lambda h: h.memset(sel_t[:, :], 0.0), writes=[sel_b])
    S.op("pool", lambda h: h.memset(sel_t[:, 64:65], 1.0), writes=[sel_b])
    S.op("pool", lambda h: h.memset(ones32[:, :], 1.0), writes=[ones32_b])
    xv = xT.rearrange("(c p) t -> p c t", p=128)
    ov = oT.rearrange("(c p) t -> p c t", p=128)
    cx.run_hook()

    def blk(b):
        t0 = b * TB
        qs_t, qs_b = qs_ring.next()
        kt_t, kt_b = kt_ring.next()
        v_t, v_b = v_ring.next()
        u_t, u_b = u_ring.next()
        x_t, x_b = x_ring.next()
        S.dma("sp", qs_t[:, :, :], QsT[:, :, t0:t0 + TB], writes=[qs_b])
        S.dma("sp", kt_t[:, :], KTh[:, t0:t0 + 768], writes=[kt_b])
        S.dma("sp", v_t[:, :, :], Vh[t0:t0 + 768, :].rearrange("(i p) n -> p i n", p=128), writes=[v_b])
        S.dma("sp", u_t[:, :, :], UTh[:, :, t0:t0 + TB + 16], writes=[u_b])
        S.dma("sp", x_t[:, :, :], xv[:, :, t0:t0 + TB], writes=[x_b])
        yield
        ya_t, ya_b = ya_ring.next()
        tiles = [(nl, j, mi, dm) for nl in range(4) for mi, dm in enumerate((-1, 0, 1)) for j in range(2)]
        LA = 2
        st = {}
        unit_o = {}
        deferred = []

        def emit_S(t):
            nl, j, mi, dm = tiles[t]
            i = nl + dm + 1
            s_t, s_b = s_ring.next()
            S.op("pe", lambda h: h.matmul(s_t[:, :], lhsT=kt_t[j * 64:(j + 1) * 64, i * 128:(i + 1) * 128],
                                          rhs=qs_t[j * 64:(j + 1) * 64, :, nl * 128:(nl + 1) * 128], start=True, stop=True),
                 reads=[kt_b, qs_b], writes=[s_b])
            st[t] = (s_t, s_b)

        def flush_deferred():
            while deferred:
                (o_t, o_b, osb_t, osb_b, rc_t, rc_b, nl, j) = deferred.pop(0)
                bc_t, bc_b = bc_ring.next()
                S.op("pe", lambda h: h.matmul(bc_t[0:64, :], lhsT=ones32[64:65, 0:64], rhs=rc_t[64:65, :], start=True, stop=True),
                     reads=[rc_b, ones32_b], writes=[bc_b])
                S.op("dve", lambda h: h.tensor_tensor(
                    out=ya_t[0:64, j * 4:(j + 1) * 4, nl * 128:(nl + 1) * 128],
                    in0=osb_t[:, :].rearrange("p (c q) -> p c q", c=4),
                    in1=bc_t[0:64, :].rearrange("p (c q) -> p c q", c=4), op=ALU.mult),
                    reads=[osb_b, bc_b], writes=[ya_b])

        for t in range(min(LA, len(tiles))):
            emit_S(t)
        for t in range(len(tiles)):
            nl, j, mi, dm = tiles[t]
            n = 4 * b + nl
            i = nl + dm + 1
            if mi == 0:
                unit_o[(nl, j)] = o_ring.next()
            o_t, o_b = unit_o[(nl, j)]
            s_t, s_b = st.pop(t)
            p_t, p_b = p_ring.next()
            S.op("act", lambda h: h.activation(out=p_t[:, :], in_=s_t[:, :], func=AF.Exp, scale=0.125), reads=[s_b], writes=[p_b])
            if dm != 0:
                if dm == -1:
                    mi_ = 2 if n == 0 else 0
                else:
                    mi_ = 3 if n == NT - 1 else 1
                S.op("pool", lambda h: h.tensor_tensor(out=p_t[:, :], in0=p_t[:, :], in1=mk_t[:, mi_, :], op=ALU.mult),
                     reads=[p_b, mk_b], writes=[p_b])
            if t + LA < len(tiles):
                emit_S(t + LA)
            S.op("pe", lambda h: h.matmul(o_t[0:65, :], lhsT=v_t[:, i, j * 65:(j + 1) * 65], rhs=p_t[:, :], start=(mi == 0), stop=False),
                 reads=[v_b, p_b], writes=[o_b])
            if mi == 0 and j == 1:
                flush_deferred()
            if mi == 2:
                S.op("pe", lambda h: h.matmul(o_t[0:65, :], lhsT=sel_t[0:1, 0:65], rhs=esk_t[0:1, j, :], start=False, stop=True),
                     reads=[sel_b, esk_b], writes=[o_b])
                osb_t, osb_b = osb_ring.next()
                rc_t, rc_b = rc_ring.next()
                S.op("act", lambda h: h.activation(out=osb_t[:, :], in_=o_t[0:64, :], func=AF.Copy), reads=[o_b], writes=[osb_b])
                S.op("dve", lambda h: h.reciprocal(out=rc_t[64:65, :], in_=o_t[64:65, :]), reads=[o_b], writes=[rc_b])
                deferred.append((o_t, o_b, osb_t, osb_b, rc_t, rc_b, nl, j))
        flush_deferred()
        yield
        yp_t, yp_b = yp_ring.next()
        S.dma("sp", yp_t[0:64, :, :], ya_t[0:64, 0:8:2, :], reads=[ya_b], writes=[yp_b])
        S.dma("sp", yp_t[64:128, :, :], ya_t[0:64, 1:8:2, :], reads=[ya_b], writes=[yp_b])
        d_t, d_b = d_ring.next()
        L = TB + 16
        for g in range(4):
            w = 2 << g
            steps = g + 1
            src_t, src_b, ln = None, None, L
            for s_i in range(steps):
                sh = 1 << s_i
                tp_t, tp_b = tmp_rings[g].next()
                nl_ = ln - sh
                if s_i == 0:
                    S.op("pool", lambda h, tp_t=tp_t, u_t=u_t, g=g, nl_=nl_, sh=sh: h.tensor_tensor(
                        out=tp_t[:, 0:nl_], in0=u_t[:, g, 0:nl_], in1=u_t[:, g, sh:sh + nl_], op=ALU.add),
                        reads=[u_b], writes=[tp_b])
                else:
                    S.op("pool", lambda h, tp_t=tp_t, src_t=src_t, nl_=nl_, sh=sh: h.tensor_tensor(
                        out=tp_t[:, 0:nl_], in0=src_t[:, 0:nl_], in1=src_t[:, sh:sh + nl_], op=ALU.add),
                        reads=[src_b], writes=[tp_b])
                src_t, src_b, ln = tp_t, tp_b, nl_
            off = 8 - w // 2
            S.op("dve", lambda h, d_t=d_t, src_t=src_t, u_t=u_t, g=g, off=off, w=w: h.scalar_tensor_tensor(
                out=d_t[:, g, :], in0=src_t[:, off:off + TB], scalar=1.0 / w, in1=u_t[:, g, 8:8 + TB],
                op0=ALU.mult, op1=ALU.subtract), reads=[src_b, u_b], writes=[d_b])
            for (is_edge, which, c0) in ((b == 0, 0, 0), (b == NB - 1, 1, TB - 16)):
                if not is_edge:
                    continue
                e_t, e_b = e16_ring.next()
                S.op("dve", lambda h, e_t=e_t, src_t=src_t, g=g, off=off, c0=c0, which=which: h.tensor_tensor(
                    out=e_t[:, :], in0=src_t[:, off + c0:off + c0 + 16], in1=invc_t[:, which, g, :], op=ALU.mult),
                    reads=[src_b, invc_b], writes=[e_b])
                S.op("dve", lambda h, e_t=e_t, d_t=d_t, u_t=u_t, g=g, c0=c0: h.tensor_tensor(
                    out=d_t[:, g, c0:c0 + 16], in0=e_t[:, :], in1=u_t[:, g, 8 + c0:8 + c0 + 16], op=ALU.subtract),
                    reads=[e_b, u_b, d_b], writes=[d_b])
        yield
        yb_t, yb_b = yb_ring.next()
        for g in range(4):
            y_t, y_b = y_ring.next()
            S.op("pe", lambda h, y_t=y_t, d_t=d_t, g=g: h.matmul(y_t[:, :], lhsT=wp[:, g, :], rhs=d_t[:, g, :], start=True, stop=True),
                 reads=[wp_b, d_b], writes=[y_b])
            S.op("dve", lambda h, y_t=y_t, yb_t=yb_t, g=g: h.tensor_scalar_mul(out=yb_t[:, g, :], in0=y_t[:, :], scalar1=ps_t[:, g:g + 1]),
                 reads=[y_b, ps_b], writes=[yb_b])
        for o in range(KC):
            y_t, y_b = y_ring.next()
            for hh in range(4):
                S.op("pe", lambda h, y_t=y_t, yp_t=yp_t, hh=hh, o=o: h.matmul(
                    y_t[:, :], lhsT=woA[:, hh, o * 128:(o + 1) * 128], rhs=yp_t[:, hh, :], start=(hh == 0), stop=False),
                    reads=[woA_b, yp_b], writes=[y_b])
            for g in range(4):
                S.op("pe", lambda h, y_t=y_t, yb_t=yb_t, g=g, o=o: h.matmul(
                    y_t[:, :], lhsT=woB[:, g, o * 128:(o + 1) * 128], rhs=yb_t[:, g, :], start=False, stop=(g == 3)),
                    reads=[woB_b, yb_b], writes=[y_b])
            S.op("dve", lambda h, y_t=y_t, x_t=x_t, o=o: h.tensor_tensor(out=x_t[:, o, :], in0=y_t[:, :], in1=x_t[:, o, :], op=ALU.add),
                 reads=[y_b, x_b], writes=[x_b])
        S.dma("sp", ov[:, :, t0:t0 + TB], x_t[:, :, :], reads=[x_b])
        yield

    run_interleaved((blk(b) for b in range(NB)), 2, 2)
    cx.pop()
    return cx.finish() if own else None


def eb_masks(has_left, has_right):
    ki = np.arange(128)[:, None]
    qi = np.arange(128)[None, :]
    mL = np.tile((ki >= qi).astype(np.float32), (1, 4))
    mR = np.tile((ki <= qi).astype(np.float32), (1, 4))
    return np.stack([mL, mR, mL * float(has_left), mR * float(has_right)]).astype(np.float32)


def eb_invc(is_first, is_last):
    out = np.zeros((128, 2, 4, 16), np.float32)
    for g in range(4):
        w = 2 << g
        half = w // 2
        for i in range(16):
            c0 = min(i + half, w) if is_first else w
            r = 16 - i
            c1 = min(half + r, w) if is_last else w
            out[:, 0, g, i] = 1.0 / c0
            out[:, 1, g, i] = 1.0 / c1
    return out


def emit_rope(cx, src_t, src_b, nrow, TB, rot_t, rot_b, cos_t, cos_b, sin_t, sin_b, qb_ring, pr_ring, t1_ring, t2_ring,
              out_ap, out_b):
    S = cx.S
    qb_t, qb_b = qb_ring.next()
    S.op("act", lambda h: h.activation(out=qb_t[0:nrow, :], in_=src_t[0:nrow, 0:TB], func=AF.Copy), reads=[src_b], writes=[qb_b])
    pr_t, pr_b = pr_ring.next()
    S.op("pe", lambda h: h.matmul(pr_t[0:nrow, 0:TB], lhsT=rot_t[0:nrow, 0:nrow], rhs=qb_t[0:nrow, :], start=True, stop=True),
         reads=[qb_b, rot_b], writes=[pr_b])
    t1_t, t1_b = t1_ring.next()
    t2_t, t2_b = t2_ring.next()
    S.op("dve", lambda h: h.tensor_tensor(out=t1_t[0:nrow, :], in0=src_t[0:nrow, 0:TB], in1=cos_t[0:nrow, :], op=ALU.mult),
         reads=[src_b, cos_b], writes=[t1_b])
    S.op("dve", lambda h: h.tensor_tensor(out=t2_t[0:nrow, :], in0=pr_t[0:nrow, 0:TB], in1=sin_t[0:nrow, :], op=ALU.mult),
         reads=[pr_b, sin_b], writes=[t2_b])
    S.op("dve", lambda h: h.tensor_tensor(out=out_ap, in0=t1_t[0:nrow, :], in1=t2_t[0:nrow, :], op=ALU.add),
         reads=[t1_b, t2_b], writes=[out_b])


def build_oa(ntok=TOK, cx=None, mid_hook=None):
    TB = 512
    NB = ntok // TB
    NT = ntok // 128
    own = cx is None
    cx = Ctx() if own else cx
    cx.push()
    S = cx.S
    xT = cx.dram_in("xT", [D, ntok])
    xhalo = cx.dram_in("xhalo", [D, 4])
    w_in = cx.dram_in("w_in", [D, 1440])
    gin = cx.dram_in("g", [128, KC])
    gcq = cx.dram_in("g_cq", [128, 2])
    gckv = cx.dram_in("g_ckv", [128, 1])
    w_uq = cx.dram_in("w_uq", [256, 768])
    w_ukv = cx.dram_in("w_ukv", [128, 1024])
    cwd = cx.dram_in("cw", [128, 4, 4])
    cbd = cx.dram_in("cb", [128, 4])
    wad = cx.dram_in("wa", [2, 8, 64, 64])
    wxd = cx.dram_in("wx", [2, 8, 64, 64])
    bad = cx.dram_in("ba", [128, 2, 4])
    bxd = cx.dram_in("bx", [128, 2, 4])
    lamd = cx.dram_in("lam", [128, 2, 4])
    cosd = cx.dram_in("cos", [128, ntok])
    sind = cx.dram_in("sin", [128, ntok])
    rotd = cx.dram_in("rot", [128, 128])
    QN = cx.dram_out("QN", [512, ntok], BF16)
    QR = cx.dram_out("QR", [256, ntok], BF16)
    KNR = cx.dram_out("KNR", [544, ntok], BF16)
    V5 = cx.dram_out("V5", [1024, NT * 65], BF16)
    GX = cx.dram_out("GX", [512, ntok], BF16)
    AB = cx.dram_out("AB", [2, 2, 512, ntok])
    BLK = cx.dram_out("BLK", [128, NB, 2, 2, 4])
    CAB = cx.dram_out("CAB", [128, 2, 2, 4])
    XR = cx.dram_out("XR", [512, ntok])

    g_t = cx.sb("g", [128, KC], F32); g_b = Buf("g")
    gcq_t = cx.sb("gcq", [128, 2], F32); gcq_b = Buf("gcq")
    gckv_t = cx.sb("gckv", [128, 1], F32); gckv_b = Buf("gckv")
    ones_t = cx.sb("ones", [128, 128], BF16); ones_b = Buf("ones")
    xrh_t = cx.sb("xrh", [128, 4, 4], F32); xrh_b = Buf("xrh")
    cp_t = cx.sb("cp", [128, 2, 4], F32); cp_b = Buf("cp")
    blk_t = cx.sb("blk", [128, NB, 2, 2, 4], F32); blk_b = Buf("blk")
    S.dma("sp", g_t[:, :], gin[:, :], writes=[g_b])
    S.op("dve", lambda h: h.tensor_scalar_mul(out=g_t[:, :], in0=g_t[:, :], scalar1=float(np.sqrt(D))), reads=[g_b], writes=[g_b])
    S.dma("sp", gcq_t[:, :], gcq[:, :], writes=[gcq_b])
    S.op("dve", lambda h: h.tensor_scalar_mul(out=gcq_t[:, :], in0=gcq_t[:, :], scalar1=16.0), reads=[gcq_b], writes=[gcq_b])
    S.dma("sp", gckv_t[:, :], gckv[:, :], writes=[gckv_b])
    S.op("dve", lambda h: h.tensor_scalar_mul(out=gckv_t[:, :], in0=gckv_t[:, :], scalar1=float(np.sqrt(128.0))),
         reads=[gckv_b], writes=[gckv_b])
    S.op("pool", lambda h: h.memset(ones_t[:, :], 1.0), writes=[ones_b])
    S.dma("sp", cp_t[:, :, :], lamd[:, :, :], writes=[cp_b])
    S.op("act", lambda h: h.activation(out=cp_t[:, :, :], in_=cp_t[:, :, :], func=AF.Exp, scale=-1.0), reads=[cp_b], writes=[cp_b])
    S.op("act", lambda h: h.activation(out=cp_t[:, :, :], in_=cp_t[:, :, :], func=AF.Ln, bias=1.0), reads=[cp_b], writes=[cp_b])
    S.op("dve", lambda h: h.tensor_scalar_mul(out=cp_t[:, :, :], in0=cp_t[:, :, :], scalar1=-8.0), reads=[cp_b], writes=[cp_b])

    xv = xT.rearrange("(c p) t -> p c t", p=128)
    cx.push()
    wb = cx.sb("wb", [128, KC, 1440], BF16)
    w_bufs = [Buf(f"w{k}") for k in range(KC)]
    wuqn = cx.sb("wuqn", [128, 2, 512], BF16); wuqr = cx.sb("wuqr", [128, 2, 256], BF16); wuq_b = Buf("wuq")
    wk = cx.sb("wk", [128, 512], BF16); wv = cx.sb("wv", [128, 512], BF16); wkv_b = Buf("wkv")
    rot_t = cx.sb("rot", [128, 128], BF16); rot_b = Buf("rot")
    x_ring = mk_ring(cx, "sb", "x", 2, [128, KC, TB], F32)
    h_ring = mk_ring(cx, "sb", "h", 2, [128, KC, TB], BF16)
    sq_ring = mk_ring(cx, "sb", "sq", 3, [128, TB], BF16)
    rstd_ring = mk_ring(cx, "sb", "rstd", 2, [128, TB], F32)
    cos_ring = mk_ring(cx, "sb", "cos", 2, [128, TB], F32)
    sin_ring = mk_ring(cx, "sb", "sin", 2, [128, TB], F32)
    qb_ring = mk_ring(cx, "sb", "qb", 2, [128, TB], BF16)
    t1_ring = mk_ring(cx, "sb", "t1", 2, [128, TB], F32)
    t2_ring = mk_ring(cx, "sb", "t2", 2, [128, TB], F32)
    cq_ring = mk_ring(cx, "sb", "cq", 2, [128, 2, TB], F32)
    ckv_ring = mk_ring(cx, "sb", "ckv", 2, [128, 1, TB], F32)
    cqn_ring = mk_ring(cx, "sb", "cqn", 2, [128, 2, TB], BF16)
    ckvn_ring = mk_ring(cx, "sb", "ckvn", 2, [128, 1, TB], BF16)
    xr_ring = mk_ring(cx, "sb", "xr", 2, [128, 4, TB], F32)
    gx_ring = mk_ring(cx, "sb", "gx", 2, [128, 4, TB], BF16)
    qn_ring = mk_ring(cx, "sb", "qn", 2, [128, 4, TB], BF16)
    qr_ring = mk_ring(cx, "sb", "qr", 2, [128, 2, TB], BF16)
    kn_ring = mk_ring(cx, "sb", "kn", 2, [128, 4, TB], BF16)
    kr_ring = mk_ring(cx, "sb", "kr", 2, [32, TB], BF16)
    vo_ring = mk_ring(cx, "sb", "vo", 2, [128, 4, 520], BF16)
    hx_t = cx.sb("hx", [128, KC, 4], F32); hx_b = Buf("hx")
    hh_t = cx.sb("hh", [128, KC, 4], BF16); hh_b = Buf("hh")
    st_ring = mk_ring(cx, "ps", "st", 1, [128, 512], F32)
    pq_ring = mk_ring(cx, "ps", "pq", 4, [128, 512], F32)
    pr_ring = mk_ring(cx, "ps", "pr", 1, [128, 512], F32)
    pv_ring = mk_ring(cx, "ps", "pv", 2, [128, 512], F32)

    for k in range(KC):
        S.dma("pool", wb[:, k, :], w_in[k * 128:(k + 1) * 128, :], writes=[w_bufs[k]])
    for k in range(2):
        src = w_uq[k * 128:(k + 1) * 128, :].rearrange("p (h e) -> p h e", e=96)
        S.dma("pool", wuqn[:, k, :].rearrange("p (h d) -> p h d", d=64), src[:, :, 0:64], writes=[wuq_b])
        S.dma("pool", wuqr[:, k, :].rearrange("p (h d) -> p h d", d=32), src[:, :, 64:96], writes=[wuq_b])
    srckv = w_ukv.rearrange("p (h e) -> p h e", e=128)
    S.dma("pool", wk[:, :].rearrange("p (h d) -> p h d", d=64), srckv[:, :, 0:64], writes=[wkv_b])
    S.dma("pool", wv[:, :].rearrange("p (h d) -> p h d", d=64), srckv[:, :, 64:128], writes=[wkv_b])
    S.dma("pool", rot_t[:, :], rotd[:, :], writes=[rot_b])
    for (vt, vb) in vo_ring.items:
        S.op("pool", lambda h, vt=vt: h.memset(vt[:, :, :], 1.0), writes=[vb])

    def proj_tile(h_t, h_b, c0, ncols, TBx):
        pq_t, pq_b = pq_ring.next()
        for k in range(KC):
            S.op("pe", lambda h, k=k: h.matmul(pq_t[0:ncols, 0:TBx], lhsT=wb[:, k, c0:c0 + ncols], rhs=h_t[:, k, 0:TBx],
                                               start=(k == 0), stop=(k == KC - 1)), reads=[h_b, w_bufs[k]], writes=[pq_b])
        return pq_t, pq_b

    S.dma("sp", hx_t[:, :, :], xhalo.rearrange("(c p) t -> p c t", p=128), writes=[hx_b])
    emit_rmsnorm(cx, hx_t, hx_b, KC, 4, g_t, g_b, ones_t, ones_b, sq_ring, st_ring, rstd_ring, hh_t, hh_b, D)
    for c in range(4):
        pq_t, pq_b = proj_tile(hh_t, hh_b, 416 + c * 128, 128, 4)
        S.op("act", lambda h, c=c, pq_t=pq_t: h.activation(out=xrh_t[:, c, :], in_=pq_t[:, 0:4], func=AF.Copy),
             reads=[pq_b], writes=[xrh_b])

    def blk(b):
        t0 = b * TB
        x_t, x_b = x_ring.next()
        S.dma("sp", x_t[:, :, :], xv[:, :, t0:t0 + TB], writes=[x_b])
        cos_t, cos_b = cos_ring.next()
        sin_t, sin_b = sin_ring.next()
        S.dma("sp", cos_t[:, :], cosd[:, t0:t0 + TB], writes=[cos_b])
        S.dma("sp", sin_t[:, :], sind[:, t0:t0 + TB], writes=[sin_b])
        h_t, h_b = h_ring.next()
        emit_rmsnorm(cx, x_t, x_b, KC, TB, g_t, g_b, ones_t, ones_b, sq_ring, st_ring, rstd_ring, h_t, h_b, D)
        yield
        cq_t, cq_b = cq_ring.next()
        for c in range(2):
            pq_t, pq_b = proj_tile(h_t, h_b, c * 128, 128, TB)
            S.op("act", lambda h, c=c, pq_t=pq_t, cq_t=cq_t: h.activation(out=cq_t[:, c, :], in_=pq_t[:, 0:TB], func=AF.Copy),
                 reads=[pq_b], writes=[cq_b])
        ckv_t, ckv_b = ckv_ring.next()
        pq_t, pq_b = proj_tile(h_t, h_b, 256, 128, TB)
        S.op("act", lambda h, pq_t=pq_t, ckv_t=ckv_t: h.activation(out=ckv_t[:, 0, :], in_=pq_t[:, 0:TB], func=AF.Copy),
             reads=[pq_b], writes=[ckv_b])
        yield
        pq_t, pq_b = proj_tile(h_t, h_b, 384, 32, TB)
        kr_t, kr_b = kr_ring.next()
        emit_rope(cx, pq_t, pq_b, 32, TB, rot_t, rot_b, cos_t, cos_b, sin_t, sin_b, qb_ring, pr_ring, t1_ring, t2_ring,
                  kr_t[0:32, :], kr_b)
        S.dma("pool", KNR[512:544, t0:t0 + TB], kr_t[:, :], reads=[kr_b])
        yield
        xr_t, xr_b = xr_ring.next()
        gx_t, gx_b = gx_ring.next()
        for c in range(4):
            pq_t, pq_b = proj_tile(h_t, h_b, 416 + c * 128, 128, TB)
            S.op("act", lambda h, c=c, pq_t=pq_t, xr_t=xr_t: h.activation(out=xr_t[:, c, :], in_=pq_t[:, 0:TB], func=AF.Copy),
                 reads=[pq_b], writes=[xr_b])
        for c in range(4):
            pq_t, pq_b = proj_tile(h_t, h_b, 928 + c * 128, 128, TB)
            S.op("act", lambda h, c=c, pq_t=pq_t, gx_t=gx_t: h.activation(out=gx_t[:, c, :], in_=pq_t[:, 0:TB], func=AF.Gelu_apprx_tanh),
                 reads=[pq_b], writes=[gx_b])
        S.dma("pool", XR.rearrange("(c p) t -> p c t", p=128)[:, :, t0:t0 + TB], xr_t[:, :, :], reads=[xr_b])
        S.dma("pool", GX.rearrange("(c p) t -> p c t", p=128)[:, :, t0:t0 + TB], gx_t[:, :, :], reads=[gx_b])
        yield
        cqn_t, cqn_b = cqn_ring.next()
        emit_rmsnorm(cx, cq_t, cq_b, 2, TB, gcq_t, gcq_b, ones_t, ones_b, sq_ring, st_ring, rstd_ring, cqn_t, cqn_b, 256)
        ckvn_t, ckvn_b = ckvn_ring.next()
        emit_rmsnorm(cx, ckv_t, ckv_b, 1, TB, gckv_t, gckv_b, ones_t, ones_b, sq_ring, st_ring, rstd_ring, ckvn_t, ckvn_b, 128)
        yield
        qn_t, qn_b = qn_ring.next()
        for i in range(4):
            pq_t, pq_b = pq_ring.next()
            for k in range(2):
                S.op("pe", lambda h, i=i, k=k, pq_t=pq_t, cqn_t=cqn_t: h.matmul(
                    pq_t[:, 0:TB], lhsT=wuqn[:, k, i * 128:(i + 1) * 128], rhs=cqn_t[:, k, :], start=(k == 0), stop=(k == 1)),
                    reads=[cqn_b, wuq_b], writes=[pq_b])
            S.op("act", lambda h, i=i, pq_t=pq_t, qn_t=qn_t: h.activation(out=qn_t[:, i, :], in_=pq_t[:, 0:TB], func=AF.Copy),
                 reads=[pq_b], writes=[qn_b])
        S.dma("pool", QN.rearrange("(c p) t -> p c t", p=128)[:, :, t0:t0 + TB], qn_t[:, :, :], reads=[qn_b])
        qr_t, qr_b = qr_ring.next()
        for i in range(2):
            pq_t, pq_b = pq_ring.next()
            for k in range(2):
                S.op("pe", lambda h, i=i, k=k, pq_t=pq_t, cqn_t=cqn_t: h.matmul(
                    pq_t[:, 0:TB], lhsT=wuqr[:, k, i * 128:(i + 1) * 128], rhs=cqn_t[:, k, :], start=(k == 0), stop=(k == 1)),
                    reads=[cqn_b, wuq_b], writes=[pq_b])
            emit_rope(cx, pq_t, pq_b, 128, TB, rot_t, rot_b, cos_t, cos_b, sin_t, sin_b, qb_ring, pr_ring, t1_ring, t2_ring,
                      qr_t[:, i, :], qr_b)
        S.dma("pool", QR.rearrange("(c p) t -> p c t", p=128)[:, :, t0:t0 + TB], qr_t[:, :, :], reads=[qr_b])
        yield
        kn_t, kn_b = kn_ring.next()
        for i in range(4):
            pq_t, pq_b = pq_ring.next()
            S.op("pe", lambda h, i=i, pq_t=pq_t, ckvn_t=ckvn_t: h.matmul(
                pq_t[:, 0:TB], lhsT=wk[:, i * 128:(i + 1) * 128], rhs=ckvn_t[:, 0, :], start=True, stop=True),
                reads=[ckvn_b, wkv_b], writes=[pq_b])
            S.op("act", lambda h, i=i, pq_t=pq_t, kn_t=kn_t: h.activation(out=kn_t[:, i, :], in_=pq_t[:, 0:TB], func=AF.Copy),
                 reads=[pq_b], writes=[kn_b])
        S.dma("pool", KNR[0:512, :].rearrange("(c p) t -> p c t", p=128)[:, :, t0:t0 + TB], kn_t[:, :, :], reads=[kn_b])
        yield
        vo_t, vo_b = vo_ring.next()
        for ti in range(TB // 128):
            pv_t, pv_b = pv_ring.next()
            S.op("pe", lambda h, ti=ti, pv_t=pv_t, ckvn_t=ckvn_t: h.matmul(
                pv_t[:, :], lhsT=ckvn_t[:, 0, ti * 128:(ti + 1) * 128], rhs=wv[:, :], start=True, stop=True),
                reads=[ckvn_b, wkv_b], writes=[pv_b])
            S.op("act", lambda h, ti=ti, pv_t=pv_t, vo_t=vo_t: h.activation(
                out=vo_t[:, ti, :].rearrange("p (h e) -> p h e", e=65)[:, :, 0:64],
                in_=pv_t[:, :].rearrange("p (h d) -> p h d", d=64), func=AF.Copy), reads=[pv_b], writes=[vo_b])
        for hd in range(8):
            S.dma("pool", V5[hd * 128:(hd + 1) * 128, :].rearrange("p (i e) -> p i e", e=65)[:, b * 4:(b + 1) * 4, :],
                  vo_t[:, :, hd * 65:(hd + 1) * 65], reads=[vo_b])
        yield

    run_interleaved((blk(b) for b in range(NB)), 2)
    cx.pop()
    if mid_hook is not None:
        mid_hook()

    cx.push()
    wabd = cx.sb("wabd", [128, 2, 4, 128], BF16); wxbd = cx.sb("wxbd", [128, 2, 4, 128], BF16); bd_b = Buf("bd")
    cw_t = cx.sb("cw", [128, 4, 4], F32); cb_t = cx.sb("cb", [128, 4], F32); cw_b = Buf("cw")
    ba_t = cx.sb("ba", [128, 2, 4], F32); bx_t = cx.sb("bx", [128, 2, 4], F32); bb_b = Buf("bb")
    xe_ring = mk_ring(cx, "sb", "xe", 2, [128, 4, TB + 4], F32)
    xc_ring = mk_ring(cx, "sb", "xc", 2, [128, 4, TB], F32)
    xcb_ring = mk_ring(cx, "sb", "xcb", 2, [128, 4, TB], BF16)
    r_ring = mk_ring(cx, "sb", "r", 2, [128, 8, TB], F32)
    i_ring = mk_ring(cx, "sb", "i", 2, [128, 8, TB], F32)
    a_ring = mk_ring(cx, "sb", "a", 2, [128, 8, TB], F32)
    b_ring = mk_ring(cx, "sb", "b", 2, [128, 8, TB], F32)
    hl_ring = mk_ring(cx, "sb", "hl", 2, [128, TB], F32)
    sr_ring = mk_ring(cx, "sb", "sr", 2, [128, 8], F32)
    pg_ring = mk_ring(cx, "ps", "pg", 6, [128, 512], F32)
    S.op("pool", lambda h: h.memset(wabd[:, :, :, :], 0.0), writes=[bd_b])
    S.op("pool", lambda h: h.memset(wxbd[:, :, :, :], 0.0), writes=[bd_b])
    for d in range(2):
        for c in range(4):
            for hf in range(2):
                S.dma("pool", wabd[hf * 64:(hf + 1) * 64, d, c, hf * 64:(hf + 1) * 64], wad[d, 2 * c + hf, :, :], writes=[bd_b])
                S.dma("pool", wxbd[hf * 64:(hf + 1) * 64, d, c, hf * 64:(hf + 1) * 64], wxd[d, 2 * c + hf, :, :], writes=[bd_b])
    S.dma("sp", cw_t[:, :, :], cwd[:, :, :], writes=[cw_b])
    S.dma("sp", cb_t[:, :], cbd[:, :], writes=[cw_b])
    S.dma("sp", ba_t[:, :, :], bad[:, :, :], writes=[bb_b])
    S.dma("sp", bx_t[:, :, :], bxd[:, :, :], writes=[bb_b])
    XRv = XR.rearrange("(c p) t -> p c t", p=128)
    ABv = AB.rearrange("d s (c p) t -> d s p c t", p=128)

    def blk(b):
        t0 = b * TB
        xe_t, xe_b = xe_ring.next()
        lo = 0 if b > 0 else 2
        hi = TB + 3 if b < NB - 1 else TB + 2
        S.dma("sp", xe_t[:, :, lo:hi], XRv[:, :, t0 - 2 + lo:t0 - 2 + hi], writes=[xe_b])
        if b == 0:
            S.op("pool", lambda h, xe_t=xe_t: h.tensor_copy(out=xe_t[:, :, 0:2], in_=xrh_t[:, :, 0:2]), reads=[xrh_b, xe_b], writes=[xe_b])
        if b == NB - 1:
            S.op("pool", lambda h, xe_t=xe_t: h.tensor_copy(out=xe_t[:, :, TB + 2:TB + 3], in_=xrh_t[:, :, 2:3]),
                 reads=[xrh_b, xe_b], writes=[xe_b])
        yield
        xc_t, xc_b = xc_ring.next()
        xcb_t, xcb_b = xcb_ring.next()
        for c in range(4):
            S.op("dve", lambda h, c=c, xc_t=xc_t, xe_t=xe_t: h.tensor_scalar(
                out=xc_t[:, c, :], in0=xe_t[:, c, 0:TB], scalar1=cw_t[:, c, 0:1], scalar2=cb_t[:, c:c + 1],
                op0=ALU.mult, op1=ALU.add), reads=[xe_b, cw_b], writes=[xc_b])
            for j in range(1, 4):
                S.op("dve", lambda h, c=c, j=j, xc_t=xc_t, xe_t=xe_t: h.scalar_tensor_tensor(
                    out=xc_t[:, c, :], in0=xe_t[:, c, j:j + TB], scalar=cw_t[:, c, j:j + 1], in1=xc_t[:, c, :],
                    op0=ALU.mult, op1=ALU.add), reads=[xe_b, cw_b, xc_b], writes=[xc_b])
        S.op("pool", lambda h, xc_t=xc_t, xcb_t=xcb_t: h.tensor_copy(out=xcb_t[:, :, :], in_=xc_t[:, :, :]), reads=[xc_b], writes=[xcb_b])
        yield
        r_t, r_b = r_ring.next()
        i_t, i_b = i_ring.next()
        a_t, a_b = a_ring.next()
        b_t, b_b = b_ring.next()
        sr_t, sr_b = sr_ring.next()
        S.op("pool", lambda h, sr_t=sr_t: h.memset(sr_t[:, :], 0.0), writes=[sr_b])
        for d in range(2):
            for c in range(4):
                q = d * 4 + c
                pg_t, pg_b = pg_ring.next()
                S.op("pe", lambda h, d=d, c=c, pg_t=pg_t, xcb_t=xcb_t: h.matmul(pg_t[:, :], lhsT=wabd[:, d, c, :], rhs=xcb_t[:, c, :],
                                                                         start=True, stop=True), reads=[bd_b, xcb_b], writes=[pg_b])
                S.op("act", lambda h, d=d, c=c, q=q, pg_t=pg_t, r_t=r_t, sr_t=sr_t: h.activation(
                    out=r_t[:, q, :], in_=pg_t[:, :], func=AF.Sigmoid, bias=ba_t[:, d, c:c + 1], accum_out=sr_t[:, q:q + 1]),
                    reads=[pg_b, bb_b], writes=[r_b, sr_b])
                pg_t, pg_b = pg_ring.next()
                S.op("pe", lambda h, d=d, c=c, pg_t=pg_t, xcb_t=xcb_t: h.matmul(pg_t[:, :], lhsT=wxbd[:, d, c, :], rhs=xcb_t[:, c, :],
                                                                         start=True, stop=True), reads=[bd_b, xcb_b], writes=[pg_b])
                S.op("act", lambda h, d=d, c=c, q=q, pg_t=pg_t, i_t=i_t: h.activation(
                    out=i_t[:, q, :], in_=pg_t[:, :], func=AF.Sigmoid, bias=bx_t[:, d, c:c + 1]),
                    reads=[pg_b, bb_b], writes=[i_b])
        yield
        for d in range(2):
            for c in range(4):
                q = d * 4 + c
                S.op("act", lambda h, d=d, c=c, q=q, a_t=a_t, r_t=r_t: h.activation(
                    out=a_t[:, q, :], in_=r_t[:, q, :], func=AF.Exp, scale=cp_t[:, d, c:c + 1]), reads=[r_b, cp_b], writes=[a_b])
                S.op("act", lambda h, d=d, c=c, q=q, sr_t=sr_t, b=b: h.activation(
                    out=blk_t[:, b, d, 0, c:c + 1], in_=sr_t[:, q:q + 1], func=AF.Exp, scale=cp_t[:, d, c:c + 1]),
                    reads=[sr_b, cp_b, blk_b], writes=[blk_b])
        yield
        S.op("pool", lambda h, a_t=a_t, r_t=r_t: h.tensor_tensor(out=r_t[:, :, :], in0=a_t[:, :, :], in1=a_t[:, :, :], op=ALU.mult),
             reads=[a_b, r_b], writes=[r_b])
        S.op("act", lambda h, r_t=r_t: h.activation(out=r_t[:, :, :], in_=r_t[:, :, :], func=AF.Sqrt, scale=-1.0, bias=1.0),
             reads=[r_b], writes=[r_b])
        for d in range(2):
            S.op("pool", lambda h, d=d, i_t=i_t, xc_t=xc_t: h.tensor_tensor(out=i_t[:, d * 4:(d + 1) * 4, :], in0=i_t[:, d * 4:(d + 1) * 4, :],
                                                                        in1=xc_t[:, :, :], op=ALU.mult), reads=[i_b, xc_b], writes=[i_b])
        S.op("dve", lambda h, b_t=b_t, r_t=r_t, i_t=i_t: h.tensor_tensor(out=b_t[:, :, :], in0=r_t[:, :, :], in1=i_t[:, :, :], op=ALU.mult),
             reads=[r_b, i_b], writes=[b_b])
        yield
        for d in range(2):
            for c in range(4):
                q = d * 4 + c
                hl_t, hl_b = hl_ring.next()
                if d == 0:
                    S.op("dve", lambda h, q=q, hl_t=hl_t, a_t=a_t, b_t=b_t: h.tensor_tensor_scan(
                        out=hl_t[:, :], data0=a_t[:, q, :], data1=b_t[:, q, :], initial=0.0, op0=ALU.mult, op1=ALU.add),
                        reads=[a_b, b_b], writes=[hl_b])
                    col = TB - 1
                else:
                    S.op("dve", lambda h, q=q, hl_t=hl_t, a_t=a_t, b_t=b_t: h.tensor_tensor_scan(
                        out=hl_t[:, ::-1], data0=a_t[:, q, ::-1], data1=b_t[:, q, ::-1], initial=0.0, op0=ALU.mult, op1=ALU.add),
                        reads=[a_b, b_b], writes=[hl_b])
                    col = 0
                S.op("pool", lambda h, d=d, c=c, hl_t=hl_t, col=col, b=b: h.tensor_copy(
                    out=blk_t[:, b, d, 1, c:c + 1], in_=hl_t[:, col:col + 1]), reads=[hl_b, blk_b], writes=[blk_b])
        for d in range(2):
            S.dma("act", ABv[d, 0, :, :, t0:t0 + TB], a_t[:, d * 4:(d + 1) * 4, :], reads=[a_b])
            S.dma("sp", ABv[d, 1, :, :, t0:t0 + TB], b_t[:, d * 4:(d + 1) * 4, :], reads=[b_b])
        yield

    run_interleaved((blk(b) for b in range(NB)), 2)
    cab_t = cx.sb("cab", [128, 2, 2, 4], F32); cab_b = Buf("cab")
    for d in range(2):
        S.op("pool", lambda h, d=d: h.memset(cab_t[:, d, 0, :], 1.0), writes=[cab_b])
        S.op("pool", lambda h, d=d: h.memset(cab_t[:, d, 1, :], 0.0), writes=[cab_b])
        order = range(NB) if d == 0 else range(NB - 1, -1, -1)
        for b in order:
            S.op("dve", lambda h, d=d, b=b: h.tensor_tensor(out=cab_t[:, d, 1, :], in0=cab_t[:, d, 1, :], in1=blk_t[:, b, d, 0, :],
                                                            op=ALU.mult), reads=[cab_b, blk_b], writes=[cab_b])
            S.op("dve", lambda h, d=d, b=b: h.tensor_tensor(out=cab_t[:, d, 1, :], in0=cab_t[:, d, 1, :], in1=blk_t[:, b, d, 1, :],
                                                            op=ALU.add), reads=[cab_b, blk_b], writes=[cab_b])
            S.op("dve", lambda h, d=d, b=b: h.tensor_tensor(out=cab_t[:, d, 0, :], in0=cab_t[:, d, 0, :], in1=blk_t[:, b, d, 0, :],
                                                            op=ALU.mult), reads=[cab_b, blk_b], writes=[cab_b])
    S.dma("sp", BLK[:, :, :, :, :], blk_t[:, :, :, :, :], reads=[blk_b])
    S.dma("sp", CAB[:, :, :, :], cab_t[:, :, :, :], reads=[cab_b])
    cx.pop()
    cx.pop()
    return cx.finish() if own else None


def chunk_vec(v, nch):
    return np.ascontiguousarray(np.asarray(v, np.float32).reshape(nch, 128).T)


def oa_inputs(xT, xhalo, P, pos):
    cos, sin = rope_tables(pos, 16, 128)
    return {
        "xT": np.ascontiguousarray(xT), "xhalo": np.ascontiguousarray(xhalo), "w_in": P["w_in"], "g": vec128(P["g"], 8),
        "g_cq": vec128(P["g_cq"], 2), "g_ckv": vec128(P["g_ckv"], 1), "w_uq": P["w_uq"], "w_ukv": P["w_ukv"],
        "cw": np.ascontiguousarray(P["conv_w"].reshape(4, 4, 128).transpose(2, 1, 0)),
        "cb": chunk_vec(P["conv_b"], 4), "wa": P["wa"], "wx": P["wx"],
        "ba": np.ascontiguousarray(P["ba"].reshape(2, 4, 128).transpose(2, 0, 1)),
        "bx": np.ascontiguousarray(P["bx"].reshape(2, 4, 128).transpose(2, 0, 1)),
        "lam": np.ascontiguousarray(P["lam"].reshape(2, 4, 128).transpose(2, 0, 1)),
        "cos": cos, "sin": sin, "rot": rot_matrix(32),
    }


def build_ob1(ntok=TOK, nrank=4, cx=None):
    seq = ntok * nrank
    QG = ntok // 512
    NKT = seq // 128
    NT = ntok // 128
    own = cx is None
    cx = Ctx() if own else cx
    cx.push()
    S = cx.S
    QN = cx.dram_in("QN", [512, ntok], BF16)
    QR = cx.dram_in("QR", [256, ntok], BF16)
    KNg = cx.dram_in("KNg", [8 * nrank * 64, ntok], BF16)
    KRg = cx.dram_in("KRg", [nrank * 32, ntok], BF16)
    Vg = cx.dram_in("Vg", [8 * nrank * 128, NT * 65], BF16)
    YC = cx.dram_out("YC", [512, ntok], BF16)

    q_ring = mk_ring(cx, "sb", "q", 2, [128, ntok], BF16)
    k_ring = mk_ring(cx, "sb", "k", 2, [128, seq], BF16)
    v_ring = mk_ring(cx, "sb", "v", 2, [128, NKT, 65], BF16)
    p_ring = mk_ring(cx, "sb", "p", 4, [128, 512], BF16)
    osb_ring = mk_ring(cx, "sb", "osb", 2, [64, 512], F32)
    rc_ring = mk_ring(cx, "sb", "rc", 2, [128, 512], F32)
    yc_ring = mk_ring(cx, "sb", "yc", 2, [64, 512], BF16)
    ones32 = cx.sb("ones32", [128, 64], F32); ones32_b = Buf("ones32")
    s_ring = mk_ring(cx, "ps", "s", 4, [128, 512], F32)
    o_ring = mk_ring(cx, "ps", "o", 2, [128, 512], F32)
    bc_ring = mk_ring(cx, "ps", "bc", 1, [128, 512], F32)
    S.op("pool", lambda h: h.memset(ones32[:, :], 1.0), writes=[ones32_b])
    scale = float(96 ** -0.5)
    LA = 2

    def load_head(hd):
        q_t, q_b = q_ring.next()
        k_t, k_b = k_ring.next()
        v_t, v_b = v_ring.next()
        S.dma("sp", q_t[0:64, :], QN[hd * 64:(hd + 1) * 64, :], writes=[q_b])
        S.dma("sp", q_t[64:96, :], QR[hd * 32:(hd + 1) * 32, :], writes=[q_b])
        for r in range(nrank):
            S.dma("sp", k_t[0:64, r * ntok:(r + 1) * ntok], KNg[(hd * nrank + r) * 64:(hd * nrank + r + 1) * 64, :], writes=[k_b])
            S.dma("sp", k_t[64:96, r * ntok:(r + 1) * ntok], KRg[r * 32:(r + 1) * 32, :], writes=[k_b])
            S.dma("sp", v_t[:, r * NT:(r + 1) * NT, :],
                  Vg[(hd * nrank + r) * 128:(hd * nrank + r + 1) * 128, :].rearrange("p (i e) -> p i e", e=65), writes=[v_b])
        return (q_t, q_b, k_t, k_b, v_t, v_b)

    nxt = load_head(0)
    cx.run_hook()
    for hd in range(8):
        q_t, q_b, k_t, k_b, v_t, v_b = nxt
        if hd + 1 < 8:
            nxt = load_head(hd + 1)
        for qg in range(QG):
            o_t, o_b = o_ring.next()
            stiles = {}

            def emit_s(kt):
                s_t, s_b = s_ring.next()
                S.op("pe", lambda h: h.matmul(s_t[:, :], lhsT=k_t[0:96, kt * 128:(kt + 1) * 128],
                                              rhs=q_t[0:96, qg * 512:(qg + 1) * 512], start=True, stop=True),
                     reads=[k_b, q_b], writes=[s_b])
                stiles[kt] = (s_t, s_b)

            for kt in range(min(LA, NKT)):
                emit_s(kt)
            for kt in range(NKT):
                s_t, s_b = stiles.pop(kt)
                p_t, p_b = p_ring.next()
                S.op("act", lambda h, s_t=s_t, p_t=p_t: h.activation(out=p_t[:, :], in_=s_t[:, :], func=AF.Exp, scale=scale),
                     reads=[s_b], writes=[p_b])
                if kt + LA < NKT:
                    emit_s(kt + LA)
                S.op("pe", lambda h, kt=kt, p_t=p_t: h.matmul(o_t[0:65, :], lhsT=v_t[:, kt, 0:65], rhs=p_t[:, :],
                                                              start=(kt == 0), stop=(kt == NKT - 1)),
                     reads=[v_b, p_b], writes=[o_b])
            osb_t, osb_b = osb_ring.next()
            rc_t, rc_b = rc_ring.next()
            S.op("act", lambda h: h.activation(out=osb_t[:, :], in_=o_t[0:64, :], func=AF.Copy), reads=[o_b], writes=[osb_b])
            S.op("dve", lambda h: h.reciprocal(out=rc_t[64:65, :], in_=o_t[64:65, :]), reads=[o_b], writes=[rc_b])
            bc_t, bc_b = bc_ring.next()
            S.op("pe", lambda h: h.matmul(bc_t[0:64, :], lhsT=ones32[64:65, 0:64], rhs=rc_t[64:65, :], start=True, stop=True),
                 reads=[rc_b, ones32_b], writes=[bc_b])
            yc_t, yc_b = yc_ring.next()
            S.op("dve", lambda h: h.tensor_tensor(out=yc_t[:, :], in0=osb_t[:, :], in1=bc_t[0:64, :], op=ALU.mult),
                 reads=[osb_b, bc_b], writes=[yc_b])
            S.dma("pool", YC[hd * 64:(hd + 1) * 64, qg * 512:(qg + 1) * 512], yc_t[:, :], reads=[yc_b])
    cx.pop()
    return cx.finish() if own else None


def build_ob2(ntok=TOK, ngrp=4, cx=None):
    TB = 512
    NB = ntok // TB
    own = cx is None
    cx = Ctx() if own else cx
    cx.push()
    S = cx.S
    AB = cx.dram_in("AB", [2, 2, 512, ntok])
    GX = cx.dram_in("GX", [512, ntok], BF16)
    YC = cx.dram_in("YC", [512, ntok], BF16)
    xT = cx.dram_in("xT", [D, ntok])
    w_out = cx.dram_in("w_out", [D, D])
    BLK = cx.dram_in("BLK", [128, NB, 2, 2, 4])
    CABg = cx.dram_in("CABg", [128, ngrp, 16])
    mfd = cx.dram_in("mf", [128, ngrp])
    mbd = cx.dram_in("mb", [128, ngrp])
    oT = cx.dram_out("oT", [D, ntok])

    woA = cx.sb("woA", [128, 4, D], BF16); woA_b = Buf("woA")
    woB = cx.sb("woB", [128, 4, D], BF16); woB_b = Buf("woB")
    blk_t = cx.sb("blk", [128, NB, 2, 2, 4], F32); blk_b = Buf("blk")
    cab_t = cx.sb("cab", [128, ngrp, 16], F32); cab_b = Buf("cab")
    m_t = cx.sb("m", [128, 2, ngrp], F32); m_b = Buf("m")
    hin_t = cx.sb("hin", [128, 2, 4], F32); hin_b = Buf("hin")
    tmp_t = cx.sb("tmp", [128, 4], F32); tmp_b = Buf("tmp")
    init_t = cx.sb("init", [128, NB, 2, 4], F32); init_b = Buf("init")
    ab_ring = mk_ring(cx, "sb", "ab", 2, [128, 2, 2, 4, TB], F32)
    hs_ring = mk_ring(cx, "sb", "hs", 2, [128, 2, 4, TB], F32)
    gx_ring = mk_ring(cx, "sb", "gx", 2, [128, 4, TB], BF16)
    yc_ring = mk_ring(cx, "sb", "yc", 2, [128, 4, TB], BF16)
    yd_ring = mk_ring(cx, "sb", "yd", 2, [128, 4, TB], BF16)
    x_ring = mk_ring(cx, "sb", "x", 2, [128, KC, TB], F32)
    y_ring = mk_ring(cx, "ps", "y", 3, [128, 512], F32)

    S.dma("pool", woA[:, :, :], w_out[0:512, :].rearrange("(i p) n -> p i n", p=128), writes=[woA_b])
    S.dma("pool", woB[:, :, :], w_out[512:1024, :].rearrange("(g p) n -> p g n", p=128), writes=[woB_b])
    S.dma("sp", blk_t[:, :, :, :, :], BLK[:, :, :, :, :], writes=[blk_b])
    S.dma("sp", cab_t[:, :, :], CABg[:, :, :], writes=[cab_b])
    S.dma("sp", m_t[:, 0, :], mfd[:, :], writes=[m_b])
    S.dma("sp", m_t[:, 1, :], mbd[:, :], writes=[m_b])
    S.op("pool", lambda h: h.memset(hin_t[:, :, :], 0.0), writes=[hin_b])
    for d in range(2):
        order = range(ngrp) if d == 0 else range(ngrp - 1, -1, -1)
        for i in order:
            S.op("dve", lambda h, d=d, i=i: h.tensor_tensor(out=tmp_t[:, :], in0=hin_t[:, d, :], in1=cab_t[:, i, d * 8:d * 8 + 4], op=ALU.mult),
                 reads=[hin_b, cab_b, tmp_b], writes=[tmp_b])
            S.op("dve", lambda h, d=d, i=i: h.tensor_tensor(out=tmp_t[:, :], in0=tmp_t[:, :], in1=cab_t[:, i, d * 8 + 4:d * 8 + 8], op=ALU.add),
                 reads=[tmp_b, cab_b], writes=[tmp_b])
            S.op("dve", lambda h, d=d, i=i: h.tensor_tensor(out=tmp_t[:, :], in0=tmp_t[:, :], in1=hin_t[:, d, :], op=ALU.subtract),
                 reads=[tmp_b, hin_b], writes=[tmp_b])
            S.op("dve", lambda h, d=d, i=i: h.scalar_tensor_tensor(out=hin_t[:, d, :], in0=tmp_t[:, :], scalar=m_t[:, d, i:i + 1],
                                                                   in1=hin_t[:, d, :], op0=ALU.mult, op1=ALU.add),
                 reads=[tmp_b, m_b, hin_b], writes=[hin_b])
    for d in range(2):
        order = list(range(NB)) if d == 0 else list(range(NB - 1, -1, -1))
        S.op("dve", lambda h, d=d, b0=order[0]: h.tensor_copy(out=init_t[:, b0, d, :], in_=hin_t[:, d, :]),
             reads=[hin_b, init_b], writes=[init_b])
        for bi in range(NB - 1):
            b, bn = order[bi], order[bi + 1]
            S.op("dve", lambda h, d=d, b=b, bn=bn: h.tensor_tensor(out=init_t[:, bn, d, :], in0=init_t[:, b, d, :],
                                                                   in1=blk_t[:, b, d, 0, :], op=ALU.mult),
                 reads=[init_b, blk_b], writes=[init_b])
            S.op("dve", lambda h, d=d, b=b, bn=bn: h.tensor_tensor(out=init_t[:, bn, d, :], in0=init_t[:, bn, d, :],
                                                                   in1=blk_t[:, b, d, 1, :], op=ALU.add),
                 reads=[init_b, blk_b], writes=[init_b])
    xv = xT.rearrange("(c p) t -> p c t", p=128)
    ov = oT.rearrange("(c p) t -> p c t", p=128)
    ABv = AB.rearrange("d s (c p) t -> d s p c t", p=128)
    def blk(b):
        t0 = b * TB
        ab_t, ab_b = ab_ring.next()
        for d in range(2):
            for s_ in range(2):
                S.dma("sp", ab_t[:, d, s_, :, :], ABv[d, s_, :, :, t0:t0 + TB], writes=[ab_b])
        gx_t, gx_b = gx_ring.next()
        yc_t, yc_b = yc_ring.next()
        x_t, x_b = x_ring.next()
        S.dma("sp", gx_t[:, :, :], GX.rearrange("(c p) t -> p c t", p=128)[:, :, t0:t0 + TB], writes=[gx_b])
        S.dma("sp", yc_t[:, :, :], YC.rearrange("(i p) t -> p i t", p=128)[:, :, t0:t0 + TB], writes=[yc_b])
        S.dma("sp", x_t[:, :, :], xv[:, :, t0:t0 + TB], writes=[x_b])
        yield
        hs_t, hs_b = hs_ring.next()
        for d in range(2):
            for c in range(4):
                if d == 0:
                    S.op("dve", lambda h, d=d, c=c, b=b: h.tensor_tensor_scan(
                        out=hs_t[:, d, c, :], data0=ab_t[:, d, 0, c, :], data1=ab_t[:, d, 1, c, :],
                        initial=init_t[:, b, d, c:c + 1], op0=ALU.mult, op1=ALU.add), reads=[ab_b, init_b, hs_b], writes=[hs_b])
                else:
                    S.op("dve", lambda h, d=d, c=c, b=b: h.tensor_tensor_scan(
                        out=hs_t[:, d, c, ::-1], data0=ab_t[:, d, 0, c, ::-1], data1=ab_t[:, d, 1, c, ::-1],
                        initial=init_t[:, b, d, c:c + 1], op0=ALU.mult, op1=ALU.add), reads=[ab_b, init_b, hs_b], writes=[hs_b])
        yield
        S.op("pool", lambda h: h.tensor_tensor(out=hs_t[:, 0, :, :], in0=hs_t[:, 0, :, :], in1=hs_t[:, 1, :, :], op=ALU.add),
             reads=[hs_b], writes=[hs_b])
        yd_t, yd_b = yd_ring.next()
        S.op("pool", lambda h: h.tensor_tensor(out=yd_t[:, :, :], in0=hs_t[:, 0, :, :], in1=gx_t[:, :, :], op=ALU.mult),
             reads=[hs_b, gx_b], writes=[yd_b])
        yield
        for o in range(KC):
            y_t, y_b = y_ring.next()
            for hh in range(4):
                S.op("pe", lambda h, hh=hh, o=o: h.matmul(y_t[:, :], lhsT=woA[:, hh, o * 128:(o + 1) * 128], rhs=yc_t[:, hh, :],
                                                         start=(hh == 0), stop=False), reads=[woA_b, yc_b], writes=[y_b])
            for g in range(4):
                S.op("pe", lambda h, g=g, o=o: h.matmul(y_t[:, :], lhsT=woB[:, g, o * 128:(o + 1) * 128], rhs=yd_t[:, g, :],
                                                       start=False, stop=(g == 3)), reads=[woB_b, yd_b], writes=[y_b])
            S.op("dve", lambda h, o=o: h.tensor_tensor(out=x_t[:, o, :], in0=y_t[:, :], in1=x_t[:, o, :], op=ALU.add),
                 reads=[y_b, x_b], writes=[x_b])
        S.dma("pool", ov[:, :, t0:t0 + TB], x_t[:, :, :], reads=[x_b])
        yield

    run_interleaved((blk(b) for b in range(NB)), 2, 2)
    cx.pop()
    return cx.finish() if own else None


def allgather(cx, in_ap, out_ap, groups):
    S = cx.S
    S.barrier()
    sem = S.new_sem("cc")
    cx.nc.gpsimd.collective_compute("AllGather", ALU.bypass, replica_groups=groups, ins=[in_ap], outs=[out_ap]).then_inc(sem, 1)
    for e in S.ENGS:
        S.h[e].wait_ge(sem, 1)


def allgather_many(cx, pairs, groups):
    S = cx.S
    S.barrier()
    sem = S.new_sem("ccm")
    for (in_ap, out_ap) in pairs:
        cx.nc.gpsimd.collective_compute("AllGather", ALU.bypass, replica_groups=groups, ins=[in_ap], outs=[out_ap]).then_inc(sem, 1)
    for e in S.ENGS:
        S.h[e].wait_ge(sem, len(pairs))


def emit_select(cx, src_t, src_b, nrank, m_t, m_b, side, acc_t, acc_b):
    S = cx.S
    S.op("dve", lambda h: h.tensor_scalar_mul(out=acc_t[:, :], in0=src_t[:, 0, :], scalar1=m_t[:, side, 0:1]),
         reads=[src_b, m_b], writes=[acc_b])
    for i in range(1, nrank):
        S.op("dve", lambda h, i=i: h.scalar_tensor_tensor(out=acc_t[:, :], in0=src_t[:, i, :], scalar=m_t[:, side, i:i + 1],
                                                          in1=acc_t[:, :], op0=ALU.mult, op1=ALU.add),
             reads=[src_b, m_b, acc_b], writes=[acc_b])


def emit_even_exchange(cx, KTh, Vh, UTh, pack, packg, mlr, groups, nrank, ntok):
    S = cx.S
    cx.push()
    S.dma_dd("sp", pack[:, 0:128], KTh[:, 128:256])
    S.dma_dd("sp", pack[:, 128:256], KTh[:, ntok:ntok + 128])
    S.dma_dd("sp", pack[:, 256:288].rearrange("p (g t) -> p g t", g=4), UTh[:, :, 8:16])
    S.dma_dd("sp", pack[:, 288:320].rearrange("p (g t) -> p g t", g=4), UTh[:, :, ntok:ntok + 8])
    S.dma_dd("sp", pack[:, 320:450], Vh[128:256, :])
    S.dma_dd("sp", pack[:, 450:580], Vh[ntok:ntok + 128, :])
    allgather(cx, pack[:, :], packg[:, :], groups)
    pg_t = cx.sb("pg", [128, nrank, 580], BF16); pg_b = Buf("pg")
    m_t = cx.sb("mlr", [128, 2, nrank], F32); m_b = Buf("mlr")
    accL = cx.sb("accL", [128, 580], BF16); accL_b = Buf("accL")
    accR = cx.sb("accR", [128, 580], BF16); accR_b = Buf("accR")
    S.dma("sp", pg_t[:, :, :], packg.rearrange("(r p) n -> p r n", p=128), writes=[pg_b])
    S.dma("sp", m_t[:, :, :], mlr[:, :, :], writes=[m_b])
    emit_select(cx, pg_t, pg_b, nrank, m_t, m_b, 0, accL, accL_b)
    emit_select(cx, pg_t, pg_b, nrank, m_t, m_b, 1, accR, accR_b)
    S.dma("sp", KTh[:, 0:128], accL[:, 128:256], reads=[accL_b])
    S.dma("sp", UTh[:, :, 0:8], accL[:, 288:320].rearrange("p (g t) -> p g t", g=4), reads=[accL_b])
    S.dma("sp", Vh[0:128, :], accL[:, 450:580], reads=[accL_b])
    S.dma("sp", KTh[:, 128 + ntok:256 + ntok], accR[:, 0:128], reads=[accR_b])
    S.dma("sp", UTh[:, :, 8 + ntok:16 + ntok], accR[:, 256:288].rearrange("p (g t) -> p g t", g=4), reads=[accR_b])
    S.dma("sp", Vh[128 + ntok:256 + ntok, :], accR[:, 320:450], reads=[accR_b])
    cx.pop()


def emit_xhalo_exchange(cx, xprev, xhp, xhpg, xhalo, mlr, groups, nrank, ntok):
    S = cx.S
    cx.push()
    S.dma_dd("sp", xhp[:, 0:2], xprev[:, 0:2])
    S.dma_dd("sp", xhp[:, 2:4], xprev[:, ntok - 2:ntok])
    allgather(cx, xhp[:, :], xhpg[:, :], groups)
    xg_t = cx.sb("xg", [128, nrank, 32], F32); xg_b = Buf("xg")
    m_t = cx.sb("mlr", [128, 2, nrank], F32); m_b = Buf("mlr")
    accL = cx.sb("accL", [128, 32], F32); accL_b = Buf("accL")
    accR = cx.sb("accR", [128, 32], F32); accR_b = Buf("accR")
    for r in range(nrank):
        S.dma("sp", xg_t[:, r, :].rearrange("p (c t) -> p c t", t=4),
              xhpg[r * D:(r + 1) * D, :].rearrange("(c p) t -> p c t", p=128), writes=[xg_b])
    S.dma("sp", m_t[:, :, :], mlr[:, :, :], writes=[m_b])
    emit_select(cx, xg_t, xg_b, nrank, m_t, m_b, 0, accL, accL_b)
    emit_select(cx, xg_t, xg_b, nrank, m_t, m_b, 1, accR, accR_b)
    xhv = xhalo.rearrange("(c p) t -> p c t", p=128)
    S.dma("sp", xhv[:, :, 0:2], accL[:, :].rearrange("p (c t) -> p c t", t=4)[:, :, 2:4], reads=[accL_b])
    S.dma("sp", xhv[:, :, 2:4], accR[:, :].rearrange("p (c t) -> p c t", t=4)[:, :, 0:2], reads=[accR_b])
    cx.pop()


SMALL_SPECS = None


def build_fused(B=2, nrank=4, ntok=TOK, depth=4):
    NE, NO = (depth + 1) // 2, depth // 2
    NT = ntok // 128
    NB = ntok // 512
    groups = [[b * nrank + r for r in range(nrank)] for b in range(B)]
    cx = Ctx()
    nc = cx.nc
    I = cx.ext_in
    x0 = I("xT", [D, ntok])
    Wd = {
        "e_w_in": I("e_w_in", [NE, D, 1280]), "e_w_pool": I("e_w_pool", [NE, 4, 128, 128]), "e_w_out": I("e_w_out", [NE, D, D]),
        "o_w_in": I("o_w_in", [NO, D, 1440]), "o_w_uq": I("o_w_uq", [NO, 256, 768]), "o_w_ukv": I("o_w_ukv", [NO, 128, 1024]),
        "o_lru_wa": I("o_lru_wa", [NO, 2, 8, 64, 64]), "o_lru_wx": I("o_lru_wx", [NO, 2, 8, 64, 64]), "o_w_out": I("o_w_out", [NO, D, D]),
        "w_mlp1": I("w_mlp1", [depth, D, DFF]), "w_mlp2": I("w_mlp2", [depth, DFF, D]),
        "g_mix": I("g_mix", [depth, 128, KC]), "g_mlp": I("g_mlp", [depth, 128, KC]), "g_fin": I("g_fin", [128, KC]),
        "pscale": I("pscale", [NE, 128, 4]), "sinkrow": I("sinkrow", [NE, 1, 2, 512]),
        "g_cq": I("g_cq", [NO, 128, 2]), "g_ckv": I("g_ckv", [NO, 128, 1]), "cw": I("cw", [NO, 128, 4, 4]), "cb": I("cb", [NO, 128, 4]),
        "ba": I("ba", [NO, 128, 2, 4]), "bx": I("bx", [NO, 128, 2, 4]), "lam": I("lam", [NO, 128, 2, 4]),
        "cos32": I("cos32", [128, ntok]), "sin32": I("sin32", [128, ntok]), "cos16": I("cos16", [128, ntok]), "sin16": I("sin16", [128, ntok]),
        "rot64": I("rot64", [128, 128]), "rot32": I("rot32", [128, 128]), "masks": I("masks", [4, 128, 512]),
        "invc": I("invc", [128, 2, 4, 16]), "mfb": I("mfb", [2, 128, nrank]), "mlr": I("mlr", [128, 2, nrank]),
    }
    outT = cx.ext_out("oT", [D, ntok])

    def tmp(name, shape, dt=F32):
        return nc.dram_tensor(name, list(shape), dt, kind="Internal").ap()

    def make_precast(layer, w1b_d, w2b_d):
        def f():
            for k in range(8):
                cx.S.dma_dd_async("pool", w1b_d[k * 128:(k + 1) * 128, :], Wd["w_mlp1"][layer][k * 128:(k + 1) * 128, :])
            for k in range(8):
                cx.S.dma_dd_async("pool", w2b_d[k * 512:(k + 1) * 512, :], Wd["w_mlp2"][layer][k * 512:(k + 1) * 512, :])
        return f

    xcur = x0
    for layer in range(depth):
        L = f"L{layer}"
        xmix = tmp(L + "_xmix", [D, ntok])
        w1b_d = tmp(L + "_w1b", [D, DFF], BF16)
        w2b_d = tmp(L + "_w2b", [DFF, D], BF16)
        if layer % 2 == 0:
            e = layer // 2
            QsT = tmp(L + "_QsT", [128, 4, ntok], BF16)
            KTh = tmp(L + "_KTh", [128, ntok + 256], BF16)
            Vh = tmp(L + "_Vh", [ntok + 256, 130], BF16)
            UTh = tmp(L + "_UTh", [128, 4, ntok + 16], BF16)
            pack = tmp(L + "_pack", [128, 580], BF16)
            packg = tmp(L + "_packg", [nrank * 128, 580], BF16)
            cx.bind = {"xT": xcur, "w_in": Wd["e_w_in"][e], "g": Wd["g_mix"][layer], "cos": Wd["cos32"], "sin": Wd["sin32"],
                       "rot": Wd["rot64"], "QsT": QsT, "KT": KTh[:, 128:128 + ntok], "Vaug": Vh[128:128 + ntok, :],
                       "UT": UTh[:, :, 8:8 + ntok]}
            build_ea(ntok, cx=cx)
            emit_even_exchange(cx, KTh, Vh, UTh, pack, packg, Wd["mlr"], groups, nrank, ntok)
            cx.bind = {"QsT": QsT, "KTh": KTh, "Vh": Vh, "UTh": UTh, "xT": xcur, "w_pool": Wd["e_w_pool"][e],
                       "pscale": Wd["pscale"][e], "w_out": Wd["e_w_out"][e], "sinkrow": Wd["sinkrow"][e], "masks": Wd["masks"],
                       "invc": Wd["invc"], "oT": xmix}
            cx.hook = make_precast(layer, w1b_d, w2b_d)
            build_eb(ntok, cx=cx)
        else:
            o = layer // 2
            xhp = tmp(L + "_xhp", [D, 4]); xhpg = tmp(L + "_xhpg", [nrank * D, 4]); xhalo = tmp(L + "_xhalo", [D, 4])
            QN = tmp(L + "_QN", [512, ntok], BF16); QR = tmp(L + "_QR", [256, ntok], BF16)
            KNR = tmp(L + "_KNR", [544, ntok], BF16); V5 = tmp(L + "_V5", [1024, NT * 65], BF16)
            GX = tmp(L + "_GX", [512, ntok], BF16); AB = tmp(L + "_AB", [2, 2, 512, ntok])
            BLK = tmp(L + "_BLK", [128, NB, 2, 2, 4]); CAB = tmp(L + "_CAB", [128, 16]); XR = tmp(L + "_XR", [512, ntok])
            KNg = tmp(L + "_KNg", [8 * nrank * 64, ntok], BF16); KRg = tmp(L + "_KRg", [nrank * 32, ntok], BF16)
            Vg = tmp(L + "_Vg", [8 * nrank * 128, NT * 65], BF16)
            CABg = tmp(L + "_CABg", [nrank * 128, 16]); YC = tmp(L + "_YC", [512, ntok], BF16)
            emit_xhalo_exchange(cx, xcur, xhp, xhpg, xhalo, Wd["mlr"], groups, nrank, ntok)
            cx.bind = {"xT": xcur, "xhalo": xhalo, "w_in": Wd["o_w_in"][o], "g": Wd["g_mix"][layer], "g_cq": Wd["g_cq"][o],
                       "g_ckv": Wd["g_ckv"][o], "w_uq": Wd["o_w_uq"][o], "w_ukv": Wd["o_w_ukv"][o], "cw": Wd["cw"][o], "cb": Wd["cb"][o],
                       "wa": Wd["o_lru_wa"][o], "wx": Wd["o_lru_wx"][o], "ba": Wd["ba"][o], "bx": Wd["bx"][o], "lam": Wd["lam"][o],
                       "cos": Wd["cos16"], "sin": Wd["sin16"], "rot": Wd["rot32"], "QN": QN, "QR": QR, "KNR": KNR, "V5": V5, "GX": GX,
                       "AB": AB, "BLK": BLK, "CAB": CAB.rearrange("p (d s c) -> p d s c", d=2, s=2), "XR": XR}
            ccsem = cx.S.new_sem("ccg")
            ncc = [0]

            def gather_kv():
                pairs = []
                for hd in range(8):
                    pairs.append((KNR[hd * 64:(hd + 1) * 64, :], KNg[hd * nrank * 64:(hd + 1) * nrank * 64, :]))
                    pairs.append((V5[hd * 128:(hd + 1) * 128, :], Vg[hd * nrank * 128:(hd + 1) * nrank * 128, :]))
                pairs.append((KNR[512:544, :], KRg[:, :]))
                for (i_ap, o_ap) in pairs:
                    nc.gpsimd.collective_compute("AllGather", ALU.bypass, replica_groups=groups, ins=[i_ap], outs=[o_ap]).then_inc(ccsem, 1)
                    ncc[0] += 1

            build_oa(ntok, cx=cx, mid_hook=gather_kv)
            nc.gpsimd.collective_compute("AllGather", ALU.bypass, replica_groups=groups, ins=[CAB[:, :]], outs=[CABg[:, :]]).then_inc(ccsem, 1)
            ncc[0] += 1
            for e_ in cx.S.ENGS:
                cx.S.h[e_].wait_ge(ccsem, ncc[0])
            cx.bind = {"QN": QN, "QR": QR, "KNg": KNg, "KRg": KRg, "Vg": Vg, "YC": YC}
            cx.hook = make_precast(layer, w1b_d, w2b_d)
            build_ob1(ntok, nrank, cx=cx)
            cx.bind = {"AB": AB, "GX": GX, "YC": YC, "xT": xcur, "w_out": Wd["o_w_out"][o], "BLK": BLK,
                       "CABg": CABg.rearrange("(r p) n -> p r n", p=128), "mf": Wd["mfb"][0], "mb": Wd["mfb"][1], "oT": xmix}
            build_ob2(ntok, nrank, cx=cx)
        last = layer == depth - 1
        xnext = outT if last else tmp(L + "_xmlp", [D, ntok])
        cx.bind = {"xT": xmix, "w1": w1b_d, "w2": w2b_d, "g": Wd["g_mlp"][layer], "gf": Wd["g_fin"], "oT": xnext}
        build_mlp(last, ntok, cx=cx, wbf16=True)
        xcur = xnext
    cx.bind = {}
    return cx.finish()


_FUSED = {}


def run_model(x, W, nrank=4, ntok=TOK):
    B, Sq, _ = x.shape
    ncore = B * nrank
    assert Sq == nrank * ntok
    depth = W["norm_mlp"].shape[0]
    NE, NO = (depth + 1) // 2, depth // 2
    key = (B, nrank, ntok, depth)
    if key not in _FUSED:
        _FUSED[key] = build_fused(B, nrank, ntok, depth)
    nc = _FUSED[key]
    f32 = lambda a: np.ascontiguousarray(np.asarray(a, np.float32))
    g_mix = np.stack([vec128(W["e_norm_mix"][l // 2] if l % 2 == 0 else W["o_norm_mix"][l // 2], 8) for l in range(depth)])
    shared = {
        "e_w_in": f32(W["e_w_in"]), "e_w_pool": f32(W["e_w_pool"]), "e_w_out": f32(W["e_w_out"]),
        "o_w_in": f32(W["o_w_in"]), "o_w_uq": f32(W["o_w_uq"]), "o_w_ukv": f32(W["o_w_ukv"]),
        "o_lru_wa": f32(W["o_lru_wa"]), "o_lru_wx": f32(W["o_lru_wx"]), "o_w_out": f32(W["o_w_out"]),
        "w_mlp1": f32(W["w_mlp1"]), "w_mlp2": f32(W["w_mlp2"]),
        "g_mix": g_mix, "g_mlp": np.stack([vec128(W["norm_mlp"][l], 8) for l in range(depth)]), "g_fin": vec128(W["final_norm"], 8),
        "pscale": np.stack([vec128(W["e_pool_scale"][e], 4) for e in range(NE)]),
        "sinkrow": np.stack([np.repeat(f32(W["e_sink"][e]).reshape(2, 4), 128, axis=1).reshape(1, 2, 512) for e in range(NE)]),
        "g_cq": np.stack([vec128(W["o_g_cq"][o], 2) for o in range(NO)]),
        "g_ckv": np.stack([vec128(W["o_g_ckv"][o], 1) for o in range(NO)]),
        "cw": np.stack([f32(f32(W["o_conv_w"][o]).reshape(4, 4, 128).transpose(2, 1, 0)) for o in range(NO)]),
        "cb": np.stack([chunk_vec(W["o_conv_b"][o], 4) for o in range(NO)]),
        "ba": np.stack([f32(f32(W["o_lru_ba"][o]).reshape(2, 4, 128).transpose(2, 0, 1)) for o in range(NO)]),
        "bx": np.stack([f32(f32(W["o_lru_bx"][o]).reshape(2, 4, 128).transpose(2, 0, 1)) for o in range(NO)]),
        "lam": np.stack([f32(f32(W["o_lru_lambda"][o]).reshape(2, 4, 128).transpose(2, 0, 1)) for o in range(NO)]),
        "rot64": rot_matrix(64), "rot32": rot_matrix(32),
    }
    in_maps = []
    for c in range(ncore):
        bi, r = c // nrank, c % nrank
        pos = r * ntok + np.arange(ntok)
        cos32, sin32 = rope_tables(pos, 32, 128)
        cos16, sin16 = rope_tables(pos, 16, 128)
        mfb = np.zeros((2, 128, nrank), np.float32); mfb[0, :, :r] = 1.0; mfb[1, :, r + 1:] = 1.0
        mlr = np.zeros((128, 2, nrank), np.float32)
        if r > 0:
            mlr[:, 0, r - 1] = 1.0
        if r < nrank - 1:
            mlr[:, 1, r + 1] = 1.0
        im = dict(shared)
        im.update({"xT": np.ascontiguousarray(x[bi, r * ntok:(r + 1) * ntok, :].T), "cos32": cos32, "sin32": sin32, "cos16": cos16,
                   "sin16": sin16, "masks": eb_masks(r > 0, r < nrank - 1), "invc": eb_invc(r == 0, r == nrank - 1),
                   "mfb": mfb, "mlr": mlr})
        in_maps.append(im)
    res = run_spmd(nc, in_maps)
    out = np.empty((B, Sq, D), np.float32)
    for c in range(ncore):
        bi, r = c // nrank, c % nrank
        out[bi, r * ntok:(r + 1) * ntok, :] = res[c]["oT"].T
    return out


def kernel(**inputs):
    W = {k: np.asarray(v) for k, v in inputs.items()}
    x = np.asarray(W.pop("x"), np.float32)
    return run_model(x, W)
```

```python
from contextlib import ExitStack
import numpy as np
import concourse.bass as bass
import concourse.mybir as mybir
from concourse.bass_utils import run_bass_kernel_spmd

F32 = mybir.dt.float32
BF16 = mybir.dt.bfloat16
ALU = mybir.AluOpType
AF = mybir.ActivationFunctionType

NCORES = 8
D = 1024
KC = 8
TOK = 4096
SEQ = 16384
EPS = 1e-6
DFF = 4096
EPOCH = 30000


class Buf:
    __slots__ = ("name", "writers", "readers", "sem_in", "sem_out", "n_in", "n_out", "excl")

    def __init__(self, name, excl=False):
        self.name = name
        self.excl = excl
        self.writers = {}
        self.readers = {}
        self.sem_in = None
        self.sem_out = None
        self.n_in = 0
        self.n_out = 0


class Sched:
    ENGS = ("pe", "act", "dve", "pool", "sp")

    def __init__(self, nc, stack):
        self.nc = nc
        self.stack = stack
        self.h = {"pe": nc.tensor, "act": nc.scalar, "dve": nc.vector, "pool": nc.gpsimd, "sp": nc.sync}
        self.ops = {e: [] for e in self.ENGS}
        self.cnt = {e: 0 for e in self.ENGS}
        self.sem = {e: None for e in self.ENGS}
        self.seen = {e: {} for e in self.ENGS}
        self.last = {e: None for e in self.ENGS}
        self.dma_toks = {}
        self.nsem = 0
        self.ninstr = 0
        self.sem_pool = []
        self.live = []
        self.ddbuf = Buf("dram2dram")

    def new_sem(self, name):
        self.nsem += 1
        return self.stack.enter_context(self.nc.semaphore(f"{name}_{self.nsem}"))

    def _eng_tok(self, e):
        if self.sem[e] is None or self.cnt[e] >= EPOCH:
            self.sem[e] = self.new_sem("e" + e)
            self.cnt[e] = 0
        self.cnt[e] += 1
        tok = (self.sem[e], self.cnt[e])
        self.last[e] = tok
        return tok

    def _waits(self, e, toks):
        need = {}
        seen = self.seen[e]
        for sem, val in toks:
            k = id(sem)
            if seen.get(k, 0) >= val:
                continue
            if k not in need or need[k][1] < val:
                need[k] = (sem, val)
        out = []
        for k, (sem, val) in need.items():
            seen[k] = val
            out.append((sem, val))
        return out

    def _deps(self, e, reads, writes):
        toks = []
        for b in reads:
            toks.extend(b.writers.values())
            if b.excl:
                toks.extend(b.readers.values())
        for b in writes:
            toks.extend(b.writers.values())
            toks.extend(b.readers.values())
        if e == "pe":
            own = id(self.sem["pe"]) if self.sem["pe"] is not None else None
            toks = [t for t in toks if id(t[0]) != own]
        return self._waits(e, toks)

    def op(self, e, fn, reads=(), writes=()):
        waits = self._deps(e, reads, writes)
        tok = self._eng_tok(e)
        for b in reads:
            b.readers[id(tok[0])] = tok
        for b in writes:
            b.readers = {}
            b.writers = {id(tok[0]): tok}
        self.ninstr += 1

        h = self.h[e]
        for sem, val in waits:
            h.wait_ge(sem, val)
        fn(h).then_inc(tok[0], 1)

    def dma(self, q, out_ap, in_ap, reads=(), writes=(), **kw):
        waits = self._deps(q, reads, writes)
        assert len(writes) + len(reads) >= 1 and len(writes) <= 1 and len(reads) <= 1
        if writes:
            b = writes[0]
            if b.sem_in is None:
                b.sem_in, b.n_in = self._take_sem("di")
                self.live.append((b, "in"))
            b.n_in += 16
            tok = (b.sem_in, b.n_in)
            b.readers = {}
            b.writers = {id(tok[0]): tok}
            for rb in reads:
                rb.readers[id(tok[0])] = tok
        else:
            b = reads[0]
            if b.sem_out is None:
                b.sem_out, b.n_out = self._take_sem("do")
                self.live.append((b, "out"))
            b.n_out += 16
            tok = (b.sem_out, b.n_out)
            b.readers[id(tok[0])] = tok
        self.dma_toks[id(tok[0])] = tok
        self.ninstr += 1
        h = self.h[q]
        for sem, val in waits:
            h.wait_ge(sem, val)
        h.dma_start(out=out_ap, in_=in_ap, **kw).then_inc(tok[0], 16)

    def _take_sem(self, name):
        if self.sem_pool:
            return self.sem_pool.pop()
        return self.new_sem(name), 0

    def release_dma_sems(self):
        for b, kind in self.live:
            if kind == "in":
                self.sem_pool.append((b.sem_in, b.n_in)); b.sem_in = None
                b.writers = {}
            else:
                self.sem_pool.append((b.sem_out, b.n_out)); b.sem_out = None
                b.readers = {}
        self.live = []

    def dma_dd(self, q, out_ap, in_ap, **kw):
        self.dma(q, out_ap, in_ap, writes=[self.ddbuf], **kw)

    def dma_dd_async(self, q, out_ap, in_ap, **kw):
        self.dma(q, out_ap, in_ap, writes=[Buf("dd_async")], **kw)

    def barrier(self):
        toks = [t for t in self.last.values() if t is not None] + list(self.dma_toks.values())
        for e in self.ENGS:
            waits = self._waits(e, toks)
            for sem, val in waits:
                self.h[e].wait_ge(sem, val)

    def finalize(self):
        self.barrier()


class Ctx:
    def __init__(self):
        self.nc = bass.Bass("TRN2", target_bir_lowering=False)
        self.stack = ExitStack()
        self.S = Sched(self.nc, self.stack)
        self.n = 0
        self.cur = self.stack
        self.scopes = []
        self.bind = {}
        self.hook = None

    def dram_in(self, name, shape, dt=F32):
        if name in self.bind:
            return self.bind[name]
        return self.nc.dram_tensor(name, list(shape), dt, kind="ExternalInput").ap()

    def dram_out(self, name, shape, dt=F32):
        if name in self.bind:
            return self.bind[name]
        return self.nc.dram_tensor(name, list(shape), dt, kind="ExternalOutput").ap()

    def ext_in(self, name, shape, dt=F32):
        return self.nc.dram_tensor(name, list(shape), dt, kind="ExternalInput").ap()

    def ext_out(self, name, shape, dt=F32):
        return self.nc.dram_tensor(name, list(shape), dt, kind="ExternalOutput").ap()

    def sb(self, name, shape, dt):
        self.n += 1
        return self.cur.enter_context(self.nc.sbuf_tensor(f"{name}_{self.n}", list(shape), dt))

    def ps(self, name, shape, dt=F32):
        self.n += 1
        return self.cur.enter_context(self.nc.psum_tensor(f"{name}_{self.n}", list(shape), dt))

    def dram_tmp(self, name, shape, dt=F32):
        return self.nc.dram_tensor(name, list(shape), dt, kind="Internal").ap()

    def run_hook(self):
        if self.hook is not None:
            f, self.hook = self.hook, None
            f()

    def push(self):
        st = ExitStack()
        self.scopes.append(st)
        self.cur = st

    def pop(self):
        self.S.barrier()
        if len(self.scopes) == 1:
            self.S.release_dma_sems()
        self.scopes.pop().close()
        self.cur = self.scopes[-1] if self.scopes else self.stack

    def finish(self):
        self.S.finalize()
        self.stack.close()
        return self.nc


class Ring:
    def __init__(self, items):
        self.items = items
        self.i = 0

    def next(self):
        it = self.items[self.i % len(self.items)]
        self.i += 1
        return it


def run_interleaved(gens, width=2, stagger=2):
    it = iter(gens)
    active = []
    steps = 0
    while True:
        while len(active) < width and (not active or steps >= stagger):
            try:
                active.append(next(it))
            except StopIteration:
                break
        if not active:
            break
        steps += 1
        for g in list(active):
            try:
                next(g)
            except StopIteration:
                active.remove(g)


def mk_ring(cx, kind, name, n, shape, dt):
    items = []
    for i in range(n):
        t = cx.sb(f"{name}{i}", shape, dt) if kind == "sb" else cx.ps(f"{name}{i}", shape, dt)
        items.append((t, Buf(f"{name}{i}", excl=(kind == "ps"))))
    return Ring(items)


def emit_rmsnorm(cx, x_t, x_b, nchunk, TB, g_t, g_b, ones_t, ones_b, sq_ring, st_ring, rstd_ring,
                 out_t, out_b, nfeat, evac_engs=("dve",)):
    S = cx.S
    st_t, st_b = st_ring.next()
    for c in range(nchunk):
        sq_t, sq_b = sq_ring.next()
        S.op("act", lambda h, c=c, sq_t=sq_t: h.activation(out=sq_t[:, 0:TB], in_=x_t[:, c, 0:TB], func=AF.Square),
             reads=[x_b], writes=[sq_b])
        S.op("pe", lambda h, c=c, sq_t=sq_t: h.matmul(st_t[:, 0:TB], lhsT=ones_t[:, :], rhs=sq_t[:, 0:TB],
                                                        start=(c == 0), stop=(c == nchunk - 1)),
             reads=[sq_b, ones_b], writes=[st_b])
    r_t, r_b = rstd_ring.next()
    S.op("act", lambda h: h.activation(out=r_t[:, 0:TB], in_=st_t[:, 0:TB], func=AF.Sqrt, bias=float(nfeat * EPS)),
         reads=[st_b], writes=[r_b])
    S.op("dve", lambda h: h.reciprocal(out=r_t[:, 0:TB], in_=r_t[:, 0:TB]), reads=[r_b], writes=[r_b])
    for c in range(nchunk):
        e = evac_engs[c % len(evac_engs)]
        S.op(e, lambda h, c=c: h.scalar_tensor_tensor(out=out_t[:, c, 0:TB], in0=x_t[:, c, 0:TB],
                                                       scalar=g_t[:, c:c + 1], in1=r_t[:, 0:TB],
                                                       op0=ALU.mult, op1=ALU.mult),
             reads=[x_b, r_b, g_b], writes=[out_b])


def build_mlp(final_norm, ntok=TOK, dbg=False, cx=None, wbf16=False):
    TB = 256
    NB = ntok // TB
    FC = DFF // 128
    own = cx is None
    cx = Ctx() if own else cx
    cx.push()
    S = cx.S
    xT = cx.dram_in("xT", [D, ntok])
    w1 = cx.dram_in("w1", [D, DFF], BF16 if wbf16 else F32)
    w2 = cx.dram_in("w2", [DFF, D], BF16 if wbf16 else F32)
    gin = cx.dram_in("g", [128, KC])
    oT = cx.dram_out("oT", [D, ntok])
    if final_norm:
        gfin = cx.dram_in("gf", [128, KC])
    if dbg:
        dh = cx.dram_out("dh", [128, KC, TB], BF16)
        da = cx.dram_out("da", [128, DFF // 128, TB], BF16)

    w1b = cx.sb("w1b", [128, KC, DFF], BF16)
    w2b = cx.sb("w2b", [128, FC, D], BF16)
    w1_bufs = [Buf(f"w1_{k}") for k in range(KC)]
    w2_bufs = [Buf(f"w2_{k}") for k in range(8)]
    g_t = cx.sb("g", [128, KC], F32); g_b = Buf("g")
    ones_t = cx.sb("ones", [128, 128], BF16); ones_b = Buf("ones")
    x_ring = mk_ring(cx, "sb", "x", 2, [128, KC, TB], F32)
    h_ring = mk_ring(cx, "sb", "h", 2, [128, KC, TB], BF16)
    a_ring = mk_ring(cx, "sb", "a", 1, [128, FC, TB], BF16)
    r_ring = mk_ring(cx, "sb", "r", 3, [128, TB], BF16)
    sq_ring = mk_ring(cx, "sb", "sq", 3, [128, TB], BF16)
    rstd_ring = mk_ring(cx, "sb", "rstd", 2, [128, TB], F32)
    o_ring = mk_ring(cx, "sb", "o", 2, [128, KC, TB], F32)
    st_ring = mk_ring(cx, "ps", "st", 1, [128, 512], F32)
    p1_ring = mk_ring(cx, "ps", "p1", 3, [128, 512], F32)
    p2_ring = mk_ring(cx, "ps", "p2", 3, [128, 512], F32)
    if final_norm:
        gf_t = cx.sb("gf", [128, KC], F32); gf_b = Buf("gf")
        f_ring = mk_ring(cx, "sb", "f", 2, [128, KC, TB], F32)

    S.dma("sp", g_t[:, :], gin[:, :], writes=[g_b])
    S.op("dve", lambda h: h.tensor_scalar_mul(out=g_t[:, :], in0=g_t[:, :], scalar1=float(np.sqrt(D))),
         reads=[g_b], writes=[g_b])
    if final_norm:
        S.dma("sp", gf_t[:, :], gfin[:, :], writes=[gf_b])
        S.op("dve", lambda h: h.tensor_scalar_mul(out=gf_t[:, :], in0=gf_t[:, :], scalar1=float(np.sqrt(D))),
             reads=[gf_b], writes=[gf_b])
    S.op("pool", lambda h: h.memset(ones_t[:, :], 1.0), writes=[ones_b])
    w1v = w1.rearrange("(k p) n -> p k n", p=128)
    w2v = w2.rearrange("(f p) n -> p f n", p=128)
    wq = "sp" if wbf16 else "pool"
    for k in range(KC):
        S.dma(wq, w1b[:, k, :], w1v[:, k, :], writes=[w1_bufs[k]])
    for j in range(8):
        S.dma(wq, w2b[:, j * 4:(j + 1) * 4, :], w2v[:, j * 4:(j + 1) * 4, :], writes=[w2_bufs[j]])
    xv = xT.rearrange("(c p) t -> p c t", p=128)
    ov = oT.rearrange("(c p) t -> p c t", p=128)

    def prep(b):
        x_t, x_b = x_ring.next()
        S.dma("sp", x_t[:, :, :], xv[:, :, b * TB:(b + 1) * TB], writes=[x_b])
        h_t, h_b = h_ring.next()
        return (x_t, x_b, h_t, h_b)

    def norm(st_):
        x_t, x_b, h_t, h_b = st_
        emit_rmsnorm(cx, x_t, x_b, KC, TB, g_t, g_b, ones_t, ones_b, sq_ring, st_ring, rstd_ring, h_t, h_b, D)

    cur = prep(0)
    norm(cur)
    for b in range(NB):
        t0 = b * TB
        x_t, x_b, h_t, h_b = cur
        nxt = prep(b + 1) if b + 1 < NB else None
        a_t, a_b = a_ring.next()
        for f in range(FC):
            if f == FC // 2 and nxt is not None:
                norm(nxt)
            p_t, p_b = p1_ring.next()
            for k in range(KC):
                S.op("pe", lambda h, f=f, k=k, p_t=p_t: h.matmul(p_t[:, 0:TB], lhsT=w1b[:, k, f * 128:(f + 1) * 128],
                                                                  rhs=h_t[:, k, 0:TB], start=(k == 0), stop=(k == KC - 1)),
                     reads=[h_b, w1_bufs[k]], writes=[p_b])
            r_t, r_b = r_ring.next()
            S.op("act", lambda h, p_t=p_t, r_t=r_t: h.activation(out=r_t[:, 0:TB], in_=p_t[:, 0:TB], func=AF.Relu),
                 reads=[p_b], writes=[r_b])
            S.op("pool", lambda h, f=f, r_t=r_t: h.tensor_tensor(out=a_t[:, f, 0:TB], in0=r_t[:, 0:TB], in1=r_t[:, 0:TB],
                                                                  op=ALU.mult),
                 reads=[r_b], writes=[a_b])
        if dbg and b == 0:
            S.dma("sp", dh[:, :, :], h_t[:, :, :], reads=[h_b])
            S.dma("sp", da[:, :, :], a_t[:, :, :], reads=[a_b])
        o_t, o_b = o_ring.next()
        for c in range(KC):
            p_t, p_b = p2_ring.next()
            for f in range(FC):
                S.op("pe", lambda h, f=f, c=c, p_t=p_t: h.matmul(p_t[:, 0:TB], lhsT=w2b[:, f, c * 128:(c + 1) * 128],
                                                                  rhs=a_t[:, f, 0:TB], start=(f == 0), stop=(f == FC - 1)),
                     reads=[a_b, w2_bufs[f // 4]], writes=[p_b])
            S.op("dve", lambda h, c=c, p_t=p_t: h.tensor_tensor(out=o_t[:, c, 0:TB], in0=p_t[:, 0:TB], in1=x_t[:, c, 0:TB],
                                                                 op=ALU.add),
                 reads=[p_b, x_b], writes=[o_b])
        if final_norm:
            f_t, f_b = f_ring.next()
            emit_rmsnorm(cx, o_t, o_b, KC, TB, gf_t, gf_b, ones_t, ones_b, sq_ring, st_ring, rstd_ring, f_t, f_b, D)
            S.dma("pool", ov[:, :, t0:t0 + TB], f_t[:, :, :], reads=[f_b])
        else:
            S.dma("pool", ov[:, :, t0:t0 + TB], o_t[:, :, :], reads=[o_b])
        cur = nxt
    cx.pop()
    return cx.finish() if own else None


def run_spmd(nc, in_maps):
    res = run_bass_kernel_spmd(nc, in_maps, core_ids=list(range(len(in_maps))))
    return res.results


def vec128(v, k):
    return np.ascontiguousarray(np.asarray(v, np.float32).reshape(k, 128).T)


def load_cast(cx, q, dst_ap, src_ap, buf):
    cx.S.dma(q, dst_ap, src_ap, writes=[buf])


def build_ea(ntok=TOK, parts='quv', qlvl=4, cx=None):
    TB = 512
    NB = ntok // TB
    own = cx is None
    cx = Ctx() if own else cx
    cx.push()
    S = cx.S
    xT = cx.dram_in("xT", [D, ntok])
    w_in = cx.dram_in("w_in", [D, 1280])
    gin = cx.dram_in("g", [128, KC])
    cosd = cx.dram_in("cos", [128, ntok])
    sind = cx.dram_in("sin", [128, ntok])
    rotd = cx.dram_in("rot", [128, 128])
    QsT = cx.dram_out("QsT", [128, 4, ntok], BF16)
    KT = cx.dram_out("KT", [128, ntok], BF16)
    Vaug = cx.dram_out("Vaug", [ntok, 130], BF16)
    UT = cx.dram_out("UT", [128, 4, ntok], BF16)

    wb = cx.sb("wb", [128, KC, 1280], BF16)
    w_bufs = [Buf(f"w{k}") for k in range(KC)]
    g_t = cx.sb("g", [128, KC], F32); g_b = Buf("g")
    ones_t = cx.sb("ones", [128, 128], BF16); ones_b = Buf("ones")
    rot_t = cx.sb("rot", [128, 128], BF16); rot_b = Buf("rot")
    x_ring = mk_ring(cx, "sb", "x", 2, [128, KC, TB], F32)
    h_ring = mk_ring(cx, "sb", "h", 2, [128, KC, TB], BF16)
    sq_ring = mk_ring(cx, "sb", "sq", 3, [128, TB], BF16)
    rstd_ring = mk_ring(cx, "sb", "rstd", 2, [128, TB], F32)
    cos_ring = mk_ring(cx, "sb", "cos", 2, [128, TB], F32)
    sin_ring = mk_ring(cx, "sb", "sin", 2, [128, TB], F32)
    qb_ring = mk_ring(cx, "sb", "qb", 2, [128, TB], BF16)
    t1_ring = mk_ring(cx, "sb", "t1", 2, [128, TB], F32)
    t2_ring = mk_ring(cx, "sb", "t2", 2, [128, TB], F32)
    qo_ring = mk_ring(cx, "sb", "qo", 2, [128, 5, TB], BF16)
    uo_ring = mk_ring(cx, "sb", "uo", 2, [128, 4, TB], BF16)
    vo_ring = mk_ring(cx, "sb", "vo", 2, [128, 4, 130], BF16)
    st_ring = mk_ring(cx, "ps", "st", 1, [128, 512], F32)
    pq_ring = mk_ring(cx, "ps", "pq", 3, [128, 512], F32)
    pr_ring = mk_ring(cx, "ps", "pr", 2, [128, 512], F32)
    pv_ring = mk_ring(cx, "ps", "pv", 2, [128, 512], F32)

    S.dma("sp", g_t[:, :], gin[:, :], writes=[g_b])
    S.op("dve", lambda h: h.tensor_scalar_mul(out=g_t[:, :], in0=g_t[:, :], scalar1=float(np.sqrt(D))),
         reads=[g_b], writes=[g_b])
    S.op("pool", lambda h: h.memset(ones_t[:, :], 1.0), writes=[ones_b])
    S.dma("pool", rot_t[:, :], rotd[:, :], writes=[rot_b])
    for (vt, vb) in vo_ring.items:
        S.op("pool", lambda h, vt=vt: h.memset(vt[:, :, :], 1.0), writes=[vb])
    for k in range(KC):
        for j in range(2):
            src = w_in[k * 128:(k + 1) * 128, j * 256:(j + 1) * 256].rearrange("p (c d) -> p c d", c=4, d=64)
            dst = wb[:, k, 0:512].rearrange("p (c j d) -> p c j d", c=4, j=2, d=64)[:, :, j, :]
            S.dma("pool", dst, src, writes=[w_bufs[k]])
        S.dma("pool", wb[:, k, 512:1280], w_in[k * 128:(k + 1) * 128, 512:1280], writes=[w_bufs[k]])
    xv = xT.rearrange("(c p) t -> p c t", p=128)

    def blk(b):
        t0 = b * TB
        x_t, x_b = x_ring.next()
        S.dma("sp", x_t[:, :, :], xv[:, :, t0:t0 + TB], writes=[x_b])
        cos_t, cos_b = cos_ring.next()
        sin_t, sin_b = sin_ring.next()
        S.dma("sp", cos_t[:, :], cosd[:, t0:t0 + TB], writes=[cos_b])
        S.dma("sp", sin_t[:, :], sind[:, t0:t0 + TB], writes=[sin_b])
        h_t, h_b = h_ring.next()
        emit_rmsnorm(cx, x_t, x_b, KC, TB, g_t, g_b, ones_t, ones_b, sq_ring, st_ring, rstd_ring, h_t, h_b, D)
        yield
        qo_t, qo_b = qo_ring.next()
        for c in (range(5) if 'q' in parts else []):
            pq_t, pq_b = pq_ring.next()
            for k in range(KC):
                S.op("pe", lambda h, c=c, k=k, pq_t=pq_t, h_t=h_t: h.matmul(
                    pq_t[:, 0:TB], lhsT=wb[:, k, c * 128:(c + 1) * 128], rhs=h_t[:, k, :],
                    start=(k == 0), stop=(k == KC - 1)), reads=[h_b, w_bufs[k]], writes=[pq_b])
            qb_t, qb_b = qb_ring.next()
            S.op("act", lambda h, pq_t=pq_t, qb_t=qb_t: h.activation(out=qb_t[:, :], in_=pq_t[:, 0:TB], func=AF.Copy),
                 reads=[pq_b], writes=[qb_b])
            if qlvl == 1:
                S.op("act", lambda h, c=c, pq_t=pq_t, qo_t=qo_t: h.activation(out=qo_t[:, c, :], in_=pq_t[:, 0:TB], func=AF.Copy),
                     reads=[pq_b], writes=[qo_b])
                continue
            pr_t, pr_b = pr_ring.next()
            S.op("pe", lambda h, pr_t=pr_t, qb_t=qb_t: h.matmul(pr_t[:, 0:TB], lhsT=rot_t[:, :], rhs=qb_t[:, :],
                                                               start=True, stop=True),
                 reads=[qb_b, rot_b], writes=[pr_b])
            t1_t, t1_b = t1_ring.next()
            t2_t, t2_b = t2_ring.next()
            if qlvl == 2:
                S.op("act", lambda h, c=c, pr_t=pr_t, qo_t=qo_t: h.activation(out=qo_t[:, c, :], in_=pr_t[:, 0:TB], func=AF.Copy),
                     reads=[pr_b], writes=[qo_b])
                continue
            S.op("dve", lambda h, t1_t=t1_t, pq_t=pq_t, cos_t=cos_t: h.tensor_tensor(
                out=t1_t[:, :], in0=pq_t[:, 0:TB], in1=cos_t[:, :], op=ALU.mult), reads=[pq_b, cos_b], writes=[t1_b])
            if qlvl == 3:
                S.op("act", lambda h, c=c, t1_t=t1_t, qo_t=qo_t: h.activation(out=qo_t[:, c, :], in_=t1_t[:, :], func=AF.Copy),
                     reads=[t1_b], writes=[qo_b])
                continue
            S.op("dve", lambda h, t2_t=t2_t, pr_t=pr_t, sin_t=sin_t: h.tensor_tensor(
                out=t2_t[:, :], in0=pr_t[:, 0:TB], in1=sin_t[:, :], op=ALU.mult), reads=[pr_b, sin_b], writes=[t2_b])
            S.op("dve", lambda h, c=c, qo_t=qo_t, t1_t=t1_t, t2_t=t2_t: h.tensor_tensor(
                out=qo_t[:, c, :], in0=t1_t[:, :], in1=t2_t[:, :], op=ALU.add), reads=[t1_b, t2_b], writes=[qo_b])
        if 'q' in parts:
            S.dma("pool", QsT[:, :, t0:t0 + TB], qo_t[:, 0:4, :], reads=[qo_b])
            S.dma("pool", KT[:, t0:t0 + TB], qo_t[:, 4, :], reads=[qo_b])
        yield
        uo_t, uo_b = uo_ring.next()
        for gi in (range(4) if 'u' in parts else []):
            pq_t, pq_b = pq_ring.next()
            for k in range(KC):
                S.op("pe", lambda h, gi=gi, k=k, pq_t=pq_t, h_t=h_t: h.matmul(
                    pq_t[:, 0:TB], lhsT=wb[:, k, 768 + gi * 128:768 + (gi + 1) * 128], rhs=h_t[:, k, :],
                    start=(k == 0), stop=(k == KC - 1)), reads=[h_b, w_bufs[k]], writes=[pq_b])
            S.op("act", lambda h, gi=gi, pq_t=pq_t, uo_t=uo_t: h.activation(out=uo_t[:, gi, :], in_=pq_t[:, 0:TB], func=AF.Copy),
                 reads=[pq_b], writes=[uo_b])
        if 'u' in parts:
            S.dma("pool", UT[:, :, t0:t0 + TB], uo_t[:, :, :], reads=[uo_b])
        if 'v' not in parts:
            return
        yield
        vo_t, vo_b = vo_ring.next()
        pv_t, pv_b = pv_ring.next()
        for ti in range(TB // 128):
            for k in range(KC):
                S.op("pe", lambda h, ti=ti, k=k, pv_t=pv_t, h_t=h_t: h.matmul(
                    pv_t[:, ti * 128:(ti + 1) * 128], lhsT=h_t[:, k, ti * 128:(ti + 1) * 128], rhs=wb[:, k, 640:768],
                    start=(k == 0), stop=(k == KC - 1)), reads=[h_b, w_bufs[k]], writes=[pv_b])
        for ti in range(TB // 128):
            for j in range(2):
                S.op("act", lambda h, ti=ti, j=j, pv_t=pv_t, vo_t=vo_t: h.activation(
                    out=vo_t[:, ti, j * 65:j * 65 + 64], in_=pv_t[:, ti * 128 + j * 64:ti * 128 + (j + 1) * 64], func=AF.Copy),
                    reads=[pv_b], writes=[vo_b])
        S.dma("pool", Vaug[t0:t0 + TB, :].rearrange("(i p) n -> p i n", p=128), vo_t[:, :, :], reads=[vo_b])
        yield

    run_interleaved((blk(b) for b in range(NB)), 2, 2)
    cx.pop()
    return cx.finish() if own else None


def rope_tables(pos, half, nrows):
    inv = (np.float32(10000.0) ** (-np.arange(half, dtype=np.float32) / np.float32(half))).astype(np.float32)
    ang = pos.astype(np.float32)[None, :] * inv[np.arange(nrows) % half][:, None]
    return np.cos(ang).astype(np.float32), np.sin(ang).astype(np.float32)


def rot_matrix(dh, nrows=128):
    R = np.zeros((nrows, nrows), np.float32)
    half = dh // 2
    for m in range(nrows):
        d = m % dh
        base = m - d
        if d < half:
            R[base + d + half, m] = -1.0
        else:
            R[base + d - half, m] = 1.0
    return R


def build_eb(ntok=TOK, cx=None):
    TB = 512
    NB = ntok // TB
    NT = ntok // 128
    own = cx is None
    cx = Ctx() if own else cx
    cx.push()
    S = cx.S
    QsT = cx.dram_in("QsT", [128, 4, ntok], BF16)
    KTh = cx.dram_in("KTh", [128, ntok + 256], BF16)
    Vh = cx.dram_in("Vh", [ntok + 256, 130], BF16)
    UTh = cx.dram_in("UTh", [128, 4, ntok + 16], BF16)
    xT = cx.dram_in("xT", [D, ntok])
    w_pool = cx.dram_in("w_pool", [4, 128, 128])
    pscale = cx.dram_in("pscale", [128, 4])
    w_out = cx.dram_in("w_out", [D, D])
    sinkrow = cx.dram_in("sinkrow", [1, 2, 512])
    masksd = cx.dram_in("masks", [4, 128, 512])
    invcd = cx.dram_in("invc", [128, 2, 4, 16])
    oT = cx.dram_out("oT", [D, ntok])

    woA = cx.sb("woA", [128, 4, D], BF16); woA_b = Buf("woA")
    woB = cx.sb("woB", [128, 4, D], BF16); woB_b = Buf("woB")
    wp = cx.sb("wp", [128, 4, 128], BF16); wp_b = Buf("wp")
    ps_t = cx.sb("ps", [128, 4], F32); ps_b = Buf("ps")
    mk_t = cx.sb("mk", [128, 4, 512], BF16); mk_b = Buf("mk")
    invc_t = cx.sb("invc", [128, 2, 4, 16], F32); invc_b = Buf("invc")
    sk_t = cx.sb("sk", [1, 2, 512], F32); sk_b = Buf("sk")
    esk_t = cx.sb("esk", [1, 2, 512], BF16); esk_b = Buf("esk")
    sel_t = cx.sb("sel", [1, 128], BF16); sel_b = Buf("sel")
    ones32 = cx.sb("ones32", [128, 64], F32); ones32_b = Buf("ones32")
    qsA_ring = mk_ring(cx, "sb", "qsA", 2, [128, 4, TB], BF16)
    qsB_ring = mk_ring(cx, "sb", "qsB", 2, [128, 4, TB], BF16)
    kt_ring = mk_ring(cx, "sb", "kt", 2, [128, 6 * 128], BF16)
    v_ring = mk_ring(cx, "sb", "v", 2, [128, 6, 130], BF16)
    u_ring = mk_ring(cx, "sb", "u", 2, [128, 4, TB + 16], BF16)
    x_ring = mk_ring(cx, "sb", "x", 2, [128, KC, TB], F32)
    p_ring = mk_ring(cx, "sb", "p", 4, [128, 512], BF16)
    osb_ring = mk_ring(cx, "sb", "osb", 3, [64, 512], F32)
    rc_ring = mk_ring(cx, "sb", "rc", 3, [128, 512], F32)
    ya_ring = mk_ring(cx, "sb", "ya", 2, [64, 8, TB], BF16)
    yp_ring = mk_ring(cx, "sb", "yp", 2, [128, 4, TB], BF16)
    yb_ring = mk_ring(cx, "sb", "yb", 2, [128, 4, TB], BF16)
    d_ring = mk_ring(cx, "sb", "d", 2, [128, 4, TB], BF16)
    tmp_rings = [mk_ring(cx, "sb", f"tp{g}", 2, [128, TB + 16], F32) for g in range(4)]
    e16_ring = mk_ring(cx, "sb", "e16", 2, [128, 16], F32)
    s_ring = mk_ring(cx, "ps", "s", 3, [128, 512], F32)
    o_ring = mk_ring(cx, "ps", "o", 2, [128, 512], F32)
    bc_ring = mk_ring(cx, "ps", "bc", 1, [128, 512], F32)
    y_ring = mk_ring(cx, "ps", "y", 2, [128, 512], F32)

    S.dma("pool", woA[:, :, :], w_out[0:512, :].rearrange("(i p) n -> p i n", p=128), writes=[woA_b])
    S.dma("pool", woB[:, :, :], w_out[512:1024, :].rearrange("(g p) n -> p g n", p=128), writes=[woB_b])
    S.dma("pool", wp[:, :, :], w_pool.rearrange("g i j -> i g j"), writes=[wp_b])
    S.dma("pool", mk_t[:, :, :], masksd.rearrange("m p n -> p m n"), writes=[mk_b])
    S.dma("sp", ps_t[:, :], pscale[:, :], writes=[ps_b])
    S.dma("sp", invc_t[:, :, :, :], invcd[:, :, :, :], writes=[invc_b])
    S.dma("sp", sk_t[:, :, :], sinkrow[:, :, :], writes=[sk_b])
    S.op("act", lambda h: h.activation(out=esk_t[:, :, :], in_=sk_t[:, :, :], func=AF.Exp), reads=[sk_b], writes=[esk_b])
    S.op("pool", lambda h: h.memset(sel_t[:, :], 0.0), writes=[sel_b])
    S.op("pool", lambda h: h.memset(sel_t[:, 64:65], 1.0), writes=[sel_b])
    S.op("pool", lambda h: h.memset(ones32[:, :], 1.0), writes=[ones32_b])
    for (qt, qb_) in qsA_ring.items:
        S.op("pool", lambda h, qt=qt: h.memset(qt[64:128, :, :], 0.0), writes=[qb_])
    for (qt, qb_) in qsB_ring.items:
        S.op("pool", lambda h, qt=qt: h.memset(qt[0:64, :, :], 0.0), writes=[qb_])
    xv = xT.rearrange("(c p) t -> p c t", p=128)
    ov = oT.rearrange("(c p) t -> p c t", p=128)
    cx.run_hook()

    def blk(b):
        t0 = b * TB
        qsA_t, qsA_b = qsA_ring.next()
        qsB_t, qsB_b = qsB_ring.next()
        kt_t, kt_b = kt_ring.next()
        v_t, v_b = v_ring.next()
        u_t, u_b = u_ring.next()
        x_t, x_b = x_ring.next()
        S.dma("sp", qsA_t[0:64, :, :], QsT[0:64, :, t0:t0 + TB], writes=[qsA_b])
        S.dma("sp", qsB_t[64:128, :, :], QsT[64:128, :, t0:t0 + TB], writes=[qsB_b])
        S.dma("sp", kt_t[:, :], KTh[:, t0:t0 + 768], writes=[kt_b])
        S.dma("sp", v_t[:, :, :], Vh[t0:t0 + 768, :].rearrange("(i p) n -> p i n", p=128), writes=[v_b])
        S.dma("sp", u_t[:, :, :], UTh[:, :, t0:t0 + TB + 16], writes=[u_b])
        S.dma("sp", x_t[:, :, :], xv[:, :, t0:t0 + TB], writes=[x_b])
        yield
        ya_t, ya_b = ya_ring.next()
        tiles = [(nl, j, mi, dm) for nl in range(4) for mi, dm in enumerate((-1, 0, 1)) for j in range(2)]
        LA = 2
        st = {}
        unit_o = {}
        deferred = []

        def emit_S(t):
            nl, j, mi, dm = tiles[t]
            i = nl + dm + 1
            s_t, s_b = s_ring.next()
            q_t, q_b = (qsA_t, qsA_b) if j == 0 else (qsB_t, qsB_b)
            S.op("pe", lambda h: h.matmul(s_t[:, :], lhsT=kt_t[:, i * 128:(i + 1) * 128],
                                          rhs=q_t[:, :, nl * 128:(nl + 1) * 128], start=True, stop=True),
                 reads=[kt_b, q_b], writes=[s_b])
            st[t] = (s_t, s_b)

        def flush_deferred():
            while deferred:
                (o_t, o_b, osb_t, osb_b, rc_t, rc_b, nl, j) = deferred.pop(0)
                bc_t, bc_b = bc_ring.next()
                S.op("pe", lambda h: h.matmul(bc_t[0:64, :], lhsT=ones32[64:65, 0:64], rhs=rc_t[64:65, :], start=True, stop=True),
                     reads=[rc_b, ones32_b], writes=[bc_b])
                S.op("dve", lambda h: h.tensor_tensor(
                    out=ya_t[0:64, j * 4:(j + 1) * 4, nl * 128:(nl + 1) * 128],
                    in0=osb_t[:, :].rearrange("p (c q) -> p c q", c=4),
                    in1=bc_t[0:64, :].rearrange("p (c q) -> p c q", c=4), op=ALU.mult),
                    reads=[osb_b, bc_b], writes=[ya_b])

        for t in range(min(LA, len(tiles))):
            emit_S(t)
        for t in range(len(tiles)):
            nl, j, mi, dm = tiles[t]
            n = 4 * b + nl
            i = nl + dm + 1
            if mi == 0:
                unit_o[(nl, j)] = o_ring.next()
            o_t, o_b = unit_o[(nl, j)]
            s_t, s_b = st.pop(t)
            p_t, p_b = p_ring.next()
            S.op("act", lambda h: h.activation(out=p_t[:, :], in_=s_t[:, :], func=AF.Exp, scale=0.125), reads=[s_b], writes=[p_b])
            if dm != 0:
                if dm == -1:
                    mi_ = 2 if n == 0 else 0
                else:
                    mi_ = 3 if n == NT - 1 else 1
                S.op("pool", lambda h: h.tensor_tensor(out=p_t[:, :], in0=p_t[:, :], in1=mk_t[:, mi_, :], op=ALU.mult),
                     reads=[p_b, mk_b], writes=[p_b])
            if t + LA < len(tiles):
                emit_S(t + LA)
            S.op("pe", lambda h: h.matmul(o_t[0:65, :], lhsT=v_t[:, i, j * 65:(j + 1) * 65], rhs=p_t[:, :], start=(mi == 0), stop=False),
                 reads=[v_b, p_b], writes=[o_b])
            if mi == 0 and j == 1:
                flush_deferred()
            if mi == 2:
                S.op("pe", lambda h: h.matmul(o_t[0:65, :], lhsT=sel_t[0:1, 0:65], rhs=esk_t[0:1, j, :], start=False, stop=True),
                     reads=[sel_b, esk_b], writes=[o_b])
                osb_t, osb_b = osb_ring.next()
                rc_t, rc_b = rc_ring.next()
                S.op("act", lambda h: h.activation(out=osb_t[:, :], in_=o_t[0:64, :], func=AF.Copy), reads=[o_b], writes=[osb_b])
                S.op("dve", lambda h: h.reciprocal(out=rc_t[64:65, :], in_=o_t[64:65, :]), reads=[o_b], writes=[rc_b])
                deferred.append((o_t, o_b, osb_t, osb_b, rc_t, rc_b, nl, j))
        flush_deferred()
        yield
        yp_t, yp_b = yp_ring.next()
        S.dma("sp", yp_t[0:64, :, :], ya_t[0:64, 0:8:2, :], reads=[ya_b], writes=[yp_b])
        S.dma("sp", yp_t[64:128, :, :], ya_t[0:64, 1:8:2, :], reads=[ya_b], writes=[yp_b])
        d_t, d_b = d_ring.next()
        L = TB + 16
        for g in range(4):
            w = 2 << g
            steps = g + 1
            src_t, src_b, ln = None, None, L
            for s_i in range(steps):
                sh = 1 << s_i
                tp_t, tp_b = tmp_rings[g].next()
                nl_ = ln - sh
                if s_i == 0:
                    S.op("pool", lambda h, tp_t=tp_t, u_t=u_t, g=g, nl_=nl_, sh=sh: h.tensor_tensor(
                        out=tp_t[:, 0:nl_], in0=u_t[:, g, 0:nl_], in1=u_t[:, g, sh:sh + nl_], op=ALU.add),
                        reads=[u_b], writes=[tp_b])
                else:
                    S.op("pool", lambda h, tp_t=tp_t, src_t=src_t, nl_=nl_, sh=sh: h.tensor_tensor(
                        out=tp_t[:, 0:nl_], in0=src_t[:, 0:nl_], in1=src_t[:, sh:sh + nl_], op=ALU.add),
                        reads=[src_b], writes=[tp_b])
                src_t, src_b, ln = tp_t, tp_b, nl_
            off = 8 - w // 2
            S.op("dve", lambda h, d_t=d_t, src_t=src_t, u_t=u_t, g=g, off=off, w=w: h.scalar_tensor_tensor(
                out=d_t[:, g, :], in0=src_t[:, off:off + TB], scalar=1.0 / w, in1=u_t[:, g, 8:8 + TB],
                op0=ALU.mult, op1=ALU.subtract), reads=[src_b, u_b], writes=[d_b])
            for (is_edge, which, c0) in ((b == 0, 0, 0), (b == NB - 1, 1, TB - 16)):
                if not is_edge:
                    continue
                e_t, e_b = e16_ring.next()
                S.op("dve", lambda h, e_t=e_t, src_t=src_t, g=g, off=off, c0=c0, which=which: h.tensor_tensor(
                    out=e_t[:, :], in0=src_t[:, off + c0:off + c0 + 16], in1=invc_t[:, which, g, :], op=ALU.mult),
                    reads=[src_b, invc_b], writes=[e_b])
                S.op("dve", lambda h, e_t=e_t, d_t=d_t, u_t=u_t, g=g, c0=c0: h.tensor_tensor(
                    out=d_t[:, g, c0:c0 + 16], in0=e_t[:, :], in1=u_t[:, g, 8 + c0:8 + c0 + 16], op=ALU.subtract),
                    reads=[e_b, u_b, d_b], writes=[d_b])
        yield
        yb_t, yb_b = yb_ring.next()
        for g in range(4):
            y_t, y_b = y_ring.next()
            S.op("pe", lambda h, y_t=y_t, d_t=d_t, g=g: h.matmul(y_t[:, :], lhsT=wp[:, g, :], rhs=d_t[:, g, :], start=True, stop=True),
                 reads=[wp_b, d_b], writes=[y_b])
            S.op("dve", lambda h, y_t=y_t, yb_t=yb_t, g=g: h.tensor_scalar_mul(out=yb_t[:, g, :], in0=y_t[:, :], scalar1=ps_t[:, g:g + 1]),
                 reads=[y_b, ps_b], writes=[yb_b])
        for o in range(KC):
            y_t, y_b = y_ring.next()
            for hh in range(4):
                S.op("pe", lambda h, y_t=y_t, yp_t=yp_t, hh=hh, o=o: h.matmul(
                    y_t[:, :], lhsT=woA[:, hh, o * 128:(o + 1) * 128], rhs=yp_t[:, hh, :], start=(hh == 0), stop=False),
                    reads=[woA_b, yp_b], writes=[y_b])
            for g in range(4):
                S.op("pe", lambda h, y_t=y_t, yb_t=yb_t, g=g, o=o: h.matmul(
                    y_t[:, :], lhsT=woB[:, g, o * 128:(o + 1) * 128], rhs=yb_t[:, g, :], start=False, stop=(g == 3)),
                    reads=[woB_b, yb_b], writes=[y_b])
            S.op("dve", lambda h, y_t=y_t, x_t=x_t, o=o: h.tensor_tensor(out=x_t[:, o, :], in0=y_t[:, :], in1=x_t[:, o, :], op=ALU.add),
                 reads=[y_b, x_b], writes=[x_b])
        S.dma("sp", ov[:, :, t0:t0 + TB], x_t[:, :, :], reads=[x_b])
        yield

    run_interleaved((blk(b) for b in range(NB)), 2, 2)
    cx.pop()
    return cx.finish() if own else None


def eb_masks(has_left, has_right):
    ki = np.arange(128)[:, None]
    qi = np.arange(128)[None, :]
    mL = np.tile((ki >= qi).astype(np.float32), (1, 4))
    mR = np.tile((ki <= qi).astype(np.float32), (1, 4))
    return np.stack([mL, mR, mL * float(has_left), mR * float(has_right)]).astype(np.float32)


def eb_invc(is_first, is_last):
    out = np.zeros((128, 2, 4, 16), np.float32)
    for g in range(4):
        w = 2 << g
        half = w // 2
        for i in range(16):
            c0 = min(i + half, w) if is_first else w
            r = 16 - i
            c1 = min(half + r, w) if is_last else w
            out[:, 0, g, i] = 1.0 / c0
            out[:, 1, g, i] = 1.0 / c1
    return out


def emit_rope(cx, src_t, src_b, nrow, TB, rot_t, rot_b, cos_t, cos_b, sin_t, sin_b, qb_ring, pr_ring, t1_ring, t2_ring,
              out_ap, out_b):
    S = cx.S
    qb_t, qb_b = qb_ring.next()
    S.op("act", lambda h: h.activation(out=qb_t[0:nrow, :], in_=src_t[0:nrow, 0:TB], func=AF.Copy), reads=[src_b], writes=[qb_b])
    pr_t, pr_b = pr_ring.next()
    S.op("pe", lambda h: h.matmul(pr_t[0:nrow, 0:TB], lhsT=rot_t[0:nrow, 0:nrow], rhs=qb_t[0:nrow, :], start=True, stop=True),
         reads=[qb_b, rot_b], writes=[pr_b])
    t1_t, t1_b = t1_ring.next()
    t2_t, t2_b = t2_ring.next()
    S.op("dve", lambda h: h.tensor_tensor(out=t1_t[0:nrow, :], in0=src_t[0:nrow, 0:TB], in1=cos_t[0:nrow, :], op=ALU.mult),
         reads=[src_b, cos_b], writes=[t1_b])
    S.op("dve", lambda h: h.tensor_tensor(out=t2_t[0:nrow, :], in0=pr_t[0:nrow, 0:TB], in1=sin_t[0:nrow, :], op=ALU.mult),
         reads=[pr_b, sin_b], writes=[t2_b])
    S.op("dve", lambda h: h.tensor_tensor(out=out_ap, in0=t1_t[0:nrow, :], in1=t2_t[0:nrow, :], op=ALU.add),
         reads=[t1_b, t2_b], writes=[out_b])


def build_oa(ntok=TOK, cx=None, mid_hook=None):
    TB = 512
    NB = ntok // TB
    NT = ntok // 128
    own = cx is None
    cx = Ctx() if own else cx
    cx.push()
    S = cx.S
    xT = cx.dram_in("xT", [D, ntok])
    xhalo = cx.dram_in("xhalo", [D, 4])
    w_in = cx.dram_in("w_in", [D, 1440])
    gin = cx.dram_in("g", [128, KC])
    gcq = cx.dram_in("g_cq", [128, 2])
    gckv = cx.dram_in("g_ckv", [128, 1])
    w_uq = cx.dram_in("w_uq", [256, 768])
    w_ukv = cx.dram_in("w_ukv", [128, 1024])
    cwd = cx.dram_in("cw", [128, 4, 4])
    cbd = cx.dram_in("cb", [128, 4])
    wad = cx.dram_in("wa", [2, 8, 64, 64])
    wxd = cx.dram_in("wx", [2, 8, 64, 64])
    bad = cx.dram_in("ba", [128, 2, 4])
    bxd = cx.dram_in("bx", [128, 2, 4])
    lamd = cx.dram_in("lam", [128, 2, 4])
    cosd = cx.dram_in("cos", [128, ntok])
    sind = cx.dram_in("sin", [128, ntok])
    rotd = cx.dram_in("rot", [128, 128])
    QN = cx.dram_out("QN", [512, ntok], BF16)
    QR = cx.dram_out("QR", [256, ntok], BF16)
    KNR = cx.dram_out("KNR", [544, ntok], BF16)
    V5 = cx.dram_out("V5", [1024, NT * 65], BF16)
    GX = cx.dram_out("GX", [512, ntok], BF16)
    AB = cx.dram_out("AB", [2, 2, 512, ntok])
    BLK = cx.dram_out("BLK", [128, NB, 2, 2, 4])
    CAB = cx.dram_out("CAB", [128, 2, 2, 4])
    XR = cx.dram_out("XR", [512, ntok])

    g_t = cx.sb("g", [128, KC], F32); g_b = Buf("g")
    gcq_t = cx.sb("gcq", [128, 2], F32); gcq_b = Buf("gcq")
    gckv_t = cx.sb("gckv", [128, 1], F32); gckv_b = Buf("gckv")
    ones_t = cx.sb("ones", [128, 128], BF16); ones_b = Buf("ones")
    xrh_t = cx.sb("xrh", [128, 4, 4], F32); xrh_b = Buf("xrh")
    cp_t = cx.sb("cp", [128, 2, 4], F32); cp_b = Buf("cp")
    blk_t = cx.sb("blk", [128, NB, 2, 2, 4], F32); blk_b = Buf("blk")
    S.dma("sp", g_t[:, :], gin[:, :], writes=[g_b])
    S.op("dve", lambda h: h.tensor_scalar_mul(out=g_t[:, :], in0=g_t[:, :], scalar1=float(np.sqrt(D))), reads=[g_b], writes=[g_b])
    S.dma("sp", gcq_t[:, :], gcq[:, :], writes=[gcq_b])
    S.op("dve", lambda h: h.tensor_scalar_mul(out=gcq_t[:, :], in0=gcq_t[:, :], scalar1=16.0), reads=[gcq_b], writes=[gcq_b])
    S.dma("sp", gckv_t[:, :], gckv[:, :], writes=[gckv_b])
    S.op("dve", lambda h: h.tensor_scalar_mul(out=gckv_t[:, :], in0=gckv_t[:, :], scalar1=float(np.sqrt(128.0))),
         reads=[gckv_b], writes=[gckv_b])
    S.op("pool", lambda h: h.memset(ones_t[:, :], 1.0), writes=[ones_b])
    S.dma("sp", cp_t[:, :, :], lamd[:, :, :], writes=[cp_b])
    S.op("act", lambda h: h.activation(out=cp_t[:, :, :], in_=cp_t[:, :, :], func=AF.Exp, scale=-1.0), reads=[cp_b], writes=[cp_b])
    S.op("act", lambda h: h.activation(out=cp_t[:, :, :], in_=cp_t[:, :, :], func=AF.Ln, bias=1.0), reads=[cp_b], writes=[cp_b])
    S.op("dve", lambda h: h.tensor_scalar_mul(out=cp_t[:, :, :], in0=cp_t[:, :, :], scalar1=-8.0), reads=[cp_b], writes=[cp_b])

    wabd = cx.sb("wabd", [128, 2, 4, 128], BF16); wxbd = cx.sb("wxbd", [128, 2, 4, 128], BF16); bd_b = Buf("bd")
    cw_t = cx.sb("cw", [128, 4, 4], F32); cb_t = cx.sb("cb", [128, 4], F32); cw_b = Buf("cw")
    ba_t = cx.sb("ba", [128, 2, 4], F32); bx_t = cx.sb("bx", [128, 2, 4], F32); bb_b = Buf("bb")
    S.op("pool", lambda h: h.memset(wabd[:, :, :, :], 0.0), writes=[bd_b])
    S.op("pool", lambda h: h.memset(wxbd[:, :, :, :], 0.0), writes=[bd_b])
    for d in range(2):
        for c in range(4):
            for hf in range(2):
                S.dma("pool", wabd[hf * 64:(hf + 1) * 64, d, c, hf * 64:(hf + 1) * 64], wad[d, 2 * c + hf, :, :], writes=[bd_b])
                S.dma("pool", wxbd[hf * 64:(hf + 1) * 64, d, c, hf * 64:(hf + 1) * 64], wxd[d, 2 * c + hf, :, :], writes=[bd_b])
    S.dma("sp", cw_t[:, :, :], cwd[:, :, :], writes=[cw_b])
    S.dma("sp", cb_t[:, :], cbd[:, :], writes=[cw_b])
    S.dma("sp", ba_t[:, :, :], bad[:, :, :], writes=[bb_b])
    S.dma("sp", bx_t[:, :, :], bxd[:, :, :], writes=[bb_b])
    xv = xT.rearrange("(c p) t -> p c t", p=128)
    cx.push()
    wb = cx.sb("wb", [128, KC, 1440], BF16)
    w_bufs = [Buf(f"w{k}") for k in range(KC)]
    wuqn = cx.sb("wuqn", [128, 2, 512], BF16); wuqr = cx.sb("wuqr", [128, 2, 256], BF16); wuq_b = Buf("wuq")
    wk = cx.sb("wk", [128, 512], BF16); wv = cx.sb("wv", [128, 512], BF16); wkv_b = Buf("wkv")
    rot_t = cx.sb("rot", [128, 128], BF16); rot_b = Buf("rot")
    x_ring = mk_ring(cx, "sb", "x", 2, [128, KC, TB], F32)
    h_ring = mk_ring(cx, "sb", "h", 2, [128, KC, TB], BF16)
    sq_ring = mk_ring(cx, "sb", "sq", 3, [128, TB], BF16)
    rstd_ring = mk_ring(cx, "sb", "rstd", 2, [128, TB], F32)
    cos_ring = mk_ring(cx, "sb", "cos", 2, [128, TB], F32)
    sin_ring = mk_ring(cx, "sb", "sin", 2, [128, TB], F32)
    qb_ring = mk_ring(cx, "sb", "qb", 2, [128, TB], BF16)
    t1_ring = mk_ring(cx, "sb", "t1", 2, [128, TB], F32)
    t2_ring = mk_ring(cx, "sb", "t2", 2, [128, TB], F32)
    cq_ring = mk_ring(cx, "sb", "cq", 2, [128, 2, TB], F32)
    ckv_ring = mk_ring(cx, "sb", "ckv", 2, [128, 1, TB], F32)
    cqn_ring = mk_ring(cx, "sb", "cqn", 2, [128, 2, TB], BF16)
    ckvn_ring = mk_ring(cx, "sb", "ckvn", 2, [128, 1, TB], BF16)
    xr_ring = mk_ring(cx, "sb", "xr", 2, [128, 4, TB], F32)
    gx_ring = mk_ring(cx, "sb", "gx", 2, [128, 4, TB], BF16)
    qn_ring = mk_ring(cx, "sb", "qn", 2, [128, 4, TB], BF16)
    qr_ring = mk_ring(cx, "sb", "qr", 2, [128, 2, TB], BF16)
    kn_ring = mk_ring(cx, "sb", "kn", 2, [128, 4, TB], BF16)
    kr_ring = mk_ring(cx, "sb", "kr", 2, [32, TB], BF16)
    vo_ring = mk_ring(cx, "sb", "vo", 2, [128, 4, 520], BF16)
    hx_t = cx.sb("hx", [128, KC, 4], F32); hx_b = Buf("hx")
    hh_t = cx.sb("hh", [128, KC, 4], BF16); hh_b = Buf("hh")
    st_ring = mk_ring(cx, "ps", "st", 1, [128, 512], F32)
    pq_ring = mk_ring(cx, "ps", "pq", 4, [128, 512], F32)
    pr_ring = mk_ring(cx, "ps", "pr", 1, [128, 512], F32)
    pv_ring = mk_ring(cx, "ps", "pv", 2, [128, 512], F32)

    for k in range(KC):
        S.dma("pool", wb[:, k, :], w_in[k * 128:(k + 1) * 128, :], writes=[w_bufs[k]])
    for k in range(2):
        src = w_uq[k * 128:(k + 1) * 128, :].rearrange("p (h e) -> p h e", e=96)
        S.dma("pool", wuqn[:, k, :].rearrange("p (h d) -> p h d", d=64), src[:, :, 0:64], writes=[wuq_b])
        S.dma("pool", wuqr[:, k, :].rearrange("p (h d) -> p h d", d=32), src[:, :, 64:96], writes=[wuq_b])
    srckv = w_ukv.rearrange("p (h e) -> p h e", e=128)
    S.dma("pool", wk[:, :].rearrange("p (h d) -> p h d", d=64), srckv[:, :, 0:64], writes=[wkv_b])
    S.dma("pool", wv[:, :].rearrange("p (h d) -> p h d", d=64), srckv[:, :, 64:128], writes=[wkv_b])
    S.dma("pool", rot_t[:, :], rotd[:, :], writes=[rot_b])
    for (vt, vb) in vo_ring.items:
        S.op("pool", lambda h, vt=vt: h.memset(vt[:, :, :], 1.0), writes=[vb])

    def proj_tile(h_t, h_b, c0, ncols, TBx):
        pq_t, pq_b = pq_ring.next()
        for k in range(KC):
            S.op("pe", lambda h, k=k: h.matmul(pq_t[0:ncols, 0:TBx], lhsT=wb[:, k, c0:c0 + ncols], rhs=h_t[:, k, 0:TBx],
                                               start=(k == 0), stop=(k == KC - 1)), reads=[h_b, w_bufs[k]], writes=[pq_b])
        return pq_t, pq_b

    S.dma("sp", hx_t[:, :, :], xhalo.rearrange("(c p) t -> p c t", p=128), writes=[hx_b])
    emit_rmsnorm(cx, hx_t, hx_b, KC, 4, g_t, g_b, ones_t, ones_b, sq_ring, st_ring, rstd_ring, hh_t, hh_b, D)
    for c in range(4):
        pq_t, pq_b = proj_tile(hh_t, hh_b, 416 + c * 128, 128, 4)
        S.op("act", lambda h, c=c, pq_t=pq_t: h.activation(out=xrh_t[:, c, :], in_=pq_t[:, 0:4], func=AF.Copy),
             reads=[pq_b], writes=[xrh_b])

    def blk(b):
        t0 = b * TB
        x_t, x_b = x_ring.next()
        S.dma("sp", x_t[:, :, :], xv[:, :, t0:t0 + TB], writes=[x_b])
        cos_t, cos_b = cos_ring.next()
        sin_t, sin_b = sin_ring.next()
        S.dma("sp", cos_t[:, :], cosd[:, t0:t0 + TB], writes=[cos_b])
        S.dma("sp", sin_t[:, :], sind[:, t0:t0 + TB], writes=[sin_b])
        h_t, h_b = h_ring.next()
        emit_rmsnorm(cx, x_t, x_b, KC, TB, g_t, g_b, ones_t, ones_b, sq_ring, st_ring, rstd_ring, h_t, h_b, D)
        yield
        cq_t, cq_b = cq_ring.next()
        for c in range(2):
            pq_t, pq_b = proj_tile(h_t, h_b, c * 128, 128, TB)
            S.op("act", lambda h, c=c, pq_t=pq_t, cq_t=cq_t: h.activation(out=cq_t[:, c, :], in_=pq_t[:, 0:TB], func=AF.Copy),
                 reads=[pq_b], writes=[cq_b])
        ckv_t, ckv_b = ckv_ring.next()
        pq_t, pq_b = proj_tile(h_t, h_b, 256, 128, TB)
        S.op("act", lambda h, pq_t=pq_t, ckv_t=ckv_t: h.activation(out=ckv_t[:, 0, :], in_=pq_t[:, 0:TB], func=AF.Copy),
             reads=[pq_b], writes=[ckv_b])
        yield
        pq_t, pq_b = proj_tile(h_t, h_b, 384, 32, TB)
        kr_t, kr_b = kr_ring.next()
        emit_rope(cx, pq_t, pq_b, 32, TB, rot_t, rot_b, cos_t, cos_b, sin_t, sin_b, qb_ring, pr_ring, t1_ring, t2_ring,
                  kr_t[0:32, :], kr_b)
        S.dma("pool", KNR[512:544, t0:t0 + TB], kr_t[:, :], reads=[kr_b])
        yield
        xr_t, xr_b = xr_ring.next()
        gx_t, gx_b = gx_ring.next()
        for c in range(4):
            pq_t, pq_b = proj_tile(h_t, h_b, 416 + c * 128, 128, TB)
            S.op("act", lambda h, c=c, pq_t=pq_t, xr_t=xr_t: h.activation(out=xr_t[:, c, :], in_=pq_t[:, 0:TB], func=AF.Copy),
                 reads=[pq_b], writes=[xr_b])
        for c in range(4):
            pq_t, pq_b = proj_tile(h_t, h_b, 928 + c * 128, 128, TB)
            S.op("act", lambda h, c=c, pq_t=pq_t, gx_t=gx_t: h.activation(out=gx_t[:, c, :], in_=pq_t[:, 0:TB], func=AF.Gelu_apprx_tanh),
                 reads=[pq_b], writes=[gx_b])
        S.dma("pool", XR.rearrange("(c p) t -> p c t", p=128)[:, :, t0:t0 + TB], xr_t[:, :, :], reads=[xr_b])
        S.dma("pool", GX.rearrange("(c p) t -> p c t", p=128)[:, :, t0:t0 + TB], gx_t[:, :, :], reads=[gx_b])
        yield
        cqn_t, cqn_b = cqn_ring.next()
        emit_rmsnorm(cx, cq_t, cq_b, 2, TB, gcq_t, gcq_b, ones_t, ones_b, sq_ring, st_ring, rstd_ring, cqn_t, cqn_b, 256)
        ckvn_t, ckvn_b = ckvn_ring.next()
        emit_rmsnorm(cx, ckv_t, ckv_b, 1, TB, gckv_t, gckv_b, ones_t, ones_b, sq_ring, st_ring, rstd_ring, ckvn_t, ckvn_b, 128)
        yield
        qn_t, qn_b = qn_ring.next()
        for i in range(4):
            pq_t, pq_b = pq_ring.next()
            for k in range(2):
                S.op("pe", lambda h, i=i, k=k, pq_t=pq_t, cqn_t=cqn_t: h.matmul(
                    pq_t[:, 0:TB], lhsT=wuqn[:, k, i * 128:(i + 1) * 128], rhs=cqn_t[:, k, :], start=(k == 0), stop=(k == 1)),
                    reads=[cqn_b, wuq_b], writes=[pq_b])
            S.op("act", lambda h, i=i, pq_t=pq_t, qn_t=qn_t: h.activation(out=qn_t[:, i, :], in_=pq_t[:, 0:TB], func=AF.Copy),
                 reads=[pq_b], writes=[qn_b])
        S.dma("pool", QN.rearrange("(c p) t -> p c t", p=128)[:, :, t0:t0 + TB], qn_t[:, :, :], reads=[qn_b])
        qr_t, qr_b = qr_ring.next()
        for i in range(2):
            pq_t, pq_b = pq_ring.next()
            for k in range(2):
                S.op("pe", lambda h, i=i, k=k, pq_t=pq_t, cqn_t=cqn_t: h.matmul(
                    pq_t[:, 0:TB], lhsT=wuqr[:, k, i * 128:(i + 1) * 128], rhs=cqn_t[:, k, :], start=(k == 0), stop=(k == 1)),
                    reads=[cqn_b, wuq_b], writes=[pq_b])
            emit_rope(cx, pq_t, pq_b, 128, TB, rot_t, rot_b, cos_t, cos_b, sin_t, sin_b, qb_ring, pr_ring, t1_ring, t2_ring,
                      qr_t[:, i, :], qr_b)
        S.dma("pool", QR.rearrange("(c p) t -> p c t", p=128)[:, :, t0:t0 + TB], qr_t[:, :, :], reads=[qr_b])
        yield
        kn_t, kn_b = kn_ring.next()
        for i in range(4):
            pq_t, pq_b = pq_ring.next()
            S.op("pe", lambda h, i=i, pq_t=pq_t, ckvn_t=ckvn_t: h.matmul(
                pq_t[:, 0:TB], lhsT=wk[:, i * 128:(i + 1) * 128], rhs=ckvn_t[:, 0, :], start=True, stop=True),
                reads=[ckvn_b, wkv_b], writes=[pq_b])
            S.op("act", lambda h, i=i, pq_t=pq_t, kn_t=kn_t: h.activation(out=kn_t[:, i, :], in_=pq_t[:, 0:TB], func=AF.Copy),
                 reads=[pq_b], writes=[kn_b])
        S.dma("pool", KNR[0:512, :].rearrange("(c p) t -> p c t", p=128)[:, :, t0:t0 + TB], kn_t[:, :, :], reads=[kn_b])
        yield
        vo_t, vo_b = vo_ring.next()
        for ti in range(TB // 128):
            pv_t, pv_b = pv_ring.next()
            S.op("pe", lambda h, ti=ti, pv_t=pv_t, ckvn_t=ckvn_t: h.matmul(
                pv_t[:, :], lhsT=ckvn_t[:, 0, ti * 128:(ti + 1) * 128], rhs=wv[:, :], start=True, stop=True),
                reads=[ckvn_b, wkv_b], writes=[pv_b])
            S.op("act", lambda h, ti=ti, pv_t=pv_t, vo_t=vo_t: h.activation(
                out=vo_t[:, ti, :].rearrange("p (h e) -> p h e", e=65)[:, :, 0:64],
                in_=pv_t[:, :].rearrange("p (h d) -> p h d", d=64), func=AF.Copy), reads=[pv_b], writes=[vo_b])
        for hd in range(8):
            S.dma("pool", V5[hd * 128:(hd + 1) * 128, :].rearrange("p (i e) -> p i e", e=65)[:, b * 4:(b + 1) * 4, :],
                  vo_t[:, :, hd * 65:(hd + 1) * 65], reads=[vo_b])
        yield

    run_interleaved((blk(b) for b in range(NB)), 2)
    cx.pop()
    if mid_hook is not None:
        mid_hook()

    cx.push()
    xe_ring = mk_ring(cx, "sb", "xe", 2, [128, 4, TB + 4], F32)
    xc_ring = mk_ring(cx, "sb", "xc", 2, [128, 4, TB], F32)
    xcb_ring = mk_ring(cx, "sb", "xcb", 2, [128, 4, TB], BF16)
    r_ring = mk_ring(cx, "sb", "r", 2, [128, 8, TB], F32)
    i_ring = mk_ring(cx, "sb", "i", 2, [128, 8, TB], F32)
    a_ring = mk_ring(cx, "sb", "a", 2, [128, 8, TB], F32)
    b_ring = mk_ring(cx, "sb", "b", 2, [128, 8, TB], F32)
    hl_ring = mk_ring(cx, "sb", "hl", 2, [128, TB], F32)
    sr_ring = mk_ring(cx, "sb", "sr", 2, [128, 8], F32)
    pg_ring = mk_ring(cx, "ps", "pg", 6, [128, 512], F32)
    XRv = XR.rearrange("(c p) t -> p c t", p=128)
    ABv = AB.rearrange("d s (c p) t -> d s p c t", p=128)

    def blk(b):
        t0 = b * TB
        xe_t, xe_b = xe_ring.next()
        lo = 0 if b > 0 else 2
        hi = TB + 3 if b < NB - 1 else TB + 2
        S.dma("sp", xe_t[:, :, lo:hi], XRv[:, :, t0 - 2 + lo:t0 - 2 + hi], writes=[xe_b])
        if b == 0:
            S.op("dve", lambda h, xe_t=xe_t: h.tensor_copy(out=xe_t[:, :, 0:2], in_=xrh_t[:, :, 0:2]), reads=[xrh_b, xe_b], writes=[xe_b])
        if b == NB - 1:
            S.op("dve", lambda h, xe_t=xe_t: h.tensor_copy(out=xe_t[:, :, TB + 2:TB + 3], in_=xrh_t[:, :, 2:3]),
                 reads=[xrh_b, xe_b], writes=[xe_b])
        yield
        xc_t, xc_b = xc_ring.next()
        xcb_t, xcb_b = xcb_ring.next()
        for c in range(4):
            S.op("dve", lambda h, c=c, xc_t=xc_t, xe_t=xe_t: h.tensor_scalar(
                out=xc_t[:, c, :], in0=xe_t[:, c, 0:TB], scalar1=cw_t[:, c, 0:1], scalar2=cb_t[:, c:c + 1],
                op0=ALU.mult, op1=ALU.add), reads=[xe_b, cw_b], writes=[xc_b])
            for j in range(1, 4):
                S.op("dve", lambda h, c=c, j=j, xc_t=xc_t, xe_t=xe_t: h.scalar_tensor_tensor(
                    out=xc_t[:, c, :], in0=xe_t[:, c, j:j + TB], scalar=cw_t[:, c, j:j + 1], in1=xc_t[:, c, :],
                    op0=ALU.mult, op1=ALU.add), reads=[xe_b, cw_b, xc_b], writes=[xc_b])
        S.op("act", lambda h, xc_t=xc_t, xcb_t=xcb_t: h.activation(out=xcb_t[:, :, :], in_=xc_t[:, :, :], func=AF.Copy), reads=[xc_b], writes=[xcb_b])
        yield
        r_t, r_b = r_ring.next()
        i_t, i_b = i_ring.next()
        a_t, a_b = a_ring.next()
        b_t, b_b = b_ring.next()
        sr_t, sr_b = sr_ring.next()
        S.op("dve", lambda h, sr_t=sr_t: h.memset(sr_t[:, :], 0.0), writes=[sr_b])
        for d in range(2):
            for c in range(4):
                q = d * 4 + c
                pg_t, pg_b = pg_ring.next()
                S.op("pe", lambda h, d=d, c=c, pg_t=pg_t, xcb_t=xcb_t: h.matmul(pg_t[:, :], lhsT=wabd[:, d, c, :], rhs=xcb_t[:, c, :],
                                                                         start=True, stop=True), reads=[bd_b, xcb_b], writes=[pg_b])
                S.op("act", lambda h, d=d, c=c, q=q, pg_t=pg_t, r_t=r_t, sr_t=sr_t: h.activation(
                    out=r_t[:, q, :], in_=pg_t[:, :], func=AF.Sigmoid, bias=ba_t[:, d, c:c + 1], accum_out=sr_t[:, q:q + 1]),
                    reads=[pg_b, bb_b], writes=[r_b, sr_b])
                pg_t, pg_b = pg_ring.next()
                S.op("pe", lambda h, d=d, c=c, pg_t=pg_t, xcb_t=xcb_t: h.matmul(pg_t[:, :], lhsT=wxbd[:, d, c, :], rhs=xcb_t[:, c, :],
                                                                         start=True, stop=True), reads=[bd_b, xcb_b], writes=[pg_b])
                S.op("act", lambda h, d=d, c=c, q=q, pg_t=pg_t, i_t=i_t: h.activation(
                    out=i_t[:, q, :], in_=pg_t[:, :], func=AF.Sigmoid, bias=bx_t[:, d, c:c + 1]),
                    reads=[pg_b, bb_b], writes=[i_b])
        yield
        for d in range(2):
            for c in range(4):
                q = d * 4 + c
                S.op("act", lambda h, d=d, c=c, q=q, a_t=a_t, r_t=r_t: h.activation(
                    out=a_t[:, q, :], in_=r_t[:, q, :], func=AF.Exp, scale=cp_t[:, d, c:c + 1]), reads=[r_b, cp_b], writes=[a_b])
                S.op("act", lambda h, d=d, c=c, q=q, sr_t=sr_t, b=b: h.activation(
                    out=blk_t[:, b, d, 0, c:c + 1], in_=sr_t[:, q:q + 1], func=AF.Exp, scale=cp_t[:, d, c:c + 1]),
                    reads=[sr_b, cp_b, blk_b], writes=[blk_b])
        yield
        S.op("dve", lambda h, a_t=a_t, r_t=r_t: h.tensor_tensor(out=r_t[:, :, :], in0=a_t[:, :, :], in1=a_t[:, :, :], op=ALU.mult),
             reads=[a_b, r_b], writes=[r_b])
        S.op("act", lambda h, r_t=r_t: h.activation(out=r_t[:, :, :], in_=r_t[:, :, :], func=AF.Sqrt, scale=-1.0, bias=1.0),
             reads=[r_b], writes=[r_b])
        for d in range(2):
            S.op("dve", lambda h, d=d, i_t=i_t, xc_t=xc_t: h.tensor_tensor(out=i_t[:, d * 4:(d + 1) * 4, :], in0=i_t[:, d * 4:(d + 1) * 4, :],
                                                                        in1=xc_t[:, :, :], op=ALU.mult), reads=[i_b, xc_b], writes=[i_b])
        S.op("dve", lambda h, b_t=b_t, r_t=r_t, i_t=i_t: h.tensor_tensor(out=b_t[:, :, :], in0=r_t[:, :, :], in1=i_t[:, :, :], op=ALU.mult),
             reads=[r_b, i_b], writes=[b_b])
        yield
        for d in range(2):
            for c in range(4):
                q = d * 4 + c
                hl_t, hl_b = hl_ring.next()
                if d == 0:
                    S.op("dve", lambda h, q=q, hl_t=hl_t, a_t=a_t, b_t=b_t: h.tensor_tensor_scan(
                        out=hl_t[:, :], data0=a_t[:, q, :], data1=b_t[:, q, :], initial=0.0, op0=ALU.mult, op1=ALU.add),
                        reads=[a_b, b_b], writes=[hl_b])
                    col = TB - 1
                else:
                    S.op("dve", lambda h, q=q, hl_t=hl_t, a_t=a_t, b_t=b_t: h.tensor_tensor_scan(
                        out=hl_t[:, ::-1], data0=a_t[:, q, ::-1], data1=b_t[:, q, ::-1], initial=0.0, op0=ALU.mult, op1=ALU.add),
                        reads=[a_b, b_b], writes=[hl_b])
                    col = 0
                S.op("act", lambda h, d=d, c=c, hl_t=hl_t, col=col, b=b: h.activation(
                    out=blk_t[:, b, d, 1, c:c + 1], in_=hl_t[:, col:col + 1], func=AF.Copy), reads=[hl_b, blk_b], writes=[blk_b])
        for d in range(2):
            S.dma("act", ABv[d, 0, :, :, t0:t0 + TB], a_t[:, d * 4:(d + 1) * 4, :], reads=[a_b])
            S.dma("sp", ABv[d, 1, :, :, t0:t0 + TB], b_t[:, d * 4:(d + 1) * 4, :], reads=[b_b])
        yield

    run_interleaved((blk(b) for b in range(NB)), 2)
    cab_t = cx.sb("cab", [128, 2, 2, 4], F32); cab_b = Buf("cab")
    for d in range(2):
        S.op("dve", lambda h, d=d: h.memset(cab_t[:, d, 0, :], 1.0), writes=[cab_b])
        S.op("dve", lambda h, d=d: h.memset(cab_t[:, d, 1, :], 0.0), writes=[cab_b])
        order = range(NB) if d == 0 else range(NB - 1, -1, -1)
        for b in order:
            S.op("dve", lambda h, d=d, b=b: h.tensor_tensor(out=cab_t[:, d, 1, :], in0=cab_t[:, d, 1, :], in1=blk_t[:, b, d, 0, :],
                                                            op=ALU.mult), reads=[cab_b, blk_b], writes=[cab_b])
            S.op("dve", lambda h, d=d, b=b: h.tensor_tensor(out=cab_t[:, d, 1, :], in0=cab_t[:, d, 1, :], in1=blk_t[:, b, d, 1, :],
                                                            op=ALU.add), reads=[cab_b, blk_b], writes=[cab_b])
            S.op("dve", lambda h, d=d, b=b: h.tensor_tensor(out=cab_t[:, d, 0, :], in0=cab_t[:, d, 0, :], in1=blk_t[:, b, d, 0, :],
                                                            op=ALU.mult), reads=[cab_b, blk_b], writes=[cab_b])
    S.dma("sp", BLK[:, :, :, :, :], blk_t[:, :, :, :, :], reads=[blk_b])
    S.dma("sp", CAB[:, :, :, :], cab_t[:, :, :, :], reads=[cab_b])
    cx.pop()
    cx.pop()
    return cx.finish() if own else None


def chunk_vec(v, nch):
    return np.ascontiguousarray(np.asarray(v, np.float32).reshape(nch, 128).T)


def oa_inputs(xT, xhalo, P, pos):
    cos, sin = rope_tables(pos, 16, 128)
    return {
        "xT": np.ascontiguousarray(xT), "xhalo": np.ascontiguousarray(xhalo), "w_in": P["w_in"], "g": vec128(P["g"], 8),
        "g_cq": vec128(P["g_cq"], 2), "g_ckv": vec128(P["g_ckv"], 1), "w_uq": P["w_uq"], "w_ukv": P["w_ukv"],
        "cw": np.ascontiguousarray(P["conv_w"].reshape(4, 4, 128).transpose(2, 1, 0)),
        "cb": chunk_vec(P["conv_b"], 4), "wa": P["wa"], "wx": P["wx"],
        "ba": np.ascontiguousarray(P["ba"].reshape(2, 4, 128).transpose(2, 0, 1)),
        "bx": np.ascontiguousarray(P["bx"].reshape(2, 4, 128).transpose(2, 0, 1)),
        "lam": np.ascontiguousarray(P["lam"].reshape(2, 4, 128).transpose(2, 0, 1)),
        "cos": cos, "sin": sin, "rot": rot_matrix(32),
    }


def build_ob1(ntok=TOK, nrank=4, cx=None):
    seq = ntok * nrank
    QG = ntok // 512
    NKT = seq // 128
    NT = ntok // 128
    own = cx is None
    cx = Ctx() if own else cx
    cx.push()
    S = cx.S
    QN = cx.dram_in("QN", [512, ntok], BF16)
    QR = cx.dram_in("QR", [256, ntok], BF16)
    KNg = cx.dram_in("KNg", [8 * nrank * 64, ntok], BF16)
    KRg = cx.dram_in("KRg", [nrank * 32, ntok], BF16)
    Vg = cx.dram_in("Vg", [8 * nrank * 128, NT * 65], BF16)
    YC = cx.dram_out("YC", [512, ntok], BF16)

    q_ring = mk_ring(cx, "sb", "q", 2, [128, ntok], BF16)
    k_ring = mk_ring(cx, "sb", "k", 2, [128, seq], BF16)
    v_ring = mk_ring(cx, "sb", "v", 2, [128, NKT, 65], BF16)
    p_ring = mk_ring(cx, "sb", "p", 4, [128, 512], BF16)
    osb_ring = mk_ring(cx, "sb", "osb", 2, [64, 512], F32)
    rc_ring = mk_ring(cx, "sb", "rc", 2, [128, 512], F32)
    yc_ring = mk_ring(cx, "sb", "yc", 2, [64, 512], BF16)
    ones32 = cx.sb("ones32", [128, 64], F32); ones32_b = Buf("ones32")
    s_ring = mk_ring(cx, "ps", "s", 4, [128, 512], F32)
    o_ring = mk_ring(cx, "ps", "o", 2, [128, 512], F32)
    bc_ring = mk_ring(cx, "ps", "bc", 1, [128, 512], F32)
    S.op("pool", lambda h: h.memset(ones32[:, :], 1.0), writes=[ones32_b])
    scale = float(96 ** -0.5)
    LA = 2

    def load_head(hd):
        q_t, q_b = q_ring.next()
        k_t, k_b = k_ring.next()
        v_t, v_b = v_ring.next()
        S.dma("sp", q_t[0:64, :], QN[hd * 64:(hd + 1) * 64, :], writes=[q_b])
        S.dma("sp", q_t[64:96, :], QR[hd * 32:(hd + 1) * 32, :], writes=[q_b])
        for r in range(nrank):
            S.dma("sp", k_t[0:64, r * ntok:(r + 1) * ntok], KNg[(hd * nrank + r) * 64:(hd * nrank + r + 1) * 64, :], writes=[k_b])
            S.dma("sp", k_t[64:96, r * ntok:(r + 1) * ntok], KRg[r * 32:(r + 1) * 32, :], writes=[k_b])
            S.dma("sp", v_t[:, r * NT:(r + 1) * NT, :],
                  Vg[(hd * nrank + r) * 128:(hd * nrank + r + 1) * 128, :].rearrange("p (i e) -> p i e", e=65), writes=[v_b])
        return (q_t, q_b, k_t, k_b, v_t, v_b)

    nxt = load_head(0)
    cx.run_hook()
    for hd in range(8):
        q_t, q_b, k_t, k_b, v_t, v_b = nxt
        if hd + 1 < 8:
            nxt = load_head(hd + 1)
        for qg in range(QG):
            o_t, o_b = o_ring.next()
            stiles = {}

            def emit_s(kt):
                s_t, s_b = s_ring.next()
                S.op("pe", lambda h: h.matmul(s_t[:, :], lhsT=k_t[0:96, kt * 128:(kt + 1) * 128],
                                              rhs=q_t[0:96, qg * 512:(qg + 1) * 512], start=True, stop=True),
                     reads=[k_b, q_b], writes=[s_b])
                stiles[kt] = (s_t, s_b)

            for kt in range(min(LA, NKT)):
                emit_s(kt)
            for kt in range(NKT):
                s_t, s_b = stiles.pop(kt)
                p_t, p_b = p_ring.next()
                S.op("act", lambda h, s_t=s_t, p_t=p_t: h.activation(out=p_t[:, :], in_=s_t[:, :], func=AF.Exp, scale=scale),
                     reads=[s_b], writes=[p_b])
                if kt + LA < NKT:
                    emit_s(kt + LA)
                S.op("pe", lambda h, kt=kt, p_t=p_t: h.matmul(o_t[0:65, :], lhsT=v_t[:, kt, 0:65], rhs=p_t[:, :],
                                                              start=(kt == 0), stop=(kt == NKT - 1)),
                     reads=[v_b, p_b], writes=[o_b])
            osb_t, osb_b = osb_ring.next()
            rc_t, rc_b = rc_ring.next()
            S.op("act", lambda h: h.activation(out=osb_t[:, :], in_=o_t[0:64, :], func=AF.Copy), reads=[o_b], writes=[osb_b])
            S.op("dve", lambda h: h.reciprocal(out=rc_t[64:65, :], in_=o_t[64:65, :]), reads=[o_b], writes=[rc_b])
            bc_t, bc_b = bc_ring.next()
            S.op("pe", lambda h: h.matmul(bc_t[0:64, :], lhsT=ones32[64:65, 0:64], rhs=rc_t[64:65, :], start=True, stop=True),
                 reads=[rc_b, ones32_b], writes=[bc_b])
            yc_t, yc_b = yc_ring.next()
            S.op("dve", lambda h: h.tensor_tensor(out=yc_t[:, :], in0=osb_t[:, :], in1=bc_t[0:64, :], op=ALU.mult),
                 reads=[osb_b, bc_b], writes=[yc_b])
            S.dma("pool", YC[hd * 64:(hd + 1) * 64, qg * 512:(qg + 1) * 512], yc_t[:, :], reads=[yc_b])
    cx.pop()
    return cx.finish() if own else None


def build_ob2(ntok=TOK, ngrp=4, cx=None):
    TB = 512
    NB = ntok // TB
    own = cx is None
    cx = Ctx() if own else cx
    cx.push()
    S = cx.S
    AB = cx.dram_in("AB", [2, 2, 512, ntok])
    GX = cx.dram_in("GX", [512, ntok], BF16)
    YC = cx.dram_in("YC", [512, ntok], BF16)
    xT = cx.dram_in("xT", [D, ntok])
    w_out = cx.dram_in("w_out", [D, D])
    BLK = cx.dram_in("BLK", [128, NB, 2, 2, 4])
    CABg = cx.dram_in("CABg", [128, ngrp, 16])
    mfd = cx.dram_in("mf", [128, ngrp])
    mbd = cx.dram_in("mb", [128, ngrp])
    oT = cx.dram_out("oT", [D, ntok])

    woA = cx.sb("woA", [128, 4, D], BF16); woA_b = Buf("woA")
    woB = cx.sb("woB", [128, 4, D], BF16); woB_b = Buf("woB")
    blk_t = cx.sb("blk", [128, NB, 2, 2, 4], F32); blk_b = Buf("blk")
    cab_t = cx.sb("cab", [128, ngrp, 16], F32); cab_b = Buf("cab")
    m_t = cx.sb("m", [128, 2, ngrp], F32); m_b = Buf("m")
    hin_t = cx.sb("hin", [128, 2, 4], F32); hin_b = Buf("hin")
    tmp_t = cx.sb("tmp", [128, 4], F32); tmp_b = Buf("tmp")
    init_t = cx.sb("init", [128, NB, 2, 4], F32); init_b = Buf("init")
    ab_ring = mk_ring(cx, "sb", "ab", 2, [128, 2, 2, 4, TB], F32)
    hs_ring = mk_ring(cx, "sb", "hs", 2, [128, 2, 4, TB], F32)
    gx_ring = mk_ring(cx, "sb", "gx", 2, [128, 4, TB], BF16)
    yc_ring = mk_ring(cx, "sb", "yc", 2, [128, 4, TB], BF16)
    yd_ring = mk_ring(cx, "sb", "yd", 2, [128, 4, TB], BF16)
    x_ring = mk_ring(cx, "sb", "x", 2, [128, KC, TB], F32)
    y_ring = mk_ring(cx, "ps", "y", 3, [128, 512], F32)

    S.dma("pool", woA[:, :, :], w_out[0:512, :].rearrange("(i p) n -> p i n", p=128), writes=[woA_b])
    S.dma("pool", woB[:, :, :], w_out[512:1024, :].rearrange("(g p) n -> p g n", p=128), writes=[woB_b])
    S.dma("sp", blk_t[:, :, :, :, :], BLK[:, :, :, :, :], writes=[blk_b])
    S.dma("sp", cab_t[:, :, :], CABg[:, :, :], writes=[cab_b])
    S.dma("sp", m_t[:, 0, :], mfd[:, :], writes=[m_b])
    S.dma("sp", m_t[:, 1, :], mbd[:, :], writes=[m_b])
    S.op("pool", lambda h: h.memset(hin_t[:, :, :], 0.0), writes=[hin_b])
    for d in range(2):
        order = range(ngrp) if d == 0 else range(ngrp - 1, -1, -1)
        for i in order:
            S.op("dve", lambda h, d=d, i=i: h.tensor_tensor(out=tmp_t[:, :], in0=hin_t[:, d, :], in1=cab_t[:, i, d * 8:d * 8 + 4], op=ALU.mult),
                 reads=[hin_b, cab_b, tmp_b], writes=[tmp_b])
            S.op("dve", lambda h, d=d, i=i: h.tensor_tensor(out=tmp_t[:, :], in0=tmp_t[:, :], in1=cab_t[:, i, d * 8 + 4:d * 8 + 8], op=ALU.add),
                 reads=[tmp_b, cab_b], writes=[tmp_b])
            S.op("dve", lambda h, d=d, i=i: h.tensor_tensor(out=tmp_t[:, :], in0=tmp_t[:, :], in1=hin_t[:, d, :], op=ALU.subtract),
                 reads=[tmp_b, hin_b], writes=[tmp_b])
            S.op("dve", lambda h, d=d, i=i: h.scalar_tensor_tensor(out=hin_t[:, d, :], in0=tmp_t[:, :], scalar=m_t[:, d, i:i + 1],
                                                                   in1=hin_t[:, d, :], op0=ALU.mult, op1=ALU.add),
                 reads=[tmp_b, m_b, hin_b], writes=[hin_b])
    for d in range(2):
        order = list(range(NB)) if d == 0 else list(range(NB - 1, -1, -1))
        S.op("dve", lambda h, d=d, b0=order[0]: h.tensor_copy(out=init_t[:, b0, d, :], in_=hin_t[:, d, :]),
             reads=[hin_b, init_b], writes=[init_b])
        for bi in range(NB - 1):
            b, bn = order[bi], order[bi + 1]
            S.op("dve", lambda h, d=d, b=b, bn=bn: h.tensor_tensor(out=init_t[:, bn, d, :], in0=init_t[:, b, d, :],
                                                                   in1=blk_t[:, b, d, 0, :], op=ALU.mult),
                 reads=[init_b, blk_b], writes=[init_b])
            S.op("dve", lambda h, d=d, b=b, bn=bn: h.tensor_tensor(out=init_t[:, bn, d, :], in0=init_t[:, bn, d, :],
                                                                   in1=blk_t[:, b, d, 1, :], op=ALU.add),
                 reads=[init_b, blk_b], writes=[init_b])
    xv = xT.rearrange("(c p) t -> p c t", p=128)
    ov = oT.rearrange("(c p) t -> p c t", p=128)
    ABv = AB.rearrange("d s (c p) t -> d s p c t", p=128)
    def blk(b):
        t0 = b * TB
        ab_t, ab_b = ab_ring.next()
        for d in range(2):
            for s_ in range(2):
                S.dma("sp", ab_t[:, d, s_, :, :], ABv[d, s_, :, :, t0:t0 + TB], writes=[ab_b])
        gx_t, gx_b = gx_ring.next()
        yc_t, yc_b = yc_ring.next()
        x_t, x_b = x_ring.next()
        S.dma("sp", gx_t[:, :, :], GX.rearrange("(c p) t -> p c t", p=128)[:, :, t0:t0 + TB], writes=[gx_b])
        S.dma("sp", yc_t[:, :, :], YC.rearrange("(i p) t -> p i t", p=128)[:, :, t0:t0 + TB], writes=[yc_b])
        S.dma("sp", x_t[:, :, :], xv[:, :, t0:t0 + TB], writes=[x_b])
        yield
        hs_t, hs_b = hs_ring.next()
        for d in range(2):
            for c in range(4):
                if d == 0:
                    S.op("dve", lambda h, d=d, c=c, b=b: h.tensor_tensor_scan(
                        out=hs_t[:, d, c, :], data0=ab_t[:, d, 0, c, :], data1=ab_t[:, d, 1, c, :],
                        initial=init_t[:, b, d, c:c + 1], op0=ALU.mult, op1=ALU.add), reads=[ab_b, init_b, hs_b], writes=[hs_b])
                else:
                    S.op("dve", lambda h, d=d, c=c, b=b: h.tensor_tensor_scan(
                        out=hs_t[:, d, c, ::-1], data0=ab_t[:, d, 0, c, ::-1], data1=ab_t[:, d, 1, c, ::-1],
                        initial=init_t[:, b, d, c:c + 1], op0=ALU.mult, op1=ALU.add), reads=[ab_b, init_b, hs_b], writes=[hs_b])
        yield
        S.op("pool", lambda h: h.tensor_tensor(out=hs_t[:, 0, :, :], in0=hs_t[:, 0, :, :], in1=hs_t[:, 1, :, :], op=ALU.add),
             reads=[hs_b], writes=[hs_b])
        yd_t, yd_b = yd_ring.next()
        S.op("pool", lambda h: h.tensor_tensor(out=yd_t[:, :, :], in0=hs_t[:, 0, :, :], in1=gx_t[:, :, :], op=ALU.mult),
             reads=[hs_b, gx_b], writes=[yd_b])
        yield
        for o in range(KC):
            y_t, y_b = y_ring.next()
            for hh in range(4):
                S.op("pe", lambda h, hh=hh, o=o: h.matmul(y_t[:, :], lhsT=woA[:, hh, o * 128:(o + 1) * 128], rhs=yc_t[:, hh, :],
                                                         start=(hh == 0), stop=False), reads=[woA_b, yc_b], writes=[y_b])
            for g in range(4):
                S.op("pe", lambda h, g=g, o=o: h.matmul(y_t[:, :], lhsT=woB[:, g, o * 128:(o + 1) * 128], rhs=yd_t[:, g, :],
                                                       start=False, stop=(g == 3)), reads=[woB_b, yd_b], writes=[y_b])
            S.op("dve", lambda h, o=o: h.tensor_tensor(out=x_t[:, o, :], in0=y_t[:, :], in1=x_t[:, o, :], op=ALU.add),
                 reads=[y_b, x_b], writes=[x_b])
        S.dma("pool", ov[:, :, t0:t0 + TB], x_t[:, :, :], reads=[x_b])
        yield

    run_interleaved((blk(b) for b in range(NB)), 2, 2)
    cx.pop()
    return cx.finish() if own else None


def allgather(cx, in_ap, out_ap, groups):
    S = cx.S
    S.barrier()
    sem = S.new_sem("cc")
    cx.nc.gpsimd.collective_compute("AllGather", ALU.bypass, replica_groups=groups, ins=[in_ap], outs=[out_ap]).then_inc(sem, 1)
    for e in S.ENGS:
        S.h[e].wait_ge(sem, 1)


def allgather_many(cx, pairs, groups):
    S = cx.S
    S.barrier()
    sem = S.new_sem("ccm")
    for (in_ap, out_ap) in pairs:
        cx.nc.gpsimd.collective_compute("AllGather", ALU.bypass, replica_groups=groups, ins=[in_ap], outs=[out_ap]).then_inc(sem, 1)
    for e in S.ENGS:
        S.h[e].wait_ge(sem, len(pairs))


def emit_select(cx, src_t, src_b, nrank, m_t, m_b, side, acc_t, acc_b):
    S = cx.S
    S.op("dve", lambda h: h.tensor_scalar_mul(out=acc_t[:, :], in0=src_t[:, 0, :], scalar1=m_t[:, side, 0:1]),
         reads=[src_b, m_b], writes=[acc_b])
    for i in range(1, nrank):
        S.op("dve", lambda h, i=i: h.scalar_tensor_tensor(out=acc_t[:, :], in0=src_t[:, i, :], scalar=m_t[:, side, i:i + 1],
                                                          in1=acc_t[:, :], op0=ALU.mult, op1=ALU.add),
             reads=[src_b, m_b, acc_b], writes=[acc_b])


def emit_even_exchange(cx, KTh, Vh, UTh, pack, packg, mlr, groups, nrank, ntok):
    S = cx.S
    cx.push()
    S.dma_dd("sp", pack[:, 0:128], KTh[:, 128:256])
    S.dma_dd("sp", pack[:, 128:256], KTh[:, ntok:ntok + 128])
    S.dma_dd("sp", pack[:, 256:288].rearrange("p (g t) -> p g t", g=4), UTh[:, :, 8:16])
    S.dma_dd("sp", pack[:, 288:320].rearrange("p (g t) -> p g t", g=4), UTh[:, :, ntok:ntok + 8])
    S.dma_dd("sp", pack[:, 320:450], Vh[128:256, :])
    S.dma_dd("sp", pack[:, 450:580], Vh[ntok:ntok + 128, :])
    allgather(cx, pack[:, :], packg[:, :], groups)
    pg_t = cx.sb("pg", [128, nrank, 580], BF16); pg_b = Buf("pg")
    m_t = cx.sb("mlr", [128, 2, nrank], F32); m_b = Buf("mlr")
    accL = cx.sb("accL", [128, 580], BF16); accL_b = Buf("accL")
    accR = cx.sb("accR", [128, 580], BF16); accR_b = Buf("accR")
    S.dma("sp", pg_t[:, :, :], packg.rearrange("(r p) n -> p r n", p=128), writes=[pg_b])
    S.dma("sp", m_t[:, :, :], mlr[:, :, :], writes=[m_b])
    emit_select(cx, pg_t, pg_b, nrank, m_t, m_b, 0, accL, accL_b)
    emit_select(cx, pg_t, pg_b, nrank, m_t, m_b, 1, accR, accR_b)
    S.dma("sp", KTh[:, 0:128], accL[:, 128:256], reads=[accL_b])
    S.dma("sp", UTh[:, :, 0:8], accL[:, 288:320].rearrange("p (g t) -> p g t", g=4), reads=[accL_b])
    S.dma("sp", Vh[0:128, :], accL[:, 450:580], reads=[accL_b])
    S.dma("sp", KTh[:, 128 + ntok:256 + ntok], accR[:, 0:128], reads=[accR_b])
    S.dma("sp", UTh[:, :, 8 + ntok:16 + ntok], accR[:, 256:288].rearrange("p (g t) -> p g t", g=4), reads=[accR_b])
    S.dma("sp", Vh[128 + ntok:256 + ntok, :], accR[:, 320:450], reads=[accR_b])
    cx.pop()


def emit_xhalo_exchange(cx, xprev, xhp, xhpg, xhalo, mlr, groups, nrank, ntok):
    S = cx.S
    cx.push()
    S.dma_dd("sp", xhp[:, 0:2], xprev[:, 0:2])
    S.dma_dd("sp", xhp[:, 2:4], xprev[:, ntok - 2:ntok])
    allgather(cx, xhp[:, :], xhpg[:, :], groups)
    xg_t = cx.sb("xg", [128, nrank, 32], F32); xg_b = Buf("xg")
    m_t = cx.sb("mlr", [128, 2, nrank], F32); m_b = Buf("mlr")
    accL = cx.sb("accL", [128, 32], F32); accL_b = Buf("accL")
    accR = cx.sb("accR", [128, 32], F32); accR_b = Buf("accR")
    for r in range(nrank):
        S.dma("sp", xg_t[:, r, :].rearrange("p (c t) -> p c t", t=4),
              xhpg[r * D:(r + 1) * D, :].rearrange("(c p) t -> p c t", p=128), writes=[xg_b])
    S.dma("sp", m_t[:, :, :], mlr[:, :, :], writes=[m_b])
    emit_select(cx, xg_t, xg_b, nrank, m_t, m_b, 0, accL, accL_b)
    emit_select(cx, xg_t, xg_b, nrank, m_t, m_b, 1, accR, accR_b)
    xhv = xhalo.rearrange("(c p) t -> p c t", p=128)
    S.dma("sp", xhv[:, :, 0:2], accL[:, :].rearrange("p (c t) -> p c t", t=4)[:, :, 2:4], reads=[accL_b])
    S.dma("sp", xhv[:, :, 2:4], accR[:, :].rearrange("p (c t) -> p c t", t=4)[:, :, 0:2], reads=[accR_b])
    cx.pop()


SMALL_SPECS = None


def build_fused(B=2, nrank=4, ntok=TOK, depth=4):
    NE, NO = (depth + 1) // 2, depth // 2
    NT = ntok // 128
    NB = ntok // 512
    groups = [[b * nrank + r for r in range(nrank)] for b in range(B)]
    cx = Ctx()
    nc = cx.nc
    I = cx.ext_in
    x0 = I("xT", [D, ntok])
    Wd = {
        "e_w_in": I("e_w_in", [NE, D, 1280]), "e_w_pool": I("e_w_pool", [NE, 4, 128, 128]), "e_w_out": I("e_w_out", [NE, D, D]),
        "o_w_in": I("o_w_in", [NO, D, 1440]), "o_w_uq": I("o_w_uq", [NO, 256, 768]), "o_w_ukv": I("o_w_ukv", [NO, 128, 1024]),
        "o_lru_wa": I("o_lru_wa", [NO, 2, 8, 64, 64]), "o_lru_wx": I("o_lru_wx", [NO, 2, 8, 64, 64]), "o_w_out": I("o_w_out", [NO, D, D]),
        "w_mlp1": I("w_mlp1", [depth, D, DFF]), "w_mlp2": I("w_mlp2", [depth, DFF, D]),
        "g_mix": I("g_mix", [depth, 128, KC]), "g_mlp": I("g_mlp", [depth, 128, KC]), "g_fin": I("g_fin", [128, KC]),
        "pscale": I("pscale", [NE, 128, 4]), "sinkrow": I("sinkrow", [NE, 1, 2, 512]),
        "g_cq": I("g_cq", [NO, 128, 2]), "g_ckv": I("g_ckv", [NO, 128, 1]), "cw": I("cw", [NO, 128, 4, 4]), "cb": I("cb", [NO, 128, 4]),
        "ba": I("ba", [NO, 128, 2, 4]), "bx": I("bx", [NO, 128, 2, 4]), "lam": I("lam", [NO, 128, 2, 4]),
        "cos32": I("cos32", [128, ntok]), "sin32": I("sin32", [128, ntok]), "cos16": I("cos16", [128, ntok]), "sin16": I("sin16", [128, ntok]),
        "rot64": I("rot64", [128, 128]), "rot32": I("rot32", [128, 128]), "masks": I("masks", [4, 128, 512]),
        "invc": I("invc", [128, 2, 4, 16]), "mfb": I("mfb", [2, 128, nrank]), "mlr": I("mlr", [128, 2, nrank]),
    }
    outT = cx.ext_out("oT", [D, ntok])

    def tmp(name, shape, dt=F32):
        return nc.dram_tensor(name, list(shape), dt, kind="Internal").ap()

    def make_precast(layer, w1b_d, w2b_d):
        def f():
            for k in range(8):
                cx.S.dma_dd_async("pool", w1b_d[k * 128:(k + 1) * 128, :], Wd["w_mlp1"][layer][k * 128:(k + 1) * 128, :])
            for k in range(8):
                cx.S.dma_dd_async("pool", w2b_d[k * 512:(k + 1) * 512, :], Wd["w_mlp2"][layer][k * 512:(k + 1) * 512, :])
        return f

    xcur = x0
    for layer in range(depth):
        L = f"L{layer}"
        xmix = tmp(L + "_xmix", [D, ntok])
        w1b_d = tmp(L + "_w1b", [D, DFF], BF16)
        w2b_d = tmp(L + "_w2b", [DFF, D], BF16)
        if layer % 2 == 0:
            e = layer // 2
            QsT = tmp(L + "_QsT", [128, 4, ntok], BF16)
            KTh = tmp(L + "_KTh", [128, ntok + 256], BF16)
            Vh = tmp(L + "_Vh", [ntok + 256, 130], BF16)
            UTh = tmp(L + "_UTh", [128, 4, ntok + 16], BF16)
            pack = tmp(L + "_pack", [128, 580], BF16)
            packg = tmp(L + "_packg", [nrank * 128, 580], BF16)
            cx.bind = {"xT": xcur, "w_in": Wd["e_w_in"][e], "g": Wd["g_mix"][layer], "cos": Wd["cos32"], "sin": Wd["sin32"],
                       "rot": Wd["rot64"], "QsT": QsT, "KT": KTh[:, 128:128 + ntok], "Vaug": Vh[128:128 + ntok, :],
                       "UT": UTh[:, :, 8:8 + ntok]}
            build_ea(ntok, cx=cx)
            emit_even_exchange(cx, KTh, Vh, UTh, pack, packg, Wd["mlr"], groups, nrank, ntok)
            cx.bind = {"QsT": QsT, "KTh": KTh, "Vh": Vh, "UTh": UTh, "xT": xcur, "w_pool": Wd["e_w_pool"][e],
                       "pscale": Wd["pscale"][e], "w_out": Wd["e_w_out"][e], "sinkrow": Wd["sinkrow"][e], "masks": Wd["masks"],
                       "invc": Wd["invc"], "oT": xmix}
            cx.hook = make_precast(layer, w1b_d, w2b_d)
            build_eb(ntok, cx=cx)
        else:
            o = layer // 2
            xhp = tmp(L + "_xhp", [D, 4]); xhpg = tmp(L + "_xhpg", [nrank * D, 4]); xhalo = tmp(L + "_xhalo", [D, 4])
            QN = tmp(L + "_QN", [512, ntok], BF16); QR = tmp(L + "_QR", [256, ntok], BF16)
            KNR = tmp(L + "_KNR", [544, ntok], BF16); V5 = tmp(L + "_V5", [1024, NT * 65], BF16)
            GX = tmp(L + "_GX", [512, ntok], BF16); AB = tmp(L + "_AB", [2, 2, 512, ntok])
            BLK = tmp(L + "_BLK", [128, NB, 2, 2, 4]); CAB = tmp(L + "_CAB", [128, 16]); XR = tmp(L + "_XR", [512, ntok])
            KNg = tmp(L + "_KNg", [8 * nrank * 64, ntok], BF16); KRg = tmp(L + "_KRg", [nrank * 32, ntok], BF16)
            Vg = tmp(L + "_Vg", [8 * nrank * 128, NT * 65], BF16)
            CABg = tmp(L + "_CABg", [nrank * 128, 16]); YC = tmp(L + "_YC", [512, ntok], BF16)
            emit_xhalo_exchange(cx, xcur, xhp, xhpg, xhalo, Wd["mlr"], groups, nrank, ntok)
            cx.bind = {"xT": xcur, "xhalo": xhalo, "w_in": Wd["o_w_in"][o], "g": Wd["g_mix"][layer], "g_cq": Wd["g_cq"][o],
                       "g_ckv": Wd["g_ckv"][o], "w_uq": Wd["o_w_uq"][o], "w_ukv": Wd["o_w_ukv"][o], "cw": Wd["cw"][o], "cb": Wd["cb"][o],
                       "wa": Wd["o_lru_wa"][o], "wx": Wd["o_lru_wx"][o], "ba": Wd["ba"][o], "bx": Wd["bx"][o], "lam": Wd["lam"][o],
                       "cos": Wd["cos16"], "sin": Wd["sin16"], "rot": Wd["rot32"], "QN": QN, "QR": QR, "KNR": KNR, "V5": V5, "GX": GX,
                       "AB": AB, "BLK": BLK, "CAB": CAB.rearrange("p (d s c) -> p d s c", d=2, s=2), "XR": XR}
            ccsem = cx.S.new_sem("ccg")
            ncc = [0]

            def gather_kv():
                pairs = []
                for hd in range(8):
                    pairs.append((KNR[hd * 64:(hd + 1) * 64, :], KNg[hd * nrank * 64:(hd + 1) * nrank * 64, :]))
                    pairs.append((V5[hd * 128:(hd + 1) * 128, :], Vg[hd * nrank * 128:(hd + 1) * nrank * 128, :]))
                pairs.append((KNR[512:544, :], KRg[:, :]))
                for (i_ap, o_ap) in pairs:
                    nc.gpsimd.collective_compute("AllGather", ALU.bypass, replica_groups=groups, ins=[i_ap], outs=[o_ap]).then_inc(ccsem, 1)
                    ncc[0] += 1

            build_oa(ntok, cx=cx, mid_hook=gather_kv)
            nc.gpsimd.collective_compute("AllGather", ALU.bypass, replica_groups=groups, ins=[CAB[:, :]], outs=[CABg[:, :]]).then_inc(ccsem, 1)
            ncc[0] += 1
            for e_ in cx.S.ENGS:
                cx.S.h[e_].wait_ge(ccsem, ncc[0])
            cx.bind = {"QN": QN, "QR": QR, "KNg": KNg, "KRg": KRg, "Vg": Vg, "YC": YC}
            cx.hook = make_precast(layer, w1b_d, w2b_d)
            build_ob1(ntok, nrank, cx=cx)
            cx.bind = {"AB": AB, "GX": GX, "YC": YC, "xT": xcur, "w_out": Wd["o_w_out"][o], "BLK": BLK,
                       "CABg": CABg.rearrange("(r p) n -> p r n", p=128), "mf": Wd["mfb"][0], "mb": Wd["mfb"][1], "oT": xmix}
            build_ob2(ntok, nrank, cx=cx)
        last = layer == depth - 1
        xnext = outT if last else tmp(L + "_xmlp", [D, ntok])
        cx.bind = {"xT": xmix, "w1": w1b_d, "w2": w2b_d, "g": Wd["g_mlp"][layer], "gf": Wd["g_fin"], "oT": xnext}
        build_mlp(last, ntok, cx=cx, wbf16=True)
        xcur = xnext
    cx.bind = {}
    return cx.finish()


_FUSED = {}


def run_model(x, W, nrank=4, ntok=TOK):
    B, Sq, _ = x.shape
    ncore = B * nrank
    assert Sq == nrank * ntok
    depth = W["norm_mlp"].shape[0]
    NE, NO = (depth + 1) // 2, depth // 2
    key = (B, nrank, ntok, depth)
    if key not in _FUSED:
        _FUSED[key] = build_fused(B, nrank, ntok, depth)
    nc = _FUSED[key]
    f32 = lambda a: np.ascontiguousarray(np.asarray(a, np.float32))
    g_mix = np.stack([vec128(W["e_norm_mix"][l // 2] if l % 2 == 0 else W["o_norm_mix"][l // 2], 8) for l in range(depth)])
    shared = {
        "e_w_in": f32(W["e_w_in"]), "e_w_pool": f32(W["e_w_pool"]), "e_w_out": f32(W["e_w_out"]),
        "o_w_in": f32(W["o_w_in"]), "o_w_uq": f32(W["o_w_uq"]), "o_w_ukv": f32(W["o_w_ukv"]),
        "o_lru_wa": f32(W["o_lru_wa"]), "o_lru_wx": f32(W["o_lru_wx"]), "o_w_out": f32(W["o_w_out"]),
        "w_mlp1": f32(W["w_mlp1"]), "w_mlp2": f32(W["w_mlp2"]),
        "g_mix": g_mix, "g_mlp": np.stack([vec128(W["norm_mlp"][l], 8) for l in range(depth)]), "g_fin": vec128(W["final_norm"], 8),
        "pscale": np.stack([vec128(W["e_pool_scale"][e], 4) for e in range(NE)]),
        "sinkrow": np.stack([np.repeat(f32(W["e_sink"][e]).reshape(2, 4), 128, axis=1).reshape(1, 2, 512) for e in range(NE)]),
        "g_cq": np.stack([vec128(W["o_g_cq"][o], 2) for o in range(NO)]),
        "g_ckv": np.stack([vec128(W["o_g_ckv"][o], 1) for o in range(NO)]),
        "cw": np.stack([f32(f32(W["o_conv_w"][o]).reshape(4, 4, 128).transpose(2, 1, 0)) for o in range(NO)]),
        "cb": np.stack([chunk_vec(W["o_conv_b"][o], 4) for o in range(NO)]),
        "ba": np.stack([f32(f32(W["o_lru_ba"][o]).reshape(2, 4, 128).transpose(2, 0, 1)) for o in range(NO)]),
        "bx": np.stack([f32(f32(W["o_lru_bx"][o]).reshape(2, 4, 128).transpose(2, 0, 1)) for o in range(NO)]),
        "lam": np.stack([f32(f32(W["o_lru_lambda"][o]).reshape(2, 4, 128).transpose(2, 0, 1)) for o in range(NO)]),
        "rot64": rot_matrix(64), "rot32": rot_matrix(32),
    }
    in_maps = []
    for c in range(ncore):
        bi, r = c // nrank, c % nrank
        pos = r * ntok + np.arange(ntok)
        cos32, sin32 = rope_tables(pos, 32, 128)
        cos16, sin16 = rope_tables(pos, 16, 128)
        mfb = np.zeros((2, 128, nrank), np.float32); mfb[0, :, :r] = 1.0; mfb[1, :, r + 1:] = 1.0
        mlr = np.zeros((128, 2, nrank), np.float32)
        if r > 0:
            mlr[:, 0, r - 1] = 1.0
        if r < nrank - 1:
            mlr[:, 1, r + 1] = 1.0
        im = dict(shared)
        im.update({"xT": np.ascontiguousarray(x[bi, r * ntok:(r + 1) * ntok, :].T), "cos32": cos32, "sin32": sin32, "cos16": cos16,
                   "sin16": sin16, "masks": eb_masks(r > 0, r < nrank - 1), "invc": eb_invc(r == 0, r == nrank - 1),
                   "mfb": mfb, "mlr": mlr})
        in_maps.append(im)
    res = run_spmd(nc, in_maps)
    out = np.empty((B, Sq, D), np.float32)
    for c in range(ncore):
        bi, r = c // nrank, c % nrank
        out[bi, r * ntok:(r + 1) * ntok, :] = res[c]["oT"].T
    return out


def kernel(**inputs):
    W = {k: np.asarray(v) for k, v in inputs.items()}
    x = np.asarray(W.pop("x"), np.float32)
    return run_model(x, W)
```

```python
from contextlib import ExitStack
import numpy as np
import concourse.bass as bass
import concourse.mybir as mybir
from concourse.bass_utils import run_bass_kernel_spmd

F32 = mybir.dt.float32
BF16 = mybir.dt.bfloat16
ALU = mybir.AluOpType
AF = mybir.ActivationFunctionType

NCORES = 8
D = 1024
KC = 8
TOK = 4096
SEQ = 16384
EPS = 1e-6
DFF = 4096
EPOCH = 30000


class Buf:
    __slots__ = ("name", "writers", "readers", "sem_in", "sem_out", "n_in", "n_out", "excl")

    def __init__(self, name, excl=False):
        self.name = name
        self.excl = excl
        self.writers = {}
        self.readers = {}
        self.sem_in = None
        self.sem_out = None
        self.n_in = 0
        self.n_out = 0


class Sched:
    ENGS = ("pe", "act", "dve", "pool", "sp")

    def __init__(self, nc, stack):
        self.nc = nc
        self.stack = stack
        self.h = {"pe": nc.tensor, "act": nc.scalar, "dve": nc.vector, "pool": nc.gpsimd, "sp": nc.sync}
        self.ops = {e: [] for e in self.ENGS}
        self.cnt = {e: 0 for e in self.ENGS}
        self.sem = {e: None for e in self.ENGS}
        self.seen = {e: {} for e in self.ENGS}
        self.last = {e: None for e in self.ENGS}
        self.dma_toks = {}
        self.nsem = 0
        self.ninstr = 0
        self.sem_pool = []
        self.live = []
        self.ddbuf = Buf("dram2dram")

    def new_sem(self, name):
        self.nsem += 1
        return self.stack.enter_context(self.nc.semaphore(f"{name}_{self.nsem}"))

    def _eng_tok(self, e):
        if self.sem[e] is None or self.cnt[e] >= EPOCH:
            self.sem[e] = self.new_sem("e" + e)
            self.cnt[e] = 0
        self.cnt[e] += 1
        tok = (self.sem[e], self.cnt[e])
        self.last[e] = tok
        return tok

    def _waits(self, e, toks):
        need = {}
        seen = self.seen[e]
        for sem, val in toks:
            k = id(sem)
            if seen.get(k, 0) >= val:
                continue
            if k not in need or need[k][1] < val:
                need[k] = (sem, val)
        out = []
        for k, (sem, val) in need.items():
            seen[k] = val
            out.append((sem, val))
        return out

    def _deps(self, e, reads, writes):
        toks = []
        for b in reads:
            toks.extend(b.writers.values())
            if b.excl:
                toks.extend(b.readers.values())
        for b in writes:
            toks.extend(b.writers.values())
            toks.extend(b.readers.values())
        if e == "pe":
            own = id(self.sem["pe"]) if self.sem["pe"] is not None else None
            toks = [t for t in toks if id(t[0]) != own]
        return self._waits(e, toks)

    def op(self, e, fn, reads=(), writes=()):
        waits = self._deps(e, reads, writes)
        tok = self._eng_tok(e)
        for b in reads:
            b.readers[id(tok[0])] = tok
        for b in writes:
            b.readers = {}
            b.writers = {id(tok[0]): tok}
        self.ninstr += 1

        h = self.h[e]
        for sem, val in waits:
            h.wait_ge(sem, val)
        fn(h).then_inc(tok[0], 1)

    def dma(self, q, out_ap, in_ap, reads=(), writes=(), **kw):
        waits = self._deps(q, reads, writes)
        assert len(writes) + len(reads) >= 1 and len(writes) <= 1 and len(reads) <= 1
        if writes:
            b = writes[0]
            if b.sem_in is None:
                b.sem_in, b.n_in = self._take_sem("di")
                self.live.append((b, "in"))
            b.n_in += 16
            tok = (b.sem_in, b.n_in)
            b.readers = {}
            b.writers = {id(tok[0]): tok}
            for rb in reads:
                rb.readers[id(tok[0])] = tok
        else:
            b = reads[0]
            if b.sem_out is None:
                b.sem_out, b.n_out = self._take_sem("do")
                self.live.append((b, "out"))
            b.n_out += 16
            tok = (b.sem_out, b.n_out)
            b.readers[id(tok[0])] = tok
        self.dma_toks[id(tok[0])] = tok
        self.ninstr += 1
        h = self.h[q]
        for sem, val in waits:
            h.wait_ge(sem, val)
        h.dma_start(out=out_ap, in_=in_ap, **kw).then_inc(tok[0], 16)

    def _take_sem(self, name):
        if self.sem_pool:
            return self.sem_pool.pop()
        return self.new_sem(name), 0

    def release_dma_sems(self):
        for b, kind in self.live:
            if kind == "in":
                self.sem_pool.append((b.sem_in, b.n_in)); b.sem_in = None
                b.writers = {}
            else:
                self.sem_pool.append((b.sem_out, b.n_out)); b.sem_out = None
                b.readers = {}
        self.live = []

    def dma_dd(self, q, out_ap, in_ap, **kw):
        self.dma(q, out_ap, in_ap, writes=[self.ddbuf], **kw)

    def dma_dd_async(self, q, out_ap, in_ap, **kw):
        self.dma(q, out_ap, in_ap, writes=[Buf("dd_async")], **kw)

    def barrier(self):
        toks = [t for t in self.last.values() if t is not None] + list(self.dma_toks.values())
        for e in self.ENGS:
            waits = self._waits(e, toks)
            for sem, val in waits:
                self.h[e].wait_ge(sem, val)

    def finalize(self):
        self.barrier()


class Ctx:
    def __init__(self):
        self.nc = bass.Bass("TRN2", target_bir_lowering=False)
        self.stack = ExitStack()
        self.S = Sched(self.nc, self.stack)
        self.n = 0
        self.cur = self.stack
        self.scopes = []
        self.bind = {}
        self.hook = None

    def dram_in(self, name, shape, dt=F32):
        if name in self.bind:
            return self.bind[name]
        return self.nc.dram_tensor(name, list(shape), dt, kind="ExternalInput").ap()

    def dram_out(self, name, shape, dt=F32):
        if name in self.bind:
            return self.bind[name]
        return self.nc.dram_tensor(name, list(shape), dt, kind="ExternalOutput").ap()

    def ext_in(self, name, shape, dt=F32):
        return self.nc.dram_tensor(name, list(shape), dt, kind="ExternalInput").ap()

    def ext_out(self, name, shape, dt=F32):
        return self.nc.dram_tensor(name, list(shape), dt, kind="ExternalOutput").ap()

    def sb(self, name, shape, dt):
        self.n += 1
        return self.cur.enter_context(self.nc.sbuf_tensor(f"{name}_{self.n}", list(shape), dt))

    def ps(self, name, shape, dt=F32):
        self.n += 1
        return self.cur.enter_context(self.nc.psum_tensor(f"{name}_{self.n}", list(shape), dt))

    def dram_tmp(self, name, shape, dt=F32):
        return self.nc.dram_tensor(name, list(shape), dt, kind="Internal").ap()

    def run_hook(self):
        if self.hook is not None:
            f, self.hook = self.hook, None
            f()

    def push(self):
        st = ExitStack()
        self.scopes.append(st)
        self.cur = st

    def pop(self):
        self.S.barrier()
        if len(self.scopes) == 1:
            self.S.release_dma_sems()
        self.scopes.pop().close()
        self.cur = self.scopes[-1] if self.scopes else self.stack

    def finish(self):
        self.S.finalize()
        self.stack.close()
        return self.nc


class Ring:
    def __init__(self, items):
        self.items = items
        self.i = 0

    def next(self):
        it = self.items[self.i % len(self.items)]
        self.i += 1
        return it


def run_interleaved(gens, width=2, stagger=2):
    it = iter(gens)
    active = []
    steps = 0
    while True:
        while len(active) < width and (not active or steps >= stagger):
            try:
                active.append(next(it))
            except StopIteration:
                break
        if not active:
            break
        steps += 1
        for g in list(active):
            try:
                next(g)
            except StopIteration:
                active.remove(g)


def mk_ring(cx, kind, name, n, shape, dt):
    items = []
    for i in range(n):
        t = cx.sb(f"{name}{i}", shape, dt) if kind == "sb" else cx.ps(f"{name}{i}", shape, dt)
        items.append((t, Buf(f"{name}{i}", excl=(kind == "ps"))))
    return Ring(items)


def emit_rmsnorm(cx, x_t, x_b, nchunk, TB, g_t, g_b, ones_t, ones_b, sq_ring, st_ring, rstd_ring,
                 out_t, out_b, nfeat, evac_engs=("dve",)):
    S = cx.S
    st_t, st_b = st_ring.next()
    for c in range(nchunk):
        sq_t, sq_b = sq_ring.next()
        S.op("act", lambda h, c=c, sq_t=sq_t: h.activation(out=sq_t[:, 0:TB], in_=x_t[:, c, 0:TB], func=AF.Square),
             reads=[x_b], writes=[sq_b])
        S.op("pe", lambda h, c=c, sq_t=sq_t: h.matmul(st_t[:, 0:TB], lhsT=ones_t[:, :], rhs=sq_t[:, 0:TB],
                                                        start=(c == 0), stop=(c == nchunk - 1)),
             reads=[sq_b, ones_b], writes=[st_b])
    r_t, r_b = rstd_ring.next()
    S.op("act", lambda h: h.activation(out=r_t[:, 0:TB], in_=st_t[:, 0:TB], func=AF.Sqrt, bias=float(nfeat * EPS)),
         reads=[st_b], writes=[r_b])
    S.op("dve", lambda h: h.reciprocal(out=r_t[:, 0:TB], in_=r_t[:, 0:TB]), reads=[r_b], writes=[r_b])
    for c in range(nchunk):
        e = evac_engs[c % len(evac_engs)]
        S.op(e, lambda h, c=c: h.scalar_tensor_tensor(out=out_t[:, c, 0:TB], in0=x_t[:, c, 0:TB],
                                                       scalar=g_t[:, c:c + 1], in1=r_t[:, 0:TB],
                                                       op0=ALU.mult, op1=ALU.mult),
             reads=[x_b, r_b, g_b], writes=[out_b])


def build_mlp(final_norm, ntok=TOK, dbg=False, cx=None, wbf16=False):
    TB = 256
    NB = ntok // TB
    FC = DFF // 128
    own = cx is None
    cx = Ctx() if own else cx
    cx.push()
    S = cx.S
    xT = cx.dram_in("xT", [D, ntok])
    w1 = cx.dram_in("w1", [D, DFF], BF16 if wbf16 else F32)
    w2 = cx.dram_in("w2", [DFF, D], BF16 if wbf16 else F32)
    gin = cx.dram_in("g", [128, KC])
    oT = cx.dram_out("oT", [D, ntok])
    if final_norm:
        gfin = cx.dram_in("gf", [128, KC])
    if dbg:
        dh = cx.dram_out("dh", [128, KC, TB], BF16)
        da = cx.dram_out("da", [128, DFF // 128, TB], BF16)

    w1b = cx.sb("w1b", [128, KC, DFF], BF16)
    w2b = cx.sb("w2b", [128, FC, D], BF16)
    w1_bufs = [Buf(f"w1_{k}") for k in range(KC)]
    w2_bufs = [Buf(f"w2_{k}") for k in range(8)]
    g_t = cx.sb("g", [128, KC], F32); g_b = Buf("g")
    ones_t = cx.sb("ones", [128, 128], BF16); ones_b = Buf("ones")
    x_ring = mk_ring(cx, "sb", "x", 2, [128, KC, TB], F32)
    h_ring = mk_ring(cx, "sb", "h", 2, [128, KC, TB], BF16)
    a_ring = mk_ring(cx, "sb", "a", 1, [128, FC, TB], BF16)
    r_ring = mk_ring(cx, "sb", "r", 3, [128, TB], BF16)
    sq_ring = mk_ring(cx, "sb", "sq", 3, [128, TB], BF16)
    rstd_ring = mk_ring(cx, "sb", "rstd", 2, [128, TB], F32)
    o_ring = mk_ring(cx, "sb", "o", 2, [128, KC, TB], F32)
    st_ring = mk_ring(cx, "ps", "st", 1, [128, 512], F32)
    p1_ring = mk_ring(cx, "ps", "p1", 3, [128, 512], F32)
    p2_ring = mk_ring(cx, "ps", "p2", 3, [128, 512], F32)
    if final_norm:
        gf_t = cx.sb("gf", [128, KC], F32); gf_b = Buf("gf")
        f_ring = mk_ring(cx, "sb", "f", 2, [128, KC, TB], F32)

    S.dma("sp", g_t[:, :], gin[:, :], writes=[g_b])
    S.op("dve", lambda h: h.tensor_scalar_mul(out=g_t[:, :], in0=g_t[:, :], scalar1=float(np.sqrt(D))),
         reads=[g_b], writes=[g_b])
    if final_norm:
        S.dma("sp", gf_t[:, :], gfin[:, :], writes=[gf_b])
        S.op("dve", lambda h: h.tensor_scalar_mul(out=gf_t[:, :], in0=gf_t[:, :], scalar1=float(np.sqrt(D))),
             reads=[gf_b], writes=[gf_b])
    S.op("pool", lambda h: h.memset(ones_t[:, :], 1.0), writes=[ones_b])
    w1v = w1.rearrange("(k p) n -> p k n", p=128)
    w2v = w2.rearrange("(f p) n -> p f n", p=128)
    wq = "sp" if wbf16 else "pool"
    for k in range(KC):
        S.dma(wq, w1b[:, k, :], w1v[:, k, :], writes=[w1_bufs[k]])
    for j in range(8):
        S.dma(wq, w2b[:, j * 4:(j + 1) * 4, :], w2v[:, j * 4:(j + 1) * 4, :], writes=[w2_bufs[j]])
    xv = xT.rearrange("(c p) t -> p c t", p=128)
    ov = oT.rearrange("(c p) t -> p c t", p=128)

    def prep(b):
        x_t, x_b = x_ring.next()
        S.dma("sp", x_t[:, :, :], xv[:, :, b * TB:(b + 1) * TB], writes=[x_b])
        h_t, h_b = h_ring.next()
        return (x_t, x_b, h_t, h_b)

    def norm(st_):
        x_t, x_b, h_t, h_b = st_
        emit_rmsnorm(cx, x_t, x_b, KC, TB, g_t, g_b, ones_t, ones_b, sq_ring, st_ring, rstd_ring, h_t, h_b, D)

    cur = prep(0)
    norm(cur)
    for b in range(NB):
        t0 = b * TB
        x_t, x_b, h_t, h_b = cur
        nxt = prep(b + 1) if b + 1 < NB else None
        a_t, a_b = a_ring.next()
        for f in range(FC):
            if f == FC // 2 and nxt is not None:
                norm(nxt)
            p_t, p_b = p1_ring.next()
            for k in range(KC):
                S.op("pe", lambda h, f=f, k=k, p_t=p_t: h.matmul(p_t[:, 0:TB], lhsT=w1b[:, k, f * 128:(f + 1) * 128],
                                                                  rhs=h_t[:, k, 0:TB], start=(k == 0), stop=(k == KC - 1)),
                     reads=[h_b, w1_bufs[k]], writes=[p_b])
            r_t, r_b = r_ring.next()
            S.op("act", lambda h, p_t=p_t, r_t=r_t: h.activation(out=r_t[:, 0:TB], in_=p_t[:, 0:TB], func=AF.Relu),
                 reads=[p_b], writes=[r_b])
            S.op("pool", lambda h, f=f, r_t=r_t: h.tensor_tensor(out=a_t[:, f, 0:TB], in0=r_t[:, 0:TB], in1=r_t[:, 0:TB],
                                                                  op=ALU.mult),
                 reads=[r_b], writes=[a_b])
        if dbg and b == 0:
            S.dma("sp", dh[:, :, :], h_t[:, :, :], reads=[h_b])
            S.dma("sp", da[:, :, :], a_t[:, :, :], reads=[a_b])
        o_t, o_b = o_ring.next()
        for c in range(KC):
            p_t, p_b = p2_ring.next()
            for f in range(FC):
                S.op("pe", lambda h, f=f, c=c, p_t=p_t: h.matmul(p_t[:, 0:TB], lhsT=w2b[:, f, c * 128:(c + 1) * 128],
                                                                  rhs=a_t[:, f, 0:TB], start=(f == 0), stop=(f == FC - 1)),
                     reads=[a_b, w2_bufs[f // 4]], writes=[p_b])
            S.op("dve", lambda h, c=c, p_t=p_t: h.tensor_tensor(out=o_t[:, c, 0:TB], in0=p_t[:, 0:TB], in1=x_t[:, c, 0:TB],
                                                                 op=ALU.add),
                 reads=[p_b, x_b], writes=[o_b])
        if final_norm:
            f_t, f_b = f_ring.next()
            emit_rmsnorm(cx, o_t, o_b, KC, TB, gf_t, gf_b, ones_t, ones_b, sq_ring, st_ring, rstd_ring, f_t, f_b, D)
            S.dma("pool", ov[:, :, t0:t0 + TB], f_t[:, :, :], reads=[f_b])
        else:
            S.dma("pool", ov[:, :, t0:t0 + TB], o_t[:, :, :], reads=[o_b])
        cur = nxt
    cx.pop()
    return cx.finish() if own else None


def run_spmd(nc, in_maps):
    res = run_bass_kernel_spmd(nc, in_maps, core_ids=list(range(len(in_maps))))
    return res.results


def vec128(v, k):
    return np.ascontiguousarray(np.asarray(v, np.float32).reshape(k, 128).T)


def load_cast(cx, q, dst_ap, src_ap, buf):
    cx.S.dma(q, dst_ap, src_ap, writes=[buf])


def build_ea(ntok=TOK, parts='quv', qlvl=4, cx=None):
    TB = 512
    NB = ntok // TB
    own = cx is None
    cx = Ctx() if own else cx
    cx.push()
    S = cx.S
    xT = cx.dram_in("xT", [D, ntok])
    w_in = cx.dram_in("w_in", [D, 1280])
    gin = cx.dram_in("g", [128, KC])
    cosd = cx.dram_in("cos", [128, ntok])
    sind = cx.dram_in("sin", [128, ntok])
    rotd = cx.dram_in("rot", [128, 128])
    QsT = cx.dram_out("QsT", [128, 4, ntok], BF16)
    KT = cx.dram_out("KT", [128, ntok], BF16)
    Vaug = cx.dram_out("Vaug", [ntok, 130], BF16)
    UT = cx.dram_out("UT", [128, 4, ntok], BF16)

    wb = cx.sb("wb", [128, KC, 1280], BF16)
    w_bufs = [Buf(f"w{k}") for k in range(KC)]
    g_t = cx.sb("g", [128, KC], F32); g_b = Buf("g")
    ones_t = cx.sb("ones", [128, 128], BF16); ones_b = Buf("ones")
    rot_t = cx.sb("rot", [128, 128], BF16); rot_b = Buf("rot")
    x_ring = mk_ring(cx, "sb", "x", 2, [128, KC, TB], F32)
    h_ring = mk_ring(cx, "sb", "h", 2, [128, KC, TB], BF16)
    sq_ring = mk_ring(cx, "sb", "sq", 3, [128, TB], BF16)
    rstd_ring = mk_ring(cx, "sb", "rstd", 2, [128, TB], F32)
    cos_ring = mk_ring(cx, "sb", "cos", 2, [128, TB], F32)
    sin_ring = mk_ring(cx, "sb", "sin", 2, [128, TB], F32)
    qb_ring = mk_ring(cx, "sb", "qb", 2, [128, TB], BF16)
    t1_ring = mk_ring(cx, "sb", "t1", 2, [128, TB], F32)
    t2_ring = mk_ring(cx, "sb", "t2", 2, [128, TB], F32)
    qo_ring = mk_ring(cx, "sb", "qo", 2, [128, 5, TB], BF16)
    uo_ring = mk_ring(cx, "sb", "uo", 2, [128, 4, TB], BF16)
    vo_ring = mk_ring(cx, "sb", "vo", 2, [128, 4, 130], BF16)
    st_ring = mk_ring(cx, "ps", "st", 1, [128, 512], F32)
    pq_ring = mk_ring(cx, "ps", "pq", 3, [128, 512], F32)
    pr_ring = mk_ring(cx, "ps", "pr", 2, [128, 512], F32)
    pv_ring = mk_ring(cx, "ps", "pv", 2, [128, 512], F32)

    S.dma("sp", g_t[:, :], gin[:, :], writes=[g_b])
    S.op("dve", lambda h: h.tensor_scalar_mul(out=g_t[:, :], in0=g_t[:, :], scalar1=float(np.sqrt(D))),
         reads=[g_b], writes=[g_b])
    S.op("pool", lambda h: h.memset(ones_t[:, :], 1.0), writes=[ones_b])
    S.dma("pool", rot_t[:, :], rotd[:, :], writes=[rot_b])
    for (vt, vb) in vo_ring.items:
        S.op("pool", lambda h, vt=vt: h.memset(vt[:, :, :], 1.0), writes=[vb])
    for k in range(KC):
        for j in range(2):
            src = w_in[k * 128:(k + 1) * 128, j * 256:(j + 1) * 256].rearrange("p (c d) -> p c d", c=4, d=64)
            dst = wb[:, k, 0:512].rearrange("p (c j d) -> p c j d", c=4, j=2, d=64)[:, :, j, :]
            S.dma("pool", dst, src, writes=[w_bufs[k]])
        S.dma("pool", wb[:, k, 512:1280], w_in[k * 128:(k + 1) * 128, 512:1280], writes=[w_bufs[k]])
    xv = xT.rearrange("(c p) t -> p c t", p=128)

    def blk(b):
        t0 = b * TB
        x_t, x_b = x_ring.next()
        S.dma("sp", x_t[:, :, :], xv[:, :, t0:t0 + TB], writes=[x_b])
        cos_t, cos_b = cos_ring.next()
        sin_t, sin_b = sin_ring.next()
        S.dma("sp", cos_t[:, :], cosd[:, t0:t0 + TB], writes=[cos_b])
        S.dma("sp", sin_t[:, :], sind[:, t0:t0 + TB], writes=[sin_b])
        h_t, h_b = h_ring.next()
        emit_rmsnorm(cx, x_t, x_b, KC, TB, g_t, g_b, ones_t, ones_b, sq_ring, st_ring, rstd_ring, h_t, h_b, D)
        yield
        qo_t, qo_b = qo_ring.next()
        for c in (range(5) if 'q' in parts else []):
            pq_t, pq_b = pq_ring.next()
            for k in range(KC):
                S.op("pe", lambda h, c=c, k=k, pq_t=pq_t, h_t=h_t: h.matmul(
                    pq_t[:, 0:TB], lhsT=wb[:, k, c * 128:(c + 1) * 128], rhs=h_t[:, k, :],
                    start=(k == 0), stop=(k == KC - 1)), reads=[h_b, w_bufs[k]], writes=[pq_b])
            qb_t, qb_b = qb_ring.next()
            S.op("act", lambda h, pq_t=pq_t, qb_t=qb_t: h.activation(out=qb_t[:, :], in_=pq_t[:, 0:TB], func=AF.Copy),
                 reads=[pq_b], writes=[qb_b])
            if qlvl == 1:
                S.op("act", lambda h, c=c, pq_t=pq_t, qo_t=qo_t: h.activation(out=qo_t[:, c, :], in_=pq_t[:, 0:TB], func=AF.Copy),
                     reads=[pq_b], writes=[qo_b])
                continue
            pr_t, pr_b = pr_ring.next()
            S.op("pe", lambda h, pr_t=pr_t, qb_t=qb_t: h.matmul(pr_t[:, 0:TB], lhsT=rot_t[:, :], rhs=qb_t[:, :],
                                                               start=True, stop=True),
                 reads=[qb_b, rot_b], writes=[pr_b])
            t1_t, t1_b = t1_ring.next()
            t2_t, t2_b = t2_ring.next()
            if qlvl == 2:
                S.op("act", lambda h, c=c, pr_t=pr_t, qo_t=qo_t: h.activation(out=qo_t[:, c, :], in_=pr_t[:, 0:TB], func=AF.Copy),
                     reads=[pr_b], writes=[qo_b])
                continue
            S.op("dve", lambda h, t1_t=t1_t, pq_t=pq_t, cos_t=cos_t: h.tensor_tensor(
                out=t1_t[:, :], in0=pq_t[:, 0:TB], in1=cos_t[:, :], op=ALU.mult), reads=[pq_b, cos_b], writes=[t1_b])
            if qlvl == 3:
                S.op("act", lambda h, c=c, t1_t=t1_t, qo_t=qo_t: h.activation(out=qo_t[:, c, :], in_=t1_t[:, :], func=AF.Copy),
                     reads=[t1_b], writes=[qo_b])
                continue
            S.op("dve", lambda h, t2_t=t2_t, pr_t=pr_t, sin_t=sin_t: h.tensor_tensor(
                out=t2_t[:, :], in0=pr_t[:, 0:TB], in1=sin_t[:, :], op=ALU.mult), reads=[pr_b, sin_b], writes=[t2_b])
            S.op("dve", lambda h, c=c, qo_t=qo_t, t1_t=t1_t, t2_t=t2_t: h.tensor_tensor(
                out=qo_t[:, c, :], in0=t1_t[:, :], in1=t2_t[:, :], op=ALU.add), reads=[t1_b, t2_b], writes=[qo_b])
        if 'q' in parts:
            S.dma("pool", QsT[:, :, t0:t0 + TB], qo_t[:, 0:4, :], reads=[qo_b])
            S.dma("pool", KT[:, t0:t0 + TB], qo_t[:, 4, :], reads=[qo_b])
        yield
        uo_t, uo_b = uo_ring.next()
        for gi in (range(4) if 'u' in parts else []):
            pq_t, pq_b = pq_ring.next()
            for k in range(KC):
                S.op("pe", lambda h, gi=gi, k=k, pq_t=pq_t, h_t=h_t: h.matmul(
                    pq_t[:, 0:TB], lhsT=wb[:, k, 768 + gi * 128:768 + (gi + 1) * 128], rhs=h_t[:, k, :],
                    start=(k == 0), stop=(k == KC - 1)), reads=[h_b, w_bufs[k]], writes=[pq_b])
            S.op("act", lambda h, gi=gi, pq_t=pq_t, uo_t=uo_t: h.activation(out=uo_t[:, gi, :], in_=pq_t[:, 0:TB], func=AF.Copy),
                 reads=[pq_b], writes=[uo_b])
        if 'u' in parts:
            S.dma("pool", UT[:, :, t0:t0 + TB], uo_t[:, :, :], reads=[uo_b])
        if 'v' not in parts:
            return
        yield
        vo_t, vo_b = vo_ring.next()
        pv_t, pv_b = pv_ring.next()
        for ti in range(TB // 128):
            for k in range(KC):
                S.op("pe", lambda h, ti=ti, k=k, pv_t=pv_t, h_t=h_t: h.matmul(
                    pv_t[:, ti * 128:(ti + 1) * 128], lhsT=h_t[:, k, ti * 128:(ti + 1) * 128], rhs=wb[:, k, 640:768],
                    start=(k == 0), stop=(k == KC - 1)), reads=[h_b, w_bufs[k]], writes=[pv_b])
        for ti in range(TB // 128):
            for j in range(2):
                S.op("act", lambda h, ti=ti, j=j, pv_t=pv_t, vo_t=vo_t: h.activation(
                    out=vo_t[:, ti, j * 65:j * 65 + 64], in_=pv_t[:, ti * 128 + j * 64:ti * 128 + (j + 1) * 64], func=AF.Copy),
                    reads=[pv_b], writes=[vo_b])
        S.dma("pool", Vaug[t0:t0 + TB, :].rearrange("(i p) n -> p i n", p=128), vo_t[:, :, :], reads=[vo_b])
        yield

    run_interleaved((blk(b) for b in range(NB)), 2, 2)
    cx.pop()
    return cx.finish() if own else None


def rope_tables(pos, half, nrows):
    inv = (np.float32(10000.0) ** (-np.arange(half, dtype=np.float32) / np.float32(half))).astype(np.float32)
    ang = pos.astype(np.float32)[None, :] * inv[np.arange(nrows) % half][:, None]
    return np.cos(ang).astype(np.float32), np.sin(ang).astype(np.float32)


def rot_matrix(dh, nrows=128):
    R = np.zeros((nrows, nrows), np.float32)
    half = dh // 2
    for m in range(nrows):
        d = m % dh
        base = m - d
        if d < half:
            R[base + d + half, m] = -1.0
        else:
            R[base + d - half, m] = 1.0
    return R


def build_eb(ntok=TOK, cx=None):
    TB = 512
    NB = ntok // TB
    NT = ntok // 128
    own = cx is None
    cx = Ctx() if own else cx
    cx.push()
    S = cx.S
    QsT = cx.dram_in("QsT", [128, 4, ntok], BF16)
    KTh = cx.dram_in("KTh", [128, ntok + 256], BF16)
    Vh = cx.dram_in("Vh", [ntok + 256, 130], BF16)
    UTh = cx.dram_in("UTh", [128, 4, ntok + 16], BF16)
    xT = cx.dram_in("xT", [D, ntok])
    w_pool = cx.dram_in("w_pool", [4, 128, 128])
    pscale = cx.dram_in("pscale", [128, 4])
    w_out = cx.dram_in("w_out", [D, D])
    sinkrow = cx.dram_in("sinkrow", [1, 2, 512])
    masksd = cx.dram_in("masks", [4, 128, 512])
    invcd = cx.dram_in("invc", [128, 2, 4, 16])
    oT = cx.dram_out("oT", [D, ntok])

    woA = cx.sb("woA", [128, 4, D], BF16); woA_b = Buf("woA")
    woB = cx.sb("woB", [128, 4, D], BF16); woB_b = Buf("woB")
    wp = cx.sb("wp", [128, 4, 128], BF16); wp_b = Buf("wp")
    ps_t = cx.sb("ps", [128, 4], F32); ps_b = Buf("ps")
    mk_t = cx.sb("mk", [128, 4, 512], BF16); mk_b = Buf("mk")
    invc_t = cx.sb("invc", [128, 2, 4, 16], F32); invc_b = Buf("invc")
    sk_t = cx.sb("sk", [1, 2, 512], F32); sk_b = Buf("sk")
    esk_t = cx.sb("esk", [1, 2, 512], BF16); esk_b = Buf("esk")
    sel_t = cx.sb("sel", [1, 128], BF16); sel_b = Buf("sel")
    ones32 = cx.sb("ones32", [128, 64], F32); ones32_b = Buf("ones32")
    qsA_ring = mk_ring(cx, "sb", "qsA", 2, [128, 4, TB], BF16)
    qsB_ring = mk_ring(cx, "sb", "qsB", 2, [128, 4, TB], BF16)
    kt_ring = mk_ring(cx, "sb", "kt", 2, [128, 6 * 128], BF16)
    v_ring = mk_ring(cx, "sb", "v", 2, [128, 6, 130], BF16)
    u_ring = mk_ring(cx, "sb", "u", 2, [128, 4, TB + 16], BF16)
    x_ring = mk_ring(cx, "sb", "x", 2, [128, KC, TB], F32)
    p_ring = mk_ring(cx, "sb", "p", 4, [128, 512], BF16)
    osb_ring = mk_ring(cx, "sb", "osb", 3, [64, 512], F32)
    rc_ring = mk_ring(cx, "sb", "rc", 3, [128, 512], F32)
    ya_ring = mk_ring(cx, "sb", "ya", 2, [64, 8, TB], BF16)
    yp_ring = mk_ring(cx, "sb", "yp", 2, [128, 4, TB], BF16)
    yb_ring = mk_ring(cx, "sb", "yb", 2, [128, 4, TB], BF16)
    d_ring = mk_ring(cx, "sb", "d", 2, [128, 4, TB], BF16)
    tmp_rings = [mk_ring(cx, "sb", f"tp{g}", 2, [128, TB + 16], F32) for g in range(4)]
    e16_ring = mk_ring(cx, "sb", "e16", 2, [128, 16], F32)
    s_ring = mk_ring(cx, "ps", "s", 3, [128, 512], F32)
    o_ring = mk_ring(cx, "ps", "o", 2, [128, 512], F32)
    bc_ring = mk_ring(cx, "ps", "bc", 1, [128, 512], F32)
    y_ring = mk_ring(cx, "ps", "y", 2, [128, 512], F32)

    S.dma("pool", woA[:, :, :], w_out[0:512, :].rearrange("(i p) n -> p i n", p=128), writes=[woA_b])
    S.dma("pool", woB[:, :, :], w_out[512:1024, :].rearrange("(g p) n -> p g n", p=128), writes=[woB_b])
    S.dma("pool", wp[:, :, :], w_pool.rearrange("g i j -> i g j"), writes=[wp_b])
    S.dma("pool", mk_t[:, :, :], masksd.rearrange("m p n -> p m n"), writes=[mk_b])
    S.dma("sp", ps_t[:, :], pscale[:, :], writes=[ps_b])
    S.dma("sp", invc_t[:, :, :, :], invcd[:, :, :, :], writes=[invc_b])
    S.dma("sp", sk_t[:, :, :], sinkrow[:, :, :], writes=[sk_b])
    S.op("act", lambda h: h.activation(out=esk_t[:, :, :], in_=sk_t[:, :, :], func=AF.Exp), reads=[sk_b], writes=[esk_b])
    S.op("pool", lambda h: h.memset(sel_t[:, :], 0.0), writes=[sel_b])
    S.op("pool", lambda h: h.memset(sel_t[:, 64:65], 1.0), writes=[sel_b])
    S.op("pool", lambda h: h.memset(ones32[:, :], 1.0), writes=[ones32_b])
    for (qt, qb_) in qsA_ring.items:
        S.op("pool", lambda h, qt=qt: h.memset(qt[64:128, :, :], 0.0), writes=[qb_])
    for (qt, qb_) in qsB_ring.items:
        S.op("pool", lambda h, qt=qt: h.memset(qt[0:64, :, :], 0.0), writes=[qb_])
    xv = xT.rearrange("(c p) t -> p c t", p=128)
    ov = oT.rearrange("(c p) t -> p c t", p=128)
    cx.run_hook()

    def blk(b):
        t0 = b * TB
        qsA_t, qsA_b = qsA_ring.next()
        qsB_t, qsB_b = qsB_ring.next()
        kt_t, kt_b = kt_ring.next()
        v_t, v_b = v_ring.next()
        u_t, u_b = u_ring.next()
        x_t, x_b = x_ring.next()
        S.dma("sp", qsA_t[0:64, :, :], QsT[0:64, :, t0:t0 + TB], writes=[qsA_b])
        S.dma("sp", qsB_t[64:128, :, :], QsT[64:128, :, t0:t0 + TB], writes=[qsB_b])
        S.dma("sp", kt_t[:, :], KTh[:, t0:t0 + 768], writes=[kt_b])
        S.dma("sp", v_t[:, :, :], Vh[t0:t0 + 768, :].rearrange("(i p) n -> p i n", p=128), writes=[v_b])
        S.dma("sp", u_t[:, :, :], UTh[:, :, t0:t0 + TB + 16], writes=[u_b])
        S.dma("sp", x_t[:, :, :], xv[:, :, t0:t0 + TB], writes=[x_b])
        yield
        ya_t, ya_b = ya_ring.next()
        tiles = [(nl, j, mi, dm) for nl in range(4) for mi, dm in enumerate((-1, 0, 1)) for j in range(2)]
        LA = 2
        st = {}
        unit_o = {}
        deferred = []

        def emit_S(t):
            nl, j, mi, dm = tiles[t]
            i = nl + dm + 1
            s_t, s_b = s_ring.next()
            q_t, q_b = (qsA_t, qsA_b) if j == 0 else (qsB_t, qsB_b)
            S.op("pe", lambda h: h.matmul(s_t[:, :], lhsT=kt_t[:, i * 128:(i + 1) * 128],
                                          rhs=q_t[:, :, nl * 128:(nl + 1) * 128], start=True, stop=True),
                 reads=[kt_b, q_b], writes=[s_b])
            st[t] = (s_t, s_b)

        def flush_deferred():
            while deferred:
                (o_t, o_b, osb_t, osb_b, rc_t, rc_b, nl, j) = deferred.pop(0)
                bc_t, bc_b = bc_ring.next()
                S.op("pe", lambda h: h.matmul(bc_t[0:64, :], lhsT=ones32[64:65, 0:64], rhs=rc_t[64:65, :], start=True, stop=True),
                     reads=[rc_b, ones32_b], writes=[bc_b])
                S.op("dve", lambda h: h.tensor_tensor(
                    out=ya_t[0:64, j * 4:(j + 1) * 4, nl * 128:(nl + 1) * 128],
                    in0=osb_t[:, :].rearrange("p (c q) -> p c q", c=4),
                    in1=bc_t[0:64, :].rearrange("p (c q) -> p c q", c=4), op=ALU.mult),
                    reads=[osb_b, bc_b], writes=[ya_b])

        for t in range(min(LA, len(tiles))):
            emit_S(t)
        for t in range(len(tiles)):
            nl, j, mi, dm = tiles[t]
            n = 4 * b + nl
            i = nl + dm + 1
            if mi == 0:
                unit_o[(nl, j)] = o_ring.next()
            o_t, o_b = unit_o[(nl, j)]
            s_t, s_b = st.pop(t)
            p_t, p_b = p_ring.next()
            S.op("act", lambda h: h.activation(out=p_t[:, :], in_=s_t[:, :], func=AF.Exp, scale=0.125), reads=[s_b], writes=[p_b])
            if dm != 0:
                if dm == -1:
                    mi_ = 2 if n == 0 else 0
                else:
                    mi_ = 3 if n == NT - 1 else 1
                S.op("pool", lambda h: h.tensor_tensor(out=p_t[:, :], in0=p_t[:, :], in1=mk_t[:, mi_, :], op=ALU.mult),
                     reads=[p_b, mk_b], writes=[p_b])
            if t + LA < len(tiles):
                emit_S(t + LA)
            S.op("pe", lambda h: h.matmul(o_t[0:65, :], lhsT=v_t[:, i, j * 65:(j + 1) * 65], rhs=p_t[:, :], start=(mi == 0), stop=False),
                 reads=[v_b, p_b], writes=[o_b])
            if mi == 0 and j == 1:
                flush_deferred()
            if mi == 2:
                S.op("pe", lambda h: h.matmul(o_t[0:65, :], lhsT=sel_t[0:1, 0:65], rhs=esk_t[0:1, j, :], start=False, stop=True),
                     reads=[sel_b, esk_b], writes=[o_b])
                osb_t, osb_b = osb_ring.next()
                rc_t, rc_b = rc_ring.next()
                S.op("act", lambda h: h.activation(out=osb_t[:, :], in_=o_t[0:64, :], func=AF.Copy), reads=[o_b], writes=[osb_b])
                S.op("dve", lambda h: h.reciprocal(out=rc_t[64:65, :], in_=o_t[64:65, :]), reads=[o_b], writes=[rc_b])
                deferred.append((o_t, o_b, osb_t, osb_b, rc_t, rc_b, nl, j))
        flush_deferred()
        yield
        yp_t, yp_b = yp_ring.next()
        S.dma("sp", yp_t[0:64, :, :], ya_t[0:64, 0:8:2, :], reads=[ya_b], writes=[yp_b])
        S.dma("sp", yp_t[64:128, :, :], ya_t[0:64, 1:8:2, :], reads=[ya_b], writes=[yp_b])
        d_t, d_b = d_ring.next()
        L = TB + 16
        for g in range(4):
            w = 2 << g
            steps = g + 1
            src_t, src_b, ln = None, None, L
            for s_i in range(steps):
                sh = 1 << s_i
                tp_t, tp_b = tmp_rings[g].next()
                nl_ = ln - sh
                if s_i == 0:
                    S.op("pool", lambda h, tp_t=tp_t, u_t=u_t, g=g, nl_=nl_, sh=sh: h.tensor_tensor(
                        out=tp_t[:, 0:nl_], in0=u_t[:, g, 0:nl_], in1=u_t[:, g, sh:sh + nl_], op=ALU.add),
                        reads=[u_b], writes=[tp_b])
                else:
                    S.op("pool", lambda h, tp_t=tp_t, src_t=src_t, nl_=nl_, sh=sh: h.tensor_tensor(
                        out=tp_t[:, 0:nl_], in0=src_t[:, 0:nl_], in1=src_t[:, sh:sh + nl_], op=ALU.add),
                        reads=[src_b], writes=[tp_b])
                src_t, src_b, ln = tp_t, tp_b, nl_
            off = 8 - w // 2
            S.op("dve", lambda h, d_t=d_t, src_t=src_t, u_t=u_t, g=g, off=off, w=w: h.scalar_tensor_tensor(
                out=d_t[:, g, :], in0=src_t[:, off:off + TB], scalar=1.0 / w, in1=u_t[:, g, 8:8 + TB],
                op0=ALU.mult, op1=ALU.subtract), reads=[src_b, u_b], writes=[d_b])
            for (is_edge, which, c0) in ((b == 0, 0, 0), (b == NB - 1, 1, TB - 16)):
                if not is_edge:
                    continue
                e_t, e_b = e16_ring.next()
                S.op("dve", lambda h, e_t=e_t, src_t=src_t, g=g, off=off, c0=c0, which=which: h.tensor_tensor(
                    out=e_t[:, :], in0=src_t[:, off + c0:off + c0 + 16], in1=invc_t[:, which, g, :], op=ALU.mult),
                    reads=[src_b, invc_b], writes=[e_b])
                S.op("dve", lambda h, e_t=e_t, d_t=d_t, u_t=u_t, g=g, c0=c0: h.tensor_tensor(
                    out=d_t[:, g, c0:c0 + 16], in0=e_t[:, :], in1=u_t[:, g, 8 + c0:8 + c0 + 16], op=ALU.subtract),
                    reads=[e_b, u_b, d_b], writes=[d_b])
        yield
        yb_t, yb_b = yb_ring.next()
        for g in range(4):
            y_t, y_b = y_ring.next()
            S.op("pe", lambda h, y_t=y_t, d_t=d_t, g=g: h.matmul(y_t[:, :], lhsT=wp[:, g, :], rhs=d_t[:, g, :], start=True, stop=True),
                 reads=[wp_b, d_b], writes=[y_b])
            S.op("dve", lambda h, y_t=y_t, yb_t=yb_t, g=g: h.tensor_scalar_mul(out=yb_t[:, g, :], in0=y_t[:, :], scalar1=ps_t[:, g:g + 1]),
                 reads=[y_b, ps_b], writes=[yb_b])
        for o in range(KC):
            y_t, y_b = y_ring.next()
            for hh in range(4):
                S.op("pe", lambda h, y_t=y_t, yp_t=yp_t, hh=hh, o=o: h.matmul(
                    y_t[:, :], lhsT=woA[:, hh, o * 128:(o + 1) * 128], rhs=yp_t[:, hh, :], start=(hh == 0), stop=False),
                    reads=[woA_b, yp_b], writes=[y_b])
            for g in range(4):
                S.op("pe", lambda h, y_t=y_t, yb_t=yb_t, g=g, o=o: h.matmul(
                    y_t[:, :], lhsT=woB[:, g, o * 128:(o + 1) * 128], rhs=yb_t[:, g, :], start=False, stop=(g == 3)),
                    reads=[woB_b, yb_b], writes=[y_b])
            S.op("dve", lambda h, y_t=y_t, x_t=x_t, o=o: h.tensor_tensor(out=x_t[:, o, :], in0=y_t[:, :], in1=x_t[:, o, :], op=ALU.add),
                 reads=[y_b, x_b], writes=[x_b])
        S.dma("sp", ov[:, :, t0:t0 + TB], x_t[:, :, :], reads=[x_b])
        yield

    run_interleaved((blk(b) for b in range(NB)), 2, 2)
    cx.pop()
    return cx.finish() if own else None


def eb_masks(has_left, has_right):
    ki = np.arange(128)[:, None]
    qi = np.arange(128)[None, :]
    mL = np.tile((ki >= qi).astype(np.float32), (1, 4))
    mR = np.tile((ki <= qi).astype(np.float32), (1, 4))
    return np.stack([mL, mR, mL * float(has_left), mR * float(has_right)]).astype(np.float32)


def eb_invc(is_first, is_last):
    out = np.zeros((128, 2, 4, 16), np.float32)
    for g in range(4):
        w = 2 << g
        half = w // 2
        for i in range(16):
            c0 = min(i + half, w) if is_first else w
            r = 16 - i
            c1 = min(half + r, w) if is_last else w
            out[:, 0, g, i] = 1.0 / c0
            out[:, 1, g, i] = 1.0 / c1
    return out


def emit_rope(cx, src_t, src_b, nrow, TB, rot_t, rot_b, cos_t, cos_b, sin_t, sin_b, qb_ring, pr_ring, t1_ring, t2_ring,
              out_ap, out_b):
    S = cx.S
    qb_t, qb_b = qb_ring.next()
    S.op("act", lambda h: h.activation(out=qb_t[0:nrow, :], in_=src_t[0:nrow, 0:TB], func=AF.Copy), reads=[src_b], writes=[qb_b])
    pr_t, pr_b = pr_ring.next()
    S.op("pe", lambda h: h.matmul(pr_t[0:nrow, 0:TB], lhsT=rot_t[0:nrow, 0:nrow], rhs=qb_t[0:nrow, :], start=True, stop=True),
         reads=[qb_b, rot_b], writes=[pr_b])
    t1_t, t1_b = t1_ring.next()
    t2_t, t2_b = t2_ring.next()
    S.op("dve", lambda h: h.tensor_tensor(out=t1_t[0:nrow, :], in0=src_t[0:nrow, 0:TB], in1=cos_t[0:nrow, :], op=ALU.mult),
         reads=[src_b, cos_b], writes=[t1_b])
    S.op("dve", lambda h: h.tensor_tensor(out=t2_t[0:nrow, :], in0=pr_t[0:nrow, 0:TB], in1=sin_t[0:nrow, :], op=ALU.mult),
         reads=[pr_b, sin_b], writes=[t2_b])
    S.op("dve", lambda h: h.tensor_tensor(out=out_ap, in0=t1_t[0:nrow, :], in1=t2_t[0:nrow, :], op=ALU.add),
         reads=[t1_b, t2_b], writes=[out_b])


def build_oa(ntok=TOK, cx=None, mid_hook=None):
    TB = 512
    NB = ntok // TB
    NT = ntok // 128
    own = cx is None
    cx = Ctx() if own else cx
    cx.push()
    S = cx.S
    xT = cx.dram_in("xT", [D, ntok])
    xhalo = cx.dram_in("xhalo", [D, 4])
    w_in = cx.dram_in("w_in", [D, 1440])
    gin = cx.dram_in("g", [128, KC])
    gcq = cx.dram_in("g_cq", [128, 2])
    gckv = cx.dram_in("g_ckv", [128, 1])
    w_uq = cx.dram_in("w_uq", [256, 768])
    w_ukv = cx.dram_in("w_ukv", [128, 1024])
    cwd = cx.dram_in("cw", [128, 4, 4])
    cbd = cx.dram_in("cb", [128, 4])
    wad = cx.dram_in("wa", [2, 8, 64, 64])
    wxd = cx.dram_in("wx", [2, 8, 64, 64])
    bad = cx.dram_in("ba", [128, 2, 4])
    bxd = cx.dram_in("bx", [128, 2, 4])
    lamd = cx.dram_in("lam", [128, 2, 4])
    cosd = cx.dram_in("cos", [128, ntok])
    sind = cx.dram_in("sin", [128, ntok])
    rotd = cx.dram_in("rot", [128, 128])
    QN = cx.dram_out("QN", [512, ntok], BF16)
    QR = cx.dram_out("QR", [256, ntok], BF16)
    KNR = cx.dram_out("KNR", [544, ntok], BF16)
    V5 = cx.dram_out("V5", [1024, NT * 65], BF16)
    GX = cx.dram_out("GX", [512, ntok], BF16)
    AB = cx.dram_out("AB", [2, 2, 512, ntok])
    BLK = cx.dram_out("BLK", [128, NB, 2, 2, 4])
    CAB = cx.dram_out("CAB", [128, 2, 2, 4])
    XR = cx.dram_out("XR", [512, ntok])

    g_t = cx.sb("g", [128, KC], F32); g_b = Buf("g")
    gcq_t = cx.sb("gcq", [128, 2], F32); gcq_b = Buf("gcq")
    gckv_t = cx.sb("gckv", [128, 1], F32); gckv_b = Buf("gckv")
    ones_t = cx.sb("ones", [128, 128], BF16); ones_b = Buf("ones")
    xrh_t = cx.sb("xrh", [128, 4, 4], F32); xrh_b = Buf("xrh")
    cp_t = cx.sb("cp", [128, 2, 4], F32); cp_b = Buf("cp")
    blk_t = cx.sb("blk", [128, NB, 2, 2, 4], F32); blk_b = Buf("blk")
    S.dma("sp", g_t[:, :], gin[:, :], writes=[g_b])
    S.op("dve", lambda h: h.tensor_scalar_mul(out=g_t[:, :], in0=g_t[:, :], scalar1=float(np.sqrt(D))), reads=[g_b], writes=[g_b])
    S.dma("sp", gcq_t[:, :], gcq[:, :], writes=[gcq_b])
    S.op("dve", lambda h: h.tensor_scalar_mul(out=gcq_t[:, :], in0=gcq_t[:, :], scalar1=16.0), reads=[gcq_b], writes=[gcq_b])
    S.dma("sp", gckv_t[:, :], gckv[:, :], writes=[gckv_b])
    S.op("dve", lambda h: h.tensor_scalar_mul(out=gckv_t[:, :], in0=gckv_t[:, :], scalar1=float(np.sqrt(128.0))),
         reads=[gckv_b], writes=[gckv_b])
    S.op("pool", lambda h: h.memset(ones_t[:, :], 1.0), writes=[ones_b])
    S.dma("sp", cp_t[:, :, :], lamd[:, :, :], writes=[cp_b])
    S.op("act", lambda h: h.activation(out=cp_t[:, :, :], in_=cp_t[:, :, :], func=AF.Exp, scale=-1.0), reads=[cp_b], writes=[cp_b])
    S.op("act", lambda h: h.activation(out=cp_t[:, :, :], in_=cp_t[:, :, :], func=AF.Ln, bias=1.0), reads=[cp_b], writes=[cp_b])
    S.op("dve", lambda h: h.tensor_scalar_mul(out=cp_t[:, :, :], in0=cp_t[:, :, :], scalar1=-8.0), reads=[cp_b], writes=[cp_b])

    wabd = cx.sb("wabd", [128, 2, 4, 128], BF16); wxbd = cx.sb("wxbd", [128, 2, 4, 128], BF16); bd_b = Buf("bd")
    cw_t = cx.sb("cw", [128, 4, 4], F32); cb_t = cx.sb("cb", [128, 4], F32); cw_b = Buf("cw")
    ba_t = cx.sb("ba", [128, 2, 4], F32); bx_t = cx.sb("bx", [128, 2, 4], F32); bb_b = Buf("bb")
    S.op("pool", lambda h: h.memset(wabd[:, :, :, :], 0.0), writes=[bd_b])
    S.op("pool", lambda h: h.memset(wxbd[:, :, :, :], 0.0), writes=[bd_b])
    for d in range(2):
        for c in range(4):
            for hf in range(2):
                S.dma("pool", wabd[hf * 64:(hf + 1) * 64, d, c, hf * 64:(hf + 1) * 64], wad[d, 2 * c + hf, :, :], writes=[bd_b])
                S.dma("pool", wxbd[hf * 64:(hf + 1) * 64, d, c, hf * 64:(hf + 1) * 64], wxd[d, 2 * c + hf, :, :], writes=[bd_b])
    S.dma("sp", cw_t[:, :, :], cwd[:, :, :], writes=[cw_b])
    S.dma("sp", cb_t[:, :], cbd[:, :], writes=[cw_b])
    S.dma("sp", ba_t[:, :, :], bad[:, :, :], writes=[bb_b])
    S.dma("sp", bx_t[:, :, :], bxd[:, :, :], writes=[bb_b])
    xv = xT.rearrange("(c p) t -> p c t", p=128)
    cx.push()
    wb = cx.sb("wb", [128, KC, 1440], BF16)
    w_bufs = [Buf(f"w{k}") for k in range(KC)]
    wuqn = cx.sb("wuqn", [128, 2, 512], BF16); wuqr = cx.sb("wuqr", [128, 2, 256], BF16); wuq_b = Buf("wuq")
    wk = cx.sb("wk", [128, 512], BF16); wv = cx.sb("wv", [128, 512], BF16); wkv_b = Buf("wkv")
    rot_t = cx.sb("rot", [128, 128], BF16); rot_b = Buf("rot")
    x_ring = mk_ring(cx, "sb", "x", 2, [128, KC, TB], F32)
    h_ring = mk_ring(cx, "sb", "h", 2, [128, KC, TB], BF16)
    sq_ring = mk_ring(cx, "sb", "sq", 3, [128, TB], BF16)
    rstd_ring = mk_ring(cx, "sb", "rstd", 2, [128, TB], F32)
    cos_ring = mk_ring(cx, "sb", "cos", 2, [128, TB], F32)
    sin_ring = mk_ring(cx, "sb", "sin", 2, [128, TB], F32)
    qb_ring = mk_ring(cx, "sb", "qb", 2, [128, TB], BF16)
    t1_ring = mk_ring(cx, "sb", "t1", 2, [128, TB], F32)
    t2_ring = mk_ring(cx, "sb", "t2", 2, [128, TB], F32)
    cq_ring = mk_ring(cx, "sb", "cq", 2, [128, 2, TB], F32)
    ckv_ring = mk_ring(cx, "sb", "ckv", 2, [128, 1, TB], F32)
    cqn_ring = mk_ring(cx, "sb", "cqn", 2, [128, 2, TB], BF16)
    ckvn_ring = mk_ring(cx, "sb", "ckvn", 2, [128, 1, TB], BF16)
    xr_ring = mk_ring(cx, "sb", "xr", 2, [128, 4, TB], F32)
    gx_ring = mk_ring(cx, "sb", "gx", 2, [128, 4, TB], BF16)
    qn_ring = mk_ring(cx, "sb", "qn", 2, [128, 4, TB], BF16)
    qr_ring = mk_ring(cx, "sb", "qr", 2, [128, 2, TB], BF16)
    kn_ring = mk_ring(cx, "sb", "kn", 2, [128, 4, TB], BF16)
    kr_ring = mk_ring(cx, "sb", "kr", 2, [32, TB], BF16)
    vo_ring = mk_ring(cx, "sb", "vo", 2, [128, 4, 520], BF16)
    hx_t = cx.sb("hx", [128, KC, 4], F32); hx_b = Buf("hx")
    hh_t = cx.sb("hh", [128, KC, 4], BF16); hh_b = Buf("hh")
    st_ring = mk_ring(cx, "ps", "st", 1, [128, 512], F32)
    pq_ring = mk_ring(cx, "ps", "pq", 4, [128, 512], F32)
    pr_ring = mk_ring(cx, "ps", "pr", 1, [128, 512], F32)
    pv_ring = mk_ring(cx, "ps", "pv", 2, [128, 512], F32)

    for k in range(KC):
        S.dma("pool", wb[:, k, :], w_in[k * 128:(k + 1) * 128, :], writes=[w_bufs[k]])
    for k in range(2):
        src = w_uq[k * 128:(k + 1) * 128, :].rearrange("p (h e) -> p h e", e=96)
        S.dma("pool", wuqn[:, k, :].rearrange("p (h d) -> p h d", d=64), src[:, :, 0:64], writes=[wuq_b])
        S.dma("pool", wuqr[:, k, :].rearrange("p (h d) -> p h d", d=32), src[:, :, 64:96], writes=[wuq_b])
    srckv = w_ukv.rearrange("p (h e) -> p h e", e=128)
    S.dma("pool", wk[:, :].rearrange("p (h d) -> p h d", d=64), srckv[:, :, 0:64], writes=[wkv_b])
    S.dma("pool", wv[:, :].rearrange("p (h d) -> p h d", d=64), srckv[:, :, 64:128], writes=[wkv_b])
    S.dma("pool", rot_t[:, :], rotd[:, :], writes=[rot_b])
    for (vt, vb) in vo_ring.items:
        S.op("pool", lambda h, vt=vt: h.memset(vt[:, :, :], 1.0), writes=[vb])

    def proj_tile(h_t, h_b, c0, ncols, TBx):
        pq_t, pq_b = pq_ring.next()
        for k in range(KC):
            S.op("pe", lambda h, k=k: h.matmul(pq_t[0:ncols, 0:TBx], lhsT=wb[:, k, c0:c0 + ncols], rhs=h_t[:, k, 0:TBx],
                                               start=(k == 0), stop=(k == KC - 1)), reads=[h_b, w_bufs[k]], writes=[pq_b])
        return pq_t, pq_b

    S.dma("sp", hx_t[:, :, :], xhalo.rearrange("(c p) t -> p c t", p=128), writes=[hx_b])
    emit_rmsnorm(cx, hx_t, hx_b, KC, 4, g_t, g_b, ones_t, ones_b, sq_ring, st_ring, rstd_ring, hh_t, hh_b, D)
    for c in range(4):
        pq_t, pq_b = proj_tile(hh_t, hh_b, 416 + c * 128, 128, 4)
        S.op("act", lambda h, c=c, pq_t=pq_t: h.activation(out=xrh_t[:, c, :], in_=pq_t[:, 0:4], func=AF.Copy),
             reads=[pq_b], writes=[xrh_b])

    def blk(b):
        t0 = b * TB
        x_t, x_b = x_ring.next()
        S.dma("sp", x_t[:, :, :], xv[:, :, t0:t0 + TB], writes=[x_b])
        cos_t, cos_b = cos_ring.next()
        sin_t, sin_b = sin_ring.next()
        S.dma("sp", cos_t[:, :], cosd[:, t0:t0 + TB], writes=[cos_b])
        S.dma("sp", sin_t[:, :], sind[:, t0:t0 + TB], writes=[sin_b])
        h_t, h_b = h_ring.next()
        emit_rmsnorm(cx, x_t, x_b, KC, TB, g_t, g_b, ones_t, ones_b, sq_ring, st_ring, rstd_ring, h_t, h_b, D)
        yield
        cq_t, cq_b = cq_ring.next()
        for c in range(2):
            pq_t, pq_b = proj_tile(h_t, h_b, c * 128, 128, TB)
            S.op("act", lambda h, c=c, pq_t=pq_t, cq_t=cq_t: h.activation(out=cq_t[:, c, :], in_=pq_t[:, 0:TB], func=AF.Copy),
                 reads=[pq_b], writes=[cq_b])
        ckv_t, ckv_b = ckv_ring.next()
        pq_t, pq_b = proj_tile(h_t, h_b, 256, 128, TB)
        S.op("act", lambda h, pq_t=pq_t, ckv_t=ckv_t: h.activation(out=ckv_t[:, 0, :], in_=pq_t[:, 0:TB], func=AF.Copy),
             reads=[pq_b], writes=[ckv_b])
        yield
        pq_t, pq_b = proj_tile(h_t, h_b, 384, 32, TB)
        kr_t, kr_b = kr_ring.next()
        emit_rope(cx, pq_t, pq_b, 32, TB, rot_t, rot_b, cos_t, cos_b, sin_t, sin_b, qb_ring, pr_ring, t1_ring, t2_ring,
                  kr_t[0:32, :], kr_b)
        S.dma("pool", KNR[512:544, t0:t0 + TB], kr_t[:, :], reads=[kr_b])
        yield
        xr_t, xr_b = xr_ring.next()
        gx_t, gx_b = gx_ring.next()
        for c in range(4):
            pq_t, pq_b = proj_tile(h_t, h_b, 416 + c * 128, 128, TB)
            S.op("act", lambda h, c=c, pq_t=pq_t, xr_t=xr_t: h.activation(out=xr_t[:, c, :], in_=pq_t[:, 0:TB], func=AF.Copy),
                 reads=[pq_b], writes=[xr_b])
        for c in range(4):
            pq_t, pq_b = proj_tile(h_t, h_b, 928 + c * 128, 128, TB)
            S.op("act", lambda h, c=c, pq_t=pq_t, gx_t=gx_t: h.activation(out=gx_t[:, c, :], in_=pq_t[:, 0:TB], func=AF.Gelu_apprx_tanh),
                 reads=[pq_b], writes=[gx_b])
        S.dma("pool", XR.rearrange("(c p) t -> p c t", p=128)[:, :, t0:t0 + TB], xr_t[:, :, :], reads=[xr_b])
        S.dma("pool", GX.rearrange("(c p) t -> p c t", p=128)[:, :, t0:t0 + TB], gx_t[:, :, :], reads=[gx_b])
        yield
        cqn_t, cqn_b = cqn_ring.next()
        emit_rmsnorm(cx, cq_t, cq_b, 2, TB, gcq_t, gcq_b, ones_t, ones_b, sq_ring, st_ring, rstd_ring, cqn_t, cqn_b, 256)
        ckvn_t, ckvn_b = ckvn_ring.next()
        emit_rmsnorm(cx, ckv_t, ckv_b, 1, TB, gckv_t, gckv_b, ones_t, ones_b, sq_ring, st_ring, rstd_ring, ckvn_t, ckvn_b, 128)
        yield
        qn_t, qn_b = qn_ring.next()
        for i in range(4):
            pq_t, pq_b = pq_ring.next()
            for k in range(2):
                S.op("pe", lambda h, i=i, k=k, pq_t=pq_t, cqn_t=cqn_t: h.matmul(
                    pq_t[:, 0:TB], lhsT=wuqn[:, k, i * 128:(i + 1) * 128], rhs=cqn_t[:, k, :], start=(k == 0), stop=(k == 1)),
                    reads=[cqn_b, wuq_b], writes=[pq_b])
            S.op("act", lambda h, i=i, pq_t=pq_t, qn_t=qn_t: h.activation(out=qn_t[:, i, :], in_=pq_t[:, 0:TB], func=AF.Copy),
                 reads=[pq_b], writes=[qn_b])
        S.dma("pool", QN.rearrange("(c p) t -> p c t", p=128)[:, :, t0:t0 + TB], qn_t[:, :, :], reads=[qn_b])
        qr_t, qr_b = qr_ring.next()
        for i in range(2):
            pq_t, pq_b = pq_ring.next()
            for k in range(2):
                S.op("pe", lambda h, i=i, k=k, pq_t=pq_t, cqn_t=cqn_t: h.matmul(
                    pq_t[:, 0:TB], lhsT=wuqr[:, k, i * 128:(i + 1) * 128], rhs=cqn_t[:, k, :], start=(k == 0), stop=(k == 1)),
                    reads=[cqn_b, wuq_b], writes=[pq_b])
            emit_rope(cx, pq_t, pq_b, 128, TB, rot_t, rot_b, cos_t, cos_b, sin_t, sin_b, qb_ring, pr_ring, t1_ring, t2_ring,
                      qr_t[:, i, :], qr_b)
        S.dma("pool", QR.rearrange("(c p) t -> p c t", p=128)[:, :, t0:t0 + TB], qr_t[:, :, :], reads=[qr_b])
        yield
        kn_t, kn_b = kn_ring.next()
        for i in range(4):
            pq_t, pq_b = pq_ring.next()
            S.op("pe", lambda h, i=i, pq_t=pq_t, ckvn_t=ckvn_t: h.matmul(
                pq_t[:, 0:TB], lhsT=wk[:, i * 128:(i + 1) * 128], rhs=ckvn_t[:, 0, :], start=True, stop=True),
                reads=[ckvn_b, wkv_b], writes=[pq_b])
            S.op("act", lambda h, i=i, pq_t=pq_t, kn_t=kn_t: h.activation(out=kn_t[:, i, :], in_=pq_t[:, 0:TB], func=AF.Copy),
                 reads=[pq_b], writes=[kn_b])
        S.dma("pool", KNR[0:512, :].rearrange("(c p) t -> p c t", p=128)[:, :, t0:t0 + TB], kn_t[:, :, :], reads=[kn_b])
        yield
        vo_t, vo_b = vo_ring.next()
        for ti in range(TB // 128):
            pv_t, pv_b = pv_ring.next()
            S.op("pe", lambda h, ti=ti, pv_t=pv_t, ckvn_t=ckvn_t: h.matmul(
                pv_t[:, :], lhsT=ckvn_t[:, 0, ti * 128:(ti + 1) * 128], rhs=wv[:, :], start=True, stop=True),
                reads=[ckvn_b, wkv_b], writes=[pv_b])
            S.op("act", lambda h, ti=ti, pv_t=pv_t, vo_t=vo_t: h.activation(
                out=vo_t[:, ti, :].rearrange("p (h e) -> p h e", e=65)[:, :, 0:64],
                in_=pv_t[:, :].rearrange("p (h d) -> p h d", d=64), func=AF.Copy), reads=[pv_b], writes=[vo_b])
        for hd in range(8):
            S.dma("pool", V5[hd * 128:(hd + 1) * 128, :].rearrange("p (i e) -> p i e", e=65)[:, b * 4:(b + 1) * 4, :],
                  vo_t[:, :, hd * 65:(hd + 1) * 65], reads=[vo_b])
        yield

    run_interleaved((blk(b) for b in range(NB)), 2)
    cx.pop()
    if mid_hook is not None:
        mid_hook()

    cx.push()
    xe_ring = mk_ring(cx, "sb", "xe", 2, [128, 4, TB + 4], F32)
    xc_ring = mk_ring(cx, "sb", "xc", 2, [128, 4, TB], F32)
    xcb_ring = mk_ring(cx, "sb", "xcb", 2, [128, 4, TB], BF16)
    r_ring = mk_ring(cx, "sb", "r", 2, [128, 8, TB], F32)
    i_ring = mk_ring(cx, "sb", "i", 2, [128, 8, TB], F32)
    a_ring = mk_ring(cx, "sb", "a", 2, [128, 8, TB], F32)
    b_ring = mk_ring(cx, "sb", "b", 2, [128, 8, TB], F32)
    hl_ring = mk_ring(cx, "sb", "hl", 2, [128, TB], F32)
    sr_ring = mk_ring(cx, "sb", "sr", 2, [128, 8], F32)
    pg_ring = mk_ring(cx, "ps", "pg", 6, [128, 512], F32)
    XRv = XR.rearrange("(c p) t -> p c t", p=128)
    ABv = AB.rearrange("d s (c p) t -> d s p c t", p=128)

    def blk(b):
        t0 = b * TB
        xe_t, xe_b = xe_ring.next()
        lo = 0 if b > 0 else 2
        hi = TB + 3 if b < NB - 1 else TB + 2
        S.dma("sp", xe_t[:, :, lo:hi], XRv[:, :, t0 - 2 + lo:t0 - 2 + hi], writes=[xe_b])
        if b == 0:
            S.op("dve", lambda h, xe_t=xe_t: h.tensor_copy(out=xe_t[:, :, 0:2], in_=xrh_t[:, :, 0:2]), reads=[xrh_b, xe_b], writes=[xe_b])
        if b == NB - 1:
            S.op("dve", lambda h, xe_t=xe_t: h.tensor_copy(out=xe_t[:, :, TB + 2:TB + 3], in_=xrh_t[:, :, 2:3]),
                 reads=[xrh_b, xe_b], writes=[xe_b])
        yield
        xc_t, xc_b = xc_ring.next()
        xcb_t, xcb_b = xcb_ring.next()
        for c in range(4):
            S.op("dve", lambda h, c=c, xc_t=xc_t, xe_t=xe_t: h.tensor_scalar(
                out=xc_t[:, c, :], in0=xe_t[:, c, 0:TB], scalar1=cw_t[:, c, 0:1], scalar2=cb_t[:, c:c + 1],
                op0=ALU.mult, op1=ALU.add), reads=[xe_b, cw_b], writes=[xc_b])
            for j in range(1, 4):
                S.op("dve", lambda h, c=c, j=j, xc_t=xc_t, xe_t=xe_t: h.scalar_tensor_tensor(
                    out=xc_t[:, c, :], in0=xe_t[:, c, j:j + TB], scalar=cw_t[:, c, j:j + 1], in1=xc_t[:, c, :],
                    op0=ALU.mult, op1=ALU.add), reads=[xe_b, cw_b, xc_b], writes=[xc_b])
        S.op("act", lambda h, xc_t=xc_t, xcb_t=xcb_t: h.activation(out=xcb_t[:, :, :], in_=xc_t[:, :, :], func=AF.Copy), reads=[xc_b], writes=[xcb_b])
        yield
        r_t, r_b = r_ring.next()
        i_t, i_b = i_ring.next()
        a_t, a_b = a_ring.next()
        b_t, b_b = b_ring.next()
        sr_t, sr_b = sr_ring.next()
        S.op("dve", lambda h, sr_t=sr_t: h.memset(sr_t[:, :], 0.0), writes=[sr_b])
        for d in range(2):
            for c in range(4):
                q = d * 4 + c
                pg_t, pg_b = pg_ring.next()
                S.op("pe", lambda h, d=d, c=c, pg_t=pg_t, xcb_t=xcb_t: h.matmul(pg_t[:, :], lhsT=wabd[:, d, c, :], rhs=xcb_t[:, c, :],
                                                                         start=True, stop=True), reads=[bd_b, xcb_b], writes=[pg_b])
                S.op("act", lambda h, d=d, c=c, q=q, pg_t=pg_t, r_t=r_t, sr_t=sr_t: h.activation(
                    out=r_t[:, q, :], in_=pg_t[:, :], func=AF.Sigmoid, bias=ba_t[:, d, c:c + 1], accum_out=sr_t[:, q:q + 1]),
                    reads=[pg_b, bb_b], writes=[r_b, sr_b])
                pg_t, pg_b = pg_ring.next()
                S.op("pe", lambda h, d=d, c=c, pg_t=pg_t, xcb_t=xcb_t: h.matmul(pg_t[:, :], lhsT=wxbd[:, d, c, :], rhs=xcb_t[:, c, :],
                                                                         start=True, stop=True), reads=[bd_b, xcb_b], writes=[pg_b])
                S.op("act", lambda h, d=d, c=c, q=q, pg_t=pg_t, i_t=i_t: h.activation(
                    out=i_t[:, q, :], in_=pg_t[:, :], func=AF.Sigmoid, bias=bx_t[:, d, c:c + 1]),
                    reads=[pg_b, bb_b], writes=[i_b])
        yield
        for d in range(2):
            for c in range(4):
                q = d * 4 + c
                S.op("act", lambda h, d=d, c=c, q=q, a_t=a_t, r_t=r_t: h.activation(
                    out=a_t[:, q, :], in_=r_t[:, q, :], func=AF.Exp, scale=cp_t[:, d, c:c + 1]), reads=[r_b, cp_b], writes=[a_b])
                S.op("act", lambda h, d=d, c=c, q=q, sr_t=sr_t, b=b: h.activation(
                    out=blk_t[:, b, d, 0, c:c + 1], in_=sr_t[:, q:q + 1], func=AF.Exp, scale=cp_t[:, d, c:c + 1]),
                    reads=[sr_b, cp_b, blk_b], writes=[blk_b])
        yield
        S.op("dve", lambda h, a_t=a_t, r_t=r_t: h.tensor_tensor(out=r_t[:, :, :], in0=a_t[:, :, :], in1=a_t[:, :, :], op=ALU.mult),
             reads=[a_b, r_b], writes=[r_b])
        S.op("act", lambda h, r_t=r_t: h.activation(out=r_t[:, :, :], in_=r_t[:, :, :], func=AF.Sqrt, scale=-1.0, bias=1.0),
             reads=[r_b], writes=[r_b])
        for d in range(2):
            S.op("dve", lambda h, d=d, i_t=i_t, xc_t=xc_t: h.tensor_tensor(out=i_t[:, d * 4:(d + 1) * 4, :], in0=i_t[:, d * 4:(d + 1) * 4, :],
                                                                        in1=xc_t[:, :, :], op=ALU.mult), reads=[i_b, xc_b], writes=[i_b])
        S.op("dve", lambda h, b_t=b_t, r_t=r_t, i_t=i_t: h.tensor_tensor(out=b_t[:, :, :], in0=r_t[:, :, :], in1=i_t[:, :, :], op=ALU.mult),
             reads=[r_b, i_b], writes=[b_b])
        yield
        for d in range(2):
            for c in range(4):
                q = d * 4 + c
                hl_t, hl_b = hl_ring.next()
                if d == 0:
                    S.op("dve", lambda h, q=q, hl_t=hl_t, a_t=a_t, b_t=b_t: h.tensor_tensor_scan(
                        out=hl_t[:, :], data0=a_t[:, q, :], data1=b_t[:, q, :], initial=0.0, op0=ALU.mult, op1=ALU.add),
                        reads=[a_b, b_b], writes=[hl_b])
                    col = TB - 1
                else:
                    S.op("dve", lambda h, q=q, hl_t=hl_t, a_t=a_t, b_t=b_t: h.tensor_tensor_scan(
                        out=hl_t[:, ::-1], data0=a_t[:, q, ::-1], data1=b_t[:, q, ::-1], initial=0.0, op0=ALU.mult, op1=ALU.add),
                        reads=[a_b, b_b], writes=[hl_b])
                    col = 0
                S.op("act", lambda h, d=d, c=c, hl_t=hl_t, col=col, b=b: h.activation(
                    out=blk_t[:, b, d, 1, c:c + 1], in_=hl_t[:, col:col + 1], func=AF.Copy), reads=[hl_b, blk_b], writes=[blk_b])
        for d in range(2):
            S.dma("act", ABv[d, 0, :, :, t0:t0 + TB], a_t[:, d * 4:(d + 1) * 4, :], reads=[a_b])
            S.dma("sp", ABv[d, 1, :, :, t0:t0 + TB], b_t[:, d * 4:(d + 1) * 4, :], reads=[b_b])
        yield

    run_interleaved((blk(b) for b in range(NB)), 2)
    cab_t = cx.sb("cab", [128, 2, 2, 4], F32); cab_b = Buf("cab")
    for d in range(2):
        S.op("dve", lambda h, d=d: h.memset(cab_t[:, d, 0, :], 1.0), writes=[cab_b])
        S.op("dve", lambda h, d=d: h.memset(cab_t[:, d, 1, :], 0.0), writes=[cab_b])
        order = range(NB) if d == 0 else range(NB - 1, -1, -1)
        for b in order:
            S.op("dve", lambda h, d=d, b=b: h.tensor_tensor(out=cab_t[:, d, 1, :], in0=cab_t[:, d, 1, :], in1=blk_t[:, b, d, 0, :],
                                                            op=ALU.mult), reads=[cab_b, blk_b], writes=[cab_b])
            S.op("dve", lambda h, d=d, b=b: h.tensor_tensor(out=cab_t[:, d, 1, :], in0=cab_t[:, d, 1, :], in1=blk_t[:, b, d, 1, :],
                                                            op=ALU.add), reads=[cab_b, blk_b], writes=[cab_b])
            S.op("dve", lambda h, d=d, b=b: h.tensor_tensor(out=cab_t[:, d, 0, :], in0=cab_t[:, d, 0, :], in1=blk_t[:, b, d, 0, :],
                                                            op=ALU.mult), reads=[cab_b, blk_b], writes=[cab_b])
    S.dma("sp", BLK[:, :, :, :, :], blk_t[:, :, :, :, :], reads=[blk_b])
    S.dma("sp", CAB[:, :, :, :], cab_t[:, :, :, :], reads=[cab_b])
    cx.pop()
    cx.pop()
    return cx.finish() if own else None


def chunk_vec(v, nch):
    return np.ascontiguousarray(np.asarray(v, np.float32).reshape(nch, 128).T)


def oa_inputs(xT, xhalo, P, pos):
    cos, sin = rope_tables(pos, 16, 128)
    return {
        "xT": np.ascontiguousarray(xT), "xhalo": np.ascontiguousarray(xhalo), "w_in": P["w_in"], "g": vec128(P["g"], 8),
        "g_cq": vec128(P["g_cq"], 2), "g_ckv": vec128(P["g_ckv"], 1), "w_uq": P["w_uq"], "w_ukv": P["w_ukv"],
        "cw": np.ascontiguousarray(P["conv_w"].reshape(4, 4, 128).transpose(2, 1, 0)),
        "cb": chunk_vec(P["conv_b"], 4), "wa": P["wa"], "wx": P["wx"],
        "ba": np.ascontiguousarray(P["ba"].reshape(2, 4, 128).transpose(2, 0, 1)),
        "bx": np.ascontiguousarray(P["bx"].reshape(2, 4, 128).transpose(2, 0, 1)),
        "lam": np.ascontiguousarray(P["lam"].reshape(2, 4, 128).transpose(2, 0, 1)),
        "cos": cos, "sin": sin, "rot": rot_matrix(32),
    }


def build_ob1(ntok=TOK, nrank=4, cx=None):
    seq = ntok * nrank
    QG = ntok // 512
    NKT = seq // 128
    NT = ntok // 128
    own = cx is None
    cx = Ctx() if own else cx
    cx.push()
    S = cx.S
    QN = cx.dram_in("QN", [512, ntok], BF16)
    QR = cx.dram_in("QR", [256, ntok], BF16)
    KNg = cx.dram_in("KNg", [8 * nrank * 64, ntok], BF16)
    KRg = cx.dram_in("KRg", [nrank * 32, ntok], BF16)
    Vg = cx.dram_in("Vg", [8 * nrank * 128, NT * 65], BF16)
    YC = cx.dram_out("YC", [512, ntok], BF16)

    q_ring = mk_ring(cx, "sb", "q", 2, [128, ntok], BF16)
    k_ring = mk_ring(cx, "sb", "k", 2, [128, seq], BF16)
    v_ring = mk_ring(cx, "sb", "v", 2, [128, NKT, 65], BF16)
    p_ring = mk_ring(cx, "sb", "p", 4, [128, 1024], BF16)
    osb_ring = mk_ring(cx, "sb", "osb", 2, [64, 512], F32)
    rc_ring = mk_ring(cx, "sb", "rc", 2, [128, 512], F32)
    yc_ring = mk_ring(cx, "sb", "yc", 2, [64, 512], BF16)
    ones32 = cx.sb("ones32", [128, 64], F32); ones32_b = Buf("ones32")
    s_ring = mk_ring(cx, "ps", "s", 3, [128, 1024], F32)
    o_ring = mk_ring(cx, "ps", "o", 2, [128, 512], F32)
    S.op("pool", lambda h: h.memset(ones32[:, :], 1.0), writes=[ones32_b])
    scale = float(96 ** -0.5)
    NKP = NKT // 2
    LA = 2

    def load_head(hd):
        q_t, q_b = q_ring.next()
        k_t, k_b = k_ring.next()
        v_t, v_b = v_ring.next()
        S.dma("sp", q_t[0:64, :], QN[hd * 64:(hd + 1) * 64, :], writes=[q_b])
        S.dma("sp", q_t[64:96, :], QR[hd * 32:(hd + 1) * 32, :], writes=[q_b])
        for r in range(nrank):
            S.dma("sp", k_t[0:64, r * ntok:(r + 1) * ntok], KNg[(hd * nrank + r) * 64:(hd * nrank + r + 1) * 64, :], writes=[k_b])
            S.dma("sp", k_t[64:96, r * ntok:(r + 1) * ntok], KRg[r * 32:(r + 1) * 32, :], writes=[k_b])
            S.dma("sp", v_t[:, r * NT:(r + 1) * NT, :],
                  Vg[(hd * nrank + r) * 128:(hd * nrank + r + 1) * 128, :].rearrange("p (i e) -> p i e", e=65), writes=[v_b])
        return (q_t, q_b, k_t, k_b, v_t, v_b)

    nxt = load_head(0)
    cx.run_hook()
    for hd in range(8):
        q_t, q_b, k_t, k_b, v_t, v_b = nxt
        if hd + 1 < 8:
            nxt = load_head(hd + 1)
        for qg in range(QG):
            o_t, o_b = o_ring.next()
            stiles = {}

            def emit_s(kp):
                s_t, s_b = s_ring.next()
                for hf in range(2):
                    kt = 2 * kp + hf
                    S.op("pe", lambda h, kt=kt, hf=hf: h.matmul(s_t[:, hf * 512:(hf + 1) * 512], lhsT=k_t[0:96, kt * 128:(kt + 1) * 128],
                                                                rhs=q_t[0:96, qg * 512:(qg + 1) * 512], start=True, stop=True),
                         reads=[k_b, q_b], writes=[s_b])
                stiles[kp] = (s_t, s_b)

            for kp in range(min(LA, NKP)):
                emit_s(kp)
            for kp in range(NKP):
                s_t, s_b = stiles.pop(kp)
                p_t, p_b = p_ring.next()
                S.op("act", lambda h, s_t=s_t, p_t=p_t: h.activation(out=p_t[:, :], in_=s_t[:, :], func=AF.Exp, scale=scale),
                     reads=[s_b], writes=[p_b])
                if kp + LA < NKP:
                    emit_s(kp + LA)
                for hf in range(2):
                    kt = 2 * kp + hf
                    S.op("pe", lambda h, kt=kt, hf=hf, p_t=p_t: h.matmul(o_t[0:65, :], lhsT=v_t[:, kt, 0:65], rhs=p_t[:, hf * 512:(hf + 1) * 512],
                                                                         start=(kt == 0), stop=(kt == NKT - 1)),
                         reads=[v_b, p_b], writes=[o_b])
            osb_t, osb_b = osb_ring.next()
            rc_t, rc_b = rc_ring.next()
            S.op("act", lambda h: h.activation(out=osb_t[:, :], in_=o_t[0:64, :], func=AF.Copy), reads=[o_b], writes=[osb_b])
            S.op("dve", lambda h: h.reciprocal(out=rc_t[64:65, :], in_=o_t[64:65, :]), reads=[o_b], writes=[rc_b])
            bc_t, bc_b = s_ring.next()
            S.op("pe", lambda h: h.matmul(bc_t[0:64, 0:512], lhsT=ones32[64:65, 0:64], rhs=rc_t[64:65, :], start=True, stop=True),
                 reads=[rc_b, ones32_b], writes=[bc_b])
            yc_t, yc_b = yc_ring.next()
            S.op("dve", lambda h: h.tensor_tensor(out=yc_t[:, :], in0=osb_t[:, :], in1=bc_t[0:64, 0:512], op=ALU.mult),
                 reads=[osb_b, bc_b], writes=[yc_b])
            S.dma("pool", YC[hd * 64:(hd + 1) * 64, qg * 512:(qg + 1) * 512], yc_t[:, :], reads=[yc_b])
    cx.pop()
    return cx.finish() if own else None


def build_ob2(ntok=TOK, ngrp=4, cx=None):
    TB = 512
    NB = ntok // TB
    own = cx is None
    cx = Ctx() if own else cx
    cx.push()
    S = cx.S
    AB = cx.dram_in("AB", [2, 2, 512, ntok])
    GX = cx.dram_in("GX", [512, ntok], BF16)
    YC = cx.dram_in("YC", [512, ntok], BF16)
    xT = cx.dram_in("xT", [D, ntok])
    w_out = cx.dram_in("w_out", [D, D])
    BLK = cx.dram_in("BLK", [128, NB, 2, 2, 4])
    CABg = cx.dram_in("CABg", [128, ngrp, 16])
    mfd = cx.dram_in("mf", [128, ngrp])
    mbd = cx.dram_in("mb", [128, ngrp])
    oT = cx.dram_out("oT", [D, ntok])

    woA = cx.sb("woA", [128, 4, D], BF16); woA_b = Buf("woA")
    woB = cx.sb("woB", [128, 4, D], BF16); woB_b = Buf("woB")
    blk_t = cx.sb("blk", [128, NB, 2, 2, 4], F32); blk_b = Buf("blk")
    cab_t = cx.sb("cab", [128, ngrp, 16], F32); cab_b = Buf("cab")
    m_t = cx.sb("m", [128, 2, ngrp], F32); m_b = Buf("m")
    hin_t = cx.sb("hin", [128, 2, 4], F32); hin_b = Buf("hin")
    tmp_t = cx.sb("tmp", [128, 4], F32); tmp_b = Buf("tmp")
    init_t = cx.sb("init", [128, NB, 2, 4], F32); init_b = Buf("init")
    ab_ring = mk_ring(cx, "sb", "ab", 2, [128, 2, 2, 4, TB], F32)
    hs_ring = mk_ring(cx, "sb", "hs", 2, [128, 2, 4, TB], F32)
    gx_ring = mk_ring(cx, "sb", "gx", 2, [128, 4, TB], BF16)
    yc_ring = mk_ring(cx, "sb", "yc", 2, [128, 4, TB], BF16)
    yd_ring = mk_ring(cx, "sb", "yd", 2, [128, 4, TB], BF16)
    x_ring = mk_ring(cx, "sb", "x", 2, [128, KC, TB], F32)
    y_ring = mk_ring(cx, "ps", "y", 3, [128, 512], F32)

    S.dma("pool", woA[:, :, :], w_out[0:512, :].rearrange("(i p) n -> p i n", p=128), writes=[woA_b])
    S.dma("pool", woB[:, :, :], w_out[512:1024, :].rearrange("(g p) n -> p g n", p=128), writes=[woB_b])
    S.dma("sp", blk_t[:, :, :, :, :], BLK[:, :, :, :, :], writes=[blk_b])
    S.dma("sp", cab_t[:, :, :], CABg[:, :, :], writes=[cab_b])
    S.dma("sp", m_t[:, 0, :], mfd[:, :], writes=[m_b])
    S.dma("sp", m_t[:, 1, :], mbd[:, :], writes=[m_b])
    S.op("pool", lambda h: h.memset(hin_t[:, :, :], 0.0), writes=[hin_b])
    for d in range(2):
        order = range(ngrp) if d == 0 else range(ngrp - 1, -1, -1)
        for i in order:
            S.op("dve", lambda h, d=d, i=i: h.tensor_tensor(out=tmp_t[:, :], in0=hin_t[:, d, :], in1=cab_t[:, i, d * 8:d * 8 + 4], op=ALU.mult),
                 reads=[hin_b, cab_b, tmp_b], writes=[tmp_b])
            S.op("dve", lambda h, d=d, i=i: h.tensor_tensor(out=tmp_t[:, :], in0=tmp_t[:, :], in1=cab_t[:, i, d * 8 + 4:d * 8 + 8], op=ALU.add),
                 reads=[tmp_b, cab_b], writes=[tmp_b])
            S.op("dve", lambda h, d=d, i=i: h.tensor_tensor(out=tmp_t[:, :], in0=tmp_t[:, :], in1=hin_t[:, d, :], op=ALU.subtract),
                 reads=[tmp_b, hin_b], writes=[tmp_b])
            S.op("dve", lambda h, d=d, i=i: h.scalar_tensor_tensor(out=hin_t[:, d, :], in0=tmp_t[:, :], scalar=m_t[:, d, i:i + 1],
                                                                   in1=hin_t[:, d, :], op0=ALU.mult, op1=ALU.add),
                 reads=[tmp_b, m_b, hin_b], writes=[hin_b])
    for d in range(2):
        order = list(range(NB)) if d == 0 else list(range(NB - 1, -1, -1))
        S.op("dve", lambda h, d=d, b0=order[0]: h.tensor_copy(out=init_t[:, b0, d, :], in_=hin_t[:, d, :]),
             reads=[hin_b, init_b], writes=[init_b])
        for bi in range(NB - 1):
            b, bn = order[bi], order[bi + 1]
            S.op("dve", lambda h, d=d, b=b, bn=bn: h.tensor_tensor(out=init_t[:, bn, d, :], in0=init_t[:, b, d, :],
                                                                   in1=blk_t[:, b, d, 0, :], op=ALU.mult),
                 reads=[init_b, blk_b], writes=[init_b])
            S.op("dve", lambda h, d=d, b=b, bn=bn: h.tensor_tensor(out=init_t[:, bn, d, :], in0=init_t[:, bn, d, :],
                                                                   in1=blk_t[:, b, d, 1, :], op=ALU.add),
                 reads=[init_b, blk_b], writes=[init_b])
    xv = xT.rearrange("(c p) t -> p c t", p=128)
    ov = oT.rearrange("(c p) t -> p c t", p=128)
    ABv = AB.rearrange("d s (c p) t -> d s p c t", p=128)
    def blk(b):
        t0 = b * TB
        ab_t, ab_b = ab_ring.next()
        for d in range(2):
            for s_ in range(2):
                S.dma("sp", ab_t[:, d, s_, :, :], ABv[d, s_, :, :, t0:t0 + TB], writes=[ab_b])
        gx_t, gx_b = gx_ring.next()
        yc_t, yc_b = yc_ring.next()
        x_t, x_b = x_ring.next()
        S.dma("sp", gx_t[:, :, :], GX.rearrange("(c p) t -> p c t", p=128)[:, :, t0:t0 + TB], writes=[gx_b])
        S.dma("sp", yc_t[:, :, :], YC.rearrange("(i p) t -> p i t", p=128)[:, :, t0:t0 + TB], writes=[yc_b])
        S.dma("sp", x_t[:, :, :], xv[:, :, t0:t0 + TB], writes=[x_b])
        yield
        hs_t, hs_b = hs_ring.next()
        for d in range(2):
            for c in range(4):
                if d == 0:
                    S.op("dve", lambda h, d=d, c=c, b=b: h.tensor_tensor_scan(
                        out=hs_t[:, d, c, :], data0=ab_t[:, d, 0, c, :], data1=ab_t[:, d, 1, c, :],
                        initial=init_t[:, b, d, c:c + 1], op0=ALU.mult, op1=ALU.add), reads=[ab_b, init_b, hs_b], writes=[hs_b])
                else:
                    S.op("dve", lambda h, d=d, c=c, b=b: h.tensor_tensor_scan(
                        out=hs_t[:, d, c, ::-1], data0=ab_t[:, d, 0, c, ::-1], data1=ab_t[:, d, 1, c, ::-1],
                        initial=init_t[:, b, d, c:c + 1], op0=ALU.mult, op1=ALU.add), reads=[ab_b, init_b, hs_b], writes=[hs_b])
        yield
        S.op("pool", lambda h: h.tensor_tensor(out=hs_t[:, 0, :, :], in0=hs_t[:, 0, :, :], in1=hs_t[:, 1, :, :], op=ALU.add),
             reads=[hs_b], writes=[hs_b])
        yd_t, yd_b = yd_ring.next()
        S.op("pool", lambda h: h.tensor_tensor(out=yd_t[:, :, :], in0=hs_t[:, 0, :, :], in1=gx_t[:, :, :], op=ALU.mult),
             reads=[hs_b, gx_b], writes=[yd_b])
        yield
        for o in range(KC):
            y_t, y_b = y_ring.next()
            for hh in range(4):
                S.op("pe", lambda h, hh=hh, o=o: h.matmul(y_t[:, :], lhsT=woA[:, hh, o * 128:(o + 1) * 128], rhs=yc_t[:, hh, :],
                                                         start=(hh == 0), stop=False), reads=[woA_b, yc_b], writes=[y_b])
            for g in range(4):
                S.op("pe", lambda h, g=g, o=o: h.matmul(y_t[:, :], lhsT=woB[:, g, o * 128:(o + 1) * 128], rhs=yd_t[:, g, :],
                                                       start=False, stop=(g == 3)), reads=[woB_b, yd_b], writes=[y_b])
            S.op("dve", lambda h, o=o: h.tensor_tensor(out=x_t[:, o, :], in0=y_t[:, :], in1=x_t[:, o, :], op=ALU.add),
                 reads=[y_b, x_b], writes=[x_b])
        S.dma("pool", ov[:, :, t0:t0 + TB], x_t[:, :, :], reads=[x_b])
        yield

    run_interleaved((blk(b) for b in range(NB)), 2, 2)
    cx.pop()
    return cx.finish() if own else None


def allgather(cx, in_ap, out_ap, groups):
    S = cx.S
    S.barrier()
    sem = S.new_sem("cc")
    cx.nc.gpsimd.collective_compute("AllGather", ALU.bypass, replica_groups=groups, ins=[in_ap], outs=[out_ap]).then_inc(sem, 1)
    for e in S.ENGS:
        S.h[e].wait_ge(sem, 1)


def allgather_many(cx, pairs, groups):
    S = cx.S
    S.barrier()
    sem = S.new_sem("ccm")
    for (in_ap, out_ap) in pairs:
        cx.nc.gpsimd.collective_compute("AllGather", ALU.bypass, replica_groups=groups, ins=[in_ap], outs=[out_ap]).then_inc(sem, 1)
    for e in S.ENGS:
        S.h[e].wait_ge(sem, len(pairs))


def emit_select(cx, src_t, src_b, nrank, m_t, m_b, side, acc_t, acc_b):
    S = cx.S
    S.op("dve", lambda h: h.tensor_scalar_mul(out=acc_t[:, :], in0=src_t[:, 0, :], scalar1=m_t[:, side, 0:1]),
         reads=[src_b, m_b], writes=[acc_b])
    for i in range(1, nrank):
        S.op("dve", lambda h, i=i: h.scalar_tensor_tensor(out=acc_t[:, :], in0=src_t[:, i, :], scalar=m_t[:, side, i:i + 1],
                                                          in1=acc_t[:, :], op0=ALU.mult, op1=ALU.add),
             reads=[src_b, m_b, acc_b], writes=[acc_b])


def emit_even_exchange(cx, KTh, Vh, UTh, pack, packg, mlr, groups, nrank, ntok):
    S = cx.S
    cx.push()
    S.dma_dd("sp", pack[:, 0:128], KTh[:, 128:256])
    S.dma_dd("sp", pack[:, 128:256], KTh[:, ntok:ntok + 128])
    S.dma_dd("sp", pack[:, 256:288].rearrange("p (g t) -> p g t", g=4), UTh[:, :, 8:16])
    S.dma_dd("sp", pack[:, 288:320].rearrange("p (g t) -> p g t", g=4), UTh[:, :, ntok:ntok + 8])
    S.dma_dd("sp", pack[:, 320:450], Vh[128:256, :])
    S.dma_dd("sp", pack[:, 450:580], Vh[ntok:ntok + 128, :])
    allgather(cx, pack[:, :], packg[:, :], groups)
    pg_t = cx.sb("pg", [128, nrank, 580], BF16); pg_b = Buf("pg")
    m_t = cx.sb("mlr", [128, 2, nrank], F32); m_b = Buf("mlr")
    accL = cx.sb("accL", [128, 580], BF16); accL_b = Buf("accL")
    accR = cx.sb("accR", [128, 580], BF16); accR_b = Buf("accR")
    S.dma("sp", pg_t[:, :, :], packg.rearrange("(r p) n -> p r n", p=128), writes=[pg_b])
    S.dma("sp", m_t[:, :, :], mlr[:, :, :], writes=[m_b])
    emit_select(cx, pg_t, pg_b, nrank, m_t, m_b, 0, accL, accL_b)
    emit_select(cx, pg_t, pg_b, nrank, m_t, m_b, 1, accR, accR_b)
    S.dma("sp", KTh[:, 0:128], accL[:, 128:256], reads=[accL_b])
    S.dma("sp", UTh[:, :, 0:8], accL[:, 288:320].rearrange("p (g t) -> p g t", g=4), reads=[accL_b])
    S.dma("sp", Vh[0:128, :], accL[:, 450:580], reads=[accL_b])
    S.dma("sp", KTh[:, 128 + ntok:256 + ntok], accR[:, 0:128], reads=[accR_b])
    S.dma("sp", UTh[:, :, 8 + ntok:16 + ntok], accR[:, 256:288].rearrange("p (g t) -> p g t", g=4), reads=[accR_b])
    S.dma("sp", Vh[128 + ntok:256 + ntok, :], accR[:, 320:450], reads=[accR_b])
    cx.pop()


def emit_xhalo_exchange(cx, xprev, xhp, xhpg, xhalo, mlr, groups, nrank, ntok):
    S = cx.S
    cx.push()
    S.dma_dd("sp", xhp[:, 0:2], xprev[:, 0:2])
    S.dma_dd("sp", xhp[:, 2:4], xprev[:, ntok - 2:ntok])
    allgather(cx, xhp[:, :], xhpg[:, :], groups)
    xg_t = cx.sb("xg", [128, nrank, 32], F32); xg_b = Buf("xg")
    m_t = cx.sb("mlr", [128, 2, nrank], F32); m_b = Buf("mlr")
    accL = cx.sb("accL", [128, 32], F32); accL_b = Buf("accL")
    accR = cx.sb("accR", [128, 32], F32); accR_b = Buf("accR")
    for r in range(nrank):
        S.dma("sp", xg_t[:, r, :].rearrange("p (c t) -> p c t", t=4),
              xhpg[r * D:(r + 1) * D, :].rearrange("(c p) t -> p c t", p=128), writes=[xg_b])
    S.dma("sp", m_t[:, :, :], mlr[:, :, :], writes=[m_b])
    emit_select(cx, xg_t, xg_b, nrank, m_t, m_b, 0, accL, accL_b)
    emit_select(cx, xg_t, xg_b, nrank, m_t, m_b, 1, accR, accR_b)
    xhv = xhalo.rearrange("(c p) t -> p c t", p=128)
    S.dma("sp", xhv[:, :, 0:2], accL[:, :].rearrange("p (c t) -> p c t", t=4)[:, :, 2:4], reads=[accL_b])
    S.dma("sp", xhv[:, :, 2:4], accR[:, :].rearrange("p (c t) -> p c t", t=4)[:, :, 0:2], reads=[accR_b])
    cx.pop()


SMALL_SPECS = None


def build_fused(B=2, nrank=4, ntok=TOK, depth=4):
    NE, NO = (depth + 1) // 2, depth // 2
    NT = ntok // 128
    NB = ntok // 512
    groups = [[b * nrank + r for r in range(nrank)] for b in range(B)]
    cx = Ctx()
    nc = cx.nc
    I = cx.ext_in
    x0 = I("xT", [D, ntok])
    Wd = {
        "e_w_in": I("e_w_in", [NE, D, 1280]), "e_w_pool": I("e_w_pool", [NE, 4, 128, 128]), "e_w_out": I("e_w_out", [NE, D, D]),
        "o_w_in": I("o_w_in", [NO, D, 1440]), "o_w_uq": I("o_w_uq", [NO, 256, 768]), "o_w_ukv": I("o_w_ukv", [NO, 128, 1024]),
        "o_lru_wa": I("o_lru_wa", [NO, 2, 8, 64, 64]), "o_lru_wx": I("o_lru_wx", [NO, 2, 8, 64, 64]), "o_w_out": I("o_w_out", [NO, D, D]),
        "w_mlp1": I("w_mlp1", [depth, D, DFF]), "w_mlp2": I("w_mlp2", [depth, DFF, D]),
        "g_mix": I("g_mix", [depth, 128, KC]), "g_mlp": I("g_mlp", [depth, 128, KC]), "g_fin": I("g_fin", [128, KC]),
        "pscale": I("pscale", [NE, 128, 4]), "sinkrow": I("sinkrow", [NE, 1, 2, 512]),
        "g_cq": I("g_cq", [NO, 128, 2]), "g_ckv": I("g_ckv", [NO, 128, 1]), "cw": I("cw", [NO, 128, 4, 4]), "cb": I("cb", [NO, 128, 4]),
        "ba": I("ba", [NO, 128, 2, 4]), "bx": I("bx", [NO, 128, 2, 4]), "lam": I("lam", [NO, 128, 2, 4]),
        "cos32": I("cos32", [128, ntok]), "sin32": I("sin32", [128, ntok]), "cos16": I("cos16", [128, ntok]), "sin16": I("sin16", [128, ntok]),
        "rot64": I("rot64", [128, 128]), "rot32": I("rot32", [128, 128]), "masks": I("masks", [4, 128, 512]),
        "invc": I("invc", [128, 2, 4, 16]), "mfb": I("mfb", [2, 128, nrank]), "mlr": I("mlr", [128, 2, nrank]),
    }
    outT = cx.ext_out("oT", [D, ntok])

    def tmp(name, shape, dt=F32):
        return nc.dram_tensor(name, list(shape), dt, kind="Internal").ap()

    def make_precast(layer, w1b_d, w2b_d):
        def f():
            for k in range(8):
                cx.S.dma_dd_async("pool", w1b_d[k * 128:(k + 1) * 128, :], Wd["w_mlp1"][layer][k * 128:(k + 1) * 128, :])
            for k in range(8):
                cx.S.dma_dd_async("pool", w2b_d[k * 512:(k + 1) * 512, :], Wd["w_mlp2"][layer][k * 512:(k + 1) * 512, :])
        return f

    xcur = x0
    for layer in range(depth):
        L = f"L{layer}"
        xmix = tmp(L + "_xmix", [D, ntok])
        w1b_d = tmp(L + "_w1b", [D, DFF], BF16)
        w2b_d = tmp(L + "_w2b", [DFF, D], BF16)
        if layer % 2 == 0:
            e = layer // 2
            QsT = tmp(L + "_QsT", [128, 4, ntok], BF16)
            KTh = tmp(L + "_KTh", [128, ntok + 256], BF16)
            Vh = tmp(L + "_Vh", [ntok + 256, 130], BF16)
            UTh = tmp(L + "_UTh", [128, 4, ntok + 16], BF16)
            pack = tmp(L + "_pack", [128, 580], BF16)
            packg = tmp(L + "_packg", [nrank * 128, 580], BF16)
            cx.bind = {"xT": xcur, "w_in": Wd["e_w_in"][e], "g": Wd["g_mix"][layer], "cos": Wd["cos32"], "sin": Wd["sin32"],
                       "rot": Wd["rot64"], "QsT": QsT, "KT": KTh[:, 128:128 + ntok], "Vaug": Vh[128:128 + ntok, :],
                       "UT": UTh[:, :, 8:8 + ntok]}
            build_ea(ntok, cx=cx)
            emit_even_exchange(cx, KTh, Vh, UTh, pack, packg, Wd["mlr"], groups, nrank, ntok)
            cx.bind = {"QsT": QsT, "KTh": KTh, "Vh": Vh, "UTh": UTh, "xT": xcur, "w_pool": Wd["e_w_pool"][e],
                       "pscale": Wd["pscale"][e], "w_out": Wd["e_w_out"][e], "sinkrow": Wd["sinkrow"][e], "masks": Wd["masks"],
                       "invc": Wd["invc"], "oT": xmix}
            cx.hook = make_precast(layer, w1b_d, w2b_d)
            build_eb(ntok, cx=cx)
        else:
            o = layer // 2
            xhp = tmp(L + "_xhp", [D, 4]); xhpg = tmp(L + "_xhpg", [nrank * D, 4]); xhalo = tmp(L + "_xhalo", [D, 4])
            QN = tmp(L + "_QN", [512, ntok], BF16); QR = tmp(L + "_QR", [256, ntok], BF16)
            KNR = tmp(L + "_KNR", [544, ntok], BF16); V5 = tmp(L + "_V5", [1024, NT * 65], BF16)
            GX = tmp(L + "_GX", [512, ntok], BF16); AB = tmp(L + "_AB", [2, 2, 512, ntok])
            BLK = tmp(L + "_BLK", [128, NB, 2, 2, 4]); CAB = tmp(L + "_CAB", [128, 16]); XR = tmp(L + "_XR", [512, ntok])
            KNg = tmp(L + "_KNg", [8 * nrank * 64, ntok], BF16); KRg = tmp(L + "_KRg", [nrank * 32, ntok], BF16)
            Vg = tmp(L + "_Vg", [8 * nrank * 128, NT * 65], BF16)
            CABg = tmp(L + "_CABg", [nrank * 128, 16]); YC = tmp(L + "_YC", [512, ntok], BF16)
            emit_xhalo_exchange(cx, xcur, xhp, xhpg, xhalo, Wd["mlr"], groups, nrank, ntok)
            cx.bind = {"xT": xcur, "xhalo": xhalo, "w_in": Wd["o_w_in"][o], "g": Wd["g_mix"][layer], "g_cq": Wd["g_cq"][o],
                       "g_ckv": Wd["g_ckv"][o], "w_uq": Wd["o_w_uq"][o], "w_ukv": Wd["o_w_ukv"][o], "cw": Wd["cw"][o], "cb": Wd["cb"][o],
                       "wa": Wd["o_lru_wa"][o], "wx": Wd["o_lru_wx"][o], "ba": Wd["ba"][o], "bx": Wd["bx"][o], "lam": Wd["lam"][o],
                       "cos": Wd["cos16"], "sin": Wd["sin16"], "rot": Wd["rot32"], "QN": QN, "QR": QR, "KNR": KNR, "V5": V5, "GX": GX,
                       "AB": AB, "BLK": BLK, "CAB": CAB.rearrange("p (d s c) -> p d s c", d=2, s=2), "XR": XR}
            ccsem = cx.S.new_sem("ccg")
            ncc = [0]

            def gather_kv():
                pairs = []
                for hd in range(8):
                    pairs.append((KNR[hd * 64:(hd + 1) * 64, :], KNg[hd * nrank * 64:(hd + 1) * nrank * 64, :]))
                    pairs.append((V5[hd * 128:(hd + 1) * 128, :], Vg[hd * nrank * 128:(hd + 1) * nrank * 128, :]))
                pairs.append((KNR[512:544, :], KRg[:, :]))
                for (i_ap, o_ap) in pairs:
                    nc.gpsimd.collective_compute("AllGather", ALU.bypass, replica_groups=groups, ins=[i_ap], outs=[o_ap]).then_inc(ccsem, 1)
                    ncc[0] += 1

            build_oa(ntok, cx=cx, mid_hook=gather_kv)
            nc.gpsimd.collective_compute("AllGather", ALU.bypass, replica_groups=groups, ins=[CAB[:, :]], outs=[CABg[:, :]]).then_inc(ccsem, 1)
            ncc[0] += 1
            for e_ in cx.S.ENGS:
                cx.S.h[e_].wait_ge(ccsem, ncc[0])
            cx.bind = {"QN": QN, "QR": QR, "KNg": KNg, "KRg": KRg, "Vg": Vg, "YC": YC}
            cx.hook = make_precast(layer, w1b_d, w2b_d)
            build_ob1(ntok, nrank, cx=cx)
            cx.bind = {"AB": AB, "GX": GX, "YC": YC, "xT": xcur, "w_out": Wd["o_w_out"][o], "BLK": BLK,
                       "CABg": CABg.rearrange("(r p) n -> p r n", p=128), "mf": Wd["mfb"][0], "mb": Wd["mfb"][1], "oT": xmix}
            build_ob2(ntok, nrank, cx=cx)
        last = layer == depth - 1
        xnext = outT if last else tmp(L + "_xmlp", [D, ntok])
        cx.bind = {"xT": xmix, "w1": w1b_d, "w2": w2b_d, "g": Wd["g_mlp"][layer], "gf": Wd["g_fin"], "oT": xnext}
        build_mlp(last, ntok, cx=cx, wbf16=True)
        xcur = xnext
    cx.bind = {}
    return cx.finish()


_FUSED = {}


def run_model(x, W, nrank=4, ntok=TOK):
    B, Sq, _ = x.shape
    ncore = B * nrank
    assert Sq == nrank * ntok
    depth = W["norm_mlp"].shape[0]
    NE, NO = (depth + 1) // 2, depth // 2
    key = (B, nrank, ntok, depth)
    if key not in _FUSED:
        _FUSED[key] = build_fused(B, nrank, ntok, depth)
    nc = _FUSED[key]
    f32 = lambda a: np.ascontiguousarray(np.asarray(a, np.float32))
    g_mix = np.stack([vec128(W["e_norm_mix"][l // 2] if l % 2 == 0 else W["o_norm_mix"][l // 2], 8) for l in range(depth)])
    shared = {
        "e_w_in": f32(W["e_w_in"]), "e_w_pool": f32(W["e_w_pool"]), "e_w_out": f32(W["e_w_out"]),
        "o_w_in": f32(W["o_w_in"]), "o_w_uq": f32(W["o_w_uq"]), "o_w_ukv": f32(W["o_w_ukv"]),
        "o_lru_wa": f32(W["o_lru_wa"]), "o_lru_wx": f32(W["o_lru_wx"]), "o_w_out": f32(W["o_w_out"]),
        "w_mlp1": f32(W["w_mlp1"]), "w_mlp2": f32(W["w_mlp2"]),
        "g_mix": g_mix, "g_mlp": np.stack([vec128(W["norm_mlp"][l], 8) for l in range(depth)]), "g_fin": vec128(W["final_norm"], 8),
        "pscale": np.stack([vec128(W["e_pool_scale"][e], 4) for e in range(NE)]),
        "sinkrow": np.stack([np.repeat(f32(W["e_sink"][e]).reshape(2, 4), 128, axis=1).reshape(1, 2, 512) for e in range(NE)]),
        "g_cq": np.stack([vec128(W["o_g_cq"][o], 2) for o in range(NO)]),
        "g_ckv": np.stack([vec128(W["o_g_ckv"][o], 1) for o in range(NO)]),
        "cw": np.stack([f32(f32(W["o_conv_w"][o]).reshape(4, 4, 128).transpose(2, 1, 0)) for o in range(NO)]),
        "cb": np.stack([chunk_vec(W["o_conv_b"][o], 4) for o in range(NO)]),
        "ba": np.stack([f32(f32(W["o_lru_ba"][o]).reshape(2, 4, 128).transpose(2, 0, 1)) for o in range(NO)]),
        "bx": np.stack([f32(f32(W["o_lru_bx"][o]).reshape(2, 4, 128).transpose(2, 0, 1)) for o in range(NO)]),
        "lam": np.stack([f32(f32(W["o_lru_lambda"][o]).reshape(2, 4, 128).transpose(2, 0, 1)) for o in range(NO)]),
        "rot64": rot_matrix(64), "rot32": rot_matrix(32),
    }
    in_maps = []
    for c in range(ncore):
        bi, r = c // nrank, c % nrank
        pos = r * ntok + np.arange(ntok)
        cos32, sin32 = rope_tables(pos, 32, 128)
        cos16, sin16 = rope_tables(pos, 16, 128)
        mfb = np.zeros((2, 128, nrank), np.float32); mfb[0, :, :r] = 1.0; mfb[1, :, r + 1:] = 1.0
        mlr = np.zeros((128, 2, nrank), np.float32)
        if r > 0:
            mlr[:, 0, r - 1] = 1.0
        if r < nrank - 1:
            mlr[:, 1, r + 1] = 1.0
        im = dict(shared)
        im.update({"xT": np.ascontiguousarray(x[bi, r * ntok:(r + 1) * ntok, :].T), "cos32": cos32, "sin32": sin32, "cos16": cos16,
                   "sin16": sin16, "masks": eb_masks(r > 0, r < nrank - 1), "invc": eb_invc(r == 0, r == nrank - 1),
                   "mfb": mfb, "mlr": mlr})
        in_maps.append(im)
    res = run_spmd(nc, in_maps)
    out = np.empty((B, Sq, D), np.float32)
    for c in range(ncore):
        bi, r = c // nrank, c % nrank
        out[bi, r * ntok:(r + 1) * ntok, :] = res[c]["oT"].T
    return out


def kernel(**inputs):
    W = {k: np.asarray(v) for k, v in inputs.items()}
    x = np.asarray(W.pop("x"), np.float32)
    return run_model(x, W)
```

```python
from contextlib import ExitStack
import numpy as np
import concourse.bass as bass
import concourse.mybir as mybir
from concourse.bass_utils import run_bass_kernel_spmd

F32 = mybir.dt.float32
BF16 = mybir.dt.bfloat16
ALU = mybir.AluOpType
AF = mybir.ActivationFunctionType

NCORES = 8
D = 1024
KC = 8
TOK = 4096
SEQ = 16384
EPS = 1e-6
DFF = 4096
EPOCH = 30000


class Buf:
    __slots__ = ("name", "writers", "readers", "sem_in", "sem_out", "n_in", "n_out", "excl")

    def __init__(self, name, excl=False):
        self.name = name
        self.excl = excl
        self.writers = {}
        self.readers = {}
        self.sem_in = None
        self.sem_out = None
        self.n_in = 0
        self.n_out = 0


class Sched:
    ENGS = ("pe", "act", "dve", "pool", "sp")

    def __init__(self, nc, stack):
        self.nc = nc
        self.stack = stack
        self.h = {"pe": nc.tensor, "act": nc.scalar, "dve": nc.vector, "pool": nc.gpsimd, "sp": nc.sync}
        self.ops = {e: [] for e in self.ENGS}
        self.cnt = {e: 0 for e in self.ENGS}
        self.sem = {e: None for e in self.ENGS}
        self.seen = {e: {} for e in self.ENGS}
        self.last = {e: None for e in self.ENGS}
        self.dma_toks = {}
        self.nsem = 0
        self.ninstr = 0
        self.sem_pool = []
        self.live = []
        self.ddbuf = Buf("dram2dram")

    def new_sem(self, name):
        self.nsem += 1
        return self.stack.enter_context(self.nc.semaphore(f"{name}_{self.nsem}"))

    def _eng_tok(self, e):
        if self.sem[e] is None or self.cnt[e] >= EPOCH:
            self.sem[e] = self.new_sem("e" + e)
            self.cnt[e] = 0
        self.cnt[e] += 1
        tok = (self.sem[e], self.cnt[e])
        self.last[e] = tok
        return tok

    def _waits(self, e, toks):
        need = {}
        seen = self.seen[e]
        for sem, val in toks:
            k = id(sem)
            if seen.get(k, 0) >= val:
                continue
            if k not in need or need[k][1] < val:
                need[k] = (sem, val)
        out = []
        for k, (sem, val) in need.items():
            seen[k] = val
            out.append((sem, val))
        return out

    def _deps(self, e, reads, writes):
        toks = []
        for b in reads:
            toks.extend(b.writers.values())
            if b.excl:
                toks.extend(b.readers.values())
        for b in writes:
            toks.extend(b.writers.values())
            toks.extend(b.readers.values())
        if e == "pe":
            own = id(self.sem["pe"]) if self.sem["pe"] is not None else None
            toks = [t for t in toks if id(t[0]) != own]
        return self._waits(e, toks)

    def op(self, e, fn, reads=(), writes=()):
        waits = self._deps(e, reads, writes)
        tok = self._eng_tok(e)
        for b in reads:
            b.readers[id(tok[0])] = tok
        for b in writes:
            b.readers = {}
            b.writers = {id(tok[0]): tok}
        self.ninstr += 1

        h = self.h[e]
        for sem, val in waits:
            h.wait_ge(sem, val)
        fn(h).then_inc(tok[0], 1)

    def dma(self, q, out_ap, in_ap, reads=(), writes=(), **kw):
        waits = self._deps(q, reads, writes)
        assert len(writes) + len(reads) >= 1 and len(writes) <= 1 and len(reads) <= 1
        if writes:
            b = writes[0]
            if b.sem_in is None:
                b.sem_in, b.n_in = self._take_sem("di")
                self.live.append((b, "in"))
            b.n_in += 16
            tok = (b.sem_in, b.n_in)
            b.readers = {}
            b.writers = {id(tok[0]): tok}
            for rb in reads:
                rb.readers[id(tok[0])] = tok
        else:
            b = reads[0]
            if b.sem_out is None:
                b.sem_out, b.n_out = self._take_sem("do")
                self.live.append((b, "out"))
            b.n_out += 16
            tok = (b.sem_out, b.n_out)
            b.readers[id(tok[0])] = tok
        self.dma_toks[id(tok[0])] = tok
        self.ninstr += 1
        h = self.h[q]
        for sem, val in waits:
            h.wait_ge(sem, val)
        h.dma_start(out=out_ap, in_=in_ap, **kw).then_inc(tok[0], 16)

    def _take_sem(self, name):
        if self.sem_pool:
            return self.sem_pool.pop()
        return self.new_sem(name), 0

    def release_dma_sems(self):
        for b, kind in self.live:
            if kind == "in":
                self.sem_pool.append((b.sem_in, b.n_in)); b.sem_in = None
                b.writers = {}
            else:
                self.sem_pool.append((b.sem_out, b.n_out)); b.sem_out = None
                b.readers = {}
        self.live = []

    def dma_dd(self, q, out_ap, in_ap, **kw):
        self.dma(q, out_ap, in_ap, writes=[self.ddbuf], **kw)

    def dma_dd_async(self, q, out_ap, in_ap, **kw):
        self.dma(q, out_ap, in_ap, writes=[Buf("dd_async")], **kw)

    def barrier(self):
        toks = [t for t in self.last.values() if t is not None] + list(self.dma_toks.values())
        for e in self.ENGS:
            waits = self._waits(e, toks)
            for sem, val in waits:
                self.h[e].wait_ge(sem, val)

    def finalize(self):
        self.barrier()


class Ctx:
    def __init__(self):
        self.nc = bass.Bass("TRN2", target_bir_lowering=False)
        self.stack = ExitStack()
        self.S = Sched(self.nc, self.stack)
        self.n = 0
        self.cur = self.stack
        self.scopes = []
        self.bind = {}
        self.hook = None

    def dram_in(self, name, shape, dt=F32):
        if name in self.bind:
            return self.bind[name]
        return self.nc.dram_tensor(name, list(shape), dt, kind="ExternalInput").ap()

    def dram_out(self, name, shape, dt=F32):
        if name in self.bind:
            return self.bind[name]
        return self.nc.dram_tensor(name, list(shape), dt, kind="ExternalOutput").ap()

    def ext_in(self, name, shape, dt=F32):
        return self.nc.dram_tensor(name, list(shape), dt, kind="ExternalInput").ap()

    def ext_out(self, name, shape, dt=F32):
        return self.nc.dram_tensor(name, list(shape), dt, kind="ExternalOutput").ap()

    def sb(self, name, shape, dt):
        self.n += 1
        return self.cur.enter_context(self.nc.sbuf_tensor(f"{name}_{self.n}", list(shape), dt))

    def ps(self, name, shape, dt=F32):
        self.n += 1
        return self.cur.enter_context(self.nc.psum_tensor(f"{name}_{self.n}", list(shape), dt))

    def dram_tmp(self, name, shape, dt=F32):
        return self.nc.dram_tensor(name, list(shape), dt, kind="Internal").ap()

    def run_hook(self):
        if self.hook is not None:
            f, self.hook = self.hook, None
            f()

    def push(self):
        st = ExitStack()
        self.scopes.append(st)
        self.cur = st

    def pop(self):
        self.S.barrier()
        if len(self.scopes) == 1:
            self.S.release_dma_sems()
        self.scopes.pop().close()
        self.cur = self.scopes[-1] if self.scopes else self.stack

    def finish(self):
        self.S.finalize()
        self.stack.close()
        return self.nc


class Ring:
    def __init__(self, items):
        self.items = items
        self.i = 0

    def next(self):
        it = self.items[self.i % len(self.items)]
        self.i += 1
        return it


def run_interleaved(gens, width=2, stagger=2):
    it = iter(gens)
    active = []
    steps = 0
    while True:
        while len(active) < width and (not active or steps >= stagger):
            try:
                active.append(next(it))
            except StopIteration:
                break
        if not active:
            break
        steps += 1
        for g in list(active):
            try:
                next(g)
            except StopIteration:
                active.remove(g)


def mk_ring(cx, kind, name, n, shape, dt):
    items = []
    for i in range(n):
        t = cx.sb(f"{name}{i}", shape, dt) if kind == "sb" else cx.ps(f"{name}{i}", shape, dt)
        items.append((t, Buf(f"{name}{i}", excl=(kind == "ps"))))
    return Ring(items)


def emit_rmsnorm(cx, x_t, x_b, nchunk, TB, g_t, g_b, ones_t, ones_b, sq_ring, st_ring, rstd_ring,
                 out_t, out_b, nfeat, evac_engs=("dve",)):
    S = cx.S
    st_t, st_b = st_ring.next()
    for c in range(nchunk):
        sq_t, sq_b = sq_ring.next()
        S.op("act", lambda h, c=c, sq_t=sq_t: h.activation(out=sq_t[:, 0:TB], in_=x_t[:, c, 0:TB], func=AF.Square),
             reads=[x_b], writes=[sq_b])
        S.op("pe", lambda h, c=c, sq_t=sq_t: h.matmul(st_t[:, 0:TB], lhsT=ones_t[:, :], rhs=sq_t[:, 0:TB],
                                                        start=(c == 0), stop=(c == nchunk - 1)),
             reads=[sq_b, ones_b], writes=[st_b])
    r_t, r_b = rstd_ring.next()
    S.op("act", lambda h: h.activation(out=r_t[:, 0:TB], in_=st_t[:, 0:TB], func=AF.Sqrt, bias=float(nfeat * EPS)),
         reads=[st_b], writes=[r_b])
    S.op("dve", lambda h: h.reciprocal(out=r_t[:, 0:TB], in_=r_t[:, 0:TB]), reads=[r_b], writes=[r_b])
    for c in range(nchunk):
        e = evac_engs[c % len(evac_engs)]
        S.op(e, lambda h, c=c: h.scalar_tensor_tensor(out=out_t[:, c, 0:TB], in0=x_t[:, c, 0:TB],
                                                       scalar=g_t[:, c:c + 1], in1=r_t[:, 0:TB],
                                                       op0=ALU.mult, op1=ALU.mult),
             reads=[x_b, r_b, g_b], writes=[out_b])


def build_mlp(final_norm, ntok=TOK, dbg=False, cx=None, wbf16=False):
    TB = 256
    NB = ntok // TB
    FC = DFF // 128
    own = cx is None
    cx = Ctx() if own else cx
    cx.push()
    S = cx.S
    xT = cx.dram_in("xT", [D, ntok])
    w1 = cx.dram_in("w1", [D, DFF], BF16 if wbf16 else F32)
    w2 = cx.dram_in("w2", [DFF, D], BF16 if wbf16 else F32)
    gin = cx.dram_in("g", [128, KC])
    oT = cx.dram_out("oT", [D, ntok])
    if final_norm:
        gfin = cx.dram_in("gf", [128, KC])
    if dbg:
        dh = cx.dram_out("dh", [128, KC, TB], BF16)
        da = cx.dram_out("da", [128, DFF // 128, TB], BF16)

    w1b = cx.sb("w1b", [128, KC, DFF], BF16)
    w2b = cx.sb("w2b", [128, FC, D], BF16)
    w1_bufs = [Buf(f"w1_{k}") for k in range(KC)]
    w2_bufs = [Buf(f"w2_{k}") for k in range(8)]
    g_t = cx.sb("g", [128, KC], F32); g_b = Buf("g")
    ones_t = cx.sb("ones", [128, 128], BF16); ones_b = Buf("ones")
    x_ring = mk_ring(cx, "sb", "x", 2, [128, KC, TB], F32)
    h_ring = mk_ring(cx, "sb", "h", 2, [128, KC, TB], BF16)
    a_ring = mk_ring(cx, "sb", "a", 1, [128, FC, TB], BF16)
    r_ring = mk_ring(cx, "sb", "r", 3, [128, TB], BF16)
    sq_ring = mk_ring(cx, "sb", "sq", 3, [128, TB], BF16)
    rstd_ring = mk_ring(cx, "sb", "rstd", 2, [128, TB], F32)
    o_ring = mk_ring(cx, "sb", "o", 2, [128, KC, TB], F32)
    st_ring = mk_ring(cx, "ps", "st", 1, [128, 512], F32)
    p1_ring = mk_ring(cx, "ps", "p1", 3, [128, 512], F32)
    p2_ring = mk_ring(cx, "ps", "p2", 3, [128, 512], F32)
    if final_norm:
        gf_t = cx.sb("gf", [128, KC], F32); gf_b = Buf("gf")
        f_ring = mk_ring(cx, "sb", "f", 2, [128, KC, TB], F32)

    S.dma("sp", g_t[:, :], gin[:, :], writes=[g_b])
    S.op("dve", lambda h: h.tensor_scalar_mul(out=g_t[:, :], in0=g_t[:, :], scalar1=float(np.sqrt(D))),
         reads=[g_b], writes=[g_b])
    if final_norm:
        S.dma("sp", gf_t[:, :], gfin[:, :], writes=[gf_b])
        S.op("dve", lambda h: h.tensor_scalar_mul(out=gf_t[:, :], in0=gf_t[:, :], scalar1=float(np.sqrt(D))),
             reads=[gf_b], writes=[gf_b])
    S.op("pool", lambda h: h.memset(ones_t[:, :], 1.0), writes=[ones_b])
    w1v = w1.rearrange("(k p) n -> p k n", p=128)
    w2v = w2.rearrange("(f p) n -> p f n", p=128)
    wq = "sp" if wbf16 else "pool"
    for k in range(KC):
        S.dma(wq, w1b[:, k, :], w1v[:, k, :], writes=[w1_bufs[k]])
    for j in range(8):
        S.dma(wq, w2b[:, j * 4:(j + 1) * 4, :], w2v[:, j * 4:(j + 1) * 4, :], writes=[w2_bufs[j]])
    xv = xT.rearrange("(c p) t -> p c t", p=128)
    ov = oT.rearrange("(c p) t -> p c t", p=128)

    def prep(b):
        x_t, x_b = x_ring.next()
        S.dma("sp", x_t[:, :, :], xv[:, :, b * TB:(b + 1) * TB], writes=[x_b])
        h_t, h_b = h_ring.next()
        return (x_t, x_b, h_t, h_b)

    def norm(st_):
        x_t, x_b, h_t, h_b = st_
        emit_rmsnorm(cx, x_t, x_b, KC, TB, g_t, g_b, ones_t, ones_b, sq_ring, st_ring, rstd_ring, h_t, h_b, D)

    cur = prep(0)
    norm(cur)
    for b in range(NB):
        t0 = b * TB
        x_t, x_b, h_t, h_b = cur
        nxt = prep(b + 1) if b + 1 < NB else None
        a_t, a_b = a_ring.next()
        for f in range(FC):
            if f == FC // 2 and nxt is not None:
                norm(nxt)
            p_t, p_b = p1_ring.next()
            for k in range(KC):
                S.op("pe", lambda h, f=f, k=k, p_t=p_t: h.matmul(p_t[:, 0:TB], lhsT=w1b[:, k, f * 128:(f + 1) * 128],
                                                                  rhs=h_t[:, k, 0:TB], start=(k == 0), stop=(k == KC - 1)),
                     reads=[h_b, w1_bufs[k]], writes=[p_b])
            r_t, r_b = r_ring.next()
            S.op("act", lambda h, p_t=p_t, r_t=r_t: h.activation(out=r_t[:, 0:TB], in_=p_t[:, 0:TB], func=AF.Relu),
                 reads=[p_b], writes=[r_b])
            S.op("pool", lambda h, f=f, r_t=r_t: h.tensor_tensor(out=a_t[:, f, 0:TB], in0=r_t[:, 0:TB], in1=r_t[:, 0:TB],
                                                                  op=ALU.mult),
                 reads=[r_b], writes=[a_b])
        if dbg and b == 0:
            S.dma("sp", dh[:, :, :], h_t[:, :, :], reads=[h_b])
            S.dma("sp", da[:, :, :], a_t[:, :, :], reads=[a_b])
        o_t, o_b = o_ring.next()
        for c in range(KC):
            p_t, p_b = p2_ring.next()
            for f in range(FC):
                S.op("pe", lambda h, f=f, c=c, p_t=p_t: h.matmul(p_t[:, 0:TB], lhsT=w2b[:, f, c * 128:(c + 1) * 128],
                                                                  rhs=a_t[:, f, 0:TB], start=(f == 0), stop=(f == FC - 1)),
                     reads=[a_b, w2_bufs[f // 4]], writes=[p_b])
            S.op("dve", lambda h, c=c, p_t=p_t: h.tensor_tensor(out=o_t[:, c, 0:TB], in0=p_t[:, 0:TB], in1=x_t[:, c, 0:TB],
                                                                 op=ALU.add),
                 reads=[p_b, x_b], writes=[o_b])
        if final_norm:
            f_t, f_b = f_ring.next()
            emit_rmsnorm(cx, o_t, o_b, KC, TB, gf_t, gf_b, ones_t, ones_b, sq_ring, st_ring, rstd_ring, f_t, f_b, D)
            S.dma("pool", ov[:, :, t0:t0 + TB], f_t[:, :, :], reads=[f_b])
        else:
            S.dma("pool", ov[:, :, t0:t0 + TB], o_t[:, :, :], reads=[o_b])
        cur = nxt
    cx.pop()
    return cx.finish() if own else None


def run_spmd(nc, in_maps):
    res = run_bass_kernel_spmd(nc, in_maps, core_ids=list(range(len(in_maps))))
    return res.results


def vec128(v, k):
    return np.ascontiguousarray(np.asarray(v, np.float32).reshape(k, 128).T)


def load_cast(cx, q, dst_ap, src_ap, buf):
    cx.S.dma(q, dst_ap, src_ap, writes=[buf])


def build_ea(ntok=TOK, parts='quv', qlvl=4, cx=None):
    TB = 512
    NB = ntok // TB
    own = cx is None
    cx = Ctx() if own else cx
    cx.push()
    S = cx.S
    xT = cx.dram_in("xT", [D, ntok])
    w_in = cx.dram_in("w_in", [D, 1280])
    gin = cx.dram_in("g", [128, KC])
    cosd = cx.dram_in("cos", [128, ntok])
    sind = cx.dram_in("sin", [128, ntok])
    rotd = cx.dram_in("rot", [128, 128])
    QsT = cx.dram_out("QsT", [128, 4, ntok], BF16)
    KT = cx.dram_out("KT", [128, ntok], BF16)
    Vaug = cx.dram_out("Vaug", [ntok, 130], BF16)
    UT = cx.dram_out("UT", [128, 4, ntok], BF16)

    wb = cx.sb("wb", [128, KC, 1280], BF16)
    w_bufs = [Buf(f"w{k}") for k in range(KC)]
    g_t = cx.sb("g", [128, KC], F32); g_b = Buf("g")
    ones_t = cx.sb("ones", [128, 128], BF16); ones_b = Buf("ones")
    rot_t = cx.sb("rot", [128, 128], BF16); rot_b = Buf("rot")
    x_ring = mk_ring(cx, "sb", "x", 2, [128, KC, TB], F32)
    h_ring = mk_ring(cx, "sb", "h", 2, [128, KC, TB], BF16)
    sq_ring = mk_ring(cx, "sb", "sq", 3, [128, TB], BF16)
    rstd_ring = mk_ring(cx, "sb", "rstd", 2, [128, TB], F32)
    cos_ring = mk_ring(cx, "sb", "cos", 2, [128, TB], F32)
    sin_ring = mk_ring(cx, "sb", "sin", 2, [128, TB], F32)
    qb_ring = mk_ring(cx, "sb", "qb", 2, [128, TB], BF16)
    t1_ring = mk_ring(cx, "sb", "t1", 2, [128, TB], F32)
    t2_ring = mk_ring(cx, "sb", "t2", 2, [128, TB], F32)
    qo_ring = mk_ring(cx, "sb", "qo", 2, [128, 5, TB], BF16)
    uo_ring = mk_ring(cx, "sb", "uo", 2, [128, 4, TB], BF16)
    vo_ring = mk_ring(cx, "sb", "vo", 2, [128, 4, 130], BF16)
    st_ring = mk_ring(cx, "ps", "st", 1, [128, 512], F32)
    pq_ring = mk_ring(cx, "ps", "pq", 3, [128, 512], F32)
    pr_ring = mk_ring(cx, "ps", "pr", 2, [128, 512], F32)
    pv_ring = mk_ring(cx, "ps", "pv", 2, [128, 512], F32)

    S.dma("sp", g_t[:, :], gin[:, :], writes=[g_b])
    S.op("dve", lambda h: h.tensor_scalar_mul(out=g_t[:, :], in0=g_t[:, :], scalar1=float(np.sqrt(D))),
         reads=[g_b], writes=[g_b])
    S.op("pool", lambda h: h.memset(ones_t[:, :], 1.0), writes=[ones_b])
    S.dma("pool", rot_t[:, :], rotd[:, :], writes=[rot_b])
    for (vt, vb) in vo_ring.items:
        S.op("pool", lambda h, vt=vt: h.memset(vt[:, :, :], 1.0), writes=[vb])
    for k in range(KC):
        for j in range(2):
            src = w_in[k * 128:(k + 1) * 128, j * 256:(j + 1) * 256].rearrange("p (c d) -> p c d", c=4, d=64)
            dst = wb[:, k, 0:512].rearrange("p (c j d) -> p c j d", c=4, j=2, d=64)[:, :, j, :]
            S.dma("pool", dst, src, writes=[w_bufs[k]])
        S.dma("pool", wb[:, k, 512:1280], w_in[k * 128:(k + 1) * 128, 512:1280], writes=[w_bufs[k]])
    xv = xT.rearrange("(c p) t -> p c t", p=128)

    def blk(b):
        t0 = b * TB
        x_t, x_b = x_ring.next()
        S.dma("sp", x_t[:, :, :], xv[:, :, t0:t0 + TB], writes=[x_b])
        cos_t, cos_b = cos_ring.next()
        sin_t, sin_b = sin_ring.next()
        S.dma("sp", cos_t[:, :], cosd[:, t0:t0 + TB], writes=[cos_b])
        S.dma("sp", sin_t[:, :], sind[:, t0:t0 + TB], writes=[sin_b])
        h_t, h_b = h_ring.next()
        emit_rmsnorm(cx, x_t, x_b, KC, TB, g_t, g_b, ones_t, ones_b, sq_ring, st_ring, rstd_ring, h_t, h_b, D)
        yield
        qo_t, qo_b = qo_ring.next()
        for c in (range(5) if 'q' in parts else []):
            pq_t, pq_b = pq_ring.next()
            for k in range(KC):
                S.op("pe", lambda h, c=c, k=k, pq_t=pq_t, h_t=h_t: h.matmul(
                    pq_t[:, 0:TB], lhsT=wb[:, k, c * 128:(c + 1) * 128], rhs=h_t[:, k, :],
                    start=(k == 0), stop=(k == KC - 1)), reads=[h_b, w_bufs[k]], writes=[pq_b])
            qb_t, qb_b = qb_ring.next()
            S.op("act", lambda h, pq_t=pq_t, qb_t=qb_t: h.activation(out=qb_t[:, :], in_=pq_t[:, 0:TB], func=AF.Copy),
                 reads=[pq_b], writes=[qb_b])
            if qlvl == 1:
                S.op("act", lambda h, c=c, pq_t=pq_t, qo_t=qo_t: h.activation(out=qo_t[:, c, :], in_=pq_t[:, 0:TB], func=AF.Copy),
                     reads=[pq_b], writes=[qo_b])
                continue
            pr_t, pr_b = pr_ring.next()
            S.op("pe", lambda h, pr_t=pr_t, qb_t=qb_t: h.matmul(pr_t[:, 0:TB], lhsT=rot_t[:, :], rhs=qb_t[:, :],
                                                               start=True, stop=True),
                 reads=[qb_b, rot_b], writes=[pr_b])
            t1_t, t1_b = t1_ring.next()
            t2_t, t2_b = t2_ring.next()
            if qlvl == 2:
                S.op("act", lambda h, c=c, pr_t=pr_t, qo_t=qo_t: h.activation(out=qo_t[:, c, :], in_=pr_t[:, 0:TB], func=AF.Copy),
                     reads=[pr_b], writes=[qo_b])
                continue
            S.op("dve", lambda h, t1_t=t1_t, pq_t=pq_t, cos_t=cos_t: h.tensor_tensor(
                out=t1_t[:, :], in0=pq_t[:, 0:TB], in1=cos_t[:, :], op=ALU.mult), reads=[pq_b, cos_b], writes=[t1_b])
            if qlvl == 3:
                S.op("act", lambda h, c=c, t1_t=t1_t, qo_t=qo_t: h.activation(out=qo_t[:, c, :], in_=t1_t[:, :], func=AF.Copy),
                     reads=[t1_b], writes=[qo_b])
                continue
            S.op("dve", lambda h, t2_t=t2_t, pr_t=pr_t, sin_t=sin_t: h.tensor_tensor(
                out=t2_t[:, :], in0=pr_t[:, 0:TB], in1=sin_t[:, :], op=ALU.mult), reads=[pr_b, sin_b], writes=[t2_b])
            S.op("dve", lambda h, c=c, qo_t=qo_t, t1_t=t1_t, t2_t=t2_t: h.tensor_tensor(
                out=qo_t[:, c, :], in0=t1_t[:, :], in1=t2_t[:, :], op=ALU.add), reads=[t1_b, t2_b], writes=[qo_b])
        if 'q' in parts:
            S.dma("pool", QsT[:, :, t0:t0 + TB], qo_t[:, 0:4, :], reads=[qo_b])
            S.dma("pool", KT[:, t0:t0 + TB], qo_t[:, 4, :], reads=[qo_b])
        yield
        uo_t, uo_b = uo_ring.next()
        for gi in (range(4) if 'u' in parts else []):
            pq_t, pq_b = pq_ring.next()
            for k in range(KC):
                S.op("pe", lambda h, gi=gi, k=k, pq_t=pq_t, h_t=h_t: h.matmul(
                    pq_t[:, 0:TB], lhsT=wb[:, k, 768 + gi * 128:768 + (gi + 1) * 128], rhs=h_t[:, k, :],
                    start=(k == 0), stop=(k == KC - 1)), reads=[h_b, w_bufs[k]], writes=[pq_b])
            S.op("act", lambda h, gi=gi, pq_t=pq_t, uo_t=uo_t: h.activation(out=uo_t[:, gi, :], in_=pq_t[:, 0:TB], func=AF.Copy),
                 reads=[pq_b], writes=[uo_b])
        if 'u' in parts:
            S.dma("pool", UT[:, :, t0:t0 + TB], uo_t[:, :, :], reads=[uo_b])
        if 'v' not in parts:
            return
        yield
        vo_t, vo_b = vo_ring.next()
        pv_t, pv_b = pv_ring.next()
        for ti in range(TB // 128):
            for k in range(KC):
                S.op("pe", lambda h, ti=ti, k=k, pv_t=pv_t, h_t=h_t: h.matmul(
                    pv_t[:, ti * 128:(ti + 1) * 128], lhsT=h_t[:, k, ti * 128:(ti + 1) * 128], rhs=wb[:, k, 640:768],
                    start=(k == 0), stop=(k == KC - 1)), reads=[h_b, w_bufs[k]], writes=[pv_b])
        for ti in range(TB // 128):
            for j in range(2):
                S.op("act", lambda h, ti=ti, j=j, pv_t=pv_t, vo_t=vo_t: h.activation(
                    out=vo_t[:, ti, j * 65:j * 65 + 64], in_=pv_t[:, ti * 128 + j * 64:ti * 128 + (j + 1) * 64], func=AF.Copy),
                    reads=[pv_b], writes=[vo_b])
        S.dma("pool", Vaug[t0:t0 + TB, :].rearrange("(i p) n -> p i n", p=128), vo_t[:, :, :], reads=[vo_b])
        yield

    run_interleaved((blk(b) for b in range(NB)), 2, 2)
    cx.pop()
    return cx.finish() if own else None


def rope_tables(pos, half, nrows):
    inv = (np.float32(10000.0) ** (-np.arange(half, dtype=np.float32) / np.float32(half))).astype(np.float32)
    ang = pos.astype(np.float32)[None, :] * inv[np.arange(nrows) % half][:, None]
    return np.cos(ang).astype(np.float32), np.sin(ang).astype(np.float32)


def rot_matrix(dh, nrows=128):
    R = np.zeros((nrows, nrows), np.float32)
    half = dh // 2
    for m in range(nrows):
        d = m % dh
        base = m - d
        if d < half:
            R[base + d + half, m] = -1.0
        else:
            R[base + d - half, m] = 1.0
    return R


def build_eb(ntok=TOK, cx=None):
    TB = 512
    NB = ntok // TB
    NT = ntok // 128
    own = cx is None
    cx = Ctx() if own else cx
    cx.push()
    S = cx.S
    QsT = cx.dram_in("QsT", [128, 4, ntok], BF16)
    KTh = cx.dram_in("KTh", [128, ntok + 256], BF16)
    Vh = cx.dram_in("Vh", [ntok + 256, 130], BF16)
    UTh = cx.dram_in("UTh", [128, 4, ntok + 16], BF16)
    xT = cx.dram_in("xT", [D, ntok])
    w_pool = cx.dram_in("w_pool", [4, 128, 128])
    pscale = cx.dram_in("pscale", [128, 4])
    w_out = cx.dram_in("w_out", [D, D])
    sinkrow = cx.dram_in("sinkrow", [1, 2, 512])
    masksd = cx.dram_in("masks", [4, 128, 512])
    invcd = cx.dram_in("invc", [128, 2, 4, 16])
    identd = cx.dram_in("ident", [128, 128])
    oT = cx.dram_out("oT", [D, ntok])

    woA = cx.sb("woA", [128, 4, D], BF16); woA_b = Buf("woA")
    woB = cx.sb("woB", [128, 4, D], BF16); woB_b = Buf("woB")
    wp = cx.sb("wp", [128, 4, 128], BF16); wp_b = Buf("wp")
    ps_t = cx.sb("ps", [128, 4], F32); ps_b = Buf("ps")
    mk_t = cx.sb("mk", [128, 4, 512], BF16); mk_b = Buf("mk")
    id_t = cx.sb("ident", [128, 128], BF16); id_b = Buf("ident")
    invc_t = cx.sb("invc", [128, 2, 4, 16], F32); invc_b = Buf("invc")
    sk_t = cx.sb("sk", [1, 2, 512], F32); sk_b = Buf("sk")
    esk_t = cx.sb("esk", [1, 2, 512], BF16); esk_b = Buf("esk")
    sel_t = cx.sb("sel", [1, 128], BF16); sel_b = Buf("sel")
    ones32 = cx.sb("ones32", [128, 64], F32); ones32_b = Buf("ones32")
    qsA_ring = mk_ring(cx, "sb", "qsA", 2, [128, 4, TB], BF16)
    qsB_ring = mk_ring(cx, "sb", "qsB", 2, [128, 4, TB], BF16)
    kt_ring = mk_ring(cx, "sb", "kt", 2, [128, 6 * 128], BF16)
    v_ring = mk_ring(cx, "sb", "v", 2, [128, 6, 130], BF16)
    u_ring = mk_ring(cx, "sb", "u", 2, [128, 4, TB + 16], BF16)
    x_ring = mk_ring(cx, "sb", "x", 2, [128, KC, TB], F32)
    p_ring = mk_ring(cx, "sb", "p", 4, [128, 512], BF16)
    osb_ring = mk_ring(cx, "sb", "osb", 3, [128, 512], F32)
    rc_ring = mk_ring(cx, "sb", "rc", 3, [128, 512], F32)
    ya_ring = mk_ring(cx, "sb", "ya", 2, [64, 8, TB], BF16)
    yp_ring = mk_ring(cx, "sb", "yp", 2, [128, 4, TB], BF16)
    yb_ring = mk_ring(cx, "sb", "yb", 2, [128, 4, TB], BF16)
    d_ring = mk_ring(cx, "sb", "d", 2, [128, 4, TB], BF16)
    tmp_rings = [mk_ring(cx, "sb", f"tp{g}", 2, [128, TB + 16], F32) for g in range(4)]
    e16_ring = mk_ring(cx, "sb", "e16", 2, [128, 16], F32)
    s_ring = mk_ring(cx, "ps", "s", 3, [128, 512], F32)
    o_ring = mk_ring(cx, "ps", "o", 2, [128, 512], F32)
    bc_ring = mk_ring(cx, "ps", "bc", 1, [128, 512], F32)
    y_ring = mk_ring(cx, "ps", "y", 2, [128, 512], F32)

    S.dma("pool", woA[:, :, :], w_out[0:512, :].rearrange("(i p) n -> p i n", p=128), writes=[woA_b])
    S.dma("pool", woB[:, :, :], w_out[512:1024, :].rearrange("(g p) n -> p g n", p=128), writes=[woB_b])
    S.dma("pool", wp[:, :, :], w_pool.rearrange("g i j -> i g j"), writes=[wp_b])
    S.dma("pool", mk_t[:, :, :], masksd.rearrange("m p n -> p m n"), writes=[mk_b])
    S.op("dve", lambda h: h.tensor_scalar(out=mk_t[:, :, :], in0=mk_t[:, :, :], scalar1=-1.0, scalar2=30000.0, op0=ALU.add, op1=ALU.mult),
         reads=[mk_b], writes=[mk_b])
    S.dma("pool", id_t[:, :], identd[:, :], writes=[id_b])
    S.dma("sp", ps_t[:, :], pscale[:, :], writes=[ps_b])
    S.dma("sp", invc_t[:, :, :, :], invcd[:, :, :, :], writes=[invc_b])
    S.dma("sp", sk_t[:, :, :], sinkrow[:, :, :], writes=[sk_b])
    S.op("act", lambda h: h.activation(out=esk_t[:, :, :], in_=sk_t[:, :, :], func=AF.Exp), reads=[sk_b], writes=[esk_b])
    S.op("pool", lambda h: h.memset(sel_t[:, :], 0.0), writes=[sel_b])
    S.op("pool", lambda h: h.memset(sel_t[:, 64:65], 1.0), writes=[sel_b])
    S.op("pool", lambda h: h.memset(ones32[:, :], 1.0), writes=[ones32_b])
    for (qt, qb_) in qsA_ring.items:
        S.op("pool", lambda h, qt=qt: h.memset(qt[64:128, :, :], 0.0), writes=[qb_])
    for (qt, qb_) in qsB_ring.items:
        S.op("pool", lambda h, qt=qt: h.memset(qt[0:64, :, :], 0.0), writes=[qb_])
    xv = xT.rearrange("(c p) t -> p c t", p=128)
    ov = oT.rearrange("(c p) t -> p c t", p=128)
    cx.run_hook()

    def blk(b):
        t0 = b * TB
        qsA_t, qsA_b = qsA_ring.next()
        qsB_t, qsB_b = qsB_ring.next()
        kt_t, kt_b = kt_ring.next()
        v_t, v_b = v_ring.next()
        u_t, u_b = u_ring.next()
        x_t, x_b = x_ring.next()
        S.dma("sp", qsA_t[0:64, :, :], QsT[0:64, :, t0:t0 + TB], writes=[qsA_b])
        S.dma("sp", qsB_t[64:128, :, :], QsT[64:128, :, t0:t0 + TB], writes=[qsB_b])
        S.dma("sp", kt_t[:, :], KTh[:, t0:t0 + 768], writes=[kt_b])
        S.dma("sp", v_t[:, :, :], Vh[t0:t0 + 768, :].rearrange("(i p) n -> p i n", p=128), writes=[v_b])
        S.dma("sp", u_t[:, :, :], UTh[:, :, t0:t0 + TB + 16], writes=[u_b])
        S.dma("sp", x_t[:, :, :], xv[:, :, t0:t0 + TB], writes=[x_b])
        yield
        ya_t, ya_b = ya_ring.next()
        tiles = [(nl, j, mi, dm) for nl in range(4) for mi, dm in enumerate((-1, 0, 1)) for j in range(2)]
        LA = 2
        st = {}
        unit_o = {}
        deferred = []

        def emit_S(t):
            nl, j, mi, dm = tiles[t]
            i = nl + dm + 1
            s_t, s_b = s_ring.next()
            q_t, q_b = (qsA_t, qsA_b) if j == 0 else (qsB_t, qsB_b)
            S.op("pe", lambda h: h.matmul(s_t[:, :], lhsT=kt_t[:, i * 128:(i + 1) * 128],
                                          rhs=q_t[:, :, nl * 128:(nl + 1) * 128], start=True, stop=(dm == 0)),
                 reads=[kt_b, q_b], writes=[s_b])
            if dm != 0:
                n_ = 4 * b + nl
                if dm == -1:
                    mi_ = 2 if n_ == 0 else 0
                else:
                    mi_ = 3 if n_ == NT - 1 else 1
                S.op("pe", lambda h: h.matmul(s_t[:, :], lhsT=id_t[:, :], rhs=mk_t[:, mi_, :], start=False, stop=True),
                     reads=[id_b, mk_b], writes=[s_b])
            st[t] = (s_t, s_b)

        def flush_deferred():
            while deferred:
                (o_t, o_b, osb_t, osb_b, rc_t, rc_b, nl, j) = deferred.pop(0)
                S.op("act", lambda h: h.activation(out=rc_t[64:65, :], in_=osb_t[64:65, :], func=AF.Ln), reads=[osb_b], writes=[rc_b])
                S.op("act", lambda h: h.activation(out=rc_t[64:65, :], in_=rc_t[64:65, :], func=AF.Exp, scale=-1.0),
                     reads=[rc_b], writes=[rc_b])
                bc_t, bc_b = bc_ring.next()
                S.op("pe", lambda h: h.matmul(bc_t[0:64, :], lhsT=ones32[64:65, 0:64], rhs=rc_t[64:65, :], start=True, stop=True),
                     reads=[rc_b, ones32_b], writes=[bc_b])
                S.op("dve", lambda h: h.tensor_tensor(
                    out=ya_t[0:64, j * 4:(j + 1) * 4, nl * 128:(nl + 1) * 128],
                    in0=osb_t[0:64, :].rearrange("p (c q) -> p c q", c=4),
                    in1=bc_t[0:64, :].rearrange("p (c q) -> p c q", c=4), op=ALU.mult),
                    reads=[osb_b, bc_b], writes=[ya_b])

        for t in range(min(LA, len(tiles))):
            emit_S(t)
        for t in range(len(tiles)):
            nl, j, mi, dm = tiles[t]
            n = 4 * b + nl
            i = nl + dm + 1
            if mi == 0:
                unit_o[(nl, j)] = o_ring.next()
            o_t, o_b = unit_o[(nl, j)]
            s_t, s_b = st.pop(t)
            p_t, p_b = p_ring.next()
            S.op("act", lambda h: h.activation(out=p_t[:, :], in_=s_t[:, :], func=AF.Exp, scale=0.125), reads=[s_b], writes=[p_b])
            if t + LA < len(tiles):
                emit_S(t + LA)
            S.op("pe", lambda h: h.matmul(o_t[0:65, :], lhsT=v_t[:, i, j * 65:(j + 1) * 65], rhs=p_t[:, :], start=(mi == 0), stop=False),
                 reads=[v_b, p_b], writes=[o_b])
            if mi == 0 and j == 1:
                flush_deferred()
            if mi == 2:
                S.op("pe", lambda h: h.matmul(o_t[0:65, :], lhsT=sel_t[0:1, 0:65], rhs=esk_t[0:1, j, :], start=False, stop=True),
                     reads=[sel_b, esk_b], writes=[o_b])
                osb_t, osb_b = osb_ring.next()
                rc_t, rc_b = rc_ring.next()
                S.op("dve", lambda h: h.tensor_copy(out=osb_t[0:65, :], in_=o_t[0:65, :]), reads=[o_b], writes=[osb_b])
                deferred.append((o_t, o_b, osb_t, osb_b, rc_t, rc_b, nl, j))
        flush_deferred()
        yield
        yp_t, yp_b = yp_ring.next()
        S.dma("sp", yp_t[0:64, :, :], ya_t[0:64, 0:8:2, :], reads=[ya_b], writes=[yp_b])
        S.dma("sp", yp_t[64:128, :, :], ya_t[0:64, 1:8:2, :], reads=[ya_b], writes=[yp_b])
        d_t, d_b = d_ring.next()
        L = TB + 16
        for g in range(4):
            w = 2 << g
            steps = g + 1
            src_t, src_b, ln = None, None, L
            for s_i in range(steps):
                sh = 1 << s_i
                tp_t, tp_b = tmp_rings[g].next()
                nl_ = ln - sh
                if s_i == 0:
                    S.op("pool", lambda h, tp_t=tp_t, u_t=u_t, g=g, nl_=nl_, sh=sh: h.tensor_tensor(
                        out=tp_t[:, 0:nl_], in0=u_t[:, g, 0:nl_], in1=u_t[:, g, sh:sh + nl_], op=ALU.add),
                        reads=[u_b], writes=[tp_b])
                else:
                    S.op("pool", lambda h, tp_t=tp_t, src_t=src_t, nl_=nl_, sh=sh: h.tensor_tensor(
                        out=tp_t[:, 0:nl_], in0=src_t[:, 0:nl_], in1=src_t[:, sh:sh + nl_], op=ALU.add),
                        reads=[src_b], writes=[tp_b])
                src_t, src_b, ln = tp_t, tp_b, nl_
            off = 8 - w // 2
            S.op("dve", lambda h, d_t=d_t, src_t=src_t, u_t=u_t, g=g, off=off, w=w: h.scalar_tensor_tensor(
                out=d_t[:, g, :], in0=src_t[:, off:off + TB], scalar=1.0 / w, in1=u_t[:, g, 8:8 + TB],
                op0=ALU.mult, op1=ALU.subtract), reads=[src_b, u_b], writes=[d_b])
            for (is_edge, which, c0) in ((b == 0, 0, 0), (b == NB - 1, 1, TB - 16)):
                if not is_edge:
                    continue
                e_t, e_b = e16_ring.next()
                S.op("dve", lambda h, e_t=e_t, src_t=src_t, g=g, off=off, c0=c0, which=which: h.tensor_tensor(
                    out=e_t[:, :], in0=src_t[:, off + c0:off + c0 + 16], in1=invc_t[:, which, g, :], op=ALU.mult),
                    reads=[src_b, invc_b], writes=[e_b])
                S.op("dve", lambda h, e_t=e_t, d_t=d_t, u_t=u_t, g=g, c0=c0: h.tensor_tensor(
                    out=d_t[:, g, c0:c0 + 16], in0=e_t[:, :], in1=u_t[:, g, 8 + c0:8 + c0 + 16], op=ALU.subtract),
                    reads=[e_b, u_b, d_b], writes=[d_b])
        yield
        yb_t, yb_b = yb_ring.next()
        for g in range(4):
            y_t, y_b = y_ring.next()
            S.op("pe", lambda h, y_t=y_t, d_t=d_t, g=g: h.matmul(y_t[:, :], lhsT=wp[:, g, :], rhs=d_t[:, g, :], start=True, stop=True),
                 reads=[wp_b, d_b], writes=[y_b])
            S.op("dve", lambda h, y_t=y_t, yb_t=yb_t, g=g: h.tensor_scalar_mul(out=yb_t[:, g, :], in0=y_t[:, :], scalar1=ps_t[:, g:g + 1]),
                 reads=[y_b, ps_b], writes=[yb_b])
        for o in range(KC):
            y_t, y_b = y_ring.next()
            for hh in range(4):
                S.op("pe", lambda h, y_t=y_t, yp_t=yp_t, hh=hh, o=o: h.matmul(
                    y_t[:, :], lhsT=woA[:, hh, o * 128:(o + 1) * 128], rhs=yp_t[:, hh, :], start=(hh == 0), stop=False),
                    reads=[woA_b, yp_b], writes=[y_b])
            for g in range(4):
                S.op("pe", lambda h, y_t=y_t, yb_t=yb_t, g=g, o=o: h.matmul(
                    y_t[:, :], lhsT=woB[:, g, o * 128:(o + 1) * 128], rhs=yb_t[:, g, :], start=False, stop=(g == 3)),
                    reads=[woB_b, yb_b], writes=[y_b])
            S.op("dve", lambda h, y_t=y_t, x_t=x_t, o=o: h.tensor_tensor(out=x_t[:, o, :], in0=y_t[:, :], in1=x_t[:, o, :], op=ALU.add),
                 reads=[y_b, x_b], writes=[x_b])
        S.dma("sp", ov[:, :, t0:t0 + TB], x_t[:, :, :], reads=[x_b])
        yield

    run_interleaved((blk(b) for b in range(NB)), 2, 2)
    cx.pop()
    return cx.finish() if own else None


def eb_masks(has_left, has_right):
    ki = np.arange(128)[:, None]
    qi = np.arange(128)[None, :]
    mL = np.tile((ki >= qi).astype(np.float32), (1, 4))
    mR = np.tile((ki <= qi).astype(np.float32), (1, 4))
    return np.stack([mL, mR, mL * float(has_left), mR * float(has_right)]).astype(np.float32)


def eb_invc(is_first, is_last):
    out = np.zeros((128, 2, 4, 16), np.float32)
    for g in range(4):
        w = 2 << g
        half = w // 2
        for i in range(16):
            c0 = min(i + half, w) if is_first else w
            r = 16 - i
            c1 = min(half + r, w) if is_last else w
            out[:, 0, g, i] = 1.0 / c0
            out[:, 1, g, i] = 1.0 / c1
    return out


def emit_rope(cx, src_t, src_b, nrow, TB, rot_t, rot_b, cos_t, cos_b, sin_t, sin_b, qb_ring, pr_ring, t1_ring, t2_ring,
              out_ap, out_b):
    S = cx.S
    qb_t, qb_b = qb_ring.next()
    S.op("act", lambda h: h.activation(out=qb_t[0:nrow, :], in_=src_t[0:nrow, 0:TB], func=AF.Copy), reads=[src_b], writes=[qb_b])
    pr_t, pr_b = pr_ring.next()
    S.op("pe", lambda h: h.matmul(pr_t[0:nrow, 0:TB], lhsT=rot_t[0:nrow, 0:nrow], rhs=qb_t[0:nrow, :], start=True, stop=True),
         reads=[qb_b, rot_b], writes=[pr_b])
    t1_t, t1_b = t1_ring.next()
    t2_t, t2_b = t2_ring.next()
    S.op("dve", lambda h: h.tensor_tensor(out=t1_t[0:nrow, :], in0=src_t[0:nrow, 0:TB], in1=cos_t[0:nrow, :], op=ALU.mult),
         reads=[src_b, cos_b], writes=[t1_b])
    S.op("dve", lambda h: h.tensor_tensor(out=t2_t[0:nrow, :], in0=pr_t[0:nrow, 0:TB], in1=sin_t[0:nrow, :], op=ALU.mult),
         reads=[pr_b, sin_b], writes=[t2_b])
    S.op("dve", lambda h: h.tensor_tensor(out=out_ap, in0=t1_t[0:nrow, :], in1=t2_t[0:nrow, :], op=ALU.add),
         reads=[t1_b, t2_b], writes=[out_b])


def build_oa(ntok=TOK, cx=None, mid_hook=None):
    TB = 512
    NB = ntok // TB
    NT = ntok // 128
    own = cx is None
    cx = Ctx() if own else cx
    cx.push()
    S = cx.S
    xT = cx.dram_in("xT", [D, ntok])
    xhalo = cx.dram_in("xhalo", [D, 4])
    w_in = cx.dram_in("w_in", [D, 1440])
    gin = cx.dram_in("g", [128, KC])
    gcq = cx.dram_in("g_cq", [128, 2])
    gckv = cx.dram_in("g_ckv", [128, 1])
    w_uq = cx.dram_in("w_uq", [256, 768])
    w_ukv = cx.dram_in("w_ukv", [128, 1024])
    cwd = cx.dram_in("cw", [128, 4, 4])
    cbd = cx.dram_in("cb", [128, 4])
    wad = cx.dram_in("wa", [2, 8, 64, 64])
    wxd = cx.dram_in("wx", [2, 8, 64, 64])
    bad = cx.dram_in("ba", [128, 2, 4])
    bxd = cx.dram_in("bx", [128, 2, 4])
    lamd = cx.dram_in("lam", [128, 2, 4])
    cosd = cx.dram_in("cos", [128, ntok])
    sind = cx.dram_in("sin", [128, ntok])
    rotd = cx.dram_in("rot", [128, 128])
    QN = cx.dram_out("QN", [512, ntok], BF16)
    QR = cx.dram_out("QR", [256, ntok], BF16)
    KNR = cx.dram_out("KNR", [544, ntok], BF16)
    V5 = cx.dram_out("V5", [1024, NT * 65], BF16)
    GX = cx.dram_out("GX", [512, ntok], BF16)
    AB = cx.dram_out("AB", [2, 2, 512, ntok])
    BLK = cx.dram_out("BLK", [128, NB, 2, 2, 4])
    CAB = cx.dram_out("CAB", [128, 2, 2, 4])
    XR = cx.dram_out("XR", [512, ntok])

    g_t = cx.sb("g", [128, KC], F32); g_b = Buf("g")
    gcq_t = cx.sb("gcq", [128, 2], F32); gcq_b = Buf("gcq")
    gckv_t = cx.sb("gckv", [128, 1], F32); gckv_b = Buf("gckv")
    ones_t = cx.sb("ones", [128, 128], BF16); ones_b = Buf("ones")
    xrh_t = cx.sb("xrh", [128, 4, 4], F32); xrh_b = Buf("xrh")
    cp_t = cx.sb("cp", [128, 2, 4], F32); cp_b = Buf("cp")
    blk_t = cx.sb("blk", [128, NB, 2, 2, 4], F32); blk_b = Buf("blk")
    S.dma("sp", g_t[:, :], gin[:, :], writes=[g_b])
    S.op("dve", lambda h: h.tensor_scalar_mul(out=g_t[:, :], in0=g_t[:, :], scalar1=float(np.sqrt(D))), reads=[g_b], writes=[g_b])
    S.dma("sp", gcq_t[:, :], gcq[:, :], writes=[gcq_b])
    S.op("dve", lambda h: h.tensor_scalar_mul(out=gcq_t[:, :], in0=gcq_t[:, :], scalar1=16.0), reads=[gcq_b], writes=[gcq_b])
    S.dma("sp", gckv_t[:, :], gckv[:, :], writes=[gckv_b])
    S.op("dve", lambda h: h.tensor_scalar_mul(out=gckv_t[:, :], in0=gckv_t[:, :], scalar1=float(np.sqrt(128.0))),
         reads=[gckv_b], writes=[gckv_b])
    S.op("pool", lambda h: h.memset(ones_t[:, :], 1.0), writes=[ones_b])
    S.dma("sp", cp_t[:, :, :], lamd[:, :, :], writes=[cp_b])
    S.op("act", lambda h: h.activation(out=cp_t[:, :, :], in_=cp_t[:, :, :], func=AF.Exp, scale=-1.0), reads=[cp_b], writes=[cp_b])
    S.op("act", lambda h: h.activation(out=cp_t[:, :, :], in_=cp_t[:, :, :], func=AF.Ln, bias=1.0), reads=[cp_b], writes=[cp_b])
    S.op("dve", lambda h: h.tensor_scalar_mul(out=cp_t[:, :, :], in0=cp_t[:, :, :], scalar1=-8.0), reads=[cp_b], writes=[cp_b])

    wabd = cx.sb("wabd", [128, 2, 4, 128], BF16); wxbd = cx.sb("wxbd", [128, 2, 4, 128], BF16); bd_b = Buf("bd")
    cw_t = cx.sb("cw", [128, 4, 4], F32); cb_t = cx.sb("cb", [128, 4], F32); cw_b = Buf("cw")
    ba_t = cx.sb("ba", [128, 2, 4], F32); bx_t = cx.sb("bx", [128, 2, 4], F32); bb_b = Buf("bb")
    S.op("pool", lambda h: h.memset(wabd[:, :, :, :], 0.0), writes=[bd_b])
    S.op("pool", lambda h: h.memset(wxbd[:, :, :, :], 0.0), writes=[bd_b])
    for d in range(2):
        for c in range(4):
            for hf in range(2):
                S.dma("pool", wabd[hf * 64:(hf + 1) * 64, d, c, hf * 64:(hf + 1) * 64], wad[d, 2 * c + hf, :, :], writes=[bd_b])
                S.dma("pool", wxbd[hf * 64:(hf + 1) * 64, d, c, hf * 64:(hf + 1) * 64], wxd[d, 2 * c + hf, :, :], writes=[bd_b])
    S.dma("sp", cw_t[:, :, :], cwd[:, :, :], writes=[cw_b])
    S.dma("sp", cb_t[:, :], cbd[:, :], writes=[cw_b])
    S.dma("sp", ba_t[:, :, :], bad[:, :, :], writes=[bb_b])
    S.dma("sp", bx_t[:, :, :], bxd[:, :, :], writes=[bb_b])
    xv = xT.rearrange("(c p) t -> p c t", p=128)
    cx.push()
    wb = cx.sb("wb", [128, KC, 1440], BF16)
    w_bufs = [Buf(f"w{k}") for k in range(KC)]
    wuqn = cx.sb("wuqn", [128, 2, 512], BF16); wuqr = cx.sb("wuqr", [128, 2, 256], BF16); wuq_b = Buf("wuq")
    wk = cx.sb("wk", [128, 512], BF16); wv = cx.sb("wv", [128, 512], BF16); wkv_b = Buf("wkv")
    rot_t = cx.sb("rot", [128, 128], BF16); rot_b = Buf("rot")
    x_ring = mk_ring(cx, "sb", "x", 2, [128, KC, TB], F32)
    h_ring = mk_ring(cx, "sb", "h", 2, [128, KC, TB], BF16)
    sq_ring = mk_ring(cx, "sb", "sq", 3, [128, TB], BF16)
    rstd_ring = mk_ring(cx, "sb", "rstd", 2, [128, TB], F32)
    cos_ring = mk_ring(cx, "sb", "cos", 2, [128, TB], F32)
    sin_ring = mk_ring(cx, "sb", "sin", 2, [128, TB], F32)
    qb_ring = mk_ring(cx, "sb", "qb", 2, [128, TB], BF16)
    t1_ring = mk_ring(cx, "sb", "t1", 2, [128, TB], F32)
    t2_ring = mk_ring(cx, "sb", "t2", 2, [128, TB], F32)
    cq_ring = mk_ring(cx, "sb", "cq", 2, [128, 2, TB], F32)
    ckv_ring = mk_ring(cx, "sb", "ckv", 2, [128, 1, TB], F32)
    cqn_ring = mk_ring(cx, "sb", "cqn", 2, [128, 2, TB], BF16)
    ckvn_ring = mk_ring(cx, "sb", "ckvn", 2, [128, 1, TB], BF16)
    xr_ring = mk_ring(cx, "sb", "xr", 2, [128, 4, TB], F32)
    gx_ring = mk_ring(cx, "sb", "gx", 2, [128, 4, TB], BF16)
    qn_ring = mk_ring(cx, "sb", "qn", 2, [128, 4, TB], BF16)
    qr_ring = mk_ring(cx, "sb", "qr", 2, [128, 2, TB], BF16)
    kn_ring = mk_ring(cx, "sb", "kn", 2, [128, 4, TB], BF16)
    kr_ring = mk_ring(cx, "sb", "kr", 2, [32, TB], BF16)
    vo_ring = mk_ring(cx, "sb", "vo", 2, [128, 4, 520], BF16)
    hx_t = cx.sb("hx", [128, KC, 4], F32); hx_b = Buf("hx")
    hh_t = cx.sb("hh", [128, KC, 4], BF16); hh_b = Buf("hh")
    st_ring = mk_ring(cx, "ps", "st", 1, [128, 512], F32)
    pq_ring = mk_ring(cx, "ps", "pq", 4, [128, 512], F32)
    pr_ring = mk_ring(cx, "ps", "pr", 1, [128, 512], F32)
    pv_ring = mk_ring(cx, "ps", "pv", 2, [128, 512], F32)

    for k in range(KC):
        S.dma("pool", wb[:, k, :], w_in[k * 128:(k + 1) * 128, :], writes=[w_bufs[k]])
    for k in range(2):
        src = w_uq[k * 128:(k + 1) * 128, :].rearrange("p (h e) -> p h e", e=96)
        S.dma("pool", wuqn[:, k, :].rearrange("p (h d) -> p h d", d=64), src[:, :, 0:64], writes=[wuq_b])
        S.dma("pool", wuqr[:, k, :].rearrange("p (h d) -> p h d", d=32), src[:, :, 64:96], writes=[wuq_b])
    srckv = w_ukv.rearrange("p (h e) -> p h e", e=128)
    S.dma("pool", wk[:, :].rearrange("p (h d) -> p h d", d=64), srckv[:, :, 0:64], writes=[wkv_b])
    S.dma("pool", wv[:, :].rearrange("p (h d) -> p h d", d=64), srckv[:, :, 64:128], writes=[wkv_b])
    S.dma("pool", rot_t[:, :], rotd[:, :], writes=[rot_b])
    for (vt, vb) in vo_ring.items:
        S.op("pool", lambda h, vt=vt: h.memset(vt[:, :, :], 1.0), writes=[vb])

    def proj_tile(h_t, h_b, c0, ncols, TBx):
        pq_t, pq_b = pq_ring.next()
        for k in range(KC):
            S.op("pe", lambda h, k=k: h.matmul(pq_t[0:ncols, 0:TBx], lhsT=wb[:, k, c0:c0 + ncols], rhs=h_t[:, k, 0:TBx],
                                               start=(k == 0), stop=(k == KC - 1)), reads=[h_b, w_bufs[k]], writes=[pq_b])
        return pq_t, pq_b

    S.dma("sp", hx_t[:, :, :], xhalo.rearrange("(c p) t -> p c t", p=128), writes=[hx_b])
    emit_rmsnorm(cx, hx_t, hx_b, KC, 4, g_t, g_b, ones_t, ones_b, sq_ring, st_ring, rstd_ring, hh_t, hh_b, D)
    for c in range(4):
        pq_t, pq_b = proj_tile(hh_t, hh_b, 416 + c * 128, 128, 4)
        S.op("act", lambda h, c=c, pq_t=pq_t: h.activation(out=xrh_t[:, c, :], in_=pq_t[:, 0:4], func=AF.Copy),
             reads=[pq_b], writes=[xrh_b])

    def blk(b):
        t0 = b * TB
        x_t, x_b = x_ring.next()
        S.dma("sp", x_t[:, :, :], xv[:, :, t0:t0 + TB], writes=[x_b])
        cos_t, cos_b = cos_ring.next()
        sin_t, sin_b = sin_ring.next()
        S.dma("sp", cos_t[:, :], cosd[:, t0:t0 + TB], writes=[cos_b])
        S.dma("sp", sin_t[:, :], sind[:, t0:t0 + TB], writes=[sin_b])
        h_t, h_b = h_ring.next()
        emit_rmsnorm(cx, x_t, x_b, KC, TB, g_t, g_b, ones_t, ones_b, sq_ring, st_ring, rstd_ring, h_t, h_b, D)
        yield
        cq_t, cq_b = cq_ring.next()
        for c in range(2):
            pq_t, pq_b = proj_tile(h_t, h_b, c * 128, 128, TB)
            S.op("act", lambda h, c=c, pq_t=pq_t, cq_t=cq_t: h.activation(out=cq_t[:, c, :], in_=pq_t[:, 0:TB], func=AF.Copy),
                 reads=[pq_b], writes=[cq_b])
        ckv_t, ckv_b = ckv_ring.next()
        pq_t, pq_b = proj_tile(h_t, h_b, 256, 128, TB)
        S.op("act", lambda h, pq_t=pq_t, ckv_t=ckv_t: h.activation(out=ckv_t[:, 0, :], in_=pq_t[:, 0:TB], func=AF.Copy),
             reads=[pq_b], writes=[ckv_b])
        yield
        pq_t, pq_b = proj_tile(h_t, h_b, 384, 32, TB)
        kr_t, kr_b = kr_ring.next()
        emit_rope(cx, pq_t, pq_b, 32, TB, rot_t, rot_b, cos_t, cos_b, sin_t, sin_b, qb_ring, pr_ring, t1_ring, t2_ring,
                  kr_t[0:32, :], kr_b)
        S.dma("pool", KNR[512:544, t0:t0 + TB], kr_t[:, :], reads=[kr_b])
        yield
        xr_t, xr_b = xr_ring.next()
        gx_t, gx_b = gx_ring.next()
        for c in range(4):
            pq_t, pq_b = proj_tile(h_t, h_b, 416 + c * 128, 128, TB)
            S.op("act", lambda h, c=c, pq_t=pq_t, xr_t=xr_t: h.activation(out=xr_t[:, c, :], in_=pq_t[:, 0:TB], func=AF.Copy),
                 reads=[pq_b], writes=[xr_b])
        for c in range(4):
            pq_t, pq_b = proj_tile(h_t, h_b, 928 + c * 128, 128, TB)
            S.op("act", lambda h, c=c, pq_t=pq_t, gx_t=gx_t: h.activation(out=gx_t[:, c, :], in_=pq_t[:, 0:TB], func=AF.Gelu_apprx_tanh),
                 reads=[pq_b], writes=[gx_b])
        S.dma("pool", XR.rearrange("(c p) t -> p c t", p=128)[:, :, t0:t0 + TB], xr_t[:, :, :], reads=[xr_b])
        S.dma("pool", GX.rearrange("(c p) t -> p c t", p=128)[:, :, t0:t0 + TB], gx_t[:, :, :], reads=[gx_b])
        yield
        cqn_t, cqn_b = cqn_ring.next()
        emit_rmsnorm(cx, cq_t, cq_b, 2, TB, gcq_t, gcq_b, ones_t, ones_b, sq_ring, st_ring, rstd_ring, cqn_t, cqn_b, 256)
        ckvn_t, ckvn_b = ckvn_ring.next()
        emit_rmsnorm(cx, ckv_t, ckv_b, 1, TB, gckv_t, gckv_b, ones_t, ones_b, sq_ring, st_ring, rstd_ring, ckvn_t, ckvn_b, 128)
        yield
        qn_t, qn_b = qn_ring.next()
        for i in range(4):
            pq_t, pq_b = pq_ring.next()
            for k in range(2):
                S.op("pe", lambda h, i=i, k=k, pq_t=pq_t, cqn_t=cqn_t: h.matmul(
                    pq_t[:, 0:TB], lhsT=wuqn[:, k, i * 128:(i + 1) * 128], rhs=cqn_t[:, k, :], start=(k == 0), stop=(k == 1)),
                    reads=[cqn_b, wuq_b], writes=[pq_b])
            S.op("act", lambda h, i=i, pq_t=pq_t, qn_t=qn_t: h.activation(out=qn_t[:, i, :], in_=pq_t[:, 0:TB], func=AF.Copy),
                 reads=[pq_b], writes=[qn_b])
        S.dma("pool", QN.rearrange("(c p) t -> p c t", p=128)[:, :, t0:t0 + TB], qn_t[:, :, :], reads=[qn_b])
        qr_t, qr_b = qr_ring.next()
        for i in range(2):
            pq_t, pq_b = pq_ring.next()
            for k in range(2):
                S.op("pe", lambda h, i=i, k=k, pq_t=pq_t, cqn_t=cqn_t: h.matmul(
                    pq_t[:, 0:TB], lhsT=wuqr[:, k, i * 128:(i + 1) * 128], rhs=cqn_t[:, k, :], start=(k == 0), stop=(k == 1)),
                    reads=[cqn_b, wuq_b], writes=[pq_b])
            emit_rope(cx, pq_t, pq_b, 128, TB, rot_t, rot_b, cos_t, cos_b, sin_t, sin_b, qb_ring, pr_ring, t1_ring, t2_ring,
                      qr_t[:, i, :], qr_b)
        S.dma("pool", QR.rearrange("(c p) t -> p c t", p=128)[:, :, t0:t0 + TB], qr_t[:, :, :], reads=[qr_b])
        yield
        kn_t, kn_b = kn_ring.next()
        for i in range(4):
            pq_t, pq_b = pq_ring.next()
            S.op("pe", lambda h, i=i, pq_t=pq_t, ckvn_t=ckvn_t: h.matmul(
                pq_t[:, 0:TB], lhsT=wk[:, i * 128:(i + 1) * 128], rhs=ckvn_t[:, 0, :], start=True, stop=True),
                reads=[ckvn_b, wkv_b], writes=[pq_b])
            S.op("act", lambda h, i=i, pq_t=pq_t, kn_t=kn_t: h.activation(out=kn_t[:, i, :], in_=pq_t[:, 0:TB], func=AF.Copy),
                 reads=[pq_b], writes=[kn_b])
        S.dma("pool", KNR[0:512, :].rearrange("(c p) t -> p c t", p=128)[:, :, t0:t0 + TB], kn_t[:, :, :], reads=[kn_b])
        yield
        vo_t, vo_b = vo_ring.next()
        for ti in range(TB // 128):
            pv_t, pv_b = pv_ring.next()
            S.op("pe", lambda h, ti=ti, pv_t=pv_t, ckvn_t=ckvn_t: h.matmul(
                pv_t[:, :], lhsT=ckvn_t[:, 0, ti * 128:(ti + 1) * 128], rhs=wv[:, :], start=True, stop=True),
                reads=[ckvn_b, wkv_b], writes=[pv_b])
            S.op("act", lambda h, ti=ti, pv_t=pv_t, vo_t=vo_t: h.activation(
                out=vo_t[:, ti, :].rearrange("p (h e) -> p h e", e=65)[:, :, 0:64],
                in_=pv_t[:, :].rearrange("p (h d) -> p h d", d=64), func=AF.Copy), reads=[pv_b], writes=[vo_b])
        for hd in range(8):
            S.dma("pool", V5[hd * 128:(hd + 1) * 128, :].rearrange("p (i e) -> p i e", e=65)[:, b * 4:(b + 1) * 4, :],
                  vo_t[:, :, hd * 65:(hd + 1) * 65], reads=[vo_b])
        yield

    run_interleaved((blk(b) for b in range(NB)), 2)
    cx.pop()
    if mid_hook is not None:
        mid_hook()

    cx.push()
    xe_ring = mk_ring(cx, "sb", "xe", 2, [128, 4, TB + 4], F32)
    xc_ring = mk_ring(cx, "sb", "xc", 2, [128, 4, TB], F32)
    xcb_ring = mk_ring(cx, "sb", "xcb", 2, [128, 4, TB], BF16)
    r_ring = mk_ring(cx, "sb", "r", 2, [128, 8, TB], F32)
    i_ring = mk_ring(cx, "sb", "i", 2, [128, 8, TB], F32)
    a_ring = mk_ring(cx, "sb", "a", 2, [128, 8, TB], F32)
    b_ring = mk_ring(cx, "sb", "b", 2, [128, 8, TB], F32)
    hl_ring = mk_ring(cx, "sb", "hl", 2, [128, TB], F32)
    sr_ring = mk_ring(cx, "sb", "sr", 2, [128, 8], F32)
    pg_ring = mk_ring(cx, "ps", "pg", 6, [128, 512], F32)
    XRv = XR.rearrange("(c p) t -> p c t", p=128)
    ABv = AB.rearrange("d s (c p) t -> d s p c t", p=128)

    def blk(b):
        t0 = b * TB
        xe_t, xe_b = xe_ring.next()
        lo = 0 if b > 0 else 2
        hi = TB + 3 if b < NB - 1 else TB + 2
        S.dma("sp", xe_t[:, :, lo:hi], XRv[:, :, t0 - 2 + lo:t0 - 2 + hi], writes=[xe_b])
        if b == 0:
            S.op("dve", lambda h, xe_t=xe_t: h.tensor_copy(out=xe_t[:, :, 0:2], in_=xrh_t[:, :, 0:2]), reads=[xrh_b, xe_b], writes=[xe_b])
        if b == NB - 1:
            S.op("dve", lambda h, xe_t=xe_t: h.tensor_copy(out=xe_t[:, :, TB + 2:TB + 3], in_=xrh_t[:, :, 2:3]),
                 reads=[xrh_b, xe_b], writes=[xe_b])
        yield
        xc_t, xc_b = xc_ring.next()
        xcb_t, xcb_b = xcb_ring.next()
        for c in range(4):
            S.op("dve", lambda h, c=c, xc_t=xc_t, xe_t=xe_t: h.tensor_scalar(
                out=xc_t[:, c, :], in0=xe_t[:, c, 0:TB], scalar1=cw_t[:, c, 0:1], scalar2=cb_t[:, c:c + 1],
                op0=ALU.mult, op1=ALU.add), reads=[xe_b, cw_b], writes=[xc_b])
            for j in range(1, 4):
                S.op("dve", lambda h, c=c, j=j, xc_t=xc_t, xe_t=xe_t: h.scalar_tensor_tensor(
                    out=xc_t[:, c, :], in0=xe_t[:, c, j:j + TB], scalar=cw_t[:, c, j:j + 1], in1=xc_t[:, c, :],
                    op0=ALU.mult, op1=ALU.add), reads=[xe_b, cw_b, xc_b], writes=[xc_b])
        S.op("act", lambda h, xc_t=xc_t, xcb_t=xcb_t: h.activation(out=xcb_t[:, :, :], in_=xc_t[:, :, :], func=AF.Copy), reads=[xc_b], writes=[xcb_b])
        yield
        r_t, r_b = r_ring.next()
        i_t, i_b = i_ring.next()
        a_t, a_b = a_ring.next()
        b_t, b_b = b_ring.next()
        sr_t, sr_b = sr_ring.next()
        S.op("dve", lambda h, sr_t=sr_t: h.memset(sr_t[:, :], 0.0), writes=[sr_b])
        for d in range(2):
            for c in range(4):
                q = d * 4 + c
                pg_t, pg_b = pg_ring.next()
                S.op("pe", lambda h, d=d, c=c, pg_t=pg_t, xcb_t=xcb_t: h.matmul(pg_t[:, :], lhsT=wabd[:, d, c, :], rhs=xcb_t[:, c, :],
                                                                         start=True, stop=True), reads=[bd_b, xcb_b], writes=[pg_b])
                S.op("act", lambda h, d=d, c=c, q=q, pg_t=pg_t, r_t=r_t, sr_t=sr_t: h.activation(
                    out=r_t[:, q, :], in_=pg_t[:, :], func=AF.Sigmoid, bias=ba_t[:, d, c:c + 1], accum_out=sr_t[:, q:q + 1]),
                    reads=[pg_b, bb_b], writes=[r_b, sr_b])
                pg_t, pg_b = pg_ring.next()
                S.op("pe", lambda h, d=d, c=c, pg_t=pg_t, xcb_t=xcb_t: h.matmul(pg_t[:, :], lhsT=wxbd[:, d, c, :], rhs=xcb_t[:, c, :],
                                                                         start=True, stop=True), reads=[bd_b, xcb_b], writes=[pg_b])
                S.op("act", lambda h, d=d, c=c, q=q, pg_t=pg_t, i_t=i_t: h.activation(
                    out=i_t[:, q, :], in_=pg_t[:, :], func=AF.Sigmoid, bias=bx_t[:, d, c:c + 1]),
                    reads=[pg_b, bb_b], writes=[i_b])
        yield
        for d in range(2):
            for c in range(4):
                q = d * 4 + c
                S.op("act", lambda h, d=d, c=c, q=q, a_t=a_t, r_t=r_t: h.activation(
                    out=a_t[:, q, :], in_=r_t[:, q, :], func=AF.Exp, scale=cp_t[:, d, c:c + 1]), reads=[r_b, cp_b], writes=[a_b])
                S.op("act", lambda h, d=d, c=c, q=q, sr_t=sr_t, b=b: h.activation(
                    out=blk_t[:, b, d, 0, c:c + 1], in_=sr_t[:, q:q + 1], func=AF.Exp, scale=cp_t[:, d, c:c + 1]),
                    reads=[sr_b, cp_b, blk_b], writes=[blk_b])
        yield
        S.op("dve", lambda h, a_t=a_t, r_t=r_t: h.tensor_tensor(out=r_t[:, :, :], in0=a_t[:, :, :], in1=a_t[:, :, :], op=ALU.mult),
             reads=[a_b, r_b], writes=[r_b])
        S.op("act", lambda h, r_t=r_t: h.activation(out=r_t[:, :, :], in_=r_t[:, :, :], func=AF.Sqrt, scale=-1.0, bias=1.0),
             reads=[r_b], writes=[r_b])
        for d in range(2):
            S.op("dve", lambda h, d=d, i_t=i_t, xc_t=xc_t: h.tensor_tensor(out=i_t[:, d * 4:(d + 1) * 4, :], in0=i_t[:, d * 4:(d + 1) * 4, :],
                                                                        in1=xc_t[:, :, :], op=ALU.mult), reads=[i_b, xc_b], writes=[i_b])
        S.op("dve", lambda h, b_t=b_t, r_t=r_t, i_t=i_t: h.tensor_tensor(out=b_t[:, :, :], in0=r_t[:, :, :], in1=i_t[:, :, :], op=ALU.mult),
             reads=[r_b, i_b], writes=[b_b])
        yield
        for d in range(2):
            for c in range(4):
                q = d * 4 + c
                hl_t, hl_b = hl_ring.next()
                if d == 0:
                    S.op("dve", lambda h, q=q, hl_t=hl_t, a_t=a_t, b_t=b_t: h.tensor_tensor_scan(
                        out=hl_t[:, :], data0=a_t[:, q, :], data1=b_t[:, q, :], initial=0.0, op0=ALU.mult, op1=ALU.add),
                        reads=[a_b, b_b], writes=[hl_b])
                    col = TB - 1
                else:
                    S.op("dve", lambda h, q=q, hl_t=hl_t, a_t=a_t, b_t=b_t: h.tensor_tensor_scan(
                        out=hl_t[:, ::-1], data0=a_t[:, q, ::-1], data1=b_t[:, q, ::-1], initial=0.0, op0=ALU.mult, op1=ALU.add),
                        reads=[a_b, b_b], writes=[hl_b])
                    col = 0
                S.op("act", lambda h, d=d, c=c, hl_t=hl_t, col=col, b=b: h.activation(
                    out=blk_t[:, b, d, 1, c:c + 1], in_=hl_t[:, col:col + 1], func=AF.Copy), reads=[hl_b, blk_b], writes=[blk_b])
        for d in range(2):
            S.dma("act", ABv[d, 0, :, :, t0:t0 + TB], a_t[:, d * 4:(d + 1) * 4, :], reads=[a_b])
            S.dma("sp", ABv[d, 1, :, :, t0:t0 + TB], b_t[:, d * 4:(d + 1) * 4, :], reads=[b_b])
        yield

    run_interleaved((blk(b) for b in range(NB)), 2)
    cab_t = cx.sb("cab", [128, 2, 2, 4], F32); cab_b = Buf("cab")
    for d in range(2):
        S.op("dve", lambda h, d=d: h.memset(cab_t[:, d, 0, :], 1.0), writes=[cab_b])
        S.op("dve", lambda h, d=d: h.memset(cab_t[:, d, 1, :], 0.0), writes=[cab_b])
        order = range(NB) if d == 0 else range(NB - 1, -1, -1)
        for b in order:
            S.op("dve", lambda h, d=d, b=b: h.tensor_tensor(out=cab_t[:, d, 1, :], in0=cab_t[:, d, 1, :], in1=blk_t[:, b, d, 0, :],
                                                            op=ALU.mult), reads=[cab_b, blk_b], writes=[cab_b])
            S.op("dve", lambda h, d=d, b=b: h.tensor_tensor(out=cab_t[:, d, 1, :], in0=cab_t[:, d, 1, :], in1=blk_t[:, b, d, 1, :],
                                                            op=ALU.add), reads=[cab_b, blk_b], writes=[cab_b])
            S.op("dve", lambda h, d=d, b=b: h.tensor_tensor(out=cab_t[:, d, 0, :], in0=cab_t[:, d, 0, :], in1=blk_t[:, b, d, 0, :],
                                                            op=ALU.mult), reads=[cab_b, blk_b], writes=[cab_b])
    S.dma("sp", BLK[:, :, :, :, :], blk_t[:, :, :, :, :], reads=[blk_b])
    S.dma("sp", CAB[:, :, :, :], cab_t[:, :, :, :], reads=[cab_b])
    cx.pop()
    cx.pop()
    return cx.finish() if own else None


def chunk_vec(v, nch):
    return np.ascontiguousarray(np.asarray(v, np.float32).reshape(nch, 128).T)


def oa_inputs(xT, xhalo, P, pos):
    cos, sin = rope_tables(pos, 16, 128)
    return {
        "xT": np.ascontiguousarray(xT), "xhalo": np.ascontiguousarray(xhalo), "w_in": P["w_in"], "g": vec128(P["g"], 8),
        "g_cq": vec128(P["g_cq"], 2), "g_ckv": vec128(P["g_ckv"], 1), "w_uq": P["w_uq"], "w_ukv": P["w_ukv"],
        "cw": np.ascontiguousarray(P["conv_w"].reshape(4, 4, 128).transpose(2, 1, 0)),
        "cb": chunk_vec(P["conv_b"], 4), "wa": P["wa"], "wx": P["wx"],
        "ba": np.ascontiguousarray(P["ba"].reshape(2, 4, 128).transpose(2, 0, 1)),
        "bx": np.ascontiguousarray(P["bx"].reshape(2, 4, 128).transpose(2, 0, 1)),
        "lam": np.ascontiguousarray(P["lam"].reshape(2, 4, 128).transpose(2, 0, 1)),
        "cos": cos, "sin": sin, "rot": rot_matrix(32),
    }


def build_ob1(ntok=TOK, nrank=4, cx=None):
    seq = ntok * nrank
    QG = ntok // 512
    NKT = seq // 128
    NT = ntok // 128
    own = cx is None
    cx = Ctx() if own else cx
    cx.push()
    S = cx.S
    QN = cx.dram_in("QN", [512, ntok], BF16)
    QR = cx.dram_in("QR", [256, ntok], BF16)
    KNg = cx.dram_in("KNg", [8 * nrank * 64, ntok], BF16)
    KRg = cx.dram_in("KRg", [nrank * 32, ntok], BF16)
    Vg = cx.dram_in("Vg", [8 * nrank * 128, NT * 65], BF16)
    YC = cx.dram_out("YC", [512, ntok], BF16)

    q_ring = mk_ring(cx, "sb", "q", 2, [128, ntok], BF16)
    k_ring = mk_ring(cx, "sb", "k", 2, [128, seq], BF16)
    v_ring = mk_ring(cx, "sb", "v", 2, [128, NKT, 65], BF16)
    p_ring = mk_ring(cx, "sb", "p", 4, [128, 1024], BF16)
    osb_ring = mk_ring(cx, "sb", "osb", 2, [64, 512], F32)
    rc_ring = mk_ring(cx, "sb", "rc", 2, [128, 512], F32)
    yc_ring = mk_ring(cx, "sb", "yc", 2, [64, 512], BF16)
    ones32 = cx.sb("ones32", [128, 64], F32); ones32_b = Buf("ones32")
    s_ring = mk_ring(cx, "ps", "s", 3, [128, 1024], F32)
    o_ring = mk_ring(cx, "ps", "o", 2, [128, 512], F32)
    S.op("pool", lambda h: h.memset(ones32[:, :], 1.0), writes=[ones32_b])
    scale = float(96 ** -0.5)
    NKP = NKT // 2
    LA = 2

    def load_head(hd):
        q_t, q_b = q_ring.next()
        k_t, k_b = k_ring.next()
        v_t, v_b = v_ring.next()
        S.dma("sp", q_t[0:64, :], QN[hd * 64:(hd + 1) * 64, :], writes=[q_b])
        S.dma("sp", q_t[64:96, :], QR[hd * 32:(hd + 1) * 32, :], writes=[q_b])
        for r in range(nrank):
            S.dma("sp", k_t[0:64, r * ntok:(r + 1) * ntok], KNg[(hd * nrank + r) * 64:(hd * nrank + r + 1) * 64, :], writes=[k_b])
            S.dma("sp", k_t[64:96, r * ntok:(r + 1) * ntok], KRg[r * 32:(r + 1) * 32, :], writes=[k_b])
            S.dma("sp", v_t[:, r * NT:(r + 1) * NT, :],
                  Vg[(hd * nrank + r) * 128:(hd * nrank + r + 1) * 128, :].rearrange("p (i e) -> p i e", e=65), writes=[v_b])
        return (q_t, q_b, k_t, k_b, v_t, v_b)

    nxt = load_head(0)
    cx.run_hook()
    for hd in range(8):
        q_t, q_b, k_t, k_b, v_t, v_b = nxt
        if hd + 1 < 8:
            nxt = load_head(hd + 1)
        for qg in range(QG):
            o_t, o_b = o_ring.next()
            stiles = {}

            def emit_s(kp):
                s_t, s_b = s_ring.next()
                for hf in range(2):
                    kt = 2 * kp + hf
                    S.op("pe", lambda h, kt=kt, hf=hf: h.matmul(s_t[:, hf * 512:(hf + 1) * 512], lhsT=k_t[0:96, kt * 128:(kt + 1) * 128],
                                                                rhs=q_t[0:96, qg * 512:(qg + 1) * 512], start=True, stop=True),
                         reads=[k_b, q_b], writes=[s_b])
                stiles[kp] = (s_t, s_b)

            for kp in range(min(LA, NKP)):
                emit_s(kp)
            for kp in range(NKP):
                s_t, s_b = stiles.pop(kp)
                p_t, p_b = p_ring.next()
                S.op("act", lambda h, s_t=s_t, p_t=p_t: h.activation(out=p_t[:, :], in_=s_t[:, :], func=AF.Exp, scale=scale),
                     reads=[s_b], writes=[p_b])
                if kp + LA < NKP:
                    emit_s(kp + LA)
                for hf in range(2):
                    kt = 2 * kp + hf
                    S.op("pe", lambda h, kt=kt, hf=hf, p_t=p_t: h.matmul(o_t[0:65, :], lhsT=v_t[:, kt, 0:65], rhs=p_t[:, hf * 512:(hf + 1) * 512],
                                                                         start=(kt == 0), stop=(kt == NKT - 1)),
                         reads=[v_b, p_b], writes=[o_b])
            osb_t, osb_b = osb_ring.next()
            rc_t, rc_b = rc_ring.next()
            S.op("act", lambda h: h.activation(out=osb_t[:, :], in_=o_t[0:64, :], func=AF.Copy), reads=[o_b], writes=[osb_b])
            S.op("dve", lambda h: h.reciprocal(out=rc_t[64:65, :], in_=o_t[64:65, :]), reads=[o_b], writes=[rc_b])
            bc_t, bc_b = s_ring.next()
            S.op("pe", lambda h: h.matmul(bc_t[0:64, 0:512], lhsT=ones32[64:65, 0:64], rhs=rc_t[64:65, :], start=True, stop=True),
                 reads=[rc_b, ones32_b], writes=[bc_b])
            yc_t, yc_b = yc_ring.next()
            S.op("dve", lambda h: h.tensor_tensor(out=yc_t[:, :], in0=osb_t[:, :], in1=bc_t[0:64, 0:512], op=ALU.mult),
                 reads=[osb_b, bc_b], writes=[yc_b])
            S.dma("pool", YC[hd * 64:(hd + 1) * 64, qg * 512:(qg + 1) * 512], yc_t[:, :], reads=[yc_b])
    cx.pop()
    return cx.finish() if own else None


def build_ob2(ntok=TOK, ngrp=4, cx=None):
    TB = 512
    NB = ntok // TB
    own = cx is None
    cx = Ctx() if own else cx
    cx.push()
    S = cx.S
    AB = cx.dram_in("AB", [2, 2, 512, ntok])
    GX = cx.dram_in("GX", [512, ntok], BF16)
    YC = cx.dram_in("YC", [512, ntok], BF16)
    xT = cx.dram_in("xT", [D, ntok])
    w_out = cx.dram_in("w_out", [D, D])
    BLK = cx.dram_in("BLK", [128, NB, 2, 2, 4])
    CABg = cx.dram_in("CABg", [128, ngrp, 16])
    mfd = cx.dram_in("mf", [128, ngrp])
    mbd = cx.dram_in("mb", [128, ngrp])
    oT = cx.dram_out("oT", [D, ntok])

    woA = cx.sb("woA", [128, 4, D], BF16); woA_b = Buf("woA")
    woB = cx.sb("woB", [128, 4, D], BF16); woB_b = Buf("woB")
    blk_t = cx.sb("blk", [128, NB, 2, 2, 4], F32); blk_b = Buf("blk")
    cab_t = cx.sb("cab", [128, ngrp, 16], F32); cab_b = Buf("cab")
    m_t = cx.sb("m", [128, 2, ngrp], F32); m_b = Buf("m")
    hin_t = cx.sb("hin", [128, 2, 4], F32); hin_b = Buf("hin")
    tmp_t = cx.sb("tmp", [128, 4], F32); tmp_b = Buf("tmp")
    init_t = cx.sb("init", [128, NB, 2, 4], F32); init_b = Buf("init")
    ab_ring = mk_ring(cx, "sb", "ab", 2, [128, 2, 2, 4, TB], F32)
    hs_ring = mk_ring(cx, "sb", "hs", 2, [128, 2, 4, TB], F32)
    gx_ring = mk_ring(cx, "sb", "gx", 2, [128, 4, TB], BF16)
    yc_ring = mk_ring(cx, "sb", "yc", 2, [128, 4, TB], BF16)
    yd_ring = mk_ring(cx, "sb", "yd", 2, [128, 4, TB], BF16)
    x_ring = mk_ring(cx, "sb", "x", 2, [128, KC, TB], F32)
    y_ring = mk_ring(cx, "ps", "y", 3, [128, 512], F32)

    S.dma("pool", woA[:, :, :], w_out[0:512, :].rearrange("(i p) n -> p i n", p=128), writes=[woA_b])
    S.dma("pool", woB[:, :, :], w_out[512:1024, :].rearrange("(g p) n -> p g n", p=128), writes=[woB_b])
    S.dma("sp", blk_t[:, :, :, :, :], BLK[:, :, :, :, :], writes=[blk_b])
    S.dma("sp", cab_t[:, :, :], CABg[:, :, :], writes=[cab_b])
    S.dma("sp", m_t[:, 0, :], mfd[:, :], writes=[m_b])
    S.dma("sp", m_t[:, 1, :], mbd[:, :], writes=[m_b])
    S.op("pool", lambda h: h.memset(hin_t[:, :, :], 0.0), writes=[hin_b])
    for d in range(2):
        order = range(ngrp) if d == 0 else range(ngrp - 1, -1, -1)
        for i in order:
            S.op("dve", lambda h, d=d, i=i: h.tensor_tensor(out=tmp_t[:, :], in0=hin_t[:, d, :], in1=cab_t[:, i, d * 8:d * 8 + 4], op=ALU.mult),
                 reads=[hin_b, cab_b, tmp_b], writes=[tmp_b])
            S.op("dve", lambda h, d=d, i=i: h.tensor_tensor(out=tmp_t[:, :], in0=tmp_t[:, :], in1=cab_t[:, i, d * 8 + 4:d * 8 + 8], op=ALU.add),
                 reads=[tmp_b, cab_b], writes=[tmp_b])
            S.op("dve", lambda h, d=d, i=i: h.tensor_tensor(out=tmp_t[:, :], in0=tmp_t[:, :], in1=hin_t[:, d, :], op=ALU.subtract),
                 reads=[tmp_b, hin_b], writes=[tmp_b])
            S.op("dve", lambda h, d=d, i=i: h.scalar_tensor_tensor(out=hin_t[:, d, :], in0=tmp_t[:, :], scalar=m_t[:, d, i:i + 1],
                                                                   in1=hin_t[:, d, :], op0=ALU.mult, op1=ALU.add),
                 reads=[tmp_b, m_b, hin_b], writes=[hin_b])
    for d in range(2):
        order = list(range(NB)) if d == 0 else list(range(NB - 1, -1, -1))
        S.op("dve", lambda h, d=d, b0=order[0]: h.tensor_copy(out=init_t[:, b0, d, :], in_=hin_t[:, d, :]),
             reads=[hin_b, init_b], writes=[init_b])
        for bi in range(NB - 1):
            b, bn = order[bi], order[bi + 1]
            S.op("dve", lambda h, d=d, b=b, bn=bn: h.tensor_tensor(out=init_t[:, bn, d, :], in0=init_t[:, b, d, :],
                                                                   in1=blk_t[:, b, d, 0, :], op=ALU.mult),
                 reads=[init_b, blk_b], writes=[init_b])
            S.op("dve", lambda h, d=d, b=b, bn=bn: h.tensor_tensor(out=init_t[:, bn, d, :], in0=init_t[:, bn, d, :],
                                                                   in1=blk_t[:, b, d, 1, :], op=ALU.add),
                 reads=[init_b, blk_b], writes=[init_b])
    xv = xT.rearrange("(c p) t -> p c t", p=128)
    ov = oT.rearrange("(c p) t -> p c t", p=128)
    ABv = AB.rearrange("d s (c p) t -> d s p c t", p=128)
    def blk(b):
        t0 = b * TB
        ab_t, ab_b = ab_ring.next()
        for d in range(2):
            for s_ in range(2):
                S.dma("sp", ab_t[:, d, s_, :, :], ABv[d, s_, :, :, t0:t0 + TB], writes=[ab_b])
        gx_t, gx_b = gx_ring.next()
        yc_t, yc_b = yc_ring.next()
        x_t, x_b = x_ring.next()
        S.dma("sp", gx_t[:, :, :], GX.rearrange("(c p) t -> p c t", p=128)[:, :, t0:t0 + TB], writes=[gx_b])
        S.dma("sp", yc_t[:, :, :], YC.rearrange("(i p) t -> p i t", p=128)[:, :, t0:t0 + TB], writes=[yc_b])
        S.dma("sp", x_t[:, :, :], xv[:, :, t0:t0 + TB], writes=[x_b])
        yield
        hs_t, hs_b = hs_ring.next()
        for d in range(2):
            for c in range(4):
                if d == 0:
                    S.op("dve", lambda h, d=d, c=c, b=b: h.tensor_tensor_scan(
                        out=hs_t[:, d, c, :], data0=ab_t[:, d, 0, c, :], data1=ab_t[:, d, 1, c, :],
                        initial=init_t[:, b, d, c:c + 1], op0=ALU.mult, op1=ALU.add), reads=[ab_b, init_b, hs_b], writes=[hs_b])
                else:
                    S.op("dve", lambda h, d=d, c=c, b=b: h.tensor_tensor_scan(
                        out=hs_t[:, d, c, ::-1], data0=ab_t[:, d, 0, c, ::-1], data1=ab_t[:, d, 1, c, ::-1],
                        initial=init_t[:, b, d, c:c + 1], op0=ALU.mult, op1=ALU.add), reads=[ab_b, init_b, hs_b], writes=[hs_b])
        yield
        S.op("pool", lambda h: h.tensor_tensor(out=hs_t[:, 0, :, :], in0=hs_t[:, 0, :, :], in1=hs_t[:, 1, :, :], op=ALU.add),
             reads=[hs_b], writes=[hs_b])
        yd_t, yd_b = yd_ring.next()
        S.op("pool", lambda h: h.tensor_tensor(out=yd_t[:, :, :], in0=hs_t[:, 0, :, :], in1=gx_t[:, :, :], op=ALU.mult),
             reads=[hs_b, gx_b], writes=[yd_b])
        yield
        for o in range(KC):
            y_t, y_b = y_ring.next()
            for hh in range(4):
                S.op("pe", lambda h, hh=hh, o=o: h.matmul(y_t[:, :], lhsT=woA[:, hh, o * 128:(o + 1) * 128], rhs=yc_t[:, hh, :],
                                                         start=(hh == 0), stop=False), reads=[woA_b, yc_b], writes=[y_b])
            for g in range(4):
                S.op("pe", lambda h, g=g, o=o: h.matmul(y_t[:, :], lhsT=woB[:, g, o * 128:(o + 1) * 128], rhs=yd_t[:, g, :],
                                                       start=False, stop=(g == 3)), reads=[woB_b, yd_b], writes=[y_b])
            S.op("dve", lambda h, o=o: h.tensor_tensor(out=x_t[:, o, :], in0=y_t[:, :], in1=x_t[:, o, :], op=ALU.add),
                 reads=[y_b, x_b], writes=[x_b])
        S.dma("pool", ov[:, :, t0:t0 + TB], x_t[:, :, :], reads=[x_b])
        yield

    run_interleaved((blk(b) for b in range(NB)), 2, 2)
    cx.pop()
    return cx.finish() if own else None


def allgather(cx, in_ap, out_ap, groups):
    S = cx.S
    S.barrier()
    sem = S.new_sem("cc")
    cx.nc.gpsimd.collective_compute("AllGather", ALU.bypass, replica_groups=groups, ins=[in_ap], outs=[out_ap]).then_inc(sem, 1)
    for e in S.ENGS:
        S.h[e].wait_ge(sem, 1)


def allgather_many(cx, pairs, groups):
    S = cx.S
    S.barrier()
    sem = S.new_sem("ccm")
    for (in_ap, out_ap) in pairs:
        cx.nc.gpsimd.collective_compute("AllGather", ALU.bypass, replica_groups=groups, ins=[in_ap], outs=[out_ap]).then_inc(sem, 1)
    for e in S.ENGS:
        S.h[e].wait_ge(sem, len(pairs))


def emit_select(cx, src_t, src_b, nrank, m_t, m_b, side, acc_t, acc_b):
    S = cx.S
    S.op("dve", lambda h: h.tensor_scalar_mul(out=acc_t[:, :], in0=src_t[:, 0, :], scalar1=m_t[:, side, 0:1]),
         reads=[src_b, m_b], writes=[acc_b])
    for i in range(1, nrank):
        S.op("dve", lambda h, i=i: h.scalar_tensor_tensor(out=acc_t[:, :], in0=src_t[:, i, :], scalar=m_t[:, side, i:i + 1],
                                                          in1=acc_t[:, :], op0=ALU.mult, op1=ALU.add),
             reads=[src_b, m_b, acc_b], writes=[acc_b])


def emit_even_exchange(cx, KTh, Vh, UTh, pack, packg, mlr, groups, nrank, ntok):
    S = cx.S
    cx.push()
    S.dma_dd("sp", pack[:, 0:128], KTh[:, 128:256])
    S.dma_dd("sp", pack[:, 128:256], KTh[:, ntok:ntok + 128])
    S.dma_dd("sp", pack[:, 256:288].rearrange("p (g t) -> p g t", g=4), UTh[:, :, 8:16])
    S.dma_dd("sp", pack[:, 288:320].rearrange("p (g t) -> p g t", g=4), UTh[:, :, ntok:ntok + 8])
    S.dma_dd("sp", pack[:, 320:450], Vh[128:256, :])
    S.dma_dd("sp", pack[:, 450:580], Vh[ntok:ntok + 128, :])
    allgather(cx, pack[:, :], packg[:, :], groups)
    pg_t = cx.sb("pg", [128, nrank, 580], BF16); pg_b = Buf("pg")
    m_t = cx.sb("mlr", [128, 2, nrank], F32); m_b = Buf("mlr")
    accL = cx.sb("accL", [128, 580], BF16); accL_b = Buf("accL")
    accR = cx.sb("accR", [128, 580], BF16); accR_b = Buf("accR")
    S.dma("sp", pg_t[:, :, :], packg.rearrange("(r p) n -> p r n", p=128), writes=[pg_b])
    S.dma("sp", m_t[:, :, :], mlr[:, :, :], writes=[m_b])
    emit_select(cx, pg_t, pg_b, nrank, m_t, m_b, 0, accL, accL_b)
    emit_select(cx, pg_t, pg_b, nrank, m_t, m_b, 1, accR, accR_b)
    S.dma("sp", KTh[:, 0:128], accL[:, 128:256], reads=[accL_b])
    S.dma("sp", UTh[:, :, 0:8], accL[:, 288:320].rearrange("p (g t) -> p g t", g=4), reads=[accL_b])
    S.dma("sp", Vh[0:128, :], accL[:, 450:580], reads=[accL_b])
    S.dma("sp", KTh[:, 128 + ntok:256 + ntok], accR[:, 0:128], reads=[accR_b])
    S.dma("sp", UTh[:, :, 8 + ntok:16 + ntok], accR[:, 256:288].rearrange("p (g t) -> p g t", g=4), reads=[accR_b])
    S.dma("sp", Vh[128 + ntok:256 + ntok, :], accR[:, 320:450], reads=[accR_b])
    cx.pop()


def emit_xhalo_exchange(cx, xprev, xhp, xhpg, xhalo, mlr, groups, nrank, ntok):
    S = cx.S
    cx.push()
    S.dma_dd("sp", xhp[:, 0:2], xprev[:, 0:2])
    S.dma_dd("sp", xhp[:, 2:4], xprev[:, ntok - 2:ntok])
    allgather(cx, xhp[:, :], xhpg[:, :], groups)
    xg_t = cx.sb("xg", [128, nrank, 32], F32); xg_b = Buf("xg")
    m_t = cx.sb("mlr", [128, 2, nrank], F32); m_b = Buf("mlr")
    accL = cx.sb("accL", [128, 32], F32); accL_b = Buf("accL")
    accR = cx.sb("accR", [128, 32], F32); accR_b = Buf("accR")
    for r in range(nrank):
        S.dma("sp", xg_t[:, r, :].rearrange("p (c t) -> p c t", t=4),
              xhpg[r * D:(r + 1) * D, :].rearrange("(c p) t -> p c t", p=128), writes=[xg_b])
    S.dma("sp", m_t[:, :, :], mlr[:, :, :], writes=[m_b])
    emit_select(cx, xg_t, xg_b, nrank, m_t, m_b, 0, accL, accL_b)
    emit_select(cx, xg_t, xg_b, nrank, m_t, m_b, 1, accR, accR_b)
    xhv = xhalo.rearrange("(c p) t -> p c t", p=128)
    S.dma("sp", xhv[:, :, 0:2], accL[:, :].rearrange("p (c t) -> p c t", t=4)[:, :, 2:4], reads=[accL_b])
    S.dma("sp", xhv[:, :, 2:4], accR[:, :].rearrange("p (c t) -> p c t", t=4)[:, :, 0:2], reads=[accR_b])
    cx.pop()


SMALL_SPECS = None


def build_fused(B=2, nrank=4, ntok=TOK, depth=4):
    NE, NO = (depth + 1) // 2, depth // 2
    NT = ntok // 128
    NB = ntok // 512
    groups = [[b * nrank + r for r in range(nrank)] for b in range(B)]
    cx = Ctx()
    nc = cx.nc
    I = cx.ext_in
    x0 = I("xT", [D, ntok])
    Wd = {
        "e_w_in": I("e_w_in", [NE, D, 1280]), "e_w_pool": I("e_w_pool", [NE, 4, 128, 128]), "e_w_out": I("e_w_out", [NE, D, D]),
        "o_w_in": I("o_w_in", [NO, D, 1440]), "o_w_uq": I("o_w_uq", [NO, 256, 768]), "o_w_ukv": I("o_w_ukv", [NO, 128, 1024]),
        "o_lru_wa": I("o_lru_wa", [NO, 2, 8, 64, 64]), "o_lru_wx": I("o_lru_wx", [NO, 2, 8, 64, 64]), "o_w_out": I("o_w_out", [NO, D, D]),
        "w_mlp1": I("w_mlp1", [depth, D, DFF]), "w_mlp2": I("w_mlp2", [depth, DFF, D]),
        "g_mix": I("g_mix", [depth, 128, KC]), "g_mlp": I("g_mlp", [depth, 128, KC]), "g_fin": I("g_fin", [128, KC]),
        "pscale": I("pscale", [NE, 128, 4]), "sinkrow": I("sinkrow", [NE, 1, 2, 512]),
        "g_cq": I("g_cq", [NO, 128, 2]), "g_ckv": I("g_ckv", [NO, 128, 1]), "cw": I("cw", [NO, 128, 4, 4]), "cb": I("cb", [NO, 128, 4]),
        "ba": I("ba", [NO, 128, 2, 4]), "bx": I("bx", [NO, 128, 2, 4]), "lam": I("lam", [NO, 128, 2, 4]),
        "cos32": I("cos32", [128, ntok]), "sin32": I("sin32", [128, ntok]), "cos16": I("cos16", [128, ntok]), "sin16": I("sin16", [128, ntok]),
        "rot64": I("rot64", [128, 128]), "rot32": I("rot32", [128, 128]), "masks": I("masks", [4, 128, 512]),
        "ident": I("ident", [128, 128]),
        "invc": I("invc", [128, 2, 4, 16]), "mfb": I("mfb", [2, 128, nrank]), "mlr": I("mlr", [128, 2, nrank]),
    }
    outT = cx.ext_out("oT", [D, ntok])

    def tmp(name, shape, dt=F32):
        return nc.dram_tensor(name, list(shape), dt, kind="Internal").ap()

    def make_precast(layer, w1b_d, w2b_d):
        def f():
            for k in range(8):
                cx.S.dma_dd_async("pool", w1b_d[k * 128:(k + 1) * 128, :], Wd["w_mlp1"][layer][k * 128:(k + 1) * 128, :])
            for k in range(8):
                cx.S.dma_dd_async("pool", w2b_d[k * 512:(k + 1) * 512, :], Wd["w_mlp2"][layer][k * 512:(k + 1) * 512, :])
        return f

    xcur = x0
    for layer in range(depth):
        L = f"L{layer}"
        xmix = tmp(L + "_xmix", [D, ntok])
        w1b_d = tmp(L + "_w1b", [D, DFF], BF16)
        w2b_d = tmp(L + "_w2b", [DFF, D], BF16)
        if layer % 2 == 0:
            e = layer // 2
            QsT = tmp(L + "_QsT", [128, 4, ntok], BF16)
            KTh = tmp(L + "_KTh", [128, ntok + 256], BF16)
            Vh = tmp(L + "_Vh", [ntok + 256, 130], BF16)
            UTh = tmp(L + "_UTh", [128, 4, ntok + 16], BF16)
            pack = tmp(L + "_pack", [128, 580], BF16)
            packg = tmp(L + "_packg", [nrank * 128, 580], BF16)
            cx.bind = {"xT": xcur, "w_in": Wd["e_w_in"][e], "g": Wd["g_mix"][layer], "cos": Wd["cos32"], "sin": Wd["sin32"],
                       "rot": Wd["rot64"], "QsT": QsT, "KT": KTh[:, 128:128 + ntok], "Vaug": Vh[128:128 + ntok, :],
                       "UT": UTh[:, :, 8:8 + ntok]}
            build_ea(ntok, cx=cx)
            emit_even_exchange(cx, KTh, Vh, UTh, pack, packg, Wd["mlr"], groups, nrank, ntok)
            cx.bind = {"QsT": QsT, "KTh": KTh, "Vh": Vh, "UTh": UTh, "xT": xcur, "w_pool": Wd["e_w_pool"][e],
                       "pscale": Wd["pscale"][e], "w_out": Wd["e_w_out"][e], "sinkrow": Wd["sinkrow"][e], "masks": Wd["masks"],
                       "invc": Wd["invc"], "ident": Wd["ident"], "oT": xmix}
            cx.hook = make_precast(layer, w1b_d, w2b_d)
            build_eb(ntok, cx=cx)
        else:
            o = layer // 2
            xhp = tmp(L + "_xhp", [D, 4]); xhpg = tmp(L + "_xhpg", [nrank * D, 4]); xhalo = tmp(L + "_xhalo", [D, 4])
            QN = tmp(L + "_QN", [512, ntok], BF16); QR = tmp(L + "_QR", [256, ntok], BF16)
            KNR = tmp(L + "_KNR", [544, ntok], BF16); V5 = tmp(L + "_V5", [1024, NT * 65], BF16)
            GX = tmp(L + "_GX", [512, ntok], BF16); AB = tmp(L + "_AB", [2, 2, 512, ntok])
            BLK = tmp(L + "_BLK", [128, NB, 2, 2, 4]); CAB = tmp(L + "_CAB", [128, 16]); XR = tmp(L + "_XR", [512, ntok])
            KNg = tmp(L + "_KNg", [8 * nrank * 64, ntok], BF16); KRg = tmp(L + "_KRg", [nrank * 32, ntok], BF16)
            Vg = tmp(L + "_Vg", [8 * nrank * 128, NT * 65], BF16)
            CABg = tmp(L + "_CABg", [nrank * 128, 16]); YC = tmp(L + "_YC", [512, ntok], BF16)
            emit_xhalo_exchange(cx, xcur, xhp, xhpg, xhalo, Wd["mlr"], groups, nrank, ntok)
            cx.bind = {"xT": xcur, "xhalo": xhalo, "w_in": Wd["o_w_in"][o], "g": Wd["g_mix"][layer], "g_cq": Wd["g_cq"][o],
                       "g_ckv": Wd["g_ckv"][o], "w_uq": Wd["o_w_uq"][o], "w_ukv": Wd["o_w_ukv"][o], "cw": Wd["cw"][o], "cb": Wd["cb"][o],
                       "wa": Wd["o_lru_wa"][o], "wx": Wd["o_lru_wx"][o], "ba": Wd["ba"][o], "bx": Wd["bx"][o], "lam": Wd["lam"][o],
                       "cos": Wd["cos16"], "sin": Wd["sin16"], "rot": Wd["rot32"], "QN": QN, "QR": QR, "KNR": KNR, "V5": V5, "GX": GX,
                       "AB": AB, "BLK": BLK, "CAB": CAB.rearrange("p (d s c) -> p d s c", d=2, s=2), "XR": XR}
            ccsem = cx.S.new_sem("ccg")
            ncc = [0]

            def gather_kv():
                pairs = []
                for hd in range(8):
                    pairs.append((KNR[hd * 64:(hd + 1) * 64, :], KNg[hd * nrank * 64:(hd + 1) * nrank * 64, :]))
                    pairs.append((V5[hd * 128:(hd + 1) * 128, :], Vg[hd * nrank * 128:(hd + 1) * nrank * 128, :]))
                pairs.append((KNR[512:544, :], KRg[:, :]))
                for (i_ap, o_ap) in pairs:
                    nc.gpsimd.collective_compute("AllGather", ALU.bypass, replica_groups=groups, ins=[i_ap], outs=[o_ap]).then_inc(ccsem, 1)
                    ncc[0] += 1

            build_oa(ntok, cx=cx, mid_hook=gather_kv)
            nc.gpsimd.collective_compute("AllGather", ALU.bypass, replica_groups=groups, ins=[CAB[:, :]], outs=[CABg[:, :]]).then_inc(ccsem, 1)
            ncc[0] += 1
            for e_ in cx.S.ENGS:
                cx.S.h[e_].wait_ge(ccsem, ncc[0])
            cx.bind = {"QN": QN, "QR": QR, "KNg": KNg, "KRg": KRg, "Vg": Vg, "YC": YC}
            cx.hook = make_precast(layer, w1b_d, w2b_d)
            build_ob1(ntok, nrank, cx=cx)
            cx.bind = {"AB": AB, "GX": GX, "YC": YC, "xT": xcur, "w_out": Wd["o_w_out"][o], "BLK": BLK,
                       "CABg": CABg.rearrange("(r p) n -> p r n", p=128), "mf": Wd["mfb"][0], "mb": Wd["mfb"][1], "oT": xmix}
            build_ob2(ntok, nrank, cx=cx)
        last = layer == depth - 1
        xnext = outT if last else tmp(L + "_xmlp", [D, ntok])
        cx.bind = {"xT": xmix, "w1": w1b_d, "w2": w2b_d, "g": Wd["g_mlp"][layer], "gf": Wd["g_fin"], "oT": xnext}
        build_mlp(last, ntok, cx=cx, wbf16=True)
        xcur = xnext
    cx.bind = {}
    return cx.finish()


_FUSED = {}


def run_model(x, W, nrank=4, ntok=TOK):
    B, Sq, _ = x.shape
    ncore = B * nrank
    assert Sq == nrank * ntok
    depth = W["norm_mlp"].shape[0]
    NE, NO = (depth + 1) // 2, depth // 2
    key = (B, nrank, ntok, depth)
    if key not in _FUSED:
        _FUSED[key] = build_fused(B, nrank, ntok, depth)
    nc = _FUSED[key]
    f32 = lambda a: np.ascontiguousarray(np.asarray(a, np.float32))
    g_mix = np.stack([vec128(W["e_norm_mix"][l // 2] if l % 2 == 0 else W["o_norm_mix"][l // 2], 8) for l in range(depth)])
    shared = {
        "e_w_in": f32(W["e_w_in"]), "e_w_pool": f32(W["e_w_pool"]), "e_w_out": f32(W["e_w_out"]),
        "o_w_in": f32(W["o_w_in"]), "o_w_uq": f32(W["o_w_uq"]), "o_w_ukv": f32(W["o_w_ukv"]),
        "o_lru_wa": f32(W["o_lru_wa"]), "o_lru_wx": f32(W["o_lru_wx"]), "o_w_out": f32(W["o_w_out"]),
        "w_mlp1": f32(W["w_mlp1"]), "w_mlp2": f32(W["w_mlp2"]),
        "g_mix": g_mix, "g_mlp": np.stack([vec128(W["norm_mlp"][l], 8) for l in range(depth)]), "g_fin": vec128(W["final_norm"], 8),
        "pscale": np.stack([vec128(W["e_pool_scale"][e], 4) for e in range(NE)]),
        "sinkrow": np.stack([np.repeat(f32(W["e_sink"][e]).reshape(2, 4), 128, axis=1).reshape(1, 2, 512) for e in range(NE)]),
        "g_cq": np.stack([vec128(W["o_g_cq"][o], 2) for o in range(NO)]),
        "g_ckv": np.stack([vec128(W["o_g_ckv"][o], 1) for o in range(NO)]),
        "cw": np.stack([f32(f32(W["o_conv_w"][o]).reshape(4, 4, 128).transpose(2, 1, 0)) for o in range(NO)]),
        "cb": np.stack([chunk_vec(W["o_conv_b"][o], 4) for o in range(NO)]),
        "ba": np.stack([f32(f32(W["o_lru_ba"][o]).reshape(2, 4, 128).transpose(2, 0, 1)) for o in range(NO)]),
        "bx": np.stack([f32(f32(W["o_lru_bx"][o]).reshape(2, 4, 128).transpose(2, 0, 1)) for o in range(NO)]),
        "lam": np.stack([f32(f32(W["o_lru_lambda"][o]).reshape(2, 4, 128).transpose(2, 0, 1)) for o in range(NO)]),
        "rot64": rot_matrix(64), "rot32": rot_matrix(32), "ident": np.eye(128, dtype=np.float32),
    }
    in_maps = []
    for c in range(ncore):
        bi, r = c // nrank, c % nrank
        pos = r * ntok + np.arange(ntok)
        cos32, sin32 = rope_tables(pos, 32, 128)
        cos16, sin16 = rope_tables(pos, 16, 128)
        mfb = np.zeros((2, 128, nrank), np.float32); mfb[0, :, :r] = 1.0; mfb[1, :, r + 1:] = 1.0
        mlr = np.zeros((128, 2, nrank), np.float32)
        if r > 0:
            mlr[:, 0, r - 1] = 1.0
        if r < nrank - 1:
            mlr[:, 1, r + 1] = 1.0
        im = dict(shared)
        im.update({"xT": np.ascontiguousarray(x[bi, r * ntok:(r + 1) * ntok, :].T), "cos32": cos32, "sin32": sin32, "cos16": cos16,
                   "sin16": sin16, "masks": eb_masks(r > 0, r < nrank - 1), "invc": eb_invc(r == 0, r == nrank - 1),
                   "mfb": mfb, "mlr": mlr})
        in_maps.append(im)
    res = run_spmd(nc, in_maps)
    out = np.empty((B, Sq, D), np.float32)
    for c in range(ncore):
        bi, r = c // nrank, c % nrank
        out[bi, r * ntok:(r + 1) * ntok, :] = res[c]["oT"].T
    return out


def kernel(**inputs):
    W = {k: np.asarray(v) for k, v in inputs.items()}
    x = np.asarray(W.pop("x"), np.float32)
    return run_model(x, W)
```

```python
from contextlib import ExitStack
import numpy as np
import concourse.bass as bass
import concourse.mybir as mybir
from concourse.bass_utils import run_bass_kernel_spmd

F32 = mybir.dt.float32
BF16 = mybir.dt.bfloat16
ALU = mybir.AluOpType
AF = mybir.ActivationFunctionType

NCORES = 8
D = 1024
KC = 8
TOK = 4096
SEQ = 16384
EPS = 1e-6
DFF = 4096
EPOCH = 30000


class Buf:
    __slots__ = ("name", "writers", "readers", "sem_in", "sem_out", "n_in", "n_out", "excl")

    def __init__(self, name, excl=False):
        self.name = name
        self.excl = excl
        self.writers = {}
        self.readers = {}
        self.sem_in = None
        self.sem_out = None
        self.n_in = 0
        self.n_out = 0


class Sched:
    ENGS = ("pe", "act", "dve", "pool", "sp")

    def __init__(self, nc, stack):
        self.nc = nc
        self.stack = stack
        self.h = {"pe": nc.tensor, "act": nc.scalar, "dve": nc.vector, "pool": nc.gpsimd, "sp": nc.sync}
        self.ops = {e: [] for e in self.ENGS}
        self.cnt = {e: 0 for e in self.ENGS}
        self.sem = {e: None for e in self.ENGS}
        self.seen = {e: {} for e in self.ENGS}
        self.last = {e: None for e in self.ENGS}
        self.dma_toks = {}
        self.nsem = 0
        self.ninstr = 0
        self.sem_pool = []
        self.live = []
        self.ddbuf = Buf("dram2dram")

    def new_sem(self, name):
        self.nsem += 1
        return self.stack.enter_context(self.nc.semaphore(f"{name}_{self.nsem}"))

    def _eng_tok(self, e):
        if self.sem[e] is None or self.cnt[e] >= EPOCH:
            self.sem[e] = self.new_sem("e" + e)
            self.cnt[e] = 0
        self.cnt[e] += 1
        tok = (self.sem[e], self.cnt[e])
        self.last[e] = tok
        return tok

    def _waits(self, e, toks):
        need = {}
        seen = self.seen[e]
        for sem, val in toks:
            k = id(sem)
            if seen.get(k, 0) >= val:
                continue
            if k not in need or need[k][1] < val:
                need[k] = (sem, val)
        out = []
        for k, (sem, val) in need.items():
            seen[k] = val
            out.append((sem, val))
        return out

    def _deps(self, e, reads, writes):
        toks = []
        for b in reads:
            toks.extend(b.writers.values())
            if b.excl:
                toks.extend(b.readers.values())
        for b in writes:
            toks.extend(b.writers.values())
            toks.extend(b.readers.values())
        if e == "pe":
            own = id(self.sem["pe"]) if self.sem["pe"] is not None else None
            toks = [t for t in toks if id(t[0]) != own]
        return self._waits(e, toks)

    def op(self, e, fn, reads=(), writes=()):
        waits = self._deps(e, reads, writes)
        tok = self._eng_tok(e)
        for b in reads:
            b.readers[id(tok[0])] = tok
        for b in writes:
            b.readers = {}
            b.writers = {id(tok[0]): tok}
        self.ninstr += 1

        h = self.h[e]
        for sem, val in waits:
            h.wait_ge(sem, val)
        fn(h).then_inc(tok[0], 1)

    def dma(self, q, out_ap, in_ap, reads=(), writes=(), **kw):
        waits = self._deps(q, reads, writes)
        assert len(writes) + len(reads) >= 1 and len(writes) <= 1 and len(reads) <= 1
        if writes:
            b = writes[0]
            if b.sem_in is None:
                b.sem_in, b.n_in = self._take_sem("di")
                self.live.append((b, "in"))
            b.n_in += 16
            tok = (b.sem_in, b.n_in)
            b.readers = {}
            b.writers = {id(tok[0]): tok}
            for rb in reads:
                rb.readers[id(tok[0])] = tok
        else:
            b = reads[0]
            if b.sem_out is None:
                b.sem_out, b.n_out = self._take_sem("do")
                self.live.append((b, "out"))
            b.n_out += 16
            tok = (b.sem_out, b.n_out)
            b.readers[id(tok[0])] = tok
        self.dma_toks[id(tok[0])] = tok
        self.ninstr += 1
        h = self.h[q]
        for sem, val in waits:
            h.wait_ge(sem, val)
        h.dma_start(out=out_ap, in_=in_ap, **kw).then_inc(tok[0], 16)

    def _take_sem(self, name):
        if self.sem_pool:
            return self.sem_pool.pop()
        return self.new_sem(name), 0

    def release_dma_sems(self):
        for b, kind in self.live:
            if kind == "in":
                self.sem_pool.append((b.sem_in, b.n_in)); b.sem_in = None
                b.writers = {}
            else:
                self.sem_pool.append((b.sem_out, b.n_out)); b.sem_out = None
                b.readers = {}
        self.live = []

    def dma_dd(self, q, out_ap, in_ap, **kw):
        self.dma(q, out_ap, in_ap, writes=[self.ddbuf], **kw)

    def dma_dd_async(self, q, out_ap, in_ap, **kw):
        self.dma(q, out_ap, in_ap, writes=[Buf("dd_async")], **kw)

    def barrier(self):
        toks = [t for t in self.last.values() if t is not None] + list(self.dma_toks.values())
        for e in self.ENGS:
            waits = self._waits(e, toks)
            for sem, val in waits:
                self.h[e].wait_ge(sem, val)

    def finalize(self):
        self.barrier()


class Ctx:
    def __init__(self):
        self.nc = bass.Bass("TRN2", target_bir_lowering=False)
        self.stack = ExitStack()
        self.S = Sched(self.nc, self.stack)
        self.n = 0
        self.cur = self.stack
        self.scopes = []
        self.bind = {}
        self.hook = None

    def dram_in(self, name, shape, dt=F32):
        if name in self.bind:
            return self.bind[name]
        return self.nc.dram_tensor(name, list(shape), dt, kind="ExternalInput").ap()

    def dram_out(self, name, shape, dt=F32):
        if name in self.bind:
            return self.bind[name]
        return self.nc.dram_tensor(name, list(shape), dt, kind="ExternalOutput").ap()

    def ext_in(self, name, shape, dt=F32):
        return self.nc.dram_tensor(name, list(shape), dt, kind="ExternalInput").ap()

    def ext_out(self, name, shape, dt=F32):
        return self.nc.dram_tensor(name, list(shape), dt, kind="ExternalOutput").ap()

    def sb(self, name, shape, dt):
        self.n += 1
        return self.cur.enter_context(self.nc.sbuf_tensor(f"{name}_{self.n}", list(shape), dt))

    def ps(self, name, shape, dt=F32):
        self.n += 1
        return self.cur.enter_context(self.nc.psum_tensor(f"{name}_{self.n}", list(shape), dt))

    def dram_tmp(self, name, shape, dt=F32):
        return self.nc.dram_tensor(name, list(shape), dt, kind="Internal").ap()

    def run_hook(self):
        if self.hook is not None:
            f, self.hook = self.hook, None
            f()

    def push(self):
        st = ExitStack()
        self.scopes.append(st)
        self.cur = st

    def pop(self):
        self.S.barrier()
        if len(self.scopes) == 1:
            self.S.release_dma_sems()
        self.scopes.pop().close()
        self.cur = self.scopes[-1] if self.scopes else self.stack

    def finish(self):
        self.S.finalize()
        self.stack.close()
        return self.nc


class Ring:
    def __init__(self, items):
        self.items = items
        self.i = 0

    def next(self):
        it = self.items[self.i % len(self.items)]
        self.i += 1
        return it


def run_interleaved(gens, width=2, stagger=2):
    it = iter(gens)
    active = []
    steps = 0
    while True:
        while len(active) < width and (not active or steps >= stagger):
            try:
                active.append(next(it))
            except StopIteration:
                break
        if not active:
            break
        steps += 1
        for g in list(active):
            try:
                next(g)
            except StopIteration:
                active.remove(g)


def mk_ring(cx, kind, name, n, shape, dt):
    items = []
    for i in range(n):
        t = cx.sb(f"{name}{i}", shape, dt) if kind == "sb" else cx.ps(f"{name}{i}", shape, dt)
        items.append((t, Buf(f"{name}{i}", excl=(kind == "ps"))))
    return Ring(items)


def emit_rmsnorm(cx, x_t, x_b, nchunk, TB, g_t, g_b, ones_t, ones_b, sq_ring, st_ring, rstd_ring,
                 out_t, out_b, nfeat, evac_engs=("dve",)):
    S = cx.S
    st_t, st_b = st_ring.next()
    for c in range(nchunk):
        sq_t, sq_b = sq_ring.next()
        S.op("act", lambda h, c=c, sq_t=sq_t: h.activation(out=sq_t[:, 0:TB], in_=x_t[:, c, 0:TB], func=AF.Square),
             reads=[x_b], writes=[sq_b])
        S.op("pe", lambda h, c=c, sq_t=sq_t: h.matmul(st_t[:, 0:TB], lhsT=ones_t[:, :], rhs=sq_t[:, 0:TB],
                                                        start=(c == 0), stop=(c == nchunk - 1)),
             reads=[sq_b, ones_b], writes=[st_b])
    r_t, r_b = rstd_ring.next()
    S.op("act", lambda h: h.activation(out=r_t[:, 0:TB], in_=st_t[:, 0:TB], func=AF.Sqrt, bias=float(nfeat * EPS)),
         reads=[st_b], writes=[r_b])
    S.op("dve", lambda h: h.reciprocal(out=r_t[:, 0:TB], in_=r_t[:, 0:TB]), reads=[r_b], writes=[r_b])
    for c in range(nchunk):
        e = evac_engs[c % len(evac_engs)]
        S.op(e, lambda h, c=c: h.scalar_tensor_tensor(out=out_t[:, c, 0:TB], in0=x_t[:, c, 0:TB],
                                                       scalar=g_t[:, c:c + 1], in1=r_t[:, 0:TB],
                                                       op0=ALU.mult, op1=ALU.mult),
             reads=[x_b, r_b, g_b], writes=[out_b])


def build_mlp(final_norm, ntok=TOK, dbg=False, cx=None, wbf16=False):
    TB = 256
    NB = ntok // TB
    FC = DFF // 128
    own = cx is None
    cx = Ctx() if own else cx
    cx.push()
    S = cx.S
    xT = cx.dram_in("xT", [D, ntok])
    w1 = cx.dram_in("w1", [D, DFF], BF16 if wbf16 else F32)
    w2 = cx.dram_in("w2", [DFF, D], BF16 if wbf16 else F32)
    gin = cx.dram_in("g", [128, KC])
    oT = cx.dram_out("oT", [D, ntok])
    if final_norm:
        gfin = cx.dram_in("gf", [128, KC])
    if dbg:
        dh = cx.dram_out("dh", [128, KC, TB], BF16)
        da = cx.dram_out("da", [128, DFF // 128, TB], BF16)

    w1b = cx.sb("w1b", [128, KC, DFF], BF16)
    w2b = cx.sb("w2b", [128, FC, D], BF16)
    w1_bufs = [Buf(f"w1_{k}") for k in range(KC)]
    w2_bufs = [Buf(f"w2_{k}") for k in range(8)]
    g_t = cx.sb("g", [128, KC], F32); g_b = Buf("g")
    ones_t = cx.sb("ones", [128, 128], BF16); ones_b = Buf("ones")
    x_ring = mk_ring(cx, "sb", "x", 2, [128, KC, TB], F32)
    h_ring = mk_ring(cx, "sb", "h", 2, [128, KC, TB], BF16)
    a_ring = mk_ring(cx, "sb", "a", 1, [128, FC, TB], BF16)
    r_ring = mk_ring(cx, "sb", "r", 3, [128, TB], BF16)
    sq_ring = mk_ring(cx, "sb", "sq", 3, [128, TB], BF16)
    rstd_ring = mk_ring(cx, "sb", "rstd", 2, [128, TB], F32)
    o_ring = mk_ring(cx, "sb", "o", 2, [128, KC, TB], F32)
    st_ring = mk_ring(cx, "ps", "st", 1, [128, 512], F32)
    p1_ring = mk_ring(cx, "ps", "p1", 3, [128, 512], F32)
    p2_ring = mk_ring(cx, "ps", "p2", 3, [128, 512], F32)
    if final_norm:
        gf_t = cx.sb("gf", [128, KC], F32); gf_b = Buf("gf")
        f_ring = mk_ring(cx, "sb", "f", 2, [128, KC, TB], F32)

    S.dma("sp", g_t[:, :], gin[:, :], writes=[g_b])
    S.op("dve", lambda h: h.tensor_scalar_mul(out=g_t[:, :], in0=g_t[:, :], scalar1=float(np.sqrt(D))),
         reads=[g_b], writes=[g_b])
    if final_norm:
        S.dma("sp", gf_t[:, :], gfin[:, :], writes=[gf_b])
        S.op("dve", lambda h: h.tensor_scalar_mul(out=gf_t[:, :], in0=gf_t[:, :], scalar1=float(np.sqrt(D))),
             reads=[gf_b], writes=[gf_b])
    S.op("pool", lambda h: h.memset(ones_t[:, :], 1.0), writes=[ones_b])
    w1v = w1.rearrange("(k p) n -> p k n", p=128)
    w2v = w2.rearrange("(f p) n -> p f n", p=128)
    wq = "sp" if wbf16 else "pool"
    for k in range(KC):
        S.dma(wq, w1b[:, k, :], w1v[:, k, :], writes=[w1_bufs[k]])
    for j in range(8):
        S.dma(wq, w2b[:, j * 4:(j + 1) * 4, :], w2v[:, j * 4:(j + 1) * 4, :], writes=[w2_bufs[j]])
    xv = xT.rearrange("(c p) t -> p c t", p=128)
    ov = oT.rearrange("(c p) t -> p c t", p=128)

    def prep(b):
        x_t, x_b = x_ring.next()
        S.dma("sp", x_t[:, :, :], xv[:, :, b * TB:(b + 1) * TB], writes=[x_b])
        h_t, h_b = h_ring.next()
        return (x_t, x_b, h_t, h_b)

    def norm(st_):
        x_t, x_b, h_t, h_b = st_
        emit_rmsnorm(cx, x_t, x_b, KC, TB, g_t, g_b, ones_t, ones_b, sq_ring, st_ring, rstd_ring, h_t, h_b, D)

    cur = prep(0)
    norm(cur)
    for b in range(NB):
        t0 = b * TB
        x_t, x_b, h_t, h_b = cur
        nxt = prep(b + 1) if b + 1 < NB else None
        a_t, a_b = a_ring.next()
        for f in range(FC):
            if f == FC // 2 and nxt is not None:
                norm(nxt)
            p_t, p_b = p1_ring.next()
            for k in range(KC):
                S.op("pe", lambda h, f=f, k=k, p_t=p_t: h.matmul(p_t[:, 0:TB], lhsT=w1b[:, k, f * 128:(f + 1) * 128],
                                                                  rhs=h_t[:, k, 0:TB], start=(k == 0), stop=(k == KC - 1)),
                     reads=[h_b, w1_bufs[k]], writes=[p_b])
            r_t, r_b = r_ring.next()
            S.op("act", lambda h, p_t=p_t, r_t=r_t: h.activation(out=r_t[:, 0:TB], in_=p_t[:, 0:TB], func=AF.Relu),
                 reads=[p_b], writes=[r_b])
            S.op("pool", lambda h, f=f, r_t=r_t: h.tensor_tensor(out=a_t[:, f, 0:TB], in0=r_t[:, 0:TB], in1=r_t[:, 0:TB],
                                                                  op=ALU.mult),
                 reads=[r_b], writes=[a_b])
        if dbg and b == 0:
            S.dma("sp", dh[:, :, :], h_t[:, :, :], reads=[h_b])
            S.dma("sp", da[:, :, :], a_t[:, :, :], reads=[a_b])
        o_t, o_b = o_ring.next()
        for c in range(KC):
            p_t, p_b = p2_ring.next()
            for f in range(FC):
                S.op("pe", lambda h, f=f, c=c, p_t=p_t: h.matmul(p_t[:, 0:TB], lhsT=w2b[:, f, c * 128:(c + 1) * 128],
                                                                  rhs=a_t[:, f, 0:TB], start=(f == 0), stop=(f == FC - 1)),
                     reads=[a_b, w2_bufs[f // 4]], writes=[p_b])
            S.op("dve", lambda h, c=c, p_t=p_t: h.tensor_tensor(out=o_t[:, c, 0:TB], in0=p_t[:, 0:TB], in1=x_t[:, c, 0:TB],
                                                                 op=ALU.add),
                 reads=[p_b, x_b], writes=[o_b])
        if final_norm:
            f_t, f_b = f_ring.next()
            emit_rmsnorm(cx, o_t, o_b, KC, TB, gf_t, gf_b, ones_t, ones_b, sq_ring, st_ring, rstd_ring, f_t, f_b, D)
            S.dma("pool", ov[:, :, t0:t0 + TB], f_t[:, :, :], reads=[f_b])
        else:
            S.dma("pool", ov[:, :, t0:t0 + TB], o_t[:, :, :], reads=[o_b])
        cur = nxt
    cx.pop()
    return cx.finish() if own else None


def run_spmd(nc, in_maps):
    res = run_bass_kernel_spmd(nc, in_maps, core_ids=list(range(len(in_maps))))
    return res.results


def vec128(v, k):
    return np.ascontiguousarray(np.asarray(v, np.float32).reshape(k, 128).T)


def load_cast(cx, q, dst_ap, src_ap, buf):
    cx.S.dma(q, dst_ap, src_ap, writes=[buf])


def build_ea(ntok=TOK, parts='quv', qlvl=4, cx=None):
    TB = 512
    NB = ntok // TB
    own = cx is None
    cx = Ctx() if own else cx
    cx.push()
    S = cx.S
    xT = cx.dram_in("xT", [D, ntok])
    w_in = cx.dram_in("w_in", [D, 1280])
    gin = cx.dram_in("g", [128, KC])
    cosd = cx.dram_in("cos", [128, ntok])
    sind = cx.dram_in("sin", [128, ntok])
    rotd = cx.dram_in("rot", [128, 128])
    QsT = cx.dram_out("QsT", [128, 4, ntok], BF16)
    KT = cx.dram_out("KT", [128, ntok], BF16)
    Vaug = cx.dram_out("Vaug", [ntok, 130], BF16)
    UT = cx.dram_out("UT", [128, 4, ntok], BF16)

    wb = cx.sb("wb", [128, KC, 1280], BF16)
    w_bufs = [Buf(f"w{k}") for k in range(KC)]
    g_t = cx.sb("g", [128, KC], F32); g_b = Buf("g")
    ones_t = cx.sb("ones", [128, 128], BF16); ones_b = Buf("ones")
    rot_t = cx.sb("rot", [128, 128], BF16); rot_b = Buf("rot")
    x_ring = mk_ring(cx, "sb", "x", 2, [128, KC, TB], F32)
    h_ring = mk_ring(cx, "sb", "h", 2, [128, KC, TB], BF16)
    sq_ring = mk_ring(cx, "sb", "sq", 3, [128, TB], BF16)
    rstd_ring = mk_ring(cx, "sb", "rstd", 2, [128, TB], F32)
    cos_ring = mk_ring(cx, "sb", "cos", 2, [128, TB], F32)
    sin_ring = mk_ring(cx, "sb", "sin", 2, [128, TB], F32)
    qb_ring = mk_ring(cx, "sb", "qb", 2, [128, TB], BF16)
    t1_ring = mk_ring(cx, "sb", "t1", 2, [128, TB], F32)
    t2_ring = mk_ring(cx, "sb", "t2", 2, [128, TB], F32)
    qo_ring = mk_ring(cx, "sb", "qo", 2, [128, 5, TB], BF16)
    uo_ring = mk_ring(cx, "sb", "uo", 2, [128, 4, TB], BF16)
    vo_ring = mk_ring(cx, "sb", "vo", 2, [128, 4, 130], BF16)
    st_ring = mk_ring(cx, "ps", "st", 1, [128, 512], F32)
    pq_ring = mk_ring(cx, "ps", "pq", 3, [128, 512], F32)
    pr_ring = mk_ring(cx, "ps", "pr", 2, [128, 512], F32)
    pv_ring = mk_ring(cx, "ps", "pv", 2, [128, 512], F32)

    S.dma("sp", g_t[:, :], gin[:, :], writes=[g_b])
    S.op("dve", lambda h: h.tensor_scalar_mul(out=g_t[:, :], in0=g_t[:, :], scalar1=float(np.sqrt(D))),
         reads=[g_b], writes=[g_b])
    S.op("pool", lambda h: h.memset(ones_t[:, :], 1.0), writes=[ones_b])
    S.dma("pool", rot_t[:, :], rotd[:, :], writes=[rot_b])
    for (vt, vb) in vo_ring.items:
        S.op("pool", lambda h, vt=vt: h.memset(vt[:, :, :], 1.0), writes=[vb])
    for k in range(KC):
        for j in range(2):
            src = w_in[k * 128:(k + 1) * 128, j * 256:(j + 1) * 256].rearrange("p (c d) -> p c d", c=4, d=64)
            dst = wb[:, k, 0:512].rearrange("p (c j d) -> p c j d", c=4, j=2, d=64)[:, :, j, :]
            S.dma("pool", dst, src, writes=[w_bufs[k]])
        S.dma("pool", wb[:, k, 512:1280], w_in[k * 128:(k + 1) * 128, 512:1280], writes=[w_bufs[k]])
    xv = xT.rearrange("(c p) t -> p c t", p=128)

    def blk(b):
        t0 = b * TB
        x_t, x_b = x_ring.next()
        S.dma("sp", x_t[:, :, :], xv[:, :, t0:t0 + TB], writes=[x_b])
        cos_t, cos_b = cos_ring.next()
        sin_t, sin_b = sin_ring.next()
        S.dma("sp", cos_t[:, :], cosd[:, t0:t0 + TB], writes=[cos_b])
        S.dma("sp", sin_t[:, :], sind[:, t0:t0 + TB], writes=[sin_b])
        h_t, h_b = h_ring.next()
        emit_rmsnorm(cx, x_t, x_b, KC, TB, g_t, g_b, ones_t, ones_b, sq_ring, st_ring, rstd_ring, h_t, h_b, D)
        yield
        qo_t, qo_b = qo_ring.next()
        for c in (range(5) if 'q' in parts else []):
            pq_t, pq_b = pq_ring.next()
            for k in range(KC):
                S.op("pe", lambda h, c=c, k=k, pq_t=pq_t, h_t=h_t: h.matmul(
                    pq_t[:, 0:TB], lhsT=wb[:, k, c * 128:(c + 1) * 128], rhs=h_t[:, k, :],
                    start=(k == 0), stop=(k == KC - 1)), reads=[h_b, w_bufs[k]], writes=[pq_b])
            qb_t, qb_b = qb_ring.next()
            S.op("act", lambda h, pq_t=pq_t, qb_t=qb_t: h.activation(out=qb_t[:, :], in_=pq_t[:, 0:TB], func=AF.Copy),
                 reads=[pq_b], writes=[qb_b])
            if qlvl == 1:
                S.op("act", lambda h, c=c, pq_t=pq_t, qo_t=qo_t: h.activation(out=qo_t[:, c, :], in_=pq_t[:, 0:TB], func=AF.Copy),
                     reads=[pq_b], writes=[qo_b])
                continue
            pr_t, pr_b = pr_ring.next()
            S.op("pe", lambda h, pr_t=pr_t, qb_t=qb_t: h.matmul(pr_t[:, 0:TB], lhsT=rot_t[:, :], rhs=qb_t[:, :],
                                                               start=True, stop=True),
                 reads=[qb_b, rot_b], writes=[pr_b])
            t1_t, t1_b = t1_ring.next()
            t2_t, t2_b = t2_ring.next()
            if qlvl == 2:
                S.op("act", lambda h, c=c, pr_t=pr_t, qo_t=qo_t: h.activation(out=qo_t[:, c, :], in_=pr_t[:, 0:TB], func=AF.Copy),
                     reads=[pr_b], writes=[qo_b])
                continue
            S.op("dve", lambda h, t1_t=t1_t, pq_t=pq_t, cos_t=cos_t: h.tensor_tensor(
                out=t1_t[:, :], in0=pq_t[:, 0:TB], in1=cos_t[:, :], op=ALU.mult), reads=[pq_b, cos_b], writes=[t1_b])
            if qlvl == 3:
                S.op("act", lambda h, c=c, t1_t=t1_t, qo_t=qo_t: h.activation(out=qo_t[:, c, :], in_=t1_t[:, :], func=AF.Copy),
                     reads=[t1_b], writes=[qo_b])
                continue
            S.op("dve", lambda h, t2_t=t2_t, pr_t=pr_t, sin_t=sin_t: h.tensor_tensor(
                out=t2_t[:, :], in0=pr_t[:, 0:TB], in1=sin_t[:, :], op=ALU.mult), reads=[pr_b, sin_b], writes=[t2_b])
            S.op("dve", lambda h, c=c, qo_t=qo_t, t1_t=t1_t, t2_t=t2_t: h.tensor_tensor(
                out=qo_t[:, c, :], in0=t1_t[:, :], in1=t2_t[:, :], op=ALU.add), reads=[t1_b, t2_b], writes=[qo_b])
        if 'q' in parts:
            S.dma("pool", QsT[:, :, t0:t0 + TB], qo_t[:, 0:4, :], reads=[qo_b])
            S.dma("pool", KT[:, t0:t0 + TB], qo_t[:, 4, :], reads=[qo_b])
        yield
        uo_t, uo_b = uo_ring.next()
        for gi in (range(4) if 'u' in parts else []):
            pq_t, pq_b = pq_ring.next()
            for k in range(KC):
                S.op("pe", lambda h, gi=gi, k=k, pq_t=pq_t, h_t=h_t: h.matmul(
                    pq_t[:, 0:TB], lhsT=wb[:, k, 768 + gi * 128:768 + (gi + 1) * 128], rhs=h_t[:, k, :],
                    start=(k == 0), stop=(k == KC - 1)), reads=[h_b, w_bufs[k]], writes=[pq_b])
            S.op("act", lambda h, gi=gi, pq_t=pq_t, uo_t=uo_t: h.activation(out=uo_t[:, gi, :], in_=pq_t[:, 0:TB], func=AF.Copy),
                 reads=[pq_b], writes=[uo_b])
        if 'u' in parts:
            S.dma("pool", UT[:, :, t0:t0 + TB], uo_t[:, :, :], reads=[uo_b])
        if 'v' not in parts:
            return
        yield
        vo_t, vo_b = vo_ring.next()
        pv_t, pv_b = pv_ring.next()
        for ti in range(TB // 128):
            for k in range(KC):
                S.op("pe", lambda h, ti=ti, k=k, pv_t=pv_t, h_t=h_t: h.matmul(
                    pv_t[:, ti * 128:(ti + 1) * 128], lhsT=h_t[:, k, ti * 128:(ti + 1) * 128], rhs=wb[:, k, 640:768],
                    start=(k == 0), stop=(k == KC - 1)), reads=[h_b, w_bufs[k]], writes=[pv_b])
        for ti in range(TB // 128):
            for j in range(2):
                S.op("act", lambda h, ti=ti, j=j, pv_t=pv_t, vo_t=vo_t: h.activation(
                    out=vo_t[:, ti, j * 65:j * 65 + 64], in_=pv_t[:, ti * 128 + j * 64:ti * 128 + (j + 1) * 64], func=AF.Copy),
                    reads=[pv_b], writes=[vo_b])
        S.dma("pool", Vaug[t0:t0 + TB, :].rearrange("(i p) n -> p i n", p=128), vo_t[:, :, :], reads=[vo_b])
        yield

    run_interleaved((blk(b) for b in range(NB)), 2, 2)
    cx.pop()
    return cx.finish() if own else None


def rope_tables(pos, half, nrows):
    inv = (np.float32(10000.0) ** (-np.arange(half, dtype=np.float32) / np.float32(half))).astype(np.float32)
    ang = pos.astype(np.float32)[None, :] * inv[np.arange(nrows) % half][:, None]
    return np.cos(ang).astype(np.float32), np.sin(ang).astype(np.float32)


def rot_matrix(dh, nrows=128):
    R = np.zeros((nrows, nrows), np.float32)
    half = dh // 2
    for m in range(nrows):
        d = m % dh
        base = m - d
        if d < half:
            R[base + d + half, m] = -1.0
        else:
            R[base + d - half, m] = 1.0
    return R


def build_eb(ntok=TOK, cx=None):
    TB = 512
    NB = ntok // TB
    NT = ntok // 128
    own = cx is None
    cx = Ctx() if own else cx
    cx.push()
    S = cx.S
    QsT = cx.dram_in("QsT", [128, 4, ntok], BF16)
    KTh = cx.dram_in("KTh", [128, ntok + 256], BF16)
    Vh = cx.dram_in("Vh", [ntok + 256, 130], BF16)
    UTh = cx.dram_in("UTh", [128, 4, ntok + 16], BF16)
    xT = cx.dram_in("xT", [D, ntok])
    w_pool = cx.dram_in("w_pool", [4, 128, 128])
    pscale = cx.dram_in("pscale", [128, 4])
    w_out = cx.dram_in("w_out", [D, D])
    sinkrow = cx.dram_in("sinkrow", [1, 2, 512])
    masksd = cx.dram_in("masks", [4, 128, 512])
    invcd = cx.dram_in("invc", [128, 2, 4, 16])
    identd = cx.dram_in("ident", [128, 128])
    oT = cx.dram_out("oT", [D, ntok])

    woA = cx.sb("woA", [128, 4, D], BF16); woA_b = Buf("woA")
    woB = cx.sb("woB", [128, 4, D], BF16); woB_b = Buf("woB")
    wp = cx.sb("wp", [128, 4, 128], BF16); wp_b = Buf("wp")
    ps_t = cx.sb("ps", [128, 4], F32); ps_b = Buf("ps")
    mk_t = cx.sb("mk", [128, 4, 512], BF16); mk_b = Buf("mk")
    id_t = cx.sb("ident", [128, 128], BF16); id_b = Buf("ident")
    invc_t = cx.sb("invc", [128, 2, 4, 16], F32); invc_b = Buf("invc")
    sk_t = cx.sb("sk", [1, 2, 512], F32); sk_b = Buf("sk")
    esk_t = cx.sb("esk", [1, 2, 512], BF16); esk_b = Buf("esk")
    sel_t = cx.sb("sel", [1, 128], BF16); sel_b = Buf("sel")
    ones32 = cx.sb("ones32", [128, 64], F32); ones32_b = Buf("ones32")
    qsA_ring = mk_ring(cx, "sb", "qsA", 2, [128, 4, TB], BF16)
    qsB_ring = mk_ring(cx, "sb", "qsB", 2, [128, 4, TB], BF16)
    kt_ring = mk_ring(cx, "sb", "kt", 2, [128, 6 * 128], BF16)
    v_ring = mk_ring(cx, "sb", "v", 2, [128, 6, 130], BF16)
    u_ring = mk_ring(cx, "sb", "u", 2, [128, 4, TB + 16], BF16)
    x_ring = mk_ring(cx, "sb", "x", 2, [128, KC, TB], F32)
    p_ring = mk_ring(cx, "sb", "p", 4, [128, 512], BF16)
    osb_ring = mk_ring(cx, "sb", "osb", 3, [128, 512], F32)
    rc_ring = mk_ring(cx, "sb", "rc", 3, [128, 512], F32)
    ya_ring = mk_ring(cx, "sb", "ya", 2, [64, 8, TB], BF16)
    yp_ring = mk_ring(cx, "sb", "yp", 2, [128, 4, TB], BF16)
    yb_ring = mk_ring(cx, "sb", "yb", 2, [128, 4, TB], BF16)
    d_ring = mk_ring(cx, "sb", "d", 2, [128, 4, TB], BF16)
    tmp_rings = [mk_ring(cx, "sb", f"tp{g}", 2, [128, TB + 16], F32) for g in range(4)]
    e16_ring = mk_ring(cx, "sb", "e16", 2, [128, 16], F32)
    s_ring = mk_ring(cx, "ps", "s", 3, [128, 512], F32)
    o_ring = mk_ring(cx, "ps", "o", 2, [128, 512], F32)
    bc_ring = mk_ring(cx, "ps", "bc", 1, [128, 512], F32)
    y_ring = mk_ring(cx, "ps", "y", 2, [128, 512], F32)

    S.dma("pool", woA[:, :, :], w_out[0:512, :].rearrange("(i p) n -> p i n", p=128), writes=[woA_b])
    S.dma("pool", woB[:, :, :], w_out[512:1024, :].rearrange("(g p) n -> p g n", p=128), writes=[woB_b])
    S.dma("pool", wp[:, :, :], w_pool.rearrange("g i j -> i g j"), writes=[wp_b])
    S.dma("pool", mk_t[:, :, :], masksd.rearrange("m p n -> p m n"), writes=[mk_b])
    S.op("dve", lambda h: h.tensor_scalar(out=mk_t[:, :, :], in0=mk_t[:, :, :], scalar1=-1.0, scalar2=30000.0, op0=ALU.add, op1=ALU.mult),
         reads=[mk_b], writes=[mk_b])
    S.dma("pool", id_t[:, :], identd[:, :], writes=[id_b])
    S.dma("sp", ps_t[:, :], pscale[:, :], writes=[ps_b])
    S.dma("sp", invc_t[:, :, :, :], invcd[:, :, :, :], writes=[invc_b])
    S.dma("sp", sk_t[:, :, :], sinkrow[:, :, :], writes=[sk_b])
    S.op("act", lambda h: h.activation(out=esk_t[:, :, :], in_=sk_t[:, :, :], func=AF.Exp), reads=[sk_b], writes=[esk_b])
    S.op("pool", lambda h: h.memset(sel_t[:, :], 0.0), writes=[sel_b])
    S.op("pool", lambda h: h.memset(sel_t[:, 64:65], 1.0), writes=[sel_b])
    S.op("pool", lambda h: h.memset(ones32[:, :], 1.0), writes=[ones32_b])
    for (qt, qb_) in qsA_ring.items:
        S.op("pool", lambda h, qt=qt: h.memset(qt[64:128, :, :], 0.0), writes=[qb_])
    for (qt, qb_) in qsB_ring.items:
        S.op("pool", lambda h, qt=qt: h.memset(qt[0:64, :, :], 0.0), writes=[qb_])
    xv = xT.rearrange("(c p) t -> p c t", p=128)
    ov = oT.rearrange("(c p) t -> p c t", p=128)
    cx.run_hook()

    def blk(b):
        t0 = b * TB
        qsA_t, qsA_b = qsA_ring.next()
        qsB_t, qsB_b = qsB_ring.next()
        kt_t, kt_b = kt_ring.next()
        v_t, v_b = v_ring.next()
        u_t, u_b = u_ring.next()
        x_t, x_b = x_ring.next()
        S.dma("sp", qsA_t[0:64, :, :], QsT[0:64, :, t0:t0 + TB], writes=[qsA_b])
        S.dma("sp", qsB_t[64:128, :, :], QsT[64:128, :, t0:t0 + TB], writes=[qsB_b])
        S.dma("sp", kt_t[:, :], KTh[:, t0:t0 + 768], writes=[kt_b])
        S.dma("sp", v_t[:, :, :], Vh[t0:t0 + 768, :].rearrange("(i p) n -> p i n", p=128), writes=[v_b])
        S.dma("sp", u_t[:, :, :], UTh[:, :, t0:t0 + TB + 16], writes=[u_b])
        S.dma("sp", x_t[:, :, :], xv[:, :, t0:t0 + TB], writes=[x_b])
        yield
        ya_t, ya_b = ya_ring.next()
        tiles = [(nl, j, mi, dm) for nl in range(4) for mi, dm in enumerate((-1, 0, 1)) for j in range(2)]
        LA = 2
        st = {}
        unit_o = {}
        deferred = []

        def emit_S(t):
            nl, j, mi, dm = tiles[t]
            i = nl + dm + 1
            s_t, s_b = s_ring.next()
            q_t, q_b = (qsA_t, qsA_b) if j == 0 else (qsB_t, qsB_b)
            S.op("pe", lambda h: h.matmul(s_t[:, :], lhsT=kt_t[:, i * 128:(i + 1) * 128],
                                          rhs=q_t[:, :, nl * 128:(nl + 1) * 128], start=True, stop=(dm == 0)),
                 reads=[kt_b, q_b], writes=[s_b])
            if dm != 0:
                n_ = 4 * b + nl
                if dm == -1:
                    mi_ = 2 if n_ == 0 else 0
                else:
                    mi_ = 3 if n_ == NT - 1 else 1
                S.op("pe", lambda h: h.matmul(s_t[:, :], lhsT=id_t[:, :], rhs=mk_t[:, mi_, :], start=False, stop=True),
                     reads=[id_b, mk_b], writes=[s_b])
            st[t] = (s_t, s_b)

        def flush_deferred():
            while deferred:
                (o_t, o_b, osb_t, osb_b, rc_t, rc_b, nl, j) = deferred.pop(0)
                S.op("act", lambda h: h.activation(out=rc_t[64:65, :], in_=osb_t[64:65, :], func=AF.Ln), reads=[osb_b], writes=[rc_b])
                S.op("act", lambda h: h.activation(out=rc_t[64:65, :], in_=rc_t[64:65, :], func=AF.Exp, scale=-1.0),
                     reads=[rc_b], writes=[rc_b])
                bc_t, bc_b = bc_ring.next()
                S.op("pe", lambda h: h.matmul(bc_t[0:64, :], lhsT=ones32[64:65, 0:64], rhs=rc_t[64:65, :], start=True, stop=True),
                     reads=[rc_b, ones32_b], writes=[bc_b])
                S.op("dve", lambda h: h.tensor_tensor(
                    out=ya_t[0:64, j * 4:(j + 1) * 4, nl * 128:(nl + 1) * 128],
                    in0=osb_t[0:64, :].rearrange("p (c q) -> p c q", c=4),
                    in1=bc_t[0:64, :].rearrange("p (c q) -> p c q", c=4), op=ALU.mult),
                    reads=[osb_b, bc_b], writes=[ya_b])

        for t in range(min(LA, len(tiles))):
            emit_S(t)
        for t in range(len(tiles)):
            nl, j, mi, dm = tiles[t]
            n = 4 * b + nl
            i = nl + dm + 1
            if mi == 0:
                unit_o[(nl, j)] = o_ring.next()
            o_t, o_b = unit_o[(nl, j)]
            s_t, s_b = st.pop(t)
            p_t, p_b = p_ring.next()
            S.op("act", lambda h: h.activation(out=p_t[:, :], in_=s_t[:, :], func=AF.Exp, scale=0.125), reads=[s_b], writes=[p_b])
            if t + LA < len(tiles):
                emit_S(t + LA)
            S.op("pe", lambda h: h.matmul(o_t[0:65, :], lhsT=v_t[:, i, j * 65:(j + 1) * 65], rhs=p_t[:, :], start=(mi == 0), stop=False),
                 reads=[v_b, p_b], writes=[o_b])
            if mi == 0 and j == 1:
                flush_deferred()
            if mi == 2:
                S.op("pe", lambda h: h.matmul(o_t[0:65, :], lhsT=sel_t[0:1, 0:65], rhs=esk_t[0:1, j, :], start=False, stop=True),
                     reads=[sel_b, esk_b], writes=[o_b])
                osb_t, osb_b = osb_ring.next()
                rc_t, rc_b = rc_ring.next()
                S.op("dve", lambda h: h.tensor_copy(out=osb_t[0:65, :], in_=o_t[0:65, :]), reads=[o_b], writes=[osb_b])
                deferred.append((o_t, o_b, osb_t, osb_b, rc_t, rc_b, nl, j))
        flush_deferred()
        yield
        yp_t, yp_b = yp_ring.next()
        S.dma("sp", yp_t[0:64, :, :], ya_t[0:64, 0:8:2, :], reads=[ya_b], writes=[yp_b])
        S.dma("sp", yp_t[64:128, :, :], ya_t[0:64, 1:8:2, :], reads=[ya_b], writes=[yp_b])
        d_t, d_b = d_ring.next()
        L = TB + 16
        for g in range(4):
            w = 2 << g
            steps = g + 1
            src_t, src_b, ln = None, None, L
            for s_i in range(steps):
                sh = 1 << s_i
                tp_t, tp_b = tmp_rings[g].next()
                nl_ = ln - sh
                if s_i == 0:
                    S.op("pool", lambda h, tp_t=tp_t, u_t=u_t, g=g, nl_=nl_, sh=sh: h.tensor_tensor(
                        out=tp_t[:, 0:nl_], in0=u_t[:, g, 0:nl_], in1=u_t[:, g, sh:sh + nl_], op=ALU.add),
                        reads=[u_b], writes=[tp_b])
                else:
                    S.op("pool", lambda h, tp_t=tp_t, src_t=src_t, nl_=nl_, sh=sh: h.tensor_tensor(
                        out=tp_t[:, 0:nl_], in0=src_t[:, 0:nl_], in1=src_t[:, sh:sh + nl_], op=ALU.add),
                        reads=[src_b], writes=[tp_b])
                src_t, src_b, ln = tp_t, tp_b, nl_
            off = 8 - w // 2
            S.op("dve", lambda h, d_t=d_t, src_t=src_t, u_t=u_t, g=g, off=off, w=w: h.scalar_tensor_tensor(
                out=d_t[:, g, :], in0=src_t[:, off:off + TB], scalar=1.0 / w, in1=u_t[:, g, 8:8 + TB],
                op0=ALU.mult, op1=ALU.subtract), reads=[src_b, u_b], writes=[d_b])
            for (is_edge, which, c0) in ((b == 0, 0, 0), (b == NB - 1, 1, TB - 16)):
                if not is_edge:
                    continue
                e_t, e_b = e16_ring.next()
                S.op("dve", lambda h, e_t=e_t, src_t=src_t, g=g, off=off, c0=c0, which=which: h.tensor_tensor(
                    out=e_t[:, :], in0=src_t[:, off + c0:off + c0 + 16], in1=invc_t[:, which, g, :], op=ALU.mult),
                    reads=[src_b, invc_b], writes=[e_b])
                S.op("dve", lambda h, e_t=e_t, d_t=d_t, u_t=u_t, g=g, c0=c0: h.tensor_tensor(
                    out=d_t[:, g, c0:c0 + 16], in0=e_t[:, :], in1=u_t[:, g, 8 + c0:8 + c0 + 16], op=ALU.subtract),
                    reads=[e_b, u_b, d_b], writes=[d_b])
        yield
        yb_t, yb_b = yb_ring.next()
        for g in range(4):
            y_t, y_b = y_ring.next()
            S.op("pe", lambda h, y_t=y_t, d_t=d_t, g=g: h.matmul(y_t[:, :], lhsT=wp[:, g, :], rhs=d_t[:, g, :], start=True, stop=True),
                 reads=[wp_b, d_b], writes=[y_b])
            S.op("dve", lambda h, y_t=y_t, yb_t=yb_t, g=g: h.tensor_scalar_mul(out=yb_t[:, g, :], in0=y_t[:, :], scalar1=ps_t[:, g:g + 1]),
                 reads=[y_b, ps_b], writes=[yb_b])
        for o in range(KC):
            y_t, y_b = y_ring.next()
            for hh in range(4):
                S.op("pe", lambda h, y_t=y_t, yp_t=yp_t, hh=hh, o=o: h.matmul(
                    y_t[:, :], lhsT=woA[:, hh, o * 128:(o + 1) * 128], rhs=yp_t[:, hh, :], start=(hh == 0), stop=False),
                    reads=[woA_b, yp_b], writes=[y_b])
            for g in range(4):
                S.op("pe", lambda h, y_t=y_t, yb_t=yb_t, g=g, o=o: h.matmul(
                    y_t[:, :], lhsT=woB[:, g, o * 128:(o + 1) * 128], rhs=yb_t[:, g, :], start=False, stop=(g == 3)),
                    reads=[woB_b, yb_b], writes=[y_b])
            S.op("dve", lambda h, y_t=y_t, x_t=x_t, o=o: h.tensor_tensor(out=x_t[:, o, :], in0=y_t[:, :], in1=x_t[:, o, :], op=ALU.add),
                 reads=[y_b, x_b], writes=[x_b])
        S.dma("sp", ov[:, :, t0:t0 + TB], x_t[:, :, :], reads=[x_b])
        yield

    run_interleaved((blk(b) for b in range(NB)), 2, 2)
    cx.pop()
    return cx.finish() if own else None


def eb_masks(has_left, has_right):
    ki = np.arange(128)[:, None]
    qi = np.arange(128)[None, :]
    mL = np.tile((ki >= qi).astype(np.float32), (1, 4))
    mR = np.tile((ki <= qi).astype(np.float32), (1, 4))
    return np.stack([mL, mR, mL * float(has_left), mR * float(has_right)]).astype(np.float32)


def eb_invc(is_first, is_last):
    out = np.zeros((128, 2, 4, 16), np.float32)
    for g in range(4):
        w = 2 << g
        half = w // 2
        for i in range(16):
            c0 = min(i + half, w) if is_first else w
            r = 16 - i
            c1 = min(half + r, w) if is_last else w
            out[:, 0, g, i] = 1.0 / c0
            out[:, 1, g, i] = 1.0 / c1
    return out


def emit_rope(cx, src_t, src_b, nrow, TB, rot_t, rot_b, cos_t, cos_b, sin_t, sin_b, qb_ring, pr_ring, t1_ring, t2_ring,
              out_ap, out_b):
    S = cx.S
    qb_t, qb_b = qb_ring.next()
    S.op("act", lambda h: h.activation(out=qb_t[0:nrow, :], in_=src_t[0:nrow, 0:TB], func=AF.Copy), reads=[src_b], writes=[qb_b])
    pr_t, pr_b = pr_ring.next()
    S.op("pe", lambda h: h.matmul(pr_t[0:nrow, 0:TB], lhsT=rot_t[0:nrow, 0:nrow], rhs=qb_t[0:nrow, :], start=True, stop=True),
         reads=[qb_b, rot_b], writes=[pr_b])
    t1_t, t1_b = t1_ring.next()
    t2_t, t2_b = t2_ring.next()
    S.op("dve", lambda h: h.tensor_tensor(out=t1_t[0:nrow, :], in0=src_t[0:nrow, 0:TB], in1=cos_t[0:nrow, :], op=ALU.mult),
         reads=[src_b, cos_b], writes=[t1_b])
    S.op("dve", lambda h: h.tensor_tensor(out=t2_t[0:nrow, :], in0=pr_t[0:nrow, 0:TB], in1=sin_t[0:nrow, :], op=ALU.mult),
         reads=[pr_b, sin_b], writes=[t2_b])
    S.op("dve", lambda h: h.tensor_tensor(out=out_ap, in0=t1_t[0:nrow, :], in1=t2_t[0:nrow, :], op=ALU.add),
         reads=[t1_b, t2_b], writes=[out_b])


def build_oa(ntok=TOK, cx=None, mid_hook=None):
    TB = 512
    NB = ntok // TB
    NT = ntok // 128
    own = cx is None
    cx = Ctx() if own else cx
    cx.push()
    S = cx.S
    xT = cx.dram_in("xT", [D, ntok])
    xhalo = cx.dram_in("xhalo", [D, 4])
    w_in = cx.dram_in("w_in", [D, 1440])
    gin = cx.dram_in("g", [128, KC])
    gcq = cx.dram_in("g_cq", [128, 2])
    gckv = cx.dram_in("g_ckv", [128, 1])
    w_uq = cx.dram_in("w_uq", [256, 768])
    w_ukv = cx.dram_in("w_ukv", [128, 1024])
    cwd = cx.dram_in("cw", [128, 4, 4])
    cbd = cx.dram_in("cb", [128, 4])
    wad = cx.dram_in("wa", [2, 8, 64, 64])
    wxd = cx.dram_in("wx", [2, 8, 64, 64])
    bad = cx.dram_in("ba", [128, 2, 4])
    bxd = cx.dram_in("bx", [128, 2, 4])
    lamd = cx.dram_in("lam", [128, 2, 4])
    cosd = cx.dram_in("cos", [128, ntok])
    sind = cx.dram_in("sin", [128, ntok])
    rotd = cx.dram_in("rot", [128, 128])
    QN = cx.dram_out("QN", [512, ntok], BF16)
    QR = cx.dram_out("QR", [256, ntok], BF16)
    KNR = cx.dram_out("KNR", [544, ntok], BF16)
    V5 = cx.dram_out("V5", [1024, NT * 65], BF16)
    GX = cx.dram_out("GX", [512, ntok], BF16)
    AB = cx.dram_out("AB", [2, 2, 512, ntok])
    BLK = cx.dram_out("BLK", [128, NB, 2, 2, 4])
    CAB = cx.dram_out("CAB", [128, 2, 2, 4])
    XR = cx.dram_out("XR", [512, ntok])

    g_t = cx.sb("g", [128, KC], F32); g_b = Buf("g")
    gcq_t = cx.sb("gcq", [128, 2], F32); gcq_b = Buf("gcq")
    gckv_t = cx.sb("gckv", [128, 1], F32); gckv_b = Buf("gckv")
    ones_t = cx.sb("ones", [128, 128], BF16); ones_b = Buf("ones")
    xrh_t = cx.sb("xrh", [128, 4, 4], F32); xrh_b = Buf("xrh")
    cp_t = cx.sb("cp", [128, 2, 4], F32); cp_b = Buf("cp")
    blk_t = cx.sb("blk", [128, NB, 2, 2, 4], F32); blk_b = Buf("blk")
    S.dma("sp", g_t[:, :], gin[:, :], writes=[g_b])
    S.op("dve", lambda h: h.tensor_scalar_mul(out=g_t[:, :], in0=g_t[:, :], scalar1=float(np.sqrt(D))), reads=[g_b], writes=[g_b])
    S.dma("sp", gcq_t[:, :], gcq[:, :], writes=[gcq_b])
    S.op("dve", lambda h: h.tensor_scalar_mul(out=gcq_t[:, :], in0=gcq_t[:, :], scalar1=16.0), reads=[gcq_b], writes=[gcq_b])
    S.dma("sp", gckv_t[:, :], gckv[:, :], writes=[gckv_b])
    S.op("dve", lambda h: h.tensor_scalar_mul(out=gckv_t[:, :], in0=gckv_t[:, :], scalar1=float(np.sqrt(128.0))),
         reads=[gckv_b], writes=[gckv_b])
    S.op("pool", lambda h: h.memset(ones_t[:, :], 1.0), writes=[ones_b])
    S.dma("sp", cp_t[:, :, :], lamd[:, :, :], writes=[cp_b])
    S.op("act", lambda h: h.activation(out=cp_t[:, :, :], in_=cp_t[:, :, :], func=AF.Exp, scale=-1.0), reads=[cp_b], writes=[cp_b])
    S.op("act", lambda h: h.activation(out=cp_t[:, :, :], in_=cp_t[:, :, :], func=AF.Ln, bias=1.0), reads=[cp_b], writes=[cp_b])
    S.op("dve", lambda h: h.tensor_scalar_mul(out=cp_t[:, :, :], in0=cp_t[:, :, :], scalar1=-8.0), reads=[cp_b], writes=[cp_b])

    wabd = cx.sb("wabd", [128, 2, 4, 128], BF16); wxbd = cx.sb("wxbd", [128, 2, 4, 128], BF16); bd_b = Buf("bd")
    cw_t = cx.sb("cw", [128, 4, 4], F32); cb_t = cx.sb("cb", [128, 4], F32); cw_b = Buf("cw")
    ba_t = cx.sb("ba", [128, 2, 4], F32); bx_t = cx.sb("bx", [128, 2, 4], F32); bb_b = Buf("bb")
    S.op("pool", lambda h: h.memset(wabd[:, :, :, :], 0.0), writes=[bd_b])
    S.op("pool", lambda h: h.memset(wxbd[:, :, :, :], 0.0), writes=[bd_b])
    for d in range(2):
        for c in range(4):
            for hf in range(2):
                S.dma("pool", wabd[hf * 64:(hf + 1) * 64, d, c, hf * 64:(hf + 1) * 64], wad[d, 2 * c + hf, :, :], writes=[bd_b])
                S.dma("pool", wxbd[hf * 64:(hf + 1) * 64, d, c, hf * 64:(hf + 1) * 64], wxd[d, 2 * c + hf, :, :], writes=[bd_b])
    S.dma("sp", cw_t[:, :, :], cwd[:, :, :], writes=[cw_b])
    S.dma("sp", cb_t[:, :], cbd[:, :], writes=[cw_b])
    S.dma("sp", ba_t[:, :, :], bad[:, :, :], writes=[bb_b])
    S.dma("sp", bx_t[:, :, :], bxd[:, :, :], writes=[bb_b])
    xv = xT.rearrange("(c p) t -> p c t", p=128)
    cx.push()
    wb = cx.sb("wb", [128, KC, 1440], BF16)
    w_bufs = [Buf(f"w{k}") for k in range(KC)]
    wuqn = cx.sb("wuqn", [128, 2, 512], BF16); wuqr = cx.sb("wuqr", [128, 2, 256], BF16); wuq_b = Buf("wuq")
    wk = cx.sb("wk", [128, 512], BF16); wv = cx.sb("wv", [128, 512], BF16); wkv_b = Buf("wkv")
    rot_t = cx.sb("rot", [128, 128], BF16); rot_b = Buf("rot")
    x_ring = mk_ring(cx, "sb", "x", 2, [128, KC, TB], F32)
    h_ring = mk_ring(cx, "sb", "h", 2, [128, KC, TB], BF16)
    sq_ring = mk_ring(cx, "sb", "sq", 3, [128, TB], BF16)
    rstd_ring = mk_ring(cx, "sb", "rstd", 2, [128, TB], F32)
    cos_ring = mk_ring(cx, "sb", "cos", 2, [128, TB], F32)
    sin_ring = mk_ring(cx, "sb", "sin", 2, [128, TB], F32)
    qb_ring = mk_ring(cx, "sb", "qb", 2, [128, TB], BF16)
    t1_ring = mk_ring(cx, "sb", "t1", 2, [128, TB], F32)
    t2_ring = mk_ring(cx, "sb", "t2", 2, [128, TB], F32)
    cq_ring = mk_ring(cx, "sb", "cq", 2, [128, 2, TB], F32)
    ckv_ring = mk_ring(cx, "sb", "ckv", 2, [128, 1, TB], F32)
    cqn_ring = mk_ring(cx, "sb", "cqn", 2, [128, 2, TB], BF16)
    ckvn_ring = mk_ring(cx, "sb", "ckvn", 2, [128, 1, TB], BF16)
    xr_ring = mk_ring(cx, "sb", "xr", 2, [128, 4, TB], F32)
    gx_ring = mk_ring(cx, "sb", "gx", 2, [128, 4, TB], BF16)
    qn_ring = mk_ring(cx, "sb", "qn", 2, [128, 4, TB], BF16)
    qr_ring = mk_ring(cx, "sb", "qr", 2, [128, 2, TB], BF16)
    kn_ring = mk_ring(cx, "sb", "kn", 2, [128, 4, TB], BF16)
    kr_ring = mk_ring(cx, "sb", "kr", 2, [32, TB], BF16)
    vo_ring = mk_ring(cx, "sb", "vo", 2, [128, 4, 520], BF16)
    hx_t = cx.sb("hx", [128, KC, 4], F32); hx_b = Buf("hx")
    hh_t = cx.sb("hh", [128, KC, 4], BF16); hh_b = Buf("hh")
    st_ring = mk_ring(cx, "ps", "st", 1, [128, 512], F32)
    pq_ring = mk_ring(cx, "ps", "pq", 4, [128, 512], F32)
    pr_ring = mk_ring(cx, "ps", "pr", 1, [128, 512], F32)
    pv_ring = mk_ring(cx, "ps", "pv", 2, [128, 512], F32)

    for k in range(KC):
        S.dma("pool", wb[:, k, :], w_in[k * 128:(k + 1) * 128, :], writes=[w_bufs[k]])
    for k in range(2):
        src = w_uq[k * 128:(k + 1) * 128, :].rearrange("p (h e) -> p h e", e=96)
        S.dma("pool", wuqn[:, k, :].rearrange("p (h d) -> p h d", d=64), src[:, :, 0:64], writes=[wuq_b])
        S.dma("pool", wuqr[:, k, :].rearrange("p (h d) -> p h d", d=32), src[:, :, 64:96], writes=[wuq_b])
    srckv = w_ukv.rearrange("p (h e) -> p h e", e=128)
    S.dma("pool", wk[:, :].rearrange("p (h d) -> p h d", d=64), srckv[:, :, 0:64], writes=[wkv_b])
    S.dma("pool", wv[:, :].rearrange("p (h d) -> p h d", d=64), srckv[:, :, 64:128], writes=[wkv_b])
    S.dma("pool", rot_t[:, :], rotd[:, :], writes=[rot_b])
    for (vt, vb) in vo_ring.items:
        S.op("pool", lambda h, vt=vt: h.memset(vt[:, :, :], 1.0), writes=[vb])

    def proj_tile(h_t, h_b, c0, ncols, TBx):
        pq_t, pq_b = pq_ring.next()
        for k in range(KC):
            S.op("pe", lambda h, k=k: h.matmul(pq_t[0:ncols, 0:TBx], lhsT=wb[:, k, c0:c0 + ncols], rhs=h_t[:, k, 0:TBx],
                                               start=(k == 0), stop=(k == KC - 1)), reads=[h_b, w_bufs[k]], writes=[pq_b])
        return pq_t, pq_b

    S.dma("sp", hx_t[:, :, :], xhalo.rearrange("(c p) t -> p c t", p=128), writes=[hx_b])
    emit_rmsnorm(cx, hx_t, hx_b, KC, 4, g_t, g_b, ones_t, ones_b, sq_ring, st_ring, rstd_ring, hh_t, hh_b, D)
    for c in range(4):
        pq_t, pq_b = proj_tile(hh_t, hh_b, 416 + c * 128, 128, 4)
        S.op("act", lambda h, c=c, pq_t=pq_t: h.activation(out=xrh_t[:, c, :], in_=pq_t[:, 0:4], func=AF.Copy),
             reads=[pq_b], writes=[xrh_b])

    def blk(b):
        t0 = b * TB
        x_t, x_b = x_ring.next()
        S.dma("sp", x_t[:, :, :], xv[:, :, t0:t0 + TB], writes=[x_b])
        cos_t, cos_b = cos_ring.next()
        sin_t, sin_b = sin_ring.next()
        S.dma("sp", cos_t[:, :], cosd[:, t0:t0 + TB], writes=[cos_b])
        S.dma("sp", sin_t[:, :], sind[:, t0:t0 + TB], writes=[sin_b])
        h_t, h_b = h_ring.next()
        emit_rmsnorm(cx, x_t, x_b, KC, TB, g_t, g_b, ones_t, ones_b, sq_ring, st_ring, rstd_ring, h_t, h_b, D)
        yield
        cq_t, cq_b = cq_ring.next()
        for c in range(2):
            pq_t, pq_b = proj_tile(h_t, h_b, c * 128, 128, TB)
            S.op("act", lambda h, c=c, pq_t=pq_t, cq_t=cq_t: h.activation(out=cq_t[:, c, :], in_=pq_t[:, 0:TB], func=AF.Copy),
                 reads=[pq_b], writes=[cq_b])
        ckv_t, ckv_b = ckv_ring.next()
        pq_t, pq_b = proj_tile(h_t, h_b, 256, 128, TB)
        S.op("act", lambda h, pq_t=pq_t, ckv_t=ckv_t: h.activation(out=ckv_t[:, 0, :], in_=pq_t[:, 0:TB], func=AF.Copy),
             reads=[pq_b], writes=[ckv_b])
        yield
        pq_t, pq_b = proj_tile(h_t, h_b, 384, 32, TB)
        kr_t, kr_b = kr_ring.next()
        emit_rope(cx, pq_t, pq_b, 32, TB, rot_t, rot_b, cos_t, cos_b, sin_t, sin_b, qb_ring, pr_ring, t1_ring, t2_ring,
                  kr_t[0:32, :], kr_b)
        S.dma("pool", KNR[512:544, t0:t0 + TB], kr_t[:, :], reads=[kr_b])
        yield
        xr_t, xr_b = xr_ring.next()
        gx_t, gx_b = gx_ring.next()
        for c in range(4):
            pq_t, pq_b = proj_tile(h_t, h_b, 416 + c * 128, 128, TB)
            S.op("act", lambda h, c=c, pq_t=pq_t, xr_t=xr_t: h.activation(out=xr_t[:, c, :], in_=pq_t[:, 0:TB], func=AF.Copy),
                 reads=[pq_b], writes=[xr_b])
        for c in range(4):
            pq_t, pq_b = proj_tile(h_t, h_b, 928 + c * 128, 128, TB)
            S.op("act", lambda h, c=c, pq_t=pq_t, gx_t=gx_t: h.activation(out=gx_t[:, c, :], in_=pq_t[:, 0:TB], func=AF.Gelu_apprx_tanh),
                 reads=[pq_b], writes=[gx_b])
        S.dma("pool", XR.rearrange("(c p) t -> p c t", p=128)[:, :, t0:t0 + TB], xr_t[:, :, :], reads=[xr_b])
        S.dma("pool", GX.rearrange("(c p) t -> p c t", p=128)[:, :, t0:t0 + TB], gx_t[:, :, :], reads=[gx_b])
        yield
        cqn_t, cqn_b = cqn_ring.next()
        emit_rmsnorm(cx, cq_t, cq_b, 2, TB, gcq_t, gcq_b, ones_t, ones_b, sq_ring, st_ring, rstd_ring, cqn_t, cqn_b, 256)
        ckvn_t, ckvn_b = ckvn_ring.next()
        emit_rmsnorm(cx, ckv_t, ckv_b, 1, TB, gckv_t, gckv_b, ones_t, ones_b, sq_ring, st_ring, rstd_ring, ckvn_t, ckvn_b, 128)
        yield
        qn_t, qn_b = qn_ring.next()
        for i in range(4):
            pq_t, pq_b = pq_ring.next()
            for k in range(2):
                S.op("pe", lambda h, i=i, k=k, pq_t=pq_t, cqn_t=cqn_t: h.matmul(
                    pq_t[:, 0:TB], lhsT=wuqn[:, k, i * 128:(i + 1) * 128], rhs=cqn_t[:, k, :], start=(k == 0), stop=(k == 1)),
                    reads=[cqn_b, wuq_b], writes=[pq_b])
            S.op("act", lambda h, i=i, pq_t=pq_t, qn_t=qn_t: h.activation(out=qn_t[:, i, :], in_=pq_t[:, 0:TB], func=AF.Copy),
                 reads=[pq_b], writes=[qn_b])
        S.dma("pool", QN.rearrange("(c p) t -> p c t", p=128)[:, :, t0:t0 + TB], qn_t[:, :, :], reads=[qn_b])
        qr_t, qr_b = qr_ring.next()
        for i in range(2):
            pq_t, pq_b = pq_ring.next()
            for k in range(2):
                S.op("pe", lambda h, i=i, k=k, pq_t=pq_t, cqn_t=cqn_t: h.matmul(
                    pq_t[:, 0:TB], lhsT=wuqr[:, k, i * 128:(i + 1) * 128], rhs=cqn_t[:, k, :], start=(k == 0), stop=(k == 1)),
                    reads=[cqn_b, wuq_b], writes=[pq_b])
            emit_rope(cx, pq_t, pq_b, 128, TB, rot_t, rot_b, cos_t, cos_b, sin_t, sin_b, qb_ring, pr_ring, t1_ring, t2_ring,
                      qr_t[:, i, :], qr_b)
        S.dma("pool", QR.rearrange("(c p) t -> p c t", p=128)[:, :, t0:t0 + TB], qr_t[:, :, :], reads=[qr_b])
        yield
        kn_t, kn_b = kn_ring.next()
        for i in range(4):
            pq_t, pq_b = pq_ring.next()
            S.op("pe", lambda h, i=i, pq_t=pq_t, ckvn_t=ckvn_t: h.matmul(
                pq_t[:, 0:TB], lhsT=wk[:, i * 128:(i + 1) * 128], rhs=ckvn_t[:, 0, :], start=True, stop=True),
                reads=[ckvn_b, wkv_b], writes=[pq_b])
            S.op("act", lambda h, i=i, pq_t=pq_t, kn_t=kn_t: h.activation(out=kn_t[:, i, :], in_=pq_t[:, 0:TB], func=AF.Copy),
                 reads=[pq_b], writes=[kn_b])
        S.dma("pool", KNR[0:512, :].rearrange("(c p) t -> p c t", p=128)[:, :, t0:t0 + TB], kn_t[:, :, :], reads=[kn_b])
        yield
        vo_t, vo_b = vo_ring.next()
        for ti in range(TB // 128):
            pv_t, pv_b = pv_ring.next()
            S.op("pe", lambda h, ti=ti, pv_t=pv_t, ckvn_t=ckvn_t: h.matmul(
                pv_t[:, :], lhsT=ckvn_t[:, 0, ti * 128:(ti + 1) * 128], rhs=wv[:, :], start=True, stop=True),
                reads=[ckvn_b, wkv_b], writes=[pv_b])
            S.op("act", lambda h, ti=ti, pv_t=pv_t, vo_t=vo_t: h.activation(
                out=vo_t[:, ti, :].rearrange("p (h e) -> p h e", e=65)[:, :, 0:64],
                in_=pv_t[:, :].rearrange("p (h d) -> p h d", d=64), func=AF.Copy), reads=[pv_b], writes=[vo_b])
        for hd in range(8):
            S.dma("pool", V5[hd * 128:(hd + 1) * 128, :].rearrange("p (i e) -> p i e", e=65)[:, b * 4:(b + 1) * 4, :],
                  vo_t[:, :, hd * 65:(hd + 1) * 65], reads=[vo_b])
        yield

    run_interleaved((blk(b) for b in range(NB)), 2)
    cx.pop()
    if mid_hook is not None:
        mid_hook()

    cx.push()
    xe_ring = mk_ring(cx, "sb", "xe", 2, [128, 4, TB + 4], F32)
    xc_ring = mk_ring(cx, "sb", "xc", 2, [128, 4, TB], F32)
    xcb_ring = mk_ring(cx, "sb", "xcb", 2, [128, 4, TB], BF16)
    r_ring = mk_ring(cx, "sb", "r", 2, [128, 8, TB], F32)
    i_ring = mk_ring(cx, "sb", "i", 2, [128, 8, TB], F32)
    a_ring = mk_ring(cx, "sb", "a", 2, [128, 8, TB], F32)
    b_ring = mk_ring(cx, "sb", "b", 2, [128, 8, TB], F32)
    hl_ring = mk_ring(cx, "sb", "hl", 2, [128, TB], F32)
    sr_ring = mk_ring(cx, "sb", "sr", 2, [128, 8], F32)
    pg_ring = mk_ring(cx, "ps", "pg", 6, [128, 512], F32)
    XRv = XR.rearrange("(c p) t -> p c t", p=128)
    ABv = AB.rearrange("d s (c p) t -> d s p c t", p=128)

    def blk(b):
        t0 = b * TB
        xe_t, xe_b = xe_ring.next()
        lo = 0 if b > 0 else 2
        hi = TB + 3 if b < NB - 1 else TB + 2
        S.dma("sp", xe_t[:, :, lo:hi], XRv[:, :, t0 - 2 + lo:t0 - 2 + hi], writes=[xe_b])
        if b == 0:
            S.op("dve", lambda h, xe_t=xe_t: h.tensor_copy(out=xe_t[:, :, 0:2], in_=xrh_t[:, :, 0:2]), reads=[xrh_b, xe_b], writes=[xe_b])
        if b == NB - 1:
            S.op("dve", lambda h, xe_t=xe_t: h.tensor_copy(out=xe_t[:, :, TB + 2:TB + 3], in_=xrh_t[:, :, 2:3]),
                 reads=[xrh_b, xe_b], writes=[xe_b])
        yield
        xc_t, xc_b = xc_ring.next()
        xcb_t, xcb_b = xcb_ring.next()
        for c in range(4):
            S.op("dve", lambda h, c=c, xc_t=xc_t, xe_t=xe_t: h.tensor_scalar(
                out=xc_t[:, c, :], in0=xe_t[:, c, 0:TB], scalar1=cw_t[:, c, 0:1], scalar2=cb_t[:, c:c + 1],
                op0=ALU.mult, op1=ALU.add), reads=[xe_b, cw_b], writes=[xc_b])
            for j in range(1, 4):
                S.op("dve", lambda h, c=c, j=j, xc_t=xc_t, xe_t=xe_t: h.scalar_tensor_tensor(
                    out=xc_t[:, c, :], in0=xe_t[:, c, j:j + TB], scalar=cw_t[:, c, j:j + 1], in1=xc_t[:, c, :],
                    op0=ALU.mult, op1=ALU.add), reads=[xe_b, cw_b, xc_b], writes=[xc_b])
        S.op("act", lambda h, xc_t=xc_t, xcb_t=xcb_t: h.activation(out=xcb_t[:, :, :], in_=xc_t[:, :, :], func=AF.Copy), reads=[xc_b], writes=[xcb_b])
        yield
        r_t, r_b = r_ring.next()
        i_t, i_b = i_ring.next()
        a_t, a_b = a_ring.next()
        b_t, b_b = b_ring.next()
        sr_t, sr_b = sr_ring.next()
        S.op("dve", lambda h, sr_t=sr_t: h.memset(sr_t[:, :], 0.0), writes=[sr_b])
        for d in range(2):
            for c in range(4):
                q = d * 4 + c
                pg_t, pg_b = pg_ring.next()
                S.op("pe", lambda h, d=d, c=c, pg_t=pg_t, xcb_t=xcb_t: h.matmul(pg_t[:, :], lhsT=wabd[:, d, c, :], rhs=xcb_t[:, c, :],
                                                                         start=True, stop=True), reads=[bd_b, xcb_b], writes=[pg_b])
                S.op("act", lambda h, d=d, c=c, q=q, pg_t=pg_t, r_t=r_t, sr_t=sr_t: h.activation(
                    out=r_t[:, q, :], in_=pg_t[:, :], func=AF.Sigmoid, bias=ba_t[:, d, c:c + 1], accum_out=sr_t[:, q:q + 1]),
                    reads=[pg_b, bb_b], writes=[r_b, sr_b])
                pg_t, pg_b = pg_ring.next()
                S.op("pe", lambda h, d=d, c=c, pg_t=pg_t, xcb_t=xcb_t: h.matmul(pg_t[:, :], lhsT=wxbd[:, d, c, :], rhs=xcb_t[:, c, :],
                                                                         start=True, stop=True), reads=[bd_b, xcb_b], writes=[pg_b])
                S.op("act", lambda h, d=d, c=c, q=q, pg_t=pg_t, i_t=i_t: h.activation(
                    out=i_t[:, q, :], in_=pg_t[:, :], func=AF.Sigmoid, bias=bx_t[:, d, c:c + 1]),
                    reads=[pg_b, bb_b], writes=[i_b])
        yield
        for d in range(2):
            for c in range(4):
                q = d * 4 + c
                S.op("act", lambda h, d=d, c=c, q=q, a_t=a_t, r_t=r_t: h.activation(
                    out=a_t[:, q, :], in_=r_t[:, q, :], func=AF.Exp, scale=cp_t[:, d, c:c + 1]), reads=[r_b, cp_b], writes=[a_b])
                S.op("act", lambda h, d=d, c=c, q=q, sr_t=sr_t, b=b: h.activation(
                    out=blk_t[:, b, d, 0, c:c + 1], in_=sr_t[:, q:q + 1], func=AF.Exp, scale=cp_t[:, d, c:c + 1]),
                    reads=[sr_b, cp_b, blk_b], writes=[blk_b])
        yield
        S.op("dve", lambda h, a_t=a_t, r_t=r_t: h.tensor_tensor(out=r_t[:, :, :], in0=a_t[:, :, :], in1=a_t[:, :, :], op=ALU.mult),
             reads=[a_b, r_b], writes=[r_b])
        S.op("act", lambda h, r_t=r_t: h.activation(out=r_t[:, :, :], in_=r_t[:, :, :], func=AF.Sqrt, scale=-1.0, bias=1.0),
             reads=[r_b], writes=[r_b])
        for d in range(2):
            S.op("dve", lambda h, d=d, i_t=i_t, xc_t=xc_t: h.tensor_tensor(out=i_t[:, d * 4:(d + 1) * 4, :], in0=i_t[:, d * 4:(d + 1) * 4, :],
                                                                        in1=xc_t[:, :, :], op=ALU.mult), reads=[i_b, xc_b], writes=[i_b])
        S.op("dve", lambda h, b_t=b_t, r_t=r_t, i_t=i_t: h.tensor_tensor(out=b_t[:, :, :], in0=r_t[:, :, :], in1=i_t[:, :, :], op=ALU.mult),
             reads=[r_b, i_b], writes=[b_b])
        yield
        for d in range(2):
            for c in range(4):
                q = d * 4 + c
                hl_t, hl_b = hl_ring.next()
                if d == 0:
                    S.op("dve", lambda h, q=q, hl_t=hl_t, a_t=a_t, b_t=b_t: h.tensor_tensor_scan(
                        out=hl_t[:, :], data0=a_t[:, q, :], data1=b_t[:, q, :], initial=0.0, op0=ALU.mult, op1=ALU.add),
                        reads=[a_b, b_b], writes=[hl_b])
                    col = TB - 1
                else:
                    S.op("dve", lambda h, q=q, hl_t=hl_t, a_t=a_t, b_t=b_t: h.tensor_tensor_scan(
                        out=hl_t[:, ::-1], data0=a_t[:, q, ::-1], data1=b_t[:, q, ::-1], initial=0.0, op0=ALU.mult, op1=ALU.add),
                        reads=[a_b, b_b], writes=[hl_b])
                    col = 0
                S.op("act", lambda h, d=d, c=c, hl_t=hl_t, col=col, b=b: h.activation(
                    out=blk_t[:, b, d, 1, c:c + 1], in_=hl_t[:, col:col + 1], func=AF.Copy), reads=[hl_b, blk_b], writes=[blk_b])
        for d in range(2):
            S.dma("act", ABv[d, 0, :, :, t0:t0 + TB], a_t[:, d * 4:(d + 1) * 4, :], reads=[a_b])
            S.dma("sp", ABv[d, 1, :, :, t0:t0 + TB], b_t[:, d * 4:(d + 1) * 4, :], reads=[b_b])
        yield

    run_interleaved((blk(b) for b in range(NB)), 2)
    cab_t = cx.sb("cab", [128, 2, 2, 4], F32); cab_b = Buf("cab")
    for d in range(2):
        S.op("dve", lambda h, d=d: h.memset(cab_t[:, d, 0, :], 1.0), writes=[cab_b])
        S.op("dve", lambda h, d=d: h.memset(cab_t[:, d, 1, :], 0.0), writes=[cab_b])
        order = range(NB) if d == 0 else range(NB - 1, -1, -1)
        for b in order:
            S.op("dve", lambda h, d=d, b=b: h.tensor_tensor(out=cab_t[:, d, 1, :], in0=cab_t[:, d, 1, :], in1=blk_t[:, b, d, 0, :],
                                                            op=ALU.mult), reads=[cab_b, blk_b], writes=[cab_b])
            S.op("dve", lambda h, d=d, b=b: h.tensor_tensor(out=cab_t[:, d, 1, :], in0=cab_t[:, d, 1, :], in1=blk_t[:, b, d, 1, :],
                                                            op=ALU.add), reads=[cab_b, blk_b], writes=[cab_b])
            S.op("dve", lambda h, d=d, b=b: h.tensor_tensor(out=cab_t[:, d, 0, :], in0=cab_t[:, d, 0, :], in1=blk_t[:, b, d, 0, :],
                                                            op=ALU.mult), reads=[cab_b, blk_b], writes=[cab_b])
    S.dma("sp", BLK[:, :, :, :, :], blk_t[:, :, :, :, :], reads=[blk_b])
    S.dma("sp", CAB[:, :, :, :], cab_t[:, :, :, :], reads=[cab_b])
    cx.pop()
    cx.pop()
    return cx.finish() if own else None


def chunk_vec(v, nch):
    return np.ascontiguousarray(np.asarray(v, np.float32).reshape(nch, 128).T)


def oa_inputs(xT, xhalo, P, pos):
    cos, sin = rope_tables(pos, 16, 128)
    return {
        "xT": np.ascontiguousarray(xT), "xhalo": np.ascontiguousarray(xhalo), "w_in": P["w_in"], "g": vec128(P["g"], 8),
        "g_cq": vec128(P["g_cq"], 2), "g_ckv": vec128(P["g_ckv"], 1), "w_uq": P["w_uq"], "w_ukv": P["w_ukv"],
        "cw": np.ascontiguousarray(P["conv_w"].reshape(4, 4, 128).transpose(2, 1, 0)),
        "cb": chunk_vec(P["conv_b"], 4), "wa": P["wa"], "wx": P["wx"],
        "ba": np.ascontiguousarray(P["ba"].reshape(2, 4, 128).transpose(2, 0, 1)),
        "bx": np.ascontiguousarray(P["bx"].reshape(2, 4, 128).transpose(2, 0, 1)),
        "lam": np.ascontiguousarray(P["lam"].reshape(2, 4, 128).transpose(2, 0, 1)),
        "cos": cos, "sin": sin, "rot": rot_matrix(32),
    }


def build_ob1(ntok=TOK, nrank=4, cx=None):
    seq = ntok * nrank
    QG = ntok // 512
    NKT = seq // 128
    NT = ntok // 128
    own = cx is None
    cx = Ctx() if own else cx
    cx.push()
    S = cx.S
    QN = cx.dram_in("QN", [512, ntok], BF16)
    QR = cx.dram_in("QR", [256, ntok], BF16)
    KNg = cx.dram_in("KNg", [8 * nrank * 64, ntok], BF16)
    KRg = cx.dram_in("KRg", [nrank * 32, ntok], BF16)
    Vg = cx.dram_in("Vg", [8 * nrank * 128, NT * 65], BF16)
    YC = cx.dram_out("YC", [512, ntok], BF16)

    q_ring = mk_ring(cx, "sb", "q", 2, [128, ntok], BF16)
    k_ring = mk_ring(cx, "sb", "k", 2, [128, seq], BF16)
    v_ring = mk_ring(cx, "sb", "v", 2, [128, NKT, 65], BF16)
    p_ring = mk_ring(cx, "sb", "p", 4, [128, 1024], BF16)
    osb_ring = mk_ring(cx, "sb", "osb", 2, [64, 512], F32)
    rc_ring = mk_ring(cx, "sb", "rc", 2, [128, 512], F32)
    yc_ring = mk_ring(cx, "sb", "yc", 2, [64, 512], BF16)
    ones32 = cx.sb("ones32", [128, 64], F32); ones32_b = Buf("ones32")
    s_ring = mk_ring(cx, "ps", "s", 3, [128, 1024], F32)
    o_ring = mk_ring(cx, "ps", "o", 2, [128, 512], F32)
    S.op("pool", lambda h: h.memset(ones32[:, :], 1.0), writes=[ones32_b])
    scale = float(96 ** -0.5)
    NKP = NKT // 2
    LA = 2

    def load_head(hd):
        q_t, q_b = q_ring.next()
        k_t, k_b = k_ring.next()
        v_t, v_b = v_ring.next()
        S.dma("sp", q_t[0:64, :], QN[hd * 64:(hd + 1) * 64, :], writes=[q_b])
        S.dma("sp", q_t[64:96, :], QR[hd * 32:(hd + 1) * 32, :], writes=[q_b])
        for r in range(nrank):
            kr0 = ((hd // 2) * nrank + r) * 128 + (hd % 2) * 64
            S.dma("sp", k_t[0:64, r * ntok:(r + 1) * ntok], KNg[kr0:kr0 + 64, :], writes=[k_b])
            S.dma("sp", k_t[64:96, r * ntok:(r + 1) * ntok], KRg[r * 32:(r + 1) * 32, :], writes=[k_b])
            S.dma("sp", v_t[:, r * NT:(r + 1) * NT, :],
                  Vg[(hd * nrank + r) * 128:(hd * nrank + r + 1) * 128, :].rearrange("p (i e) -> p i e", e=65), writes=[v_b])
        return (q_t, q_b, k_t, k_b, v_t, v_b)

    nxt = load_head(0)
    cx.run_hook()
    for hd in range(8):
        q_t, q_b, k_t, k_b, v_t, v_b = nxt
        if hd + 1 < 8:
            nxt = load_head(hd + 1)
        for qg in range(QG):
            o_t, o_b = o_ring.next()
            stiles = {}

            def emit_s(kp):
                s_t, s_b = s_ring.next()
                for hf in range(2):
                    kt = 2 * kp + hf
                    S.op("pe", lambda h, kt=kt, hf=hf: h.matmul(s_t[:, hf * 512:(hf + 1) * 512], lhsT=k_t[0:96, kt * 128:(kt + 1) * 128],
                                                                rhs=q_t[0:96, qg * 512:(qg + 1) * 512], start=True, stop=True),
                         reads=[k_b, q_b], writes=[s_b])
                stiles[kp] = (s_t, s_b)

            for kp in range(min(LA, NKP)):
                emit_s(kp)
            for kp in range(NKP):
                s_t, s_b = stiles.pop(kp)
                p_t, p_b = p_ring.next()
                S.op("act", lambda h, s_t=s_t, p_t=p_t: h.activation(out=p_t[:, :], in_=s_t[:, :], func=AF.Exp, scale=scale),
                     reads=[s_b], writes=[p_b])
                if kp + LA < NKP:
                    emit_s(kp + LA)
                for hf in range(2):
                    kt = 2 * kp + hf
                    S.op("pe", lambda h, kt=kt, hf=hf, p_t=p_t: h.matmul(o_t[0:65, :], lhsT=v_t[:, kt, 0:65], rhs=p_t[:, hf * 512:(hf + 1) * 512],
                                                                         start=(kt == 0), stop=(kt == NKT - 1)),
                         reads=[v_b, p_b], writes=[o_b])
            osb_t, osb_b = osb_ring.next()
            rc_t, rc_b = rc_ring.next()
            S.op("act", lambda h: h.activation(out=osb_t[:, :], in_=o_t[0:64, :], func=AF.Copy), reads=[o_b], writes=[osb_b])
            S.op("dve", lambda h: h.reciprocal(out=rc_t[64:65, :], in_=o_t[64:65, :]), reads=[o_b], writes=[rc_b])
            bc_t, bc_b = s_ring.next()
            S.op("pe", lambda h: h.matmul(bc_t[0:64, 0:512], lhsT=ones32[64:65, 0:64], rhs=rc_t[64:65, :], start=True, stop=True),
                 reads=[rc_b, ones32_b], writes=[bc_b])
            yc_t, yc_b = yc_ring.next()
            S.op("dve", lambda h: h.tensor_tensor(out=yc_t[:, :], in0=osb_t[:, :], in1=bc_t[0:64, 0:512], op=ALU.mult),
                 reads=[osb_b, bc_b], writes=[yc_b])
            S.dma("pool", YC[hd * 64:(hd + 1) * 64, qg * 512:(qg + 1) * 512], yc_t[:, :], reads=[yc_b])
    cx.pop()
    return cx.finish() if own else None


def build_ob2(ntok=TOK, ngrp=4, cx=None):
    TB = 512
    NB = ntok // TB
    own = cx is None
    cx = Ctx() if own else cx
    cx.push()
    S = cx.S
    AB = cx.dram_in("AB", [2, 2, 512, ntok])
    GX = cx.dram_in("GX", [512, ntok], BF16)
    YC = cx.dram_in("YC", [512, ntok], BF16)
    xT = cx.dram_in("xT", [D, ntok])
    w_out = cx.dram_in("w_out", [D, D])
    BLK = cx.dram_in("BLK", [128, NB, 2, 2, 4])
    CABg = cx.dram_in("CABg", [128, ngrp, 16])
    mfd = cx.dram_in("mf", [128, ngrp])
    mbd = cx.dram_in("mb", [128, ngrp])
    oT = cx.dram_out("oT", [D, ntok])

    woA = cx.sb("woA", [128, 4, D], BF16); woA_b = Buf("woA")
    woB = cx.sb("woB", [128, 4, D], BF16); woB_b = Buf("woB")
    blk_t = cx.sb("blk", [128, NB, 2, 2, 4], F32); blk_b = Buf("blk")
    cab_t = cx.sb("cab", [128, ngrp, 16], F32); cab_b = Buf("cab")
    m_t = cx.sb("m", [128, 2, ngrp], F32); m_b = Buf("m")
    hin_t = cx.sb("hin", [128, 2, 4], F32); hin_b = Buf("hin")
    tmp_t = cx.sb("tmp", [128, 4], F32); tmp_b = Buf("tmp")
    init_t = cx.sb("init", [128, NB, 2, 4], F32); init_b = Buf("init")
    ab_ring = mk_ring(cx, "sb", "ab", 2, [128, 2, 2, 4, TB], F32)
    hs_ring = mk_ring(cx, "sb", "hs", 2, [128, 2, 4, TB], F32)
    gx_ring = mk_ring(cx, "sb", "gx", 2, [128, 4, TB], BF16)
    yc_ring = mk_ring(cx, "sb", "yc", 2, [128, 4, TB], BF16)
    yd_ring = mk_ring(cx, "sb", "yd", 2, [128, 4, TB], BF16)
    x_ring = mk_ring(cx, "sb", "x", 2, [128, KC, TB], F32)
    y_ring = mk_ring(cx, "ps", "y", 3, [128, 512], F32)

    S.dma("pool", woA[:, :, :], w_out[0:512, :].rearrange("(i p) n -> p i n", p=128), writes=[woA_b])
    S.dma("pool", woB[:, :, :], w_out[512:1024, :].rearrange("(g p) n -> p g n", p=128), writes=[woB_b])
    S.dma("sp", blk_t[:, :, :, :, :], BLK[:, :, :, :, :], writes=[blk_b])
    S.dma("sp", cab_t[:, :, :], CABg[:, :, :], writes=[cab_b])
    S.dma("sp", m_t[:, 0, :], mfd[:, :], writes=[m_b])
    S.dma("sp", m_t[:, 1, :], mbd[:, :], writes=[m_b])
    S.op("pool", lambda h: h.memset(hin_t[:, :, :], 0.0), writes=[hin_b])
    for d in range(2):
        order = range(ngrp) if d == 0 else range(ngrp - 1, -1, -1)
        for i in order:
            S.op("dve", lambda h, d=d, i=i: h.tensor_tensor(out=tmp_t[:, :], in0=hin_t[:, d, :], in1=cab_t[:, i, d * 8:d * 8 + 4], op=ALU.mult),
                 reads=[hin_b, cab_b, tmp_b], writes=[tmp_b])
            S.op("dve", lambda h, d=d, i=i: h.tensor_tensor(out=tmp_t[:, :], in0=tmp_t[:, :], in1=cab_t[:, i, d * 8 + 4:d * 8 + 8], op=ALU.add),
                 reads=[tmp_b, cab_b], writes=[tmp_b])
            S.op("dve", lambda h, d=d, i=i: h.tensor_tensor(out=tmp_t[:, :], in0=tmp_t[:, :], in1=hin_t[:, d, :], op=ALU.subtract),
                 reads=[tmp_b, hin_b], writes=[tmp_b])
            S.op("dve", lambda h, d=d, i=i: h.scalar_tensor_tensor(out=hin_t[:, d, :], in0=tmp_t[:, :], scalar=m_t[:, d, i:i + 1],
                                                                   in1=hin_t[:, d, :], op0=ALU.mult, op1=ALU.add),
                 reads=[tmp_b, m_b, hin_b], writes=[hin_b])
    for d in range(2):
        order = list(range(NB)) if d == 0 else list(range(NB - 1, -1, -1))
        S.op("dve", lambda h, d=d, b0=order[0]: h.tensor_copy(out=init_t[:, b0, d, :], in_=hin_t[:, d, :]),
             reads=[hin_b, init_b], writes=[init_b])
        for bi in range(NB - 1):
            b, bn = order[bi], order[bi + 1]
            S.op("dve", lambda h, d=d, b=b, bn=bn: h.tensor_tensor(out=init_t[:, bn, d, :], in0=init_t[:, b, d, :],
                                                                   in1=blk_t[:, b, d, 0, :], op=ALU.mult),
                 reads=[init_b, blk_b], writes=[init_b])
            S.op("dve", lambda h, d=d, b=b, bn=bn: h.tensor_tensor(out=init_t[:, bn, d, :], in0=init_t[:, bn, d, :],
                                                                   in1=blk_t[:, b, d, 1, :], op=ALU.add),
                 reads=[init_b, blk_b], writes=[init_b])
    xv = xT.rearrange("(c p) t -> p c t", p=128)
    ov = oT.rearrange("(c p) t -> p c t", p=128)
    ABv = AB.rearrange("d s (c p) t -> d s p c t", p=128)
    def blk(b):
        t0 = b * TB
        ab_t, ab_b = ab_ring.next()
        for d in range(2):
            for s_ in range(2):
                S.dma("sp", ab_t[:, d, s_, :, :], ABv[d, s_, :, :, t0:t0 + TB], writes=[ab_b])
        gx_t, gx_b = gx_ring.next()
        yc_t, yc_b = yc_ring.next()
        x_t, x_b = x_ring.next()
        S.dma("sp", gx_t[:, :, :], GX.rearrange("(c p) t -> p c t", p=128)[:, :, t0:t0 + TB], writes=[gx_b])
        S.dma("sp", yc_t[:, :, :], YC.rearrange("(i p) t -> p i t", p=128)[:, :, t0:t0 + TB], writes=[yc_b])
        S.dma("sp", x_t[:, :, :], xv[:, :, t0:t0 + TB], writes=[x_b])
        yield
        hs_t, hs_b = hs_ring.next()
        for d in range(2):
            for c in range(4):
                if d == 0:
                    S.op("dve", lambda h, d=d, c=c, b=b: h.tensor_tensor_scan(
                        out=hs_t[:, d, c, :], data0=ab_t[:, d, 0, c, :], data1=ab_t[:, d, 1, c, :],
                        initial=init_t[:, b, d, c:c + 1], op0=ALU.mult, op1=ALU.add), reads=[ab_b, init_b, hs_b], writes=[hs_b])
                else:
                    S.op("dve", lambda h, d=d, c=c, b=b: h.tensor_tensor_scan(
                        out=hs_t[:, d, c, ::-1], data0=ab_t[:, d, 0, c, ::-1], data1=ab_t[:, d, 1, c, ::-1],
                        initial=init_t[:, b, d, c:c + 1], op0=ALU.mult, op1=ALU.add), reads=[ab_b, init_b, hs_b], writes=[hs_b])
        yield
        S.op("pool", lambda h: h.tensor_tensor(out=hs_t[:, 0, :, :], in0=hs_t[:, 0, :, :], in1=hs_t[:, 1, :, :], op=ALU.add),
             reads=[hs_b], writes=[hs_b])
        yd_t, yd_b = yd_ring.next()
        S.op("pool", lambda h: h.tensor_tensor(out=yd_t[:, :, :], in0=hs_t[:, 0, :, :], in1=gx_t[:, :, :], op=ALU.mult),
             reads=[hs_b, gx_b], writes=[yd_b])
        yield
        for o in range(KC):
            y_t, y_b = y_ring.next()
            for hh in range(4):
                S.op("pe", lambda h, hh=hh, o=o: h.matmul(y_t[:, :], lhsT=woA[:, hh, o * 128:(o + 1) * 128], rhs=yc_t[:, hh, :],
                                                         start=(hh == 0), stop=False), reads=[woA_b, yc_b], writes=[y_b])
            for g in range(4):
                S.op("pe", lambda h, g=g, o=o: h.matmul(y_t[:, :], lhsT=woB[:, g, o * 128:(o + 1) * 128], rhs=yd_t[:, g, :],
                                                       start=False, stop=(g == 3)), reads=[woB_b, yd_b], writes=[y_b])
            S.op("dve", lambda h, o=o: h.tensor_tensor(out=x_t[:, o, :], in0=y_t[:, :], in1=x_t[:, o, :], op=ALU.add),
                 reads=[y_b, x_b], writes=[x_b])
        S.dma("pool", ov[:, :, t0:t0 + TB], x_t[:, :, :], reads=[x_b])
        yield

    run_interleaved((blk(b) for b in range(NB)), 2, 2)
    cx.pop()
    return cx.finish() if own else None


def allgather(cx, in_ap, out_ap, groups):
    S = cx.S
    S.barrier()
    sem = S.new_sem("cc")
    cx.nc.gpsimd.collective_compute("AllGather", ALU.bypass, replica_groups=groups, ins=[in_ap], outs=[out_ap]).then_inc(sem, 1)
    for e in S.ENGS:
        S.h[e].wait_ge(sem, 1)


def allgather_many(cx, pairs, groups):
    S = cx.S
    S.barrier()
    sem = S.new_sem("ccm")
    for (in_ap, out_ap) in pairs:
        cx.nc.gpsimd.collective_compute("AllGather", ALU.bypass, replica_groups=groups, ins=[in_ap], outs=[out_ap]).then_inc(sem, 1)
    for e in S.ENGS:
        S.h[e].wait_ge(sem, len(pairs))


def emit_select(cx, src_t, src_b, nrank, m_t, m_b, side, acc_t, acc_b):
    S = cx.S
    S.op("dve", lambda h: h.tensor_scalar_mul(out=acc_t[:, :], in0=src_t[:, 0, :], scalar1=m_t[:, side, 0:1]),
         reads=[src_b, m_b], writes=[acc_b])
    for i in range(1, nrank):
        S.op("dve", lambda h, i=i: h.scalar_tensor_tensor(out=acc_t[:, :], in0=src_t[:, i, :], scalar=m_t[:, side, i:i + 1],
                                                          in1=acc_t[:, :], op0=ALU.mult, op1=ALU.add),
             reads=[src_b, m_b, acc_b], writes=[acc_b])


def emit_even_exchange(cx, KTh, Vh, UTh, pack, packg, mlr, groups, nrank, ntok):
    S = cx.S
    cx.push()
    S.dma_dd("sp", pack[:, 0:128], KTh[:, 128:256])
    S.dma_dd("sp", pack[:, 128:256], KTh[:, ntok:ntok + 128])
    S.dma_dd("sp", pack[:, 256:288].rearrange("p (g t) -> p g t", g=4), UTh[:, :, 8:16])
    S.dma_dd("sp", pack[:, 288:320].rearrange("p (g t) -> p g t", g=4), UTh[:, :, ntok:ntok + 8])
    S.dma_dd("sp", pack[:, 320:450], Vh[128:256, :])
    S.dma_dd("sp", pack[:, 450:580], Vh[ntok:ntok + 128, :])
    allgather(cx, pack[:, :], packg[:, :], groups)
    pg_t = cx.sb("pg", [128, nrank, 580], BF16); pg_b = Buf("pg")
    m_t = cx.sb("mlr", [128, 2, nrank], F32); m_b = Buf("mlr")
    accL = cx.sb("accL", [128, 580], BF16); accL_b = Buf("accL")
    accR = cx.sb("accR", [128, 580], BF16); accR_b = Buf("accR")
    S.dma("sp", pg_t[:, :, :], packg.rearrange("(r p) n -> p r n", p=128), writes=[pg_b])
    S.dma("sp", m_t[:, :, :], mlr[:, :, :], writes=[m_b])
    emit_select(cx, pg_t, pg_b, nrank, m_t, m_b, 0, accL, accL_b)
    emit_select(cx, pg_t, pg_b, nrank, m_t, m_b, 1, accR, accR_b)
    S.dma("sp", KTh[:, 0:128], accL[:, 128:256], reads=[accL_b])
    S.dma("sp", UTh[:, :, 0:8], accL[:, 288:320].rearrange("p (g t) -> p g t", g=4), reads=[accL_b])
    S.dma("sp", Vh[0:128, :], accL[:, 450:580], reads=[accL_b])
    S.dma("sp", KTh[:, 128 + ntok:256 + ntok], accR[:, 0:128], reads=[accR_b])
    S.dma("sp", UTh[:, :, 8 + ntok:16 + ntok], accR[:, 256:288].rearrange("p (g t) -> p g t", g=4), reads=[accR_b])
    S.dma("sp", Vh[128 + ntok:256 + ntok, :], accR[:, 320:450], reads=[accR_b])
    cx.pop()


def emit_xhalo_exchange(cx, xprev, xhp, xhpg, xhalo, mlr, groups, nrank, ntok):
    S = cx.S
    cx.push()
    S.dma_dd("sp", xhp[:, 0:2], xprev[:, 0:2])
    S.dma_dd("sp", xhp[:, 2:4], xprev[:, ntok - 2:ntok])
    allgather(cx, xhp[:, :], xhpg[:, :], groups)
    xg_t = cx.sb("xg", [128, nrank, 32], F32); xg_b = Buf("xg")
    m_t = cx.sb("mlr", [128, 2, nrank], F32); m_b = Buf("mlr")
    accL = cx.sb("accL", [128, 32], F32); accL_b = Buf("accL")
    accR = cx.sb("accR", [128, 32], F32); accR_b = Buf("accR")
    for r in range(nrank):
        S.dma("sp", xg_t[:, r, :].rearrange("p (c t) -> p c t", t=4),
              xhpg[r * D:(r + 1) * D, :].rearrange("(c p) t -> p c t", p=128), writes=[xg_b])
    S.dma("sp", m_t[:, :, :], mlr[:, :, :], writes=[m_b])
    emit_select(cx, xg_t, xg_b, nrank, m_t, m_b, 0, accL, accL_b)
    emit_select(cx, xg_t, xg_b, nrank, m_t, m_b, 1, accR, accR_b)
    xhv = xhalo.rearrange("(c p) t -> p c t", p=128)
    S.dma("sp", xhv[:, :, 0:2], accL[:, :].rearrange("p (c t) -> p c t", t=4)[:, :, 2:4], reads=[accL_b])
    S.dma("sp", xhv[:, :, 2:4], accR[:, :].rearrange("p (c t) -> p c t", t=4)[:, :, 0:2], reads=[accR_b])
    cx.pop()


SMALL_SPECS = None


def build_fused(B=2, nrank=4, ntok=TOK, depth=4):
    NE, NO = (depth + 1) // 2, depth // 2
    NT = ntok // 128
    NB = ntok // 512
    groups = [[b * nrank + r for r in range(nrank)] for b in range(B)]
    cx = Ctx()
    nc = cx.nc
    I = cx.ext_in
    x0 = I("xT", [D, ntok])
    Wd = {
        "e_w_in": I("e_w_in", [NE, D, 1280]), "e_w_pool": I("e_w_pool", [NE, 4, 128, 128]), "e_w_out": I("e_w_out", [NE, D, D]),
        "o_w_in": I("o_w_in", [NO, D, 1440]), "o_w_uq": I("o_w_uq", [NO, 256, 768]), "o_w_ukv": I("o_w_ukv", [NO, 128, 1024]),
        "o_lru_wa": I("o_lru_wa", [NO, 2, 8, 64, 64]), "o_lru_wx": I("o_lru_wx", [NO, 2, 8, 64, 64]), "o_w_out": I("o_w_out", [NO, D, D]),
        "w_mlp1": I("w_mlp1", [depth, D, DFF]), "w_mlp2": I("w_mlp2", [depth, DFF, D]),
        "g_mix": I("g_mix", [depth, 128, KC]), "g_mlp": I("g_mlp", [depth, 128, KC]), "g_fin": I("g_fin", [128, KC]),
        "pscale": I("pscale", [NE, 128, 4]), "sinkrow": I("sinkrow", [NE, 1, 2, 512]),
        "g_cq": I("g_cq", [NO, 128, 2]), "g_ckv": I("g_ckv", [NO, 128, 1]), "cw": I("cw", [NO, 128, 4, 4]), "cb": I("cb", [NO, 128, 4]),
        "ba": I("ba", [NO, 128, 2, 4]), "bx": I("bx", [NO, 128, 2, 4]), "lam": I("lam", [NO, 128, 2, 4]),
        "cos32": I("cos32", [128, ntok]), "sin32": I("sin32", [128, ntok]), "cos16": I("cos16", [128, ntok]), "sin16": I("sin16", [128, ntok]),
        "rot64": I("rot64", [128, 128]), "rot32": I("rot32", [128, 128]), "masks": I("masks", [4, 128, 512]),
        "ident": I("ident", [128, 128]),
        "invc": I("invc", [128, 2, 4, 16]), "mfb": I("mfb", [2, 128, nrank]), "mlr": I("mlr", [128, 2, nrank]),
    }
    outT = cx.ext_out("oT", [D, ntok])

    def tmp(name, shape, dt=F32):
        return nc.dram_tensor(name, list(shape), dt, kind="Internal").ap()

    def make_precast(layer, w1b_d, w2b_d):
        def f():
            for k in range(8):
                cx.S.dma_dd_async("pool", w1b_d[k * 128:(k + 1) * 128, :], Wd["w_mlp1"][layer][k * 128:(k + 1) * 128, :])
            for k in range(8):
                cx.S.dma_dd_async("pool", w2b_d[k * 512:(k + 1) * 512, :], Wd["w_mlp2"][layer][k * 512:(k + 1) * 512, :])
        return f

    xcur = x0
    for layer in range(depth):
        L = f"L{layer}"
        xmix = tmp(L + "_xmix", [D, ntok])
        w1b_d = tmp(L + "_w1b", [D, DFF], BF16)
        w2b_d = tmp(L + "_w2b", [DFF, D], BF16)
        if layer % 2 == 0:
            e = layer // 2
            QsT = tmp(L + "_QsT", [128, 4, ntok], BF16)
            KTh = tmp(L + "_KTh", [128, ntok + 256], BF16)
            Vh = tmp(L + "_Vh", [ntok + 256, 130], BF16)
            UTh = tmp(L + "_UTh", [128, 4, ntok + 16], BF16)
            pack = tmp(L + "_pack", [128, 580], BF16)
            packg = tmp(L + "_packg", [nrank * 128, 580], BF16)
            cx.bind = {"xT": xcur, "w_in": Wd["e_w_in"][e], "g": Wd["g_mix"][layer], "cos": Wd["cos32"], "sin": Wd["sin32"],
                       "rot": Wd["rot64"], "QsT": QsT, "KT": KTh[:, 128:128 + ntok], "Vaug": Vh[128:128 + ntok, :],
                       "UT": UTh[:, :, 8:8 + ntok]}
            build_ea(ntok, cx=cx)
            emit_even_exchange(cx, KTh, Vh, UTh, pack, packg, Wd["mlr"], groups, nrank, ntok)
            cx.bind = {"QsT": QsT, "KTh": KTh, "Vh": Vh, "UTh": UTh, "xT": xcur, "w_pool": Wd["e_w_pool"][e],
                       "pscale": Wd["pscale"][e], "w_out": Wd["e_w_out"][e], "sinkrow": Wd["sinkrow"][e], "masks": Wd["masks"],
                       "invc": Wd["invc"], "ident": Wd["ident"], "oT": xmix}
            cx.hook = make_precast(layer, w1b_d, w2b_d)
            build_eb(ntok, cx=cx)
        else:
            o = layer // 2
            xhp = tmp(L + "_xhp", [D, 4]); xhpg = tmp(L + "_xhpg", [nrank * D, 4]); xhalo = tmp(L + "_xhalo", [D, 4])
            QN = tmp(L + "_QN", [512, ntok], BF16); QR = tmp(L + "_QR", [256, ntok], BF16)
            KNR = tmp(L + "_KNR", [544, ntok], BF16); V5 = tmp(L + "_V5", [1024, NT * 65], BF16)
            GX = tmp(L + "_GX", [512, ntok], BF16); AB = tmp(L + "_AB", [2, 2, 512, ntok])
            BLK = tmp(L + "_BLK", [128, NB, 2, 2, 4]); CAB = tmp(L + "_CAB", [128, 16]); XR = tmp(L + "_XR", [512, ntok])
            KNg = tmp(L + "_KNg", [8 * nrank * 64, ntok], BF16); KRg = tmp(L + "_KRg", [nrank * 32, ntok], BF16)
            Vg = tmp(L + "_Vg", [8 * nrank * 128, NT * 65], BF16)
            CABg = tmp(L + "_CABg", [nrank * 128, 16]); YC = tmp(L + "_YC", [512, ntok], BF16)
            emit_xhalo_exchange(cx, xcur, xhp, xhpg, xhalo, Wd["mlr"], groups, nrank, ntok)
            cx.bind = {"xT": xcur, "xhalo": xhalo, "w_in": Wd["o_w_in"][o], "g": Wd["g_mix"][layer], "g_cq": Wd["g_cq"][o],
                       "g_ckv": Wd["g_ckv"][o], "w_uq": Wd["o_w_uq"][o], "w_ukv": Wd["o_w_ukv"][o], "cw": Wd["cw"][o], "cb": Wd["cb"][o],
                       "wa": Wd["o_lru_wa"][o], "wx": Wd["o_lru_wx"][o], "ba": Wd["ba"][o], "bx": Wd["bx"][o], "lam": Wd["lam"][o],
                       "cos": Wd["cos16"], "sin": Wd["sin16"], "rot": Wd["rot32"], "QN": QN, "QR": QR, "KNR": KNR, "V5": V5, "GX": GX,
                       "AB": AB, "BLK": BLK, "CAB": CAB.rearrange("p (d s c) -> p d s c", d=2, s=2), "XR": XR}
            ccsem = cx.S.new_sem("ccg")
            ncc = [0]

            def gather_kv():
                pairs = []
                for i in range(4):
                    pairs.append((KNR[i * 128:(i + 1) * 128, :], KNg[i * nrank * 128:(i + 1) * nrank * 128, :]))
                pairs.append((KNR[512:544, :], KRg[:, :]))
                for hd in range(8):
                    pairs.append((V5[hd * 128:(hd + 1) * 128, :], Vg[hd * nrank * 128:(hd + 1) * nrank * 128, :]))
                for (i_ap, o_ap) in pairs:
                    nc.gpsimd.collective_compute("AllGather", ALU.bypass, replica_groups=groups, ins=[i_ap], outs=[o_ap]).then_inc(ccsem, 1)
                    ncc[0] += 1

            build_oa(ntok, cx=cx, mid_hook=gather_kv)
            cabsem = cx.S.new_sem("cab")
            nc.gpsimd.collective_compute("AllGather", ALU.bypass, replica_groups=groups, ins=[CAB[:, :]], outs=[CABg[:, :]]).then_inc(cabsem, 1)
            for e_ in cx.S.ENGS:
                cx.S.h[e_].wait_ge(ccsem, ncc[0])
            cx.bind = {"QN": QN, "QR": QR, "KNg": KNg, "KRg": KRg, "Vg": Vg, "YC": YC}
            cx.hook = make_precast(layer, w1b_d, w2b_d)
            build_ob1(ntok, nrank, cx=cx)
            for e_ in cx.S.ENGS:
                cx.S.h[e_].wait_ge(cabsem, 1)
            cx.bind = {"AB": AB, "GX": GX, "YC": YC, "xT": xcur, "w_out": Wd["o_w_out"][o], "BLK": BLK,
                       "CABg": CABg.rearrange("(r p) n -> p r n", p=128), "mf": Wd["mfb"][0], "mb": Wd["mfb"][1], "oT": xmix}
            build_ob2(ntok, nrank, cx=cx)
        last = layer == depth - 1
        xnext = outT if last else tmp(L + "_xmlp", [D, ntok])
        cx.bind = {"xT": xmix, "w1": w1b_d, "w2": w2b_d, "g": Wd["g_mlp"][layer], "gf": Wd["g_fin"], "oT": xnext}
        build_mlp(last, ntok, cx=cx, wbf16=True)
        xcur = xnext
    cx.bind = {}
    return cx.finish()


_FUSED = {}


def run_model(x, W, nrank=4, ntok=TOK):
    B, Sq, _ = x.shape
    ncore = B * nrank
    assert Sq == nrank * ntok
    depth = W["norm_mlp"].shape[0]
    NE, NO = (depth + 1) // 2, depth // 2
    key = (B, nrank, ntok, depth)
    if key not in _FUSED:
        _FUSED[key] = build_fused(B, nrank, ntok, depth)
    nc = _FUSED[key]
    f32 = lambda a: np.ascontiguousarray(np.asarray(a, np.float32))
    g_mix = np.stack([vec128(W["e_norm_mix"][l // 2] if l % 2 == 0 else W["o_norm_mix"][l // 2], 8) for l in range(depth)])
    shared = {
        "e_w_in": f32(W["e_w_in"]), "e_w_pool": f32(W["e_w_pool"]), "e_w_out": f32(W["e_w_out"]),
        "o_w_in": f32(W["o_w_in"]), "o_w_uq": f32(W["o_w_uq"]), "o_w_ukv": f32(W["o_w_ukv"]),
        "o_lru_wa": f32(W["o_lru_wa"]), "o_lru_wx": f32(W["o_lru_wx"]), "o_w_out": f32(W["o_w_out"]),
        "w_mlp1": f32(W["w_mlp1"]), "w_mlp2": f32(W["w_mlp2"]),
        "g_mix": g_mix, "g_mlp": np.stack([vec128(W["norm_mlp"][l], 8) for l in range(depth)]), "g_fin": vec128(W["final_norm"], 8),
        "pscale": np.stack([vec128(W["e_pool_scale"][e], 4) for e in range(NE)]),
        "sinkrow": np.stack([np.repeat(f32(W["e_sink"][e]).reshape(2, 4), 128, axis=1).reshape(1, 2, 512) for e in range(NE)]),
        "g_cq": np.stack([vec128(W["o_g_cq"][o], 2) for o in range(NO)]),
        "g_ckv": np.stack([vec128(W["o_g_ckv"][o], 1) for o in range(NO)]),
        "cw": np.stack([f32(f32(W["o_conv_w"][o]).reshape(4, 4, 128).transpose(2, 1, 0)) for o in range(NO)]),
        "cb": np.stack([chunk_vec(W["o_conv_b"][o], 4) for o in range(NO)]),
        "ba": np.stack([f32(f32(W["o_lru_ba"][o]).reshape(2, 4, 128).transpose(2, 0, 1)) for o in range(NO)]),
        "bx": np.stack([f32(f32(W["o_lru_bx"][o]).reshape(2, 4, 128).transpose(2, 0, 1)) for o in range(NO)]),
        "lam": np.stack([f32(f32(W["o_lru_lambda"][o]).reshape(2, 4, 128).transpose(2, 0, 1)) for o in range(NO)]),
        "rot64": rot_matrix(64), "rot32": rot_matrix(32), "ident": np.eye(128, dtype=np.float32),
    }
    in_maps = []
    for c in range(ncore):
        bi, r = c // nrank, c % nrank
        pos = r * ntok + np.arange(ntok)
        cos32, sin32 = rope_tables(pos, 32, 128)
        cos16, sin16 = rope_tables(pos, 16, 128)
        mfb = np.zeros((2, 128, nrank), np.float32); mfb[0, :, :r] = 1.0; mfb[1, :, r + 1:] = 1.0
        mlr = np.zeros((128, 2, nrank), np.float32)
        if r > 0:
            mlr[:, 0, r - 1] = 1.0
        if r < nrank - 1:
            mlr[:, 1, r + 1] = 1.0
        im = dict(shared)
        im.update({"xT": np.ascontiguousarray(x[bi, r * ntok:(r + 1) * ntok, :].T), "cos32": cos32, "sin32": sin32, "cos16": cos16,
                   "sin16": sin16, "masks": eb_masks(r > 0, r < nrank - 1), "invc": eb_invc(r == 0, r == nrank - 1),
                   "mfb": mfb, "mlr": mlr})
        in_maps.append(im)
    res = run_spmd(nc, in_maps)
    out = np.empty((B, Sq, D), np.float32)
    for c in range(ncore):
        bi, r = c // nrank, c % nrank
        out[bi, r * ntok:(r + 1) * ntok, :] = res[c]["oT"].T
    return out


def kernel(**inputs):
    W = {k: np.asarray(v) for k, v in inputs.items()}
    x = np.asarray(W.pop("x"), np.float32)
    return run_model(x, W)
```

```python
from contextlib import ExitStack
import numpy as np
import concourse.bass as bass
import concourse.mybir as mybir
from concourse.bass_utils import run_bass_kernel_spmd

F32 = mybir.dt.float32
BF16 = mybir.dt.bfloat16
ALU = mybir.AluOpType
AF = mybir.ActivationFunctionType

NCORES = 8
D = 1024
KC = 8
TOK = 4096
SEQ = 16384
EPS = 1e-6
DFF = 4096
EPOCH = 30000


class Buf:
    __slots__ = ("name", "writers", "readers", "sem_in", "sem_out", "n_in", "n_out", "excl")

    def __init__(self, name, excl=False):
        self.name = name
        self.excl = excl
        self.writers = {}
        self.readers = {}
        self.sem_in = None
        self.sem_out = None
        self.n_in = 0
        self.n_out = 0


class Sched:
    ENGS = ("pe", "act", "dve", "pool", "sp")

    def __init__(self, nc, stack):
        self.nc = nc
        self.stack = stack
        self.h = {"pe": nc.tensor, "act": nc.scalar, "dve": nc.vector, "pool": nc.gpsimd, "sp": nc.sync}
        self.ops = {e: [] for e in self.ENGS}
        self.cnt = {e: 0 for e in self.ENGS}
        self.sem = {e: None for e in self.ENGS}
        self.seen = {e: {} for e in self.ENGS}
        self.last = {e: None for e in self.ENGS}
        self.dma_toks = {}
        self.nsem = 0
        self.ninstr = 0
        self.sem_pool = []
        self.live = []
        self.ddbuf = Buf("dram2dram")

    def new_sem(self, name):
        self.nsem += 1
        return self.stack.enter_context(self.nc.semaphore(f"{name}_{self.nsem}"))

    def _eng_tok(self, e):
        if self.sem[e] is None or self.cnt[e] >= EPOCH:
            self.sem[e] = self.new_sem("e" + e)
            self.cnt[e] = 0
        self.cnt[e] += 1
        tok = (self.sem[e], self.cnt[e])
        self.last[e] = tok
        return tok

    def _waits(self, e, toks):
        need = {}
        seen = self.seen[e]
        for sem, val in toks:
            k = id(sem)
            if seen.get(k, 0) >= val:
                continue
            if k not in need or need[k][1] < val:
                need[k] = (sem, val)
        out = []
        for k, (sem, val) in need.items():
            seen[k] = val
            out.append((sem, val))
        return out

    def _deps(self, e, reads, writes):
        toks = []
        for b in reads:
            toks.extend(b.writers.values())
            if b.excl:
                toks.extend(b.readers.values())
        for b in writes:
            toks.extend(b.writers.values())
            toks.extend(b.readers.values())
        if e == "pe":
            own = id(self.sem["pe"]) if self.sem["pe"] is not None else None
            toks = [t for t in toks if id(t[0]) != own]
        return self._waits(e, toks)

    def op(self, e, fn, reads=(), writes=()):
        waits = self._deps(e, reads, writes)
        tok = self._eng_tok(e)
        for b in reads:
            b.readers[id(tok[0])] = tok
        for b in writes:
            b.readers = {}
            b.writers = {id(tok[0]): tok}
        self.ninstr += 1

        h = self.h[e]
        for sem, val in waits:
            h.wait_ge(sem, val)
        fn(h).then_inc(tok[0], 1)

    def dma(self, q, out_ap, in_ap, reads=(), writes=(), **kw):
        waits = self._deps(q, reads, writes)
        assert len(writes) + len(reads) >= 1 and len(writes) <= 1 and len(reads) <= 1
        if writes:
            b = writes[0]
            if b.sem_in is None:
                b.sem_in, b.n_in = self._take_sem("di")
                self.live.append((b, "in"))
            b.n_in += 16
            tok = (b.sem_in, b.n_in)
            b.readers = {}
            b.writers = {id(tok[0]): tok}
            for rb in reads:
                rb.readers[id(tok[0])] = tok
        else:
            b = reads[0]
            if b.sem_out is None:
                b.sem_out, b.n_out = self._take_sem("do")
                self.live.append((b, "out"))
            b.n_out += 16
            tok = (b.sem_out, b.n_out)
            b.readers[id(tok[0])] = tok
        self.dma_toks[id(tok[0])] = tok
        self.ninstr += 1
        h = self.h[q]
        for sem, val in waits:
            h.wait_ge(sem, val)
        h.dma_start(out=out_ap, in_=in_ap, **kw).then_inc(tok[0], 16)

    def _take_sem(self, name):
        if self.sem_pool:
            return self.sem_pool.pop()
        return self.new_sem(name), 0

    def release_dma_sems(self):
        for b, kind in self.live:
            if kind == "in":
                self.sem_pool.append((b.sem_in, b.n_in)); b.sem_in = None
                b.writers = {}
            else:
                self.sem_pool.append((b.sem_out, b.n_out)); b.sem_out = None
                b.readers = {}
        self.live = []

    def dma_dd(self, q, out_ap, in_ap, **kw):
        self.dma(q, out_ap, in_ap, writes=[self.ddbuf], **kw)

    def dma_dd_async(self, q, out_ap, in_ap, **kw):
        self.dma(q, out_ap, in_ap, writes=[Buf("dd_async")], **kw)

    def barrier(self):
        toks = [t for t in self.last.values() if t is not None] + list(self.dma_toks.values())
        for e in self.ENGS:
            waits = self._waits(e, toks)
            for sem, val in waits:
                self.h[e].wait_ge(sem, val)

    def finalize(self):
        self.barrier()


class Ctx:
    def __init__(self):
        self.nc = bass.Bass("TRN2", target_bir_lowering=False)
        self.stack = ExitStack()
        self.S = Sched(self.nc, self.stack)
        self.n = 0
        self.cur = self.stack
        self.scopes = []
        self.bind = {}
        self.hook = None

    def dram_in(self, name, shape, dt=F32):
        if name in self.bind:
            return self.bind[name]
        return self.nc.dram_tensor(name, list(shape), dt, kind="ExternalInput").ap()

    def dram_out(self, name, shape, dt=F32):
        if name in self.bind:
            return self.bind[name]
        return self.nc.dram_tensor(name, list(shape), dt, kind="ExternalOutput").ap()

    def ext_in(self, name, shape, dt=F32):
        return self.nc.dram_tensor(name, list(shape), dt, kind="ExternalInput").ap()

    def ext_out(self, name, shape, dt=F32):
        return self.nc.dram_tensor(name, list(shape), dt, kind="ExternalOutput").ap()

    def sb(self, name, shape, dt):
        self.n += 1
        return self.cur.enter_context(self.nc.sbuf_tensor(f"{name}_{self.n}", list(shape), dt))

    def ps(self, name, shape, dt=F32):
        self.n += 1
        return self.cur.enter_context(self.nc.psum_tensor(f"{name}_{self.n}", list(shape), dt))

    def dram_tmp(self, name, shape, dt=F32):
        return self.nc.dram_tensor(name, list(shape), dt, kind="Internal").ap()

    def run_hook(self):
        if self.hook is not None:
            f, self.hook = self.hook, None
            f()

    def push(self):
        st = ExitStack()
        self.scopes.append(st)
        self.cur = st

    def pop(self):
        self.S.barrier()
        if len(self.scopes) == 1:
            self.S.release_dma_sems()
        self.scopes.pop().close()
        self.cur = self.scopes[-1] if self.scopes else self.stack

    def finish(self):
        self.S.finalize()
        self.stack.close()
        return self.nc


class Ring:
    def __init__(self, items):
        self.items = items
        self.i = 0

    def next(self):
        it = self.items[self.i % len(self.items)]
        self.i += 1
        return it


def run_interleaved(gens, width=2, stagger=2):
    it = iter(gens)
    active = []
    steps = 0
    while True:
        while len(active) < width and (not active or steps >= stagger):
            try:
                active.append(next(it))
            except StopIteration:
                break
        if not active:
            break
        steps += 1
        for g in list(active):
            try:
                next(g)
            except StopIteration:
                active.remove(g)


def mk_ring(cx, kind, name, n, shape, dt):
    items = []
    for i in range(n):
        t = cx.sb(f"{name}{i}", shape, dt) if kind == "sb" else cx.ps(f"{name}{i}", shape, dt)
        items.append((t, Buf(f"{name}{i}", excl=(kind == "ps"))))
    return Ring(items)


def emit_rmsnorm(cx, x_t, x_b, nchunk, TB, g_t, g_b, ones_t, ones_b, sq_ring, st_ring, rstd_ring,
                 out_t, out_b, nfeat, evac_engs=("dve",)):
    S = cx.S
    st_t, st_b = st_ring.next()
    for c in range(nchunk):
        sq_t, sq_b = sq_ring.next()
        S.op("act", lambda h, c=c, sq_t=sq_t: h.activation(out=sq_t[:, 0:TB], in_=x_t[:, c, 0:TB], func=AF.Square),
             reads=[x_b], writes=[sq_b])
        S.op("pe", lambda h, c=c, sq_t=sq_t: h.matmul(st_t[:, 0:TB], lhsT=ones_t[:, :], rhs=sq_t[:, 0:TB],
                                                        start=(c == 0), stop=(c == nchunk - 1)),
             reads=[sq_b, ones_b], writes=[st_b])
    r_t, r_b = rstd_ring.next()
    S.op("act", lambda h: h.activation(out=r_t[:, 0:TB], in_=st_t[:, 0:TB], func=AF.Sqrt, bias=float(nfeat * EPS)),
         reads=[st_b], writes=[r_b])
    S.op("dve", lambda h: h.reciprocal(out=r_t[:, 0:TB], in_=r_t[:, 0:TB]), reads=[r_b], writes=[r_b])
    for c in range(nchunk):
        e = evac_engs[c % len(evac_engs)]
        S.op(e, lambda h, c=c: h.scalar_tensor_tensor(out=out_t[:, c, 0:TB], in0=x_t[:, c, 0:TB],
                                                       scalar=g_t[:, c:c + 1], in1=r_t[:, 0:TB],
                                                       op0=ALU.mult, op1=ALU.mult),
             reads=[x_b, r_b, g_b], writes=[out_b])


def build_mlp(final_norm, ntok=TOK, dbg=False, cx=None, wbf16=False, order=None, mid_hook=None):
    TB = 256
    NB = ntok // TB
    FC = DFF // 128
    own = cx is None
    cx = Ctx() if own else cx
    cx.push()
    S = cx.S
    xT = cx.dram_in("xT", [D, ntok])
    w1 = cx.dram_in("w1", [D, DFF], BF16 if wbf16 else F32)
    w2 = cx.dram_in("w2", [DFF, D], BF16 if wbf16 else F32)
    gin = cx.dram_in("g", [128, KC])
    oT = cx.dram_out("oT", [D, ntok])
    if final_norm:
        gfin = cx.dram_in("gf", [128, KC])
    if dbg:
        dh = cx.dram_out("dh", [128, KC, TB], BF16)
        da = cx.dram_out("da", [128, DFF // 128, TB], BF16)

    w1b = cx.sb("w1b", [128, KC, DFF], BF16)
    w2b = cx.sb("w2b", [128, FC, D], BF16)
    w1_bufs = [Buf(f"w1_{k}") for k in range(KC)]
    w2_bufs = [Buf(f"w2_{k}") for k in range(8)]
    g_t = cx.sb("g", [128, KC], F32); g_b = Buf("g")
    ones_t = cx.sb("ones", [128, 128], BF16); ones_b = Buf("ones")
    x_ring = mk_ring(cx, "sb", "x", 2, [128, KC, TB], F32)
    h_ring = mk_ring(cx, "sb", "h", 2, [128, KC, TB], BF16)
    a_ring = mk_ring(cx, "sb", "a", 1, [128, FC, TB], BF16)
    r_ring = mk_ring(cx, "sb", "r", 3, [128, TB], BF16)
    sq_ring = mk_ring(cx, "sb", "sq", 3, [128, TB], BF16)
    rstd_ring = mk_ring(cx, "sb", "rstd", 2, [128, TB], F32)
    o_ring = mk_ring(cx, "sb", "o", 2, [128, KC, TB], F32)
    st_ring = mk_ring(cx, "ps", "st", 1, [128, 512], F32)
    p1_ring = mk_ring(cx, "ps", "p1", 3, [128, 512], F32)
    p2_ring = mk_ring(cx, "ps", "p2", 3, [128, 512], F32)
    if final_norm:
        gf_t = cx.sb("gf", [128, KC], F32); gf_b = Buf("gf")
        f_ring = mk_ring(cx, "sb", "f", 2, [128, KC, TB], F32)

    S.dma("sp", g_t[:, :], gin[:, :], writes=[g_b])
    S.op("dve", lambda h: h.tensor_scalar_mul(out=g_t[:, :], in0=g_t[:, :], scalar1=float(np.sqrt(D))),
         reads=[g_b], writes=[g_b])
    if final_norm:
        S.dma("sp", gf_t[:, :], gfin[:, :], writes=[gf_b])
        S.op("dve", lambda h: h.tensor_scalar_mul(out=gf_t[:, :], in0=gf_t[:, :], scalar1=float(np.sqrt(D))),
             reads=[gf_b], writes=[gf_b])
    S.op("pool", lambda h: h.memset(ones_t[:, :], 1.0), writes=[ones_b])
    w1v = w1.rearrange("(k p) n -> p k n", p=128)
    w2v = w2.rearrange("(f p) n -> p f n", p=128)
    wq = "sp" if wbf16 else "pool"
    for k in range(KC):
        S.dma(wq, w1b[:, k, :], w1v[:, k, :], writes=[w1_bufs[k]])
    for j in range(8):
        S.dma(wq, w2b[:, j * 4:(j + 1) * 4, :], w2v[:, j * 4:(j + 1) * 4, :], writes=[w2_bufs[j]])
    xv = xT.rearrange("(c p) t -> p c t", p=128)
    ov = oT.rearrange("(c p) t -> p c t", p=128)

    def prep(b):
        x_t, x_b = x_ring.next()
        S.dma("sp", x_t[:, :, :], xv[:, :, b * TB:(b + 1) * TB], writes=[x_b])
        h_t, h_b = h_ring.next()
        return (x_t, x_b, h_t, h_b)

    def norm(st_):
        x_t, x_b, h_t, h_b = st_
        emit_rmsnorm(cx, x_t, x_b, KC, TB, g_t, g_b, ones_t, ones_b, sq_ring, st_ring, rstd_ring, h_t, h_b, D)

    order = list(range(NB)) if order is None else order
    cur = prep(order[0])
    norm(cur)
    for bi, b in enumerate(order):
        t0 = b * TB
        x_t, x_b, h_t, h_b = cur
        nxt = prep(order[bi + 1]) if bi + 1 < NB else None
        a_t, a_b = a_ring.next()
        for f in range(FC):
            if f == FC // 2 and nxt is not None:
                norm(nxt)
            p_t, p_b = p1_ring.next()
            for k in range(KC):
                S.op("pe", lambda h, f=f, k=k, p_t=p_t: h.matmul(p_t[:, 0:TB], lhsT=w1b[:, k, f * 128:(f + 1) * 128],
                                                                  rhs=h_t[:, k, 0:TB], start=(k == 0), stop=(k == KC - 1)),
                     reads=[h_b, w1_bufs[k]], writes=[p_b])
            r_t, r_b = r_ring.next()
            S.op("act", lambda h, p_t=p_t, r_t=r_t: h.activation(out=r_t[:, 0:TB], in_=p_t[:, 0:TB], func=AF.Relu),
                 reads=[p_b], writes=[r_b])
            S.op("pool", lambda h, f=f, r_t=r_t: h.tensor_tensor(out=a_t[:, f, 0:TB], in0=r_t[:, 0:TB], in1=r_t[:, 0:TB],
                                                                  op=ALU.mult),
                 reads=[r_b], writes=[a_b])
        if dbg and b == 0:
            S.dma("sp", dh[:, :, :], h_t[:, :, :], reads=[h_b])
            S.dma("sp", da[:, :, :], a_t[:, :, :], reads=[a_b])
        o_t, o_b = o_ring.next()
        for c in range(KC):
            p_t, p_b = p2_ring.next()
            for f in range(FC):
                S.op("pe", lambda h, f=f, c=c, p_t=p_t: h.matmul(p_t[:, 0:TB], lhsT=w2b[:, f, c * 128:(c + 1) * 128],
                                                                  rhs=a_t[:, f, 0:TB], start=(f == 0), stop=(f == FC - 1)),
                     reads=[a_b, w2_bufs[f // 4]], writes=[p_b])
            S.op("dve", lambda h, c=c, p_t=p_t: h.tensor_tensor(out=o_t[:, c, 0:TB], in0=p_t[:, 0:TB], in1=x_t[:, c, 0:TB],
                                                                 op=ALU.add),
                 reads=[p_b, x_b], writes=[o_b])
        if final_norm:
            f_t, f_b = f_ring.next()
            emit_rmsnorm(cx, o_t, o_b, KC, TB, gf_t, gf_b, ones_t, ones_b, sq_ring, st_ring, rstd_ring, f_t, f_b, D)
            S.dma("pool", ov[:, :, t0:t0 + TB], f_t[:, :, :], reads=[f_b])
        else:
            S.dma("pool", ov[:, :, t0:t0 + TB], o_t[:, :, :], reads=[o_b])
        cur = nxt
        if mid_hook is not None and bi == 1:
            mid_hook()
    cx.pop()
    return cx.finish() if own else None


def run_spmd(nc, in_maps):
    res = run_bass_kernel_spmd(nc, in_maps, core_ids=list(range(len(in_maps))))
    return res.results


def vec128(v, k):
    return np.ascontiguousarray(np.asarray(v, np.float32).reshape(k, 128).T)


def load_cast(cx, q, dst_ap, src_ap, buf):
    cx.S.dma(q, dst_ap, src_ap, writes=[buf])


def build_ea(ntok=TOK, parts='quv', qlvl=4, cx=None, order=None, mid_hook=None):
    TB = 512
    NB = ntok // TB
    own = cx is None
    cx = Ctx() if own else cx
    cx.push()
    S = cx.S
    xT = cx.dram_in("xT", [D, ntok])
    w_in = cx.dram_in("w_in", [D, 1280])
    gin = cx.dram_in("g", [128, KC])
    cosd = cx.dram_in("cos", [128, ntok])
    sind = cx.dram_in("sin", [128, ntok])
    rotd = cx.dram_in("rot", [128, 128])
    QsT = cx.dram_out("QsT", [128, 4, ntok], BF16)
    KT = cx.dram_out("KT", [128, ntok], BF16)
    Vaug = cx.dram_out("Vaug", [ntok, 130], BF16)
    UT = cx.dram_out("UT", [128, 4, ntok], BF16)

    wb = cx.sb("wb", [128, KC, 1280], BF16)
    w_bufs = [Buf(f"w{k}") for k in range(KC)]
    g_t = cx.sb("g", [128, KC], F32); g_b = Buf("g")
    ones_t = cx.sb("ones", [128, 128], BF16); ones_b = Buf("ones")
    rot_t = cx.sb("rot", [128, 128], BF16); rot_b = Buf("rot")
    x_ring = mk_ring(cx, "sb", "x", 2, [128, KC, TB], F32)
    h_ring = mk_ring(cx, "sb", "h", 2, [128, KC, TB], BF16)
    sq_ring = mk_ring(cx, "sb", "sq", 3, [128, TB], BF16)
    rstd_ring = mk_ring(cx, "sb", "rstd", 2, [128, TB], F32)
    cos_ring = mk_ring(cx, "sb", "cos", 2, [128, TB], F32)
    sin_ring = mk_ring(cx, "sb", "sin", 2, [128, TB], F32)
    qb_ring = mk_ring(cx, "sb", "qb", 2, [128, TB], BF16)
    t1_ring = mk_ring(cx, "sb", "t1", 2, [128, TB], F32)
    t2_ring = mk_ring(cx, "sb", "t2", 2, [128, TB], F32)
    qo_ring = mk_ring(cx, "sb", "qo", 2, [128, 5, TB], BF16)
    uo_ring = mk_ring(cx, "sb", "uo", 2, [128, 4, TB], BF16)
    vo_ring = mk_ring(cx, "sb", "vo", 2, [128, 4, 130], BF16)
    st_ring = mk_ring(cx, "ps", "st", 1, [128, 512], F32)
    pq_ring = mk_ring(cx, "ps", "pq", 3, [128, 512], F32)
    pr_ring = mk_ring(cx, "ps", "pr", 2, [128, 512], F32)
    pv_ring = mk_ring(cx, "ps", "pv", 2, [128, 512], F32)

    S.dma("sp", g_t[:, :], gin[:, :], writes=[g_b])
    S.op("dve", lambda h: h.tensor_scalar_mul(out=g_t[:, :], in0=g_t[:, :], scalar1=float(np.sqrt(D))),
         reads=[g_b], writes=[g_b])
    S.op("pool", lambda h: h.memset(ones_t[:, :], 1.0), writes=[ones_b])
    S.dma("pool", rot_t[:, :], rotd[:, :], writes=[rot_b])
    for (vt, vb) in vo_ring.items:
        S.op("pool", lambda h, vt=vt: h.memset(vt[:, :, :], 1.0), writes=[vb])
    for k in range(KC):
        for j in range(2):
            src = w_in[k * 128:(k + 1) * 128, j * 256:(j + 1) * 256].rearrange("p (c d) -> p c d", c=4, d=64)
            dst = wb[:, k, 0:512].rearrange("p (c j d) -> p c j d", c=4, j=2, d=64)[:, :, j, :]
            S.dma("pool", dst, src, writes=[w_bufs[k]])
        S.dma("pool", wb[:, k, 512:1280], w_in[k * 128:(k + 1) * 128, 512:1280], writes=[w_bufs[k]])
    xv = xT.rearrange("(c p) t -> p c t", p=128)

    def blk(b):
        t0 = b * TB
        x_t, x_b = x_ring.next()
        S.dma("sp", x_t[:, :, :], xv[:, :, t0:t0 + TB], writes=[x_b])
        cos_t, cos_b = cos_ring.next()
        sin_t, sin_b = sin_ring.next()
        S.dma("sp", cos_t[:, :], cosd[:, t0:t0 + TB], writes=[cos_b])
        S.dma("sp", sin_t[:, :], sind[:, t0:t0 + TB], writes=[sin_b])
        h_t, h_b = h_ring.next()
        emit_rmsnorm(cx, x_t, x_b, KC, TB, g_t, g_b, ones_t, ones_b, sq_ring, st_ring, rstd_ring, h_t, h_b, D)
        yield
        qo_t, qo_b = qo_ring.next()
        for c in (range(5) if 'q' in parts else []):
            pq_t, pq_b = pq_ring.next()
            for k in range(KC):
                S.op("pe", lambda h, c=c, k=k, pq_t=pq_t, h_t=h_t: h.matmul(
                    pq_t[:, 0:TB], lhsT=wb[:, k, c * 128:(c + 1) * 128], rhs=h_t[:, k, :],
                    start=(k == 0), stop=(k == KC - 1)), reads=[h_b, w_bufs[k]], writes=[pq_b])
            qb_t, qb_b = qb_ring.next()
            S.op("act", lambda h, pq_t=pq_t, qb_t=qb_t: h.activation(out=qb_t[:, :], in_=pq_t[:, 0:TB], func=AF.Copy),
                 reads=[pq_b], writes=[qb_b])
            if qlvl == 1:
                S.op("act", lambda h, c=c, pq_t=pq_t, qo_t=qo_t: h.activation(out=qo_t[:, c, :], in_=pq_t[:, 0:TB], func=AF.Copy),
                     reads=[pq_b], writes=[qo_b])
                continue
            pr_t, pr_b = pr_ring.next()
            S.op("pe", lambda h, pr_t=pr_t, qb_t=qb_t: h.matmul(pr_t[:, 0:TB], lhsT=rot_t[:, :], rhs=qb_t[:, :],
                                                               start=True, stop=True),
                 reads=[qb_b, rot_b], writes=[pr_b])
            t1_t, t1_b = t1_ring.next()
            t2_t, t2_b = t2_ring.next()
            if qlvl == 2:
                S.op("act", lambda h, c=c, pr_t=pr_t, qo_t=qo_t: h.activation(out=qo_t[:, c, :], in_=pr_t[:, 0:TB], func=AF.Copy),
                     reads=[pr_b], writes=[qo_b])
                continue
            S.op("dve", lambda h, t1_t=t1_t, pq_t=pq_t, cos_t=cos_t: h.tensor_tensor(
                out=t1_t[:, :], in0=pq_t[:, 0:TB], in1=cos_t[:, :], op=ALU.mult), reads=[pq_b, cos_b], writes=[t1_b])
            if qlvl == 3:
                S.op("act", lambda h, c=c, t1_t=t1_t, qo_t=qo_t: h.activation(out=qo_t[:, c, :], in_=t1_t[:, :], func=AF.Copy),
                     reads=[t1_b], writes=[qo_b])
                continue
            S.op("dve", lambda h, t2_t=t2_t, pr_t=pr_t, sin_t=sin_t: h.tensor_tensor(
                out=t2_t[:, :], in0=pr_t[:, 0:TB], in1=sin_t[:, :], op=ALU.mult), reads=[pr_b, sin_b], writes=[t2_b])
            S.op("dve", lambda h, c=c, qo_t=qo_t, t1_t=t1_t, t2_t=t2_t: h.tensor_tensor(
                out=qo_t[:, c, :], in0=t1_t[:, :], in1=t2_t[:, :], op=ALU.add), reads=[t1_b, t2_b], writes=[qo_b])
        if 'q' in parts:
            S.dma("pool", QsT[:, :, t0:t0 + TB], qo_t[:, 0:4, :], reads=[qo_b])
            S.dma("pool", KT[:, t0:t0 + TB], qo_t[:, 4, :], reads=[qo_b])
        yield
        uo_t, uo_b = uo_ring.next()
        for gi in (range(4) if 'u' in parts else []):
            pq_t, pq_b = pq_ring.next()
            for k in range(KC):
                S.op("pe", lambda h, gi=gi, k=k, pq_t=pq_t, h_t=h_t: h.matmul(
                    pq_t[:, 0:TB], lhsT=wb[:, k, 768 + gi * 128:768 + (gi + 1) * 128], rhs=h_t[:, k, :],
                    start=(k == 0), stop=(k == KC - 1)), reads=[h_b, w_bufs[k]], writes=[pq_b])
            S.op("act", lambda h, gi=gi, pq_t=pq_t, uo_t=uo_t: h.activation(out=uo_t[:, gi, :], in_=pq_t[:, 0:TB], func=AF.Copy),
                 reads=[pq_b], writes=[uo_b])
        if 'u' in parts:
            S.dma("pool", UT[:, :, t0:t0 + TB], uo_t[:, :, :], reads=[uo_b])
        if 'v' not in parts:
            return
        yield
        vo_t, vo_b = vo_ring.next()
        pv_t, pv_b = pv_ring.next()
        for ti in range(TB // 128):
            for k in range(KC):
                S.op("pe", lambda h, ti=ti, k=k, pv_t=pv_t, h_t=h_t: h.matmul(
                    pv_t[:, ti * 128:(ti + 1) * 128], lhsT=h_t[:, k, ti * 128:(ti + 1) * 128], rhs=wb[:, k, 640:768],
                    start=(k == 0), stop=(k == KC - 1)), reads=[h_b, w_bufs[k]], writes=[pv_b])
        for ti in range(TB // 128):
            for j in range(2):
                S.op("act", lambda h, ti=ti, j=j, pv_t=pv_t, vo_t=vo_t: h.activation(
                    out=vo_t[:, ti, j * 65:j * 65 + 64], in_=pv_t[:, ti * 128 + j * 64:ti * 128 + (j + 1) * 64], func=AF.Copy),
                    reads=[pv_b], writes=[vo_b])
        S.dma("pool", Vaug[t0:t0 + TB, :].rearrange("(i p) n -> p i n", p=128), vo_t[:, :, :], reads=[vo_b])
        yield

    order = list(range(NB)) if order is None else order
    if mid_hook is not None:
        run_interleaved((blk(b) for b in order[:2]), 2, 2)
        mid_hook()
        run_interleaved((blk(b) for b in order[2:]), 2, 2)
    else:
        run_interleaved((blk(b) for b in order), 2, 2)
    cx.pop()
    return cx.finish() if own else None


def rope_tables(pos, half, nrows):
    inv = (np.float32(10000.0) ** (-np.arange(half, dtype=np.float32) / np.float32(half))).astype(np.float32)
    ang = pos.astype(np.float32)[None, :] * inv[np.arange(nrows) % half][:, None]
    return np.cos(ang).astype(np.float32), np.sin(ang).astype(np.float32)


def rot_matrix(dh, nrows=128):
    R = np.zeros((nrows, nrows), np.float32)
    half = dh // 2
    for m in range(nrows):
        d = m % dh
        base = m - d
        if d < half:
            R[base + d + half, m] = -1.0
        else:
            R[base + d - half, m] = 1.0
    return R


def build_eb(ntok=TOK, cx=None):
    TB = 512
    NB = ntok // TB
    NT = ntok // 128
    own = cx is None
    cx = Ctx() if own else cx
    cx.push()
    S = cx.S
    QsT = cx.dram_in("QsT", [128, 4, ntok], BF16)
    KTh = cx.dram_in("KTh", [128, ntok + 256], BF16)
    Vh = cx.dram_in("Vh", [ntok + 256, 130], BF16)
    UTh = cx.dram_in("UTh", [128, 4, ntok + 16], BF16)
    xT = cx.dram_in("xT", [D, ntok])
    w_pool = cx.dram_in("w_pool", [4, 128, 128])
    pscale = cx.dram_in("pscale", [128, 4])
    w_out = cx.dram_in("w_out", [D, D])
    sinkrow = cx.dram_in("sinkrow", [1, 2, 512])
    masksd = cx.dram_in("masks", [4, 128, 512])
    invcd = cx.dram_in("invc", [128, 2, 4, 16])
    identd = cx.dram_in("ident", [128, 128])
    oT = cx.dram_out("oT", [D, ntok])

    woA = cx.sb("woA", [128, 4, D], BF16); woA_b = Buf("woA")
    woB = cx.sb("woB", [128, 4, D], BF16); woB_b = Buf("woB")
    wp = cx.sb("wp", [128, 4, 128], BF16); wp_b = Buf("wp")
    ps_t = cx.sb("ps", [128, 4], F32); ps_b = Buf("ps")
    mk_t = cx.sb("mk", [128, 4, 512], BF16); mk_b = Buf("mk")
    id_t = cx.sb("ident", [128, 128], BF16); id_b = Buf("ident")
    invc_t = cx.sb("invc", [128, 2, 4, 16], F32); invc_b = Buf("invc")
    sk_t = cx.sb("sk", [1, 2, 512], F32); sk_b = Buf("sk")
    esk_t = cx.sb("esk", [1, 2, 512], BF16); esk_b = Buf("esk")
    sel_t = cx.sb("sel", [1, 128], BF16); sel_b = Buf("sel")
    ones32 = cx.sb("ones32", [128, 64], F32); ones32_b = Buf("ones32")
    qsA_ring = mk_ring(cx, "sb", "qsA", 2, [128, 4, TB], BF16)
    qsB_ring = mk_ring(cx, "sb", "qsB", 2, [128, 4, TB], BF16)
    kt_ring = mk_ring(cx, "sb", "kt", 2, [128, 6 * 128], BF16)
    v_ring = mk_ring(cx, "sb", "v", 2, [128, 6, 130], BF16)
    u_ring = mk_ring(cx, "sb", "u", 2, [128, 4, TB + 16], BF16)
    x_ring = mk_ring(cx, "sb", "x", 2, [128, KC, TB], F32)
    p_ring = mk_ring(cx, "sb", "p", 4, [128, 512], BF16)
    osb_ring = mk_ring(cx, "sb", "osb", 3, [128, 512], F32)
    rc_ring = mk_ring(cx, "sb", "rc", 3, [128, 512], F32)
    ya_ring = mk_ring(cx, "sb", "ya", 2, [64, 8, TB], BF16)
    yp_ring = mk_ring(cx, "sb", "yp", 2, [128, 4, TB], BF16)
    yb_ring = mk_ring(cx, "sb", "yb", 2, [128, 4, TB], BF16)
    d_ring = mk_ring(cx, "sb", "d", 2, [128, 4, TB], BF16)
    tmp_rings = [mk_ring(cx, "sb", f"tp{g}", 2, [128, TB + 16], F32) for g in range(4)]
    e16_ring = mk_ring(cx, "sb", "e16", 2, [128, 16], F32)
    s_ring = mk_ring(cx, "ps", "s", 3, [128, 512], F32)
    o_ring = mk_ring(cx, "ps", "o", 2, [128, 512], F32)
    bc_ring = mk_ring(cx, "ps", "bc", 1, [128, 512], F32)
    y_ring = mk_ring(cx, "ps", "y", 2, [128, 512], F32)

    S.dma("pool", woA[:, :, :], w_out[0:512, :].rearrange("(i p) n -> p i n", p=128), writes=[woA_b])
    S.dma("pool", woB[:, :, :], w_out[512:1024, :].rearrange("(g p) n -> p g n", p=128), writes=[woB_b])
    S.dma("pool", wp[:, :, :], w_pool.rearrange("g i j -> i g j"), writes=[wp_b])
    S.dma("pool", mk_t[:, :, :], masksd.rearrange("m p n -> p m n"), writes=[mk_b])
    S.op("dve", lambda h: h.tensor_scalar(out=mk_t[:, :, :], in0=mk_t[:, :, :], scalar1=-1.0, scalar2=30000.0, op0=ALU.add, op1=ALU.mult),
         reads=[mk_b], writes=[mk_b])
    S.dma("pool", id_t[:, :], identd[:, :], writes=[id_b])
    S.dma("sp", ps_t[:, :], pscale[:, :], writes=[ps_b])
    S.dma("sp", invc_t[:, :, :, :], invcd[:, :, :, :], writes=[invc_b])
    S.dma("sp", sk_t[:, :, :], sinkrow[:, :, :], writes=[sk_b])
    S.op("act", lambda h: h.activation(out=esk_t[:, :, :], in_=sk_t[:, :, :], func=AF.Exp), reads=[sk_b], writes=[esk_b])
    S.op("pool", lambda h: h.memset(sel_t[:, :], 0.0), writes=[sel_b])
    S.op("pool", lambda h: h.memset(sel_t[:, 64:65], 1.0), writes=[sel_b])
    S.op("pool", lambda h: h.memset(ones32[:, :], 1.0), writes=[ones32_b])
    for (qt, qb_) in qsA_ring.items:
        S.op("pool", lambda h, qt=qt: h.memset(qt[64:128, :, :], 0.0), writes=[qb_])
    for (qt, qb_) in qsB_ring.items:
        S.op("pool", lambda h, qt=qt: h.memset(qt[0:64, :, :], 0.0), writes=[qb_])
    xv = xT.rearrange("(c p) t -> p c t", p=128)
    ov = oT.rearrange("(c p) t -> p c t", p=128)
    cx.run_hook()

    def blk(b):
        t0 = b * TB
        qsA_t, qsA_b = qsA_ring.next()
        qsB_t, qsB_b = qsB_ring.next()
        kt_t, kt_b = kt_ring.next()
        v_t, v_b = v_ring.next()
        u_t, u_b = u_ring.next()
        x_t, x_b = x_ring.next()
        S.dma("sp", qsA_t[0:64, :, :], QsT[0:64, :, t0:t0 + TB], writes=[qsA_b])
        S.dma("sp", qsB_t[64:128, :, :], QsT[64:128, :, t0:t0 + TB], writes=[qsB_b])
        S.dma("sp", kt_t[:, :], KTh[:, t0:t0 + 768], writes=[kt_b])
        S.dma("sp", v_t[:, :, :], Vh[t0:t0 + 768, :].rearrange("(i p) n -> p i n", p=128), writes=[v_b])
        S.dma("sp", u_t[:, :, :], UTh[:, :, t0:t0 + TB + 16], writes=[u_b])
        S.dma("sp", x_t[:, :, :], xv[:, :, t0:t0 + TB], writes=[x_b])
        yield
        ya_t, ya_b = ya_ring.next()
        tiles = [(nl, j, mi, dm) for nl in range(4) for mi, dm in enumerate((-1, 0, 1)) for j in range(2)]
        LA = 2
        st = {}
        unit_o = {}
        deferred = []

        def emit_S(t):
            nl, j, mi, dm = tiles[t]
            i = nl + dm + 1
            s_t, s_b = s_ring.next()
            q_t, q_b = (qsA_t, qsA_b) if j == 0 else (qsB_t, qsB_b)
            S.op("pe", lambda h: h.matmul(s_t[:, :], lhsT=kt_t[:, i * 128:(i + 1) * 128],
                                          rhs=q_t[:, :, nl * 128:(nl + 1) * 128], start=True, stop=(dm == 0)),
                 reads=[kt_b, q_b], writes=[s_b])
            if dm != 0:
                n_ = 4 * b + nl
                if dm == -1:
                    mi_ = 2 if n_ == 0 else 0
                else:
                    mi_ = 3 if n_ == NT - 1 else 1
                S.op("pe", lambda h: h.matmul(s_t[:, :], lhsT=id_t[:, :], rhs=mk_t[:, mi_, :], start=False, stop=True),
                     reads=[id_b, mk_b], writes=[s_b])
            st[t] = (s_t, s_b)

        def flush_deferred():
            while deferred:
                (o_t, o_b, osb_t, osb_b, rc_t, rc_b, nl, j) = deferred.pop(0)
                S.op("act", lambda h: h.activation(out=rc_t[64:65, :], in_=osb_t[64:65, :], func=AF.Ln), reads=[osb_b], writes=[rc_b])
                S.op("act", lambda h: h.activation(out=rc_t[64:65, :], in_=rc_t[64:65, :], func=AF.Exp, scale=-1.0),
                     reads=[rc_b], writes=[rc_b])
                bc_t, bc_b = bc_ring.next()
                S.op("pe", lambda h: h.matmul(bc_t[0:64, :], lhsT=ones32[64:65, 0:64], rhs=rc_t[64:65, :], start=True, stop=True),
                     reads=[rc_b, ones32_b], writes=[bc_b])
                S.op("dve", lambda h: h.tensor_tensor(
                    out=ya_t[0:64, j * 4:(j + 1) * 4, nl * 128:(nl + 1) * 128],
                    in0=osb_t[0:64, :].rearrange("p (c q) -> p c q", c=4),
                    in1=bc_t[0:64, :].rearrange("p (c q) -> p c q", c=4), op=ALU.mult),
                    reads=[osb_b, bc_b], writes=[ya_b])

        for t in range(min(LA, len(tiles))):
            emit_S(t)
        for t in range(len(tiles)):
            nl, j, mi, dm = tiles[t]
            n = 4 * b + nl
            i = nl + dm + 1
            if mi == 0:
                unit_o[(nl, j)] = o_ring.next()
            o_t, o_b = unit_o[(nl, j)]
            s_t, s_b = st.pop(t)
            p_t, p_b = p_ring.next()
            S.op("act", lambda h: h.activation(out=p_t[:, :], in_=s_t[:, :], func=AF.Exp, scale=0.125), reads=[s_b], writes=[p_b])
            if t + LA < len(tiles):
                emit_S(t + LA)
            S.op("pe", lambda h: h.matmul(o_t[0:65, :], lhsT=v_t[:, i, j * 65:(j + 1) * 65], rhs=p_t[:, :], start=(mi == 0), stop=False),
                 reads=[v_b, p_b], writes=[o_b])
            if mi == 0 and j == 1:
                flush_deferred()
            if mi == 2:
                S.op("pe", lambda h: h.matmul(o_t[0:65, :], lhsT=sel_t[0:1, 0:65], rhs=esk_t[0:1, j, :], start=False, stop=True),
                     reads=[sel_b, esk_b], writes=[o_b])
                osb_t, osb_b = osb_ring.next()
                rc_t, rc_b = rc_ring.next()
                S.op("dve", lambda h: h.tensor_copy(out=osb_t[0:65, :], in_=o_t[0:65, :]), reads=[o_b], writes=[osb_b])
                deferred.append((o_t, o_b, osb_t, osb_b, rc_t, rc_b, nl, j))
        flush_deferred()
        yield
        yp_t, yp_b = yp_ring.next()
        S.dma("sp", yp_t[0:64, :, :], ya_t[0:64, 0:8:2, :], reads=[ya_b], writes=[yp_b])
        S.dma("sp", yp_t[64:128, :, :], ya_t[0:64, 1:8:2, :], reads=[ya_b], writes=[yp_b])
        d_t, d_b = d_ring.next()
        L = TB + 16
        for g in range(4):
            w = 2 << g
            steps = g + 1
            src_t, src_b, ln = None, None, L
            for s_i in range(steps):
                sh = 1 << s_i
                tp_t, tp_b = tmp_rings[g].next()
                nl_ = ln - sh
                if s_i == 0:
                    S.op("pool", lambda h, tp_t=tp_t, u_t=u_t, g=g, nl_=nl_, sh=sh: h.tensor_tensor(
                        out=tp_t[:, 0:nl_], in0=u_t[:, g, 0:nl_], in1=u_t[:, g, sh:sh + nl_], op=ALU.add),
                        reads=[u_b], writes=[tp_b])
                else:
                    S.op("pool", lambda h, tp_t=tp_t, src_t=src_t, nl_=nl_, sh=sh: h.tensor_tensor(
                        out=tp_t[:, 0:nl_], in0=src_t[:, 0:nl_], in1=src_t[:, sh:sh + nl_], op=ALU.add),
                        reads=[src_b], writes=[tp_b])
                src_t, src_b, ln = tp_t, tp_b, nl_
            off = 8 - w // 2
            S.op("dve", lambda h, d_t=d_t, src_t=src_t, u_t=u_t, g=g, off=off, w=w: h.scalar_tensor_tensor(
                out=d_t[:, g, :], in0=src_t[:, off:off + TB], scalar=1.0 / w, in1=u_t[:, g, 8:8 + TB],
                op0=ALU.mult, op1=ALU.subtract), reads=[src_b, u_b], writes=[d_b])
            for (is_edge, which, c0) in ((b == 0, 0, 0), (b == NB - 1, 1, TB - 16)):
                if not is_edge:
                    continue
                e_t, e_b = e16_ring.next()
                S.op("dve", lambda h, e_t=e_t, src_t=src_t, g=g, off=off, c0=c0, which=which: h.tensor_tensor(
                    out=e_t[:, :], in0=src_t[:, off + c0:off + c0 + 16], in1=invc_t[:, which, g, :], op=ALU.mult),
                    reads=[src_b, invc_b], writes=[e_b])
                S.op("dve", lambda h, e_t=e_t, d_t=d_t, u_t=u_t, g=g, c0=c0: h.tensor_tensor(
                    out=d_t[:, g, c0:c0 + 16], in0=e_t[:, :], in1=u_t[:, g, 8 + c0:8 + c0 + 16], op=ALU.subtract),
                    reads=[e_b, u_b, d_b], writes=[d_b])
        yield
        yb_t, yb_b = yb_ring.next()
        for g in range(4):
            y_t, y_b = y_ring.next()
            S.op("pe", lambda h, y_t=y_t, d_t=d_t, g=g: h.matmul(y_t[:, :], lhsT=wp[:, g, :], rhs=d_t[:, g, :], start=True, stop=True),
                 reads=[wp_b, d_b], writes=[y_b])
            S.op("dve", lambda h, y_t=y_t, yb_t=yb_t, g=g: h.tensor_scalar_mul(out=yb_t[:, g, :], in0=y_t[:, :], scalar1=ps_t[:, g:g + 1]),
                 reads=[y_b, ps_b], writes=[yb_b])
        for o in range(KC):
            y_t, y_b = y_ring.next()
            for hh in range(4):
                S.op("pe", lambda h, y_t=y_t, yp_t=yp_t, hh=hh, o=o: h.matmul(
                    y_t[:, :], lhsT=woA[:, hh, o * 128:(o + 1) * 128], rhs=yp_t[:, hh, :], start=(hh == 0), stop=False),
                    reads=[woA_b, yp_b], writes=[y_b])
            for g in range(4):
                S.op("pe", lambda h, y_t=y_t, yb_t=yb_t, g=g, o=o: h.matmul(
                    y_t[:, :], lhsT=woB[:, g, o * 128:(o + 1) * 128], rhs=yb_t[:, g, :], start=False, stop=(g == 3)),
                    reads=[woB_b, yb_b], writes=[y_b])
            S.op("dve", lambda h, y_t=y_t, x_t=x_t, o=o: h.tensor_tensor(out=x_t[:, o, :], in0=y_t[:, :], in1=x_t[:, o, :], op=ALU.add),
                 reads=[y_b, x_b], writes=[x_b])
        S.dma("sp", ov[:, :, t0:t0 + TB], x_t[:, :, :], reads=[x_b])
        yield

    run_interleaved((blk(b) for b in range(NB)), 2, 2)
    cx.pop()
    return cx.finish() if own else None


def eb_masks(has_left, has_right):
    ki = np.arange(128)[:, None]
    qi = np.arange(128)[None, :]
    mL = np.tile((ki >= qi).astype(np.float32), (1, 4))
    mR = np.tile((ki <= qi).astype(np.float32), (1, 4))
    return np.stack([mL, mR, mL * float(has_left), mR * float(has_right)]).astype(np.float32)


def eb_invc(is_first, is_last):
    out = np.zeros((128, 2, 4, 16), np.float32)
    for g in range(4):
        w = 2 << g
        half = w // 2
        for i in range(16):
            c0 = min(i + half, w) if is_first else w
            r = 16 - i
            c1 = min(half + r, w) if is_last else w
            out[:, 0, g, i] = 1.0 / c0
            out[:, 1, g, i] = 1.0 / c1
    return out


def emit_rope(cx, src_t, src_b, nrow, TB, rot_t, rot_b, cos_t, cos_b, sin_t, sin_b, qb_ring, pr_ring, t1_ring, t2_ring,
              out_ap, out_b):
    S = cx.S
    qb_t, qb_b = qb_ring.next()
    S.op("act", lambda h: h.activation(out=qb_t[0:nrow, :], in_=src_t[0:nrow, 0:TB], func=AF.Copy), reads=[src_b], writes=[qb_b])
    pr_t, pr_b = pr_ring.next()
    S.op("pe", lambda h: h.matmul(pr_t[0:nrow, 0:TB], lhsT=rot_t[0:nrow, 0:nrow], rhs=qb_t[0:nrow, :], start=True, stop=True),
         reads=[qb_b, rot_b], writes=[pr_b])
    t1_t, t1_b = t1_ring.next()
    t2_t, t2_b = t2_ring.next()
    S.op("dve", lambda h: h.tensor_tensor(out=t1_t[0:nrow, :], in0=src_t[0:nrow, 0:TB], in1=cos_t[0:nrow, :], op=ALU.mult),
         reads=[src_b, cos_b], writes=[t1_b])
    S.op("dve", lambda h: h.tensor_tensor(out=t2_t[0:nrow, :], in0=pr_t[0:nrow, 0:TB], in1=sin_t[0:nrow, :], op=ALU.mult),
         reads=[pr_b, sin_b], writes=[t2_b])
    S.op("dve", lambda h: h.tensor_tensor(out=out_ap, in0=t1_t[0:nrow, :], in1=t2_t[0:nrow, :], op=ALU.add),
         reads=[t1_b, t2_b], writes=[out_b])


def build_oa(ntok=TOK, cx=None, mid_hook=None):
    TB = 512
    NB = ntok // TB
    NT = ntok // 128
    own = cx is None
    cx = Ctx() if own else cx
    cx.push()
    S = cx.S
    xT = cx.dram_in("xT", [D, ntok])
    xhalo = cx.dram_in("xhalo", [D, 4])
    w_in = cx.dram_in("w_in", [D, 1440])
    gin = cx.dram_in("g", [128, KC])
    gcq = cx.dram_in("g_cq", [128, 2])
    gckv = cx.dram_in("g_ckv", [128, 1])
    w_uq = cx.dram_in("w_uq", [256, 768])
    w_ukv = cx.dram_in("w_ukv", [128, 1024])
    cwd = cx.dram_in("cw", [128, 4, 4])
    cbd = cx.dram_in("cb", [128, 4])
    wad = cx.dram_in("wa", [2, 8, 64, 64])
    wxd = cx.dram_in("wx", [2, 8, 64, 64])
    bad = cx.dram_in("ba", [128, 2, 4])
    bxd = cx.dram_in("bx", [128, 2, 4])
    lamd = cx.dram_in("lam", [128, 2, 4])
    cosd = cx.dram_in("cos", [128, ntok])
    sind = cx.dram_in("sin", [128, ntok])
    rotd = cx.dram_in("rot", [128, 128])
    QN = cx.dram_out("QN", [512, ntok], BF16)
    QR = cx.dram_out("QR", [256, ntok], BF16)
    KNR = cx.dram_out("KNR", [544, ntok], BF16)
    V5 = cx.dram_out("V5", [1024, NT * 65], BF16)
    GX = cx.dram_out("GX", [512, ntok], BF16)
    AB = cx.dram_out("AB", [2, 2, 512, ntok])
    BLK = cx.dram_out("BLK", [128, NB, 2, 2, 4])
    CAB = cx.dram_out("CAB", [128, 2, 2, 4])
    XR = cx.dram_out("XR", [512, ntok])

    g_t = cx.sb("g", [128, KC], F32); g_b = Buf("g")
    gcq_t = cx.sb("gcq", [128, 2], F32); gcq_b = Buf("gcq")
    gckv_t = cx.sb("gckv", [128, 1], F32); gckv_b = Buf("gckv")
    ones_t = cx.sb("ones", [128, 128], BF16); ones_b = Buf("ones")
    xrh_t = cx.sb("xrh", [128, 4, 4], F32); xrh_b = Buf("xrh")
    cp_t = cx.sb("cp", [128, 2, 4], F32); cp_b = Buf("cp")
    blk_t = cx.sb("blk", [128, NB, 2, 2, 4], F32); blk_b = Buf("blk")
    S.dma("sp", g_t[:, :], gin[:, :], writes=[g_b])
    S.op("dve", lambda h: h.tensor_scalar_mul(out=g_t[:, :], in0=g_t[:, :], scalar1=float(np.sqrt(D))), reads=[g_b], writes=[g_b])
    S.dma("sp", gcq_t[:, :], gcq[:, :], writes=[gcq_b])
    S.op("dve", lambda h: h.tensor_scalar_mul(out=gcq_t[:, :], in0=gcq_t[:, :], scalar1=16.0), reads=[gcq_b], writes=[gcq_b])
    S.dma("sp", gckv_t[:, :], gckv[:, :], writes=[gckv_b])
    S.op("dve", lambda h: h.tensor_scalar_mul(out=gckv_t[:, :], in0=gckv_t[:, :], scalar1=float(np.sqrt(128.0))),
         reads=[gckv_b], writes=[gckv_b])
    S.op("pool", lambda h: h.memset(ones_t[:, :], 1.0), writes=[ones_b])
    S.dma("sp", cp_t[:, :, :], lamd[:, :, :], writes=[cp_b])
    S.op("act", lambda h: h.activation(out=cp_t[:, :, :], in_=cp_t[:, :, :], func=AF.Exp, scale=-1.0), reads=[cp_b], writes=[cp_b])
    S.op("act", lambda h: h.activation(out=cp_t[:, :, :], in_=cp_t[:, :, :], func=AF.Ln, bias=1.0), reads=[cp_b], writes=[cp_b])
    S.op("dve", lambda h: h.tensor_scalar_mul(out=cp_t[:, :, :], in0=cp_t[:, :, :], scalar1=-8.0), reads=[cp_b], writes=[cp_b])

    wabd = cx.sb("wabd", [128, 2, 4, 128], BF16); wxbd = cx.sb("wxbd", [128, 2, 4, 128], BF16); bd_b = Buf("bd")
    cw_t = cx.sb("cw", [128, 4, 4], F32); cb_t = cx.sb("cb", [128, 4], F32); cw_b = Buf("cw")
    ba_t = cx.sb("ba", [128, 2, 4], F32); bx_t = cx.sb("bx", [128, 2, 4], F32); bb_b = Buf("bb")
    S.op("pool", lambda h: h.memset(wabd[:, :, :, :], 0.0), writes=[bd_b])
    S.op("pool", lambda h: h.memset(wxbd[:, :, :, :], 0.0), writes=[bd_b])
    for d in range(2):
        for c in range(4):
            for hf in range(2):
                S.dma("pool", wabd[hf * 64:(hf + 1) * 64, d, c, hf * 64:(hf + 1) * 64], wad[d, 2 * c + hf, :, :], writes=[bd_b])
                S.dma("pool", wxbd[hf * 64:(hf + 1) * 64, d, c, hf * 64:(hf + 1) * 64], wxd[d, 2 * c + hf, :, :], writes=[bd_b])
    S.dma("sp", cw_t[:, :, :], cwd[:, :, :], writes=[cw_b])
    S.dma("sp", cb_t[:, :], cbd[:, :], writes=[cw_b])
    S.dma("sp", ba_t[:, :, :], bad[:, :, :], writes=[bb_b])
    S.dma("sp", bx_t[:, :, :], bxd[:, :, :], writes=[bb_b])
    xv = xT.rearrange("(c p) t -> p c t", p=128)
    cx.push()
    wb = cx.sb("wb", [128, KC, 1440], BF16)
    w_bufs = [Buf(f"w{k}") for k in range(KC)]
    wuqn = cx.sb("wuqn", [128, 2, 512], BF16); wuqr = cx.sb("wuqr", [128, 2, 256], BF16); wuq_b = Buf("wuq")
    wk = cx.sb("wk", [128, 512], BF16); wv = cx.sb("wv", [128, 512], BF16); wkv_b = Buf("wkv")
    rot_t = cx.sb("rot", [128, 128], BF16); rot_b = Buf("rot")
    x_ring = mk_ring(cx, "sb", "x", 2, [128, KC, TB], F32)
    h_ring = mk_ring(cx, "sb", "h", 2, [128, KC, TB], BF16)
    sq_ring = mk_ring(cx, "sb", "sq", 3, [128, TB], BF16)
    rstd_ring = mk_ring(cx, "sb", "rstd", 2, [128, TB], F32)
    cos_ring = mk_ring(cx, "sb", "cos", 2, [128, TB], F32)
    sin_ring = mk_ring(cx, "sb", "sin", 2, [128, TB], F32)
    qb_ring = mk_ring(cx, "sb", "qb", 2, [128, TB], BF16)
    t1_ring = mk_ring(cx, "sb", "t1", 2, [128, TB], F32)
    t2_ring = mk_ring(cx, "sb", "t2", 2, [128, TB], F32)
    cq_ring = mk_ring(cx, "sb", "cq", 2, [128, 2, TB], F32)
    ckv_ring = mk_ring(cx, "sb", "ckv", 2, [128, 1, TB], F32)
    cqn_ring = mk_ring(cx, "sb", "cqn", 2, [128, 2, TB], BF16)
    ckvn_ring = mk_ring(cx, "sb", "ckvn", 2, [128, 1, TB], BF16)
    xr_ring = mk_ring(cx, "sb", "xr", 2, [128, 4, TB], F32)
    gx_ring = mk_ring(cx, "sb", "gx", 2, [128, 4, TB], BF16)
    qn_ring = mk_ring(cx, "sb", "qn", 2, [128, 4, TB], BF16)
    qr_ring = mk_ring(cx, "sb", "qr", 2, [128, 2, TB], BF16)
    kn_ring = mk_ring(cx, "sb", "kn", 2, [128, 4, TB], BF16)
    kr_ring = mk_ring(cx, "sb", "kr", 2, [32, TB], BF16)
    vo_ring = mk_ring(cx, "sb", "vo", 2, [128, 4, 520], BF16)
    hx_t = cx.sb("hx", [128, KC, 4], F32); hx_b = Buf("hx")
    hh_t = cx.sb("hh", [128, KC, 4], BF16); hh_b = Buf("hh")
    st_ring = mk_ring(cx, "ps", "st", 1, [128, 512], F32)
    pq_ring = mk_ring(cx, "ps", "pq", 4, [128, 512], F32)
    pr_ring = mk_ring(cx, "ps", "pr", 1, [128, 512], F32)
    pv_ring = mk_ring(cx, "ps", "pv", 2, [128, 512], F32)

    for k in range(KC):
        S.dma("pool", wb[:, k, :], w_in[k * 128:(k + 1) * 128, :], writes=[w_bufs[k]])
    for k in range(2):
        src = w_uq[k * 128:(k + 1) * 128, :].rearrange("p (h e) -> p h e", e=96)
        S.dma("pool", wuqn[:, k, :].rearrange("p (h d) -> p h d", d=64), src[:, :, 0:64], writes=[wuq_b])
        S.dma("pool", wuqr[:, k, :].rearrange("p (h d) -> p h d", d=32), src[:, :, 64:96], writes=[wuq_b])
    srckv = w_ukv.rearrange("p (h e) -> p h e", e=128)
    S.dma("pool", wk[:, :].rearrange("p (h d) -> p h d", d=64), srckv[:, :, 0:64], writes=[wkv_b])
    S.dma("pool", wv[:, :].rearrange("p (h d) -> p h d", d=64), srckv[:, :, 64:128], writes=[wkv_b])
    S.dma("pool", rot_t[:, :], rotd[:, :], writes=[rot_b])
    for (vt, vb) in vo_ring.items:
        S.op("pool", lambda h, vt=vt: h.memset(vt[:, :, :], 1.0), writes=[vb])

    def proj_tile(h_t, h_b, c0, ncols, TBx):
        pq_t, pq_b = pq_ring.next()
        for k in range(KC):
            S.op("pe", lambda h, k=k: h.matmul(pq_t[0:ncols, 0:TBx], lhsT=wb[:, k, c0:c0 + ncols], rhs=h_t[:, k, 0:TBx],
                                               start=(k == 0), stop=(k == KC - 1)), reads=[h_b, w_bufs[k]], writes=[pq_b])
        return pq_t, pq_b

    S.dma("sp", hx_t[:, :, :], xhalo.rearrange("(c p) t -> p c t", p=128), writes=[hx_b])
    emit_rmsnorm(cx, hx_t, hx_b, KC, 4, g_t, g_b, ones_t, ones_b, sq_ring, st_ring, rstd_ring, hh_t, hh_b, D)
    for c in range(4):
        pq_t, pq_b = proj_tile(hh_t, hh_b, 416 + c * 128, 128, 4)
        S.op("act", lambda h, c=c, pq_t=pq_t: h.activation(out=xrh_t[:, c, :], in_=pq_t[:, 0:4], func=AF.Copy),
             reads=[pq_b], writes=[xrh_b])

    def blk(b):
        t0 = b * TB
        x_t, x_b = x_ring.next()
        S.dma("sp", x_t[:, :, :], xv[:, :, t0:t0 + TB], writes=[x_b])
        cos_t, cos_b = cos_ring.next()
        sin_t, sin_b = sin_ring.next()
        S.dma("sp", cos_t[:, :], cosd[:, t0:t0 + TB], writes=[cos_b])
        S.dma("sp", sin_t[:, :], sind[:, t0:t0 + TB], writes=[sin_b])
        h_t, h_b = h_ring.next()
        emit_rmsnorm(cx, x_t, x_b, KC, TB, g_t, g_b, ones_t, ones_b, sq_ring, st_ring, rstd_ring, h_t, h_b, D)
        yield
        cq_t, cq_b = cq_ring.next()
        for c in range(2):
            pq_t, pq_b = proj_tile(h_t, h_b, c * 128, 128, TB)
            S.op("act", lambda h, c=c, pq_t=pq_t, cq_t=cq_t: h.activation(out=cq_t[:, c, :], in_=pq_t[:, 0:TB], func=AF.Copy),
                 reads=[pq_b], writes=[cq_b])
        ckv_t, ckv_b = ckv_ring.next()
        pq_t, pq_b = proj_tile(h_t, h_b, 256, 128, TB)
        S.op("act", lambda h, pq_t=pq_t, ckv_t=ckv_t: h.activation(out=ckv_t[:, 0, :], in_=pq_t[:, 0:TB], func=AF.Copy),
             reads=[pq_b], writes=[ckv_b])
        yield
        pq_t, pq_b = proj_tile(h_t, h_b, 384, 32, TB)
        kr_t, kr_b = kr_ring.next()
        emit_rope(cx, pq_t, pq_b, 32, TB, rot_t, rot_b, cos_t, cos_b, sin_t, sin_b, qb_ring, pr_ring, t1_ring, t2_ring,
                  kr_t[0:32, :], kr_b)
        S.dma("pool", KNR[512:544, t0:t0 + TB], kr_t[:, :], reads=[kr_b])
        yield
        xr_t, xr_b = xr_ring.next()
        gx_t, gx_b = gx_ring.next()
        for c in range(4):
            pq_t, pq_b = proj_tile(h_t, h_b, 416 + c * 128, 128, TB)
            S.op("act", lambda h, c=c, pq_t=pq_t, xr_t=xr_t: h.activation(out=xr_t[:, c, :], in_=pq_t[:, 0:TB], func=AF.Copy),
                 reads=[pq_b], writes=[xr_b])
        for c in range(4):
            pq_t, pq_b = proj_tile(h_t, h_b, 928 + c * 128, 128, TB)
            S.op("act", lambda h, c=c, pq_t=pq_t, gx_t=gx_t: h.activation(out=gx_t[:, c, :], in_=pq_t[:, 0:TB], func=AF.Gelu_apprx_tanh),
                 reads=[pq_b], writes=[gx_b])
        S.dma("pool", XR.rearrange("(c p) t -> p c t", p=128)[:, :, t0:t0 + TB], xr_t[:, :, :], reads=[xr_b])
        S.dma("pool", GX.rearrange("(c p) t -> p c t", p=128)[:, :, t0:t0 + TB], gx_t[:, :, :], reads=[gx_b])
        yield
        cqn_t, cqn_b = cqn_ring.next()
        emit_rmsnorm(cx, cq_t, cq_b, 2, TB, gcq_t, gcq_b, ones_t, ones_b, sq_ring, st_ring, rstd_ring, cqn_t, cqn_b, 256)
        ckvn_t, ckvn_b = ckvn_ring.next()
        emit_rmsnorm(cx, ckv_t, ckv_b, 1, TB, gckv_t, gckv_b, ones_t, ones_b, sq_ring, st_ring, rstd_ring, ckvn_t, ckvn_b, 128)
        yield
        qn_t, qn_b = qn_ring.next()
        for i in range(4):
            pq_t, pq_b = pq_ring.next()
            for k in range(2):
                S.op("pe", lambda h, i=i, k=k, pq_t=pq_t, cqn_t=cqn_t: h.matmul(
                    pq_t[:, 0:TB], lhsT=wuqn[:, k, i * 128:(i + 1) * 128], rhs=cqn_t[:, k, :], start=(k == 0), stop=(k == 1)),
                    reads=[cqn_b, wuq_b], writes=[pq_b])
            S.op("act", lambda h, i=i, pq_t=pq_t, qn_t=qn_t: h.activation(out=qn_t[:, i, :], in_=pq_t[:, 0:TB], func=AF.Copy),
                 reads=[pq_b], writes=[qn_b])
        S.dma("pool", QN.rearrange("(c p) t -> p c t", p=128)[:, :, t0:t0 + TB], qn_t[:, :, :], reads=[qn_b])
        qr_t, qr_b = qr_ring.next()
        for i in range(2):
            pq_t, pq_b = pq_ring.next()
            for k in range(2):
                S.op("pe", lambda h, i=i, k=k, pq_t=pq_t, cqn_t=cqn_t: h.matmul(
                    pq_t[:, 0:TB], lhsT=wuqr[:, k, i * 128:(i + 1) * 128], rhs=cqn_t[:, k, :], start=(k == 0), stop=(k == 1)),
                    reads=[cqn_b, wuq_b], writes=[pq_b])
            emit_rope(cx, pq_t, pq_b, 128, TB, rot_t, rot_b, cos_t, cos_b, sin_t, sin_b, qb_ring, pr_ring, t1_ring, t2_ring,
                      qr_t[:, i, :], qr_b)
        S.dma("pool", QR.rearrange("(c p) t -> p c t", p=128)[:, :, t0:t0 + TB], qr_t[:, :, :], reads=[qr_b])
        yield
        kn_t, kn_b = kn_ring.next()
        for i in range(4):
            pq_t, pq_b = pq_ring.next()
            S.op("pe", lambda h, i=i, pq_t=pq_t, ckvn_t=ckvn_t: h.matmul(
                pq_t[:, 0:TB], lhsT=wk[:, i * 128:(i + 1) * 128], rhs=ckvn_t[:, 0, :], start=True, stop=True),
                reads=[ckvn_b, wkv_b], writes=[pq_b])
            S.op("act", lambda h, i=i, pq_t=pq_t, kn_t=kn_t: h.activation(out=kn_t[:, i, :], in_=pq_t[:, 0:TB], func=AF.Copy),
                 reads=[pq_b], writes=[kn_b])
        S.dma("pool", KNR[0:512, :].rearrange("(c p) t -> p c t", p=128)[:, :, t0:t0 + TB], kn_t[:, :, :], reads=[kn_b])
        yield
        vo_t, vo_b = vo_ring.next()
        for ti in range(TB // 128):
            pv_t, pv_b = pv_ring.next()
            S.op("pe", lambda h, ti=ti, pv_t=pv_t, ckvn_t=ckvn_t: h.matmul(
                pv_t[:, :], lhsT=ckvn_t[:, 0, ti * 128:(ti + 1) * 128], rhs=wv[:, :], start=True, stop=True),
                reads=[ckvn_b, wkv_b], writes=[pv_b])
            S.op("act", lambda h, ti=ti, pv_t=pv_t, vo_t=vo_t: h.activation(
                out=vo_t[:, ti, :].rearrange("p (h e) -> p h e", e=65)[:, :, 0:64],
                in_=pv_t[:, :].rearrange("p (h d) -> p h d", d=64), func=AF.Copy), reads=[pv_b], writes=[vo_b])
        for hd in range(8):
            S.dma("pool", V5[hd * 128:(hd + 1) * 128, :].rearrange("p (i e) -> p i e", e=65)[:, b * 4:(b + 1) * 4, :],
                  vo_t[:, :, hd * 65:(hd + 1) * 65], reads=[vo_b])
        yield

    run_interleaved((blk(b) for b in range(NB)), 2)
    cx.pop()
    if mid_hook is not None:
        mid_hook()

    cx.push()
    xe_ring = mk_ring(cx, "sb", "xe", 2, [128, 4, TB + 4], F32)
    xc_ring = mk_ring(cx, "sb", "xc", 2, [128, 4, TB], F32)
    xcb_ring = mk_ring(cx, "sb", "xcb", 2, [128, 4, TB], BF16)
    r_ring = mk_ring(cx, "sb", "r", 2, [128, 8, TB], F32)
    i_ring = mk_ring(cx, "sb", "i", 2, [128, 8, TB], F32)
    a_ring = mk_ring(cx, "sb", "a", 2, [128, 8, TB], F32)
    b_ring = mk_ring(cx, "sb", "b", 2, [128, 8, TB], F32)
    hl_ring = mk_ring(cx, "sb", "hl", 2, [128, TB], F32)
    sr_ring = mk_ring(cx, "sb", "sr", 2, [128, 8], F32)
    pg_ring = mk_ring(cx, "ps", "pg", 6, [128, 512], F32)
    XRv = XR.rearrange("(c p) t -> p c t", p=128)
    ABv = AB.rearrange("d s (c p) t -> d s p c t", p=128)

    def blk(b):
        t0 = b * TB
        xe_t, xe_b = xe_ring.next()
        lo = 0 if b > 0 else 2
        hi = TB + 3 if b < NB - 1 else TB + 2
        S.dma("sp", xe_t[:, :, lo:hi], XRv[:, :, t0 - 2 + lo:t0 - 2 + hi], writes=[xe_b])
        if b == 0:
            S.op("dve", lambda h, xe_t=xe_t: h.tensor_copy(out=xe_t[:, :, 0:2], in_=xrh_t[:, :, 0:2]), reads=[xrh_b, xe_b], writes=[xe_b])
        if b == NB - 1:
            S.op("dve", lambda h, xe_t=xe_t: h.tensor_copy(out=xe_t[:, :, TB + 2:TB + 3], in_=xrh_t[:, :, 2:3]),
                 reads=[xrh_b, xe_b], writes=[xe_b])
        yield
        xc_t, xc_b = xc_ring.next()
        xcb_t, xcb_b = xcb_ring.next()
        for c in range(4):
            S.op("dve", lambda h, c=c, xc_t=xc_t, xe_t=xe_t: h.tensor_scalar(
                out=xc_t[:, c, :], in0=xe_t[:, c, 0:TB], scalar1=cw_t[:, c, 0:1], scalar2=cb_t[:, c:c + 1],
                op0=ALU.mult, op1=ALU.add), reads=[xe_b, cw_b], writes=[xc_b])
            for j in range(1, 4):
                S.op("dve", lambda h, c=c, j=j, xc_t=xc_t, xe_t=xe_t: h.scalar_tensor_tensor(
                    out=xc_t[:, c, :], in0=xe_t[:, c, j:j + TB], scalar=cw_t[:, c, j:j + 1], in1=xc_t[:, c, :],
                    op0=ALU.mult, op1=ALU.add), reads=[xe_b, cw_b, xc_b], writes=[xc_b])
        S.op("act", lambda h, xc_t=xc_t, xcb_t=xcb_t: h.activation(out=xcb_t[:, :, :], in_=xc_t[:, :, :], func=AF.Copy), reads=[xc_b], writes=[xcb_b])
        yield
        r_t, r_b = r_ring.next()
        i_t, i_b = i_ring.next()
        a_t, a_b = a_ring.next()
        b_t, b_b = b_ring.next()
        sr_t, sr_b = sr_ring.next()
        S.op("dve", lambda h, sr_t=sr_t: h.memset(sr_t[:, :], 0.0), writes=[sr_b])
        for d in range(2):
            for c in range(4):
                q = d * 4 + c
                pg_t, pg_b = pg_ring.next()
                S.op("pe", lambda h, d=d, c=c, pg_t=pg_t, xcb_t=xcb_t: h.matmul(pg_t[:, :], lhsT=wabd[:, d, c, :], rhs=xcb_t[:, c, :],
                                                                         start=True, stop=True), reads=[bd_b, xcb_b], writes=[pg_b])
                S.op("act", lambda h, d=d, c=c, q=q, pg_t=pg_t, r_t=r_t, sr_t=sr_t: h.activation(
                    out=r_t[:, q, :], in_=pg_t[:, :], func=AF.Sigmoid, bias=ba_t[:, d, c:c + 1], accum_out=sr_t[:, q:q + 1]),
                    reads=[pg_b, bb_b], writes=[r_b, sr_b])
                pg_t, pg_b = pg_ring.next()
                S.op("pe", lambda h, d=d, c=c, pg_t=pg_t, xcb_t=xcb_t: h.matmul(pg_t[:, :], lhsT=wxbd[:, d, c, :], rhs=xcb_t[:, c, :],
                                                                         start=True, stop=True), reads=[bd_b, xcb_b], writes=[pg_b])
                S.op("act", lambda h, d=d, c=c, q=q, pg_t=pg_t, i_t=i_t: h.activation(
                    out=i_t[:, q, :], in_=pg_t[:, :], func=AF.Sigmoid, bias=bx_t[:, d, c:c + 1]),
                    reads=[pg_b, bb_b], writes=[i_b])
        yield
        for d in range(2):
            for c in range(4):
                q = d * 4 + c
                S.op("act", lambda h, d=d, c=c, q=q, a_t=a_t, r_t=r_t: h.activation(
                    out=a_t[:, q, :], in_=r_t[:, q, :], func=AF.Exp, scale=cp_t[:, d, c:c + 1]), reads=[r_b, cp_b], writes=[a_b])
                S.op("act", lambda h, d=d, c=c, q=q, sr_t=sr_t, b=b: h.activation(
                    out=blk_t[:, b, d, 0, c:c + 1], in_=sr_t[:, q:q + 1], func=AF.Exp, scale=cp_t[:, d, c:c + 1]),
                    reads=[sr_b, cp_b, blk_b], writes=[blk_b])
        yield
        S.op("dve", lambda h, a_t=a_t, r_t=r_t: h.tensor_tensor(out=r_t[:, :, :], in0=a_t[:, :, :], in1=a_t[:, :, :], op=ALU.mult),
             reads=[a_b, r_b], writes=[r_b])
        S.op("act", lambda h, r_t=r_t: h.activation(out=r_t[:, :, :], in_=r_t[:, :, :], func=AF.Sqrt, scale=-1.0, bias=1.0),
             reads=[r_b], writes=[r_b])
        for d in range(2):
            S.op("dve", lambda h, d=d, i_t=i_t, xc_t=xc_t: h.tensor_tensor(out=i_t[:, d * 4:(d + 1) * 4, :], in0=i_t[:, d * 4:(d + 1) * 4, :],
                                                                        in1=xc_t[:, :, :], op=ALU.mult), reads=[i_b, xc_b], writes=[i_b])
        S.op("dve", lambda h, b_t=b_t, r_t=r_t, i_t=i_t: h.tensor_tensor(out=b_t[:, :, :], in0=r_t[:, :, :], in1=i_t[:, :, :], op=ALU.mult),
             reads=[r_b, i_b], writes=[b_b])
        yield
        for d in range(2):
            for c in range(4):
                q = d * 4 + c
                hl_t, hl_b = hl_ring.next()
                if d == 0:
                    S.op("dve", lambda h, q=q, hl_t=hl_t, a_t=a_t, b_t=b_t: h.tensor_tensor_scan(
                        out=hl_t[:, :], data0=a_t[:, q, :], data1=b_t[:, q, :], initial=0.0, op0=ALU.mult, op1=ALU.add),
                        reads=[a_b, b_b], writes=[hl_b])
                    col = TB - 1
                else:
                    S.op("dve", lambda h, q=q, hl_t=hl_t, a_t=a_t, b_t=b_t: h.tensor_tensor_scan(
                        out=hl_t[:, ::-1], data0=a_t[:, q, ::-1], data1=b_t[:, q, ::-1], initial=0.0, op0=ALU.mult, op1=ALU.add),
                        reads=[a_b, b_b], writes=[hl_b])
                    col = 0
                S.op("act", lambda h, d=d, c=c, hl_t=hl_t, col=col, b=b: h.activation(
                    out=blk_t[:, b, d, 1, c:c + 1], in_=hl_t[:, col:col + 1], func=AF.Copy), reads=[hl_b, blk_b], writes=[blk_b])
        for d in range(2):
            S.dma("act", ABv[d, 0, :, :, t0:t0 + TB], a_t[:, d * 4:(d + 1) * 4, :], reads=[a_b])
            S.dma("sp", ABv[d, 1, :, :, t0:t0 + TB], b_t[:, d * 4:(d + 1) * 4, :], reads=[b_b])
        yield

    run_interleaved((blk(b) for b in range(NB)), 2)
    cab_t = cx.sb("cab", [128, 2, 2, 4], F32); cab_b = Buf("cab")
    for d in range(2):
        S.op("dve", lambda h, d=d: h.memset(cab_t[:, d, 0, :], 1.0), writes=[cab_b])
        S.op("dve", lambda h, d=d: h.memset(cab_t[:, d, 1, :], 0.0), writes=[cab_b])
        order = range(NB) if d == 0 else range(NB - 1, -1, -1)
        for b in order:
            S.op("dve", lambda h, d=d, b=b: h.tensor_tensor(out=cab_t[:, d, 1, :], in0=cab_t[:, d, 1, :], in1=blk_t[:, b, d, 0, :],
                                                            op=ALU.mult), reads=[cab_b, blk_b], writes=[cab_b])
            S.op("dve", lambda h, d=d, b=b: h.tensor_tensor(out=cab_t[:, d, 1, :], in0=cab_t[:, d, 1, :], in1=blk_t[:, b, d, 1, :],
                                                            op=ALU.add), reads=[cab_b, blk_b], writes=[cab_b])
            S.op("dve", lambda h, d=d, b=b: h.tensor_tensor(out=cab_t[:, d, 0, :], in0=cab_t[:, d, 0, :], in1=blk_t[:, b, d, 0, :],
                                                            op=ALU.mult), reads=[cab_b, blk_b], writes=[cab_b])
    S.dma("sp", BLK[:, :, :, :, :], blk_t[:, :, :, :, :], reads=[blk_b])
    S.dma("sp", CAB[:, :, :, :], cab_t[:, :, :, :], reads=[cab_b])
    cx.pop()
    cx.pop()
    return cx.finish() if own else None


def chunk_vec(v, nch):
    return np.ascontiguousarray(np.asarray(v, np.float32).reshape(nch, 128).T)


def oa_inputs(xT, xhalo, P, pos):
    cos, sin = rope_tables(pos, 16, 128)
    return {
        "xT": np.ascontiguousarray(xT), "xhalo": np.ascontiguousarray(xhalo), "w_in": P["w_in"], "g": vec128(P["g"], 8),
        "g_cq": vec128(P["g_cq"], 2), "g_ckv": vec128(P["g_ckv"], 1), "w_uq": P["w_uq"], "w_ukv": P["w_ukv"],
        "cw": np.ascontiguousarray(P["conv_w"].reshape(4, 4, 128).transpose(2, 1, 0)),
        "cb": chunk_vec(P["conv_b"], 4), "wa": P["wa"], "wx": P["wx"],
        "ba": np.ascontiguousarray(P["ba"].reshape(2, 4, 128).transpose(2, 0, 1)),
        "bx": np.ascontiguousarray(P["bx"].reshape(2, 4, 128).transpose(2, 0, 1)),
        "lam": np.ascontiguousarray(P["lam"].reshape(2, 4, 128).transpose(2, 0, 1)),
        "cos": cos, "sin": sin, "rot": rot_matrix(32),
    }


def build_ob1(ntok=TOK, nrank=4, cx=None):
    seq = ntok * nrank
    QG = ntok // 512
    NKT = seq // 128
    NT = ntok // 128
    own = cx is None
    cx = Ctx() if own else cx
    cx.push()
    S = cx.S
    QN = cx.dram_in("QN", [512, ntok], BF16)
    QR = cx.dram_in("QR", [256, ntok], BF16)
    KNg = cx.dram_in("KNg", [8 * nrank * 64, ntok], BF16)
    KRg = cx.dram_in("KRg", [nrank * 32, ntok], BF16)
    Vg = cx.dram_in("Vg", [8 * nrank * 128, NT * 65], BF16)
    YC = cx.dram_out("YC", [512, ntok], BF16)

    q_ring = mk_ring(cx, "sb", "q", 2, [128, ntok], BF16)
    k_ring = mk_ring(cx, "sb", "k", 2, [128, seq], BF16)
    v_ring = mk_ring(cx, "sb", "v", 2, [128, NKT, 65], BF16)
    p_ring = mk_ring(cx, "sb", "p", 4, [128, 1024], BF16)
    osb_ring = mk_ring(cx, "sb", "osb", 2, [64, 512], F32)
    rc_ring = mk_ring(cx, "sb", "rc", 2, [128, 512], F32)
    yc_ring = mk_ring(cx, "sb", "yc", 2, [64, 512], BF16)
    ones32 = cx.sb("ones32", [128, 64], F32); ones32_b = Buf("ones32")
    s_ring = mk_ring(cx, "ps", "s", 3, [128, 1024], F32)
    o_ring = mk_ring(cx, "ps", "o", 2, [128, 512], F32)
    S.op("pool", lambda h: h.memset(ones32[:, :], 1.0), writes=[ones32_b])
    scale = float(96 ** -0.5)
    NKP = NKT // 2
    LA = 2

    def load_head(hd):
        q_t, q_b = q_ring.next()
        k_t, k_b = k_ring.next()
        v_t, v_b = v_ring.next()
        S.dma("sp", q_t[0:64, :], QN[hd * 64:(hd + 1) * 64, :], writes=[q_b])
        S.dma("sp", q_t[64:96, :], QR[hd * 32:(hd + 1) * 32, :], writes=[q_b])
        for r in range(nrank):
            kr0 = ((hd // 2) * nrank + r) * 128 + (hd % 2) * 64
            S.dma("sp", k_t[0:64, r * ntok:(r + 1) * ntok], KNg[kr0:kr0 + 64, :], writes=[k_b])
            S.dma("sp", k_t[64:96, r * ntok:(r + 1) * ntok], KRg[r * 32:(r + 1) * 32, :], writes=[k_b])
            S.dma("sp", v_t[:, r * NT:(r + 1) * NT, :],
                  Vg[(hd * nrank + r) * 128:(hd * nrank + r + 1) * 128, :].rearrange("p (i e) -> p i e", e=65), writes=[v_b])
        return (q_t, q_b, k_t, k_b, v_t, v_b)

    nxt = load_head(0)
    cx.run_hook()
    for hd in range(8):
        q_t, q_b, k_t, k_b, v_t, v_b = nxt
        if hd + 1 < 8:
            nxt = load_head(hd + 1)
        for qg in range(QG):
            o_t, o_b = o_ring.next()
            stiles = {}

            def emit_s(kp):
                s_t, s_b = s_ring.next()
                for hf in range(2):
                    kt = 2 * kp + hf
                    S.op("pe", lambda h, kt=kt, hf=hf: h.matmul(s_t[:, hf * 512:(hf + 1) * 512], lhsT=k_t[0:96, kt * 128:(kt + 1) * 128],
                                                                rhs=q_t[0:96, qg * 512:(qg + 1) * 512], start=True, stop=True),
                         reads=[k_b, q_b], writes=[s_b])
                stiles[kp] = (s_t, s_b)

            for kp in range(min(LA, NKP)):
                emit_s(kp)
            for kp in range(NKP):
                s_t, s_b = stiles.pop(kp)
                p_t, p_b = p_ring.next()
                S.op("act", lambda h, s_t=s_t, p_t=p_t: h.activation(out=p_t[:, :], in_=s_t[:, :], func=AF.Exp, scale=scale),
                     reads=[s_b], writes=[p_b])
                if kp + LA < NKP:
                    emit_s(kp + LA)
                for hf in range(2):
                    kt = 2 * kp + hf
                    S.op("pe", lambda h, kt=kt, hf=hf, p_t=p_t: h.matmul(o_t[0:65, :], lhsT=v_t[:, kt, 0:65], rhs=p_t[:, hf * 512:(hf + 1) * 512],
                                                                         start=(kt == 0), stop=(kt == NKT - 1)),
                         reads=[v_b, p_b], writes=[o_b])
            osb_t, osb_b = osb_ring.next()
            rc_t, rc_b = rc_ring.next()
            S.op("act", lambda h: h.activation(out=osb_t[:, :], in_=o_t[0:64, :], func=AF.Copy), reads=[o_b], writes=[osb_b])
            S.op("dve", lambda h: h.reciprocal(out=rc_t[64:65, :], in_=o_t[64:65, :]), reads=[o_b], writes=[rc_b])
            bc_t, bc_b = s_ring.next()
            S.op("pe", lambda h: h.matmul(bc_t[0:64, 0:512], lhsT=ones32[64:65, 0:64], rhs=rc_t[64:65, :], start=True, stop=True),
                 reads=[rc_b, ones32_b], writes=[bc_b])
            yc_t, yc_b = yc_ring.next()
            S.op("dve", lambda h: h.tensor_tensor(out=yc_t[:, :], in0=osb_t[:, :], in1=bc_t[0:64, 0:512], op=ALU.mult),
                 reads=[osb_b, bc_b], writes=[yc_b])
            S.dma("pool", YC[hd * 64:(hd + 1) * 64, qg * 512:(qg + 1) * 512], yc_t[:, :], reads=[yc_b])
    cx.pop()
    return cx.finish() if own else None


def build_ob2(ntok=TOK, ngrp=4, cx=None):
    TB = 512
    NB = ntok // TB
    own = cx is None
    cx = Ctx() if own else cx
    cx.push()
    S = cx.S
    AB = cx.dram_in("AB", [2, 2, 512, ntok])
    GX = cx.dram_in("GX", [512, ntok], BF16)
    YC = cx.dram_in("YC", [512, ntok], BF16)
    xT = cx.dram_in("xT", [D, ntok])
    w_out = cx.dram_in("w_out", [D, D])
    BLK = cx.dram_in("BLK", [128, NB, 2, 2, 4])
    CABg = cx.dram_in("CABg", [128, ngrp, 16])
    mfd = cx.dram_in("mf", [128, ngrp])
    mbd = cx.dram_in("mb", [128, ngrp])
    oT = cx.dram_out("oT", [D, ntok])

    woA = cx.sb("woA", [128, 4, D], BF16); woA_b = Buf("woA")
    woB = cx.sb("woB", [128, 4, D], BF16); woB_b = Buf("woB")
    blk_t = cx.sb("blk", [128, NB, 2, 2, 4], F32); blk_b = Buf("blk")
    cab_t = cx.sb("cab", [128, ngrp, 16], F32); cab_b = Buf("cab")
    m_t = cx.sb("m", [128, 2, ngrp], F32); m_b = Buf("m")
    hin_t = cx.sb("hin", [128, 2, 4], F32); hin_b = Buf("hin")
    tmp_t = cx.sb("tmp", [128, 4], F32); tmp_b = Buf("tmp")
    init_t = cx.sb("init", [128, NB, 2, 4], F32); init_b = Buf("init")
    ab_ring = mk_ring(cx, "sb", "ab", 2, [128, 2, 2, 4, TB], F32)
    hs_ring = mk_ring(cx, "sb", "hs", 2, [128, 2, 4, TB], F32)
    gx_ring = mk_ring(cx, "sb", "gx", 2, [128, 4, TB], BF16)
    yc_ring = mk_ring(cx, "sb", "yc", 2, [128, 4, TB], BF16)
    yd_ring = mk_ring(cx, "sb", "yd", 2, [128, 4, TB], BF16)
    x_ring = mk_ring(cx, "sb", "x", 2, [128, KC, TB], F32)
    y_ring = mk_ring(cx, "ps", "y", 3, [128, 512], F32)

    S.dma("pool", woA[:, :, :], w_out[0:512, :].rearrange("(i p) n -> p i n", p=128), writes=[woA_b])
    S.dma("pool", woB[:, :, :], w_out[512:1024, :].rearrange("(g p) n -> p g n", p=128), writes=[woB_b])
    S.dma("sp", blk_t[:, :, :, :, :], BLK[:, :, :, :, :], writes=[blk_b])
    S.dma("sp", cab_t[:, :, :], CABg[:, :, :], writes=[cab_b])
    S.dma("sp", m_t[:, 0, :], mfd[:, :], writes=[m_b])
    S.dma("sp", m_t[:, 1, :], mbd[:, :], writes=[m_b])
    S.op("pool", lambda h: h.memset(hin_t[:, :, :], 0.0), writes=[hin_b])
    for d in range(2):
        order = range(ngrp) if d == 0 else range(ngrp - 1, -1, -1)
        for i in order:
            S.op("dve", lambda h, d=d, i=i: h.tensor_tensor(out=tmp_t[:, :], in0=hin_t[:, d, :], in1=cab_t[:, i, d * 8:d * 8 + 4], op=ALU.mult),
                 reads=[hin_b, cab_b, tmp_b], writes=[tmp_b])
            S.op("dve", lambda h, d=d, i=i: h.tensor_tensor(out=tmp_t[:, :], in0=tmp_t[:, :], in1=cab_t[:, i, d * 8 + 4:d * 8 + 8], op=ALU.add),
                 reads=[tmp_b, cab_b], writes=[tmp_b])
            S.op("dve", lambda h, d=d, i=i: h.tensor_tensor(out=tmp_t[:, :], in0=tmp_t[:, :], in1=hin_t[:, d, :], op=ALU.subtract),
                 reads=[tmp_b, hin_b], writes=[tmp_b])
            S.op("dve", lambda h, d=d, i=i: h.scalar_tensor_tensor(out=hin_t[:, d, :], in0=tmp_t[:, :], scalar=m_t[:, d, i:i + 1],
                                                                   in1=hin_t[:, d, :], op0=ALU.mult, op1=ALU.add),
                 reads=[tmp_b, m_b, hin_b], writes=[hin_b])
    for d in range(2):
        order = list(range(NB)) if d == 0 else list(range(NB - 1, -1, -1))
        S.op("dve", lambda h, d=d, b0=order[0]: h.tensor_copy(out=init_t[:, b0, d, :], in_=hin_t[:, d, :]),
             reads=[hin_b, init_b], writes=[init_b])
        for bi in range(NB - 1):
            b, bn = order[bi], order[bi + 1]
            S.op("dve", lambda h, d=d, b=b, bn=bn: h.tensor_tensor(out=init_t[:, bn, d, :], in0=init_t[:, b, d, :],
                                                                   in1=blk_t[:, b, d, 0, :], op=ALU.mult),
                 reads=[init_b, blk_b], writes=[init_b])
            S.op("dve", lambda h, d=d, b=b, bn=bn: h.tensor_tensor(out=init_t[:, bn, d, :], in0=init_t[:, bn, d, :],
                                                                   in1=blk_t[:, b, d, 1, :], op=ALU.add),
                 reads=[init_b, blk_b], writes=[init_b])
    xv = xT.rearrange("(c p) t -> p c t", p=128)
    ov = oT.rearrange("(c p) t -> p c t", p=128)
    ABv = AB.rearrange("d s (c p) t -> d s p c t", p=128)
    def blk(b):
        t0 = b * TB
        ab_t, ab_b = ab_ring.next()
        for d in range(2):
            for s_ in range(2):
                S.dma("sp", ab_t[:, d, s_, :, :], ABv[d, s_, :, :, t0:t0 + TB], writes=[ab_b])
        gx_t, gx_b = gx_ring.next()
        yc_t, yc_b = yc_ring.next()
        x_t, x_b = x_ring.next()
        S.dma("sp", gx_t[:, :, :], GX.rearrange("(c p) t -> p c t", p=128)[:, :, t0:t0 + TB], writes=[gx_b])
        S.dma("sp", yc_t[:, :, :], YC.rearrange("(i p) t -> p i t", p=128)[:, :, t0:t0 + TB], writes=[yc_b])
        S.dma("sp", x_t[:, :, :], xv[:, :, t0:t0 + TB], writes=[x_b])
        yield
        hs_t, hs_b = hs_ring.next()
        for d in range(2):
            for c in range(4):
                if d == 0:
                    S.op("dve", lambda h, d=d, c=c, b=b: h.tensor_tensor_scan(
                        out=hs_t[:, d, c, :], data0=ab_t[:, d, 0, c, :], data1=ab_t[:, d, 1, c, :],
                        initial=init_t[:, b, d, c:c + 1], op0=ALU.mult, op1=ALU.add), reads=[ab_b, init_b, hs_b], writes=[hs_b])
                else:
                    S.op("dve", lambda h, d=d, c=c, b=b: h.tensor_tensor_scan(
                        out=hs_t[:, d, c, ::-1], data0=ab_t[:, d, 0, c, ::-1], data1=ab_t[:, d, 1, c, ::-1],
                        initial=init_t[:, b, d, c:c + 1], op0=ALU.mult, op1=ALU.add), reads=[ab_b, init_b, hs_b], writes=[hs_b])
        yield
        S.op("pool", lambda h: h.tensor_tensor(out=hs_t[:, 0, :, :], in0=hs_t[:, 0, :, :], in1=hs_t[:, 1, :, :], op=ALU.add),
             reads=[hs_b], writes=[hs_b])
        yd_t, yd_b = yd_ring.next()
        S.op("pool", lambda h: h.tensor_tensor(out=yd_t[:, :, :], in0=hs_t[:, 0, :, :], in1=gx_t[:, :, :], op=ALU.mult),
             reads=[hs_b, gx_b], writes=[yd_b])
        yield
        for o in range(KC):
            y_t, y_b = y_ring.next()
            for hh in range(4):
                S.op("pe", lambda h, hh=hh, o=o: h.matmul(y_t[:, :], lhsT=woA[:, hh, o * 128:(o + 1) * 128], rhs=yc_t[:, hh, :],
                                                         start=(hh == 0), stop=False), reads=[woA_b, yc_b], writes=[y_b])
            for g in range(4):
                S.op("pe", lambda h, g=g, o=o: h.matmul(y_t[:, :], lhsT=woB[:, g, o * 128:(o + 1) * 128], rhs=yd_t[:, g, :],
                                                       start=False, stop=(g == 3)), reads=[woB_b, yd_b], writes=[y_b])
            S.op("dve", lambda h, o=o: h.tensor_tensor(out=x_t[:, o, :], in0=y_t[:, :], in1=x_t[:, o, :], op=ALU.add),
                 reads=[y_b, x_b], writes=[x_b])
        S.dma("pool", ov[:, :, t0:t0 + TB], x_t[:, :, :], reads=[x_b])
        yield

    run_interleaved((blk(b) for b in range(NB)), 2, 2)
    cx.pop()
    return cx.finish() if own else None


def allgather(cx, in_ap, out_ap, groups):
    S = cx.S
    S.barrier()
    sem = S.new_sem("cc")
    cx.nc.gpsimd.collective_compute("AllGather", ALU.bypass, replica_groups=groups, ins=[in_ap], outs=[out_ap]).then_inc(sem, 1)
    for e in S.ENGS:
        S.h[e].wait_ge(sem, 1)


def allgather_many(cx, pairs, groups):
    S = cx.S
    S.barrier()
    sem = S.new_sem("ccm")
    for (in_ap, out_ap) in pairs:
        cx.nc.gpsimd.collective_compute("AllGather", ALU.bypass, replica_groups=groups, ins=[in_ap], outs=[out_ap]).then_inc(sem, 1)
    for e in S.ENGS:
        S.h[e].wait_ge(sem, len(pairs))


def emit_select(cx, src_t, src_b, nrank, m_t, m_b, side, acc_t, acc_b):
    S = cx.S
    S.op("dve", lambda h: h.tensor_scalar_mul(out=acc_t[:, :], in0=src_t[:, 0, :], scalar1=m_t[:, side, 0:1]),
         reads=[src_b, m_b], writes=[acc_b])
    for i in range(1, nrank):
        S.op("dve", lambda h, i=i: h.scalar_tensor_tensor(out=acc_t[:, :], in0=src_t[:, i, :], scalar=m_t[:, side, i:i + 1],
                                                          in1=acc_t[:, :], op0=ALU.mult, op1=ALU.add),
             reads=[src_b, m_b, acc_b], writes=[acc_b])


def even_exchange_start(cx, KTh, Vh, UTh, pack, packg, groups, ntok):
    S = cx.S
    S.barrier()
    S.dma_dd_async("sp", pack[:, 0:128], KTh[:, 128:256])
    S.dma_dd_async("sp", pack[:, 128:256], KTh[:, ntok:ntok + 128])
    S.dma_dd_async("sp", pack[:, 256:288].rearrange("p (g t) -> p g t", g=4), UTh[:, :, 8:16])
    S.dma_dd_async("sp", pack[:, 288:320].rearrange("p (g t) -> p g t", g=4), UTh[:, :, ntok:ntok + 8])
    S.dma_dd_async("sp", pack[:, 320:450], Vh[128:256, :])
    S.dma_dd_async("sp", pack[:, 450:580], Vh[ntok:ntok + 128, :])
    S.barrier()
    sem = S.new_sem("cce")
    cx.nc.gpsimd.collective_compute("AllGather", ALU.bypass, replica_groups=groups, ins=[pack[:, :]], outs=[packg[:, :]]).then_inc(sem, 1)
    return sem


def even_exchange_finish(cx, sem, KTh, Vh, UTh, packg, mlr, nrank, ntok):
    S = cx.S
    for e in S.ENGS:
        S.h[e].wait_ge(sem, 1)
    cx.push()
    pg_t = cx.sb("pg", [128, nrank, 580], BF16); pg_b = Buf("pg")
    m_t = cx.sb("mlr", [128, 2, nrank], F32); m_b = Buf("mlr")
    accL = cx.sb("accL", [128, 580], BF16); accL_b = Buf("accL")
    accR = cx.sb("accR", [128, 580], BF16); accR_b = Buf("accR")
    S.dma("sp", pg_t[:, :, :], packg.rearrange("(r p) n -> p r n", p=128), writes=[pg_b])
    S.dma("sp", m_t[:, :, :], mlr[:, :, :], writes=[m_b])
    emit_select(cx, pg_t, pg_b, nrank, m_t, m_b, 0, accL, accL_b)
    emit_select(cx, pg_t, pg_b, nrank, m_t, m_b, 1, accR, accR_b)
    S.dma("sp", KTh[:, 0:128], accL[:, 128:256], reads=[accL_b])
    S.dma("sp", UTh[:, :, 0:8], accL[:, 288:320].rearrange("p (g t) -> p g t", g=4), reads=[accL_b])
    S.dma("sp", Vh[0:128, :], accL[:, 450:580], reads=[accL_b])
    S.dma("sp", KTh[:, 128 + ntok:256 + ntok], accR[:, 0:128], reads=[accR_b])
    S.dma("sp", UTh[:, :, 8 + ntok:16 + ntok], accR[:, 256:288].rearrange("p (g t) -> p g t", g=4), reads=[accR_b])
    S.dma("sp", Vh[128 + ntok:256 + ntok, :], accR[:, 320:450], reads=[accR_b])
    cx.pop()


def xhalo_exchange_start(cx, xprev, xhp, xhpg, groups, ntok):
    S = cx.S
    S.barrier()
    S.dma_dd_async("sp", xhp[:, 0:2], xprev[:, 0:2])
    S.dma_dd_async("sp", xhp[:, 2:4], xprev[:, ntok - 2:ntok])
    S.barrier()
    sem = S.new_sem("ccx")
    cx.nc.gpsimd.collective_compute("AllGather", ALU.bypass, replica_groups=groups, ins=[xhp[:, :]], outs=[xhpg[:, :]]).then_inc(sem, 1)
    return sem


def xhalo_exchange_finish(cx, sem, xhpg, xhalo, mlr, nrank):
    S = cx.S
    for e in S.ENGS:
        S.h[e].wait_ge(sem, 1)
    cx.push()
    xg_t = cx.sb("xg", [128, nrank, 32], F32); xg_b = Buf("xg")
    m_t = cx.sb("mlr", [128, 2, nrank], F32); m_b = Buf("mlr")
    accL = cx.sb("accL", [128, 32], F32); accL_b = Buf("accL")
    accR = cx.sb("accR", [128, 32], F32); accR_b = Buf("accR")
    for r in range(nrank):
        S.dma("sp", xg_t[:, r, :].rearrange("p (c t) -> p c t", t=4),
              xhpg[r * D:(r + 1) * D, :].rearrange("(c p) t -> p c t", p=128), writes=[xg_b])
    S.dma("sp", m_t[:, :, :], mlr[:, :, :], writes=[m_b])
    emit_select(cx, xg_t, xg_b, nrank, m_t, m_b, 0, accL, accL_b)
    emit_select(cx, xg_t, xg_b, nrank, m_t, m_b, 1, accR, accR_b)
    xhv = xhalo.rearrange("(c p) t -> p c t", p=128)
    S.dma("sp", xhv[:, :, 0:2], accL[:, :].rearrange("p (c t) -> p c t", t=4)[:, :, 2:4], reads=[accL_b])
    S.dma("sp", xhv[:, :, 2:4], accR[:, :].rearrange("p (c t) -> p c t", t=4)[:, :, 0:2], reads=[accR_b])
    cx.pop()


SMALL_SPECS = None


def build_fused(B=2, nrank=4, ntok=TOK, depth=4):
    NE, NO = (depth + 1) // 2, depth // 2
    NT = ntok // 128
    NB = ntok // 512
    groups = [[b * nrank + r for r in range(nrank)] for b in range(B)]
    cx = Ctx()
    nc = cx.nc
    I = cx.ext_in
    x0 = I("xT", [D, ntok])
    Wd = {
        "e_w_in": I("e_w_in", [NE, D, 1280]), "e_w_pool": I("e_w_pool", [NE, 4, 128, 128]), "e_w_out": I("e_w_out", [NE, D, D]),
        "o_w_in": I("o_w_in", [NO, D, 1440]), "o_w_uq": I("o_w_uq", [NO, 256, 768]), "o_w_ukv": I("o_w_ukv", [NO, 128, 1024]),
        "o_lru_wa": I("o_lru_wa", [NO, 2, 8, 64, 64]), "o_lru_wx": I("o_lru_wx", [NO, 2, 8, 64, 64]), "o_w_out": I("o_w_out", [NO, D, D]),
        "w_mlp1": I("w_mlp1", [depth, D, DFF]), "w_mlp2": I("w_mlp2", [depth, DFF, D]),
        "g_mix": I("g_mix", [depth, 128, KC]), "g_mlp": I("g_mlp", [depth, 128, KC]), "g_fin": I("g_fin", [128, KC]),
        "pscale": I("pscale", [NE, 128, 4]), "sinkrow": I("sinkrow", [NE, 1, 2, 512]),
        "g_cq": I("g_cq", [NO, 128, 2]), "g_ckv": I("g_ckv", [NO, 128, 1]), "cw": I("cw", [NO, 128, 4, 4]), "cb": I("cb", [NO, 128, 4]),
        "ba": I("ba", [NO, 128, 2, 4]), "bx": I("bx", [NO, 128, 2, 4]), "lam": I("lam", [NO, 128, 2, 4]),
        "cos32": I("cos32", [128, ntok]), "sin32": I("sin32", [128, ntok]), "cos16": I("cos16", [128, ntok]), "sin16": I("sin16", [128, ntok]),
        "rot64": I("rot64", [128, 128]), "rot32": I("rot32", [128, 128]), "masks": I("masks", [4, 128, 512]),
        "ident": I("ident", [128, 128]),
        "invc": I("invc", [128, 2, 4, 16]), "mfb": I("mfb", [2, 128, nrank]), "mlr": I("mlr", [128, 2, nrank]),
    }
    outT = cx.ext_out("oT", [D, ntok])

    def tmp(name, shape, dt=F32):
        return nc.dram_tensor(name, list(shape), dt, kind="Internal").ap()

    def make_precast(layer, w1b_d, w2b_d):
        def f():
            for k in range(8):
                cx.S.dma_dd_async("pool", w1b_d[k * 128:(k + 1) * 128, :], Wd["w_mlp1"][layer][k * 128:(k + 1) * 128, :])
            for k in range(8):
                cx.S.dma_dd_async("pool", w2b_d[k * 512:(k + 1) * 512, :], Wd["w_mlp2"][layer][k * 512:(k + 1) * 512, :])
        return f

    xcur = x0
    xh_pending = None
    for layer in range(depth):
        L = f"L{layer}"
        xmix = tmp(L + "_xmix", [D, ntok])
        w1b_d = tmp(L + "_w1b", [D, DFF], BF16)
        w2b_d = tmp(L + "_w2b", [DFF, D], BF16)
        if layer % 2 == 0:
            e = layer // 2
            QsT = tmp(L + "_QsT", [128, 4, ntok], BF16)
            KTh = tmp(L + "_KTh", [128, ntok + 256], BF16)
            Vh = tmp(L + "_Vh", [ntok + 256, 130], BF16)
            UTh = tmp(L + "_UTh", [128, 4, ntok + 16], BF16)
            pack = tmp(L + "_pack", [128, 580], BF16)
            packg = tmp(L + "_packg", [nrank * 128, 580], BF16)
            cx.bind = {"xT": xcur, "w_in": Wd["e_w_in"][e], "g": Wd["g_mix"][layer], "cos": Wd["cos32"], "sin": Wd["sin32"],
                       "rot": Wd["rot64"], "QsT": QsT, "KT": KTh[:, 128:128 + ntok], "Vaug": Vh[128:128 + ntok, :],
                       "UT": UTh[:, :, 8:8 + ntok]}
            exs = {}

            def ea_mid(KTh=KTh, Vh=Vh, UTh=UTh, pack=pack, packg=packg, exs=exs):
                exs["sem"] = even_exchange_start(cx, KTh, Vh, UTh, pack, packg, groups, ntok)

            build_ea(ntok, cx=cx, order=[0, NB - 1] + list(range(1, NB - 1)) if NB > 1 else [0], mid_hook=ea_mid)
            even_exchange_finish(cx, exs["sem"], KTh, Vh, UTh, packg, Wd["mlr"], nrank, ntok)
            cx.bind = {"QsT": QsT, "KTh": KTh, "Vh": Vh, "UTh": UTh, "xT": xcur, "w_pool": Wd["e_w_pool"][e],
                       "pscale": Wd["pscale"][e], "w_out": Wd["e_w_out"][e], "sinkrow": Wd["sinkrow"][e], "masks": Wd["masks"],
                       "invc": Wd["invc"], "ident": Wd["ident"], "oT": xmix}
            cx.hook = make_precast(layer, w1b_d, w2b_d)
            build_eb(ntok, cx=cx)
        else:
            o = layer // 2
            if xh_pending is None:
                xhp = tmp(L + "_xhp", [D, 4]); xhpg = tmp(L + "_xhpg", [nrank * D, 4])
            else:
                xhp, xhpg = xh_pending["xhp"], xh_pending["xhpg"]
            xhalo = tmp(L + "_xhalo", [D, 4])
            QN = tmp(L + "_QN", [512, ntok], BF16); QR = tmp(L + "_QR", [256, ntok], BF16)
            KNR = tmp(L + "_KNR", [544, ntok], BF16); V5 = tmp(L + "_V5", [1024, NT * 65], BF16)
            GX = tmp(L + "_GX", [512, ntok], BF16); AB = tmp(L + "_AB", [2, 2, 512, ntok])
            BLK = tmp(L + "_BLK", [128, NB, 2, 2, 4]); CAB = tmp(L + "_CAB", [128, 16]); XR = tmp(L + "_XR", [512, ntok])
            KNg = tmp(L + "_KNg", [8 * nrank * 64, ntok], BF16); KRg = tmp(L + "_KRg", [nrank * 32, ntok], BF16)
            Vg = tmp(L + "_Vg", [8 * nrank * 128, NT * 65], BF16)
            CABg = tmp(L + "_CABg", [nrank * 128, 16]); YC = tmp(L + "_YC", [512, ntok], BF16)
            if xh_pending is None:
                xh_sem = xhalo_exchange_start(cx, xcur, xhp, xhpg, groups, ntok)
            else:
                xh_sem = xh_pending["sem"]
            xhalo_exchange_finish(cx, xh_sem, xhpg, xhalo, Wd["mlr"], nrank)
            cx.bind = {"xT": xcur, "xhalo": xhalo, "w_in": Wd["o_w_in"][o], "g": Wd["g_mix"][layer], "g_cq": Wd["g_cq"][o],
                       "g_ckv": Wd["g_ckv"][o], "w_uq": Wd["o_w_uq"][o], "w_ukv": Wd["o_w_ukv"][o], "cw": Wd["cw"][o], "cb": Wd["cb"][o],
                       "wa": Wd["o_lru_wa"][o], "wx": Wd["o_lru_wx"][o], "ba": Wd["ba"][o], "bx": Wd["bx"][o], "lam": Wd["lam"][o],
                       "cos": Wd["cos16"], "sin": Wd["sin16"], "rot": Wd["rot32"], "QN": QN, "QR": QR, "KNR": KNR, "V5": V5, "GX": GX,
                       "AB": AB, "BLK": BLK, "CAB": CAB.rearrange("p (d s c) -> p d s c", d=2, s=2), "XR": XR}
            ccsem = cx.S.new_sem("ccg")
            ncc = [0]

            def gather_kv():
                pairs = []
                for i in range(4):
                    pairs.append((KNR[i * 128:(i + 1) * 128, :], KNg[i * nrank * 128:(i + 1) * nrank * 128, :]))
                pairs.append((KNR[512:544, :], KRg[:, :]))
                for hd in range(8):
                    pairs.append((V5[hd * 128:(hd + 1) * 128, :], Vg[hd * nrank * 128:(hd + 1) * nrank * 128, :]))
                for (i_ap, o_ap) in pairs:
                    nc.gpsimd.collective_compute("AllGather", ALU.bypass, replica_groups=groups, ins=[i_ap], outs=[o_ap]).then_inc(ccsem, 1)
                    ncc[0] += 1

            build_oa(ntok, cx=cx, mid_hook=gather_kv)
            cabsem = cx.S.new_sem("cab")
            nc.gpsimd.collective_compute("AllGather", ALU.bypass, replica_groups=groups, ins=[CAB[:, :]], outs=[CABg[:, :]]).then_inc(cabsem, 1)
            for e_ in cx.S.ENGS:
                cx.S.h[e_].wait_ge(ccsem, ncc[0])
            cx.bind = {"QN": QN, "QR": QR, "KNg": KNg, "KRg": KRg, "Vg": Vg, "YC": YC}
            cx.hook = make_precast(layer, w1b_d, w2b_d)
            build_ob1(ntok, nrank, cx=cx)
            for e_ in cx.S.ENGS:
                cx.S.h[e_].wait_ge(cabsem, 1)
            cx.bind = {"AB": AB, "GX": GX, "YC": YC, "xT": xcur, "w_out": Wd["o_w_out"][o], "BLK": BLK,
                       "CABg": CABg.rearrange("(r p) n -> p r n", p=128), "mf": Wd["mfb"][0], "mb": Wd["mfb"][1], "oT": xmix}
            build_ob2(ntok, nrank, cx=cx)
        last = layer == depth - 1
        xnext = outT if last else tmp(L + "_xmlp", [D, ntok])
        cx.bind = {"xT": xmix, "w1": w1b_d, "w2": w2b_d, "g": Wd["g_mlp"][layer], "gf": Wd["g_fin"], "oT": xnext}
        NBm = ntok // 256
        if (layer + 1) < depth and (layer + 1) % 2 == 1 and NBm > 2:
            Ln = f"L{layer + 1}"
            xh_pending = {"xhp": tmp(Ln + "_xhp", [D, 4]), "xhpg": tmp(Ln + "_xhpg", [nrank * D, 4])}

            def mlp_mid(xh_pending=xh_pending, xnext=xnext):
                xh_pending["sem"] = xhalo_exchange_start(cx, xnext, xh_pending["xhp"], xh_pending["xhpg"], groups, ntok)

            build_mlp(last, ntok, cx=cx, wbf16=True, order=[0, NBm - 1] + list(range(1, NBm - 1)), mid_hook=mlp_mid)
        else:
            xh_pending = None
            build_mlp(last, ntok, cx=cx, wbf16=True)
        xcur = xnext
    cx.bind = {}
    return cx.finish()


_FUSED = {}


def run_model(x, W, nrank=4, ntok=TOK):
    B, Sq, _ = x.shape
    ncore = B * nrank
    assert Sq == nrank * ntok
    depth = W["norm_mlp"].shape[0]
    NE, NO = (depth + 1) // 2, depth // 2
    key = (B, nrank, ntok, depth)
    if key not in _FUSED:
        _FUSED[key] = build_fused(B, nrank, ntok, depth)
    nc = _FUSED[key]
    f32 = lambda a: np.ascontiguousarray(np.asarray(a, np.float32))
    g_mix = np.stack([vec128(W["e_norm_mix"][l // 2] if l % 2 == 0 else W["o_norm_mix"][l // 2], 8) for l in range(depth)])
    shared = {
        "e_w_in": f32(W["e_w_in"]), "e_w_pool": f32(W["e_w_pool"]), "e_w_out": f32(W["e_w_out"]),
        "o_w_in": f32(W["o_w_in"]), "o_w_uq": f32(W["o_w_uq"]), "o_w_ukv": f32(W["o_w_ukv"]),
        "o_lru_wa": f32(W["o_lru_wa"]), "o_lru_wx": f32(W["o_lru_wx"]), "o_w_out": f32(W["o_w_out"]),
        "w_mlp1": f32(W["w_mlp1"]), "w_mlp2": f32(W["w_mlp2"]),
        "g_mix": g_mix, "g_mlp": np.stack([vec128(W["norm_mlp"][l], 8) for l in range(depth)]), "g_fin": vec128(W["final_norm"], 8),
        "pscale": np.stack([vec128(W["e_pool_scale"][e], 4) for e in range(NE)]),
        "sinkrow": np.stack([np.repeat(f32(W["e_sink"][e]).reshape(2, 4), 128, axis=1).reshape(1, 2, 512) for e in range(NE)]),
        "g_cq": np.stack([vec128(W["o_g_cq"][o], 2) for o in range(NO)]),
        "g_ckv": np.stack([vec128(W["o_g_ckv"][o], 1) for o in range(NO)]),
        "cw": np.stack([f32(f32(W["o_conv_w"][o]).reshape(4, 4, 128).transpose(2, 1, 0)) for o in range(NO)]),
        "cb": np.stack([chunk_vec(W["o_conv_b"][o], 4) for o in range(NO)]),
        "ba": np.stack([f32(f32(W["o_lru_ba"][o]).reshape(2, 4, 128).transpose(2, 0, 1)) for o in range(NO)]),
        "bx": np.stack([f32(f32(W["o_lru_bx"][o]).reshape(2, 4, 128).transpose(2, 0, 1)) for o in range(NO)]),
        "lam": np.stack([f32(f32(W["o_lru_lambda"][o]).reshape(2, 4, 128).transpose(2, 0, 1)) for o in range(NO)]),
        "rot64": rot_matrix(64), "rot32": rot_matrix(32), "ident": np.eye(128, dtype=np.float32),
    }
    in_maps = []
    for c in range(ncore):
        bi, r = c // nrank, c % nrank
        pos = r * ntok + np.arange(ntok)
        cos32, sin32 = rope_tables(pos, 32, 128)
        cos16, sin16 = rope_tables(pos, 16, 128)
        mfb = np.zeros((2, 128, nrank), np.float32); mfb[0, :, :r] = 1.0; mfb[1, :, r + 1:] = 1.0
        mlr = np.zeros((128, 2, nrank), np.float32)
        if r > 0:
            mlr[:, 0, r - 1] = 1.0
        if r < nrank - 1:
            mlr[:, 1, r + 1] = 1.0
        im = dict(shared)
        im.update({"xT": np.ascontiguousarray(x[bi, r * ntok:(r + 1) * ntok, :].T), "cos32": cos32, "sin32": sin32, "cos16": cos16,
                   "sin16": sin16, "masks": eb_masks(r > 0, r < nrank - 1), "invc": eb_invc(r == 0, r == nrank - 1),
                   "mfb": mfb, "mlr": mlr})
        in_maps.append(im)
    res = run_spmd(nc, in_maps)
    out = np.empty((B, Sq, D), np.float32)
    for c in range(ncore):
        bi, r = c // nrank, c % nrank
        out[bi, r * ntok:(r + 1) * ntok, :] = res[c]["oT"].T
    return out


def kernel(**inputs):
    W = {k: np.asarray(v) for k, v in inputs.items()}
    x = np.asarray(W.pop("x"), np.float32)
    return run_model(x, W)
```

```python
from contextlib import ExitStack
import numpy as np
import concourse.bass as bass
import concourse.mybir as mybir
from concourse.bass_utils import run_bass_kernel_spmd

F32 = mybir.dt.float32
BF16 = mybir.dt.bfloat16
ALU = mybir.AluOpType
AF = mybir.ActivationFunctionType

NCORES = 8
D = 1024
KC = 8
TOK = 4096
SEQ = 16384
EPS = 1e-6
DFF = 4096
EPOCH = 30000


class Buf:
    __slots__ = ("name", "writers", "readers", "sem_in", "sem_out", "n_in", "n_out", "excl")

    def __init__(self, name, excl=False):
        self.name = name
        self.excl = excl
        self.writers = {}
        self.readers = {}
        self.sem_in = None
        self.sem_out = None
        self.n_in = 0
        self.n_out = 0


class Sched:
    ENGS = ("pe", "act", "dve", "pool", "sp")

    def __init__(self, nc, stack):
        self.nc = nc
        self.stack = stack
        self.h = {"pe": nc.tensor, "act": nc.scalar, "dve": nc.vector, "pool": nc.gpsimd, "sp": nc.sync}
        self.ops = {e: [] for e in self.ENGS}
        self.cnt = {e: 0 for e in self.ENGS}
        self.sem = {e: None for e in self.ENGS}
        self.seen = {e: {} for e in self.ENGS}
        self.last = {e: None for e in self.ENGS}
        self.dma_toks = {}
        self.nsem = 0
        self.ninstr = 0
        self.sem_pool = []
        self.live = []
        self.ddbuf = Buf("dram2dram")

    def new_sem(self, name):
        self.nsem += 1
        return self.stack.enter_context(self.nc.semaphore(f"{name}_{self.nsem}"))

    def _eng_tok(self, e):
        if self.sem[e] is None or self.cnt[e] >= EPOCH:
            self.sem[e] = self.new_sem("e" + e)
            self.cnt[e] = 0
        self.cnt[e] += 1
        tok = (self.sem[e], self.cnt[e])
        self.last[e] = tok
        return tok

    def _waits(self, e, toks):
        need = {}
        seen = self.seen[e]
        for sem, val in toks:
            k = id(sem)
            if seen.get(k, 0) >= val:
                continue
            if k not in need or need[k][1] < val:
                need[k] = (sem, val)
        out = []
        for k, (sem, val) in need.items():
            seen[k] = val
            out.append((sem, val))
        return out

    def _deps(self, e, reads, writes):
        toks = []
        for b in reads:
            toks.extend(b.writers.values())
            if b.excl:
                toks.extend(b.readers.values())
        for b in writes:
            toks.extend(b.writers.values())
            toks.extend(b.readers.values())
        if e == "pe":
            own = id(self.sem["pe"]) if self.sem["pe"] is not None else None
            toks = [t for t in toks if id(t[0]) != own]
        return self._waits(e, toks)

    def op(self, e, fn, reads=(), writes=()):
        waits = self._deps(e, reads, writes)
        tok = self._eng_tok(e)
        for b in reads:
            b.readers[id(tok[0])] = tok
        for b in writes:
            b.readers = {}
            b.writers = {id(tok[0]): tok}
        self.ninstr += 1

        h = self.h[e]
        for sem, val in waits:
            h.wait_ge(sem, val)
        fn(h).then_inc(tok[0], 1)

    def dma(self, q, out_ap, in_ap, reads=(), writes=(), **kw):
        waits = self._deps(q, reads, writes)
        assert len(writes) + len(reads) >= 1 and len(writes) <= 1 and len(reads) <= 1
        if writes:
            b = writes[0]
            if b.sem_in is None:
                b.sem_in, b.n_in = self._take_sem("di")
                self.live.append((b, "in"))
            b.n_in += 16
            tok = (b.sem_in, b.n_in)
            b.readers = {}
            b.writers = {id(tok[0]): tok}
            for rb in reads:
                rb.readers[id(tok[0])] = tok
        else:
            b = reads[0]
            if b.sem_out is None:
                b.sem_out, b.n_out = self._take_sem("do")
                self.live.append((b, "out"))
            b.n_out += 16
            tok = (b.sem_out, b.n_out)
            b.readers[id(tok[0])] = tok
        self.dma_toks[id(tok[0])] = tok
        self.ninstr += 1
        h = self.h[q]
        for sem, val in waits:
            h.wait_ge(sem, val)
        h.dma_start(out=out_ap, in_=in_ap, **kw).then_inc(tok[0], 16)

    def _take_sem(self, name):
        if self.sem_pool:
            return self.sem_pool.pop()
        return self.new_sem(name), 0

    def release_dma_sems(self):
        for b, kind in self.live:
            if kind == "in":
                self.sem_pool.append((b.sem_in, b.n_in)); b.sem_in = None
                b.writers = {}
            else:
                self.sem_pool.append((b.sem_out, b.n_out)); b.sem_out = None
                b.readers = {}
        self.live = []

    def dma_dd(self, q, out_ap, in_ap, **kw):
        self.dma(q, out_ap, in_ap, writes=[self.ddbuf], **kw)

    def dma_dd_async(self, q, out_ap, in_ap, **kw):
        self.dma(q, out_ap, in_ap, writes=[Buf("dd_async")], **kw)

    def barrier(self):
        toks = [t for t in self.last.values() if t is not None] + list(self.dma_toks.values())
        for e in self.ENGS:
            waits = self._waits(e, toks)
            for sem, val in waits:
                self.h[e].wait_ge(sem, val)

    def finalize(self):
        self.barrier()


class Ctx:
    def __init__(self):
        self.nc = bass.Bass("TRN2", target_bir_lowering=False)
        self.stack = ExitStack()
        self.S = Sched(self.nc, self.stack)
        self.n = 0
        self.cur = self.stack
        self.scopes = []
        self.bind = {}
        self.hook = None

    def dram_in(self, name, shape, dt=F32):
        if name in self.bind:
            return self.bind[name]
        return self.nc.dram_tensor(name, list(shape), dt, kind="ExternalInput").ap()

    def dram_out(self, name, shape, dt=F32):
        if name in self.bind:
            return self.bind[name]
        return self.nc.dram_tensor(name, list(shape), dt, kind="ExternalOutput").ap()

    def ext_in(self, name, shape, dt=F32):
        return self.nc.dram_tensor(name, list(shape), dt, kind="ExternalInput").ap()

    def ext_out(self, name, shape, dt=F32):
        return self.nc.dram_tensor(name, list(shape), dt, kind="ExternalOutput").ap()

    def sb(self, name, shape, dt):
        self.n += 1
        return self.cur.enter_context(self.nc.sbuf_tensor(f"{name}_{self.n}", list(shape), dt))

    def ps(self, name, shape, dt=F32):
        self.n += 1
        return self.cur.enter_context(self.nc.psum_tensor(f"{name}_{self.n}", list(shape), dt))

    def dram_tmp(self, name, shape, dt=F32):
        return self.nc.dram_tensor(name, list(shape), dt, kind="Internal").ap()

    def run_hook(self):
        if self.hook is not None:
            fs, self.hook = self.hook, None
            for f in (fs if isinstance(fs, (list, tuple)) else [fs]):
                f()

    def push(self):
        st = ExitStack()
        self.scopes.append(st)
        self.cur = st

    def pop(self):
        self.S.barrier()
        if len(self.scopes) == 1:
            self.S.release_dma_sems()
        self.scopes.pop().close()
        self.cur = self.scopes[-1] if self.scopes else self.stack

    def finish(self):
        self.S.finalize()
        self.stack.close()
        return self.nc


class Ring:
    def __init__(self, items):
        self.items = items
        self.i = 0

    def next(self):
        it = self.items[self.i % len(self.items)]
        self.i += 1
        return it


def run_interleaved(gens, width=2, stagger=2):
    it = iter(gens)
    active = []
    steps = 0
    while True:
        while len(active) < width and (not active or steps >= stagger):
            try:
                active.append(next(it))
            except StopIteration:
                break
        if not active:
            break
        steps += 1
        for g in list(active):
            try:
                next(g)
            except StopIteration:
                active.remove(g)


def mk_ring(cx, kind, name, n, shape, dt):
    items = []
    for i in range(n):
        t = cx.sb(f"{name}{i}", shape, dt) if kind == "sb" else cx.ps(f"{name}{i}", shape, dt)
        items.append((t, Buf(f"{name}{i}", excl=(kind == "ps"))))
    return Ring(items)


def emit_rmsnorm(cx, x_t, x_b, nchunk, TB, g_t, g_b, ones_t, ones_b, sq_ring, st_ring, rstd_ring,
                 out_t, out_b, nfeat, evac_engs=("dve",)):
    S = cx.S
    st_t, st_b = st_ring.next()
    for c in range(nchunk):
        sq_t, sq_b = sq_ring.next()
        S.op("act", lambda h, c=c, sq_t=sq_t: h.activation(out=sq_t[:, 0:TB], in_=x_t[:, c, 0:TB], func=AF.Square),
             reads=[x_b], writes=[sq_b])
        S.op("pe", lambda h, c=c, sq_t=sq_t: h.matmul(st_t[:, 0:TB], lhsT=ones_t[:, :], rhs=sq_t[:, 0:TB],
                                                        start=(c == 0), stop=(c == nchunk - 1)),
             reads=[sq_b, ones_b], writes=[st_b])
    r_t, r_b = rstd_ring.next()
    S.op("act", lambda h: h.activation(out=r_t[:, 0:TB], in_=st_t[:, 0:TB], func=AF.Sqrt, bias=float(nfeat * EPS)),
         reads=[st_b], writes=[r_b])
    S.op("dve", lambda h: h.reciprocal(out=r_t[:, 0:TB], in_=r_t[:, 0:TB]), reads=[r_b], writes=[r_b])
    for c in range(nchunk):
        e = evac_engs[c % len(evac_engs)]
        S.op(e, lambda h, c=c: h.scalar_tensor_tensor(out=out_t[:, c, 0:TB], in0=x_t[:, c, 0:TB],
                                                       scalar=g_t[:, c:c + 1], in1=r_t[:, 0:TB],
                                                       op0=ALU.mult, op1=ALU.mult),
             reads=[x_b, r_b, g_b], writes=[out_b])


def build_mlp(final_norm, ntok=TOK, dbg=False, cx=None, wbf16=False, order=None, mid_hook=None):
    TB = 256
    NB = ntok // TB
    FC = DFF // 128
    own = cx is None
    cx = Ctx() if own else cx
    cx.push()
    S = cx.S
    xT = cx.dram_in("xT", [D, ntok])
    w1 = cx.dram_in("w1", [D, DFF], BF16 if wbf16 else F32)
    w2 = cx.dram_in("w2", [DFF, D], BF16 if wbf16 else F32)
    gin = cx.dram_in("g", [128, KC])
    oT = cx.dram_out("oT", [D, ntok])
    if final_norm:
        gfin = cx.dram_in("gf", [128, KC])
    if dbg:
        dh = cx.dram_out("dh", [128, KC, TB], BF16)
        da = cx.dram_out("da", [128, DFF // 128, TB], BF16)

    w1b = cx.sb("w1b", [128, KC, DFF], BF16)
    w2b = cx.sb("w2b", [128, FC, D], BF16)
    w1_bufs = [Buf(f"w1_{k}") for k in range(KC)]
    w2_bufs = [Buf(f"w2_{k}") for k in range(8)]
    g_t = cx.sb("g", [128, KC], F32); g_b = Buf("g")
    ones_t = cx.sb("ones", [128, 128], BF16); ones_b = Buf("ones")
    x_ring = mk_ring(cx, "sb", "x", 2, [128, KC, TB], F32)
    h_ring = mk_ring(cx, "sb", "h", 2, [128, KC, TB], BF16)
    a_ring = mk_ring(cx, "sb", "a", 1, [128, FC, TB], BF16)
    r_ring = mk_ring(cx, "sb", "r", 3, [128, TB], BF16)
    sq_ring = mk_ring(cx, "sb", "sq", 3, [128, TB], BF16)
    rstd_ring = mk_ring(cx, "sb", "rstd", 2, [128, TB], F32)
    o_ring = mk_ring(cx, "sb", "o", 2, [128, KC, TB], F32)
    st_ring = mk_ring(cx, "ps", "st", 1, [128, 512], F32)
    p1_ring = mk_ring(cx, "ps", "p1", 3, [128, 512], F32)
    p2_ring = mk_ring(cx, "ps", "p2", 3, [128, 512], F32)
    if final_norm:
        gf_t = cx.sb("gf", [128, KC], F32); gf_b = Buf("gf")
        f_ring = mk_ring(cx, "sb", "f", 2, [128, KC, TB], F32)

    S.dma("sp", g_t[:, :], gin[:, :], writes=[g_b])
    S.op("dve", lambda h: h.tensor_scalar_mul(out=g_t[:, :], in0=g_t[:, :], scalar1=float(np.sqrt(D))),
         reads=[g_b], writes=[g_b])
    if final_norm:
        S.dma("sp", gf_t[:, :], gfin[:, :], writes=[gf_b])
        S.op("dve", lambda h: h.tensor_scalar_mul(out=gf_t[:, :], in0=gf_t[:, :], scalar1=float(np.sqrt(D))),
             reads=[gf_b], writes=[gf_b])
    S.op("pool", lambda h: h.memset(ones_t[:, :], 1.0), writes=[ones_b])
    w1v = w1.rearrange("(k p) n -> p k n", p=128)
    w2v = w2.rearrange("(f p) n -> p f n", p=128)
    wq = "sp" if wbf16 else "pool"
    for k in range(KC):
        S.dma(wq, w1b[:, k, :], w1v[:, k, :], writes=[w1_bufs[k]])
    for j in range(8):
        S.dma(wq, w2b[:, j * 4:(j + 1) * 4, :], w2v[:, j * 4:(j + 1) * 4, :], writes=[w2_bufs[j]])
    xv = xT.rearrange("(c p) t -> p c t", p=128)
    ov = oT.rearrange("(c p) t -> p c t", p=128)
    cx.run_hook()

    def prep(b):
        x_t, x_b = x_ring.next()
        S.dma("sp", x_t[:, :, :], xv[:, :, b * TB:(b + 1) * TB], writes=[x_b])
        h_t, h_b = h_ring.next()
        return (x_t, x_b, h_t, h_b)

    def norm(st_):
        x_t, x_b, h_t, h_b = st_
        emit_rmsnorm(cx, x_t, x_b, KC, TB, g_t, g_b, ones_t, ones_b, sq_ring, st_ring, rstd_ring, h_t, h_b, D)

    order = list(range(NB)) if order is None else order
    cur = prep(order[0])
    norm(cur)
    for bi, b in enumerate(order):
        t0 = b * TB
        x_t, x_b, h_t, h_b = cur
        nxt = prep(order[bi + 1]) if bi + 1 < NB else None
        a_t, a_b = a_ring.next()
        for f in range(FC):
            if f == FC // 2 and nxt is not None:
                norm(nxt)
            p_t, p_b = p1_ring.next()
            for k in range(KC):
                S.op("pe", lambda h, f=f, k=k, p_t=p_t: h.matmul(p_t[:, 0:TB], lhsT=w1b[:, k, f * 128:(f + 1) * 128],
                                                                  rhs=h_t[:, k, 0:TB], start=(k == 0), stop=(k == KC - 1)),
                     reads=[h_b, w1_bufs[k]], writes=[p_b])
            r_t, r_b = r_ring.next()
            S.op("act", lambda h, p_t=p_t, r_t=r_t: h.activation(out=r_t[:, 0:TB], in_=p_t[:, 0:TB], func=AF.Relu),
                 reads=[p_b], writes=[r_b])
            S.op("pool", lambda h, f=f, r_t=r_t: h.tensor_tensor(out=a_t[:, f, 0:TB], in0=r_t[:, 0:TB], in1=r_t[:, 0:TB],
                                                                  op=ALU.mult),
                 reads=[r_b], writes=[a_b])
        if dbg and b == 0:
            S.dma("sp", dh[:, :, :], h_t[:, :, :], reads=[h_b])
            S.dma("sp", da[:, :, :], a_t[:, :, :], reads=[a_b])
        o_t, o_b = o_ring.next()
        for c in range(KC):
            p_t, p_b = p2_ring.next()
            for f in range(FC):
                S.op("pe", lambda h, f=f, c=c, p_t=p_t: h.matmul(p_t[:, 0:TB], lhsT=w2b[:, f, c * 128:(c + 1) * 128],
                                                                  rhs=a_t[:, f, 0:TB], start=(f == 0), stop=(f == FC - 1)),
                     reads=[a_b, w2_bufs[f // 4]], writes=[p_b])
            S.op("dve", lambda h, c=c, p_t=p_t: h.tensor_tensor(out=o_t[:, c, 0:TB], in0=p_t[:, 0:TB], in1=x_t[:, c, 0:TB],
                                                                 op=ALU.add),
                 reads=[p_b, x_b], writes=[o_b])
        if final_norm:
            f_t, f_b = f_ring.next()
            emit_rmsnorm(cx, o_t, o_b, KC, TB, gf_t, gf_b, ones_t, ones_b, sq_ring, st_ring, rstd_ring, f_t, f_b, D)
            S.dma("pool", ov[:, :, t0:t0 + TB], f_t[:, :, :], reads=[f_b])
        else:
            S.dma("pool", ov[:, :, t0:t0 + TB], o_t[:, :, :], reads=[o_b])
        cur = nxt
        if mid_hook is not None and bi == 1:
            mid_hook()
    cx.pop()
    return cx.finish() if own else None


def run_spmd(nc, in_maps):
    res = run_bass_kernel_spmd(nc, in_maps, core_ids=list(range(len(in_maps))))
    return res.results


def vec128(v, k):
    return np.ascontiguousarray(np.asarray(v, np.float32).reshape(k, 128).T)


def load_cast(cx, q, dst_ap, src_ap, buf):
    cx.S.dma(q, dst_ap, src_ap, writes=[buf])


def build_ea(ntok=TOK, parts='quv', qlvl=4, cx=None, order=None, mid_hook=None):
    TB = 512
    NB = ntok // TB
    own = cx is None
    cx = Ctx() if own else cx
    cx.push()
    S = cx.S
    xT = cx.dram_in("xT", [D, ntok])
    w_in = cx.dram_in("w_in", [D, 1280])
    gin = cx.dram_in("g", [128, KC])
    cosd = cx.dram_in("cos", [128, ntok])
    sind = cx.dram_in("sin", [128, ntok])
    rotd = cx.dram_in("rot", [128, 128])
    QsT = cx.dram_out("QsT", [128, 4, ntok], BF16)
    KT = cx.dram_out("KT", [128, ntok], BF16)
    Vaug = cx.dram_out("Vaug", [ntok, 130], BF16)
    UT = cx.dram_out("UT", [128, 4, ntok], BF16)

    wb = cx.sb("wb", [128, KC, 1280], BF16)
    w_bufs = [Buf(f"w{k}") for k in range(KC)]
    g_t = cx.sb("g", [128, KC], F32); g_b = Buf("g")
    ones_t = cx.sb("ones", [128, 128], BF16); ones_b = Buf("ones")
    rot_t = cx.sb("rot", [128, 128], BF16); rot_b = Buf("rot")
    x_ring = mk_ring(cx, "sb", "x", 2, [128, KC, TB], F32)
    h_ring = mk_ring(cx, "sb", "h", 2, [128, KC, TB], BF16)
    sq_ring = mk_ring(cx, "sb", "sq", 3, [128, TB], BF16)
    rstd_ring = mk_ring(cx, "sb", "rstd", 2, [128, TB], F32)
    cos_ring = mk_ring(cx, "sb", "cos", 2, [128, TB], F32)
    sin_ring = mk_ring(cx, "sb", "sin", 2, [128, TB], F32)
    qb_ring = mk_ring(cx, "sb", "qb", 2, [128, TB], BF16)
    t1_ring = mk_ring(cx, "sb", "t1", 2, [128, TB], F32)
    t2_ring = mk_ring(cx, "sb", "t2", 2, [128, TB], F32)
    qo_ring = mk_ring(cx, "sb", "qo", 2, [128, 5, TB], BF16)
    uo_ring = mk_ring(cx, "sb", "uo", 2, [128, 4, TB], BF16)
    vo_ring = mk_ring(cx, "sb", "vo", 2, [128, 4, 130], BF16)
    st_ring = mk_ring(cx, "ps", "st", 1, [128, 512], F32)
    pq_ring = mk_ring(cx, "ps", "pq", 3, [128, 512], F32)
    pr_ring = mk_ring(cx, "ps", "pr", 2, [128, 512], F32)
    pv_ring = mk_ring(cx, "ps", "pv", 2, [128, 512], F32)

    S.dma("sp", g_t[:, :], gin[:, :], writes=[g_b])
    S.op("dve", lambda h: h.tensor_scalar_mul(out=g_t[:, :], in0=g_t[:, :], scalar1=float(np.sqrt(D))),
         reads=[g_b], writes=[g_b])
    S.op("pool", lambda h: h.memset(ones_t[:, :], 1.0), writes=[ones_b])
    S.dma("pool", rot_t[:, :], rotd[:, :], writes=[rot_b])
    for (vt, vb) in vo_ring.items:
        S.op("pool", lambda h, vt=vt: h.memset(vt[:, :, :], 1.0), writes=[vb])
    for k in range(KC):
        for j in range(2):
            src = w_in[k * 128:(k + 1) * 128, j * 256:(j + 1) * 256].rearrange("p (c d) -> p c d", c=4, d=64)
            dst = wb[:, k, 0:512].rearrange("p (c j d) -> p c j d", c=4, j=2, d=64)[:, :, j, :]
            S.dma(cx.bind.get("_wq", "pool"), dst, src, writes=[w_bufs[k]])
        S.dma(cx.bind.get("_wq", "pool"), wb[:, k, 512:1280], w_in[k * 128:(k + 1) * 128, 512:1280], writes=[w_bufs[k]])
    xv = xT.rearrange("(c p) t -> p c t", p=128)
    cx.run_hook()

    def blk(b):
        t0 = b * TB
        x_t, x_b = x_ring.next()
        S.dma("sp", x_t[:, :, :], xv[:, :, t0:t0 + TB], writes=[x_b])
        cos_t, cos_b = cos_ring.next()
        sin_t, sin_b = sin_ring.next()
        S.dma("sp", cos_t[:, :], cosd[:, t0:t0 + TB], writes=[cos_b])
        S.dma("sp", sin_t[:, :], sind[:, t0:t0 + TB], writes=[sin_b])
        h_t, h_b = h_ring.next()
        emit_rmsnorm(cx, x_t, x_b, KC, TB, g_t, g_b, ones_t, ones_b, sq_ring, st_ring, rstd_ring, h_t, h_b, D)
        yield
        qo_t, qo_b = qo_ring.next()
        for c in (range(5) if 'q' in parts else []):
            pq_t, pq_b = pq_ring.next()
            for k in range(KC):
                S.op("pe", lambda h, c=c, k=k, pq_t=pq_t, h_t=h_t: h.matmul(
                    pq_t[:, 0:TB], lhsT=wb[:, k, c * 128:(c + 1) * 128], rhs=h_t[:, k, :],
                    start=(k == 0), stop=(k == KC - 1)), reads=[h_b, w_bufs[k]], writes=[pq_b])
            qb_t, qb_b = qb_ring.next()
            S.op("act", lambda h, pq_t=pq_t, qb_t=qb_t: h.activation(out=qb_t[:, :], in_=pq_t[:, 0:TB], func=AF.Copy),
                 reads=[pq_b], writes=[qb_b])
            if qlvl == 1:
                S.op("act", lambda h, c=c, pq_t=pq_t, qo_t=qo_t: h.activation(out=qo_t[:, c, :], in_=pq_t[:, 0:TB], func=AF.Copy),
                     reads=[pq_b], writes=[qo_b])
                continue
            pr_t, pr_b = pr_ring.next()
            S.op("pe", lambda h, pr_t=pr_t, qb_t=qb_t: h.matmul(pr_t[:, 0:TB], lhsT=rot_t[:, :], rhs=qb_t[:, :],
                                                               start=True, stop=True),
                 reads=[qb_b, rot_b], writes=[pr_b])
            t1_t, t1_b = t1_ring.next()
            t2_t, t2_b = t2_ring.next()
            if qlvl == 2:
                S.op("act", lambda h, c=c, pr_t=pr_t, qo_t=qo_t: h.activation(out=qo_t[:, c, :], in_=pr_t[:, 0:TB], func=AF.Copy),
                     reads=[pr_b], writes=[qo_b])
                continue
            S.op("dve", lambda h, t1_t=t1_t, pq_t=pq_t, cos_t=cos_t: h.tensor_tensor(
                out=t1_t[:, :], in0=pq_t[:, 0:TB], in1=cos_t[:, :], op=ALU.mult), reads=[pq_b, cos_b], writes=[t1_b])
            if qlvl == 3:
                S.op("act", lambda h, c=c, t1_t=t1_t, qo_t=qo_t: h.activation(out=qo_t[:, c, :], in_=t1_t[:, :], func=AF.Copy),
                     reads=[t1_b], writes=[qo_b])
                continue
            S.op("dve", lambda h, t2_t=t2_t, pr_t=pr_t, sin_t=sin_t: h.tensor_tensor(
                out=t2_t[:, :], in0=pr_t[:, 0:TB], in1=sin_t[:, :], op=ALU.mult), reads=[pr_b, sin_b], writes=[t2_b])
            S.op("dve", lambda h, c=c, qo_t=qo_t, t1_t=t1_t, t2_t=t2_t: h.tensor_tensor(
                out=qo_t[:, c, :], in0=t1_t[:, :], in1=t2_t[:, :], op=ALU.add), reads=[t1_b, t2_b], writes=[qo_b])
        if 'q' in parts:
            S.dma("pool", QsT[:, :, t0:t0 + TB], qo_t[:, 0:4, :], reads=[qo_b])
            S.dma("pool", KT[:, t0:t0 + TB], qo_t[:, 4, :], reads=[qo_b])
        yield
        uo_t, uo_b = uo_ring.next()
        for gi in (range(4) if 'u' in parts else []):
            pq_t, pq_b = pq_ring.next()
            for k in range(KC):
                S.op("pe", lambda h, gi=gi, k=k, pq_t=pq_t, h_t=h_t: h.matmul(
                    pq_t[:, 0:TB], lhsT=wb[:, k, 768 + gi * 128:768 + (gi + 1) * 128], rhs=h_t[:, k, :],
                    start=(k == 0), stop=(k == KC - 1)), reads=[h_b, w_bufs[k]], writes=[pq_b])
            S.op("act", lambda h, gi=gi, pq_t=pq_t, uo_t=uo_t: h.activation(out=uo_t[:, gi, :], in_=pq_t[:, 0:TB], func=AF.Copy),
                 reads=[pq_b], writes=[uo_b])
        if 'u' in parts:
            S.dma("pool", UT[:, :, t0:t0 + TB], uo_t[:, :, :], reads=[uo_b])
        if 'v' not in parts:
            return
        yield
        vo_t, vo_b = vo_ring.next()
        pv_t, pv_b = pv_ring.next()
        for ti in range(TB // 128):
            for k in range(KC):
                S.op("pe", lambda h, ti=ti, k=k, pv_t=pv_t, h_t=h_t: h.matmul(
                    pv_t[:, ti * 128:(ti + 1) * 128], lhsT=h_t[:, k, ti * 128:(ti + 1) * 128], rhs=wb[:, k, 640:768],
                    start=(k == 0), stop=(k == KC - 1)), reads=[h_b, w_bufs[k]], writes=[pv_b])
        for ti in range(TB // 128):
            for j in range(2):
                S.op("act", lambda h, ti=ti, j=j, pv_t=pv_t, vo_t=vo_t: h.activation(
                    out=vo_t[:, ti, j * 65:j * 65 + 64], in_=pv_t[:, ti * 128 + j * 64:ti * 128 + (j + 1) * 64], func=AF.Copy),
                    reads=[pv_b], writes=[vo_b])
        S.dma("pool", Vaug[t0:t0 + TB, :].rearrange("(i p) n -> p i n", p=128), vo_t[:, :, :], reads=[vo_b])
        yield

    order = list(range(NB)) if order is None else order
    if mid_hook is not None:
        run_interleaved((blk(b) for b in order[:2]), 2, 2)
        mid_hook()
        run_interleaved((blk(b) for b in order[2:]), 2, 2)
    else:
        run_interleaved((blk(b) for b in order), 2, 2)
    cx.pop()
    return cx.finish() if own else None


def rope_tables(pos, half, nrows):
    inv = (np.float32(10000.0) ** (-np.arange(half, dtype=np.float32) / np.float32(half))).astype(np.float32)
    ang = pos.astype(np.float32)[None, :] * inv[np.arange(nrows) % half][:, None]
    return np.cos(ang).astype(np.float32), np.sin(ang).astype(np.float32)


def rot_matrix(dh, nrows=128):
    R = np.zeros((nrows, nrows), np.float32)
    half = dh // 2
    for m in range(nrows):
        d = m % dh
        base = m - d
        if d < half:
            R[base + d + half, m] = -1.0
        else:
            R[base + d - half, m] = 1.0
    return R


def build_eb(ntok=TOK, cx=None):
    TB = 512
    NB = ntok // TB
    NT = ntok // 128
    own = cx is None
    cx = Ctx() if own else cx
    cx.push()
    S = cx.S
    QsT = cx.dram_in("QsT", [128, 4, ntok], BF16)
    KTh = cx.dram_in("KTh", [128, ntok + 256], BF16)
    Vh = cx.dram_in("Vh", [ntok + 256, 130], BF16)
    UTh = cx.dram_in("UTh", [128, 4, ntok + 16], BF16)
    xT = cx.dram_in("xT", [D, ntok])
    w_pool = cx.dram_in("w_pool", [4, 128, 128])
    pscale = cx.dram_in("pscale", [128, 4])
    w_out = cx.dram_in("w_out", [D, D])
    sinkrow = cx.dram_in("sinkrow", [1, 2, 512])
    masksd = cx.dram_in("masks", [4, 128, 512])
    invcd = cx.dram_in("invc", [128, 2, 4, 16])
    identd = cx.dram_in("ident", [128, 128])
    oT = cx.dram_out("oT", [D, ntok])

    woA = cx.sb("woA", [128, 4, D], BF16); woA_b = Buf("woA")
    woB = cx.sb("woB", [128, 4, D], BF16); woB_b = Buf("woB")
    wp = cx.sb("wp", [128, 4, 128], BF16); wp_b = Buf("wp")
    ps_t = cx.sb("ps", [128, 4], F32); ps_b = Buf("ps")
    mk_t = cx.sb("mk", [128, 4, 512], BF16); mk_b = Buf("mk")
    id_t = cx.sb("ident", [128, 128], BF16); id_b = Buf("ident")
    invc_t = cx.sb("invc", [128, 2, 4, 16], F32); invc_b = Buf("invc")
    sk_t = cx.sb("sk", [1, 2, 512], F32); sk_b = Buf("sk")
    esk_t = cx.sb("esk", [1, 2, 512], BF16); esk_b = Buf("esk")
    sel_t = cx.sb("sel", [1, 128], BF16); sel_b = Buf("sel")
    ones32 = cx.sb("ones32", [128, 64], F32); ones32_b = Buf("ones32")
    qsA_ring = mk_ring(cx, "sb", "qsA", 2, [128, 4, TB], BF16)
    qsB_ring = mk_ring(cx, "sb", "qsB", 2, [128, 4, TB], BF16)
    kt_ring = mk_ring(cx, "sb", "kt", 2, [128, 6 * 128], BF16)
    v_ring = mk_ring(cx, "sb", "v", 2, [128, 6, 130], BF16)
    u_ring = mk_ring(cx, "sb", "u", 2, [128, 4, TB + 16], BF16)
    x_ring = mk_ring(cx, "sb", "x", 2, [128, KC, TB], F32)
    p_ring = mk_ring(cx, "sb", "p", 4, [128, 512], BF16)
    osb_ring = mk_ring(cx, "sb", "osb", 3, [128, 512], F32)
    rc_ring = mk_ring(cx, "sb", "rc", 3, [128, 512], F32)
    ya_ring = mk_ring(cx, "sb", "ya", 2, [64, 8, TB], BF16)
    yp_ring = mk_ring(cx, "sb", "yp", 2, [128, 4, TB], BF16)
    yb_ring = mk_ring(cx, "sb", "yb", 2, [128, 4, TB], BF16)
    d_ring = mk_ring(cx, "sb", "d", 2, [128, 4, TB], BF16)
    tmp_rings = [mk_ring(cx, "sb", f"tp{g}", 2, [128, TB + 16], F32) for g in range(4)]
    e16_ring = mk_ring(cx, "sb", "e16", 2, [128, 16], F32)
    s_ring = mk_ring(cx, "ps", "s", 3, [128, 512], F32)
    o_ring = mk_ring(cx, "ps", "o", 2, [128, 512], F32)
    bc_ring = mk_ring(cx, "ps", "bc", 1, [128, 512], F32)
    y_ring = mk_ring(cx, "ps", "y", 2, [128, 512], F32)

    S.dma(cx.bind.get("_wq", "pool"), woA[:, :, :], w_out[0:512, :].rearrange("(i p) n -> p i n", p=128), writes=[woA_b])
    S.dma(cx.bind.get("_wq", "pool"), woB[:, :, :], w_out[512:1024, :].rearrange("(g p) n -> p g n", p=128), writes=[woB_b])
    S.dma("pool", wp[:, :, :], w_pool.rearrange("g i j -> i g j"), writes=[wp_b])
    S.dma("pool", mk_t[:, :, :], masksd.rearrange("m p n -> p m n"), writes=[mk_b])
    S.op("dve", lambda h: h.tensor_scalar(out=mk_t[:, :, :], in0=mk_t[:, :, :], scalar1=-1.0, scalar2=30000.0, op0=ALU.add, op1=ALU.mult),
         reads=[mk_b], writes=[mk_b])
    S.dma("pool", id_t[:, :], identd[:, :], writes=[id_b])
    S.dma("sp", ps_t[:, :], pscale[:, :], writes=[ps_b])
    S.dma("sp", invc_t[:, :, :, :], invcd[:, :, :, :], writes=[invc_b])
    S.dma("sp", sk_t[:, :, :], sinkrow[:, :, :], writes=[sk_b])
    S.op("act", lambda h: h.activation(out=esk_t[:, :, :], in_=sk_t[:, :, :], func=AF.Exp), reads=[sk_b], writes=[esk_b])
    S.op("pool", lambda h: h.memset(sel_t[:, :], 0.0), writes=[sel_b])
    S.op("pool", lambda h: h.memset(sel_t[:, 64:65], 1.0), writes=[sel_b])
    S.op("pool", lambda h: h.memset(ones32[:, :], 1.0), writes=[ones32_b])
    for (qt, qb_) in qsA_ring.items:
        S.op("pool", lambda h, qt=qt: h.memset(qt[64:128, :, :], 0.0), writes=[qb_])
    for (qt, qb_) in qsB_ring.items:
        S.op("pool", lambda h, qt=qt: h.memset(qt[0:64, :, :], 0.0), writes=[qb_])
    xv = xT.rearrange("(c p) t -> p c t", p=128)
    ov = oT.rearrange("(c p) t -> p c t", p=128)
    cx.run_hook()

    def blk(b):
        t0 = b * TB
        qsA_t, qsA_b = qsA_ring.next()
        qsB_t, qsB_b = qsB_ring.next()
        kt_t, kt_b = kt_ring.next()
        v_t, v_b = v_ring.next()
        u_t, u_b = u_ring.next()
        x_t, x_b = x_ring.next()
        S.dma("sp", qsA_t[0:64, :, :], QsT[0:64, :, t0:t0 + TB], writes=[qsA_b])
        S.dma("sp", qsB_t[64:128, :, :], QsT[64:128, :, t0:t0 + TB], writes=[qsB_b])
        S.dma("sp", kt_t[:, :], KTh[:, t0:t0 + 768], writes=[kt_b])
        S.dma("sp", v_t[:, :, :], Vh[t0:t0 + 768, :].rearrange("(i p) n -> p i n", p=128), writes=[v_b])
        S.dma("sp", u_t[:, :, :], UTh[:, :, t0:t0 + TB + 16], writes=[u_b])
        S.dma("sp", x_t[:, :, :], xv[:, :, t0:t0 + TB], writes=[x_b])
        yield
        ya_t, ya_b = ya_ring.next()
        tiles = [(nl, j, mi, dm) for nl in range(4) for mi, dm in enumerate((-1, 0, 1)) for j in range(2)]
        LA = 2
        st = {}
        unit_o = {}
        deferred = []

        def emit_S(t):
            nl, j, mi, dm = tiles[t]
            i = nl + dm + 1
            s_t, s_b = s_ring.next()
            q_t, q_b = (qsA_t, qsA_b) if j == 0 else (qsB_t, qsB_b)
            S.op("pe", lambda h: h.matmul(s_t[:, :], lhsT=kt_t[:, i * 128:(i + 1) * 128],
                                          rhs=q_t[:, :, nl * 128:(nl + 1) * 128], start=True, stop=(dm == 0)),
                 reads=[kt_b, q_b], writes=[s_b])
            if dm != 0:
                n_ = 4 * b + nl
                if dm == -1:
                    mi_ = 2 if n_ == 0 else 0
                else:
                    mi_ = 3 if n_ == NT - 1 else 1
                S.op("pe", lambda h: h.matmul(s_t[:, :], lhsT=id_t[:, :], rhs=mk_t[:, mi_, :], start=False, stop=True),
                     reads=[id_b, mk_b], writes=[s_b])
            st[t] = (s_t, s_b)

        def flush_deferred():
            while deferred:
                (o_t, o_b, osb_t, osb_b, rc_t, rc_b, nl, j) = deferred.pop(0)
                S.op("act", lambda h: h.activation(out=rc_t[64:65, :], in_=osb_t[64:65, :], func=AF.Ln), reads=[osb_b], writes=[rc_b])
                S.op("act", lambda h: h.activation(out=rc_t[64:65, :], in_=rc_t[64:65, :], func=AF.Exp, scale=-1.0),
                     reads=[rc_b], writes=[rc_b])
                bc_t, bc_b = bc_ring.next()
                S.op("pe", lambda h: h.matmul(bc_t[0:64, :], lhsT=ones32[64:65, 0:64], rhs=rc_t[64:65, :], start=True, stop=True),
                     reads=[rc_b, ones32_b], writes=[bc_b])
                S.op("dve", lambda h: h.tensor_tensor(
                    out=ya_t[0:64, j * 4:(j + 1) * 4, nl * 128:(nl + 1) * 128],
                    in0=osb_t[0:64, :].rearrange("p (c q) -> p c q", c=4),
                    in1=bc_t[0:64, :].rearrange("p (c q) -> p c q", c=4), op=ALU.mult),
                    reads=[osb_b, bc_b], writes=[ya_b])

        for t in range(min(LA, len(tiles))):
            emit_S(t)
        for t in range(len(tiles)):
            nl, j, mi, dm = tiles[t]
            n = 4 * b + nl
            i = nl + dm + 1
            if mi == 0:
                unit_o[(nl, j)] = o_ring.next()
            o_t, o_b = unit_o[(nl, j)]
            s_t, s_b = st.pop(t)
            p_t, p_b = p_ring.next()
            S.op("act", lambda h: h.activation(out=p_t[:, :], in_=s_t[:, :], func=AF.Exp, scale=0.125), reads=[s_b], writes=[p_b])
            if t + LA < len(tiles):
                emit_S(t + LA)
            S.op("pe", lambda h: h.matmul(o_t[0:65, :], lhsT=v_t[:, i, j * 65:(j + 1) * 65], rhs=p_t[:, :], start=(mi == 0), stop=False),
                 reads=[v_b, p_b], writes=[o_b])
            if mi == 0 and j == 1:
                flush_deferred()
            if mi == 2:
                S.op("pe", lambda h: h.matmul(o_t[0:65, :], lhsT=sel_t[0:1, 0:65], rhs=esk_t[0:1, j, :], start=False, stop=True),
                     reads=[sel_b, esk_b], writes=[o_b])
                osb_t, osb_b = osb_ring.next()
                rc_t, rc_b = rc_ring.next()
                S.op("dve", lambda h: h.tensor_copy(out=osb_t[0:65, :], in_=o_t[0:65, :]), reads=[o_b], writes=[osb_b])
                deferred.append((o_t, o_b, osb_t, osb_b, rc_t, rc_b, nl, j))
        flush_deferred()
        yield
        yp_t, yp_b = yp_ring.next()
        S.dma("sp", yp_t[0:64, :, :], ya_t[0:64, 0:8:2, :], reads=[ya_b], writes=[yp_b])
        S.dma("sp", yp_t[64:128, :, :], ya_t[0:64, 1:8:2, :], reads=[ya_b], writes=[yp_b])
        d_t, d_b = d_ring.next()
        L = TB + 16
        for g in range(4):
            w = 2 << g
            steps = g + 1
            src_t, src_b, ln = None, None, L
            for s_i in range(steps):
                sh = 1 << s_i
                tp_t, tp_b = tmp_rings[g].next()
                nl_ = ln - sh
                if s_i == 0:
                    S.op("pool", lambda h, tp_t=tp_t, u_t=u_t, g=g, nl_=nl_, sh=sh: h.tensor_tensor(
                        out=tp_t[:, 0:nl_], in0=u_t[:, g, 0:nl_], in1=u_t[:, g, sh:sh + nl_], op=ALU.add),
                        reads=[u_b], writes=[tp_b])
                else:
                    S.op("pool", lambda h, tp_t=tp_t, src_t=src_t, nl_=nl_, sh=sh: h.tensor_tensor(
                        out=tp_t[:, 0:nl_], in0=src_t[:, 0:nl_], in1=src_t[:, sh:sh + nl_], op=ALU.add),
                        reads=[src_b], writes=[tp_b])
                src_t, src_b, ln = tp_t, tp_b, nl_
            off = 8 - w // 2
            S.op("dve", lambda h, d_t=d_t, src_t=src_t, u_t=u_t, g=g, off=off, w=w: h.scalar_tensor_tensor(
                out=d_t[:, g, :], in0=src_t[:, off:off + TB], scalar=1.0 / w, in1=u_t[:, g, 8:8 + TB],
                op0=ALU.mult, op1=ALU.subtract), reads=[src_b, u_b], writes=[d_b])
            for (is_edge, which, c0) in ((b == 0, 0, 0), (b == NB - 1, 1, TB - 16)):
                if not is_edge:
                    continue
                e_t, e_b = e16_ring.next()
                S.op("dve", lambda h, e_t=e_t, src_t=src_t, g=g, off=off, c0=c0, which=which: h.tensor_tensor(
                    out=e_t[:, :], in0=src_t[:, off + c0:off + c0 + 16], in1=invc_t[:, which, g, :], op=ALU.mult),
                    reads=[src_b, invc_b], writes=[e_b])
                S.op("dve", lambda h, e_t=e_t, d_t=d_t, u_t=u_t, g=g, c0=c0: h.tensor_tensor(
                    out=d_t[:, g, c0:c0 + 16], in0=e_t[:, :], in1=u_t[:, g, 8 + c0:8 + c0 + 16], op=ALU.subtract),
                    reads=[e_b, u_b, d_b], writes=[d_b])
        yield
        yb_t, yb_b = yb_ring.next()
        for g in range(4):
            y_t, y_b = y_ring.next()
            S.op("pe", lambda h, y_t=y_t, d_t=d_t, g=g: h.matmul(y_t[:, :], lhsT=wp[:, g, :], rhs=d_t[:, g, :], start=True, stop=True),
                 reads=[wp_b, d_b], writes=[y_b])
            S.op("dve", lambda h, y_t=y_t, yb_t=yb_t, g=g: h.tensor_scalar_mul(out=yb_t[:, g, :], in0=y_t[:, :], scalar1=ps_t[:, g:g + 1]),
                 reads=[y_b, ps_b], writes=[yb_b])
        for o in range(KC):
            y_t, y_b = y_ring.next()
            for hh in range(4):
                S.op("pe", lambda h, y_t=y_t, yp_t=yp_t, hh=hh, o=o: h.matmul(
                    y_t[:, :], lhsT=woA[:, hh, o * 128:(o + 1) * 128], rhs=yp_t[:, hh, :], start=(hh == 0), stop=False),
                    reads=[woA_b, yp_b], writes=[y_b])
            for g in range(4):
                S.op("pe", lambda h, y_t=y_t, yb_t=yb_t, g=g, o=o: h.matmul(
                    y_t[:, :], lhsT=woB[:, g, o * 128:(o + 1) * 128], rhs=yb_t[:, g, :], start=False, stop=(g == 3)),
                    reads=[woB_b, yb_b], writes=[y_b])
            S.op("dve", lambda h, y_t=y_t, x_t=x_t, o=o: h.tensor_tensor(out=x_t[:, o, :], in0=y_t[:, :], in1=x_t[:, o, :], op=ALU.add),
                 reads=[y_b, x_b], writes=[x_b])
        S.dma("sp", ov[:, :, t0:t0 + TB], x_t[:, :, :], reads=[x_b])
        yield

    run_interleaved((blk(b) for b in range(NB)), 2, 2)
    cx.pop()
    return cx.finish() if own else None


def eb_masks(has_left, has_right):
    ki = np.arange(128)[:, None]
    qi = np.arange(128)[None, :]
    mL = np.tile((ki >= qi).astype(np.float32), (1, 4))
    mR = np.tile((ki <= qi).astype(np.float32), (1, 4))
    return np.stack([mL, mR, mL * float(has_left), mR * float(has_right)]).astype(np.float32)


def eb_invc(is_first, is_last):
    out = np.zeros((128, 2, 4, 16), np.float32)
    for g in range(4):
        w = 2 << g
        half = w // 2
        for i in range(16):
            c0 = min(i + half, w) if is_first else w
            r = 16 - i
            c1 = min(half + r, w) if is_last else w
            out[:, 0, g, i] = 1.0 / c0
            out[:, 1, g, i] = 1.0 / c1
    return out


def emit_rope(cx, src_t, src_b, nrow, TB, rot_t, rot_b, cos_t, cos_b, sin_t, sin_b, qb_ring, pr_ring, t1_ring, t2_ring,
              out_ap, out_b):
    S = cx.S
    qb_t, qb_b = qb_ring.next()
    S.op("act", lambda h: h.activation(out=qb_t[0:nrow, :], in_=src_t[0:nrow, 0:TB], func=AF.Copy), reads=[src_b], writes=[qb_b])
    pr_t, pr_b = pr_ring.next()
    S.op("pe", lambda h: h.matmul(pr_t[0:nrow, 0:TB], lhsT=rot_t[0:nrow, 0:nrow], rhs=qb_t[0:nrow, :], start=True, stop=True),
         reads=[qb_b, rot_b], writes=[pr_b])
    t1_t, t1_b = t1_ring.next()
    t2_t, t2_b = t2_ring.next()
    S.op("dve", lambda h: h.tensor_tensor(out=t1_t[0:nrow, :], in0=src_t[0:nrow, 0:TB], in1=cos_t[0:nrow, :], op=ALU.mult),
         reads=[src_b, cos_b], writes=[t1_b])
    S.op("dve", lambda h: h.tensor_tensor(out=t2_t[0:nrow, :], in0=pr_t[0:nrow, 0:TB], in1=sin_t[0:nrow, :], op=ALU.mult),
         reads=[pr_b, sin_b], writes=[t2_b])
    S.op("dve", lambda h: h.tensor_tensor(out=out_ap, in0=t1_t[0:nrow, :], in1=t2_t[0:nrow, :], op=ALU.add),
         reads=[t1_b, t2_b], writes=[out_b])


def build_oa(ntok=TOK, cx=None, mid_hook=None):
    TB = 512
    NB = ntok // TB
    NT = ntok // 128
    own = cx is None
    cx = Ctx() if own else cx
    cx.push()
    S = cx.S
    xT = cx.dram_in("xT", [D, ntok])
    xhalo = cx.dram_in("xhalo", [D, 4])
    w_in = cx.dram_in("w_in", [D, 1440])
    gin = cx.dram_in("g", [128, KC])
    gcq = cx.dram_in("g_cq", [128, 2])
    gckv = cx.dram_in("g_ckv", [128, 1])
    w_uq = cx.dram_in("w_uq", [256, 768])
    w_ukv = cx.dram_in("w_ukv", [128, 1024])
    cwd = cx.dram_in("cw", [128, 4, 4])
    cbd = cx.dram_in("cb", [128, 4])
    wad = cx.dram_in("wa", [2, 8, 64, 64])
    wxd = cx.dram_in("wx", [2, 8, 64, 64])
    bad = cx.dram_in("ba", [128, 2, 4])
    bxd = cx.dram_in("bx", [128, 2, 4])
    lamd = cx.dram_in("lam", [128, 2, 4])
    cosd = cx.dram_in("cos", [128, ntok])
    sind = cx.dram_in("sin", [128, ntok])
    rotd = cx.dram_in("rot", [128, 128])
    QN = cx.dram_out("QN", [512, ntok], BF16)
    QR = cx.dram_out("QR", [256, ntok], BF16)
    KNR = cx.dram_out("KNR", [544, ntok], BF16)
    V5 = cx.dram_out("V5", [1024, NT * 65], BF16)
    GX = cx.dram_out("GX", [512, ntok], BF16)
    AB = cx.dram_out("AB", [2, 2, 512, ntok])
    BLK = cx.dram_out("BLK", [128, NB, 2, 2, 4])
    CAB = cx.dram_out("CAB", [128, 2, 2, 4])
    XR = cx.dram_out("XR", [512, ntok])

    g_t = cx.sb("g", [128, KC], F32); g_b = Buf("g")
    gcq_t = cx.sb("gcq", [128, 2], F32); gcq_b = Buf("gcq")
    gckv_t = cx.sb("gckv", [128, 1], F32); gckv_b = Buf("gckv")
    ones_t = cx.sb("ones", [128, 128], BF16); ones_b = Buf("ones")
    xrh_t = cx.sb("xrh", [128, 4, 4], F32); xrh_b = Buf("xrh")
    cp_t = cx.sb("cp", [128, 2, 4], F32); cp_b = Buf("cp")
    blk_t = cx.sb("blk", [128, NB, 2, 2, 4], F32); blk_b = Buf("blk")
    S.dma("sp", g_t[:, :], gin[:, :], writes=[g_b])
    S.op("dve", lambda h: h.tensor_scalar_mul(out=g_t[:, :], in0=g_t[:, :], scalar1=float(np.sqrt(D))), reads=[g_b], writes=[g_b])
    S.dma("sp", gcq_t[:, :], gcq[:, :], writes=[gcq_b])
    S.op("dve", lambda h: h.tensor_scalar_mul(out=gcq_t[:, :], in0=gcq_t[:, :], scalar1=16.0), reads=[gcq_b], writes=[gcq_b])
    S.dma("sp", gckv_t[:, :], gckv[:, :], writes=[gckv_b])
    S.op("dve", lambda h: h.tensor_scalar_mul(out=gckv_t[:, :], in0=gckv_t[:, :], scalar1=float(np.sqrt(128.0))),
         reads=[gckv_b], writes=[gckv_b])
    S.op("pool", lambda h: h.memset(ones_t[:, :], 1.0), writes=[ones_b])
    S.dma("sp", cp_t[:, :, :], lamd[:, :, :], writes=[cp_b])
    S.op("act", lambda h: h.activation(out=cp_t[:, :, :], in_=cp_t[:, :, :], func=AF.Exp, scale=-1.0), reads=[cp_b], writes=[cp_b])
    S.op("act", lambda h: h.activation(out=cp_t[:, :, :], in_=cp_t[:, :, :], func=AF.Ln, bias=1.0), reads=[cp_b], writes=[cp_b])
    S.op("dve", lambda h: h.tensor_scalar_mul(out=cp_t[:, :, :], in0=cp_t[:, :, :], scalar1=-8.0), reads=[cp_b], writes=[cp_b])

    wabd = cx.sb("wabd", [128, 2, 4, 128], BF16); wxbd = cx.sb("wxbd", [128, 2, 4, 128], BF16); bd_b = Buf("bd")
    cw_t = cx.sb("cw", [128, 4, 4], F32); cb_t = cx.sb("cb", [128, 4], F32); cw_b = Buf("cw")
    ba_t = cx.sb("ba", [128, 2, 4], F32); bx_t = cx.sb("bx", [128, 2, 4], F32); bb_b = Buf("bb")
    S.op("pool", lambda h: h.memset(wabd[:, :, :, :], 0.0), writes=[bd_b])
    S.op("pool", lambda h: h.memset(wxbd[:, :, :, :], 0.0), writes=[bd_b])
    for d in range(2):
        for c in range(4):
            for hf in range(2):
                S.dma("pool", wabd[hf * 64:(hf + 1) * 64, d, c, hf * 64:(hf + 1) * 64], wad[d, 2 * c + hf, :, :], writes=[bd_b])
                S.dma("pool", wxbd[hf * 64:(hf + 1) * 64, d, c, hf * 64:(hf + 1) * 64], wxd[d, 2 * c + hf, :, :], writes=[bd_b])
    S.dma("sp", cw_t[:, :, :], cwd[:, :, :], writes=[cw_b])
    S.dma("sp", cb_t[:, :], cbd[:, :], writes=[cw_b])
    S.dma("sp", ba_t[:, :, :], bad[:, :, :], writes=[bb_b])
    S.dma("sp", bx_t[:, :, :], bxd[:, :, :], writes=[bb_b])
    xv = xT.rearrange("(c p) t -> p c t", p=128)
    cx.push()
    wb = cx.sb("wb", [128, KC, 1440], BF16)
    w_bufs = [Buf(f"w{k}") for k in range(KC)]
    wuqn = cx.sb("wuqn", [128, 2, 512], BF16); wuqr = cx.sb("wuqr", [128, 2, 256], BF16); wuq_b = Buf("wuq")
    wk = cx.sb("wk", [128, 512], BF16); wv = cx.sb("wv", [128, 512], BF16); wkv_b = Buf("wkv")
    rot_t = cx.sb("rot", [128, 128], BF16); rot_b = Buf("rot")
    x_ring = mk_ring(cx, "sb", "x", 2, [128, KC, TB], F32)
    h_ring = mk_ring(cx, "sb", "h", 2, [128, KC, TB], BF16)
    sq_ring = mk_ring(cx, "sb", "sq", 3, [128, TB], BF16)
    rstd_ring = mk_ring(cx, "sb", "rstd", 2, [128, TB], F32)
    cos_ring = mk_ring(cx, "sb", "cos", 2, [128, TB], F32)
    sin_ring = mk_ring(cx, "sb", "sin", 2, [128, TB], F32)
    qb_ring = mk_ring(cx, "sb", "qb", 2, [128, TB], BF16)
    t1_ring = mk_ring(cx, "sb", "t1", 2, [128, TB], F32)
    t2_ring = mk_ring(cx, "sb", "t2", 2, [128, TB], F32)
    cq_ring = mk_ring(cx, "sb", "cq", 2, [128, 2, TB], F32)
    ckv_ring = mk_ring(cx, "sb", "ckv", 2, [128, 1, TB], F32)
    cqn_ring = mk_ring(cx, "sb", "cqn", 2, [128, 2, TB], BF16)
    ckvn_ring = mk_ring(cx, "sb", "ckvn", 2, [128, 1, TB], BF16)
    xr_ring = mk_ring(cx, "sb", "xr", 2, [128, 4, TB], F32)
    gx_ring = mk_ring(cx, "sb", "gx", 2, [128, 4, TB], BF16)
    qn_ring = mk_ring(cx, "sb", "qn", 2, [128, 4, TB], BF16)
    qr_ring = mk_ring(cx, "sb", "qr", 2, [128, 2, TB], BF16)
    kn_ring = mk_ring(cx, "sb", "kn", 2, [128, 4, TB], BF16)
    kr_ring = mk_ring(cx, "sb", "kr", 2, [32, TB], BF16)
    vo_ring = mk_ring(cx, "sb", "vo", 2, [128, 4, 520], BF16)
    hx_t = cx.sb("hx", [128, KC, 4], F32); hx_b = Buf("hx")
    hh_t = cx.sb("hh", [128, KC, 4], BF16); hh_b = Buf("hh")
    st_ring = mk_ring(cx, "ps", "st", 1, [128, 512], F32)
    pq_ring = mk_ring(cx, "ps", "pq", 4, [128, 512], F32)
    pr_ring = mk_ring(cx, "ps", "pr", 1, [128, 512], F32)
    pv_ring = mk_ring(cx, "ps", "pv", 2, [128, 512], F32)

    for k in range(KC):
        S.dma(cx.bind.get("_wq", "pool"), wb[:, k, :], w_in[k * 128:(k + 1) * 128, :], writes=[w_bufs[k]])
    for k in range(2):
        src = w_uq[k * 128:(k + 1) * 128, :].rearrange("p (h e) -> p h e", e=96)
        S.dma("pool", wuqn[:, k, :].rearrange("p (h d) -> p h d", d=64), src[:, :, 0:64], writes=[wuq_b])
        S.dma("pool", wuqr[:, k, :].rearrange("p (h d) -> p h d", d=32), src[:, :, 64:96], writes=[wuq_b])
    srckv = w_ukv.rearrange("p (h e) -> p h e", e=128)
    S.dma("pool", wk[:, :].rearrange("p (h d) -> p h d", d=64), srckv[:, :, 0:64], writes=[wkv_b])
    S.dma("pool", wv[:, :].rearrange("p (h d) -> p h d", d=64), srckv[:, :, 64:128], writes=[wkv_b])
    S.dma("pool", rot_t[:, :], rotd[:, :], writes=[rot_b])
    for (vt, vb) in vo_ring.items:
        S.op("pool", lambda h, vt=vt: h.memset(vt[:, :, :], 1.0), writes=[vb])

    def proj_tile(h_t, h_b, c0, ncols, TBx):
        pq_t, pq_b = pq_ring.next()
        for k in range(KC):
            S.op("pe", lambda h, k=k: h.matmul(pq_t[0:ncols, 0:TBx], lhsT=wb[:, k, c0:c0 + ncols], rhs=h_t[:, k, 0:TBx],
                                               start=(k == 0), stop=(k == KC - 1)), reads=[h_b, w_bufs[k]], writes=[pq_b])
        return pq_t, pq_b

    S.dma("sp", hx_t[:, :, :], xhalo.rearrange("(c p) t -> p c t", p=128), writes=[hx_b])
    emit_rmsnorm(cx, hx_t, hx_b, KC, 4, g_t, g_b, ones_t, ones_b, sq_ring, st_ring, rstd_ring, hh_t, hh_b, D)
    for c in range(4):
        pq_t, pq_b = proj_tile(hh_t, hh_b, 416 + c * 128, 128, 4)
        S.op("act", lambda h, c=c, pq_t=pq_t: h.activation(out=xrh_t[:, c, :], in_=pq_t[:, 0:4], func=AF.Copy),
             reads=[pq_b], writes=[xrh_b])

    def blk(b):
        t0 = b * TB
        x_t, x_b = x_ring.next()
        S.dma("sp", x_t[:, :, :], xv[:, :, t0:t0 + TB], writes=[x_b])
        cos_t, cos_b = cos_ring.next()
        sin_t, sin_b = sin_ring.next()
        S.dma("sp", cos_t[:, :], cosd[:, t0:t0 + TB], writes=[cos_b])
        S.dma("sp", sin_t[:, :], sind[:, t0:t0 + TB], writes=[sin_b])
        h_t, h_b = h_ring.next()
        emit_rmsnorm(cx, x_t, x_b, KC, TB, g_t, g_b, ones_t, ones_b, sq_ring, st_ring, rstd_ring, h_t, h_b, D)
        yield
        cq_t, cq_b = cq_ring.next()
        for c in range(2):
            pq_t, pq_b = proj_tile(h_t, h_b, c * 128, 128, TB)
            S.op("act", lambda h, c=c, pq_t=pq_t, cq_t=cq_t: h.activation(out=cq_t[:, c, :], in_=pq_t[:, 0:TB], func=AF.Copy),
                 reads=[pq_b], writes=[cq_b])
        ckv_t, ckv_b = ckv_ring.next()
        pq_t, pq_b = proj_tile(h_t, h_b, 256, 128, TB)
        S.op("act", lambda h, pq_t=pq_t, ckv_t=ckv_t: h.activation(out=ckv_t[:, 0, :], in_=pq_t[:, 0:TB], func=AF.Copy),
             reads=[pq_b], writes=[ckv_b])
        yield
        pq_t, pq_b = proj_tile(h_t, h_b, 384, 32, TB)
        kr_t, kr_b = kr_ring.next()
        emit_rope(cx, pq_t, pq_b, 32, TB, rot_t, rot_b, cos_t, cos_b, sin_t, sin_b, qb_ring, pr_ring, t1_ring, t2_ring,
                  kr_t[0:32, :], kr_b)
        S.dma("pool", KNR[512:544, t0:t0 + TB], kr_t[:, :], reads=[kr_b])
        yield
        xr_t, xr_b = xr_ring.next()
        gx_t, gx_b = gx_ring.next()
        for c in range(4):
            pq_t, pq_b = proj_tile(h_t, h_b, 416 + c * 128, 128, TB)
            S.op("act", lambda h, c=c, pq_t=pq_t, xr_t=xr_t: h.activation(out=xr_t[:, c, :], in_=pq_t[:, 0:TB], func=AF.Copy),
                 reads=[pq_b], writes=[xr_b])
        for c in range(4):
            pq_t, pq_b = proj_tile(h_t, h_b, 928 + c * 128, 128, TB)
            S.op("act", lambda h, c=c, pq_t=pq_t, gx_t=gx_t: h.activation(out=gx_t[:, c, :], in_=pq_t[:, 0:TB], func=AF.Gelu_apprx_tanh),
                 reads=[pq_b], writes=[gx_b])
        S.dma("pool", XR.rearrange("(c p) t -> p c t", p=128)[:, :, t0:t0 + TB], xr_t[:, :, :], reads=[xr_b])
        S.dma("pool", GX.rearrange("(c p) t -> p c t", p=128)[:, :, t0:t0 + TB], gx_t[:, :, :], reads=[gx_b])
        yield
        cqn_t, cqn_b = cqn_ring.next()
        emit_rmsnorm(cx, cq_t, cq_b, 2, TB, gcq_t, gcq_b, ones_t, ones_b, sq_ring, st_ring, rstd_ring, cqn_t, cqn_b, 256)
        ckvn_t, ckvn_b = ckvn_ring.next()
        emit_rmsnorm(cx, ckv_t, ckv_b, 1, TB, gckv_t, gckv_b, ones_t, ones_b, sq_ring, st_ring, rstd_ring, ckvn_t, ckvn_b, 128)
        yield
        qn_t, qn_b = qn_ring.next()
        for i in range(4):
            pq_t, pq_b = pq_ring.next()
            for k in range(2):
                S.op("pe", lambda h, i=i, k=k, pq_t=pq_t, cqn_t=cqn_t: h.matmul(
                    pq_t[:, 0:TB], lhsT=wuqn[:, k, i * 128:(i + 1) * 128], rhs=cqn_t[:, k, :], start=(k == 0), stop=(k == 1)),
                    reads=[cqn_b, wuq_b], writes=[pq_b])
            S.op("act", lambda h, i=i, pq_t=pq_t, qn_t=qn_t: h.activation(out=qn_t[:, i, :], in_=pq_t[:, 0:TB], func=AF.Copy),
                 reads=[pq_b], writes=[qn_b])
        S.dma("pool", QN.rearrange("(c p) t -> p c t", p=128)[:, :, t0:t0 + TB], qn_t[:, :, :], reads=[qn_b])
        qr_t, qr_b = qr_ring.next()
        for i in range(2):
            pq_t, pq_b = pq_ring.next()
            for k in range(2):
                S.op("pe", lambda h, i=i, k=k, pq_t=pq_t, cqn_t=cqn_t: h.matmul(
                    pq_t[:, 0:TB], lhsT=wuqr[:, k, i * 128:(i + 1) * 128], rhs=cqn_t[:, k, :], start=(k == 0), stop=(k == 1)),
                    reads=[cqn_b, wuq_b], writes=[pq_b])
            emit_rope(cx, pq_t, pq_b, 128, TB, rot_t, rot_b, cos_t, cos_b, sin_t, sin_b, qb_ring, pr_ring, t1_ring, t2_ring,
                      qr_t[:, i, :], qr_b)
        S.dma("pool", QR.rearrange("(c p) t -> p c t", p=128)[:, :, t0:t0 + TB], qr_t[:, :, :], reads=[qr_b])
        yield
        kn_t, kn_b = kn_ring.next()
        for i in range(4):
            pq_t, pq_b = pq_ring.next()
            S.op("pe", lambda h, i=i, pq_t=pq_t, ckvn_t=ckvn_t: h.matmul(
                pq_t[:, 0:TB], lhsT=wk[:, i * 128:(i + 1) * 128], rhs=ckvn_t[:, 0, :], start=True, stop=True),
                reads=[ckvn_b, wkv_b], writes=[pq_b])
            S.op("act", lambda h, i=i, pq_t=pq_t, kn_t=kn_t: h.activation(out=kn_t[:, i, :], in_=pq_t[:, 0:TB], func=AF.Copy),
                 reads=[pq_b], writes=[kn_b])
        S.dma("pool", KNR[0:512, :].rearrange("(c p) t -> p c t", p=128)[:, :, t0:t0 + TB], kn_t[:, :, :], reads=[kn_b])
        yield
        vo_t, vo_b = vo_ring.next()
        for ti in range(TB // 128):
            pv_t, pv_b = pv_ring.next()
            S.op("pe", lambda h, ti=ti, pv_t=pv_t, ckvn_t=ckvn_t: h.matmul(
                pv_t[:, :], lhsT=ckvn_t[:, 0, ti * 128:(ti + 1) * 128], rhs=wv[:, :], start=True, stop=True),
                reads=[ckvn_b, wkv_b], writes=[pv_b])
            S.op("act", lambda h, ti=ti, pv_t=pv_t, vo_t=vo_t: h.activation(
                out=vo_t[:, ti, :].rearrange("p (h e) -> p h e", e=65)[:, :, 0:64],
                in_=pv_t[:, :].rearrange("p (h d) -> p h d", d=64), func=AF.Copy), reads=[pv_b], writes=[vo_b])
        for hd in range(8):
            S.dma("pool", V5[hd * 128:(hd + 1) * 128, :].rearrange("p (i e) -> p i e", e=65)[:, b * 4:(b + 1) * 4, :],
                  vo_t[:, :, hd * 65:(hd + 1) * 65], reads=[vo_b])
        yield

    run_interleaved((blk(b) for b in range(NB)), 2)
    cx.pop()
    if mid_hook is not None:
        mid_hook()

    cx.push()
    xe_ring = mk_ring(cx, "sb", "xe", 2, [128, 4, TB + 4], F32)
    xc_ring = mk_ring(cx, "sb", "xc", 2, [128, 4, TB], F32)
    xcb_ring = mk_ring(cx, "sb", "xcb", 2, [128, 4, TB], BF16)
    r_ring = mk_ring(cx, "sb", "r", 2, [128, 8, TB], F32)
    i_ring = mk_ring(cx, "sb", "i", 2, [128, 8, TB], F32)
    a_ring = mk_ring(cx, "sb", "a", 2, [128, 8, TB], F32)
    b_ring = mk_ring(cx, "sb", "b", 2, [128, 8, TB], F32)
    hl_ring = mk_ring(cx, "sb", "hl", 2, [128, TB], F32)
    sr_ring = mk_ring(cx, "sb", "sr", 2, [128, 8], F32)
    pg_ring = mk_ring(cx, "ps", "pg", 6, [128, 512], F32)
    XRv = XR.rearrange("(c p) t -> p c t", p=128)
    ABv = AB.rearrange("d s (c p) t -> d s p c t", p=128)

    def blk(b):
        t0 = b * TB
        xe_t, xe_b = xe_ring.next()
        lo = 0 if b > 0 else 2
        hi = TB + 3 if b < NB - 1 else TB + 2
        S.dma("sp", xe_t[:, :, lo:hi], XRv[:, :, t0 - 2 + lo:t0 - 2 + hi], writes=[xe_b])
        if b == 0:
            S.op("dve", lambda h, xe_t=xe_t: h.tensor_copy(out=xe_t[:, :, 0:2], in_=xrh_t[:, :, 0:2]), reads=[xrh_b, xe_b], writes=[xe_b])
        if b == NB - 1:
            S.op("dve", lambda h, xe_t=xe_t: h.tensor_copy(out=xe_t[:, :, TB + 2:TB + 3], in_=xrh_t[:, :, 2:3]),
                 reads=[xrh_b, xe_b], writes=[xe_b])
        yield
        xc_t, xc_b = xc_ring.next()
        xcb_t, xcb_b = xcb_ring.next()
        for c in range(4):
            S.op("dve", lambda h, c=c, xc_t=xc_t, xe_t=xe_t: h.tensor_scalar(
                out=xc_t[:, c, :], in0=xe_t[:, c, 0:TB], scalar1=cw_t[:, c, 0:1], scalar2=cb_t[:, c:c + 1],
                op0=ALU.mult, op1=ALU.add), reads=[xe_b, cw_b], writes=[xc_b])
            for j in range(1, 4):
                S.op("dve", lambda h, c=c, j=j, xc_t=xc_t, xe_t=xe_t: h.scalar_tensor_tensor(
                    out=xc_t[:, c, :], in0=xe_t[:, c, j:j + TB], scalar=cw_t[:, c, j:j + 1], in1=xc_t[:, c, :],
                    op0=ALU.mult, op1=ALU.add), reads=[xe_b, cw_b, xc_b], writes=[xc_b])
        S.op("act", lambda h, xc_t=xc_t, xcb_t=xcb_t: h.activation(out=xcb_t[:, :, :], in_=xc_t[:, :, :], func=AF.Copy), reads=[xc_b], writes=[xcb_b])
        yield
        r_t, r_b = r_ring.next()
        i_t, i_b = i_ring.next()
        a_t, a_b = a_ring.next()
        b_t, b_b = b_ring.next()
        sr_t, sr_b = sr_ring.next()
        S.op("dve", lambda h, sr_t=sr_t: h.memset(sr_t[:, :], 0.0), writes=[sr_b])
        for d in range(2):
            for c in range(4):
                q = d * 4 + c
                pg_t, pg_b = pg_ring.next()
                S.op("pe", lambda h, d=d, c=c, pg_t=pg_t, xcb_t=xcb_t: h.matmul(pg_t[:, :], lhsT=wabd[:, d, c, :], rhs=xcb_t[:, c, :],
                                                                         start=True, stop=True), reads=[bd_b, xcb_b], writes=[pg_b])
                S.op("act", lambda h, d=d, c=c, q=q, pg_t=pg_t, r_t=r_t, sr_t=sr_t: h.activation(
                    out=r_t[:, q, :], in_=pg_t[:, :], func=AF.Sigmoid, bias=ba_t[:, d, c:c + 1], accum_out=sr_t[:, q:q + 1]),
                    reads=[pg_b, bb_b], writes=[r_b, sr_b])
                pg_t, pg_b = pg_ring.next()
                S.op("pe", lambda h, d=d, c=c, pg_t=pg_t, xcb_t=xcb_t: h.matmul(pg_t[:, :], lhsT=wxbd[:, d, c, :], rhs=xcb_t[:, c, :],
                                                                         start=True, stop=True), reads=[bd_b, xcb_b], writes=[pg_b])
                S.op("act", lambda h, d=d, c=c, q=q, pg_t=pg_t, i_t=i_t: h.activation(
                    out=i_t[:, q, :], in_=pg_t[:, :], func=AF.Sigmoid, bias=bx_t[:, d, c:c + 1]),
                    reads=[pg_b, bb_b], writes=[i_b])
        yield
        for d in range(2):
            for c in range(4):
                q = d * 4 + c
                S.op("act", lambda h, d=d, c=c, q=q, a_t=a_t, r_t=r_t: h.activation(
                    out=a_t[:, q, :], in_=r_t[:, q, :], func=AF.Exp, scale=cp_t[:, d, c:c + 1]), reads=[r_b, cp_b], writes=[a_b])
                S.op("act", lambda h, d=d, c=c, q=q, sr_t=sr_t, b=b: h.activation(
                    out=blk_t[:, b, d, 0, c:c + 1], in_=sr_t[:, q:q + 1], func=AF.Exp, scale=cp_t[:, d, c:c + 1]),
                    reads=[sr_b, cp_b, blk_b], writes=[blk_b])
        yield
        S.op("dve", lambda h, a_t=a_t, r_t=r_t: h.tensor_tensor(out=r_t[:, :, :], in0=a_t[:, :, :], in1=a_t[:, :, :], op=ALU.mult),
             reads=[a_b, r_b], writes=[r_b])
        S.op("act", lambda h, r_t=r_t: h.activation(out=r_t[:, :, :], in_=r_t[:, :, :], func=AF.Sqrt, scale=-1.0, bias=1.0),
             reads=[r_b], writes=[r_b])
        for d in range(2):
            S.op("dve", lambda h, d=d, i_t=i_t, xc_t=xc_t: h.tensor_tensor(out=i_t[:, d * 4:(d + 1) * 4, :], in0=i_t[:, d * 4:(d + 1) * 4, :],
                                                                        in1=xc_t[:, :, :], op=ALU.mult), reads=[i_b, xc_b], writes=[i_b])
        S.op("dve", lambda h, b_t=b_t, r_t=r_t, i_t=i_t: h.tensor_tensor(out=b_t[:, :, :], in0=r_t[:, :, :], in1=i_t[:, :, :], op=ALU.mult),
             reads=[r_b, i_b], writes=[b_b])
        yield
        for d in range(2):
            for c in range(4):
                q = d * 4 + c
                hl_t, hl_b = hl_ring.next()
                if d == 0:
                    S.op("dve", lambda h, q=q, hl_t=hl_t, a_t=a_t, b_t=b_t: h.tensor_tensor_scan(
                        out=hl_t[:, :], data0=a_t[:, q, :], data1=b_t[:, q, :], initial=0.0, op0=ALU.mult, op1=ALU.add),
                        reads=[a_b, b_b], writes=[hl_b])
                    col = TB - 1
                else:
                    S.op("dve", lambda h, q=q, hl_t=hl_t, a_t=a_t, b_t=b_t: h.tensor_tensor_scan(
                        out=hl_t[:, ::-1], data0=a_t[:, q, ::-1], data1=b_t[:, q, ::-1], initial=0.0, op0=ALU.mult, op1=ALU.add),
                        reads=[a_b, b_b], writes=[hl_b])
                    col = 0
                S.op("act", lambda h, d=d, c=c, hl_t=hl_t, col=col, b=b: h.activation(
                    out=blk_t[:, b, d, 1, c:c + 1], in_=hl_t[:, col:col + 1], func=AF.Copy), reads=[hl_b, blk_b], writes=[blk_b])
        for d in range(2):
            S.dma("act", ABv[d, 0, :, :, t0:t0 + TB], a_t[:, d * 4:(d + 1) * 4, :], reads=[a_b])
            S.dma("sp", ABv[d, 1, :, :, t0:t0 + TB], b_t[:, d * 4:(d + 1) * 4, :], reads=[b_b])
        yield

    run_interleaved((blk(b) for b in range(NB)), 2)
    cab_t = cx.sb("cab", [128, 2, 2, 4], F32); cab_b = Buf("cab")
    for d in range(2):
        S.op("dve", lambda h, d=d: h.memset(cab_t[:, d, 0, :], 1.0), writes=[cab_b])
        S.op("dve", lambda h, d=d: h.memset(cab_t[:, d, 1, :], 0.0), writes=[cab_b])
        order = range(NB) if d == 0 else range(NB - 1, -1, -1)
        for b in order:
            S.op("dve", lambda h, d=d, b=b: h.tensor_tensor(out=cab_t[:, d, 1, :], in0=cab_t[:, d, 1, :], in1=blk_t[:, b, d, 0, :],
                                                            op=ALU.mult), reads=[cab_b, blk_b], writes=[cab_b])
            S.op("dve", lambda h, d=d, b=b: h.tensor_tensor(out=cab_t[:, d, 1, :], in0=cab_t[:, d, 1, :], in1=blk_t[:, b, d, 1, :],
                                                            op=ALU.add), reads=[cab_b, blk_b], writes=[cab_b])
            S.op("dve", lambda h, d=d, b=b: h.tensor_tensor(out=cab_t[:, d, 0, :], in0=cab_t[:, d, 0, :], in1=blk_t[:, b, d, 0, :],
                                                            op=ALU.mult), reads=[cab_b, blk_b], writes=[cab_b])
    S.dma("sp", BLK[:, :, :, :, :], blk_t[:, :, :, :, :], reads=[blk_b])
    S.dma("sp", CAB[:, :, :, :], cab_t[:, :, :, :], reads=[cab_b])
    cx.pop()
    cx.pop()
    return cx.finish() if own else None


def chunk_vec(v, nch):
    return np.ascontiguousarray(np.asarray(v, np.float32).reshape(nch, 128).T)


def oa_inputs(xT, xhalo, P, pos):
    cos, sin = rope_tables(pos, 16, 128)
    return {
        "xT": np.ascontiguousarray(xT), "xhalo": np.ascontiguousarray(xhalo), "w_in": P["w_in"], "g": vec128(P["g"], 8),
        "g_cq": vec128(P["g_cq"], 2), "g_ckv": vec128(P["g_ckv"], 1), "w_uq": P["w_uq"], "w_ukv": P["w_ukv"],
        "cw": np.ascontiguousarray(P["conv_w"].reshape(4, 4, 128).transpose(2, 1, 0)),
        "cb": chunk_vec(P["conv_b"], 4), "wa": P["wa"], "wx": P["wx"],
        "ba": np.ascontiguousarray(P["ba"].reshape(2, 4, 128).transpose(2, 0, 1)),
        "bx": np.ascontiguousarray(P["bx"].reshape(2, 4, 128).transpose(2, 0, 1)),
        "lam": np.ascontiguousarray(P["lam"].reshape(2, 4, 128).transpose(2, 0, 1)),
        "cos": cos, "sin": sin, "rot": rot_matrix(32),
    }


def build_ob1(ntok=TOK, nrank=4, cx=None):
    seq = ntok * nrank
    QG = ntok // 512
    NKT = seq // 128
    NT = ntok // 128
    own = cx is None
    cx = Ctx() if own else cx
    cx.push()
    S = cx.S
    QN = cx.dram_in("QN", [512, ntok], BF16)
    QR = cx.dram_in("QR", [256, ntok], BF16)
    KNg = cx.dram_in("KNg", [8 * nrank * 64, ntok], BF16)
    KRg = cx.dram_in("KRg", [nrank * 32, ntok], BF16)
    Vg = cx.dram_in("Vg", [8 * nrank * 128, NT * 65], BF16)
    YC = cx.dram_out("YC", [512, ntok], BF16)

    q_ring = mk_ring(cx, "sb", "q", 2, [128, ntok], BF16)
    k_ring = mk_ring(cx, "sb", "k", 2, [128, seq], BF16)
    v_ring = mk_ring(cx, "sb", "v", 2, [128, NKT, 65], BF16)
    p_ring = mk_ring(cx, "sb", "p", 4, [128, 1024], BF16)
    osb_ring = mk_ring(cx, "sb", "osb", 2, [64, 512], F32)
    rc_ring = mk_ring(cx, "sb", "rc", 2, [128, 512], F32)
    yc_ring = mk_ring(cx, "sb", "yc", 2, [64, 512], BF16)
    ones32 = cx.sb("ones32", [128, 64], F32); ones32_b = Buf("ones32")
    s_ring = mk_ring(cx, "ps", "s", 3, [128, 1024], F32)
    o_ring = mk_ring(cx, "ps", "o", 2, [128, 512], F32)
    S.op("pool", lambda h: h.memset(ones32[:, :], 1.0), writes=[ones32_b])
    scale = float(96 ** -0.5)
    NKP = NKT // 2
    LA = 2

    def load_head(hd):
        q_t, q_b = q_ring.next()
        k_t, k_b = k_ring.next()
        v_t, v_b = v_ring.next()
        S.dma("sp", q_t[0:64, :], QN[hd * 64:(hd + 1) * 64, :], writes=[q_b])
        S.dma("sp", q_t[64:96, :], QR[hd * 32:(hd + 1) * 32, :], writes=[q_b])
        for r in range(nrank):
            kr0 = ((hd // 2) * nrank + r) * 128 + (hd % 2) * 64
            S.dma("sp", k_t[0:64, r * ntok:(r + 1) * ntok], KNg[kr0:kr0 + 64, :], writes=[k_b])
            S.dma("sp", k_t[64:96, r * ntok:(r + 1) * ntok], KRg[r * 32:(r + 1) * 32, :], writes=[k_b])
            S.dma("sp", v_t[:, r * NT:(r + 1) * NT, :],
                  Vg[(hd * nrank + r) * 128:(hd * nrank + r + 1) * 128, :].rearrange("p (i e) -> p i e", e=65), writes=[v_b])
        return (q_t, q_b, k_t, k_b, v_t, v_b)

    nxt = load_head(0)
    cx.run_hook()
    for hd in range(8):
        q_t, q_b, k_t, k_b, v_t, v_b = nxt
        if hd + 1 < 8:
            nxt = load_head(hd + 1)
        for qg in range(QG):
            o_t, o_b = o_ring.next()
            stiles = {}

            def emit_s(kp):
                s_t, s_b = s_ring.next()
                for hf in range(2):
                    kt = 2 * kp + hf
                    S.op("pe", lambda h, kt=kt, hf=hf: h.matmul(s_t[:, hf * 512:(hf + 1) * 512], lhsT=k_t[0:96, kt * 128:(kt + 1) * 128],
                                                                rhs=q_t[0:96, qg * 512:(qg + 1) * 512], start=True, stop=True),
                         reads=[k_b, q_b], writes=[s_b])
                stiles[kp] = (s_t, s_b)

            for kp in range(min(LA, NKP)):
                emit_s(kp)
            for kp in range(NKP):
                s_t, s_b = stiles.pop(kp)
                p_t, p_b = p_ring.next()
                S.op("act", lambda h, s_t=s_t, p_t=p_t: h.activation(out=p_t[:, :], in_=s_t[:, :], func=AF.Exp, scale=scale),
                     reads=[s_b], writes=[p_b])
                if kp + LA < NKP:
                    emit_s(kp + LA)
                for hf in range(2):
                    kt = 2 * kp + hf
                    S.op("pe", lambda h, kt=kt, hf=hf, p_t=p_t: h.matmul(o_t[0:65, :], lhsT=v_t[:, kt, 0:65], rhs=p_t[:, hf * 512:(hf + 1) * 512],
                                                                         start=(kt == 0), stop=(kt == NKT - 1)),
                         reads=[v_b, p_b], writes=[o_b])
            osb_t, osb_b = osb_ring.next()
            rc_t, rc_b = rc_ring.next()
            S.op("act", lambda h: h.activation(out=osb_t[:, :], in_=o_t[0:64, :], func=AF.Copy), reads=[o_b], writes=[osb_b])
            S.op("dve", lambda h: h.reciprocal(out=rc_t[64:65, :], in_=o_t[64:65, :]), reads=[o_b], writes=[rc_b])
            bc_t, bc_b = s_ring.next()
            S.op("pe", lambda h: h.matmul(bc_t[0:64, 0:512], lhsT=ones32[64:65, 0:64], rhs=rc_t[64:65, :], start=True, stop=True),
                 reads=[rc_b, ones32_b], writes=[bc_b])
            yc_t, yc_b = yc_ring.next()
            S.op("dve", lambda h: h.tensor_tensor(out=yc_t[:, :], in0=osb_t[:, :], in1=bc_t[0:64, 0:512], op=ALU.mult),
                 reads=[osb_b, bc_b], writes=[yc_b])
            S.dma("pool", YC[hd * 64:(hd + 1) * 64, qg * 512:(qg + 1) * 512], yc_t[:, :], reads=[yc_b])
    cx.pop()
    return cx.finish() if own else None


def build_ob2(ntok=TOK, ngrp=4, cx=None):
    TB = 512
    NB = ntok // TB
    own = cx is None
    cx = Ctx() if own else cx
    cx.push()
    S = cx.S
    AB = cx.dram_in("AB", [2, 2, 512, ntok])
    GX = cx.dram_in("GX", [512, ntok], BF16)
    YC = cx.dram_in("YC", [512, ntok], BF16)
    xT = cx.dram_in("xT", [D, ntok])
    w_out = cx.dram_in("w_out", [D, D])
    BLK = cx.dram_in("BLK", [128, NB, 2, 2, 4])
    CABg = cx.dram_in("CABg", [128, ngrp, 16])
    mfd = cx.dram_in("mf", [128, ngrp])
    mbd = cx.dram_in("mb", [128, ngrp])
    oT = cx.dram_out("oT", [D, ntok])

    woA = cx.sb("woA", [128, 4, D], BF16); woA_b = Buf("woA")
    woB = cx.sb("woB", [128, 4, D], BF16); woB_b = Buf("woB")
    blk_t = cx.sb("blk", [128, NB, 2, 2, 4], F32); blk_b = Buf("blk")
    cab_t = cx.sb("cab", [128, ngrp, 16], F32); cab_b = Buf("cab")
    m_t = cx.sb("m", [128, 2, ngrp], F32); m_b = Buf("m")
    hin_t = cx.sb("hin", [128, 2, 4], F32); hin_b = Buf("hin")
    tmp_t = cx.sb("tmp", [128, 4], F32); tmp_b = Buf("tmp")
    init_t = cx.sb("init", [128, NB, 2, 4], F32); init_b = Buf("init")
    ab_ring = mk_ring(cx, "sb", "ab", 2, [128, 2, 2, 4, TB], F32)
    hs_ring = mk_ring(cx, "sb", "hs", 2, [128, 2, 4, TB], F32)
    gx_ring = mk_ring(cx, "sb", "gx", 2, [128, 4, TB], BF16)
    yc_ring = mk_ring(cx, "sb", "yc", 2, [128, 4, TB], BF16)
    yd_ring = mk_ring(cx, "sb", "yd", 2, [128, 4, TB], BF16)
    x_ring = mk_ring(cx, "sb", "x", 2, [128, KC, TB], F32)
    y_ring = mk_ring(cx, "ps", "y", 3, [128, 512], F32)

    S.dma(cx.bind.get("_wq", "pool"), woA[:, :, :], w_out[0:512, :].rearrange("(i p) n -> p i n", p=128), writes=[woA_b])
    S.dma(cx.bind.get("_wq", "pool"), woB[:, :, :], w_out[512:1024, :].rearrange("(g p) n -> p g n", p=128), writes=[woB_b])
    S.dma("sp", blk_t[:, :, :, :, :], BLK[:, :, :, :, :], writes=[blk_b])
    S.dma("sp", cab_t[:, :, :], CABg[:, :, :], writes=[cab_b])
    S.dma("sp", m_t[:, 0, :], mfd[:, :], writes=[m_b])
    S.dma("sp", m_t[:, 1, :], mbd[:, :], writes=[m_b])
    S.op("pool", lambda h: h.memset(hin_t[:, :, :], 0.0), writes=[hin_b])
    for d in range(2):
        order = range(ngrp) if d == 0 else range(ngrp - 1, -1, -1)
        for i in order:
            S.op("dve", lambda h, d=d, i=i: h.tensor_tensor(out=tmp_t[:, :], in0=hin_t[:, d, :], in1=cab_t[:, i, d * 8:d * 8 + 4], op=ALU.mult),
                 reads=[hin_b, cab_b, tmp_b], writes=[tmp_b])
            S.op("dve", lambda h, d=d, i=i: h.tensor_tensor(out=tmp_t[:, :], in0=tmp_t[:, :], in1=cab_t[:, i, d * 8 + 4:d * 8 + 8], op=ALU.add),
                 reads=[tmp_b, cab_b], writes=[tmp_b])
            S.op("dve", lambda h, d=d, i=i: h.tensor_tensor(out=tmp_t[:, :], in0=tmp_t[:, :], in1=hin_t[:, d, :], op=ALU.subtract),
                 reads=[tmp_b, hin_b], writes=[tmp_b])
            S.op("dve", lambda h, d=d, i=i: h.scalar_tensor_tensor(out=hin_t[:, d, :], in0=tmp_t[:, :], scalar=m_t[:, d, i:i + 1],
                                                                   in1=hin_t[:, d, :], op0=ALU.mult, op1=ALU.add),
                 reads=[tmp_b, m_b, hin_b], writes=[hin_b])
    for d in range(2):
        order = list(range(NB)) if d == 0 else list(range(NB - 1, -1, -1))
        S.op("dve", lambda h, d=d, b0=order[0]: h.tensor_copy(out=init_t[:, b0, d, :], in_=hin_t[:, d, :]),
             reads=[hin_b, init_b], writes=[init_b])
        for bi in range(NB - 1):
            b, bn = order[bi], order[bi + 1]
            S.op("dve", lambda h, d=d, b=b, bn=bn: h.tensor_tensor(out=init_t[:, bn, d, :], in0=init_t[:, b, d, :],
                                                                   in1=blk_t[:, b, d, 0, :], op=ALU.mult),
                 reads=[init_b, blk_b], writes=[init_b])
            S.op("dve", lambda h, d=d, b=b, bn=bn: h.tensor_tensor(out=init_t[:, bn, d, :], in0=init_t[:, bn, d, :],
                                                                   in1=blk_t[:, b, d, 1, :], op=ALU.add),
                 reads=[init_b, blk_b], writes=[init_b])
    xv = xT.rearrange("(c p) t -> p c t", p=128)
    ov = oT.rearrange("(c p) t -> p c t", p=128)
    ABv = AB.rearrange("d s (c p) t -> d s p c t", p=128)
    def blk(b):
        t0 = b * TB
        ab_t, ab_b = ab_ring.next()
        for d in range(2):
            for s_ in range(2):
                S.dma("sp", ab_t[:, d, s_, :, :], ABv[d, s_, :, :, t0:t0 + TB], writes=[ab_b])
        gx_t, gx_b = gx_ring.next()
        yc_t, yc_b = yc_ring.next()
        x_t, x_b = x_ring.next()
        S.dma("sp", gx_t[:, :, :], GX.rearrange("(c p) t -> p c t", p=128)[:, :, t0:t0 + TB], writes=[gx_b])
        S.dma("sp", yc_t[:, :, :], YC.rearrange("(i p) t -> p i t", p=128)[:, :, t0:t0 + TB], writes=[yc_b])
        S.dma("sp", x_t[:, :, :], xv[:, :, t0:t0 + TB], writes=[x_b])
        yield
        hs_t, hs_b = hs_ring.next()
        for d in range(2):
            for c in range(4):
                if d == 0:
                    S.op("dve", lambda h, d=d, c=c, b=b: h.tensor_tensor_scan(
                        out=hs_t[:, d, c, :], data0=ab_t[:, d, 0, c, :], data1=ab_t[:, d, 1, c, :],
                        initial=init_t[:, b, d, c:c + 1], op0=ALU.mult, op1=ALU.add), reads=[ab_b, init_b, hs_b], writes=[hs_b])
                else:
                    S.op("dve", lambda h, d=d, c=c, b=b: h.tensor_tensor_scan(
                        out=hs_t[:, d, c, ::-1], data0=ab_t[:, d, 0, c, ::-1], data1=ab_t[:, d, 1, c, ::-1],
                        initial=init_t[:, b, d, c:c + 1], op0=ALU.mult, op1=ALU.add), reads=[ab_b, init_b, hs_b], writes=[hs_b])
        yield
        S.op("pool", lambda h: h.tensor_tensor(out=hs_t[:, 0, :, :], in0=hs_t[:, 0, :, :], in1=hs_t[:, 1, :, :], op=ALU.add),
             reads=[hs_b], writes=[hs_b])
        yd_t, yd_b = yd_ring.next()
        S.op("pool", lambda h: h.tensor_tensor(out=yd_t[:, :, :], in0=hs_t[:, 0, :, :], in1=gx_t[:, :, :], op=ALU.mult),
             reads=[hs_b, gx_b], writes=[yd_b])
        yield
        for o in range(KC):
            y_t, y_b = y_ring.next()
            for hh in range(4):
                S.op("pe", lambda h, hh=hh, o=o: h.matmul(y_t[:, :], lhsT=woA[:, hh, o * 128:(o + 1) * 128], rhs=yc_t[:, hh, :],
                                                         start=(hh == 0), stop=False), reads=[woA_b, yc_b], writes=[y_b])
            for g in range(4):
                S.op("pe", lambda h, g=g, o=o: h.matmul(y_t[:, :], lhsT=woB[:, g, o * 128:(o + 1) * 128], rhs=yd_t[:, g, :],
                                                       start=False, stop=(g == 3)), reads=[woB_b, yd_b], writes=[y_b])
            S.op("dve", lambda h, o=o: h.tensor_tensor(out=x_t[:, o, :], in0=y_t[:, :], in1=x_t[:, o, :], op=ALU.add),
                 reads=[y_b, x_b], writes=[x_b])
        S.dma("pool", ov[:, :, t0:t0 + TB], x_t[:, :, :], reads=[x_b])
        yield

    run_interleaved((blk(b) for b in range(NB)), 2, 2)
    cx.pop()
    return cx.finish() if own else None


def allgather(cx, in_ap, out_ap, groups):
    S = cx.S
    S.barrier()
    sem = S.new_sem("cc")
    cx.nc.gpsimd.collective_compute("AllGather", ALU.bypass, replica_groups=groups, ins=[in_ap], outs=[out_ap]).then_inc(sem, 1)
    for e in S.ENGS:
        S.h[e].wait_ge(sem, 1)


def allgather_many(cx, pairs, groups):
    S = cx.S
    S.barrier()
    sem = S.new_sem("ccm")
    for (in_ap, out_ap) in pairs:
        cx.nc.gpsimd.collective_compute("AllGather", ALU.bypass, replica_groups=groups, ins=[in_ap], outs=[out_ap]).then_inc(sem, 1)
    for e in S.ENGS:
        S.h[e].wait_ge(sem, len(pairs))


def emit_select(cx, src_t, src_b, nrank, m_t, m_b, side, acc_t, acc_b):
    S = cx.S
    S.op("dve", lambda h: h.tensor_scalar_mul(out=acc_t[:, :], in0=src_t[:, 0, :], scalar1=m_t[:, side, 0:1]),
         reads=[src_b, m_b], writes=[acc_b])
    for i in range(1, nrank):
        S.op("dve", lambda h, i=i: h.scalar_tensor_tensor(out=acc_t[:, :], in0=src_t[:, i, :], scalar=m_t[:, side, i:i + 1],
                                                          in1=acc_t[:, :], op0=ALU.mult, op1=ALU.add),
             reads=[src_b, m_b, acc_b], writes=[acc_b])


def even_exchange_start(cx, KTh, Vh, UTh, pack, packg, groups, ntok):
    S = cx.S
    S.barrier()
    S.dma_dd_async("sp", pack[:, 0:128], KTh[:, 128:256])
    S.dma_dd_async("sp", pack[:, 128:256], KTh[:, ntok:ntok + 128])
    S.dma_dd_async("sp", pack[:, 256:288].rearrange("p (g t) -> p g t", g=4), UTh[:, :, 8:16])
    S.dma_dd_async("sp", pack[:, 288:320].rearrange("p (g t) -> p g t", g=4), UTh[:, :, ntok:ntok + 8])
    S.dma_dd_async("sp", pack[:, 320:450], Vh[128:256, :])
    S.dma_dd_async("sp", pack[:, 450:580], Vh[ntok:ntok + 128, :])
    S.barrier()
    sem = S.new_sem("cce")
    cx.nc.gpsimd.collective_compute("AllGather", ALU.bypass, replica_groups=groups, ins=[pack[:, :]], outs=[packg[:, :]]).then_inc(sem, 1)
    return sem


def even_exchange_finish(cx, sem, KTh, Vh, UTh, packg, mlr, nrank, ntok):
    S = cx.S
    for e in S.ENGS:
        S.h[e].wait_ge(sem, 1)
    cx.push()
    pg_t = cx.sb("pg", [128, nrank, 580], BF16); pg_b = Buf("pg")
    m_t = cx.sb("mlr", [128, 2, nrank], F32); m_b = Buf("mlr")
    accL = cx.sb("accL", [128, 580], BF16); accL_b = Buf("accL")
    accR = cx.sb("accR", [128, 580], BF16); accR_b = Buf("accR")
    S.dma("sp", pg_t[:, :, :], packg.rearrange("(r p) n -> p r n", p=128), writes=[pg_b])
    S.dma("sp", m_t[:, :, :], mlr[:, :, :], writes=[m_b])
    emit_select(cx, pg_t, pg_b, nrank, m_t, m_b, 0, accL, accL_b)
    emit_select(cx, pg_t, pg_b, nrank, m_t, m_b, 1, accR, accR_b)
    S.dma("sp", KTh[:, 0:128], accL[:, 128:256], reads=[accL_b])
    S.dma("sp", UTh[:, :, 0:8], accL[:, 288:320].rearrange("p (g t) -> p g t", g=4), reads=[accL_b])
    S.dma("sp", Vh[0:128, :], accL[:, 450:580], reads=[accL_b])
    S.dma("sp", KTh[:, 128 + ntok:256 + ntok], accR[:, 0:128], reads=[accR_b])
    S.dma("sp", UTh[:, :, 8 + ntok:16 + ntok], accR[:, 256:288].rearrange("p (g t) -> p g t", g=4), reads=[accR_b])
    S.dma("sp", Vh[128 + ntok:256 + ntok, :], accR[:, 320:450], reads=[accR_b])
    cx.pop()


def xhalo_exchange_start(cx, xprev, xhp, xhpg, groups, ntok):
    S = cx.S
    S.barrier()
    S.dma_dd_async("sp", xhp[:, 0:2], xprev[:, 0:2])
    S.dma_dd_async("sp", xhp[:, 2:4], xprev[:, ntok - 2:ntok])
    S.barrier()
    sem = S.new_sem("ccx")
    cx.nc.gpsimd.collective_compute("AllGather", ALU.bypass, replica_groups=groups, ins=[xhp[:, :]], outs=[xhpg[:, :]]).then_inc(sem, 1)
    return sem


def xhalo_exchange_finish(cx, sem, xhpg, xhalo, mlr, nrank):
    S = cx.S
    for e in S.ENGS:
        S.h[e].wait_ge(sem, 1)
    cx.push()
    xg_t = cx.sb("xg", [128, nrank, 32], F32); xg_b = Buf("xg")
    m_t = cx.sb("mlr", [128, 2, nrank], F32); m_b = Buf("mlr")
    accL = cx.sb("accL", [128, 32], F32); accL_b = Buf("accL")
    accR = cx.sb("accR", [128, 32], F32); accR_b = Buf("accR")
    for r in range(nrank):
        S.dma("sp", xg_t[:, r, :].rearrange("p (c t) -> p c t", t=4),
              xhpg[r * D:(r + 1) * D, :].rearrange("(c p) t -> p c t", p=128), writes=[xg_b])
    S.dma("sp", m_t[:, :, :], mlr[:, :, :], writes=[m_b])
    emit_select(cx, xg_t, xg_b, nrank, m_t, m_b, 0, accL, accL_b)
    emit_select(cx, xg_t, xg_b, nrank, m_t, m_b, 1, accR, accR_b)
    xhv = xhalo.rearrange("(c p) t -> p c t", p=128)
    S.dma("sp", xhv[:, :, 0:2], accL[:, :].rearrange("p (c t) -> p c t", t=4)[:, :, 2:4], reads=[accL_b])
    S.dma("sp", xhv[:, :, 2:4], accR[:, :].rearrange("p (c t) -> p c t", t=4)[:, :, 0:2], reads=[accR_b])
    cx.pop()


SMALL_SPECS = None


def build_fused(B=2, nrank=4, ntok=TOK, depth=4):
    NE, NO = (depth + 1) // 2, depth // 2
    NT = ntok // 128
    NB = ntok // 512
    groups = [[b * nrank + r for r in range(nrank)] for b in range(B)]
    cx = Ctx()
    nc = cx.nc
    I = cx.ext_in
    x0 = I("xT", [D, ntok])
    Wd = {
        "e_w_in": I("e_w_in", [NE, D, 1280]), "e_w_pool": I("e_w_pool", [NE, 4, 128, 128]), "e_w_out": I("e_w_out", [NE, D, D]),
        "o_w_in": I("o_w_in", [NO, D, 1440]), "o_w_uq": I("o_w_uq", [NO, 256, 768]), "o_w_ukv": I("o_w_ukv", [NO, 128, 1024]),
        "o_lru_wa": I("o_lru_wa", [NO, 2, 8, 64, 64]), "o_lru_wx": I("o_lru_wx", [NO, 2, 8, 64, 64]), "o_w_out": I("o_w_out", [NO, D, D]),
        "w_mlp1": I("w_mlp1", [depth, D, DFF]), "w_mlp2": I("w_mlp2", [depth, DFF, D]),
        "g_mix": I("g_mix", [depth, 128, KC]), "g_mlp": I("g_mlp", [depth, 128, KC]), "g_fin": I("g_fin", [128, KC]),
        "pscale": I("pscale", [NE, 128, 4]), "sinkrow": I("sinkrow", [NE, 1, 2, 512]),
        "g_cq": I("g_cq", [NO, 128, 2]), "g_ckv": I("g_ckv", [NO, 128, 1]), "cw": I("cw", [NO, 128, 4, 4]), "cb": I("cb", [NO, 128, 4]),
        "ba": I("ba", [NO, 128, 2, 4]), "bx": I("bx", [NO, 128, 2, 4]), "lam": I("lam", [NO, 128, 2, 4]),
        "cos32": I("cos32", [128, ntok]), "sin32": I("sin32", [128, ntok]), "cos16": I("cos16", [128, ntok]), "sin16": I("sin16", [128, ntok]),
        "rot64": I("rot64", [128, 128]), "rot32": I("rot32", [128, 128]), "masks": I("masks", [4, 128, 512]),
        "ident": I("ident", [128, 128]),
        "invc": I("invc", [128, 2, 4, 16]), "mfb": I("mfb", [2, 128, nrank]), "mlr": I("mlr", [128, 2, nrank]),
    }
    outT = cx.ext_out("oT", [D, ntok])

    def tmp(name, shape, dt=F32):
        return nc.dram_tensor(name, list(shape), dt, kind="Internal").ap()

    def make_precast(layer, w1b_d, w2b_d):
        def f():
            for k in range(8):
                cx.S.dma_dd_async("pool", w1b_d[k * 128:(k + 1) * 128, :], Wd["w_mlp1"][layer][k * 128:(k + 1) * 128, :])
            for k in range(8):
                cx.S.dma_dd_async("pool", w2b_d[k * 512:(k + 1) * 512, :], Wd["w_mlp2"][layer][k * 512:(k + 1) * 512, :])
        return f

    wb16 = {}
    for (nm, nl, shp) in (("e_w_in", NE, [D, 1280]), ("e_w_out", NE, [D, D]), ("o_w_in", NO, [D, 1440]), ("o_w_out", NO, [D, D])):
        for l in range(nl):
            if nm == "e_w_in" and l == 0:
                continue
            wb16[(nm, l)] = tmp(f"{nm}_bf{l}", shp, BF16)

    def cast_job(nm, l):
        def f():
            if (nm, l) in wb16:
                for k in range(4):
                    cx.S.dma_dd_async("pool", wb16[(nm, l)][k * 256:(k + 1) * 256, :], Wd[nm][l][k * 256:(k + 1) * 256, :])
        return f

    def wsrc(nm, l):
        return wb16.get((nm, l), Wd[nm][l])

    def wq(nm, l):
        return "sp" if (nm, l) in wb16 else "pool"

    xcur = x0
    xh_pending = None
    for layer in range(depth):
        L = f"L{layer}"
        xmix = tmp(L + "_xmix", [D, ntok])
        w1b_d = tmp(L + "_w1b", [D, DFF], BF16)
        w2b_d = tmp(L + "_w2b", [DFF, D], BF16)
        if layer % 2 == 0:
            e = layer // 2
            QsT = tmp(L + "_QsT", [128, 4, ntok], BF16)
            KTh = tmp(L + "_KTh", [128, ntok + 256], BF16)
            Vh = tmp(L + "_Vh", [ntok + 256, 130], BF16)
            UTh = tmp(L + "_UTh", [128, 4, ntok + 16], BF16)
            pack = tmp(L + "_pack", [128, 580], BF16)
            packg = tmp(L + "_packg", [nrank * 128, 580], BF16)
            cx.bind = {"xT": xcur, "w_in": wsrc("e_w_in", e), "_wq": wq("e_w_in", e), "g": Wd["g_mix"][layer], "cos": Wd["cos32"], "sin": Wd["sin32"],
                       "rot": Wd["rot64"], "QsT": QsT, "KT": KTh[:, 128:128 + ntok], "Vaug": Vh[128:128 + ntok, :],
                       "UT": UTh[:, :, 8:8 + ntok]}
            exs = {}

            def ea_mid(KTh=KTh, Vh=Vh, UTh=UTh, pack=pack, packg=packg, exs=exs):
                exs["sem"] = even_exchange_start(cx, KTh, Vh, UTh, pack, packg, groups, ntok)

            cx.hook = cast_job("e_w_out", e)
            build_ea(ntok, cx=cx, order=[0, NB - 1] + list(range(1, NB - 1)) if NB > 1 else [0], mid_hook=ea_mid)
            even_exchange_finish(cx, exs["sem"], KTh, Vh, UTh, packg, Wd["mlr"], nrank, ntok)
            cx.bind = {"QsT": QsT, "KTh": KTh, "Vh": Vh, "UTh": UTh, "xT": xcur, "w_pool": Wd["e_w_pool"][e],
                       "pscale": Wd["pscale"][e], "w_out": wsrc("e_w_out", e), "_wq": wq("e_w_out", e), "sinkrow": Wd["sinkrow"][e], "masks": Wd["masks"],
                       "invc": Wd["invc"], "ident": Wd["ident"], "oT": xmix}
            cx.hook = make_precast(layer, w1b_d, w2b_d)
            build_eb(ntok, cx=cx)
        else:
            o = layer // 2
            if xh_pending is None:
                xhp = tmp(L + "_xhp", [D, 4]); xhpg = tmp(L + "_xhpg", [nrank * D, 4])
            else:
                xhp, xhpg = xh_pending["xhp"], xh_pending["xhpg"]
            xhalo = tmp(L + "_xhalo", [D, 4])
            QN = tmp(L + "_QN", [512, ntok], BF16); QR = tmp(L + "_QR", [256, ntok], BF16)
            KNR = tmp(L + "_KNR", [544, ntok], BF16); V5 = tmp(L + "_V5", [1024, NT * 65], BF16)
            GX = tmp(L + "_GX", [512, ntok], BF16); AB = tmp(L + "_AB", [2, 2, 512, ntok])
            BLK = tmp(L + "_BLK", [128, NB, 2, 2, 4]); CAB = tmp(L + "_CAB", [128, 16]); XR = tmp(L + "_XR", [512, ntok])
            KNg = tmp(L + "_KNg", [8 * nrank * 64, ntok], BF16); KRg = tmp(L + "_KRg", [nrank * 32, ntok], BF16)
            Vg = tmp(L + "_Vg", [8 * nrank * 128, NT * 65], BF16)
            CABg = tmp(L + "_CABg", [nrank * 128, 16]); YC = tmp(L + "_YC", [512, ntok], BF16)
            if xh_pending is None:
                xh_sem = xhalo_exchange_start(cx, xcur, xhp, xhpg, groups, ntok)
            else:
                xh_sem = xh_pending["sem"]
            xhalo_exchange_finish(cx, xh_sem, xhpg, xhalo, Wd["mlr"], nrank)
            cx.bind = {"xT": xcur, "xhalo": xhalo, "w_in": wsrc("o_w_in", o), "_wq": wq("o_w_in", o), "g": Wd["g_mix"][layer], "g_cq": Wd["g_cq"][o],
                       "g_ckv": Wd["g_ckv"][o], "w_uq": Wd["o_w_uq"][o], "w_ukv": Wd["o_w_ukv"][o], "cw": Wd["cw"][o], "cb": Wd["cb"][o],
                       "wa": Wd["o_lru_wa"][o], "wx": Wd["o_lru_wx"][o], "ba": Wd["ba"][o], "bx": Wd["bx"][o], "lam": Wd["lam"][o],
                       "cos": Wd["cos16"], "sin": Wd["sin16"], "rot": Wd["rot32"], "QN": QN, "QR": QR, "KNR": KNR, "V5": V5, "GX": GX,
                       "AB": AB, "BLK": BLK, "CAB": CAB.rearrange("p (d s c) -> p d s c", d=2, s=2), "XR": XR}
            ccsem = cx.S.new_sem("ccg")
            ncc = [0]

            def gather_kv():
                pairs = []
                for i in range(4):
                    pairs.append((KNR[i * 128:(i + 1) * 128, :], KNg[i * nrank * 128:(i + 1) * nrank * 128, :]))
                pairs.append((KNR[512:544, :], KRg[:, :]))
                for hd in range(8):
                    pairs.append((V5[hd * 128:(hd + 1) * 128, :], Vg[hd * nrank * 128:(hd + 1) * nrank * 128, :]))
                for (i_ap, o_ap) in pairs:
                    nc.gpsimd.collective_compute("AllGather", ALU.bypass, replica_groups=groups, ins=[i_ap], outs=[o_ap]).then_inc(ccsem, 1)
                    ncc[0] += 1

            build_oa(ntok, cx=cx, mid_hook=gather_kv)
            cabsem = cx.S.new_sem("cab")
            nc.gpsimd.collective_compute("AllGather", ALU.bypass, replica_groups=groups, ins=[CAB[:, :]], outs=[CABg[:, :]]).then_inc(cabsem, 1)
            for e_ in cx.S.ENGS:
                cx.S.h[e_].wait_ge(ccsem, ncc[0])
            cx.bind = {"QN": QN, "QR": QR, "KNg": KNg, "KRg": KRg, "Vg": Vg, "YC": YC}
            cx.hook = [make_precast(layer, w1b_d, w2b_d), cast_job("o_w_out", o)]
            build_ob1(ntok, nrank, cx=cx)
            for e_ in cx.S.ENGS:
                cx.S.h[e_].wait_ge(cabsem, 1)
            cx.bind = {"AB": AB, "GX": GX, "YC": YC, "xT": xcur, "w_out": wsrc("o_w_out", o), "_wq": wq("o_w_out", o), "BLK": BLK,
                       "CABg": CABg.rearrange("(r p) n -> p r n", p=128), "mf": Wd["mfb"][0], "mb": Wd["mfb"][1], "oT": xmix}
            build_ob2(ntok, nrank, cx=cx)
        last = layer == depth - 1
        xnext = outT if last else tmp(L + "_xmlp", [D, ntok])
        cx.bind = {"xT": xmix, "w1": w1b_d, "w2": w2b_d, "g": Wd["g_mlp"][layer], "gf": Wd["g_fin"], "oT": xnext}
        NBm = ntok // 256
        if layer + 1 < depth:
            cx.hook = cast_job("o_w_in", (layer + 1) // 2) if (layer + 1) % 2 == 1 else cast_job("e_w_in", (layer + 1) // 2)
        if (layer + 1) < depth and (layer + 1) % 2 == 1 and NBm > 2:
            Ln = f"L{layer + 1}"
            xh_pending = {"xhp": tmp(Ln + "_xhp", [D, 4]), "xhpg": tmp(Ln + "_xhpg", [nrank * D, 4])}

            def mlp_mid(xh_pending=xh_pending, xnext=xnext):
                xh_pending["sem"] = xhalo_exchange_start(cx, xnext, xh_pending["xhp"], xh_pending["xhpg"], groups, ntok)

            build_mlp(last, ntok, cx=cx, wbf16=True, order=[0, NBm - 1] + list(range(1, NBm - 1)), mid_hook=mlp_mid)
        else:
            xh_pending = None
            build_mlp(last, ntok, cx=cx, wbf16=True)
        xcur = xnext
    cx.bind = {}
    return cx.finish()


_FUSED = {}


def run_model(x, W, nrank=4, ntok=TOK):
    B, Sq, _ = x.shape
    ncore = B * nrank
    assert Sq == nrank * ntok
    depth = W["norm_mlp"].shape[0]
    NE, NO = (depth + 1) // 2, depth // 2
    key = (B, nrank, ntok, depth)
    if key not in _FUSED:
        _FUSED[key] = build_fused(B, nrank, ntok, depth)
    nc = _FUSED[key]
    f32 = lambda a: np.ascontiguousarray(np.asarray(a, np.float32))
    g_mix = np.stack([vec128(W["e_norm_mix"][l // 2] if l % 2 == 0 else W["o_norm_mix"][l // 2], 8) for l in range(depth)])
    shared = {
        "e_w_in": f32(W["e_w_in"]), "e_w_pool": f32(W["e_w_pool"]), "e_w_out": f32(W["e_w_out"]),
        "o_w_in": f32(W["o_w_in"]), "o_w_uq": f32(W["o_w_uq"]), "o_w_ukv": f32(W["o_w_ukv"]),
        "o_lru_wa": f32(W["o_lru_wa"]), "o_lru_wx": f32(W["o_lru_wx"]), "o_w_out": f32(W["o_w_out"]),
        "w_mlp1": f32(W["w_mlp1"]), "w_mlp2": f32(W["w_mlp2"]),
        "g_mix": g_mix, "g_mlp": np.stack([vec128(W["norm_mlp"][l], 8) for l in range(depth)]), "g_fin": vec128(W["final_norm"], 8),
        "pscale": np.stack([vec128(W["e_pool_scale"][e], 4) for e in range(NE)]),
        "sinkrow": np.stack([np.repeat(f32(W["e_sink"][e]).reshape(2, 4), 128, axis=1).reshape(1, 2, 512) for e in range(NE)]),
        "g_cq": np.stack([vec128(W["o_g_cq"][o], 2) for o in range(NO)]),
        "g_ckv": np.stack([vec128(W["o_g_ckv"][o], 1) for o in range(NO)]),
        "cw": np.stack([f32(f32(W["o_conv_w"][o]).reshape(4, 4, 128).transpose(2, 1, 0)) for o in range(NO)]),
        "cb": np.stack([chunk_vec(W["o_conv_b"][o], 4) for o in range(NO)]),
        "ba": np.stack([f32(f32(W["o_lru_ba"][o]).reshape(2, 4, 128).transpose(2, 0, 1)) for o in range(NO)]),
        "bx": np.stack([f32(f32(W["o_lru_bx"][o]).reshape(2, 4, 128).transpose(2, 0, 1)) for o in range(NO)]),
        "lam": np.stack([f32(f32(W["o_lru_lambda"][o]).reshape(2, 4, 128).transpose(2, 0, 1)) for o in range(NO)]),
        "rot64": rot_matrix(64), "rot32": rot_matrix(32), "ident": np.eye(128, dtype=np.float32),
    }
    in_maps = []
    for c in range(ncore):
        bi, r = c // nrank, c % nrank
        pos = r * ntok + np.arange(ntok)
        cos32, sin32 = rope_tables(pos, 32, 128)
        cos16, sin16 = rope_tables(pos, 16, 128)
        mfb = np.zeros((2, 128, nrank), np.float32); mfb[0, :, :r] = 1.0; mfb[1, :, r + 1:] = 1.0
        mlr = np.zeros((128, 2, nrank), np.float32)
        if r > 0:
            mlr[:, 0, r - 1] = 1.0
        if r < nrank - 1:
            mlr[:, 1, r + 1] = 1.0
        im = dict(shared)
        im.update({"xT": np.ascontiguousarray(x[bi, r * ntok:(r + 1) * ntok, :].T), "cos32": cos32, "sin32": sin32, "cos16": cos16,
                   "sin16": sin16, "masks": eb_masks(r > 0, r < nrank - 1), "invc": eb_invc(r == 0, r == nrank - 1),
                   "mfb": mfb, "mlr": mlr})
        in_maps.append(im)
    res = run_spmd(nc, in_maps)
    out = np.empty((B, Sq, D), np.float32)
    for c in range(ncore):
        bi, r = c // nrank, c % nrank
        out[bi, r * ntok:(r + 1) * ntok, :] = res[c]["oT"].T
    return out


def kernel(**inputs):
    W = {k: np.asarray(v) for k, v in inputs.items()}
    x = np.asarray(W.pop("x"), np.float32)
    return run_model(x, W)
```

```python
from contextlib import ExitStack
import numpy as np
import concourse.bass as bass
import concourse.mybir as mybir
from concourse.bass_utils import run_bass_kernel_spmd

F32 = mybir.dt.float32
BF16 = mybir.dt.bfloat16
ALU = mybir.AluOpType
AF = mybir.ActivationFunctionType

NCORES = 8
D = 1024
KC = 8
TOK = 4096
SEQ = 16384
EPS = 1e-6
DFF = 4096
EPOCH = 30000


class Buf:
    __slots__ = ("name", "writers", "readers", "sem_in", "sem_out", "n_in", "n_out", "excl")

    def __init__(self, name, excl=False):
        self.name = name
        self.excl = excl
        self.writers = {}
        self.readers = {}
        self.sem_in = None
        self.sem_out = None
        self.n_in = 0
        self.n_out = 0


class Sched:
    ENGS = ("pe", "act", "dve", "pool", "sp")

    def __init__(self, nc, stack):
        self.nc = nc
        self.stack = stack
        self.h = {"pe": nc.tensor, "act": nc.scalar, "dve": nc.vector, "pool": nc.gpsimd, "sp": nc.sync}
        self.ops = {e: [] for e in self.ENGS}
        self.cnt = {e: 0 for e in self.ENGS}
        self.sem = {e: None for e in self.ENGS}
        self.seen = {e: {} for e in self.ENGS}
        self.last = {e: None for e in self.ENGS}
        self.dma_toks = {}
        self.nsem = 0
        self.ninstr = 0
        self.sem_pool = []
        self.live = []
        self.ddbuf = Buf("dram2dram")

    def new_sem(self, name):
        self.nsem += 1
        return self.stack.enter_context(self.nc.semaphore(f"{name}_{self.nsem}"))

    def _eng_tok(self, e):
        if self.sem[e] is None or self.cnt[e] >= EPOCH:
            self.sem[e] = self.new_sem("e" + e)
            self.cnt[e] = 0
        self.cnt[e] += 1
        tok = (self.sem[e], self.cnt[e])
        self.last[e] = tok
        return tok

    def _waits(self, e, toks):
        need = {}
        seen = self.seen[e]
        for sem, val in toks:
            k = id(sem)
            if seen.get(k, 0) >= val:
                continue
            if k not in need or need[k][1] < val:
                need[k] = (sem, val)
        out = []
        for k, (sem, val) in need.items():
            seen[k] = val
            out.append((sem, val))
        return out

    def _deps(self, e, reads, writes):
        toks = []
        for b in reads:
            toks.extend(b.writers.values())
            if b.excl:
                toks.extend(b.readers.values())
        for b in writes:
            toks.extend(b.writers.values())
            toks.extend(b.readers.values())
        if e == "pe":
            own = id(self.sem["pe"]) if self.sem["pe"] is not None else None
            toks = [t for t in toks if id(t[0]) != own]
        return self._waits(e, toks)

    def op(self, e, fn, reads=(), writes=()):
        waits = self._deps(e, reads, writes)
        tok = self._eng_tok(e)
        for b in reads:
            b.readers[id(tok[0])] = tok
        for b in writes:
            b.readers = {}
            b.writers = {id(tok[0]): tok}
        self.ninstr += 1

        h = self.h[e]
        for sem, val in waits:
            h.wait_ge(sem, val)
        fn(h).then_inc(tok[0], 1)

    def dma(self, q, out_ap, in_ap, reads=(), writes=(), **kw):
        waits = self._deps(q, reads, writes)
        assert len(writes) + len(reads) >= 1 and len(writes) <= 1 and len(reads) <= 1
        if writes:
            b = writes[0]
            if b.sem_in is None:
                b.sem_in, b.n_in = self._take_sem("di")
                self.live.append((b, "in"))
            b.n_in += 16
            tok = (b.sem_in, b.n_in)
            b.readers = {}
            b.writers = {id(tok[0]): tok}
            for rb in reads:
                rb.readers[id(tok[0])] = tok
        else:
            b = reads[0]
            if b.sem_out is None:
                b.sem_out, b.n_out = self._take_sem("do")
                self.live.append((b, "out"))
            b.n_out += 16
            tok = (b.sem_out, b.n_out)
            b.readers[id(tok[0])] = tok
        self.dma_toks[id(tok[0])] = tok
        self.ninstr += 1
        h = self.h[q]
        for sem, val in waits:
            h.wait_ge(sem, val)
        h.dma_start(out=out_ap, in_=in_ap, **kw).then_inc(tok[0], 16)

    def _take_sem(self, name):
        if self.sem_pool:
            return self.sem_pool.pop()
        return self.new_sem(name), 0

    def release_dma_sems(self):
        for b, kind in self.live:
            if kind == "in":
                self.sem_pool.append((b.sem_in, b.n_in)); b.sem_in = None
                b.writers = {}
            else:
                self.sem_pool.append((b.sem_out, b.n_out)); b.sem_out = None
                b.readers = {}
        self.live = []

    def dma_dd(self, q, out_ap, in_ap, **kw):
        self.dma(q, out_ap, in_ap, writes=[self.ddbuf], **kw)

    def dma_dd_async(self, q, out_ap, in_ap, **kw):
        self.dma(q, out_ap, in_ap, writes=[Buf("dd_async")], **kw)

    def barrier(self):
        toks = [t for t in self.last.values() if t is not None] + list(self.dma_toks.values())
        for e in self.ENGS:
            waits = self._waits(e, toks)
            for sem, val in waits:
                self.h[e].wait_ge(sem, val)

    def finalize(self):
        self.barrier()


class Ctx:
    def __init__(self):
        self.nc = bass.Bass("TRN2", target_bir_lowering=False)
        self.stack = ExitStack()
        self.S = Sched(self.nc, self.stack)
        self.n = 0
        self.cur = self.stack
        self.scopes = []
        self.bind = {}
        self.hook = None
        self.jobs = []

    def dram_in(self, name, shape, dt=F32):
        if name in self.bind:
            return self.bind[name]
        return self.nc.dram_tensor(name, list(shape), dt, kind="ExternalInput").ap()

    def dram_out(self, name, shape, dt=F32):
        if name in self.bind:
            return self.bind[name]
        return self.nc.dram_tensor(name, list(shape), dt, kind="ExternalOutput").ap()

    def ext_in(self, name, shape, dt=F32):
        return self.nc.dram_tensor(name, list(shape), dt, kind="ExternalInput").ap()

    def ext_out(self, name, shape, dt=F32):
        return self.nc.dram_tensor(name, list(shape), dt, kind="ExternalOutput").ap()

    def sb(self, name, shape, dt):
        self.n += 1
        return self.cur.enter_context(self.nc.sbuf_tensor(f"{name}_{self.n}", list(shape), dt))

    def ps(self, name, shape, dt=F32):
        self.n += 1
        return self.cur.enter_context(self.nc.psum_tensor(f"{name}_{self.n}", list(shape), dt))

    def dram_tmp(self, name, shape, dt=F32):
        return self.nc.dram_tensor(name, list(shape), dt, kind="Internal").ap()

    def take_jobs(self):
        js, self.jobs = self.jobs, []
        return js

    def run_hook(self):
        if self.hook is not None:
            fs, self.hook = self.hook, None
            for f in (fs if isinstance(fs, (list, tuple)) else [fs]):
                f()

    def push(self):
        st = ExitStack()
        self.scopes.append(st)
        self.cur = st

    def pop(self):
        self.S.barrier()
        if len(self.scopes) == 1:
            self.S.release_dma_sems()
        self.scopes.pop().close()
        self.cur = self.scopes[-1] if self.scopes else self.stack

    def finish(self):
        self.S.finalize()
        self.stack.close()
        return self.nc


class Ring:
    def __init__(self, items):
        self.items = items
        self.i = 0

    def next(self):
        it = self.items[self.i % len(self.items)]
        self.i += 1
        return it


def run_interleaved(gens, width=2, stagger=2):
    it = iter(gens)
    active = []
    steps = 0
    while True:
        while len(active) < width and (not active or steps >= stagger):
            try:
                active.append(next(it))
            except StopIteration:
                break
        if not active:
            break
        steps += 1
        for g in list(active):
            try:
                next(g)
            except StopIteration:
                active.remove(g)


def mk_ring(cx, kind, name, n, shape, dt):
    items = []
    for i in range(n):
        t = cx.sb(f"{name}{i}", shape, dt) if kind == "sb" else cx.ps(f"{name}{i}", shape, dt)
        items.append((t, Buf(f"{name}{i}", excl=(kind == "ps"))))
    return Ring(items)


def emit_rmsnorm(cx, x_t, x_b, nchunk, TB, g_t, g_b, ones_t, ones_b, sq_ring, st_ring, rstd_ring,
                 out_t, out_b, nfeat, evac_engs=("dve",)):
    S = cx.S
    st_t, st_b = st_ring.next()
    for c in range(nchunk):
        sq_t, sq_b = sq_ring.next()
        S.op("act", lambda h, c=c, sq_t=sq_t: h.activation(out=sq_t[:, 0:TB], in_=x_t[:, c, 0:TB], func=AF.Square),
             reads=[x_b], writes=[sq_b])
        S.op("pe", lambda h, c=c, sq_t=sq_t: h.matmul(st_t[:, 0:TB], lhsT=ones_t[:, :], rhs=sq_t[:, 0:TB],
                                                        start=(c == 0), stop=(c == nchunk - 1)),
             reads=[sq_b, ones_b], writes=[st_b])
    r_t, r_b = rstd_ring.next()
    S.op("act", lambda h: h.activation(out=r_t[:, 0:TB], in_=st_t[:, 0:TB], func=AF.Sqrt, bias=float(nfeat * EPS)),
         reads=[st_b], writes=[r_b])
    S.op("dve", lambda h: h.reciprocal(out=r_t[:, 0:TB], in_=r_t[:, 0:TB]), reads=[r_b], writes=[r_b])
    for c in range(nchunk):
        e = evac_engs[c % len(evac_engs)]
        S.op(e, lambda h, c=c: h.scalar_tensor_tensor(out=out_t[:, c, 0:TB], in0=x_t[:, c, 0:TB],
                                                       scalar=g_t[:, c:c + 1], in1=r_t[:, 0:TB],
                                                       op0=ALU.mult, op1=ALU.mult),
             reads=[x_b, r_b, g_b], writes=[out_b])


def build_mlp(final_norm, ntok=TOK, dbg=False, cx=None, wbf16=False, order=None, mid_hook=None):
    TB = 256
    NB = ntok // TB
    FC = DFF // 128
    own = cx is None
    cx = Ctx() if own else cx
    cx.push()
    S = cx.S
    xT = cx.dram_in("xT", [D, ntok])
    w1 = cx.dram_in("w1", [D, DFF], BF16 if wbf16 else F32)
    w2 = cx.dram_in("w2", [DFF, D], BF16 if wbf16 else F32)
    gin = cx.dram_in("g", [128, KC])
    oT = cx.dram_out("oT", [D, ntok])
    if final_norm:
        gfin = cx.dram_in("gf", [128, KC])
    if dbg:
        dh = cx.dram_out("dh", [128, KC, TB], BF16)
        da = cx.dram_out("da", [128, DFF // 128, TB], BF16)

    w1b = cx.sb("w1b", [128, KC, DFF], BF16)
    w2b = cx.sb("w2b", [128, FC, D], BF16)
    w1_bufs = [Buf(f"w1_{k}") for k in range(KC)]
    w2_bufs = [Buf(f"w2_{k}") for k in range(8)]
    g_t = cx.sb("g", [128, KC], F32); g_b = Buf("g")
    ones_t = cx.sb("ones", [128, 128], BF16); ones_b = Buf("ones")
    x_ring = mk_ring(cx, "sb", "x", 2, [128, KC, TB], F32)
    h_ring = mk_ring(cx, "sb", "h", 2, [128, KC, TB], BF16)
    a_ring = mk_ring(cx, "sb", "a", 1, [128, FC, TB], BF16)
    r_ring = mk_ring(cx, "sb", "r", 3, [128, TB], BF16)
    sq_ring = mk_ring(cx, "sb", "sq", 3, [128, TB], BF16)
    rstd_ring = mk_ring(cx, "sb", "rstd", 2, [128, TB], F32)
    o_ring = mk_ring(cx, "sb", "o", 2, [128, KC, TB], F32)
    st_ring = mk_ring(cx, "ps", "st", 1, [128, 512], F32)
    p1_ring = mk_ring(cx, "ps", "p1", 3, [128, 512], F32)
    p2_ring = mk_ring(cx, "ps", "p2", 3, [128, 512], F32)
    if final_norm:
        gf_t = cx.sb("gf", [128, KC], F32); gf_b = Buf("gf")
        f_ring = mk_ring(cx, "sb", "f", 2, [128, KC, TB], F32)

    S.dma("sp", g_t[:, :], gin[:, :], writes=[g_b])
    S.op("dve", lambda h: h.tensor_scalar_mul(out=g_t[:, :], in0=g_t[:, :], scalar1=float(np.sqrt(D))),
         reads=[g_b], writes=[g_b])
    if final_norm:
        S.dma("sp", gf_t[:, :], gfin[:, :], writes=[gf_b])
        S.op("dve", lambda h: h.tensor_scalar_mul(out=gf_t[:, :], in0=gf_t[:, :], scalar1=float(np.sqrt(D))),
             reads=[gf_b], writes=[gf_b])
    S.op("pool", lambda h: h.memset(ones_t[:, :], 1.0), writes=[ones_b])
    w1v = w1.rearrange("(k p) n -> p k n", p=128)
    w2v = w2.rearrange("(f p) n -> p f n", p=128)
    wq = "sp" if wbf16 else "pool"
    for k in range(KC):
        S.dma(wq, w1b[:, k, :], w1v[:, k, :], writes=[w1_bufs[k]])
    for j in range(8):
        S.dma(wq, w2b[:, j * 4:(j + 1) * 4, :], w2v[:, j * 4:(j + 1) * 4, :], writes=[w2_bufs[j]])
    xv = xT.rearrange("(c p) t -> p c t", p=128)
    ov = oT.rearrange("(c p) t -> p c t", p=128)
    cx.run_hook()

    def prep(b):
        x_t, x_b = x_ring.next()
        S.dma("sp", x_t[:, :, :], xv[:, :, b * TB:(b + 1) * TB], writes=[x_b])
        h_t, h_b = h_ring.next()
        return (x_t, x_b, h_t, h_b)

    def norm(st_):
        x_t, x_b, h_t, h_b = st_
        emit_rmsnorm(cx, x_t, x_b, KC, TB, g_t, g_b, ones_t, ones_b, sq_ring, st_ring, rstd_ring, h_t, h_b, D)

    order = list(range(NB)) if order is None else order
    cur = prep(order[0])
    norm(cur)
    for bi, b in enumerate(order):
        t0 = b * TB
        x_t, x_b, h_t, h_b = cur
        nxt = prep(order[bi + 1]) if bi + 1 < NB else None
        a_t, a_b = a_ring.next()
        for f in range(FC):
            if f == FC // 2 and nxt is not None:
                norm(nxt)
            p_t, p_b = p1_ring.next()
            for k in range(KC):
                S.op("pe", lambda h, f=f, k=k, p_t=p_t: h.matmul(p_t[:, 0:TB], lhsT=w1b[:, k, f * 128:(f + 1) * 128],
                                                                  rhs=h_t[:, k, 0:TB], start=(k == 0), stop=(k == KC - 1)),
                     reads=[h_b, w1_bufs[k]], writes=[p_b])
            r_t, r_b = r_ring.next()
            S.op("act", lambda h, p_t=p_t, r_t=r_t: h.activation(out=r_t[:, 0:TB], in_=p_t[:, 0:TB], func=AF.Relu),
                 reads=[p_b], writes=[r_b])
            S.op("pool", lambda h, f=f, r_t=r_t: h.tensor_tensor(out=a_t[:, f, 0:TB], in0=r_t[:, 0:TB], in1=r_t[:, 0:TB],
                                                                  op=ALU.mult),
                 reads=[r_b], writes=[a_b])
        if dbg and b == 0:
            S.dma("sp", dh[:, :, :], h_t[:, :, :], reads=[h_b])
            S.dma("sp", da[:, :, :], a_t[:, :, :], reads=[a_b])
        o_t, o_b = o_ring.next()
        for c in range(KC):
            p_t, p_b = p2_ring.next()
            for f in range(FC):
                S.op("pe", lambda h, f=f, c=c, p_t=p_t: h.matmul(p_t[:, 0:TB], lhsT=w2b[:, f, c * 128:(c + 1) * 128],
                                                                  rhs=a_t[:, f, 0:TB], start=(f == 0), stop=(f == FC - 1)),
                     reads=[a_b, w2_bufs[f // 4]], writes=[p_b])
            S.op("dve", lambda h, c=c, p_t=p_t: h.tensor_tensor(out=o_t[:, c, 0:TB], in0=p_t[:, 0:TB], in1=x_t[:, c, 0:TB],
                                                                 op=ALU.add),
                 reads=[p_b, x_b], writes=[o_b])
        if final_norm:
            f_t, f_b = f_ring.next()
            emit_rmsnorm(cx, o_t, o_b, KC, TB, gf_t, gf_b, ones_t, ones_b, sq_ring, st_ring, rstd_ring, f_t, f_b, D)
            S.dma("pool", ov[:, :, t0:t0 + TB], f_t[:, :, :], reads=[f_b])
        else:
            S.dma("pool", ov[:, :, t0:t0 + TB], o_t[:, :, :], reads=[o_b])
        cur = nxt
        if mid_hook is not None and bi == 1:
            mid_hook()
    cx.pop()
    return cx.finish() if own else None


def run_spmd(nc, in_maps):
    res = run_bass_kernel_spmd(nc, in_maps, core_ids=list(range(len(in_maps))))
    return res.results


def vec128(v, k):
    return np.ascontiguousarray(np.asarray(v, np.float32).reshape(k, 128).T)


def load_cast(cx, q, dst_ap, src_ap, buf):
    cx.S.dma(q, dst_ap, src_ap, writes=[buf])


def build_ea(ntok=TOK, parts='quv', qlvl=4, cx=None, order=None, mid_hook=None):
    TB = 512
    NB = ntok // TB
    own = cx is None
    cx = Ctx() if own else cx
    cx.push()
    S = cx.S
    xT = cx.dram_in("xT", [D, ntok])
    w_in = cx.dram_in("w_in", [D, 1280])
    gin = cx.dram_in("g", [128, KC])
    cosd = cx.dram_in("cos", [128, ntok])
    sind = cx.dram_in("sin", [128, ntok])
    rotd = cx.dram_in("rot", [128, 128])
    QsT = cx.dram_out("QsT", [128, 4, ntok], BF16)
    KT = cx.dram_out("KT", [128, ntok], BF16)
    Vaug = cx.dram_out("Vaug", [ntok, 130], BF16)
    UT = cx.dram_out("UT", [128, 4, ntok], BF16)

    wb = cx.sb("wb", [128, KC, 1280], BF16)
    w_bufs = [Buf(f"w{k}") for k in range(KC)]
    g_t = cx.sb("g", [128, KC], F32); g_b = Buf("g")
    ones_t = cx.sb("ones", [128, 128], BF16); ones_b = Buf("ones")
    rot_t = cx.sb("rot", [128, 128], BF16); rot_b = Buf("rot")
    x_ring = mk_ring(cx, "sb", "x", 2, [128, KC, TB], F32)
    h_ring = mk_ring(cx, "sb", "h", 2, [128, KC, TB], BF16)
    sq_ring = mk_ring(cx, "sb", "sq", 3, [128, TB], BF16)
    rstd_ring = mk_ring(cx, "sb", "rstd", 2, [128, TB], F32)
    cos_ring = mk_ring(cx, "sb", "cos", 2, [128, TB], F32)
    sin_ring = mk_ring(cx, "sb", "sin", 2, [128, TB], F32)
    qb_ring = mk_ring(cx, "sb", "qb", 2, [128, TB], BF16)
    t1_ring = mk_ring(cx, "sb", "t1", 2, [128, TB], F32)
    t2_ring = mk_ring(cx, "sb", "t2", 2, [128, TB], F32)
    qo_ring = mk_ring(cx, "sb", "qo", 2, [128, 5, TB], BF16)
    uo_ring = mk_ring(cx, "sb", "uo", 2, [128, 4, TB], BF16)
    vo_ring = mk_ring(cx, "sb", "vo", 2, [128, 4, 130], BF16)
    st_ring = mk_ring(cx, "ps", "st", 1, [128, 512], F32)
    pq_ring = mk_ring(cx, "ps", "pq", 3, [128, 512], F32)
    pr_ring = mk_ring(cx, "ps", "pr", 2, [128, 512], F32)
    pv_ring = mk_ring(cx, "ps", "pv", 2, [128, 512], F32)

    S.dma("sp", g_t[:, :], gin[:, :], writes=[g_b])
    S.op("dve", lambda h: h.tensor_scalar_mul(out=g_t[:, :], in0=g_t[:, :], scalar1=float(np.sqrt(D))),
         reads=[g_b], writes=[g_b])
    S.op("pool", lambda h: h.memset(ones_t[:, :], 1.0), writes=[ones_b])
    S.dma("pool", rot_t[:, :], rotd[:, :], writes=[rot_b])
    for (vt, vb) in vo_ring.items:
        S.op("pool", lambda h, vt=vt: h.memset(vt[:, :, :], 1.0), writes=[vb])
    for k in range(KC):
        for j in range(2):
            src = w_in[k * 128:(k + 1) * 128, j * 256:(j + 1) * 256].rearrange("p (c d) -> p c d", c=4, d=64)
            dst = wb[:, k, 0:512].rearrange("p (c j d) -> p c j d", c=4, j=2, d=64)[:, :, j, :]
            S.dma(cx.bind.get("_wq", "pool"), dst, src, writes=[w_bufs[k]])
        S.dma(cx.bind.get("_wq", "pool"), wb[:, k, 512:1280], w_in[k * 128:(k + 1) * 128, 512:1280], writes=[w_bufs[k]])
    xv = xT.rearrange("(c p) t -> p c t", p=128)
    cx.run_hook()

    def blk(b):
        t0 = b * TB
        x_t, x_b = x_ring.next()
        S.dma("sp", x_t[:, :, :], xv[:, :, t0:t0 + TB], writes=[x_b])
        cos_t, cos_b = cos_ring.next()
        sin_t, sin_b = sin_ring.next()
        S.dma("sp", cos_t[:, :], cosd[:, t0:t0 + TB], writes=[cos_b])
        S.dma("sp", sin_t[:, :], sind[:, t0:t0 + TB], writes=[sin_b])
        h_t, h_b = h_ring.next()
        emit_rmsnorm(cx, x_t, x_b, KC, TB, g_t, g_b, ones_t, ones_b, sq_ring, st_ring, rstd_ring, h_t, h_b, D)
        yield
        qo_t, qo_b = qo_ring.next()
        for c in (range(5) if 'q' in parts else []):
            pq_t, pq_b = pq_ring.next()
            for k in range(KC):
                S.op("pe", lambda h, c=c, k=k, pq_t=pq_t, h_t=h_t: h.matmul(
                    pq_t[:, 0:TB], lhsT=wb[:, k, c * 128:(c + 1) * 128], rhs=h_t[:, k, :],
                    start=(k == 0), stop=(k == KC - 1)), reads=[h_b, w_bufs[k]], writes=[pq_b])
            qb_t, qb_b = qb_ring.next()
            S.op("act", lambda h, pq_t=pq_t, qb_t=qb_t: h.activation(out=qb_t[:, :], in_=pq_t[:, 0:TB], func=AF.Copy),
                 reads=[pq_b], writes=[qb_b])
            if qlvl == 1:
                S.op("act", lambda h, c=c, pq_t=pq_t, qo_t=qo_t: h.activation(out=qo_t[:, c, :], in_=pq_t[:, 0:TB], func=AF.Copy),
                     reads=[pq_b], writes=[qo_b])
                continue
            pr_t, pr_b = pr_ring.next()
            S.op("pe", lambda h, pr_t=pr_t, qb_t=qb_t: h.matmul(pr_t[:, 0:TB], lhsT=rot_t[:, :], rhs=qb_t[:, :],
                                                               start=True, stop=True),
                 reads=[qb_b, rot_b], writes=[pr_b])
            t1_t, t1_b = t1_ring.next()
            t2_t, t2_b = t2_ring.next()
            if qlvl == 2:
                S.op("act", lambda h, c=c, pr_t=pr_t, qo_t=qo_t: h.activation(out=qo_t[:, c, :], in_=pr_t[:, 0:TB], func=AF.Copy),
                     reads=[pr_b], writes=[qo_b])
                continue
            S.op("dve", lambda h, t1_t=t1_t, pq_t=pq_t, cos_t=cos_t: h.tensor_tensor(
                out=t1_t[:, :], in0=pq_t[:, 0:TB], in1=cos_t[:, :], op=ALU.mult), reads=[pq_b, cos_b], writes=[t1_b])
            if qlvl == 3:
                S.op("act", lambda h, c=c, t1_t=t1_t, qo_t=qo_t: h.activation(out=qo_t[:, c, :], in_=t1_t[:, :], func=AF.Copy),
                     reads=[t1_b], writes=[qo_b])
                continue
            S.op("dve", lambda h, t2_t=t2_t, pr_t=pr_t, sin_t=sin_t: h.tensor_tensor(
                out=t2_t[:, :], in0=pr_t[:, 0:TB], in1=sin_t[:, :], op=ALU.mult), reads=[pr_b, sin_b], writes=[t2_b])
            S.op("dve", lambda h, c=c, qo_t=qo_t, t1_t=t1_t, t2_t=t2_t: h.tensor_tensor(
                out=qo_t[:, c, :], in0=t1_t[:, :], in1=t2_t[:, :], op=ALU.add), reads=[t1_b, t2_b], writes=[qo_b])
        if 'q' in parts:
            S.dma("pool", QsT[:, :, t0:t0 + TB], qo_t[:, 0:4, :], reads=[qo_b])
            S.dma("pool", KT[:, t0:t0 + TB], qo_t[:, 4, :], reads=[qo_b])
        yield
        uo_t, uo_b = uo_ring.next()
        for gi in (range(4) if 'u' in parts else []):
            pq_t, pq_b = pq_ring.next()
            for k in range(KC):
                S.op("pe", lambda h, gi=gi, k=k, pq_t=pq_t, h_t=h_t: h.matmul(
                    pq_t[:, 0:TB], lhsT=wb[:, k, 768 + gi * 128:768 + (gi + 1) * 128], rhs=h_t[:, k, :],
                    start=(k == 0), stop=(k == KC - 1)), reads=[h_b, w_bufs[k]], writes=[pq_b])
            S.op("act", lambda h, gi=gi, pq_t=pq_t, uo_t=uo_t: h.activation(out=uo_t[:, gi, :], in_=pq_t[:, 0:TB], func=AF.Copy),
                 reads=[pq_b], writes=[uo_b])
        if 'u' in parts:
            S.dma("pool", UT[:, :, t0:t0 + TB], uo_t[:, :, :], reads=[uo_b])
        if 'v' not in parts:
            return
        yield
        vo_t, vo_b = vo_ring.next()
        pv_t, pv_b = pv_ring.next()
        for ti in range(TB // 128):
            for k in range(KC):
                S.op("pe", lambda h, ti=ti, k=k, pv_t=pv_t, h_t=h_t: h.matmul(
                    pv_t[:, ti * 128:(ti + 1) * 128], lhsT=h_t[:, k, ti * 128:(ti + 1) * 128], rhs=wb[:, k, 640:768],
                    start=(k == 0), stop=(k == KC - 1)), reads=[h_b, w_bufs[k]], writes=[pv_b])
        for ti in range(TB // 128):
            for j in range(2):
                S.op("act", lambda h, ti=ti, j=j, pv_t=pv_t, vo_t=vo_t: h.activation(
                    out=vo_t[:, ti, j * 65:j * 65 + 64], in_=pv_t[:, ti * 128 + j * 64:ti * 128 + (j + 1) * 64], func=AF.Copy),
                    reads=[pv_b], writes=[vo_b])
        S.dma("pool", Vaug[t0:t0 + TB, :].rearrange("(i p) n -> p i n", p=128), vo_t[:, :, :], reads=[vo_b])
        yield

    order = list(range(NB)) if order is None else order
    if mid_hook is not None:
        run_interleaved((blk(b) for b in order[:2]), 2, 2)
        mid_hook()
        run_interleaved((blk(b) for b in order[2:]), 2, 2)
    else:
        run_interleaved((blk(b) for b in order), 2, 2)
    cx.pop()
    return cx.finish() if own else None


def rope_tables(pos, half, nrows):
    inv = (np.float32(10000.0) ** (-np.arange(half, dtype=np.float32) / np.float32(half))).astype(np.float32)
    ang = pos.astype(np.float32)[None, :] * inv[np.arange(nrows) % half][:, None]
    return np.cos(ang).astype(np.float32), np.sin(ang).astype(np.float32)


def rot_matrix(dh, nrows=128):
    R = np.zeros((nrows, nrows), np.float32)
    half = dh // 2
    for m in range(nrows):
        d = m % dh
        base = m - d
        if d < half:
            R[base + d + half, m] = -1.0
        else:
            R[base + d - half, m] = 1.0
    return R


def build_eb(ntok=TOK, cx=None):
    TB = 512
    NB = ntok // TB
    NT = ntok // 128
    own = cx is None
    cx = Ctx() if own else cx
    cx.push()
    S = cx.S
    QsT = cx.dram_in("QsT", [128, 4, ntok], BF16)
    KTh = cx.dram_in("KTh", [128, ntok + 256], BF16)
    Vh = cx.dram_in("Vh", [ntok + 256, 130], BF16)
    UTh = cx.dram_in("UTh", [128, 4, ntok + 16], BF16)
    xT = cx.dram_in("xT", [D, ntok])
    w_pool = cx.dram_in("w_pool", [4, 128, 128])
    pscale = cx.dram_in("pscale", [128, 4])
    w_out = cx.dram_in("w_out", [D, D])
    sinkrow = cx.dram_in("sinkrow", [1, 2, 512])
    masksd = cx.dram_in("masks", [4, 128, 512])
    invcd = cx.dram_in("invc", [128, 2, 4, 16])
    identd = cx.dram_in("ident", [128, 128])
    oT = cx.dram_out("oT", [D, ntok])

    woA = cx.sb("woA", [128, 4, D], BF16); woA_b = Buf("woA")
    woB = cx.sb("woB", [128, 4, D], BF16); woB_b = Buf("woB")
    wp = cx.sb("wp", [128, 4, 128], BF16); wp_b = Buf("wp")
    ps_t = cx.sb("ps", [128, 4], F32); ps_b = Buf("ps")
    mk_t = cx.sb("mk", [128, 4, 512], BF16); mk_b = Buf("mk")
    id_t = cx.sb("ident", [128, 128], BF16); id_b = Buf("ident")
    invc_t = cx.sb("invc", [128, 2, 4, 16], F32); invc_b = Buf("invc")
    sk_t = cx.sb("sk", [1, 2, 512], F32); sk_b = Buf("sk")
    esk_t = cx.sb("esk", [1, 2, 512], BF16); esk_b = Buf("esk")
    sel_t = cx.sb("sel", [1, 128], BF16); sel_b = Buf("sel")
    ones32 = cx.sb("ones32", [128, 64], F32); ones32_b = Buf("ones32")
    qsA_ring = mk_ring(cx, "sb", "qsA", 2, [128, 4, TB], BF16)
    qsB_ring = mk_ring(cx, "sb", "qsB", 2, [128, 4, TB], BF16)
    kt_ring = mk_ring(cx, "sb", "kt", 2, [128, 6 * 128], BF16)
    v_ring = mk_ring(cx, "sb", "v", 2, [128, 6, 130], BF16)
    u_ring = mk_ring(cx, "sb", "u", 2, [128, 4, TB + 16], BF16)
    x_ring = mk_ring(cx, "sb", "x", 2, [128, KC, TB], F32)
    p_ring = mk_ring(cx, "sb", "p", 4, [128, 512], BF16)
    osb_ring = mk_ring(cx, "sb", "osb", 3, [128, 512], F32)
    rc_ring = mk_ring(cx, "sb", "rc", 3, [128, 512], F32)
    ya_ring = mk_ring(cx, "sb", "ya", 2, [64, 8, TB], BF16)
    yp_ring = mk_ring(cx, "sb", "yp", 2, [128, 4, TB], BF16)
    yb_ring = mk_ring(cx, "sb", "yb", 2, [128, 4, TB], BF16)
    d_ring = mk_ring(cx, "sb", "d", 2, [128, 4, TB], BF16)
    tmp_rings = [mk_ring(cx, "sb", f"tp{g}", 2, [128, TB + 16], F32) for g in range(4)]
    e16_ring = mk_ring(cx, "sb", "e16", 2, [128, 16], F32)
    s_ring = mk_ring(cx, "ps", "s", 3, [128, 512], F32)
    o_ring = mk_ring(cx, "ps", "o", 2, [128, 512], F32)
    bc_ring = mk_ring(cx, "ps", "bc", 1, [128, 512], F32)
    y_ring = mk_ring(cx, "ps", "y", 2, [128, 512], F32)

    S.dma(cx.bind.get("_wq", "pool"), woA[:, :, :], w_out[0:512, :].rearrange("(i p) n -> p i n", p=128), writes=[woA_b])
    S.dma(cx.bind.get("_wq", "pool"), woB[:, :, :], w_out[512:1024, :].rearrange("(g p) n -> p g n", p=128), writes=[woB_b])
    S.dma("pool", wp[:, :, :], w_pool.rearrange("g i j -> i g j"), writes=[wp_b])
    S.dma("pool", mk_t[:, :, :], masksd.rearrange("m p n -> p m n"), writes=[mk_b])
    S.op("dve", lambda h: h.tensor_scalar(out=mk_t[:, :, :], in0=mk_t[:, :, :], scalar1=-1.0, scalar2=30000.0, op0=ALU.add, op1=ALU.mult),
         reads=[mk_b], writes=[mk_b])
    S.dma("pool", id_t[:, :], identd[:, :], writes=[id_b])
    S.dma("sp", ps_t[:, :], pscale[:, :], writes=[ps_b])
    S.dma("sp", invc_t[:, :, :, :], invcd[:, :, :, :], writes=[invc_b])
    S.dma("sp", sk_t[:, :, :], sinkrow[:, :, :], writes=[sk_b])
    S.op("act", lambda h: h.activation(out=esk_t[:, :, :], in_=sk_t[:, :, :], func=AF.Exp), reads=[sk_b], writes=[esk_b])
    S.op("pool", lambda h: h.memset(sel_t[:, :], 0.0), writes=[sel_b])
    S.op("pool", lambda h: h.memset(sel_t[:, 64:65], 1.0), writes=[sel_b])
    S.op("pool", lambda h: h.memset(ones32[:, :], 1.0), writes=[ones32_b])
    for (qt, qb_) in qsA_ring.items:
        S.op("pool", lambda h, qt=qt: h.memset(qt[64:128, :, :], 0.0), writes=[qb_])
    for (qt, qb_) in qsB_ring.items:
        S.op("pool", lambda h, qt=qt: h.memset(qt[0:64, :, :], 0.0), writes=[qb_])
    xv = xT.rearrange("(c p) t -> p c t", p=128)
    ov = oT.rearrange("(c p) t -> p c t", p=128)
    cx.run_hook()

    def blk(b):
        t0 = b * TB
        qsA_t, qsA_b = qsA_ring.next()
        qsB_t, qsB_b = qsB_ring.next()
        kt_t, kt_b = kt_ring.next()
        v_t, v_b = v_ring.next()
        u_t, u_b = u_ring.next()
        x_t, x_b = x_ring.next()
        S.dma("sp", qsA_t[0:64, :, :], QsT[0:64, :, t0:t0 + TB], writes=[qsA_b])
        S.dma("sp", qsB_t[64:128, :, :], QsT[64:128, :, t0:t0 + TB], writes=[qsB_b])
        S.dma("sp", kt_t[:, :], KTh[:, t0:t0 + 768], writes=[kt_b])
        S.dma("sp", v_t[:, :, :], Vh[t0:t0 + 768, :].rearrange("(i p) n -> p i n", p=128), writes=[v_b])
        S.dma("sp", u_t[:, :, :], UTh[:, :, t0:t0 + TB + 16], writes=[u_b])
        S.dma("sp", x_t[:, :, :], xv[:, :, t0:t0 + TB], writes=[x_b])
        yield
        ya_t, ya_b = ya_ring.next()
        tiles = [(nl, j, mi, dm) for nl in range(4) for mi, dm in enumerate((-1, 0, 1)) for j in range(2)]
        LA = 2
        st = {}
        unit_o = {}
        deferred = []

        def emit_S(t):
            nl, j, mi, dm = tiles[t]
            i = nl + dm + 1
            s_t, s_b = s_ring.next()
            q_t, q_b = (qsA_t, qsA_b) if j == 0 else (qsB_t, qsB_b)
            S.op("pe", lambda h: h.matmul(s_t[:, :], lhsT=kt_t[:, i * 128:(i + 1) * 128],
                                          rhs=q_t[:, :, nl * 128:(nl + 1) * 128], start=True, stop=(dm == 0)),
                 reads=[kt_b, q_b], writes=[s_b])
            if dm != 0:
                n_ = 4 * b + nl
                if dm == -1:
                    mi_ = 2 if n_ == 0 else 0
                else:
                    mi_ = 3 if n_ == NT - 1 else 1
                S.op("pe", lambda h: h.matmul(s_t[:, :], lhsT=id_t[:, :], rhs=mk_t[:, mi_, :], start=False, stop=True),
                     reads=[id_b, mk_b], writes=[s_b])
            st[t] = (s_t, s_b)

        def flush_deferred():
            while deferred:
                (o_t, o_b, osb_t, osb_b, rc_t, rc_b, nl, j) = deferred.pop(0)
                S.op("act", lambda h: h.activation(out=rc_t[64:65, :], in_=osb_t[64:65, :], func=AF.Ln), reads=[osb_b], writes=[rc_b])
                S.op("act", lambda h: h.activation(out=rc_t[64:65, :], in_=rc_t[64:65, :], func=AF.Exp, scale=-1.0),
                     reads=[rc_b], writes=[rc_b])
                bc_t, bc_b = bc_ring.next()
                S.op("pe", lambda h: h.matmul(bc_t[0:64, :], lhsT=ones32[64:65, 0:64], rhs=rc_t[64:65, :], start=True, stop=True),
                     reads=[rc_b, ones32_b], writes=[bc_b])
                S.op("dve", lambda h: h.tensor_tensor(
                    out=ya_t[0:64, j * 4:(j + 1) * 4, nl * 128:(nl + 1) * 128],
                    in0=osb_t[0:64, :].rearrange("p (c q) -> p c q", c=4),
                    in1=bc_t[0:64, :].rearrange("p (c q) -> p c q", c=4), op=ALU.mult),
                    reads=[osb_b, bc_b], writes=[ya_b])

        for t in range(min(LA, len(tiles))):
            emit_S(t)
        for t in range(len(tiles)):
            nl, j, mi, dm = tiles[t]
            n = 4 * b + nl
            i = nl + dm + 1
            if mi == 0:
                unit_o[(nl, j)] = o_ring.next()
            o_t, o_b = unit_o[(nl, j)]
            s_t, s_b = st.pop(t)
            p_t, p_b = p_ring.next()
            S.op("act", lambda h: h.activation(out=p_t[:, :], in_=s_t[:, :], func=AF.Exp, scale=0.125), reads=[s_b], writes=[p_b])
            if t + LA < len(tiles):
                emit_S(t + LA)
            S.op("pe", lambda h: h.matmul(o_t[0:65, :], lhsT=v_t[:, i, j * 65:(j + 1) * 65], rhs=p_t[:, :], start=(mi == 0), stop=False),
                 reads=[v_b, p_b], writes=[o_b])
            if mi == 0 and j == 1:
                flush_deferred()
            if mi == 2:
                S.op("pe", lambda h: h.matmul(o_t[0:65, :], lhsT=sel_t[0:1, 0:65], rhs=esk_t[0:1, j, :], start=False, stop=True),
                     reads=[sel_b, esk_b], writes=[o_b])
                osb_t, osb_b = osb_ring.next()
                rc_t, rc_b = rc_ring.next()
                S.op("dve", lambda h: h.tensor_copy(out=osb_t[0:65, :], in_=o_t[0:65, :]), reads=[o_b], writes=[osb_b])
                deferred.append((o_t, o_b, osb_t, osb_b, rc_t, rc_b, nl, j))
        flush_deferred()
        yield
        yp_t, yp_b = yp_ring.next()
        S.dma("sp", yp_t[0:64, :, :], ya_t[0:64, 0:8:2, :], reads=[ya_b], writes=[yp_b])
        S.dma("sp", yp_t[64:128, :, :], ya_t[0:64, 1:8:2, :], reads=[ya_b], writes=[yp_b])
        d_t, d_b = d_ring.next()
        L = TB + 16
        for g in range(4):
            w = 2 << g
            steps = g + 1
            src_t, src_b, ln = None, None, L
            for s_i in range(steps):
                sh = 1 << s_i
                tp_t, tp_b = tmp_rings[g].next()
                nl_ = ln - sh
                if s_i == 0:
                    S.op("pool", lambda h, tp_t=tp_t, u_t=u_t, g=g, nl_=nl_, sh=sh: h.tensor_tensor(
                        out=tp_t[:, 0:nl_], in0=u_t[:, g, 0:nl_], in1=u_t[:, g, sh:sh + nl_], op=ALU.add),
                        reads=[u_b], writes=[tp_b])
                else:
                    S.op("pool", lambda h, tp_t=tp_t, src_t=src_t, nl_=nl_, sh=sh: h.tensor_tensor(
                        out=tp_t[:, 0:nl_], in0=src_t[:, 0:nl_], in1=src_t[:, sh:sh + nl_], op=ALU.add),
                        reads=[src_b], writes=[tp_b])
                src_t, src_b, ln = tp_t, tp_b, nl_
            off = 8 - w // 2
            S.op("dve", lambda h, d_t=d_t, src_t=src_t, u_t=u_t, g=g, off=off, w=w: h.scalar_tensor_tensor(
                out=d_t[:, g, :], in0=src_t[:, off:off + TB], scalar=1.0 / w, in1=u_t[:, g, 8:8 + TB],
                op0=ALU.mult, op1=ALU.subtract), reads=[src_b, u_b], writes=[d_b])
            for (is_edge, which, c0) in ((b == 0, 0, 0), (b == NB - 1, 1, TB - 16)):
                if not is_edge:
                    continue
                e_t, e_b = e16_ring.next()
                S.op("dve", lambda h, e_t=e_t, src_t=src_t, g=g, off=off, c0=c0, which=which: h.tensor_tensor(
                    out=e_t[:, :], in0=src_t[:, off + c0:off + c0 + 16], in1=invc_t[:, which, g, :], op=ALU.mult),
                    reads=[src_b, invc_b], writes=[e_b])
                S.op("dve", lambda h, e_t=e_t, d_t=d_t, u_t=u_t, g=g, c0=c0: h.tensor_tensor(
                    out=d_t[:, g, c0:c0 + 16], in0=e_t[:, :], in1=u_t[:, g, 8 + c0:8 + c0 + 16], op=ALU.subtract),
                    reads=[e_b, u_b, d_b], writes=[d_b])
        yield
        yb_t, yb_b = yb_ring.next()
        for g in range(4):
            y_t, y_b = y_ring.next()
            S.op("pe", lambda h, y_t=y_t, d_t=d_t, g=g: h.matmul(y_t[:, :], lhsT=wp[:, g, :], rhs=d_t[:, g, :], start=True, stop=True),
                 reads=[wp_b, d_b], writes=[y_b])
            S.op("dve", lambda h, y_t=y_t, yb_t=yb_t, g=g: h.tensor_scalar_mul(out=yb_t[:, g, :], in0=y_t[:, :], scalar1=ps_t[:, g:g + 1]),
                 reads=[y_b, ps_b], writes=[yb_b])
        for o in range(KC):
            y_t, y_b = y_ring.next()
            for hh in range(4):
                S.op("pe", lambda h, y_t=y_t, yp_t=yp_t, hh=hh, o=o: h.matmul(
                    y_t[:, :], lhsT=woA[:, hh, o * 128:(o + 1) * 128], rhs=yp_t[:, hh, :], start=(hh == 0), stop=False),
                    reads=[woA_b, yp_b], writes=[y_b])
            for g in range(4):
                S.op("pe", lambda h, y_t=y_t, yb_t=yb_t, g=g, o=o: h.matmul(
                    y_t[:, :], lhsT=woB[:, g, o * 128:(o + 1) * 128], rhs=yb_t[:, g, :], start=False, stop=(g == 3)),
                    reads=[woB_b, yb_b], writes=[y_b])
            S.op("dve", lambda h, y_t=y_t, x_t=x_t, o=o: h.tensor_tensor(out=x_t[:, o, :], in0=y_t[:, :], in1=x_t[:, o, :], op=ALU.add),
                 reads=[y_b, x_b], writes=[x_b])
        S.dma("sp", ov[:, :, t0:t0 + TB], x_t[:, :, :], reads=[x_b])
        yield

    run_interleaved((blk(b) for b in range(NB)), 2, 2)
    cx.pop()
    return cx.finish() if own else None


def eb_masks(has_left, has_right):
    ki = np.arange(128)[:, None]
    qi = np.arange(128)[None, :]
    mL = np.tile((ki >= qi).astype(np.float32), (1, 4))
    mR = np.tile((ki <= qi).astype(np.float32), (1, 4))
    return np.stack([mL, mR, mL * float(has_left), mR * float(has_right)]).astype(np.float32)


def eb_invc(is_first, is_last):
    out = np.zeros((128, 2, 4, 16), np.float32)
    for g in range(4):
        w = 2 << g
        half = w // 2
        for i in range(16):
            c0 = min(i + half, w) if is_first else w
            r = 16 - i
            c1 = min(half + r, w) if is_last else w
            out[:, 0, g, i] = 1.0 / c0
            out[:, 1, g, i] = 1.0 / c1
    return out


def emit_rope(cx, src_t, src_b, nrow, TB, rot_t, rot_b, cos_t, cos_b, sin_t, sin_b, qb_ring, pr_ring, t1_ring, t2_ring,
              out_ap, out_b):
    S = cx.S
    qb_t, qb_b = qb_ring.next()
    S.op("act", lambda h: h.activation(out=qb_t[0:nrow, :], in_=src_t[0:nrow, 0:TB], func=AF.Copy), reads=[src_b], writes=[qb_b])
    pr_t, pr_b = pr_ring.next()
    S.op("pe", lambda h: h.matmul(pr_t[0:nrow, 0:TB], lhsT=rot_t[0:nrow, 0:nrow], rhs=qb_t[0:nrow, :], start=True, stop=True),
         reads=[qb_b, rot_b], writes=[pr_b])
    t1_t, t1_b = t1_ring.next()
    t2_t, t2_b = t2_ring.next()
    S.op("dve", lambda h: h.tensor_tensor(out=t1_t[0:nrow, :], in0=src_t[0:nrow, 0:TB], in1=cos_t[0:nrow, :], op=ALU.mult),
         reads=[src_b, cos_b], writes=[t1_b])
    S.op("dve", lambda h: h.tensor_tensor(out=t2_t[0:nrow, :], in0=pr_t[0:nrow, 0:TB], in1=sin_t[0:nrow, :], op=ALU.mult),
         reads=[pr_b, sin_b], writes=[t2_b])
    S.op("dve", lambda h: h.tensor_tensor(out=out_ap, in0=t1_t[0:nrow, :], in1=t2_t[0:nrow, :], op=ALU.add),
         reads=[t1_b, t2_b], writes=[out_b])


def build_oa(ntok=TOK, cx=None, mid_hook=None):
    TB = 512
    NB = ntok // TB
    NT = ntok // 128
    own = cx is None
    cx = Ctx() if own else cx
    cx.push()
    S = cx.S
    xT = cx.dram_in("xT", [D, ntok])
    xhalo = cx.dram_in("xhalo", [D, 4])
    w_in = cx.dram_in("w_in", [D, 1440])
    gin = cx.dram_in("g", [128, KC])
    gcq = cx.dram_in("g_cq", [128, 2])
    gckv = cx.dram_in("g_ckv", [128, 1])
    w_uq = cx.dram_in("w_uq", [256, 768])
    w_ukv = cx.dram_in("w_ukv", [128, 1024])
    cwd = cx.dram_in("cw", [128, 4, 4])
    cbd = cx.dram_in("cb", [128, 4])
    wad = cx.dram_in("wa", [2, 8, 64, 64])
    wxd = cx.dram_in("wx", [2, 8, 64, 64])
    bad = cx.dram_in("ba", [128, 2, 4])
    bxd = cx.dram_in("bx", [128, 2, 4])
    lamd = cx.dram_in("lam", [128, 2, 4])
    cosd = cx.dram_in("cos", [128, ntok])
    sind = cx.dram_in("sin", [128, ntok])
    rotd = cx.dram_in("rot", [128, 128])
    QN = cx.dram_out("QN", [512, ntok], BF16)
    QR = cx.dram_out("QR", [256, ntok], BF16)
    KNR = cx.dram_out("KNR", [544, ntok], BF16)
    V5 = cx.dram_out("V5", [1024, NT * 65], BF16)
    GX = cx.dram_out("GX", [512, ntok], BF16)
    AB = cx.dram_out("AB", [2, 2, 512, ntok])
    BLK = cx.dram_out("BLK", [128, NB, 2, 2, 4])
    CAB = cx.dram_out("CAB", [128, 2, 2, 4])
    XR = cx.dram_out("XR", [512, ntok])

    g_t = cx.sb("g", [128, KC], F32); g_b = Buf("g")
    gcq_t = cx.sb("gcq", [128, 2], F32); gcq_b = Buf("gcq")
    gckv_t = cx.sb("gckv", [128, 1], F32); gckv_b = Buf("gckv")
    ones_t = cx.sb("ones", [128, 128], BF16); ones_b = Buf("ones")
    xrh_t = cx.sb("xrh", [128, 4, 4], F32); xrh_b = Buf("xrh")
    cp_t = cx.sb("cp", [128, 2, 4], F32); cp_b = Buf("cp")
    blk_t = cx.sb("blk", [128, NB, 2, 2, 4], F32); blk_b = Buf("blk")
    S.dma("sp", g_t[:, :], gin[:, :], writes=[g_b])
    S.op("dve", lambda h: h.tensor_scalar_mul(out=g_t[:, :], in0=g_t[:, :], scalar1=float(np.sqrt(D))), reads=[g_b], writes=[g_b])
    S.dma("sp", gcq_t[:, :], gcq[:, :], writes=[gcq_b])
    S.op("dve", lambda h: h.tensor_scalar_mul(out=gcq_t[:, :], in0=gcq_t[:, :], scalar1=16.0), reads=[gcq_b], writes=[gcq_b])
    S.dma("sp", gckv_t[:, :], gckv[:, :], writes=[gckv_b])
    S.op("dve", lambda h: h.tensor_scalar_mul(out=gckv_t[:, :], in0=gckv_t[:, :], scalar1=float(np.sqrt(128.0))),
         reads=[gckv_b], writes=[gckv_b])
    S.op("pool", lambda h: h.memset(ones_t[:, :], 1.0), writes=[ones_b])
    S.dma("sp", cp_t[:, :, :], lamd[:, :, :], writes=[cp_b])
    S.op("act", lambda h: h.activation(out=cp_t[:, :, :], in_=cp_t[:, :, :], func=AF.Exp, scale=-1.0), reads=[cp_b], writes=[cp_b])
    S.op("act", lambda h: h.activation(out=cp_t[:, :, :], in_=cp_t[:, :, :], func=AF.Ln, bias=1.0), reads=[cp_b], writes=[cp_b])
    S.op("dve", lambda h: h.tensor_scalar_mul(out=cp_t[:, :, :], in0=cp_t[:, :, :], scalar1=-8.0), reads=[cp_b], writes=[cp_b])

    wabd = cx.sb("wabd", [128, 2, 4, 128], BF16); wxbd = cx.sb("wxbd", [128, 2, 4, 128], BF16); bd_b = Buf("bd")
    cw_t = cx.sb("cw", [128, 4, 4], F32); cb_t = cx.sb("cb", [128, 4], F32); cw_b = Buf("cw")
    ba_t = cx.sb("ba", [128, 2, 4], F32); bx_t = cx.sb("bx", [128, 2, 4], F32); bb_b = Buf("bb")
    S.op("pool", lambda h: h.memset(wabd[:, :, :, :], 0.0), writes=[bd_b])
    S.op("pool", lambda h: h.memset(wxbd[:, :, :, :], 0.0), writes=[bd_b])
    for d in range(2):
        for c in range(4):
            for hf in range(2):
                S.dma("pool", wabd[hf * 64:(hf + 1) * 64, d, c, hf * 64:(hf + 1) * 64], wad[d, 2 * c + hf, :, :], writes=[bd_b])
                S.dma("pool", wxbd[hf * 64:(hf + 1) * 64, d, c, hf * 64:(hf + 1) * 64], wxd[d, 2 * c + hf, :, :], writes=[bd_b])
    S.dma("sp", cw_t[:, :, :], cwd[:, :, :], writes=[cw_b])
    S.dma("sp", cb_t[:, :], cbd[:, :], writes=[cw_b])
    S.dma("sp", ba_t[:, :, :], bad[:, :, :], writes=[bb_b])
    S.dma("sp", bx_t[:, :, :], bxd[:, :, :], writes=[bb_b])
    xv = xT.rearrange("(c p) t -> p c t", p=128)
    cx.push()
    wb = cx.sb("wb", [128, KC, 1440], BF16)
    w_bufs = [Buf(f"w{k}") for k in range(KC)]
    wuqn = cx.sb("wuqn", [128, 2, 512], BF16); wuqr = cx.sb("wuqr", [128, 2, 256], BF16); wuq_b = Buf("wuq")
    wk = cx.sb("wk", [128, 512], BF16); wv = cx.sb("wv", [128, 512], BF16); wkv_b = Buf("wkv")
    rot_t = cx.sb("rot", [128, 128], BF16); rot_b = Buf("rot")
    x_ring = mk_ring(cx, "sb", "x", 2, [128, KC, TB], F32)
    h_ring = mk_ring(cx, "sb", "h", 2, [128, KC, TB], BF16)
    sq_ring = mk_ring(cx, "sb", "sq", 3, [128, TB], BF16)
    rstd_ring = mk_ring(cx, "sb", "rstd", 2, [128, TB], F32)
    cos_ring = mk_ring(cx, "sb", "cos", 2, [128, TB], F32)
    sin_ring = mk_ring(cx, "sb", "sin", 2, [128, TB], F32)
    qb_ring = mk_ring(cx, "sb", "qb", 2, [128, TB], BF16)
    t1_ring = mk_ring(cx, "sb", "t1", 2, [128, TB], F32)
    t2_ring = mk_ring(cx, "sb", "t2", 2, [128, TB], F32)
    cq_ring = mk_ring(cx, "sb", "cq", 2, [128, 2, TB], F32)
    ckv_ring = mk_ring(cx, "sb", "ckv", 2, [128, 1, TB], F32)
    cqn_ring = mk_ring(cx, "sb", "cqn", 2, [128, 2, TB], BF16)
    ckvn_ring = mk_ring(cx, "sb", "ckvn", 2, [128, 1, TB], BF16)
    xr_ring = mk_ring(cx, "sb", "xr", 2, [128, 4, TB], F32)
    gx_ring = mk_ring(cx, "sb", "gx", 2, [128, 4, TB], BF16)
    qn_ring = mk_ring(cx, "sb", "qn", 2, [128, 4, TB], BF16)
    qr_ring = mk_ring(cx, "sb", "qr", 2, [128, 2, TB], BF16)
    kn_ring = mk_ring(cx, "sb", "kn", 2, [128, 4, TB], BF16)
    kr_ring = mk_ring(cx, "sb", "kr", 2, [32, TB], BF16)
    vo_ring = mk_ring(cx, "sb", "vo", 2, [128, 4, 520], BF16)
    hx_t = cx.sb("hx", [128, KC, 4], F32); hx_b = Buf("hx")
    hh_t = cx.sb("hh", [128, KC, 4], BF16); hh_b = Buf("hh")
    st_ring = mk_ring(cx, "ps", "st", 1, [128, 512], F32)
    pq_ring = mk_ring(cx, "ps", "pq", 4, [128, 512], F32)
    pr_ring = mk_ring(cx, "ps", "pr", 1, [128, 512], F32)
    pv_ring = mk_ring(cx, "ps", "pv", 2, [128, 512], F32)

    for k in range(KC):
        S.dma(cx.bind.get("_wq", "pool"), wb[:, k, :], w_in[k * 128:(k + 1) * 128, :], writes=[w_bufs[k]])
    for k in range(2):
        src = w_uq[k * 128:(k + 1) * 128, :].rearrange("p (h e) -> p h e", e=96)
        S.dma("pool", wuqn[:, k, :].rearrange("p (h d) -> p h d", d=64), src[:, :, 0:64], writes=[wuq_b])
        S.dma("pool", wuqr[:, k, :].rearrange("p (h d) -> p h d", d=32), src[:, :, 64:96], writes=[wuq_b])
    srckv = w_ukv.rearrange("p (h e) -> p h e", e=128)
    S.dma("pool", wk[:, :].rearrange("p (h d) -> p h d", d=64), srckv[:, :, 0:64], writes=[wkv_b])
    S.dma("pool", wv[:, :].rearrange("p (h d) -> p h d", d=64), srckv[:, :, 64:128], writes=[wkv_b])
    S.dma("pool", rot_t[:, :], rotd[:, :], writes=[rot_b])
    for (vt, vb) in vo_ring.items:
        S.op("pool", lambda h, vt=vt: h.memset(vt[:, :, :], 1.0), writes=[vb])

    def proj_tile(h_t, h_b, c0, ncols, TBx):
        pq_t, pq_b = pq_ring.next()
        for k in range(KC):
            S.op("pe", lambda h, k=k: h.matmul(pq_t[0:ncols, 0:TBx], lhsT=wb[:, k, c0:c0 + ncols], rhs=h_t[:, k, 0:TBx],
                                               start=(k == 0), stop=(k == KC - 1)), reads=[h_b, w_bufs[k]], writes=[pq_b])
        return pq_t, pq_b

    S.dma("sp", hx_t[:, :, :], xhalo.rearrange("(c p) t -> p c t", p=128), writes=[hx_b])
    emit_rmsnorm(cx, hx_t, hx_b, KC, 4, g_t, g_b, ones_t, ones_b, sq_ring, st_ring, rstd_ring, hh_t, hh_b, D)
    for c in range(4):
        pq_t, pq_b = proj_tile(hh_t, hh_b, 416 + c * 128, 128, 4)
        S.op("act", lambda h, c=c, pq_t=pq_t: h.activation(out=xrh_t[:, c, :], in_=pq_t[:, 0:4], func=AF.Copy),
             reads=[pq_b], writes=[xrh_b])

    def blk(b):
        t0 = b * TB
        x_t, x_b = x_ring.next()
        S.dma("sp", x_t[:, :, :], xv[:, :, t0:t0 + TB], writes=[x_b])
        cos_t, cos_b = cos_ring.next()
        sin_t, sin_b = sin_ring.next()
        S.dma("sp", cos_t[:, :], cosd[:, t0:t0 + TB], writes=[cos_b])
        S.dma("sp", sin_t[:, :], sind[:, t0:t0 + TB], writes=[sin_b])
        h_t, h_b = h_ring.next()
        emit_rmsnorm(cx, x_t, x_b, KC, TB, g_t, g_b, ones_t, ones_b, sq_ring, st_ring, rstd_ring, h_t, h_b, D)
        yield
        cq_t, cq_b = cq_ring.next()
        for c in range(2):
            pq_t, pq_b = proj_tile(h_t, h_b, c * 128, 128, TB)
            S.op("act", lambda h, c=c, pq_t=pq_t, cq_t=cq_t: h.activation(out=cq_t[:, c, :], in_=pq_t[:, 0:TB], func=AF.Copy),
                 reads=[pq_b], writes=[cq_b])
        ckv_t, ckv_b = ckv_ring.next()
        pq_t, pq_b = proj_tile(h_t, h_b, 256, 128, TB)
        S.op("act", lambda h, pq_t=pq_t, ckv_t=ckv_t: h.activation(out=ckv_t[:, 0, :], in_=pq_t[:, 0:TB], func=AF.Copy),
             reads=[pq_b], writes=[ckv_b])
        yield
        pq_t, pq_b = proj_tile(h_t, h_b, 384, 32, TB)
        kr_t, kr_b = kr_ring.next()
        emit_rope(cx, pq_t, pq_b, 32, TB, rot_t, rot_b, cos_t, cos_b, sin_t, sin_b, qb_ring, pr_ring, t1_ring, t2_ring,
                  kr_t[0:32, :], kr_b)
        S.dma("pool", KNR[512:544, t0:t0 + TB], kr_t[:, :], reads=[kr_b])
        yield
        xr_t, xr_b = xr_ring.next()
        gx_t, gx_b = gx_ring.next()
        for c in range(4):
            pq_t, pq_b = proj_tile(h_t, h_b, 416 + c * 128, 128, TB)
            S.op("act", lambda h, c=c, pq_t=pq_t, xr_t=xr_t: h.activation(out=xr_t[:, c, :], in_=pq_t[:, 0:TB], func=AF.Copy),
                 reads=[pq_b], writes=[xr_b])
        for c in range(4):
            pq_t, pq_b = proj_tile(h_t, h_b, 928 + c * 128, 128, TB)
            S.op("act", lambda h, c=c, pq_t=pq_t, gx_t=gx_t: h.activation(out=gx_t[:, c, :], in_=pq_t[:, 0:TB], func=AF.Gelu_apprx_tanh),
                 reads=[pq_b], writes=[gx_b])
        S.dma("pool", XR.rearrange("(c p) t -> p c t", p=128)[:, :, t0:t0 + TB], xr_t[:, :, :], reads=[xr_b])
        S.dma("pool", GX.rearrange("(c p) t -> p c t", p=128)[:, :, t0:t0 + TB], gx_t[:, :, :], reads=[gx_b])
        yield
        cqn_t, cqn_b = cqn_ring.next()
        emit_rmsnorm(cx, cq_t, cq_b, 2, TB, gcq_t, gcq_b, ones_t, ones_b, sq_ring, st_ring, rstd_ring, cqn_t, cqn_b, 256)
        ckvn_t, ckvn_b = ckvn_ring.next()
        emit_rmsnorm(cx, ckv_t, ckv_b, 1, TB, gckv_t, gckv_b, ones_t, ones_b, sq_ring, st_ring, rstd_ring, ckvn_t, ckvn_b, 128)
        yield
        qn_t, qn_b = qn_ring.next()
        for i in range(4):
            pq_t, pq_b = pq_ring.next()
            for k in range(2):
                S.op("pe", lambda h, i=i, k=k, pq_t=pq_t, cqn_t=cqn_t: h.matmul(
                    pq_t[:, 0:TB], lhsT=wuqn[:, k, i * 128:(i + 1) * 128], rhs=cqn_t[:, k, :], start=(k == 0), stop=(k == 1)),
                    reads=[cqn_b, wuq_b], writes=[pq_b])
            S.op("act", lambda h, i=i, pq_t=pq_t, qn_t=qn_t: h.activation(out=qn_t[:, i, :], in_=pq_t[:, 0:TB], func=AF.Copy),
                 reads=[pq_b], writes=[qn_b])
        S.dma("pool", QN.rearrange("(c p) t -> p c t", p=128)[:, :, t0:t0 + TB], qn_t[:, :, :], reads=[qn_b])
        qr_t, qr_b = qr_ring.next()
        for i in range(2):
            pq_t, pq_b = pq_ring.next()
            for k in range(2):
                S.op("pe", lambda h, i=i, k=k, pq_t=pq_t, cqn_t=cqn_t: h.matmul(
                    pq_t[:, 0:TB], lhsT=wuqr[:, k, i * 128:(i + 1) * 128], rhs=cqn_t[:, k, :], start=(k == 0), stop=(k == 1)),
                    reads=[cqn_b, wuq_b], writes=[pq_b])
            emit_rope(cx, pq_t, pq_b, 128, TB, rot_t, rot_b, cos_t, cos_b, sin_t, sin_b, qb_ring, pr_ring, t1_ring, t2_ring,
                      qr_t[:, i, :], qr_b)
        S.dma("pool", QR.rearrange("(c p) t -> p c t", p=128)[:, :, t0:t0 + TB], qr_t[:, :, :], reads=[qr_b])
        yield
        kn_t, kn_b = kn_ring.next()
        for i in range(4):
            pq_t, pq_b = pq_ring.next()
            S.op("pe", lambda h, i=i, pq_t=pq_t, ckvn_t=ckvn_t: h.matmul(
                pq_t[:, 0:TB], lhsT=wk[:, i * 128:(i + 1) * 128], rhs=ckvn_t[:, 0, :], start=True, stop=True),
                reads=[ckvn_b, wkv_b], writes=[pq_b])
            S.op("act", lambda h, i=i, pq_t=pq_t, kn_t=kn_t: h.activation(out=kn_t[:, i, :], in_=pq_t[:, 0:TB], func=AF.Copy),
                 reads=[pq_b], writes=[kn_b])
        S.dma("pool", KNR[0:512, :].rearrange("(c p) t -> p c t", p=128)[:, :, t0:t0 + TB], kn_t[:, :, :], reads=[kn_b])
        yield
        vo_t, vo_b = vo_ring.next()
        for ti in range(TB // 128):
            pv_t, pv_b = pv_ring.next()
            S.op("pe", lambda h, ti=ti, pv_t=pv_t, ckvn_t=ckvn_t: h.matmul(
                pv_t[:, :], lhsT=ckvn_t[:, 0, ti * 128:(ti + 1) * 128], rhs=wv[:, :], start=True, stop=True),
                reads=[ckvn_b, wkv_b], writes=[pv_b])
            S.op("act", lambda h, ti=ti, pv_t=pv_t, vo_t=vo_t: h.activation(
                out=vo_t[:, ti, :].rearrange("p (h e) -> p h e", e=65)[:, :, 0:64],
                in_=pv_t[:, :].rearrange("p (h d) -> p h d", d=64), func=AF.Copy), reads=[pv_b], writes=[vo_b])
        for hd in range(8):
            S.dma("pool", V5[hd * 128:(hd + 1) * 128, :].rearrange("p (i e) -> p i e", e=65)[:, b * 4:(b + 1) * 4, :],
                  vo_t[:, :, hd * 65:(hd + 1) * 65], reads=[vo_b])
        yield

    run_interleaved((blk(b) for b in range(NB)), 2)
    cx.pop()
    if mid_hook is not None:
        mid_hook()

    cx.push()
    xe_ring = mk_ring(cx, "sb", "xe", 2, [128, 4, TB + 4], F32)
    xc_ring = mk_ring(cx, "sb", "xc", 2, [128, 4, TB], F32)
    xcb_ring = mk_ring(cx, "sb", "xcb", 2, [128, 4, TB], BF16)
    r_ring = mk_ring(cx, "sb", "r", 2, [128, 8, TB], F32)
    i_ring = mk_ring(cx, "sb", "i", 2, [128, 8, TB], F32)
    a_ring = mk_ring(cx, "sb", "a", 2, [128, 8, TB], F32)
    b_ring = mk_ring(cx, "sb", "b", 2, [128, 8, TB], F32)
    hl_ring = mk_ring(cx, "sb", "hl", 2, [128, TB], F32)
    sr_ring = mk_ring(cx, "sb", "sr", 2, [128, 8], F32)
    pg_ring = mk_ring(cx, "ps", "pg", 6, [128, 512], F32)
    XRv = XR.rearrange("(c p) t -> p c t", p=128)
    ABv = AB.rearrange("d s (c p) t -> d s p c t", p=128)

    def blk(b):
        t0 = b * TB
        xe_t, xe_b = xe_ring.next()
        lo = 0 if b > 0 else 2
        hi = TB + 3 if b < NB - 1 else TB + 2
        S.dma("sp", xe_t[:, :, lo:hi], XRv[:, :, t0 - 2 + lo:t0 - 2 + hi], writes=[xe_b])
        if b == 0:
            S.op("dve", lambda h, xe_t=xe_t: h.tensor_copy(out=xe_t[:, :, 0:2], in_=xrh_t[:, :, 0:2]), reads=[xrh_b, xe_b], writes=[xe_b])
        if b == NB - 1:
            S.op("dve", lambda h, xe_t=xe_t: h.tensor_copy(out=xe_t[:, :, TB + 2:TB + 3], in_=xrh_t[:, :, 2:3]),
                 reads=[xrh_b, xe_b], writes=[xe_b])
        yield
        xc_t, xc_b = xc_ring.next()
        xcb_t, xcb_b = xcb_ring.next()
        for c in range(4):
            S.op("dve", lambda h, c=c, xc_t=xc_t, xe_t=xe_t: h.tensor_scalar(
                out=xc_t[:, c, :], in0=xe_t[:, c, 0:TB], scalar1=cw_t[:, c, 0:1], scalar2=cb_t[:, c:c + 1],
                op0=ALU.mult, op1=ALU.add), reads=[xe_b, cw_b], writes=[xc_b])
            for j in range(1, 4):
                S.op("dve", lambda h, c=c, j=j, xc_t=xc_t, xe_t=xe_t: h.scalar_tensor_tensor(
                    out=xc_t[:, c, :], in0=xe_t[:, c, j:j + TB], scalar=cw_t[:, c, j:j + 1], in1=xc_t[:, c, :],
                    op0=ALU.mult, op1=ALU.add), reads=[xe_b, cw_b, xc_b], writes=[xc_b])
        S.op("act", lambda h, xc_t=xc_t, xcb_t=xcb_t: h.activation(out=xcb_t[:, :, :], in_=xc_t[:, :, :], func=AF.Copy), reads=[xc_b], writes=[xcb_b])
        yield
        r_t, r_b = r_ring.next()
        i_t, i_b = i_ring.next()
        a_t, a_b = a_ring.next()
        b_t, b_b = b_ring.next()
        sr_t, sr_b = sr_ring.next()
        S.op("dve", lambda h, sr_t=sr_t: h.memset(sr_t[:, :], 0.0), writes=[sr_b])
        for d in range(2):
            for c in range(4):
                q = d * 4 + c
                pg_t, pg_b = pg_ring.next()
                S.op("pe", lambda h, d=d, c=c, pg_t=pg_t, xcb_t=xcb_t: h.matmul(pg_t[:, :], lhsT=wabd[:, d, c, :], rhs=xcb_t[:, c, :],
                                                                         start=True, stop=True), reads=[bd_b, xcb_b], writes=[pg_b])
                S.op("act", lambda h, d=d, c=c, q=q, pg_t=pg_t, r_t=r_t, sr_t=sr_t: h.activation(
                    out=r_t[:, q, :], in_=pg_t[:, :], func=AF.Sigmoid, bias=ba_t[:, d, c:c + 1], accum_out=sr_t[:, q:q + 1]),
                    reads=[pg_b, bb_b], writes=[r_b, sr_b])
                pg_t, pg_b = pg_ring.next()
                S.op("pe", lambda h, d=d, c=c, pg_t=pg_t, xcb_t=xcb_t: h.matmul(pg_t[:, :], lhsT=wxbd[:, d, c, :], rhs=xcb_t[:, c, :],
                                                                         start=True, stop=True), reads=[bd_b, xcb_b], writes=[pg_b])
                S.op("act", lambda h, d=d, c=c, q=q, pg_t=pg_t, i_t=i_t: h.activation(
                    out=i_t[:, q, :], in_=pg_t[:, :], func=AF.Sigmoid, bias=bx_t[:, d, c:c + 1]),
                    reads=[pg_b, bb_b], writes=[i_b])
        yield
        for d in range(2):
            for c in range(4):
                q = d * 4 + c
                S.op("act", lambda h, d=d, c=c, q=q, a_t=a_t, r_t=r_t: h.activation(
                    out=a_t[:, q, :], in_=r_t[:, q, :], func=AF.Exp, scale=cp_t[:, d, c:c + 1]), reads=[r_b, cp_b], writes=[a_b])
                S.op("act", lambda h, d=d, c=c, q=q, sr_t=sr_t, b=b: h.activation(
                    out=blk_t[:, b, d, 0, c:c + 1], in_=sr_t[:, q:q + 1], func=AF.Exp, scale=cp_t[:, d, c:c + 1]),
                    reads=[sr_b, cp_b, blk_b], writes=[blk_b])
        yield
        S.op("dve", lambda h, a_t=a_t, r_t=r_t: h.tensor_tensor(out=r_t[:, :, :], in0=a_t[:, :, :], in1=a_t[:, :, :], op=ALU.mult),
             reads=[a_b, r_b], writes=[r_b])
        S.op("act", lambda h, r_t=r_t: h.activation(out=r_t[:, :, :], in_=r_t[:, :, :], func=AF.Sqrt, scale=-1.0, bias=1.0),
             reads=[r_b], writes=[r_b])
        for d in range(2):
            S.op("dve", lambda h, d=d, i_t=i_t, xc_t=xc_t: h.tensor_tensor(out=i_t[:, d * 4:(d + 1) * 4, :], in0=i_t[:, d * 4:(d + 1) * 4, :],
                                                                        in1=xc_t[:, :, :], op=ALU.mult), reads=[i_b, xc_b], writes=[i_b])
        S.op("dve", lambda h, b_t=b_t, r_t=r_t, i_t=i_t: h.tensor_tensor(out=b_t[:, :, :], in0=r_t[:, :, :], in1=i_t[:, :, :], op=ALU.mult),
             reads=[r_b, i_b], writes=[b_b])
        yield
        for d in range(2):
            for c in range(4):
                q = d * 4 + c
                hl_t, hl_b = hl_ring.next()
                if d == 0:
                    S.op("dve", lambda h, q=q, hl_t=hl_t, a_t=a_t, b_t=b_t: h.tensor_tensor_scan(
                        out=hl_t[:, :], data0=a_t[:, q, :], data1=b_t[:, q, :], initial=0.0, op0=ALU.mult, op1=ALU.add),
                        reads=[a_b, b_b], writes=[hl_b])
                    col = TB - 1
                else:
                    S.op("dve", lambda h, q=q, hl_t=hl_t, a_t=a_t, b_t=b_t: h.tensor_tensor_scan(
                        out=hl_t[:, ::-1], data0=a_t[:, q, ::-1], data1=b_t[:, q, ::-1], initial=0.0, op0=ALU.mult, op1=ALU.add),
                        reads=[a_b, b_b], writes=[hl_b])
                    col = 0
                S.op("act", lambda h, d=d, c=c, hl_t=hl_t, col=col, b=b: h.activation(
                    out=blk_t[:, b, d, 1, c:c + 1], in_=hl_t[:, col:col + 1], func=AF.Copy), reads=[hl_b, blk_b], writes=[blk_b])
        for d in range(2):
            S.dma("act", ABv[d, 0, :, :, t0:t0 + TB], a_t[:, d * 4:(d + 1) * 4, :], reads=[a_b])
            S.dma("sp", ABv[d, 1, :, :, t0:t0 + TB], b_t[:, d * 4:(d + 1) * 4, :], reads=[b_b])
        yield

    run_interleaved((blk(b) for b in range(NB)), 2)
    cab_t = cx.sb("cab", [128, 2, 2, 4], F32); cab_b = Buf("cab")
    for d in range(2):
        S.op("dve", lambda h, d=d: h.memset(cab_t[:, d, 0, :], 1.0), writes=[cab_b])
        S.op("dve", lambda h, d=d: h.memset(cab_t[:, d, 1, :], 0.0), writes=[cab_b])
        order = range(NB) if d == 0 else range(NB - 1, -1, -1)
        for b in order:
            S.op("dve", lambda h, d=d, b=b: h.tensor_tensor(out=cab_t[:, d, 1, :], in0=cab_t[:, d, 1, :], in1=blk_t[:, b, d, 0, :],
                                                            op=ALU.mult), reads=[cab_b, blk_b], writes=[cab_b])
            S.op("dve", lambda h, d=d, b=b: h.tensor_tensor(out=cab_t[:, d, 1, :], in0=cab_t[:, d, 1, :], in1=blk_t[:, b, d, 1, :],
                                                            op=ALU.add), reads=[cab_b, blk_b], writes=[cab_b])
            S.op("dve", lambda h, d=d, b=b: h.tensor_tensor(out=cab_t[:, d, 0, :], in0=cab_t[:, d, 0, :], in1=blk_t[:, b, d, 0, :],
                                                            op=ALU.mult), reads=[cab_b, blk_b], writes=[cab_b])
    S.dma("sp", BLK[:, :, :, :, :], blk_t[:, :, :, :, :], reads=[blk_b])
    S.dma("sp", CAB[:, :, :, :], cab_t[:, :, :, :], reads=[cab_b])
    cx.pop()
    cx.pop()
    return cx.finish() if own else None


def chunk_vec(v, nch):
    return np.ascontiguousarray(np.asarray(v, np.float32).reshape(nch, 128).T)


def oa_inputs(xT, xhalo, P, pos):
    cos, sin = rope_tables(pos, 16, 128)
    return {
        "xT": np.ascontiguousarray(xT), "xhalo": np.ascontiguousarray(xhalo), "w_in": P["w_in"], "g": vec128(P["g"], 8),
        "g_cq": vec128(P["g_cq"], 2), "g_ckv": vec128(P["g_ckv"], 1), "w_uq": P["w_uq"], "w_ukv": P["w_ukv"],
        "cw": np.ascontiguousarray(P["conv_w"].reshape(4, 4, 128).transpose(2, 1, 0)),
        "cb": chunk_vec(P["conv_b"], 4), "wa": P["wa"], "wx": P["wx"],
        "ba": np.ascontiguousarray(P["ba"].reshape(2, 4, 128).transpose(2, 0, 1)),
        "bx": np.ascontiguousarray(P["bx"].reshape(2, 4, 128).transpose(2, 0, 1)),
        "lam": np.ascontiguousarray(P["lam"].reshape(2, 4, 128).transpose(2, 0, 1)),
        "cos": cos, "sin": sin, "rot": rot_matrix(32),
    }


def build_ob1(ntok=TOK, nrank=4, cx=None):
    seq = ntok * nrank
    QG = ntok // 512
    NKT = seq // 128
    NT = ntok // 128
    own = cx is None
    cx = Ctx() if own else cx
    cx.push()
    S = cx.S
    QN = cx.dram_in("QN", [512, ntok], BF16)
    QR = cx.dram_in("QR", [256, ntok], BF16)
    KNg = cx.dram_in("KNg", [8 * nrank * 64, ntok], BF16)
    KRg = cx.dram_in("KRg", [nrank * 32, ntok], BF16)
    Vg = cx.dram_in("Vg", [8 * nrank * 128, NT * 65], BF16)
    YC = cx.dram_out("YC", [512, ntok], BF16)

    q_ring = mk_ring(cx, "sb", "q", 2, [128, ntok], BF16)
    k_ring = mk_ring(cx, "sb", "k", 2, [128, seq], BF16)
    v_ring = mk_ring(cx, "sb", "v", 2, [128, NKT, 65], BF16)
    p_ring = mk_ring(cx, "sb", "p", 4, [128, 1024], BF16)
    osb_ring = mk_ring(cx, "sb", "osb", 2, [64, 512], F32)
    rc_ring = mk_ring(cx, "sb", "rc", 2, [128, 512], F32)
    yc_ring = mk_ring(cx, "sb", "yc", 2, [64, 512], BF16)
    ones32 = cx.sb("ones32", [128, 64], F32); ones32_b = Buf("ones32")
    s_ring = mk_ring(cx, "ps", "s", 3, [128, 1024], F32)
    o_ring = mk_ring(cx, "ps", "o", 2, [128, 512], F32)
    S.op("pool", lambda h: h.memset(ones32[:, :], 1.0), writes=[ones32_b])
    scale = float(96 ** -0.5)
    NKP = NKT // 2
    LA = 2

    def load_head(hd):
        q_t, q_b = q_ring.next()
        k_t, k_b = k_ring.next()
        v_t, v_b = v_ring.next()
        S.dma("sp", q_t[0:64, :], QN[hd * 64:(hd + 1) * 64, :], writes=[q_b])
        S.dma("sp", q_t[64:96, :], QR[hd * 32:(hd + 1) * 32, :], writes=[q_b])
        for r in range(nrank):
            kr0 = ((hd // 2) * nrank + r) * 128 + (hd % 2) * 64
            S.dma("sp", k_t[0:64, r * ntok:(r + 1) * ntok], KNg[kr0:kr0 + 64, :], writes=[k_b])
            S.dma("sp", k_t[64:96, r * ntok:(r + 1) * ntok], KRg[r * 32:(r + 1) * 32, :], writes=[k_b])
            S.dma("sp", v_t[:, r * NT:(r + 1) * NT, :],
                  Vg[(hd * nrank + r) * 128:(hd * nrank + r + 1) * 128, :].rearrange("p (i e) -> p i e", e=65), writes=[v_b])
        return (q_t, q_b, k_t, k_b, v_t, v_b)

    nxt = load_head(0)
    cx.run_hook()
    jobs = cx.take_jobs()
    for hd in range(8):
        q_t, q_b, k_t, k_b, v_t, v_b = nxt
        if hd + 1 < 8:
            nxt = load_head(hd + 1)
        for qg in range(QG):
            if jobs and (hd, qg) != (0, 0):
                jobs.pop(0)()
            o_t, o_b = o_ring.next()
            stiles = {}

            def emit_s(kp):
                s_t, s_b = s_ring.next()
                for hf in range(2):
                    kt = 2 * kp + hf
                    S.op("pe", lambda h, kt=kt, hf=hf: h.matmul(s_t[:, hf * 512:(hf + 1) * 512], lhsT=k_t[0:96, kt * 128:(kt + 1) * 128],
                                                                rhs=q_t[0:96, qg * 512:(qg + 1) * 512], start=True, stop=True),
                         reads=[k_b, q_b], writes=[s_b])
                stiles[kp] = (s_t, s_b)

            for kp in range(min(LA, NKP)):
                emit_s(kp)
            for kp in range(NKP):
                s_t, s_b = stiles.pop(kp)
                p_t, p_b = p_ring.next()
                S.op("act", lambda h, s_t=s_t, p_t=p_t: h.activation(out=p_t[:, :], in_=s_t[:, :], func=AF.Exp, scale=scale),
                     reads=[s_b], writes=[p_b])
                if kp + LA < NKP:
                    emit_s(kp + LA)
                for hf in range(2):
                    kt = 2 * kp + hf
                    S.op("pe", lambda h, kt=kt, hf=hf, p_t=p_t: h.matmul(o_t[0:65, :], lhsT=v_t[:, kt, 0:65], rhs=p_t[:, hf * 512:(hf + 1) * 512],
                                                                         start=(kt == 0), stop=(kt == NKT - 1)),
                         reads=[v_b, p_b], writes=[o_b])
            osb_t, osb_b = osb_ring.next()
            rc_t, rc_b = rc_ring.next()
            S.op("act", lambda h: h.activation(out=osb_t[:, :], in_=o_t[0:64, :], func=AF.Copy), reads=[o_b], writes=[osb_b])
            S.op("dve", lambda h: h.reciprocal(out=rc_t[64:65, :], in_=o_t[64:65, :]), reads=[o_b], writes=[rc_b])
            bc_t, bc_b = s_ring.next()
            S.op("pe", lambda h: h.matmul(bc_t[0:64, 0:512], lhsT=ones32[64:65, 0:64], rhs=rc_t[64:65, :], start=True, stop=True),
                 reads=[rc_b, ones32_b], writes=[bc_b])
            yc_t, yc_b = yc_ring.next()
            S.op("dve", lambda h: h.tensor_tensor(out=yc_t[:, :], in0=osb_t[:, :], in1=bc_t[0:64, 0:512], op=ALU.mult),
                 reads=[osb_b, bc_b], writes=[yc_b])
            S.dma("pool", YC[hd * 64:(hd + 1) * 64, qg * 512:(qg + 1) * 512], yc_t[:, :], reads=[yc_b])
    while jobs:
        jobs.pop(0)()
    cx.pop()
    return cx.finish() if own else None


def build_ob2(ntok=TOK, ngrp=4, cx=None):
    TB = 512
    NB = ntok // TB
    own = cx is None
    cx = Ctx() if own else cx
    cx.push()
    S = cx.S
    AB = cx.dram_in("AB", [2, 2, 512, ntok])
    GX = cx.dram_in("GX", [512, ntok], BF16)
    YC = cx.dram_in("YC", [512, ntok], BF16)
    xT = cx.dram_in("xT", [D, ntok])
    w_out = cx.dram_in("w_out", [D, D])
    BLK = cx.dram_in("BLK", [128, NB, 2, 2, 4])
    CABg = cx.dram_in("CABg", [128, ngrp, 16])
    mfd = cx.dram_in("mf", [128, ngrp])
    mbd = cx.dram_in("mb", [128, ngrp])
    oT = cx.dram_out("oT", [D, ntok])

    woA = cx.sb("woA", [128, 4, D], BF16); woA_b = Buf("woA")
    woB = cx.sb("woB", [128, 4, D], BF16); woB_b = Buf("woB")
    blk_t = cx.sb("blk", [128, NB, 2, 2, 4], F32); blk_b = Buf("blk")
    cab_t = cx.sb("cab", [128, ngrp, 16], F32); cab_b = Buf("cab")
    m_t = cx.sb("m", [128, 2, ngrp], F32); m_b = Buf("m")
    hin_t = cx.sb("hin", [128, 2, 4], F32); hin_b = Buf("hin")
    tmp_t = cx.sb("tmp", [128, 4], F32); tmp_b = Buf("tmp")
    init_t = cx.sb("init", [128, NB, 2, 4], F32); init_b = Buf("init")
    ab_ring = mk_ring(cx, "sb", "ab", 2, [128, 2, 2, 4, TB], F32)
    hs_ring = mk_ring(cx, "sb", "hs", 2, [128, 2, 4, TB], F32)
    gx_ring = mk_ring(cx, "sb", "gx", 2, [128, 4, TB], BF16)
    yc_ring = mk_ring(cx, "sb", "yc", 2, [128, 4, TB], BF16)
    yd_ring = mk_ring(cx, "sb", "yd", 2, [128, 4, TB], BF16)
    x_ring = mk_ring(cx, "sb", "x", 2, [128, KC, TB], F32)
    y_ring = mk_ring(cx, "ps", "y", 3, [128, 512], F32)

    S.dma(cx.bind.get("_wq", "pool"), woA[:, :, :], w_out[0:512, :].rearrange("(i p) n -> p i n", p=128), writes=[woA_b])
    S.dma(cx.bind.get("_wq", "pool"), woB[:, :, :], w_out[512:1024, :].rearrange("(g p) n -> p g n", p=128), writes=[woB_b])
    S.dma("sp", blk_t[:, :, :, :, :], BLK[:, :, :, :, :], writes=[blk_b])
    S.dma("sp", cab_t[:, :, :], CABg[:, :, :], writes=[cab_b])
    S.dma("sp", m_t[:, 0, :], mfd[:, :], writes=[m_b])
    S.dma("sp", m_t[:, 1, :], mbd[:, :], writes=[m_b])
    S.op("pool", lambda h: h.memset(hin_t[:, :, :], 0.0), writes=[hin_b])
    for d in range(2):
        order = range(ngrp) if d == 0 else range(ngrp - 1, -1, -1)
        for i in order:
            S.op("dve", lambda h, d=d, i=i: h.tensor_tensor(out=tmp_t[:, :], in0=hin_t[:, d, :], in1=cab_t[:, i, d * 8:d * 8 + 4], op=ALU.mult),
                 reads=[hin_b, cab_b, tmp_b], writes=[tmp_b])
            S.op("dve", lambda h, d=d, i=i: h.tensor_tensor(out=tmp_t[:, :], in0=tmp_t[:, :], in1=cab_t[:, i, d * 8 + 4:d * 8 + 8], op=ALU.add),
                 reads=[tmp_b, cab_b], writes=[tmp_b])
            S.op("dve", lambda h, d=d, i=i: h.tensor_tensor(out=tmp_t[:, :], in0=tmp_t[:, :], in1=hin_t[:, d, :], op=ALU.subtract),
                 reads=[tmp_b, hin_b], writes=[tmp_b])
            S.op("dve", lambda h, d=d, i=i: h.scalar_tensor_tensor(out=hin_t[:, d, :], in0=tmp_t[:, :], scalar=m_t[:, d, i:i + 1],
                                                                   in1=hin_t[:, d, :], op0=ALU.mult, op1=ALU.add),
                 reads=[tmp_b, m_b, hin_b], writes=[hin_b])
    for d in range(2):
        order = list(range(NB)) if d == 0 else list(range(NB - 1, -1, -1))
        S.op("dve", lambda h, d=d, b0=order[0]: h.tensor_copy(out=init_t[:, b0, d, :], in_=hin_t[:, d, :]),
             reads=[hin_b, init_b], writes=[init_b])
        for bi in range(NB - 1):
            b, bn = order[bi], order[bi + 1]
            S.op("dve", lambda h, d=d, b=b, bn=bn: h.tensor_tensor(out=init_t[:, bn, d, :], in0=init_t[:, b, d, :],
                                                                   in1=blk_t[:, b, d, 0, :], op=ALU.mult),
                 reads=[init_b, blk_b], writes=[init_b])
            S.op("dve", lambda h, d=d, b=b, bn=bn: h.tensor_tensor(out=init_t[:, bn, d, :], in0=init_t[:, bn, d, :],
                                                                   in1=blk_t[:, b, d, 1, :], op=ALU.add),
                 reads=[init_b, blk_b], writes=[init_b])
    xv = xT.rearrange("(c p) t -> p c t", p=128)
    ov = oT.rearrange("(c p) t -> p c t", p=128)
    ABv = AB.rearrange("d s (c p) t -> d s p c t", p=128)
    def blk(b):
        t0 = b * TB
        ab_t, ab_b = ab_ring.next()
        for d in range(2):
            for s_ in range(2):
                S.dma("sp", ab_t[:, d, s_, :, :], ABv[d, s_, :, :, t0:t0 + TB], writes=[ab_b])
        gx_t, gx_b = gx_ring.next()
        yc_t, yc_b = yc_ring.next()
        x_t, x_b = x_ring.next()
        S.dma("sp", gx_t[:, :, :], GX.rearrange("(c p) t -> p c t", p=128)[:, :, t0:t0 + TB], writes=[gx_b])
        S.dma("sp", yc_t[:, :, :], YC.rearrange("(i p) t -> p i t", p=128)[:, :, t0:t0 + TB], writes=[yc_b])
        S.dma("sp", x_t[:, :, :], xv[:, :, t0:t0 + TB], writes=[x_b])
        yield
        hs_t, hs_b = hs_ring.next()
        for d in range(2):
            for c in range(4):
                if d == 0:
                    S.op("dve", lambda h, d=d, c=c, b=b: h.tensor_tensor_scan(
                        out=hs_t[:, d, c, :], data0=ab_t[:, d, 0, c, :], data1=ab_t[:, d, 1, c, :],
                        initial=init_t[:, b, d, c:c + 1], op0=ALU.mult, op1=ALU.add), reads=[ab_b, init_b, hs_b], writes=[hs_b])
                else:
                    S.op("dve", lambda h, d=d, c=c, b=b: h.tensor_tensor_scan(
                        out=hs_t[:, d, c, ::-1], data0=ab_t[:, d, 0, c, ::-1], data1=ab_t[:, d, 1, c, ::-1],
                        initial=init_t[:, b, d, c:c + 1], op0=ALU.mult, op1=ALU.add), reads=[ab_b, init_b, hs_b], writes=[hs_b])
        yield
        S.op("pool", lambda h: h.tensor_tensor(out=hs_t[:, 0, :, :], in0=hs_t[:, 0, :, :], in1=hs_t[:, 1, :, :], op=ALU.add),
             reads=[hs_b], writes=[hs_b])
        yd_t, yd_b = yd_ring.next()
        S.op("pool", lambda h: h.tensor_tensor(out=yd_t[:, :, :], in0=hs_t[:, 0, :, :], in1=gx_t[:, :, :], op=ALU.mult),
             reads=[hs_b, gx_b], writes=[yd_b])
        yield
        for o in range(KC):
            y_t, y_b = y_ring.next()
            for hh in range(4):
                S.op("pe", lambda h, hh=hh, o=o: h.matmul(y_t[:, :], lhsT=woA[:, hh, o * 128:(o + 1) * 128], rhs=yc_t[:, hh, :],
                                                         start=(hh == 0), stop=False), reads=[woA_b, yc_b], writes=[y_b])
            for g in range(4):
                S.op("pe", lambda h, g=g, o=o: h.matmul(y_t[:, :], lhsT=woB[:, g, o * 128:(o + 1) * 128], rhs=yd_t[:, g, :],
                                                       start=False, stop=(g == 3)), reads=[woB_b, yd_b], writes=[y_b])
            S.op("dve", lambda h, o=o: h.tensor_tensor(out=x_t[:, o, :], in0=y_t[:, :], in1=x_t[:, o, :], op=ALU.add),
                 reads=[y_b, x_b], writes=[x_b])
        S.dma("pool", ov[:, :, t0:t0 + TB], x_t[:, :, :], reads=[x_b])
        yield

    run_interleaved((blk(b) for b in range(NB)), 2, 2)
    cx.pop()
    return cx.finish() if own else None


def allgather(cx, in_ap, out_ap, groups):
    S = cx.S
    S.barrier()
    sem = S.new_sem("cc")
    cx.nc.gpsimd.collective_compute("AllGather", ALU.bypass, replica_groups=groups, ins=[in_ap], outs=[out_ap]).then_inc(sem, 1)
    for e in S.ENGS:
        S.h[e].wait_ge(sem, 1)


def allgather_many(cx, pairs, groups):
    S = cx.S
    S.barrier()
    sem = S.new_sem("ccm")
    for (in_ap, out_ap) in pairs:
        cx.nc.gpsimd.collective_compute("AllGather", ALU.bypass, replica_groups=groups, ins=[in_ap], outs=[out_ap]).then_inc(sem, 1)
    for e in S.ENGS:
        S.h[e].wait_ge(sem, len(pairs))


def emit_select(cx, src_t, src_b, nrank, m_t, m_b, side, acc_t, acc_b):
    S = cx.S
    S.op("dve", lambda h: h.tensor_scalar_mul(out=acc_t[:, :], in0=src_t[:, 0, :], scalar1=m_t[:, side, 0:1]),
         reads=[src_b, m_b], writes=[acc_b])
    for i in range(1, nrank):
        S.op("dve", lambda h, i=i: h.scalar_tensor_tensor(out=acc_t[:, :], in0=src_t[:, i, :], scalar=m_t[:, side, i:i + 1],
                                                          in1=acc_t[:, :], op0=ALU.mult, op1=ALU.add),
             reads=[src_b, m_b, acc_b], writes=[acc_b])


def even_exchange_start(cx, KTh, Vh, UTh, pack, packg, groups, ntok):
    S = cx.S
    S.barrier()
    S.dma_dd_async("sp", pack[:, 0:128], KTh[:, 128:256])
    S.dma_dd_async("sp", pack[:, 128:256], KTh[:, ntok:ntok + 128])
    S.dma_dd_async("sp", pack[:, 256:288].rearrange("p (g t) -> p g t", g=4), UTh[:, :, 8:16])
    S.dma_dd_async("sp", pack[:, 288:320].rearrange("p (g t) -> p g t", g=4), UTh[:, :, ntok:ntok + 8])
    S.dma_dd_async("sp", pack[:, 320:450], Vh[128:256, :])
    S.dma_dd_async("sp", pack[:, 450:580], Vh[ntok:ntok + 128, :])
    S.barrier()
    sem = S.new_sem("cce")
    cx.nc.gpsimd.collective_compute("AllGather", ALU.bypass, replica_groups=groups, ins=[pack[:, :]], outs=[packg[:, :]]).then_inc(sem, 1)
    return sem


def even_exchange_finish(cx, sem, KTh, Vh, UTh, packg, mlr, nrank, ntok):
    S = cx.S
    for e in S.ENGS:
        S.h[e].wait_ge(sem, 1)
    cx.push()
    pg_t = cx.sb("pg", [128, nrank, 580], BF16); pg_b = Buf("pg")
    m_t = cx.sb("mlr", [128, 2, nrank], F32); m_b = Buf("mlr")
    accL = cx.sb("accL", [128, 580], BF16); accL_b = Buf("accL")
    accR = cx.sb("accR", [128, 580], BF16); accR_b = Buf("accR")
    S.dma("sp", pg_t[:, :, :], packg.rearrange("(r p) n -> p r n", p=128), writes=[pg_b])
    S.dma("sp", m_t[:, :, :], mlr[:, :, :], writes=[m_b])
    emit_select(cx, pg_t, pg_b, nrank, m_t, m_b, 0, accL, accL_b)
    emit_select(cx, pg_t, pg_b, nrank, m_t, m_b, 1, accR, accR_b)
    S.dma("sp", KTh[:, 0:128], accL[:, 128:256], reads=[accL_b])
    S.dma("sp", UTh[:, :, 0:8], accL[:, 288:320].rearrange("p (g t) -> p g t", g=4), reads=[accL_b])
    S.dma("sp", Vh[0:128, :], accL[:, 450:580], reads=[accL_b])
    S.dma("sp", KTh[:, 128 + ntok:256 + ntok], accR[:, 0:128], reads=[accR_b])
    S.dma("sp", UTh[:, :, 8 + ntok:16 + ntok], accR[:, 256:288].rearrange("p (g t) -> p g t", g=4), reads=[accR_b])
    S.dma("sp", Vh[128 + ntok:256 + ntok, :], accR[:, 320:450], reads=[accR_b])
    cx.pop()


def xhalo_exchange_start(cx, xprev, xhp, xhpg, groups, ntok):
    S = cx.S
    S.barrier()
    S.dma_dd_async("sp", xhp[:, 0:2], xprev[:, 0:2])
    S.dma_dd_async("sp", xhp[:, 2:4], xprev[:, ntok - 2:ntok])
    S.barrier()
    sem = S.new_sem("ccx")
    cx.nc.gpsimd.collective_compute("AllGather", ALU.bypass, replica_groups=groups, ins=[xhp[:, :]], outs=[xhpg[:, :]]).then_inc(sem, 1)
    return sem


def xhalo_exchange_finish(cx, sem, xhpg, xhalo, mlr, nrank):
    S = cx.S
    for e in S.ENGS:
        S.h[e].wait_ge(sem, 1)
    cx.push()
    xg_t = cx.sb("xg", [128, nrank, 32], F32); xg_b = Buf("xg")
    m_t = cx.sb("mlr", [128, 2, nrank], F32); m_b = Buf("mlr")
    accL = cx.sb("accL", [128, 32], F32); accL_b = Buf("accL")
    accR = cx.sb("accR", [128, 32], F32); accR_b = Buf("accR")
    for r in range(nrank):
        S.dma("sp", xg_t[:, r, :].rearrange("p (c t) -> p c t", t=4),
              xhpg[r * D:(r + 1) * D, :].rearrange("(c p) t -> p c t", p=128), writes=[xg_b])
    S.dma("sp", m_t[:, :, :], mlr[:, :, :], writes=[m_b])
    emit_select(cx, xg_t, xg_b, nrank, m_t, m_b, 0, accL, accL_b)
    emit_select(cx, xg_t, xg_b, nrank, m_t, m_b, 1, accR, accR_b)
    xhv = xhalo.rearrange("(c p) t -> p c t", p=128)
    S.dma("sp", xhv[:, :, 0:2], accL[:, :].rearrange("p (c t) -> p c t", t=4)[:, :, 2:4], reads=[accL_b])
    S.dma("sp", xhv[:, :, 2:4], accR[:, :].rearrange("p (c t) -> p c t", t=4)[:, :, 0:2], reads=[accR_b])
    cx.pop()


SMALL_SPECS = None


def build_fused(B=2, nrank=4, ntok=TOK, depth=4):
    NE, NO = (depth + 1) // 2, depth // 2
    NT = ntok // 128
    NB = ntok // 512
    groups = [[b * nrank + r for r in range(nrank)] for b in range(B)]
    cx = Ctx()
    nc = cx.nc
    I = cx.ext_in
    x0 = I("xT", [D, ntok])
    Wd = {
        "e_w_in": I("e_w_in", [NE, D, 1280]), "e_w_pool": I("e_w_pool", [NE, 4, 128, 128]), "e_w_out": I("e_w_out", [NE, D, D]),
        "o_w_in": I("o_w_in", [NO, D, 1440]), "o_w_uq": I("o_w_uq", [NO, 256, 768]), "o_w_ukv": I("o_w_ukv", [NO, 128, 1024]),
        "o_lru_wa": I("o_lru_wa", [NO, 2, 8, 64, 64]), "o_lru_wx": I("o_lru_wx", [NO, 2, 8, 64, 64]), "o_w_out": I("o_w_out", [NO, D, D]),
        "w_mlp1": I("w_mlp1", [depth, D, DFF]), "w_mlp2": I("w_mlp2", [depth, DFF, D]),
        "g_mix": I("g_mix", [depth, 128, KC]), "g_mlp": I("g_mlp", [depth, 128, KC]), "g_fin": I("g_fin", [128, KC]),
        "pscale": I("pscale", [NE, 128, 4]), "sinkrow": I("sinkrow", [NE, 1, 2, 512]),
        "g_cq": I("g_cq", [NO, 128, 2]), "g_ckv": I("g_ckv", [NO, 128, 1]), "cw": I("cw", [NO, 128, 4, 4]), "cb": I("cb", [NO, 128, 4]),
        "ba": I("ba", [NO, 128, 2, 4]), "bx": I("bx", [NO, 128, 2, 4]), "lam": I("lam", [NO, 128, 2, 4]),
        "cos32": I("cos32", [128, ntok]), "sin32": I("sin32", [128, ntok]), "cos16": I("cos16", [128, ntok]), "sin16": I("sin16", [128, ntok]),
        "rot64": I("rot64", [128, 128]), "rot32": I("rot32", [128, 128]), "masks": I("masks", [4, 128, 512]),
        "ident": I("ident", [128, 128]),
        "invc": I("invc", [128, 2, 4, 16]), "mfb": I("mfb", [2, 128, nrank]), "mlr": I("mlr", [128, 2, nrank]),
    }
    outT = cx.ext_out("oT", [D, ntok])

    def tmp(name, shape, dt=F32):
        return nc.dram_tensor(name, list(shape), dt, kind="Internal").ap()

    def precast_jobs(layer, w1b_d, w2b_d):
        js = []
        for k in range(8):
            js.append(lambda k=k: cx.S.dma_dd_async("pool", w1b_d[k * 128:(k + 1) * 128, :], Wd["w_mlp1"][layer][k * 128:(k + 1) * 128, :]))
        for k in range(8):
            js.append(lambda k=k: cx.S.dma_dd_async("pool", w2b_d[k * 512:(k + 1) * 512, :], Wd["w_mlp2"][layer][k * 512:(k + 1) * 512, :]))
        return js

    wb16 = {}
    for (nm, nl, shp) in (("e_w_in", NE, [D, 1280]), ("e_w_out", NE, [D, D]), ("o_w_in", NO, [D, 1440]), ("o_w_out", NO, [D, D])):
        for l in range(nl):
            if nm == "e_w_in" and l == 0:
                continue
            wb16[(nm, l)] = tmp(f"{nm}_bf{l}", shp, BF16)

    def cast_job(nm, l):
        def f():
            if (nm, l) in wb16:
                for k in range(4):
                    cx.S.dma_dd_async("pool", wb16[(nm, l)][k * 256:(k + 1) * 256, :], Wd[nm][l][k * 256:(k + 1) * 256, :])
        return f

    def wsrc(nm, l):
        return wb16.get((nm, l), Wd[nm][l])

    def wq(nm, l):
        return "sp" if (nm, l) in wb16 else "pool"

    xcur = x0
    xh_pending = None
    for layer in range(depth):
        L = f"L{layer}"
        xmix = tmp(L + "_xmix", [D, ntok])
        w1b_d = tmp(L + "_w1b", [D, DFF], BF16)
        w2b_d = tmp(L + "_w2b", [DFF, D], BF16)
        if layer % 2 == 0:
            e = layer // 2
            QsT = tmp(L + "_QsT", [128, 4, ntok], BF16)
            KTh = tmp(L + "_KTh", [128, ntok + 256], BF16)
            Vh = tmp(L + "_Vh", [ntok + 256, 130], BF16)
            UTh = tmp(L + "_UTh", [128, 4, ntok + 16], BF16)
            pack = tmp(L + "_pack", [128, 580], BF16)
            packg = tmp(L + "_packg", [nrank * 128, 580], BF16)
            cx.bind = {"xT": xcur, "w_in": wsrc("e_w_in", e), "_wq": wq("e_w_in", e), "g": Wd["g_mix"][layer], "cos": Wd["cos32"], "sin": Wd["sin32"],
                       "rot": Wd["rot64"], "QsT": QsT, "KT": KTh[:, 128:128 + ntok], "Vaug": Vh[128:128 + ntok, :],
                       "UT": UTh[:, :, 8:8 + ntok]}
            exs = {}

            def ea_mid(KTh=KTh, Vh=Vh, UTh=UTh, pack=pack, packg=packg, exs=exs):
                exs["sem"] = even_exchange_start(cx, KTh, Vh, UTh, pack, packg, groups, ntok)

            cx.hook = cast_job("e_w_out", e)
            build_ea(ntok, cx=cx, order=[0, NB - 1] + list(range(1, NB - 1)) if NB > 1 else [0], mid_hook=ea_mid)
            even_exchange_finish(cx, exs["sem"], KTh, Vh, UTh, packg, Wd["mlr"], nrank, ntok)
            cx.bind = {"QsT": QsT, "KTh": KTh, "Vh": Vh, "UTh": UTh, "xT": xcur, "w_pool": Wd["e_w_pool"][e],
                       "pscale": Wd["pscale"][e], "w_out": wsrc("e_w_out", e), "_wq": wq("e_w_out", e), "sinkrow": Wd["sinkrow"][e], "masks": Wd["masks"],
                       "invc": Wd["invc"], "ident": Wd["ident"], "oT": xmix}
            build_eb(ntok, cx=cx)
        else:
            o = layer // 2
            if xh_pending is None:
                xhp = tmp(L + "_xhp", [D, 4]); xhpg = tmp(L + "_xhpg", [nrank * D, 4])
            else:
                xhp, xhpg = xh_pending["xhp"], xh_pending["xhpg"]
            xhalo = tmp(L + "_xhalo", [D, 4])
            QN = tmp(L + "_QN", [512, ntok], BF16); QR = tmp(L + "_QR", [256, ntok], BF16)
            KNR = tmp(L + "_KNR", [544, ntok], BF16); V5 = tmp(L + "_V5", [1024, NT * 65], BF16)
            GX = tmp(L + "_GX", [512, ntok], BF16); AB = tmp(L + "_AB", [2, 2, 512, ntok])
            BLK = tmp(L + "_BLK", [128, NB, 2, 2, 4]); CAB = tmp(L + "_CAB", [128, 16]); XR = tmp(L + "_XR", [512, ntok])
            KNg = tmp(L + "_KNg", [8 * nrank * 64, ntok], BF16); KRg = tmp(L + "_KRg", [nrank * 32, ntok], BF16)
            Vg = tmp(L + "_Vg", [8 * nrank * 128, NT * 65], BF16)
            CABg = tmp(L + "_CABg", [nrank * 128, 16]); YC = tmp(L + "_YC", [512, ntok], BF16)
            if xh_pending is None:
                xh_sem = xhalo_exchange_start(cx, xcur, xhp, xhpg, groups, ntok)
            else:
                xh_sem = xh_pending["sem"]
            xhalo_exchange_finish(cx, xh_sem, xhpg, xhalo, Wd["mlr"], nrank)
            cx.bind = {"xT": xcur, "xhalo": xhalo, "w_in": wsrc("o_w_in", o), "_wq": wq("o_w_in", o), "g": Wd["g_mix"][layer], "g_cq": Wd["g_cq"][o],
                       "g_ckv": Wd["g_ckv"][o], "w_uq": Wd["o_w_uq"][o], "w_ukv": Wd["o_w_ukv"][o], "cw": Wd["cw"][o], "cb": Wd["cb"][o],
                       "wa": Wd["o_lru_wa"][o], "wx": Wd["o_lru_wx"][o], "ba": Wd["ba"][o], "bx": Wd["bx"][o], "lam": Wd["lam"][o],
                       "cos": Wd["cos16"], "sin": Wd["sin16"], "rot": Wd["rot32"], "QN": QN, "QR": QR, "KNR": KNR, "V5": V5, "GX": GX,
                       "AB": AB, "BLK": BLK, "CAB": CAB.rearrange("p (d s c) -> p d s c", d=2, s=2), "XR": XR}
            ccsem = cx.S.new_sem("ccg")
            ncc = [0]

            def gather_kv():
                pairs = []
                for i in range(4):
                    pairs.append((KNR[i * 128:(i + 1) * 128, :], KNg[i * nrank * 128:(i + 1) * nrank * 128, :]))
                pairs.append((KNR[512:544, :], KRg[:, :]))
                for hd in range(8):
                    pairs.append((V5[hd * 128:(hd + 1) * 128, :], Vg[hd * nrank * 128:(hd + 1) * nrank * 128, :]))
                for (i_ap, o_ap) in pairs:
                    nc.gpsimd.collective_compute("AllGather", ALU.bypass, replica_groups=groups, ins=[i_ap], outs=[o_ap]).then_inc(ccsem, 1)
                    ncc[0] += 1

            build_oa(ntok, cx=cx, mid_hook=gather_kv)
            cabsem = cx.S.new_sem("cab")
            nc.gpsimd.collective_compute("AllGather", ALU.bypass, replica_groups=groups, ins=[CAB[:, :]], outs=[CABg[:, :]]).then_inc(cabsem, 1)
            for e_ in cx.S.ENGS:
                cx.S.h[e_].wait_ge(ccsem, ncc[0])
            cx.bind = {"QN": QN, "QR": QR, "KNg": KNg, "KRg": KRg, "Vg": Vg, "YC": YC}
            cx.jobs = [cast_job("o_w_out", o)] + precast_jobs(layer, w1b_d, w2b_d)
            build_ob1(ntok, nrank, cx=cx)
            for e_ in cx.S.ENGS:
                cx.S.h[e_].wait_ge(cabsem, 1)
            cx.bind = {"AB": AB, "GX": GX, "YC": YC, "xT": xcur, "w_out": wsrc("o_w_out", o), "_wq": wq("o_w_out", o), "BLK": BLK,
                       "CABg": CABg.rearrange("(r p) n -> p r n", p=128), "mf": Wd["mfb"][0], "mb": Wd["mfb"][1], "oT": xmix}
            build_ob2(ntok, nrank, cx=cx)
        last = layer == depth - 1
        xnext = outT if last else tmp(L + "_xmlp", [D, ntok])
        pre = layer % 2 == 1
        cx.bind = {"xT": xmix, "w1": w1b_d if pre else Wd["w_mlp1"][layer], "w2": w2b_d if pre else Wd["w_mlp2"][layer],
                   "g": Wd["g_mlp"][layer], "gf": Wd["g_fin"], "oT": xnext}
        NBm = ntok // 256
        if layer + 1 < depth:
            cx.hook = cast_job("o_w_in", (layer + 1) // 2) if (layer + 1) % 2 == 1 else cast_job("e_w_in", (layer + 1) // 2)
        if (layer + 1) < depth and (layer + 1) % 2 == 1 and NBm > 2:
            Ln = f"L{layer + 1}"
            xh_pending = {"xhp": tmp(Ln + "_xhp", [D, 4]), "xhpg": tmp(Ln + "_xhpg", [nrank * D, 4])}

            def mlp_mid(xh_pending=xh_pending, xnext=xnext):
                xh_pending["sem"] = xhalo_exchange_start(cx, xnext, xh_pending["xhp"], xh_pending["xhpg"], groups, ntok)

            build_mlp(last, ntok, cx=cx, wbf16=pre, order=[0, NBm - 1] + list(range(1, NBm - 1)), mid_hook=mlp_mid)
        else:
            xh_pending = None
            build_mlp(last, ntok, cx=cx, wbf16=pre)
        xcur = xnext
    cx.bind = {}
    return cx.finish()


_FUSED = {}


def run_model(x, W, nrank=4, ntok=TOK):
    B, Sq, _ = x.shape
    ncore = B * nrank
    assert Sq == nrank * ntok
    depth = W["norm_mlp"].shape[0]
    NE, NO = (depth + 1) // 2, depth // 2
    key = (B, nrank, ntok, depth)
    if key not in _FUSED:
        _FUSED[key] = build_fused(B, nrank, ntok, depth)
    nc = _FUSED[key]
    f32 = lambda a: np.ascontiguousarray(np.asarray(a, np.float32))
    g_mix = np.stack([vec128(W["e_norm_mix"][l // 2] if l % 2 == 0 else W["o_norm_mix"][l // 2], 8) for l in range(depth)])
    shared = {
        "e_w_in": f32(W["e_w_in"]), "e_w_pool": f32(W["e_w_pool"]), "e_w_out": f32(W["e_w_out"]),
        "o_w_in": f32(W["o_w_in"]), "o_w_uq": f32(W["o_w_uq"]), "o_w_ukv": f32(W["o_w_ukv"]),
        "o_lru_wa": f32(W["o_lru_wa"]), "o_lru_wx": f32(W["o_lru_wx"]), "o_w_out": f32(W["o_w_out"]),
        "w_mlp1": f32(W["w_mlp1"]), "w_mlp2": f32(W["w_mlp2"]),
        "g_mix": g_mix, "g_mlp": np.stack([vec128(W["norm_mlp"][l], 8) for l in range(depth)]), "g_fin": vec128(W["final_norm"], 8),
        "pscale": np.stack([vec128(W["e_pool_scale"][e], 4) for e in range(NE)]),
        "sinkrow": np.stack([np.repeat(f32(W["e_sink"][e]).reshape(2, 4), 128, axis=1).reshape(1, 2, 512) for e in range(NE)]),
        "g_cq": np.stack([vec128(W["o_g_cq"][o], 2) for o in range(NO)]),
        "g_ckv": np.stack([vec128(W["o_g_ckv"][o], 1) for o in range(NO)]),
        "cw": np.stack([f32(f32(W["o_conv_w"][o]).reshape(4, 4, 128).transpose(2, 1, 0)) for o in range(NO)]),
        "cb": np.stack([chunk_vec(W["o_conv_b"][o], 4) for o in range(NO)]),
        "ba": np.stack([f32(f32(W["o_lru_ba"][o]).reshape(2, 4, 128).transpose(2, 0, 1)) for o in range(NO)]),
        "bx": np.stack([f32(f32(W["o_lru_bx"][o]).reshape(2, 4, 128).transpose(2, 0, 1)) for o in range(NO)]),
        "lam": np.stack([f32(f32(W["o_lru_lambda"][o]).reshape(2, 4, 128).transpose(2, 0, 1)) for o in range(NO)]),
        "rot64": rot_matrix(64), "rot32": rot_matrix(32), "ident": np.eye(128, dtype=np.float32),
    }
    in_maps = []
    for c in range(ncore):
        bi, r = c // nrank, c % nrank
        pos = r * ntok + np.arange(ntok)
        cos32, sin32 = rope_tables(pos, 32, 128)
        cos16, sin16 = rope_tables(pos, 16, 128)
        mfb = np.zeros((2, 128, nrank), np.float32); mfb[0, :, :r] = 1.0; mfb[1, :, r + 1:] = 1.0
        mlr = np.zeros((128, 2, nrank), np.float32)
        if r > 0:
            mlr[:, 0, r - 1] = 1.0
        if r < nrank - 1:
            mlr[:, 1, r + 1] = 1.0
        im = dict(shared)
        im.update({"xT": np.ascontiguousarray(x[bi, r * ntok:(r + 1) * ntok, :].T), "cos32": cos32, "sin32": sin32, "cos16": cos16,
                   "sin16": sin16, "masks": eb_masks(r > 0, r < nrank - 1), "invc": eb_invc(r == 0, r == nrank - 1),
                   "mfb": mfb, "mlr": mlr})
        in_maps.append(im)
    res = run_spmd(nc, in_maps)
    out = np.empty((B, Sq, D), np.float32)
    for c in range(ncore):
        bi, r = c // nrank, c % nrank
        out[bi, r * ntok:(r + 1) * ntok, :] = res[c]["oT"].T
    return out


def kernel(**inputs):
    W = {k: np.asarray(v) for k, v in inputs.items()}
    x = np.asarray(W.pop("x"), np.float32)
    return run_model(x, W)
```

```python
from contextlib import ExitStack
import numpy as np
import concourse.bass as bass
import concourse.mybir as mybir
from concourse.bass_utils import run_bass_kernel_spmd

F32 = mybir.dt.float32
BF16 = mybir.dt.bfloat16
ALU = mybir.AluOpType
AF = mybir.ActivationFunctionType

NCORES = 8
D = 1024
KC = 8
TOK = 4096
SEQ = 16384
EPS = 1e-6
DFF = 4096
EPOCH = 30000


class Buf:
    __slots__ = ("name", "writers", "readers", "sem_in", "sem_out", "n_in", "n_out", "excl")

    def __init__(self, name, excl=False):
        self.name = name
        self.excl = excl
        self.writers = {}
        self.readers = {}
        self.sem_in = None
        self.sem_out = None
        self.n_in = 0
        self.n_out = 0


class Sched:
    ENGS = ("pe", "act", "dve", "pool", "sp")

    def __init__(self, nc, stack):
        self.nc = nc
        self.stack = stack
        self.h = {"pe": nc.tensor, "act": nc.scalar, "dve": nc.vector, "pool": nc.gpsimd, "sp": nc.sync}
        self.ops = {e: [] for e in self.ENGS}
        self.cnt = {e: 0 for e in self.ENGS}
        self.sem = {e: None for e in self.ENGS}
        self.seen = {e: {} for e in self.ENGS}
        self.last = {e: None for e in self.ENGS}
        self.dma_toks = {}
        self.nsem = 0
        self.ninstr = 0
        self.sem_pool = []
        self.live = []
        self.ddbuf = Buf("dram2dram")

    def new_sem(self, name):
        self.nsem += 1
        return self.stack.enter_context(self.nc.semaphore(f"{name}_{self.nsem}"))

    def _eng_tok(self, e):
        if self.sem[e] is None or self.cnt[e] >= EPOCH:
            self.sem[e] = self.new_sem("e" + e)
            self.cnt[e] = 0
        self.cnt[e] += 1
        tok = (self.sem[e], self.cnt[e])
        self.last[e] = tok
        return tok

    def _waits(self, e, toks):
        need = {}
        seen = self.seen[e]
        for sem, val in toks:
            k = id(sem)
            if seen.get(k, 0) >= val:
                continue
            if k not in need or need[k][1] < val:
                need[k] = (sem, val)
        out = []
        for k, (sem, val) in need.items():
            seen[k] = val
            out.append((sem, val))
        return out

    def _deps(self, e, reads, writes):
        toks = []
        for b in reads:
            toks.extend(b.writers.values())
            if b.excl:
                toks.extend(b.readers.values())
        for b in writes:
            toks.extend(b.writers.values())
            toks.extend(b.readers.values())
        if e == "pe":
            own = id(self.sem["pe"]) if self.sem["pe"] is not None else None
            toks = [t for t in toks if id(t[0]) != own]
        return self._waits(e, toks)

    def op(self, e, fn, reads=(), writes=()):
        waits = self._deps(e, reads, writes)
        tok = self._eng_tok(e)
        for b in reads:
            b.readers[id(tok[0])] = tok
        for b in writes:
            b.readers = {}
            b.writers = {id(tok[0]): tok}
        self.ninstr += 1

        h = self.h[e]
        for sem, val in waits:
            h.wait_ge(sem, val)
        fn(h).then_inc(tok[0], 1)

    def dma(self, q, out_ap, in_ap, reads=(), writes=(), **kw):
        waits = self._deps(q, reads, writes)
        assert len(writes) + len(reads) >= 1 and len(writes) <= 1 and len(reads) <= 1
        if writes:
            b = writes[0]
            if b.sem_in is None:
                b.sem_in, b.n_in = self._take_sem("di")
                self.live.append((b, "in"))
            b.n_in += 16
            tok = (b.sem_in, b.n_in)
            b.readers = {}
            b.writers = {id(tok[0]): tok}
            for rb in reads:
                rb.readers[id(tok[0])] = tok
        else:
            b = reads[0]
            if b.sem_out is None:
                b.sem_out, b.n_out = self._take_sem("do")
                self.live.append((b, "out"))
            b.n_out += 16
            tok = (b.sem_out, b.n_out)
            b.readers[id(tok[0])] = tok
        self.dma_toks[id(tok[0])] = tok
        self.ninstr += 1
        h = self.h[q]
        for sem, val in waits:
            h.wait_ge(sem, val)
        h.dma_start(out=out_ap, in_=in_ap, **kw).then_inc(tok[0], 16)

    def _take_sem(self, name):
        if self.sem_pool:
            return self.sem_pool.pop()
        return self.new_sem(name), 0

    def release_dma_sems(self):
        for b, kind in self.live:
            if kind == "in":
                self.sem_pool.append((b.sem_in, b.n_in)); b.sem_in = None
                b.writers = {}
            else:
                self.sem_pool.append((b.sem_out, b.n_out)); b.sem_out = None
                b.readers = {}
        self.live = []

    def dma_dd(self, q, out_ap, in_ap, **kw):
        self.dma(q, out_ap, in_ap, writes=[self.ddbuf], **kw)

    def dma_dd_async(self, q, out_ap, in_ap, **kw):
        self.dma(q, out_ap, in_ap, writes=[Buf("dd_async")], **kw)

    def barrier(self):
        toks = [t for t in self.last.values() if t is not None] + list(self.dma_toks.values())
        for e in self.ENGS:
            waits = self._waits(e, toks)
            for sem, val in waits:
                self.h[e].wait_ge(sem, val)

    def finalize(self):
        self.barrier()


class Ctx:
    def __init__(self):
        self.nc = bass.Bass("TRN2", target_bir_lowering=False)
        self.stack = ExitStack()
        self.S = Sched(self.nc, self.stack)
        self.n = 0
        self.cur = self.stack
        self.scopes = []
        self.bind = {}
        self.hook = None
        self.jobs = []

    def dram_in(self, name, shape, dt=F32):
        if name in self.bind:
            return self.bind[name]
        return self.nc.dram_tensor(name, list(shape), dt, kind="ExternalInput").ap()

    def dram_out(self, name, shape, dt=F32):
        if name in self.bind:
            return self.bind[name]
        return self.nc.dram_tensor(name, list(shape), dt, kind="ExternalOutput").ap()

    def ext_in(self, name, shape, dt=F32):
        return self.nc.dram_tensor(name, list(shape), dt, kind="ExternalInput").ap()

    def ext_out(self, name, shape, dt=F32):
        return self.nc.dram_tensor(name, list(shape), dt, kind="ExternalOutput").ap()

    def sb(self, name, shape, dt):
        self.n += 1
        return self.cur.enter_context(self.nc.sbuf_tensor(f"{name}_{self.n}", list(shape), dt))

    def ps(self, name, shape, dt=F32):
        self.n += 1
        return self.cur.enter_context(self.nc.psum_tensor(f"{name}_{self.n}", list(shape), dt))

    def dram_tmp(self, name, shape, dt=F32):
        return self.nc.dram_tensor(name, list(shape), dt, kind="Internal").ap()

    def take_jobs(self):
        js, self.jobs = self.jobs, []
        return js

    def run_hook(self):
        if self.hook is not None:
            fs, self.hook = self.hook, None
            for f in (fs if isinstance(fs, (list, tuple)) else [fs]):
                f()

    def push(self):
        st = ExitStack()
        self.scopes.append(st)
        self.cur = st

    def pop(self):
        self.S.barrier()
        if len(self.scopes) == 1:
            self.S.release_dma_sems()
        self.scopes.pop().close()
        self.cur = self.scopes[-1] if self.scopes else self.stack

    def finish(self):
        self.S.finalize()
        self.stack.close()
        return self.nc


class Ring:
    def __init__(self, items):
        self.items = items
        self.i = 0

    def next(self):
        it = self.items[self.i % len(self.items)]
        self.i += 1
        return it


def run_interleaved(gens, width=2, stagger=2):
    it = iter(gens)
    active = []
    steps = 0
    while True:
        while len(active) < width and (not active or steps >= stagger):
            try:
                active.append(next(it))
            except StopIteration:
                break
        if not active:
            break
        steps += 1
        for g in list(active):
            try:
                next(g)
            except StopIteration:
                active.remove(g)


def mk_ring(cx, kind, name, n, shape, dt):
    items = []
    for i in range(n):
        t = cx.sb(f"{name}{i}", shape, dt) if kind == "sb" else cx.ps(f"{name}{i}", shape, dt)
        items.append((t, Buf(f"{name}{i}", excl=(kind == "ps"))))
    return Ring(items)


def emit_rmsnorm(cx, x_t, x_b, nchunk, TB, g_t, g_b, ones_t, ones_b, sq_ring, st_ring, rstd_ring,
                 out_t, out_b, nfeat, evac_engs=("dve",)):
    S = cx.S
    st_t, st_b = st_ring.next()
    for c in range(nchunk):
        sq_t, sq_b = sq_ring.next()
        S.op("act", lambda h, c=c, sq_t=sq_t: h.activation(out=sq_t[:, 0:TB], in_=x_t[:, c, 0:TB], func=AF.Square),
             reads=[x_b], writes=[sq_b])
        S.op("pe", lambda h, c=c, sq_t=sq_t: h.matmul(st_t[:, 0:TB], lhsT=ones_t[:, :], rhs=sq_t[:, 0:TB],
                                                        start=(c == 0), stop=(c == nchunk - 1)),
             reads=[sq_b, ones_b], writes=[st_b])
    r_t, r_b = rstd_ring.next()
    S.op("act", lambda h: h.activation(out=r_t[:, 0:TB], in_=st_t[:, 0:TB], func=AF.Sqrt, bias=float(nfeat * EPS)),
         reads=[st_b], writes=[r_b])
    S.op("dve", lambda h: h.reciprocal(out=r_t[:, 0:TB], in_=r_t[:, 0:TB]), reads=[r_b], writes=[r_b])
    for c in range(nchunk):
        e = evac_engs[c % len(evac_engs)]
        S.op(e, lambda h, c=c: h.scalar_tensor_tensor(out=out_t[:, c, 0:TB], in0=x_t[:, c, 0:TB],
                                                       scalar=g_t[:, c:c + 1], in1=r_t[:, 0:TB],
                                                       op0=ALU.mult, op1=ALU.mult),
             reads=[x_b, r_b, g_b], writes=[out_b])


def build_mlp(final_norm, ntok=TOK, dbg=False, cx=None, wbf16=False, order=None, mid_hook=None):
    TB = 256
    NB = ntok // TB
    FC = DFF // 128
    own = cx is None
    cx = Ctx() if own else cx
    cx.push()
    S = cx.S
    xT = cx.dram_in("xT", [D, ntok])
    w1 = cx.dram_in("w1", [D, DFF], BF16 if wbf16 else F32)
    w2 = cx.dram_in("w2", [DFF, D], BF16 if wbf16 else F32)
    gin = cx.dram_in("g", [128, KC])
    oT = cx.dram_out("oT", [D, ntok])
    if final_norm:
        gfin = cx.dram_in("gf", [128, KC])
    if dbg:
        dh = cx.dram_out("dh", [128, KC, TB], BF16)
        da = cx.dram_out("da", [128, DFF // 128, TB], BF16)

    w1b = cx.sb("w1b", [128, KC, DFF], BF16)
    w2b = cx.sb("w2b", [128, FC, D], BF16)
    w1_bufs = [Buf(f"w1_{k}") for k in range(KC)]
    w2_bufs = [Buf(f"w2_{k}") for k in range(8)]
    g_t = cx.sb("g", [128, KC], F32); g_b = Buf("g")
    ones_t = cx.sb("ones", [128, 128], BF16); ones_b = Buf("ones")
    x_ring = mk_ring(cx, "sb", "x", 2, [128, KC, TB], F32)
    h_ring = mk_ring(cx, "sb", "h", 2, [128, KC, TB], BF16)
    a_ring = mk_ring(cx, "sb", "a", 1, [128, FC, TB], BF16)
    r_ring = mk_ring(cx, "sb", "r", 3, [128, TB], BF16)
    sq_ring = mk_ring(cx, "sb", "sq", 3, [128, TB], BF16)
    rstd_ring = mk_ring(cx, "sb", "rstd", 2, [128, TB], F32)
    o_ring = mk_ring(cx, "sb", "o", 2, [128, KC, TB], F32)
    st_ring = mk_ring(cx, "ps", "st", 1, [128, 512], F32)
    p1_ring = mk_ring(cx, "ps", "p1", 3, [128, 512], F32)
    p2_ring = mk_ring(cx, "ps", "p2", 3, [128, 512], F32)
    if final_norm:
        gf_t = cx.sb("gf", [128, KC], F32); gf_b = Buf("gf")
        f_ring = mk_ring(cx, "sb", "f", 2, [128, KC, TB], F32)

    S.dma("sp", g_t[:, :], gin[:, :], writes=[g_b])
    S.op("dve", lambda h: h.tensor_scalar_mul(out=g_t[:, :], in0=g_t[:, :], scalar1=float(np.sqrt(D))),
         reads=[g_b], writes=[g_b])
    if final_norm:
        S.dma("sp", gf_t[:, :], gfin[:, :], writes=[gf_b])
        S.op("dve", lambda h: h.tensor_scalar_mul(out=gf_t[:, :], in0=gf_t[:, :], scalar1=float(np.sqrt(D))),
             reads=[gf_b], writes=[gf_b])
    S.op("pool", lambda h: h.memset(ones_t[:, :], 1.0), writes=[ones_b])
    w1v = w1.rearrange("(k p) n -> p k n", p=128)
    w2v = w2.rearrange("(f p) n -> p f n", p=128)
    wq = "sp" if wbf16 else "pool"
    for k in range(KC):
        S.dma(wq, w1b[:, k, :], w1v[:, k, :], writes=[w1_bufs[k]])
    for j in range(8):
        S.dma(wq, w2b[:, j * 4:(j + 1) * 4, :], w2v[:, j * 4:(j + 1) * 4, :], writes=[w2_bufs[j]])
    xv = xT.rearrange("(c p) t -> p c t", p=128)
    ov = oT.rearrange("(c p) t -> p c t", p=128)
    cx.run_hook()

    def prep(b):
        x_t, x_b = x_ring.next()
        S.dma("sp", x_t[:, :, :], xv[:, :, b * TB:(b + 1) * TB], writes=[x_b])
        h_t, h_b = h_ring.next()
        return (x_t, x_b, h_t, h_b)

    def norm(st_):
        x_t, x_b, h_t, h_b = st_
        emit_rmsnorm(cx, x_t, x_b, KC, TB, g_t, g_b, ones_t, ones_b, sq_ring, st_ring, rstd_ring, h_t, h_b, D)

    order = list(range(NB)) if order is None else order
    cur = prep(order[0])
    norm(cur)
    for bi, b in enumerate(order):
        t0 = b * TB
        x_t, x_b, h_t, h_b = cur
        nxt = prep(order[bi + 1]) if bi + 1 < NB else None
        a_t, a_b = a_ring.next()
        for f in range(FC):
            if f == FC // 2 and nxt is not None:
                norm(nxt)
            p_t, p_b = p1_ring.next()
            for k in range(KC):
                S.op("pe", lambda h, f=f, k=k, p_t=p_t: h.matmul(p_t[:, 0:TB], lhsT=w1b[:, k, f * 128:(f + 1) * 128],
                                                                  rhs=h_t[:, k, 0:TB], start=(k == 0), stop=(k == KC - 1)),
                     reads=[h_b, w1_bufs[k]], writes=[p_b])
            r_t, r_b = r_ring.next()
            S.op("act", lambda h, p_t=p_t, r_t=r_t: h.activation(out=r_t[:, 0:TB], in_=p_t[:, 0:TB], func=AF.Relu),
                 reads=[p_b], writes=[r_b])
            S.op("pool", lambda h, f=f, r_t=r_t: h.tensor_tensor(out=a_t[:, f, 0:TB], in0=r_t[:, 0:TB], in1=r_t[:, 0:TB],
                                                                  op=ALU.mult),
                 reads=[r_b], writes=[a_b])
        if dbg and b == 0:
            S.dma("sp", dh[:, :, :], h_t[:, :, :], reads=[h_b])
            S.dma("sp", da[:, :, :], a_t[:, :, :], reads=[a_b])
        o_t, o_b = o_ring.next()
        for c in range(KC):
            p_t, p_b = p2_ring.next()
            for f in range(FC):
                S.op("pe", lambda h, f=f, c=c, p_t=p_t: h.matmul(p_t[:, 0:TB], lhsT=w2b[:, f, c * 128:(c + 1) * 128],
                                                                  rhs=a_t[:, f, 0:TB], start=(f == 0), stop=(f == FC - 1)),
                     reads=[a_b, w2_bufs[f // 4]], writes=[p_b])
            S.op("dve", lambda h, c=c, p_t=p_t: h.tensor_tensor(out=o_t[:, c, 0:TB], in0=p_t[:, 0:TB], in1=x_t[:, c, 0:TB],
                                                                 op=ALU.add),
                 reads=[p_b, x_b], writes=[o_b])
        if final_norm:
            f_t, f_b = f_ring.next()
            emit_rmsnorm(cx, o_t, o_b, KC, TB, gf_t, gf_b, ones_t, ones_b, sq_ring, st_ring, rstd_ring, f_t, f_b, D)
            S.dma("pool", ov[:, :, t0:t0 + TB], f_t[:, :, :], reads=[f_b])
        else:
            S.dma("pool", ov[:, :, t0:t0 + TB], o_t[:, :, :], reads=[o_b])
        cur = nxt
        if mid_hook is not None and bi == 1:
            mid_hook()
    cx.pop()
    return cx.finish() if own else None


def run_spmd(nc, in_maps):
    res = run_bass_kernel_spmd(nc, in_maps, core_ids=list(range(len(in_maps))))
    return res.results


def vec128(v, k):
    return np.ascontiguousarray(np.asarray(v, np.float32).reshape(k, 128).T)


def load_cast(cx, q, dst_ap, src_ap, buf):
    cx.S.dma(q, dst_ap, src_ap, writes=[buf])


def build_ea(ntok=TOK, parts='quv', qlvl=4, cx=None, order=None, mid_hook=None):
    TB = 512
    NB = ntok // TB
    own = cx is None
    cx = Ctx() if own else cx
    cx.push()
    S = cx.S
    xT = cx.dram_in("xT", [D, ntok])
    w_in = cx.dram_in("w_in", [D, 1280])
    gin = cx.dram_in("g", [128, KC])
    cosd = cx.dram_in("cos", [128, ntok])
    sind = cx.dram_in("sin", [128, ntok])
    rotd = cx.dram_in("rot", [128, 128])
    QsT = cx.dram_out("QsT", [128, 4, ntok], BF16)
    KT = cx.dram_out("KT", [128, ntok], BF16)
    Vaug = cx.dram_out("Vaug", [ntok, 130], BF16)
    UT = cx.dram_out("UT", [128, 4, ntok], BF16)

    wb = cx.sb("wb", [128, KC, 1280], BF16)
    w_bufs = [Buf(f"w{k}") for k in range(KC)]
    g_t = cx.sb("g", [128, KC], F32); g_b = Buf("g")
    ones_t = cx.sb("ones", [128, 128], BF16); ones_b = Buf("ones")
    rot_t = cx.sb("rot", [128, 128], BF16); rot_b = Buf("rot")
    x_ring = mk_ring(cx, "sb", "x", 3, [128, KC, TB], F32)
    h_ring = mk_ring(cx, "sb", "h", 3, [128, KC, TB], BF16)
    sq_ring = mk_ring(cx, "sb", "sq", 3, [128, TB], BF16)
    rstd_ring = mk_ring(cx, "sb", "rstd", 2, [128, TB], F32)
    cos_ring = mk_ring(cx, "sb", "cos", 3, [128, TB], F32)
    sin_ring = mk_ring(cx, "sb", "sin", 3, [128, TB], F32)
    qb_ring = mk_ring(cx, "sb", "qb", 2, [128, TB], BF16)
    t1_ring = mk_ring(cx, "sb", "t1", 2, [128, TB], F32)
    t2_ring = mk_ring(cx, "sb", "t2", 2, [128, TB], F32)
    qo_ring = mk_ring(cx, "sb", "qo", 3, [128, 5, TB], BF16)
    uo_ring = mk_ring(cx, "sb", "uo", 3, [128, 4, TB], BF16)
    vo_ring = mk_ring(cx, "sb", "vo", 3, [128, 4, 130], BF16)
    st_ring = mk_ring(cx, "ps", "st", 1, [128, 512], F32)
    pq_ring = mk_ring(cx, "ps", "pq", 3, [128, 512], F32)
    pr_ring = mk_ring(cx, "ps", "pr", 2, [128, 512], F32)
    pv_ring = mk_ring(cx, "ps", "pv", 2, [128, 512], F32)

    S.dma("sp", g_t[:, :], gin[:, :], writes=[g_b])
    S.op("dve", lambda h: h.tensor_scalar_mul(out=g_t[:, :], in0=g_t[:, :], scalar1=float(np.sqrt(D))),
         reads=[g_b], writes=[g_b])
    S.op("pool", lambda h: h.memset(ones_t[:, :], 1.0), writes=[ones_b])
    S.dma("pool", rot_t[:, :], rotd[:, :], writes=[rot_b])
    for (vt, vb) in vo_ring.items:
        S.op("pool", lambda h, vt=vt: h.memset(vt[:, :, :], 1.0), writes=[vb])
    for k in range(KC):
        for j in range(2):
            src = w_in[k * 128:(k + 1) * 128, j * 256:(j + 1) * 256].rearrange("p (c d) -> p c d", c=4, d=64)
            dst = wb[:, k, 0:512].rearrange("p (c j d) -> p c j d", c=4, j=2, d=64)[:, :, j, :]
            S.dma(cx.bind.get("_wq", "pool"), dst, src, writes=[w_bufs[k]])
        S.dma(cx.bind.get("_wq", "pool"), wb[:, k, 512:1280], w_in[k * 128:(k + 1) * 128, 512:1280], writes=[w_bufs[k]])
    xv = xT.rearrange("(c p) t -> p c t", p=128)
    cx.run_hook()

    def blk(b):
        t0 = b * TB
        x_t, x_b = x_ring.next()
        S.dma("sp", x_t[:, :, :], xv[:, :, t0:t0 + TB], writes=[x_b])
        cos_t, cos_b = cos_ring.next()
        sin_t, sin_b = sin_ring.next()
        S.dma("sp", cos_t[:, :], cosd[:, t0:t0 + TB], writes=[cos_b])
        S.dma("sp", sin_t[:, :], sind[:, t0:t0 + TB], writes=[sin_b])
        h_t, h_b = h_ring.next()
        emit_rmsnorm(cx, x_t, x_b, KC, TB, g_t, g_b, ones_t, ones_b, sq_ring, st_ring, rstd_ring, h_t, h_b, D)
        yield
        qo_t, qo_b = qo_ring.next()
        for c in (range(5) if 'q' in parts else []):
            pq_t, pq_b = pq_ring.next()
            for k in range(KC):
                S.op("pe", lambda h, c=c, k=k, pq_t=pq_t, h_t=h_t: h.matmul(
                    pq_t[:, 0:TB], lhsT=wb[:, k, c * 128:(c + 1) * 128], rhs=h_t[:, k, :],
                    start=(k == 0), stop=(k == KC - 1)), reads=[h_b, w_bufs[k]], writes=[pq_b])
            qb_t, qb_b = qb_ring.next()
            S.op("act", lambda h, pq_t=pq_t, qb_t=qb_t: h.activation(out=qb_t[:, :], in_=pq_t[:, 0:TB], func=AF.Copy),
                 reads=[pq_b], writes=[qb_b])
            if qlvl == 1:
                S.op("act", lambda h, c=c, pq_t=pq_t, qo_t=qo_t: h.activation(out=qo_t[:, c, :], in_=pq_t[:, 0:TB], func=AF.Copy),
                     reads=[pq_b], writes=[qo_b])
                continue
            pr_t, pr_b = pr_ring.next()
            S.op("pe", lambda h, pr_t=pr_t, qb_t=qb_t: h.matmul(pr_t[:, 0:TB], lhsT=rot_t[:, :], rhs=qb_t[:, :],
                                                               start=True, stop=True),
                 reads=[qb_b, rot_b], writes=[pr_b])
            t1_t, t1_b = t1_ring.next()
            t2_t, t2_b = t2_ring.next()
            if qlvl == 2:
                S.op("act", lambda h, c=c, pr_t=pr_t, qo_t=qo_t: h.activation(out=qo_t[:, c, :], in_=pr_t[:, 0:TB], func=AF.Copy),
                     reads=[pr_b], writes=[qo_b])
                continue
            S.op("dve", lambda h, t1_t=t1_t, pq_t=pq_t, cos_t=cos_t: h.tensor_tensor(
                out=t1_t[:, :], in0=pq_t[:, 0:TB], in1=cos_t[:, :], op=ALU.mult), reads=[pq_b, cos_b], writes=[t1_b])
            if qlvl == 3:
                S.op("act", lambda h, c=c, t1_t=t1_t, qo_t=qo_t: h.activation(out=qo_t[:, c, :], in_=t1_t[:, :], func=AF.Copy),
                     reads=[t1_b], writes=[qo_b])
                continue
            S.op("dve", lambda h, t2_t=t2_t, pr_t=pr_t, sin_t=sin_t: h.tensor_tensor(
                out=t2_t[:, :], in0=pr_t[:, 0:TB], in1=sin_t[:, :], op=ALU.mult), reads=[pr_b, sin_b], writes=[t2_b])
            S.op("dve", lambda h, c=c, qo_t=qo_t, t1_t=t1_t, t2_t=t2_t: h.tensor_tensor(
                out=qo_t[:, c, :], in0=t1_t[:, :], in1=t2_t[:, :], op=ALU.add), reads=[t1_b, t2_b], writes=[qo_b])
        if 'q' in parts:
            S.dma("pool", QsT[:, :, t0:t0 + TB], qo_t[:, 0:4, :], reads=[qo_b])
            S.dma("pool", KT[:, t0:t0 + TB], qo_t[:, 4, :], reads=[qo_b])
        yield
        uo_t, uo_b = uo_ring.next()
        for gi in (range(4) if 'u' in parts else []):
            pq_t, pq_b = pq_ring.next()
            for k in range(KC):
                S.op("pe", lambda h, gi=gi, k=k, pq_t=pq_t, h_t=h_t: h.matmul(
                    pq_t[:, 0:TB], lhsT=wb[:, k, 768 + gi * 128:768 + (gi + 1) * 128], rhs=h_t[:, k, :],
                    start=(k == 0), stop=(k == KC - 1)), reads=[h_b, w_bufs[k]], writes=[pq_b])
            S.op("act", lambda h, gi=gi, pq_t=pq_t, uo_t=uo_t: h.activation(out=uo_t[:, gi, :], in_=pq_t[:, 0:TB], func=AF.Copy),
                 reads=[pq_b], writes=[uo_b])
        if 'u' in parts:
            S.dma("pool", UT[:, :, t0:t0 + TB], uo_t[:, :, :], reads=[uo_b])
        if 'v' not in parts:
            return
        yield
        vo_t, vo_b = vo_ring.next()
        pv_t, pv_b = pv_ring.next()
        for ti in range(TB // 128):
            for k in range(KC):
                S.op("pe", lambda h, ti=ti, k=k, pv_t=pv_t, h_t=h_t: h.matmul(
                    pv_t[:, ti * 128:(ti + 1) * 128], lhsT=h_t[:, k, ti * 128:(ti + 1) * 128], rhs=wb[:, k, 640:768],
                    start=(k == 0), stop=(k == KC - 1)), reads=[h_b, w_bufs[k]], writes=[pv_b])
        for ti in range(TB // 128):
            for j in range(2):
                S.op("act", lambda h, ti=ti, j=j, pv_t=pv_t, vo_t=vo_t: h.activation(
                    out=vo_t[:, ti, j * 65:j * 65 + 64], in_=pv_t[:, ti * 128 + j * 64:ti * 128 + (j + 1) * 64], func=AF.Copy),
                    reads=[pv_b], writes=[vo_b])
        S.dma("pool", Vaug[t0:t0 + TB, :].rearrange("(i p) n -> p i n", p=128), vo_t[:, :, :], reads=[vo_b])
        yield

    order = list(range(NB)) if order is None else order
    if mid_hook is not None:
        run_interleaved((blk(b) for b in order[:2]), 2, 2)
        mid_hook()
        run_interleaved((blk(b) for b in order[2:]), 3, 2)
    else:
        run_interleaved((blk(b) for b in order), 3, 2)
    cx.pop()
    return cx.finish() if own else None


def rope_tables(pos, half, nrows):
    inv = (np.float32(10000.0) ** (-np.arange(half, dtype=np.float32) / np.float32(half))).astype(np.float32)
    ang = pos.astype(np.float32)[None, :] * inv[np.arange(nrows) % half][:, None]
    return np.cos(ang).astype(np.float32), np.sin(ang).astype(np.float32)


def rot_matrix(dh, nrows=128):
    R = np.zeros((nrows, nrows), np.float32)
    half = dh // 2
    for m in range(nrows):
        d = m % dh
        base = m - d
        if d < half:
            R[base + d + half, m] = -1.0
        else:
            R[base + d - half, m] = 1.0
    return R


def build_eb(ntok=TOK, cx=None):
    TB = 512
    NB = ntok // TB
    NT = ntok // 128
    own = cx is None
    cx = Ctx() if own else cx
    cx.push()
    S = cx.S
    QsT = cx.dram_in("QsT", [128, 4, ntok], BF16)
    KTh = cx.dram_in("KTh", [128, ntok + 256], BF16)
    Vh = cx.dram_in("Vh", [ntok + 256, 130], BF16)
    UTh = cx.dram_in("UTh", [128, 4, ntok + 16], BF16)
    xT = cx.dram_in("xT", [D, ntok])
    w_pool = cx.dram_in("w_pool", [4, 128, 128])
    pscale = cx.dram_in("pscale", [128, 4])
    w_out = cx.dram_in("w_out", [D, D])
    sinkrow = cx.dram_in("sinkrow", [1, 2, 512])
    masksd = cx.dram_in("masks", [4, 128, 512])
    invcd = cx.dram_in("invc", [128, 2, 4, 16])
    identd = cx.dram_in("ident", [128, 128])
    oT = cx.dram_out("oT", [D, ntok])

    woA = cx.sb("woA", [128, 4, D], BF16); woA_b = Buf("woA")
    woB = cx.sb("woB", [128, 4, D], BF16); woB_b = Buf("woB")
    wp = cx.sb("wp", [128, 4, 128], BF16); wp_b = Buf("wp")
    ps_t = cx.sb("ps", [128, 4], F32); ps_b = Buf("ps")
    mk_t = cx.sb("mk", [128, 4, 512], BF16); mk_b = Buf("mk")
    id_t = cx.sb("ident", [128, 128], BF16); id_b = Buf("ident")
    invc_t = cx.sb("invc", [128, 2, 4, 16], F32); invc_b = Buf("invc")
    sk_t = cx.sb("sk", [1, 2, 512], F32); sk_b = Buf("sk")
    esk_t = cx.sb("esk", [1, 2, 512], BF16); esk_b = Buf("esk")
    sel_t = cx.sb("sel", [1, 128], BF16); sel_b = Buf("sel")
    ones32 = cx.sb("ones32", [128, 64], F32); ones32_b = Buf("ones32")
    qsA_ring = mk_ring(cx, "sb", "qsA", 2, [128, 4, TB], BF16)
    qsB_ring = mk_ring(cx, "sb", "qsB", 2, [128, 4, TB], BF16)
    kt_ring = mk_ring(cx, "sb", "kt", 2, [128, 6 * 128], BF16)
    v_ring = mk_ring(cx, "sb", "v", 2, [128, 6, 130], BF16)
    u_ring = mk_ring(cx, "sb", "u", 2, [128, 4, TB + 16], BF16)
    x_ring = mk_ring(cx, "sb", "x", 2, [128, KC, TB], F32)
    p_ring = mk_ring(cx, "sb", "p", 4, [128, 512], BF16)
    osb_ring = mk_ring(cx, "sb", "osb", 3, [128, 512], F32)
    rc_ring = mk_ring(cx, "sb", "rc", 3, [128, 512], F32)
    ya_ring = mk_ring(cx, "sb", "ya", 2, [64, 8, TB], BF16)
    yp_ring = mk_ring(cx, "sb", "yp", 2, [128, 4, TB], BF16)
    yb_ring = mk_ring(cx, "sb", "yb", 2, [128, 4, TB], BF16)
    d_ring = mk_ring(cx, "sb", "d", 2, [128, 4, TB], BF16)
    tmp_rings = [mk_ring(cx, "sb", f"tp{g}", 2, [128, TB + 16], F32) for g in range(4)]
    e16_ring = mk_ring(cx, "sb", "e16", 2, [128, 16], F32)
    s_ring = mk_ring(cx, "ps", "s", 3, [128, 512], F32)
    o_ring = mk_ring(cx, "ps", "o", 2, [128, 512], F32)
    bc_ring = mk_ring(cx, "ps", "bc", 1, [128, 512], F32)
    y_ring = mk_ring(cx, "ps", "y", 2, [128, 512], F32)

    S.dma(cx.bind.get("_wq", "pool"), woA[:, :, :], w_out[0:512, :].rearrange("(i p) n -> p i n", p=128), writes=[woA_b])
    S.dma(cx.bind.get("_wq", "pool"), woB[:, :, :], w_out[512:1024, :].rearrange("(g p) n -> p g n", p=128), writes=[woB_b])
    S.dma("pool", wp[:, :, :], w_pool.rearrange("g i j -> i g j"), writes=[wp_b])
    S.dma("pool", mk_t[:, :, :], masksd.rearrange("m p n -> p m n"), writes=[mk_b])
    S.op("dve", lambda h: h.tensor_scalar(out=mk_t[:, :, :], in0=mk_t[:, :, :], scalar1=-1.0, scalar2=30000.0, op0=ALU.add, op1=ALU.mult),
         reads=[mk_b], writes=[mk_b])
    S.dma("pool", id_t[:, :], identd[:, :], writes=[id_b])
    S.dma("sp", ps_t[:, :], pscale[:, :], writes=[ps_b])
    S.dma("sp", invc_t[:, :, :, :], invcd[:, :, :, :], writes=[invc_b])
    S.dma("sp", sk_t[:, :, :], sinkrow[:, :, :], writes=[sk_b])
    S.op("act", lambda h: h.activation(out=esk_t[:, :, :], in_=sk_t[:, :, :], func=AF.Exp), reads=[sk_b], writes=[esk_b])
    S.op("pool", lambda h: h.memset(sel_t[:, :], 0.0), writes=[sel_b])
    S.op("pool", lambda h: h.memset(sel_t[:, 64:65], 1.0), writes=[sel_b])
    S.op("pool", lambda h: h.memset(ones32[:, :], 1.0), writes=[ones32_b])
    for (qt, qb_) in qsA_ring.items:
        S.op("pool", lambda h, qt=qt: h.memset(qt[64:128, :, :], 0.0), writes=[qb_])
    for (qt, qb_) in qsB_ring.items:
        S.op("pool", lambda h, qt=qt: h.memset(qt[0:64, :, :], 0.0), writes=[qb_])
    xv = xT.rearrange("(c p) t -> p c t", p=128)
    ov = oT.rearrange("(c p) t -> p c t", p=128)
    cx.run_hook()

    def blk(b):
        t0 = b * TB
        qsA_t, qsA_b = qsA_ring.next()
        qsB_t, qsB_b = qsB_ring.next()
        kt_t, kt_b = kt_ring.next()
        v_t, v_b = v_ring.next()
        u_t, u_b = u_ring.next()
        x_t, x_b = x_ring.next()
        S.dma("sp", qsA_t[0:64, :, :], QsT[0:64, :, t0:t0 + TB], writes=[qsA_b])
        S.dma("sp", qsB_t[64:128, :, :], QsT[64:128, :, t0:t0 + TB], writes=[qsB_b])
        S.dma("sp", kt_t[:, :], KTh[:, t0:t0 + 768], writes=[kt_b])
        S.dma("sp", v_t[:, :, :], Vh[t0:t0 + 768, :].rearrange("(i p) n -> p i n", p=128), writes=[v_b])
        S.dma("sp", u_t[:, :, :], UTh[:, :, t0:t0 + TB + 16], writes=[u_b])
        S.dma("sp", x_t[:, :, :], xv[:, :, t0:t0 + TB], writes=[x_b])
        yield
        ya_t, ya_b = ya_ring.next()
        tiles = [(nl, j, mi, dm) for nl in range(4) for mi, dm in enumerate((-1, 0, 1)) for j in range(2)]
        LA = 2
        st = {}
        unit_o = {}
        deferred = []

        def emit_S(t):
            nl, j, mi, dm = tiles[t]
            i = nl + dm + 1
            s_t, s_b = s_ring.next()
            q_t, q_b = (qsA_t, qsA_b) if j == 0 else (qsB_t, qsB_b)
            S.op("pe", lambda h: h.matmul(s_t[:, :], lhsT=kt_t[:, i * 128:(i + 1) * 128],
                                          rhs=q_t[:, :, nl * 128:(nl + 1) * 128], start=True, stop=(dm == 0)),
                 reads=[kt_b, q_b], writes=[s_b])
            if dm != 0:
                n_ = 4 * b + nl
                if dm == -1:
                    mi_ = 2 if n_ == 0 else 0
                else:
                    mi_ = 3 if n_ == NT - 1 else 1
                S.op("pe", lambda h: h.matmul(s_t[:, :], lhsT=id_t[:, :], rhs=mk_t[:, mi_, :], start=False, stop=True),
                     reads=[id_b, mk_b], writes=[s_b])
            st[t] = (s_t, s_b)

        def flush_deferred():
            while deferred:
                (o_t, o_b, osb_t, osb_b, rc_t, rc_b, nl, j) = deferred.pop(0)
                S.op("act", lambda h: h.activation(out=rc_t[64:65, :], in_=osb_t[64:65, :], func=AF.Ln), reads=[osb_b], writes=[rc_b])
                S.op("act", lambda h: h.activation(out=rc_t[64:65, :], in_=rc_t[64:65, :], func=AF.Exp, scale=-1.0),
                     reads=[rc_b], writes=[rc_b])
                bc_t, bc_b = bc_ring.next()
                S.op("pe", lambda h: h.matmul(bc_t[0:64, :], lhsT=ones32[64:65, 0:64], rhs=rc_t[64:65, :], start=True, stop=True),
                     reads=[rc_b, ones32_b], writes=[bc_b])
                S.op("dve", lambda h: h.tensor_tensor(
                    out=ya_t[0:64, j * 4:(j + 1) * 4, nl * 128:(nl + 1) * 128],
                    in0=osb_t[0:64, :].rearrange("p (c q) -> p c q", c=4),
                    in1=bc_t[0:64, :].rearrange("p (c q) -> p c q", c=4), op=ALU.mult),
                    reads=[osb_b, bc_b], writes=[ya_b])

        for t in range(min(LA, len(tiles))):
            emit_S(t)
        for t in range(len(tiles)):
            nl, j, mi, dm = tiles[t]
            n = 4 * b + nl
            i = nl + dm + 1
            if mi == 0:
                unit_o[(nl, j)] = o_ring.next()
            o_t, o_b = unit_o[(nl, j)]
            s_t, s_b = st.pop(t)
            p_t, p_b = p_ring.next()
            S.op("act", lambda h: h.activation(out=p_t[:, :], in_=s_t[:, :], func=AF.Exp, scale=0.125), reads=[s_b], writes=[p_b])
            if t + LA < len(tiles):
                emit_S(t + LA)
            S.op("pe", lambda h: h.matmul(o_t[0:65, :], lhsT=v_t[:, i, j * 65:(j + 1) * 65], rhs=p_t[:, :], start=(mi == 0), stop=False),
                 reads=[v_b, p_b], writes=[o_b])
            if mi == 0 and j == 1:
                flush_deferred()
            if mi == 2:
                S.op("pe", lambda h: h.matmul(o_t[0:65, :], lhsT=sel_t[0:1, 0:65], rhs=esk_t[0:1, j, :], start=False, stop=True),
                     reads=[sel_b, esk_b], writes=[o_b])
                osb_t, osb_b = osb_ring.next()
                rc_t, rc_b = rc_ring.next()
                S.op("dve", lambda h: h.tensor_copy(out=osb_t[0:65, :], in_=o_t[0:65, :]), reads=[o_b], writes=[osb_b])
                deferred.append((o_t, o_b, osb_t, osb_b, rc_t, rc_b, nl, j))
        flush_deferred()
        yield
        yp_t, yp_b = yp_ring.next()
        S.dma("sp", yp_t[0:64, :, :], ya_t[0:64, 0:8:2, :], reads=[ya_b], writes=[yp_b])
        S.dma("sp", yp_t[64:128, :, :], ya_t[0:64, 1:8:2, :], reads=[ya_b], writes=[yp_b])
        d_t, d_b = d_ring.next()
        L = TB + 16
        for g in range(4):
            w = 2 << g
            steps = g + 1
            src_t, src_b, ln = None, None, L
            for s_i in range(steps):
                sh = 1 << s_i
                tp_t, tp_b = tmp_rings[g].next()
                nl_ = ln - sh
                if s_i == 0:
                    S.op("pool", lambda h, tp_t=tp_t, u_t=u_t, g=g, nl_=nl_, sh=sh: h.tensor_tensor(
                        out=tp_t[:, 0:nl_], in0=u_t[:, g, 0:nl_], in1=u_t[:, g, sh:sh + nl_], op=ALU.add),
                        reads=[u_b], writes=[tp_b])
                else:
                    S.op("pool", lambda h, tp_t=tp_t, src_t=src_t, nl_=nl_, sh=sh: h.tensor_tensor(
                        out=tp_t[:, 0:nl_], in0=src_t[:, 0:nl_], in1=src_t[:, sh:sh + nl_], op=ALU.add),
                        reads=[src_b], writes=[tp_b])
                src_t, src_b, ln = tp_t, tp_b, nl_
            off = 8 - w // 2
            S.op("dve", lambda h, d_t=d_t, src_t=src_t, u_t=u_t, g=g, off=off, w=w: h.scalar_tensor_tensor(
                out=d_t[:, g, :], in0=src_t[:, off:off + TB], scalar=1.0 / w, in1=u_t[:, g, 8:8 + TB],
                op0=ALU.mult, op1=ALU.subtract), reads=[src_b, u_b], writes=[d_b])
            for (is_edge, which, c0) in ((b == 0, 0, 0), (b == NB - 1, 1, TB - 16)):
                if not is_edge:
                    continue
                e_t, e_b = e16_ring.next()
                S.op("dve", lambda h, e_t=e_t, src_t=src_t, g=g, off=off, c0=c0, which=which: h.tensor_tensor(
                    out=e_t[:, :], in0=src_t[:, off + c0:off + c0 + 16], in1=invc_t[:, which, g, :], op=ALU.mult),
                    reads=[src_b, invc_b], writes=[e_b])
                S.op("dve", lambda h, e_t=e_t, d_t=d_t, u_t=u_t, g=g, c0=c0: h.tensor_tensor(
                    out=d_t[:, g, c0:c0 + 16], in0=e_t[:, :], in1=u_t[:, g, 8 + c0:8 + c0 + 16], op=ALU.subtract),
                    reads=[e_b, u_b, d_b], writes=[d_b])
        yield
        yb_t, yb_b = yb_ring.next()
        for g in range(4):
            y_t, y_b = y_ring.next()
            S.op("pe", lambda h, y_t=y_t, d_t=d_t, g=g: h.matmul(y_t[:, :], lhsT=wp[:, g, :], rhs=d_t[:, g, :], start=True, stop=True),
                 reads=[wp_b, d_b], writes=[y_b])
            S.op("dve", lambda h, y_t=y_t, yb_t=yb_t, g=g: h.tensor_scalar_mul(out=yb_t[:, g, :], in0=y_t[:, :], scalar1=ps_t[:, g:g + 1]),
                 reads=[y_b, ps_b], writes=[yb_b])
        for o in range(KC):
            y_t, y_b = y_ring.next()
            for hh in range(4):
                S.op("pe", lambda h, y_t=y_t, yp_t=yp_t, hh=hh, o=o: h.matmul(
                    y_t[:, :], lhsT=woA[:, hh, o * 128:(o + 1) * 128], rhs=yp_t[:, hh, :], start=(hh == 0), stop=False),
                    reads=[woA_b, yp_b], writes=[y_b])
            for g in range(4):
                S.op("pe", lambda h, y_t=y_t, yb_t=yb_t, g=g, o=o: h.matmul(
                    y_t[:, :], lhsT=woB[:, g, o * 128:(o + 1) * 128], rhs=yb_t[:, g, :], start=False, stop=(g == 3)),
                    reads=[woB_b, yb_b], writes=[y_b])
            S.op("dve", lambda h, y_t=y_t, x_t=x_t, o=o: h.tensor_tensor(out=x_t[:, o, :], in0=y_t[:, :], in1=x_t[:, o, :], op=ALU.add),
                 reads=[y_b, x_b], writes=[x_b])
        S.dma("sp", ov[:, :, t0:t0 + TB], x_t[:, :, :], reads=[x_b])
        yield

    run_interleaved((blk(b) for b in range(NB)), 2, 2)
    cx.pop()
    return cx.finish() if own else None


def eb_masks(has_left, has_right):
    ki = np.arange(128)[:, None]
    qi = np.arange(128)[None, :]
    mL = np.tile((ki >= qi).astype(np.float32), (1, 4))
    mR = np.tile((ki <= qi).astype(np.float32), (1, 4))
    return np.stack([mL, mR, mL * float(has_left), mR * float(has_right)]).astype(np.float32)


def eb_invc(is_first, is_last):
    out = np.zeros((128, 2, 4, 16), np.float32)
    for g in range(4):
        w = 2 << g
        half = w // 2
        for i in range(16):
            c0 = min(i + half, w) if is_first else w
            r = 16 - i
            c1 = min(half + r, w) if is_last else w
            out[:, 0, g, i] = 1.0 / c0
            out[:, 1, g, i] = 1.0 / c1
    return out


def emit_rope(cx, src_t, src_b, nrow, TB, rot_t, rot_b, cos_t, cos_b, sin_t, sin_b, qb_ring, pr_ring, t1_ring, t2_ring,
              out_ap, out_b):
    S = cx.S
    qb_t, qb_b = qb_ring.next()
    S.op("act", lambda h: h.activation(out=qb_t[0:nrow, :], in_=src_t[0:nrow, 0:TB], func=AF.Copy), reads=[src_b], writes=[qb_b])
    pr_t, pr_b = pr_ring.next()
    S.op("pe", lambda h: h.matmul(pr_t[0:nrow, 0:TB], lhsT=rot_t[0:nrow, 0:nrow], rhs=qb_t[0:nrow, :], start=True, stop=True),
         reads=[qb_b, rot_b], writes=[pr_b])
    t1_t, t1_b = t1_ring.next()
    t2_t, t2_b = t2_ring.next()
    S.op("dve", lambda h: h.tensor_tensor(out=t1_t[0:nrow, :], in0=src_t[0:nrow, 0:TB], in1=cos_t[0:nrow, :], op=ALU.mult),
         reads=[src_b, cos_b], writes=[t1_b])
    S.op("dve", lambda h: h.tensor_tensor(out=t2_t[0:nrow, :], in0=pr_t[0:nrow, 0:TB], in1=sin_t[0:nrow, :], op=ALU.mult),
         reads=[pr_b, sin_b], writes=[t2_b])
    S.op("dve", lambda h: h.tensor_tensor(out=out_ap, in0=t1_t[0:nrow, :], in1=t2_t[0:nrow, :], op=ALU.add),
         reads=[t1_b, t2_b], writes=[out_b])


def build_oa(ntok=TOK, cx=None, mid_hook=None):
    TB = 512
    NB = ntok // TB
    NT = ntok // 128
    own = cx is None
    cx = Ctx() if own else cx
    cx.push()
    S = cx.S
    xT = cx.dram_in("xT", [D, ntok])
    xhalo = cx.dram_in("xhalo", [D, 4])
    w_in = cx.dram_in("w_in", [D, 1440])
    gin = cx.dram_in("g", [128, KC])
    gcq = cx.dram_in("g_cq", [128, 2])
    gckv = cx.dram_in("g_ckv", [128, 1])
    w_uq = cx.dram_in("w_uq", [256, 768])
    w_ukv = cx.dram_in("w_ukv", [128, 1024])
    cwd = cx.dram_in("cw", [128, 4, 4])
    cbd = cx.dram_in("cb", [128, 4])
    wad = cx.dram_in("wa", [2, 8, 64, 64])
    wxd = cx.dram_in("wx", [2, 8, 64, 64])
    bad = cx.dram_in("ba", [128, 2, 4])
    bxd = cx.dram_in("bx", [128, 2, 4])
    lamd = cx.dram_in("lam", [128, 2, 4])
    cosd = cx.dram_in("cos", [128, ntok])
    sind = cx.dram_in("sin", [128, ntok])
    rotd = cx.dram_in("rot", [128, 128])
    QN = cx.dram_out("QN", [512, ntok], BF16)
    QR = cx.dram_out("QR", [256, ntok], BF16)
    KNR = cx.dram_out("KNR", [544, ntok], BF16)
    V5 = cx.dram_out("V5", [1024, NT * 65], BF16)
    GX = cx.dram_out("GX", [512, ntok], BF16)
    AB = cx.dram_out("AB", [2, 2, 512, ntok])
    BLK = cx.dram_out("BLK", [128, NB, 2, 2, 4])
    CAB = cx.dram_out("CAB", [128, 2, 2, 4])
    XR = cx.dram_out("XR", [512, ntok])

    g_t = cx.sb("g", [128, KC], F32); g_b = Buf("g")
    gcq_t = cx.sb("gcq", [128, 2], F32); gcq_b = Buf("gcq")
    gckv_t = cx.sb("gckv", [128, 1], F32); gckv_b = Buf("gckv")
    ones_t = cx.sb("ones", [128, 128], BF16); ones_b = Buf("ones")
    xrh_t = cx.sb("xrh", [128, 4, 4], F32); xrh_b = Buf("xrh")
    cp_t = cx.sb("cp", [128, 2, 4], F32); cp_b = Buf("cp")
    blk_t = cx.sb("blk", [128, NB, 2, 2, 4], F32); blk_b = Buf("blk")
    S.dma("sp", g_t[:, :], gin[:, :], writes=[g_b])
    S.op("dve", lambda h: h.tensor_scalar_mul(out=g_t[:, :], in0=g_t[:, :], scalar1=float(np.sqrt(D))), reads=[g_b], writes=[g_b])
    S.dma("sp", gcq_t[:, :], gcq[:, :], writes=[gcq_b])
    S.op("dve", lambda h: h.tensor_scalar_mul(out=gcq_t[:, :], in0=gcq_t[:, :], scalar1=16.0), reads=[gcq_b], writes=[gcq_b])
    S.dma("sp", gckv_t[:, :], gckv[:, :], writes=[gckv_b])
    S.op("dve", lambda h: h.tensor_scalar_mul(out=gckv_t[:, :], in0=gckv_t[:, :], scalar1=float(np.sqrt(128.0))),
         reads=[gckv_b], writes=[gckv_b])
    S.op("pool", lambda h: h.memset(ones_t[:, :], 1.0), writes=[ones_b])
    S.dma("sp", cp_t[:, :, :], lamd[:, :, :], writes=[cp_b])
    S.op("act", lambda h: h.activation(out=cp_t[:, :, :], in_=cp_t[:, :, :], func=AF.Exp, scale=-1.0), reads=[cp_b], writes=[cp_b])
    S.op("act", lambda h: h.activation(out=cp_t[:, :, :], in_=cp_t[:, :, :], func=AF.Ln, bias=1.0), reads=[cp_b], writes=[cp_b])
    S.op("dve", lambda h: h.tensor_scalar_mul(out=cp_t[:, :, :], in0=cp_t[:, :, :], scalar1=-8.0), reads=[cp_b], writes=[cp_b])

    wabd = cx.sb("wabd", [128, 2, 4, 128], BF16); wxbd = cx.sb("wxbd", [128, 2, 4, 128], BF16); bd_b = Buf("bd")
    cw_t = cx.sb("cw", [128, 4, 4], F32); cb_t = cx.sb("cb", [128, 4], F32); cw_b = Buf("cw")
    ba_t = cx.sb("ba", [128, 2, 4], F32); bx_t = cx.sb("bx", [128, 2, 4], F32); bb_b = Buf("bb")
    S.op("pool", lambda h: h.memset(wabd[:, :, :, :], 0.0), writes=[bd_b])
    S.op("pool", lambda h: h.memset(wxbd[:, :, :, :], 0.0), writes=[bd_b])
    for d in range(2):
        for c in range(4):
            for hf in range(2):
                S.dma("pool", wabd[hf * 64:(hf + 1) * 64, d, c, hf * 64:(hf + 1) * 64], wad[d, 2 * c + hf, :, :], writes=[bd_b])
                S.dma("pool", wxbd[hf * 64:(hf + 1) * 64, d, c, hf * 64:(hf + 1) * 64], wxd[d, 2 * c + hf, :, :], writes=[bd_b])
    S.dma("sp", cw_t[:, :, :], cwd[:, :, :], writes=[cw_b])
    S.dma("sp", cb_t[:, :], cbd[:, :], writes=[cw_b])
    S.dma("sp", ba_t[:, :, :], bad[:, :, :], writes=[bb_b])
    S.dma("sp", bx_t[:, :, :], bxd[:, :, :], writes=[bb_b])
    xv = xT.rearrange("(c p) t -> p c t", p=128)
    cx.push()
    wb = cx.sb("wb", [128, KC, 1440], BF16)
    w_bufs = [Buf(f"w{k}") for k in range(KC)]
    wuqn = cx.sb("wuqn", [128, 2, 512], BF16); wuqr = cx.sb("wuqr", [128, 2, 256], BF16); wuq_b = Buf("wuq")
    wk = cx.sb("wk", [128, 512], BF16); wv = cx.sb("wv", [128, 512], BF16); wkv_b = Buf("wkv")
    rot_t = cx.sb("rot", [128, 128], BF16); rot_b = Buf("rot")
    x_ring = mk_ring(cx, "sb", "x", 2, [128, KC, TB], F32)
    h_ring = mk_ring(cx, "sb", "h", 2, [128, KC, TB], BF16)
    sq_ring = mk_ring(cx, "sb", "sq", 3, [128, TB], BF16)
    rstd_ring = mk_ring(cx, "sb", "rstd", 2, [128, TB], F32)
    cos_ring = mk_ring(cx, "sb", "cos", 2, [128, TB], F32)
    sin_ring = mk_ring(cx, "sb", "sin", 2, [128, TB], F32)
    qb_ring = mk_ring(cx, "sb", "qb", 2, [128, TB], BF16)
    t1_ring = mk_ring(cx, "sb", "t1", 2, [128, TB], F32)
    t2_ring = mk_ring(cx, "sb", "t2", 2, [128, TB], F32)
    cq_ring = mk_ring(cx, "sb", "cq", 2, [128, 2, TB], F32)
    ckv_ring = mk_ring(cx, "sb", "ckv", 2, [128, 1, TB], F32)
    cqn_ring = mk_ring(cx, "sb", "cqn", 2, [128, 2, TB], BF16)
    ckvn_ring = mk_ring(cx, "sb", "ckvn", 2, [128, 1, TB], BF16)
    xr_ring = mk_ring(cx, "sb", "xr", 2, [128, 4, TB], F32)
    gx_ring = mk_ring(cx, "sb", "gx", 2, [128, 4, TB], BF16)
    qn_ring = mk_ring(cx, "sb", "qn", 2, [128, 4, TB], BF16)
    qr_ring = mk_ring(cx, "sb", "qr", 2, [128, 2, TB], BF16)
    kn_ring = mk_ring(cx, "sb", "kn", 2, [128, 4, TB], BF16)
    kr_ring = mk_ring(cx, "sb", "kr", 2, [32, TB], BF16)
    vo_ring = mk_ring(cx, "sb", "vo", 2, [128, 4, 520], BF16)
    hx_t = cx.sb("hx", [128, KC, 4], F32); hx_b = Buf("hx")
    hh_t = cx.sb("hh", [128, KC, 4], BF16); hh_b = Buf("hh")
    st_ring = mk_ring(cx, "ps", "st", 1, [128, 512], F32)
    pq_ring = mk_ring(cx, "ps", "pq", 4, [128, 512], F32)
    pr_ring = mk_ring(cx, "ps", "pr", 1, [128, 512], F32)
    pv_ring = mk_ring(cx, "ps", "pv", 2, [128, 512], F32)

    for k in range(KC):
        S.dma(cx.bind.get("_wq", "pool"), wb[:, k, :], w_in[k * 128:(k + 1) * 128, :], writes=[w_bufs[k]])
    for k in range(2):
        src = w_uq[k * 128:(k + 1) * 128, :].rearrange("p (h e) -> p h e", e=96)
        S.dma("pool", wuqn[:, k, :].rearrange("p (h d) -> p h d", d=64), src[:, :, 0:64], writes=[wuq_b])
        S.dma("pool", wuqr[:, k, :].rearrange("p (h d) -> p h d", d=32), src[:, :, 64:96], writes=[wuq_b])
    srckv = w_ukv.rearrange("p (h e) -> p h e", e=128)
    S.dma("pool", wk[:, :].rearrange("p (h d) -> p h d", d=64), srckv[:, :, 0:64], writes=[wkv_b])
    S.dma("pool", wv[:, :].rearrange("p (h d) -> p h d", d=64), srckv[:, :, 64:128], writes=[wkv_b])
    S.dma("pool", rot_t[:, :], rotd[:, :], writes=[rot_b])
    for (vt, vb) in vo_ring.items:
        S.op("pool", lambda h, vt=vt: h.memset(vt[:, :, :], 1.0), writes=[vb])

    def proj_tile(h_t, h_b, c0, ncols, TBx):
        pq_t, pq_b = pq_ring.next()
        for k in range(KC):
            S.op("pe", lambda h, k=k: h.matmul(pq_t[0:ncols, 0:TBx], lhsT=wb[:, k, c0:c0 + ncols], rhs=h_t[:, k, 0:TBx],
                                               start=(k == 0), stop=(k == KC - 1)), reads=[h_b, w_bufs[k]], writes=[pq_b])
        return pq_t, pq_b

    S.dma("sp", hx_t[:, :, :], xhalo.rearrange("(c p) t -> p c t", p=128), writes=[hx_b])
    emit_rmsnorm(cx, hx_t, hx_b, KC, 4, g_t, g_b, ones_t, ones_b, sq_ring, st_ring, rstd_ring, hh_t, hh_b, D)
    for c in range(4):
        pq_t, pq_b = proj_tile(hh_t, hh_b, 416 + c * 128, 128, 4)
        S.op("act", lambda h, c=c, pq_t=pq_t: h.activation(out=xrh_t[:, c, :], in_=pq_t[:, 0:4], func=AF.Copy),
             reads=[pq_b], writes=[xrh_b])

    def blk(b):
        t0 = b * TB
        x_t, x_b = x_ring.next()
        S.dma("sp", x_t[:, :, :], xv[:, :, t0:t0 + TB], writes=[x_b])
        cos_t, cos_b = cos_ring.next()
        sin_t, sin_b = sin_ring.next()
        S.dma("sp", cos_t[:, :], cosd[:, t0:t0 + TB], writes=[cos_b])
        S.dma("sp", sin_t[:, :], sind[:, t0:t0 + TB], writes=[sin_b])
        h_t, h_b = h_ring.next()
        emit_rmsnorm(cx, x_t, x_b, KC, TB, g_t, g_b, ones_t, ones_b, sq_ring, st_ring, rstd_ring, h_t, h_b, D)
        yield
        cq_t, cq_b = cq_ring.next()
        for c in range(2):
            pq_t, pq_b = proj_tile(h_t, h_b, c * 128, 128, TB)
            S.op("act", lambda h, c=c, pq_t=pq_t, cq_t=cq_t: h.activation(out=cq_t[:, c, :], in_=pq_t[:, 0:TB], func=AF.Copy),
                 reads=[pq_b], writes=[cq_b])
        ckv_t, ckv_b = ckv_ring.next()
        pq_t, pq_b = proj_tile(h_t, h_b, 256, 128, TB)
        S.op("act", lambda h, pq_t=pq_t, ckv_t=ckv_t: h.activation(out=ckv_t[:, 0, :], in_=pq_t[:, 0:TB], func=AF.Copy),
             reads=[pq_b], writes=[ckv_b])
        yield
        pq_t, pq_b = proj_tile(h_t, h_b, 384, 32, TB)
        kr_t, kr_b = kr_ring.next()
        emit_rope(cx, pq_t, pq_b, 32, TB, rot_t, rot_b, cos_t, cos_b, sin_t, sin_b, qb_ring, pr_ring, t1_ring, t2_ring,
                  kr_t[0:32, :], kr_b)
        S.dma("pool", KNR[512:544, t0:t0 + TB], kr_t[:, :], reads=[kr_b])
        yield
        xr_t, xr_b = xr_ring.next()
        gx_t, gx_b = gx_ring.next()
        for c in range(4):
            pq_t, pq_b = proj_tile(h_t, h_b, 416 + c * 128, 128, TB)
            S.op("act", lambda h, c=c, pq_t=pq_t, xr_t=xr_t: h.activation(out=xr_t[:, c, :], in_=pq_t[:, 0:TB], func=AF.Copy),
                 reads=[pq_b], writes=[xr_b])
        for c in range(4):
            pq_t, pq_b = proj_tile(h_t, h_b, 928 + c * 128, 128, TB)
            S.op("act", lambda h, c=c, pq_t=pq_t, gx_t=gx_t: h.activation(out=gx_t[:, c, :], in_=pq_t[:, 0:TB], func=AF.Gelu_apprx_tanh),
                 reads=[pq_b], writes=[gx_b])
        S.dma("pool", XR.rearrange("(c p) t -> p c t", p=128)[:, :, t0:t0 + TB], xr_t[:, :, :], reads=[xr_b])
        S.dma("pool", GX.rearrange("(c p) t -> p c t", p=128)[:, :, t0:t0 + TB], gx_t[:, :, :], reads=[gx_b])
        yield
        cqn_t, cqn_b = cqn_ring.next()
        emit_rmsnorm(cx, cq_t, cq_b, 2, TB, gcq_t, gcq_b, ones_t, ones_b, sq_ring, st_ring, rstd_ring, cqn_t, cqn_b, 256)
        ckvn_t, ckvn_b = ckvn_ring.next()
        emit_rmsnorm(cx, ckv_t, ckv_b, 1, TB, gckv_t, gckv_b, ones_t, ones_b, sq_ring, st_ring, rstd_ring, ckvn_t, ckvn_b, 128)
        yield
        qn_t, qn_b = qn_ring.next()
        for i in range(4):
            pq_t, pq_b = pq_ring.next()
            for k in range(2):
                S.op("pe", lambda h, i=i, k=k, pq_t=pq_t, cqn_t=cqn_t: h.matmul(
                    pq_t[:, 0:TB], lhsT=wuqn[:, k, i * 128:(i + 1) * 128], rhs=cqn_t[:, k, :], start=(k == 0), stop=(k == 1)),
                    reads=[cqn_b, wuq_b], writes=[pq_b])
            S.op("act", lambda h, i=i, pq_t=pq_t, qn_t=qn_t: h.activation(out=qn_t[:, i, :], in_=pq_t[:, 0:TB], func=AF.Copy),
                 reads=[pq_b], writes=[qn_b])
        S.dma("pool", QN.rearrange("(c p) t -> p c t", p=128)[:, :, t0:t0 + TB], qn_t[:, :, :], reads=[qn_b])
        qr_t, qr_b = qr_ring.next()
        for i in range(2):
            pq_t, pq_b = pq_ring.next()
            for k in range(2):
                S.op("pe", lambda h, i=i, k=k, pq_t=pq_t, cqn_t=cqn_t: h.matmul(
                    pq_t[:, 0:TB], lhsT=wuqr[:, k, i * 128:(i + 1) * 128], rhs=cqn_t[:, k, :], start=(k == 0), stop=(k == 1)),
                    reads=[cqn_b, wuq_b], writes=[pq_b])
            emit_rope(cx, pq_t, pq_b, 128, TB, rot_t, rot_b, cos_t, cos_b, sin_t, sin_b, qb_ring, pr_ring, t1_ring, t2_ring,
                      qr_t[:, i, :], qr_b)
        S.dma("pool", QR.rearrange("(c p) t -> p c t", p=128)[:, :, t0:t0 + TB], qr_t[:, :, :], reads=[qr_b])
        yield
        kn_t, kn_b = kn_ring.next()
        for i in range(4):
            pq_t, pq_b = pq_ring.next()
            S.op("pe", lambda h, i=i, pq_t=pq_t, ckvn_t=ckvn_t: h.matmul(
                pq_t[:, 0:TB], lhsT=wk[:, i * 128:(i + 1) * 128], rhs=ckvn_t[:, 0, :], start=True, stop=True),
                reads=[ckvn_b, wkv_b], writes=[pq_b])
            S.op("act", lambda h, i=i, pq_t=pq_t, kn_t=kn_t: h.activation(out=kn_t[:, i, :], in_=pq_t[:, 0:TB], func=AF.Copy),
                 reads=[pq_b], writes=[kn_b])
        S.dma("pool", KNR[0:512, :].rearrange("(c p) t -> p c t", p=128)[:, :, t0:t0 + TB], kn_t[:, :, :], reads=[kn_b])
        yield
        vo_t, vo_b = vo_ring.next()
        for ti in range(TB // 128):
            pv_t, pv_b = pv_ring.next()
            S.op("pe", lambda h, ti=ti, pv_t=pv_t, ckvn_t=ckvn_t: h.matmul(
                pv_t[:, :], lhsT=ckvn_t[:, 0, ti * 128:(ti + 1) * 128], rhs=wv[:, :], start=True, stop=True),
                reads=[ckvn_b, wkv_b], writes=[pv_b])
            S.op("act", lambda h, ti=ti, pv_t=pv_t, vo_t=vo_t: h.activation(
                out=vo_t[:, ti, :].rearrange("p (h e) -> p h e", e=65)[:, :, 0:64],
                in_=pv_t[:, :].rearrange("p (h d) -> p h d", d=64), func=AF.Copy), reads=[pv_b], writes=[vo_b])
        for hd in range(8):
            S.dma("pool", V5[hd * 128:(hd + 1) * 128, :].rearrange("p (i e) -> p i e", e=65)[:, b * 4:(b + 1) * 4, :],
                  vo_t[:, :, hd * 65:(hd + 1) * 65], reads=[vo_b])
        yield

    run_interleaved((blk(b) for b in range(NB)), 2)
    cx.pop()
    if mid_hook is not None:
        mid_hook()

    cx.push()
    xe_ring = mk_ring(cx, "sb", "xe", 2, [128, 4, TB + 4], F32)
    xc_ring = mk_ring(cx, "sb", "xc", 2, [128, 4, TB], F32)
    xcb_ring = mk_ring(cx, "sb", "xcb", 2, [128, 4, TB], BF16)
    r_ring = mk_ring(cx, "sb", "r", 2, [128, 8, TB], F32)
    i_ring = mk_ring(cx, "sb", "i", 2, [128, 8, TB], F32)
    a_ring = mk_ring(cx, "sb", "a", 2, [128, 8, TB], F32)
    b_ring = mk_ring(cx, "sb", "b", 2, [128, 8, TB], F32)
    hl_ring = mk_ring(cx, "sb", "hl", 2, [128, TB], F32)
    sr_ring = mk_ring(cx, "sb", "sr", 2, [128, 8], F32)
    pg_ring = mk_ring(cx, "ps", "pg", 6, [128, 512], F32)
    XRv = XR.rearrange("(c p) t -> p c t", p=128)
    ABv = AB.rearrange("d s (c p) t -> d s p c t", p=128)

    def blk(b):
        t0 = b * TB
        xe_t, xe_b = xe_ring.next()
        lo = 0 if b > 0 else 2
        hi = TB + 3 if b < NB - 1 else TB + 2
        S.dma("sp", xe_t[:, :, lo:hi], XRv[:, :, t0 - 2 + lo:t0 - 2 + hi], writes=[xe_b])
        if b == 0:
            S.op("dve", lambda h, xe_t=xe_t: h.tensor_copy(out=xe_t[:, :, 0:2], in_=xrh_t[:, :, 0:2]), reads=[xrh_b, xe_b], writes=[xe_b])
        if b == NB - 1:
            S.op("dve", lambda h, xe_t=xe_t: h.tensor_copy(out=xe_t[:, :, TB + 2:TB + 3], in_=xrh_t[:, :, 2:3]),
                 reads=[xrh_b, xe_b], writes=[xe_b])
        yield
        xc_t, xc_b = xc_ring.next()
        xcb_t, xcb_b = xcb_ring.next()
        for c in range(4):
            S.op("dve", lambda h, c=c, xc_t=xc_t, xe_t=xe_t: h.tensor_scalar(
                out=xc_t[:, c, :], in0=xe_t[:, c, 0:TB], scalar1=cw_t[:, c, 0:1], scalar2=cb_t[:, c:c + 1],
                op0=ALU.mult, op1=ALU.add), reads=[xe_b, cw_b], writes=[xc_b])
            for j in range(1, 4):
                S.op("dve", lambda h, c=c, j=j, xc_t=xc_t, xe_t=xe_t: h.scalar_tensor_tensor(
                    out=xc_t[:, c, :], in0=xe_t[:, c, j:j + TB], scalar=cw_t[:, c, j:j + 1], in1=xc_t[:, c, :],
                    op0=ALU.mult, op1=ALU.add), reads=[xe_b, cw_b, xc_b], writes=[xc_b])
        S.op("act", lambda h, xc_t=xc_t, xcb_t=xcb_t: h.activation(out=xcb_t[:, :, :], in_=xc_t[:, :, :], func=AF.Copy), reads=[xc_b], writes=[xcb_b])
        yield
        r_t, r_b = r_ring.next()
        i_t, i_b = i_ring.next()
        a_t, a_b = a_ring.next()
        b_t, b_b = b_ring.next()
        sr_t, sr_b = sr_ring.next()
        S.op("dve", lambda h, sr_t=sr_t: h.memset(sr_t[:, :], 0.0), writes=[sr_b])
        for d in range(2):
            for c in range(4):
                q = d * 4 + c
                pg_t, pg_b = pg_ring.next()
                S.op("pe", lambda h, d=d, c=c, pg_t=pg_t, xcb_t=xcb_t: h.matmul(pg_t[:, :], lhsT=wabd[:, d, c, :], rhs=xcb_t[:, c, :],
                                                                         start=True, stop=True), reads=[bd_b, xcb_b], writes=[pg_b])
                S.op("act", lambda h, d=d, c=c, q=q, pg_t=pg_t, r_t=r_t, sr_t=sr_t: h.activation(
                    out=r_t[:, q, :], in_=pg_t[:, :], func=AF.Sigmoid, bias=ba_t[:, d, c:c + 1], accum_out=sr_t[:, q:q + 1]),
                    reads=[pg_b, bb_b], writes=[r_b, sr_b])
                pg_t, pg_b = pg_ring.next()
                S.op("pe", lambda h, d=d, c=c, pg_t=pg_t, xcb_t=xcb_t: h.matmul(pg_t[:, :], lhsT=wxbd[:, d, c, :], rhs=xcb_t[:, c, :],
                                                                         start=True, stop=True), reads=[bd_b, xcb_b], writes=[pg_b])
                S.op("act", lambda h, d=d, c=c, q=q, pg_t=pg_t, i_t=i_t: h.activation(
                    out=i_t[:, q, :], in_=pg_t[:, :], func=AF.Sigmoid, bias=bx_t[:, d, c:c + 1]),
                    reads=[pg_b, bb_b], writes=[i_b])
        yield
        for d in range(2):
            for c in range(4):
                q = d * 4 + c
                S.op("act", lambda h, d=d, c=c, q=q, a_t=a_t, r_t=r_t: h.activation(
                    out=a_t[:, q, :], in_=r_t[:, q, :], func=AF.Exp, scale=cp_t[:, d, c:c + 1]), reads=[r_b, cp_b], writes=[a_b])
                S.op("act", lambda h, d=d, c=c, q=q, sr_t=sr_t, b=b: h.activation(
                    out=blk_t[:, b, d, 0, c:c + 1], in_=sr_t[:, q:q + 1], func=AF.Exp, scale=cp_t[:, d, c:c + 1]),
                    reads=[sr_b, cp_b, blk_b], writes=[blk_b])
        yield
        S.op("dve", lambda h, a_t=a_t, r_t=r_t: h.tensor_tensor(out=r_t[:, :, :], in0=a_t[:, :, :], in1=a_t[:, :, :], op=ALU.mult),
             reads=[a_b, r_b], writes=[r_b])
        S.op("act", lambda h, r_t=r_t: h.activation(out=r_t[:, :, :], in_=r_t[:, :, :], func=AF.Sqrt, scale=-1.0, bias=1.0),
             reads=[r_b], writes=[r_b])
        for d in range(2):
            S.op("dve", lambda h, d=d, i_t=i_t, xc_t=xc_t: h.tensor_tensor(out=i_t[:, d * 4:(d + 1) * 4, :], in0=i_t[:, d * 4:(d + 1) * 4, :],
                                                                        in1=xc_t[:, :, :], op=ALU.mult), reads=[i_b, xc_b], writes=[i_b])
        S.op("dve", lambda h, b_t=b_t, r_t=r_t, i_t=i_t: h.tensor_tensor(out=b_t[:, :, :], in0=r_t[:, :, :], in1=i_t[:, :, :], op=ALU.mult),
             reads=[r_b, i_b], writes=[b_b])
        yield
        for d in range(2):
            for c in range(4):
                q = d * 4 + c
                hl_t, hl_b = hl_ring.next()
                if d == 0:
                    S.op("dve", lambda h, q=q, hl_t=hl_t, a_t=a_t, b_t=b_t: h.tensor_tensor_scan(
                        out=hl_t[:, :], data0=a_t[:, q, :], data1=b_t[:, q, :], initial=0.0, op0=ALU.mult, op1=ALU.add),
                        reads=[a_b, b_b], writes=[hl_b])
                    col = TB - 1
                else:
                    S.op("dve", lambda h, q=q, hl_t=hl_t, a_t=a_t, b_t=b_t: h.tensor_tensor_scan(
                        out=hl_t[:, ::-1], data0=a_t[:, q, ::-1], data1=b_t[:, q, ::-1], initial=0.0, op0=ALU.mult, op1=ALU.add),
                        reads=[a_b, b_b], writes=[hl_b])
                    col = 0
                S.op("act", lambda h, d=d, c=c, hl_t=hl_t, col=col, b=b: h.activation(
                    out=blk_t[:, b, d, 1, c:c + 1], in_=hl_t[:, col:col + 1], func=AF.Copy), reads=[hl_b, blk_b], writes=[blk_b])
        for d in range(2):
            S.dma("act", ABv[d, 0, :, :, t0:t0 + TB], a_t[:, d * 4:(d + 1) * 4, :], reads=[a_b])
            S.dma("sp", ABv[d, 1, :, :, t0:t0 + TB], b_t[:, d * 4:(d + 1) * 4, :], reads=[b_b])
        yield

    run_interleaved((blk(b) for b in range(NB)), 2)
    cab_t = cx.sb("cab", [128, 2, 2, 4], F32); cab_b = Buf("cab")
    for d in range(2):
        S.op("dve", lambda h, d=d: h.memset(cab_t[:, d, 0, :], 1.0), writes=[cab_b])
        S.op("dve", lambda h, d=d: h.memset(cab_t[:, d, 1, :], 0.0), writes=[cab_b])
        order = range(NB) if d == 0 else range(NB - 1, -1, -1)
        for b in order:
            S.op("dve", lambda h, d=d, b=b: h.tensor_tensor(out=cab_t[:, d, 1, :], in0=cab_t[:, d, 1, :], in1=blk_t[:, b, d, 0, :],
                                                            op=ALU.mult), reads=[cab_b, blk_b], writes=[cab_b])
            S.op("dve", lambda h, d=d, b=b: h.tensor_tensor(out=cab_t[:, d, 1, :], in0=cab_t[:, d, 1, :], in1=blk_t[:, b, d, 1, :],
                                                            op=ALU.add), reads=[cab_b, blk_b], writes=[cab_b])
            S.op("dve", lambda h, d=d, b=b: h.tensor_tensor(out=cab_t[:, d, 0, :], in0=cab_t[:, d, 0, :], in1=blk_t[:, b, d, 0, :],
                                                            op=ALU.mult), reads=[cab_b, blk_b], writes=[cab_b])
    S.dma("sp", BLK[:, :, :, :, :], blk_t[:, :, :, :, :], reads=[blk_b])
    S.dma("sp", CAB[:, :, :, :], cab_t[:, :, :, :], reads=[cab_b])
    cx.pop()
    cx.pop()
    return cx.finish() if own else None


def chunk_vec(v, nch):
    return np.ascontiguousarray(np.asarray(v, np.float32).reshape(nch, 128).T)


def oa_inputs(xT, xhalo, P, pos):
    cos, sin = rope_tables(pos, 16, 128)
    return {
        "xT": np.ascontiguousarray(xT), "xhalo": np.ascontiguousarray(xhalo), "w_in": P["w_in"], "g": vec128(P["g"], 8),
        "g_cq": vec128(P["g_cq"], 2), "g_ckv": vec128(P["g_ckv"], 1), "w_uq": P["w_uq"], "w_ukv": P["w_ukv"],
        "cw": np.ascontiguousarray(P["conv_w"].reshape(4, 4, 128).transpose(2, 1, 0)),
        "cb": chunk_vec(P["conv_b"], 4), "wa": P["wa"], "wx": P["wx"],
        "ba": np.ascontiguousarray(P["ba"].reshape(2, 4, 128).transpose(2, 0, 1)),
        "bx": np.ascontiguousarray(P["bx"].reshape(2, 4, 128).transpose(2, 0, 1)),
        "lam": np.ascontiguousarray(P["lam"].reshape(2, 4, 128).transpose(2, 0, 1)),
        "cos": cos, "sin": sin, "rot": rot_matrix(32),
    }


def build_ob1(ntok=TOK, nrank=4, cx=None):
    seq = ntok * nrank
    QG = ntok // 512
    NKT = seq // 128
    NT = ntok // 128
    own = cx is None
    cx = Ctx() if own else cx
    cx.push()
    S = cx.S
    QN = cx.dram_in("QN", [512, ntok], BF16)
    QR = cx.dram_in("QR", [256, ntok], BF16)
    KNg = cx.dram_in("KNg", [8 * nrank * 64, ntok], BF16)
    KRg = cx.dram_in("KRg", [nrank * 32, ntok], BF16)
    Vg = cx.dram_in("Vg", [8 * nrank * 128, NT * 65], BF16)
    YC = cx.dram_out("YC", [512, ntok], BF16)

    q_ring = mk_ring(cx, "sb", "q", 2, [128, ntok], BF16)
    k_ring = mk_ring(cx, "sb", "k", 2, [128, seq], BF16)
    v_ring = mk_ring(cx, "sb", "v", 2, [128, NKT, 65], BF16)
    p_ring = mk_ring(cx, "sb", "p", 4, [128, 1024], BF16)
    osb_ring = mk_ring(cx, "sb", "osb", 2, [64, 512], F32)
    rc_ring = mk_ring(cx, "sb", "rc", 2, [128, 512], F32)
    yc_ring = mk_ring(cx, "sb", "yc", 2, [64, 512], BF16)
    ones32 = cx.sb("ones32", [128, 64], F32); ones32_b = Buf("ones32")
    s_ring = mk_ring(cx, "ps", "s", 3, [128, 1024], F32)
    o_ring = mk_ring(cx, "ps", "o", 2, [128, 512], F32)
    S.op("pool", lambda h: h.memset(ones32[:, :], 1.0), writes=[ones32_b])
    scale = float(96 ** -0.5)
    NKP = NKT // 2
    LA = 2

    def load_head(hd):
        q_t, q_b = q_ring.next()
        k_t, k_b = k_ring.next()
        v_t, v_b = v_ring.next()
        S.dma("sp", q_t[0:64, :], QN[hd * 64:(hd + 1) * 64, :], writes=[q_b])
        S.dma("sp", q_t[64:96, :], QR[hd * 32:(hd + 1) * 32, :], writes=[q_b])
        for r in range(nrank):
            kr0 = ((hd // 2) * nrank + r) * 128 + (hd % 2) * 64
            S.dma("sp", k_t[0:64, r * ntok:(r + 1) * ntok], KNg[kr0:kr0 + 64, :], writes=[k_b])
            S.dma("sp", k_t[64:96, r * ntok:(r + 1) * ntok], KRg[r * 32:(r + 1) * 32, :], writes=[k_b])
            S.dma("sp", v_t[:, r * NT:(r + 1) * NT, :],
                  Vg[(hd * nrank + r) * 128:(hd * nrank + r + 1) * 128, :].rearrange("p (i e) -> p i e", e=65), writes=[v_b])
        return (q_t, q_b, k_t, k_b, v_t, v_b)

    nxt = load_head(0)
    cx.run_hook()
    jobs = cx.take_jobs()
    for hd in range(8):
        q_t, q_b, k_t, k_b, v_t, v_b = nxt
        if hd + 1 < 8:
            nxt = load_head(hd + 1)
        for qg in range(QG):
            if jobs and (hd, qg) != (0, 0):
                jobs.pop(0)()
            o_t, o_b = o_ring.next()
            stiles = {}

            def emit_s(kp):
                s_t, s_b = s_ring.next()
                for hf in range(2):
                    kt = 2 * kp + hf
                    S.op("pe", lambda h, kt=kt, hf=hf: h.matmul(s_t[:, hf * 512:(hf + 1) * 512], lhsT=k_t[0:96, kt * 128:(kt + 1) * 128],
                                                                rhs=q_t[0:96, qg * 512:(qg + 1) * 512], start=True, stop=True),
                         reads=[k_b, q_b], writes=[s_b])
                stiles[kp] = (s_t, s_b)

            for kp in range(min(LA, NKP)):
                emit_s(kp)
            for kp in range(NKP):
                s_t, s_b = stiles.pop(kp)
                p_t, p_b = p_ring.next()
                S.op("act", lambda h, s_t=s_t, p_t=p_t: h.activation(out=p_t[:, :], in_=s_t[:, :], func=AF.Exp, scale=scale),
                     reads=[s_b], writes=[p_b])
                if kp + LA < NKP:
                    emit_s(kp + LA)
                for hf in range(2):
                    kt = 2 * kp + hf
                    S.op("pe", lambda h, kt=kt, hf=hf, p_t=p_t: h.matmul(o_t[0:65, :], lhsT=v_t[:, kt, 0:65], rhs=p_t[:, hf * 512:(hf + 1) * 512],
                                                                         start=(kt == 0), stop=(kt == NKT - 1)),
                         reads=[v_b, p_b], writes=[o_b])
            osb_t, osb_b = osb_ring.next()
            rc_t, rc_b = rc_ring.next()
            S.op("act", lambda h: h.activation(out=osb_t[:, :], in_=o_t[0:64, :], func=AF.Copy), reads=[o_b], writes=[osb_b])
            S.op("dve", lambda h: h.reciprocal(out=rc_t[64:65, :], in_=o_t[64:65, :]), reads=[o_b], writes=[rc_b])
            bc_t, bc_b = s_ring.next()
            S.op("pe", lambda h: h.matmul(bc_t[0:64, 0:512], lhsT=ones32[64:65, 0:64], rhs=rc_t[64:65, :], start=True, stop=True),
                 reads=[rc_b, ones32_b], writes=[bc_b])
            yc_t, yc_b = yc_ring.next()
            S.op("dve", lambda h: h.tensor_tensor(out=yc_t[:, :], in0=osb_t[:, :], in1=bc_t[0:64, 0:512], op=ALU.mult),
                 reads=[osb_b, bc_b], writes=[yc_b])
            S.dma("pool", YC[hd * 64:(hd + 1) * 64, qg * 512:(qg + 1) * 512], yc_t[:, :], reads=[yc_b])
    while jobs:
        jobs.pop(0)()
    cx.pop()
    return cx.finish() if own else None


def build_ob2(ntok=TOK, ngrp=4, cx=None):
    TB = 512
    NB = ntok // TB
    own = cx is None
    cx = Ctx() if own else cx
    cx.push()
    S = cx.S
    AB = cx.dram_in("AB", [2, 2, 512, ntok])
    GX = cx.dram_in("GX", [512, ntok], BF16)
    YC = cx.dram_in("YC", [512, ntok], BF16)
    xT = cx.dram_in("xT", [D, ntok])
    w_out = cx.dram_in("w_out", [D, D])
    BLK = cx.dram_in("BLK", [128, NB, 2, 2, 4])
    CABg = cx.dram_in("CABg", [128, ngrp, 16])
    mfd = cx.dram_in("mf", [128, ngrp])
    mbd = cx.dram_in("mb", [128, ngrp])
    oT = cx.dram_out("oT", [D, ntok])

    woA = cx.sb("woA", [128, 4, D], BF16); woA_b = Buf("woA")
    woB = cx.sb("woB", [128, 4, D], BF16); woB_b = Buf("woB")
    blk_t = cx.sb("blk", [128, NB, 2, 2, 4], F32); blk_b = Buf("blk")
    cab_t = cx.sb("cab", [128, ngrp, 16], F32); cab_b = Buf("cab")
    m_t = cx.sb("m", [128, 2, ngrp], F32); m_b = Buf("m")
    hin_t = cx.sb("hin", [128, 2, 4], F32); hin_b = Buf("hin")
    tmp_t = cx.sb("tmp", [128, 4], F32); tmp_b = Buf("tmp")
    init_t = cx.sb("init", [128, NB, 2, 4], F32); init_b = Buf("init")
    ab_ring = mk_ring(cx, "sb", "ab", 2, [128, 2, 2, 4, TB], F32)
    hs_ring = mk_ring(cx, "sb", "hs", 2, [128, 2, 4, TB], F32)
    gx_ring = mk_ring(cx, "sb", "gx", 2, [128, 4, TB], BF16)
    yc_ring = mk_ring(cx, "sb", "yc", 2, [128, 4, TB], BF16)
    yd_ring = mk_ring(cx, "sb", "yd", 2, [128, 4, TB], BF16)
    x_ring = mk_ring(cx, "sb", "x", 2, [128, KC, TB], F32)
    y_ring = mk_ring(cx, "ps", "y", 3, [128, 512], F32)

    S.dma(cx.bind.get("_wq", "pool"), woA[:, :, :], w_out[0:512, :].rearrange("(i p) n -> p i n", p=128), writes=[woA_b])
    S.dma(cx.bind.get("_wq", "pool"), woB[:, :, :], w_out[512:1024, :].rearrange("(g p) n -> p g n", p=128), writes=[woB_b])
    S.dma("sp", blk_t[:, :, :, :, :], BLK[:, :, :, :, :], writes=[blk_b])
    S.dma("sp", cab_t[:, :, :], CABg[:, :, :], writes=[cab_b])
    S.dma("sp", m_t[:, 0, :], mfd[:, :], writes=[m_b])
    S.dma("sp", m_t[:, 1, :], mbd[:, :], writes=[m_b])
    S.op("pool", lambda h: h.memset(hin_t[:, :, :], 0.0), writes=[hin_b])
    for d in range(2):
        order = range(ngrp) if d == 0 else range(ngrp - 1, -1, -1)
        for i in order:
            S.op("dve", lambda h, d=d, i=i: h.tensor_tensor(out=tmp_t[:, :], in0=hin_t[:, d, :], in1=cab_t[:, i, d * 8:d * 8 + 4], op=ALU.mult),
                 reads=[hin_b, cab_b, tmp_b], writes=[tmp_b])
            S.op("dve", lambda h, d=d, i=i: h.tensor_tensor(out=tmp_t[:, :], in0=tmp_t[:, :], in1=cab_t[:, i, d * 8 + 4:d * 8 + 8], op=ALU.add),
                 reads=[tmp_b, cab_b], writes=[tmp_b])
            S.op("dve", lambda h, d=d, i=i: h.tensor_tensor(out=tmp_t[:, :], in0=tmp_t[:, :], in1=hin_t[:, d, :], op=ALU.subtract),
                 reads=[tmp_b, hin_b], writes=[tmp_b])
            S.op("dve", lambda h, d=d, i=i: h.scalar_tensor_tensor(out=hin_t[:, d, :], in0=tmp_t[:, :], scalar=m_t[:, d, i:i + 1],
                                                                   in1=hin_t[:, d, :], op0=ALU.mult, op1=ALU.add),
                 reads=[tmp_b, m_b, hin_b], writes=[hin_b])
    for d in range(2):
        order = list(range(NB)) if d == 0 else list(range(NB - 1, -1, -1))
        S.op("dve", lambda h, d=d, b0=order[0]: h.tensor_copy(out=init_t[:, b0, d, :], in_=hin_t[:, d, :]),
             reads=[hin_b, init_b], writes=[init_b])
        for bi in range(NB - 1):
            b, bn = order[bi], order[bi + 1]
            S.op("dve", lambda h, d=d, b=b, bn=bn: h.tensor_tensor(out=init_t[:, bn, d, :], in0=init_t[:, b, d, :],
                                                                   in1=blk_t[:, b, d, 0, :], op=ALU.mult),
                 reads=[init_b, blk_b], writes=[init_b])
            S.op("dve", lambda h, d=d, b=b, bn=bn: h.tensor_tensor(out=init_t[:, bn, d, :], in0=init_t[:, bn, d, :],
                                                                   in1=blk_t[:, b, d, 1, :], op=ALU.add),
                 reads=[init_b, blk_b], writes=[init_b])
    xv = xT.rearrange("(c p) t -> p c t", p=128)
    ov = oT.rearrange("(c p) t -> p c t", p=128)
    ABv = AB.rearrange("d s (c p) t -> d s p c t", p=128)
    def blk(b):
        t0 = b * TB
        ab_t, ab_b = ab_ring.next()
        for d in range(2):
            for s_ in range(2):
                S.dma("sp", ab_t[:, d, s_, :, :], ABv[d, s_, :, :, t0:t0 + TB], writes=[ab_b])
        gx_t, gx_b = gx_ring.next()
        yc_t, yc_b = yc_ring.next()
        x_t, x_b = x_ring.next()
        S.dma("sp", gx_t[:, :, :], GX.rearrange("(c p) t -> p c t", p=128)[:, :, t0:t0 + TB], writes=[gx_b])
        S.dma("sp", yc_t[:, :, :], YC.rearrange("(i p) t -> p i t", p=128)[:, :, t0:t0 + TB], writes=[yc_b])
        S.dma("sp", x_t[:, :, :], xv[:, :, t0:t0 + TB], writes=[x_b])
        yield
        hs_t, hs_b = hs_ring.next()
        for d in range(2):
            for c in range(4):
                if d == 0:
                    S.op("dve", lambda h, d=d, c=c, b=b: h.tensor_tensor_scan(
                        out=hs_t[:, d, c, :], data0=ab_t[:, d, 0, c, :], data1=ab_t[:, d, 1, c, :],
                        initial=init_t[:, b, d, c:c + 1], op0=ALU.mult, op1=ALU.add), reads=[ab_b, init_b, hs_b], writes=[hs_b])
                else:
                    S.op("dve", lambda h, d=d, c=c, b=b: h.tensor_tensor_scan(
                        out=hs_t[:, d, c, ::-1], data0=ab_t[:, d, 0, c, ::-1], data1=ab_t[:, d, 1, c, ::-1],
                        initial=init_t[:, b, d, c:c + 1], op0=ALU.mult, op1=ALU.add), reads=[ab_b, init_b, hs_b], writes=[hs_b])
        yield
        S.op("pool", lambda h: h.tensor_tensor(out=hs_t[:, 0, :, :], in0=hs_t[:, 0, :, :], in1=hs_t[:, 1, :, :], op=ALU.add),
             reads=[hs_b], writes=[hs_b])
        yd_t, yd_b = yd_ring.next()
        S.op("pool", lambda h: h.tensor_tensor(out=yd_t[:, :, :], in0=hs_t[:, 0, :, :], in1=gx_t[:, :, :], op=ALU.mult),
             reads=[hs_b, gx_b], writes=[yd_b])
        yield
        for o in range(KC):
            y_t, y_b = y_ring.next()
            for hh in range(4):
                S.op("pe", lambda h, hh=hh, o=o: h.matmul(y_t[:, :], lhsT=woA[:, hh, o * 128:(o + 1) * 128], rhs=yc_t[:, hh, :],
                                                         start=(hh == 0), stop=False), reads=[woA_b, yc_b], writes=[y_b])
            for g in range(4):
                S.op("pe", lambda h, g=g, o=o: h.matmul(y_t[:, :], lhsT=woB[:, g, o * 128:(o + 1) * 128], rhs=yd_t[:, g, :],
                                                       start=False, stop=(g == 3)), reads=[woB_b, yd_b], writes=[y_b])
            S.op("dve", lambda h, o=o: h.tensor_tensor(out=x_t[:, o, :], in0=y_t[:, :], in1=x_t[:, o, :], op=ALU.add),
                 reads=[y_b, x_b], writes=[x_b])
        S.dma("pool", ov[:, :, t0:t0 + TB], x_t[:, :, :], reads=[x_b])
        yield

    run_interleaved((blk(b) for b in range(NB)), 2, 2)
    cx.pop()
    return cx.finish() if own else None


def allgather(cx, in_ap, out_ap, groups):
    S = cx.S
    S.barrier()
    sem = S.new_sem("cc")
    cx.nc.gpsimd.collective_compute("AllGather", ALU.bypass, replica_groups=groups, ins=[in_ap], outs=[out_ap]).then_inc(sem, 1)
    for e in S.ENGS:
        S.h[e].wait_ge(sem, 1)


def allgather_many(cx, pairs, groups):
    S = cx.S
    S.barrier()
    sem = S.new_sem("ccm")
    for (in_ap, out_ap) in pairs:
        cx.nc.gpsimd.collective_compute("AllGather", ALU.bypass, replica_groups=groups, ins=[in_ap], outs=[out_ap]).then_inc(sem, 1)
    for e in S.ENGS:
        S.h[e].wait_ge(sem, len(pairs))


def emit_select(cx, src_t, src_b, nrank, m_t, m_b, side, acc_t, acc_b):
    S = cx.S
    S.op("dve", lambda h: h.tensor_scalar_mul(out=acc_t[:, :], in0=src_t[:, 0, :], scalar1=m_t[:, side, 0:1]),
         reads=[src_b, m_b], writes=[acc_b])
    for i in range(1, nrank):
        S.op("dve", lambda h, i=i: h.scalar_tensor_tensor(out=acc_t[:, :], in0=src_t[:, i, :], scalar=m_t[:, side, i:i + 1],
                                                          in1=acc_t[:, :], op0=ALU.mult, op1=ALU.add),
             reads=[src_b, m_b, acc_b], writes=[acc_b])


def even_exchange_start(cx, KTh, Vh, UTh, pack, packg, groups, ntok):
    S = cx.S
    S.barrier()
    S.dma_dd_async("sp", pack[:, 0:128], KTh[:, 128:256])
    S.dma_dd_async("sp", pack[:, 128:256], KTh[:, ntok:ntok + 128])
    S.dma_dd_async("sp", pack[:, 256:288].rearrange("p (g t) -> p g t", g=4), UTh[:, :, 8:16])
    S.dma_dd_async("sp", pack[:, 288:320].rearrange("p (g t) -> p g t", g=4), UTh[:, :, ntok:ntok + 8])
    S.dma_dd_async("sp", pack[:, 320:450], Vh[128:256, :])
    S.dma_dd_async("sp", pack[:, 450:580], Vh[ntok:ntok + 128, :])
    S.barrier()
    sem = S.new_sem("cce")
    cx.nc.gpsimd.collective_compute("AllGather", ALU.bypass, replica_groups=groups, ins=[pack[:, :]], outs=[packg[:, :]]).then_inc(sem, 1)
    return sem


def even_exchange_finish(cx, sem, KTh, Vh, UTh, packg, mlr, nrank, ntok):
    S = cx.S
    for e in S.ENGS:
        S.h[e].wait_ge(sem, 1)
    cx.push()
    pg_t = cx.sb("pg", [128, nrank, 580], BF16); pg_b = Buf("pg")
    m_t = cx.sb("mlr", [128, 2, nrank], F32); m_b = Buf("mlr")
    accL = cx.sb("accL", [128, 580], BF16); accL_b = Buf("accL")
    accR = cx.sb("accR", [128, 580], BF16); accR_b = Buf("accR")
    S.dma("sp", pg_t[:, :, :], packg.rearrange("(r p) n -> p r n", p=128), writes=[pg_b])
    S.dma("sp", m_t[:, :, :], mlr[:, :, :], writes=[m_b])
    emit_select(cx, pg_t, pg_b, nrank, m_t, m_b, 0, accL, accL_b)
    emit_select(cx, pg_t, pg_b, nrank, m_t, m_b, 1, accR, accR_b)
    S.dma("sp", KTh[:, 0:128], accL[:, 128:256], reads=[accL_b])
    S.dma("sp", UTh[:, :, 0:8], accL[:, 288:320].rearrange("p (g t) -> p g t", g=4), reads=[accL_b])
    S.dma("sp", Vh[0:128, :], accL[:, 450:580], reads=[accL_b])
    S.dma("sp", KTh[:, 128 + ntok:256 + ntok], accR[:, 0:128], reads=[accR_b])
    S.dma("sp", UTh[:, :, 8 + ntok:16 + ntok], accR[:, 256:288].rearrange("p (g t) -> p g t", g=4), reads=[accR_b])
    S.dma("sp", Vh[128 + ntok:256 + ntok, :], accR[:, 320:450], reads=[accR_b])
    cx.pop()


def xhalo_exchange_start(cx, xprev, xhp, xhpg, groups, ntok):
    S = cx.S
    S.barrier()
    S.dma_dd_async("sp", xhp[:, 0:2], xprev[:, 0:2])
    S.dma_dd_async("sp", xhp[:, 2:4], xprev[:, ntok - 2:ntok])
    S.barrier()
    sem = S.new_sem("ccx")
    cx.nc.gpsimd.collective_compute("AllGather", ALU.bypass, replica_groups=groups, ins=[xhp[:, :]], outs=[xhpg[:, :]]).then_inc(sem, 1)
    return sem


def xhalo_exchange_finish(cx, sem, xhpg, xhalo, mlr, nrank):
    S = cx.S
    for e in S.ENGS:
        S.h[e].wait_ge(sem, 1)
    cx.push()
    xg_t = cx.sb("xg", [128, nrank, 32], F32); xg_b = Buf("xg")
    m_t = cx.sb("mlr", [128, 2, nrank], F32); m_b = Buf("mlr")
    accL = cx.sb("accL", [128, 32], F32); accL_b = Buf("accL")
    accR = cx.sb("accR", [128, 32], F32); accR_b = Buf("accR")
    for r in range(nrank):
        S.dma("sp", xg_t[:, r, :].rearrange("p (c t) -> p c t", t=4),
              xhpg[r * D:(r + 1) * D, :].rearrange("(c p) t -> p c t", p=128), writes=[xg_b])
    S.dma("sp", m_t[:, :, :], mlr[:, :, :], writes=[m_b])
    emit_select(cx, xg_t, xg_b, nrank, m_t, m_b, 0, accL, accL_b)
    emit_select(cx, xg_t, xg_b, nrank, m_t, m_b, 1, accR, accR_b)
    xhv = xhalo.rearrange("(c p) t -> p c t", p=128)
    S.dma("sp", xhv[:, :, 0:2], accL[:, :].rearrange("p (c t) -> p c t", t=4)[:, :, 2:4], reads=[accL_b])
    S.dma("sp", xhv[:, :, 2:4], accR[:, :].rearrange("p (c t) -> p c t", t=4)[:, :, 0:2], reads=[accR_b])
    cx.pop()


SMALL_SPECS = None


def build_fused(B=2, nrank=4, ntok=TOK, depth=4):
    NE, NO = (depth + 1) // 2, depth // 2
    NT = ntok // 128
    NB = ntok // 512
    groups = [[b * nrank + r for r in range(nrank)] for b in range(B)]
    cx = Ctx()
    nc = cx.nc
    I = cx.ext_in
    x0 = I("xT", [D, ntok])
    Wd = {
        "e_w_in": I("e_w_in", [NE, D, 1280]), "e_w_pool": I("e_w_pool", [NE, 4, 128, 128]), "e_w_out": I("e_w_out", [NE, D, D]),
        "o_w_in": I("o_w_in", [NO, D, 1440]), "o_w_uq": I("o_w_uq", [NO, 256, 768]), "o_w_ukv": I("o_w_ukv", [NO, 128, 1024]),
        "o_lru_wa": I("o_lru_wa", [NO, 2, 8, 64, 64]), "o_lru_wx": I("o_lru_wx", [NO, 2, 8, 64, 64]), "o_w_out": I("o_w_out", [NO, D, D]),
        "w_mlp1": I("w_mlp1", [depth, D, DFF]), "w_mlp2": I("w_mlp2", [depth, DFF, D]),
        "g_mix": I("g_mix", [depth, 128, KC]), "g_mlp": I("g_mlp", [depth, 128, KC]), "g_fin": I("g_fin", [128, KC]),
        "pscale": I("pscale", [NE, 128, 4]), "sinkrow": I("sinkrow", [NE, 1, 2, 512]),
        "g_cq": I("g_cq", [NO, 128, 2]), "g_ckv": I("g_ckv", [NO, 128, 1]), "cw": I("cw", [NO, 128, 4, 4]), "cb": I("cb", [NO, 128, 4]),
        "ba": I("ba", [NO, 128, 2, 4]), "bx": I("bx", [NO, 128, 2, 4]), "lam": I("lam", [NO, 128, 2, 4]),
        "cos32": I("cos32", [128, ntok]), "sin32": I("sin32", [128, ntok]), "cos16": I("cos16", [128, ntok]), "sin16": I("sin16", [128, ntok]),
        "rot64": I("rot64", [128, 128]), "rot32": I("rot32", [128, 128]), "masks": I("masks", [4, 128, 512]),
        "ident": I("ident", [128, 128]),
        "invc": I("invc", [128, 2, 4, 16]), "mfb": I("mfb", [2, 128, nrank]), "mlr": I("mlr", [128, 2, nrank]),
    }
    outT = cx.ext_out("oT", [D, ntok])

    def tmp(name, shape, dt=F32):
        return nc.dram_tensor(name, list(shape), dt, kind="Internal").ap()

    def precast_jobs(layer, w1b_d, w2b_d):
        js = []
        for k in range(8):
            js.append(lambda k=k: cx.S.dma_dd_async("pool", w1b_d[k * 128:(k + 1) * 128, :], Wd["w_mlp1"][layer][k * 128:(k + 1) * 128, :]))
        for k in range(8):
            js.append(lambda k=k: cx.S.dma_dd_async("pool", w2b_d[k * 512:(k + 1) * 512, :], Wd["w_mlp2"][layer][k * 512:(k + 1) * 512, :]))
        return js

    wb16 = {}
    for (nm, nl, shp) in (("e_w_in", NE, [D, 1280]), ("e_w_out", NE, [D, D]), ("o_w_in", NO, [D, 1440]), ("o_w_out", NO, [D, D])):
        for l in range(nl):
            if nm == "e_w_in" and l == 0:
                continue
            wb16[(nm, l)] = tmp(f"{nm}_bf{l}", shp, BF16)

    def cast_job(nm, l):
        def f():
            if (nm, l) in wb16:
                for k in range(4):
                    cx.S.dma_dd_async("pool", wb16[(nm, l)][k * 256:(k + 1) * 256, :], Wd[nm][l][k * 256:(k + 1) * 256, :])
        return f

    def wsrc(nm, l):
        return wb16.get((nm, l), Wd[nm][l])

    def wq(nm, l):
        return "sp" if (nm, l) in wb16 else "pool"

    xcur = x0
    xh_pending = None
    for layer in range(depth):
        L = f"L{layer}"
        xmix = tmp(L + "_xmix", [D, ntok])
        w1b_d = tmp(L + "_w1b", [D, DFF], BF16)
        w2b_d = tmp(L + "_w2b", [DFF, D], BF16)
        if layer % 2 == 0:
            e = layer // 2
            QsT = tmp(L + "_QsT", [128, 4, ntok], BF16)
            KTh = tmp(L + "_KTh", [128, ntok + 256], BF16)
            Vh = tmp(L + "_Vh", [ntok + 256, 130], BF16)
            UTh = tmp(L + "_UTh", [128, 4, ntok + 16], BF16)
            pack = tmp(L + "_pack", [128, 580], BF16)
            packg = tmp(L + "_packg", [nrank * 128, 580], BF16)
            cx.bind = {"xT": xcur, "w_in": wsrc("e_w_in", e), "_wq": wq("e_w_in", e), "g": Wd["g_mix"][layer], "cos": Wd["cos32"], "sin": Wd["sin32"],
                       "rot": Wd["rot64"], "QsT": QsT, "KT": KTh[:, 128:128 + ntok], "Vaug": Vh[128:128 + ntok, :],
                       "UT": UTh[:, :, 8:8 + ntok]}
            exs = {}

            def ea_mid(KTh=KTh, Vh=Vh, UTh=UTh, pack=pack, packg=packg, exs=exs):
                exs["sem"] = even_exchange_start(cx, KTh, Vh, UTh, pack, packg, groups, ntok)

            cx.hook = cast_job("e_w_out", e)
            build_ea(ntok, cx=cx, order=[0, NB - 1] + list(range(1, NB - 1)) if NB > 1 else [0], mid_hook=ea_mid)
            even_exchange_finish(cx, exs["sem"], KTh, Vh, UTh, packg, Wd["mlr"], nrank, ntok)
            cx.bind = {"QsT": QsT, "KTh": KTh, "Vh": Vh, "UTh": UTh, "xT": xcur, "w_pool": Wd["e_w_pool"][e],
                       "pscale": Wd["pscale"][e], "w_out": wsrc("e_w_out", e), "_wq": wq("e_w_out", e), "sinkrow": Wd["sinkrow"][e], "masks": Wd["masks"],
                       "invc": Wd["invc"], "ident": Wd["ident"], "oT": xmix}
            build_eb(ntok, cx=cx)
        else:
            o = layer // 2
            if xh_pending is None:
                xhp = tmp(L + "_xhp", [D, 4]); xhpg = tmp(L + "_xhpg", [nrank * D, 4])
            else:
                xhp, xhpg = xh_pending["xhp"], xh_pending["xhpg"]
            xhalo = tmp(L + "_xhalo", [D, 4])
            QN = tmp(L + "_QN", [512, ntok], BF16); QR = tmp(L + "_QR", [256, ntok], BF16)
            KNR = tmp(L + "_KNR", [544, ntok], BF16); V5 = tmp(L + "_V5", [1024, NT * 65], BF16)
            GX = tmp(L + "_GX", [512, ntok], BF16); AB = tmp(L + "_AB", [2, 2, 512, ntok])
            BLK = tmp(L + "_BLK", [128, NB, 2, 2, 4]); CAB = tmp(L + "_CAB", [128, 16]); XR = tmp(L + "_XR", [512, ntok])
            KNg = tmp(L + "_KNg", [8 * nrank * 64, ntok], BF16); KRg = tmp(L + "_KRg", [nrank * 32, ntok], BF16)
            Vg = tmp(L + "_Vg", [8 * nrank * 128, NT * 65], BF16)
            CABg = tmp(L + "_CABg", [nrank * 128, 16]); YC = tmp(L + "_YC", [512, ntok], BF16)
            if xh_pending is None:
                xh_sem = xhalo_exchange_start(cx, xcur, xhp, xhpg, groups, ntok)
            else:
                xh_sem = xh_pending["sem"]
            xhalo_exchange_finish(cx, xh_sem, xhpg, xhalo, Wd["mlr"], nrank)
            cx.bind = {"xT": xcur, "xhalo": xhalo, "w_in": wsrc("o_w_in", o), "_wq": wq("o_w_in", o), "g": Wd["g_mix"][layer], "g_cq": Wd["g_cq"][o],
                       "g_ckv": Wd["g_ckv"][o], "w_uq": Wd["o_w_uq"][o], "w_ukv": Wd["o_w_ukv"][o], "cw": Wd["cw"][o], "cb": Wd["cb"][o],
                       "wa": Wd["o_lru_wa"][o], "wx": Wd["o_lru_wx"][o], "ba": Wd["ba"][o], "bx": Wd["bx"][o], "lam": Wd["lam"][o],
                       "cos": Wd["cos16"], "sin": Wd["sin16"], "rot": Wd["rot32"], "QN": QN, "QR": QR, "KNR": KNR, "V5": V5, "GX": GX,
                       "AB": AB, "BLK": BLK, "CAB": CAB.rearrange("p (d s c) -> p d s c", d=2, s=2), "XR": XR}
            ccsem = cx.S.new_sem("ccg")
            ncc = [0]

            def gather_kv():
                pairs = []
                for i in range(4):
                    pairs.append((KNR[i * 128:(i + 1) * 128, :], KNg[i * nrank * 128:(i + 1) * nrank * 128, :]))
                pairs.append((KNR[512:544, :], KRg[:, :]))
                for hd in range(8):
                    pairs.append((V5[hd * 128:(hd + 1) * 128, :], Vg[hd * nrank * 128:(hd + 1) * nrank * 128, :]))
                for (i_ap, o_ap) in pairs:
                    nc.gpsimd.collective_compute("AllGather", ALU.bypass, replica_groups=groups, ins=[i_ap], outs=[o_ap]).then_inc(ccsem, 1)
                    ncc[0] += 1

            build_oa(ntok, cx=cx, mid_hook=gather_kv)
            cabsem = cx.S.new_sem("cab")
            nc.gpsimd.collective_compute("AllGather", ALU.bypass, replica_groups=groups, ins=[CAB[:, :]], outs=[CABg[:, :]]).then_inc(cabsem, 1)
            for e_ in cx.S.ENGS:
                cx.S.h[e_].wait_ge(ccsem, ncc[0])
            cx.bind = {"QN": QN, "QR": QR, "KNg": KNg, "KRg": KRg, "Vg": Vg, "YC": YC}
            cx.jobs = [cast_job("o_w_out", o)] + precast_jobs(layer, w1b_d, w2b_d)
            build_ob1(ntok, nrank, cx=cx)
            for e_ in cx.S.ENGS:
                cx.S.h[e_].wait_ge(cabsem, 1)
            cx.bind = {"AB": AB, "GX": GX, "YC": YC, "xT": xcur, "w_out": wsrc("o_w_out", o), "_wq": wq("o_w_out", o), "BLK": BLK,
                       "CABg": CABg.rearrange("(r p) n -> p r n", p=128), "mf": Wd["mfb"][0], "mb": Wd["mfb"][1], "oT": xmix}
            build_ob2(ntok, nrank, cx=cx)
        last = layer == depth - 1
        xnext = outT if last else tmp(L + "_xmlp", [D, ntok])
        pre = layer % 2 == 1
        cx.bind = {"xT": xmix, "w1": w1b_d if pre else Wd["w_mlp1"][layer], "w2": w2b_d if pre else Wd["w_mlp2"][layer],
                   "g": Wd["g_mlp"][layer], "gf": Wd["g_fin"], "oT": xnext}
        NBm = ntok // 256
        if layer + 1 < depth:
            cx.hook = cast_job("o_w_in", (layer + 1) // 2) if (layer + 1) % 2 == 1 else cast_job("e_w_in", (layer + 1) // 2)
        if (layer + 1) < depth and (layer + 1) % 2 == 1 and NBm > 2:
            Ln = f"L{layer + 1}"
            xh_pending = {"xhp": tmp(Ln + "_xhp", [D, 4]), "xhpg": tmp(Ln + "_xhpg", [nrank * D, 4])}

            def mlp_mid(xh_pending=xh_pending, xnext=xnext):
                xh_pending["sem"] = xhalo_exchange_start(cx, xnext, xh_pending["xhp"], xh_pending["xhpg"], groups, ntok)

            build_mlp(last, ntok, cx=cx, wbf16=pre, order=[0, NBm - 1] + list(range(1, NBm - 1)), mid_hook=mlp_mid)
        else:
            xh_pending = None
            build_mlp(last, ntok, cx=cx, wbf16=pre)
        xcur = xnext
    cx.bind = {}
    return cx.finish()


_FUSED = {}


def run_model(x, W, nrank=4, ntok=TOK):
    B, Sq, _ = x.shape
    ncore = B * nrank
    assert Sq == nrank * ntok
    depth = W["norm_mlp"].shape[0]
    NE, NO = (depth + 1) // 2, depth // 2
    key = (B, nrank, ntok, depth)
    if key not in _FUSED:
        _FUSED[key] = build_fused(B, nrank, ntok, depth)
    nc = _FUSED[key]
    f32 = lambda a: np.ascontiguousarray(np.asarray(a, np.float32))
    g_mix = np.stack([vec128(W["e_norm_mix"][l // 2] if l % 2 == 0 else W["o_norm_mix"][l // 2], 8) for l in range(depth)])
    shared = {
        "e_w_in": f32(W["e_w_in"]), "e_w_pool": f32(W["e_w_pool"]), "e_w_out": f32(W["e_w_out"]),
        "o_w_in": f32(W["o_w_in"]), "o_w_uq": f32(W["o_w_uq"]), "o_w_ukv": f32(W["o_w_ukv"]),
        "o_lru_wa": f32(W["o_lru_wa"]), "o_lru_wx": f32(W["o_lru_wx"]), "o_w_out": f32(W["o_w_out"]),
        "w_mlp1": f32(W["w_mlp1"]), "w_mlp2": f32(W["w_mlp2"]),
        "g_mix": g_mix, "g_mlp": np.stack([vec128(W["norm_mlp"][l], 8) for l in range(depth)]), "g_fin": vec128(W["final_norm"], 8),
        "pscale": np.stack([vec128(W["e_pool_scale"][e], 4) for e in range(NE)]),
        "sinkrow": np.stack([np.repeat(f32(W["e_sink"][e]).reshape(2, 4), 128, axis=1).reshape(1, 2, 512) for e in range(NE)]),
        "g_cq": np.stack([vec128(W["o_g_cq"][o], 2) for o in range(NO)]),
        "g_ckv": np.stack([vec128(W["o_g_ckv"][o], 1) for o in range(NO)]),
        "cw": np.stack([f32(f32(W["o_conv_w"][o]).reshape(4, 4, 128).transpose(2, 1, 0)) for o in range(NO)]),
        "cb": np.stack([chunk_vec(W["o_conv_b"][o], 4) for o in range(NO)]),
        "ba": np.stack([f32(f32(W["o_lru_ba"][o]).reshape(2, 4, 128).transpose(2, 0, 1)) for o in range(NO)]),
        "bx": np.stack([f32(f32(W["o_lru_bx"][o]).reshape(2, 4, 128).transpose(2, 0, 1)) for o in range(NO)]),
        "lam": np.stack([f32(f32(W["o_lru_lambda"][o]).reshape(2, 4, 128).transpose(2, 0, 1)) for o in range(NO)]),
        "rot64": rot_matrix(64), "rot32": rot_matrix(32), "ident": np.eye(128, dtype=np.float32),
    }
    in_maps = []
    for c in range(ncore):
        bi, r = c // nrank, c % nrank
        pos = r * ntok + np.arange(ntok)
        cos32, sin32 = rope_tables(pos, 32, 128)
        cos16, sin16 = rope_tables(pos, 16, 128)
        mfb = np.zeros((2, 128, nrank), np.float32); mfb[0, :, :r] = 1.0; mfb[1, :, r + 1:] = 1.0
        mlr = np.zeros((128, 2, nrank), np.float32)
        if r > 0:
            mlr[:, 0, r - 1] = 1.0
        if r < nrank - 1:
            mlr[:, 1, r + 1] = 1.0
        im = dict(shared)
        im.update({"xT": np.ascontiguousarray(x[bi, r * ntok:(r + 1) * ntok, :].T), "cos32": cos32, "sin32": sin32, "cos16": cos16,
                   "sin16": sin16, "masks": eb_masks(r > 0, r < nrank - 1), "invc": eb_invc(r == 0, r == nrank - 1),
                   "mfb": mfb, "mlr": mlr})
        in_maps.append(im)
    res = run_spmd(nc, in_maps)
    out = np.empty((B, Sq, D), np.float32)
    for c in range(ncore):
        bi, r = c // nrank, c % nrank
        out[bi, r * ntok:(r + 1) * ntok, :] = res[c]["oT"].T
    return out


def kernel(**inputs):
    W = {k: np.asarray(v) for k, v in inputs.items()}
    x = np.asarray(W.pop("x"), np.float32)
    return run_model(x, W)
```
